# Optimizing a Trainium2 kernel written in Bass

```python
import jax, jax.numpy as jnp
from jax import lax
import numpy as np

D_MODEL = 1024
BATCH = 2
SEQ = 8192
DEPTH = 2

HEAD_DIM = 64
EPS = 1e-6
N_BRANCH = 3
CONV_WIDTH = D_MODEL
CONV_KERNEL = 31
RET_HEADS = 4
RET_QK_DIM = D_MODEL // RET_HEADS
RET_V_DIM = 2 * RET_QK_DIM
RET_CHUNK = 128
RET_THETA = 10000.0
ATTN_HEADS = D_MODEL // HEAD_DIM
ATTN_KV_HEADS = 4
ATTN_WINDOW = 128
ATTN_BLOCK = 128
ROPE_THETA = 500000.0
ROPE_DIM = HEAD_DIM // 4

IN_SPLITS = (
    2 * CONV_WIDTH,
    CONV_WIDTH,
    RET_HEADS * RET_QK_DIM,
    RET_HEADS * RET_QK_DIM,
    RET_HEADS * RET_V_DIM,
    RET_HEADS * RET_V_DIM,
    ATTN_HEADS * HEAD_DIM,
    ATTN_KV_HEADS * HEAD_DIM,
    ATTN_KV_HEADS * HEAD_DIM,
    ATTN_HEADS * HEAD_DIM,
    N_BRANCH * D_MODEL,
)
IN_WIDTH = sum(IN_SPLITS)

kernel_name = 'hybrid_conv_retention_swa_encoder'


def rms_norm(x, g):
    xf = x.astype(jnp.float32)
    y = xf * lax.rsqrt(jnp.mean(xf * xf, axis=-1, keepdims=True) + EPS) * g.astype(jnp.float32)
    return y.astype(x.dtype)


def rotary(x, positions, theta, rot_dim):
    half = rot_dim // 2
    inv_freq = theta ** (-jnp.arange(half, dtype=jnp.float32) / half)
    ang = positions.astype(jnp.float32)[:, None] * inv_freq[None, :]
    cos = jnp.cos(ang)[:, None, :]
    sin = jnp.sin(ang)[:, None, :]
    xf = x.astype(jnp.float32)
    x1 = xf[..., :half]
    x2 = xf[..., half:rot_dim]
    out = jnp.concatenate([x1 * cos - x2 * sin, x2 * cos + x1 * sin, xf[..., rot_dim:]], axis=-1)
    return out.astype(x.dtype)


def conv_module(u_glu, gate, dw, db, ln_g, ln_b):
    a, b = jnp.split(u_glu, 2, axis=-1)
    v = a * jax.nn.sigmoid(b)
    y = lax.conv_general_dilated(
        v, dw[:, None, :].astype(v.dtype), window_strides=(1,),
        padding=[(CONV_KERNEL // 2, CONV_KERNEL // 2)],
        dimension_numbers=('NWC', 'WIO', 'NWC'),
        feature_group_count=CONV_WIDTH) + db.astype(v.dtype)
    yf = y.astype(jnp.float32)
    mu = jnp.mean(yf, axis=-1, keepdims=True)
    var = jnp.mean(jnp.square(yf - mu), axis=-1, keepdims=True)
    yf = (yf - mu) * lax.rsqrt(var + EPS) * ln_g.astype(jnp.float32) + ln_b.astype(jnp.float32)
    y = jax.nn.silu(yf).astype(v.dtype)
    return y * jax.nn.silu(gate)


def retention_one_direction(q, k, v, log_gamma, inclusive):
    B, H, S, dk = q.shape
    dv = v.shape[-1]
    C = RET_CHUNK
    n = S // C
    qc = q.reshape(B, H, n, C, dk)
    kc = k.reshape(B, H, n, C, dk)
    vc = v.reshape(B, H, n, C, dv)
    idx = jnp.arange(C, dtype=jnp.float32)
    diff = idx[:, None] - idx[None, :]
    mask = (diff >= 0) if inclusive else (diff > 0)
    lg = log_gamma[:, None, None]
    decay_inner = jnp.where(mask[None], jnp.exp(lg * jnp.where(mask, diff, 0.0)[None]), 0.0)
    scores = jnp.einsum('bhnid,bhnjd->bhnij', qc, kc) * decay_inner[None, :, None]
    inner = jnp.einsum('bhnij,bhnjv->bhniv', scores, vc)
    q_decay = jnp.exp(log_gamma[:, None] * (idx[None, :] + 1.0))[None, :, :, None]
    k_decay = jnp.exp(log_gamma[:, None] * (C - 1.0 - idx[None, :]))[None, :, :, None]
    chunk_decay = jnp.exp(log_gamma * C)[None, :, None, None]

    def step(state, xs):
        q_t, k_t, v_t = xs
        cross = jnp.einsum('bhcd,bhdv->bhcv', q_t * q_decay, state)
        state = chunk_decay * state + jnp.einsum('bhcd,bhcv->bhdv', k_t * k_decay, v_t)
        return state, cross

    state0 = jnp.zeros((B, H, dk, dv), jnp.float32)
    xs = (jnp.moveaxis(qc, 2, 0), jnp.moveaxis(kc, 2, 0), jnp.moveaxis(vc, 2, 0))
    _, cross = lax.scan(step, state0, xs)
    out = inner + jnp.moveaxis(cross, 0, 2)
    return out.reshape(B, H, S, dv)


def retention_branch(q, k, v, gate, ret_decay, positions):
    B, S, _ = q.shape
    q = rotary(q.reshape(B, S, RET_HEADS, RET_QK_DIM), positions, RET_THETA, RET_QK_DIM)
    k = rotary(k.reshape(B, S, RET_HEADS, RET_QK_DIM), positions, RET_THETA, RET_QK_DIM)
    qf = jnp.transpose(q, (0, 2, 1, 3)).astype(jnp.float32)
    kf = jnp.transpose(k, (0, 2, 1, 3)).astype(jnp.float32) * (RET_QK_DIM ** -0.5)
    vf = jnp.transpose(v.reshape(B, S, RET_HEADS, RET_V_DIM), (0, 2, 1, 3)).astype(jnp.float32)
    log_gamma = -jnp.exp(ret_decay.astype(jnp.float32))
    fwd = retention_one_direction(qf, kf, vf, log_gamma[0], True)
    bwd = retention_one_direction(qf[:, :, ::-1], kf[:, :, ::-1], vf[:, :, ::-1], log_gamma[1], False)[:, :, ::-1]
    o = fwd + bwd
    mu = jnp.mean(o, axis=-1, keepdims=True)
    var = jnp.mean(jnp.square(o - mu), axis=-1, keepdims=True)
    o = (o - mu) * lax.rsqrt(var + EPS)
    o = jnp.transpose(o, (0, 2, 1, 3)).reshape(B, S, RET_HEADS * RET_V_DIM).astype(gate.dtype)
    return o * jax.nn.silu(gate)


def window_attention_branch(q, k, v, gate, q_norm_g, k_norm_g, sink, positions):
    B, S, _ = q.shape
    G = ATTN_HEADS // ATTN_KV_HEADS
    T = ATTN_BLOCK
    nb = S // T
    q = rms_norm(q.reshape(B, S, ATTN_HEADS, HEAD_DIM), q_norm_g)
    k = rms_norm(k.reshape(B, S, ATTN_KV_HEADS, HEAD_DIM), k_norm_g)
    q = rotary(q, positions, ROPE_THETA, ROPE_DIM)
    k = rotary(k, positions, ROPE_THETA, ROPE_DIM)
    v = v.reshape(B, S, ATTN_KV_HEADS, HEAD_DIM)
    qb = q.reshape(B, nb, T, ATTN_KV_HEADS, G, HEAD_DIM).astype(jnp.float32)
    pad = ((0, 0), (T, T), (0, 0), (0, 0))
    kb = jnp.pad(k, pad).reshape(B, nb + 2, T, ATTN_KV_HEADS, HEAD_DIM)
    vb = jnp.pad(v, pad).reshape(B, nb + 2, T, ATTN_KV_HEADS, HEAD_DIM)
    kwin = jnp.concatenate([kb[:, :-2], kb[:, 1:-1], kb[:, 2:]], axis=2).astype(jnp.float32)
    vwin = jnp.concatenate([vb[:, :-2], vb[:, 1:-1], vb[:, 2:]], axis=2).astype(jnp.float32)
    s = jnp.einsum('bnqhgd,bnjhd->bhgnqj', qb, kwin) * (HEAD_DIM ** -0.5)
    blk = jnp.arange(nb)[:, None, None]
    qpos = blk * T + jnp.arange(T)[None, :, None]
    kpos = (blk - 1) * T + jnp.arange(3 * T)[None, None, :]
    valid = (jnp.abs(kpos - qpos) <= ATTN_WINDOW) & (kpos >= 0) & (kpos < S)
    s = jnp.where(valid, s, -1e30)
    sink_l = sink.astype(jnp.float32).reshape(ATTN_KV_HEADS, G, 1, 1, 1)
    m = jnp.maximum(jnp.max(s, axis=-1, keepdims=True), sink_l)
    p = jnp.exp(s - m)
    p = p / (jnp.sum(p, axis=-1, keepdims=True) + jnp.exp(sink_l - m))
    o = jnp.einsum('bhgnqj,bnjhd->bnqhgd', p, vwin)
    o = o.reshape(B, S, ATTN_HEADS * HEAD_DIM).astype(gate.dtype)
    return o * jax.nn.silu(gate)


def hybrid_layer(x, positions, norm_g, w_in, b_gate, conv_dw, conv_b, conv_ln_g, conv_ln_b,
                 ret_decay, q_norm_g, k_norm_g, attn_sink, w_conv_out, w_ret_out, w_attn_out, w_out):
    B, S, D = x.shape
    h = rms_norm(x, norm_g)
    z = h @ w_in
    offsets = np.cumsum(IN_SPLITS)[:-1].tolist()
    (c_glu, c_gate, r_q, r_k, r_v, r_gate, a_q, a_k, a_v, a_gate, g_logit) = jnp.split(z, offsets, axis=-1)
    y_conv = conv_module(c_glu, c_gate, conv_dw, conv_b, conv_ln_g, conv_ln_b) @ w_conv_out
    y_ret = retention_branch(r_q, r_k, r_v, r_gate, ret_decay, positions) @ w_ret_out
    y_attn = window_attention_branch(a_q, a_k, a_v, a_gate, q_norm_g, k_norm_g, attn_sink, positions) @ w_attn_out
    g = jax.nn.sigmoid(g_logit + b_gate).reshape(B, S, N_BRANCH, D)
    merged = g[:, :, 0] * y_conv + g[:, :, 1] * y_ret + g[:, :, 2] * y_attn
    return x + merged @ w_out


def setup_inputs(seed: int = 0) -> dict:
    key = jax.random.key(seed)
    ks = jax.random.split(key, 18)
    nrm = jax.random.normal
    f32 = jnp.float32
    ret_base = jnp.log(-jnp.log1p(-(2.0 ** (-5.0 - jnp.arange(RET_HEADS, dtype=f32)))))
    return {
        'x': nrm(ks[0], (BATCH, SEQ, D_MODEL), f32),
        'norm_g': 1.0 + 0.02 * nrm(ks[1], (DEPTH, D_MODEL), f32),
        'w_in': nrm(ks[2], (DEPTH, D_MODEL, IN_WIDTH), f32) * D_MODEL ** -0.5,
        'b_gate': 0.1 * nrm(ks[3], (DEPTH, N_BRANCH * D_MODEL), f32),
        'conv_dw': nrm(ks[4], (DEPTH, CONV_KERNEL, CONV_WIDTH), f32) * CONV_KERNEL ** -0.5,
        'conv_b': 0.02 * nrm(ks[5], (DEPTH, CONV_WIDTH), f32),
        'conv_ln_g': 1.0 + 0.02 * nrm(ks[6], (DEPTH, CONV_WIDTH), f32),
        'conv_ln_b': 0.02 * nrm(ks[7], (DEPTH, CONV_WIDTH), f32),
        'ret_decay': ret_base + 0.05 * nrm(ks[8], (DEPTH, 2, RET_HEADS), f32),
        'q_norm_g': 1.0 + 0.02 * nrm(ks[9], (DEPTH, HEAD_DIM), f32),
        'k_norm_g': 1.0 + 0.02 * nrm(ks[10], (DEPTH, HEAD_DIM), f32),
        'attn_sink': 0.5 * nrm(ks[11], (DEPTH, ATTN_HEADS), f32),
        'w_conv_out': nrm(ks[12], (DEPTH, CONV_WIDTH, D_MODEL), f32) * CONV_WIDTH ** -0.5,
        'w_ret_out': nrm(ks[13], (DEPTH, RET_HEADS * RET_V_DIM, D_MODEL), f32) * (RET_HEADS * RET_V_DIM) ** -0.5,
        'w_attn_out': nrm(ks[14], (DEPTH, ATTN_HEADS * HEAD_DIM, D_MODEL), f32) * (ATTN_HEADS * HEAD_DIM) ** -0.5,
        'w_out': nrm(ks[15], (DEPTH, D_MODEL, D_MODEL), f32) * D_MODEL ** -0.5,
    }


def reference(x, norm_g, w_in, b_gate, conv_dw, conv_b, conv_ln_g, conv_ln_b, ret_decay,
              q_norm_g, k_norm_g, attn_sink, w_conv_out, w_ret_out, w_attn_out, w_out):
    positions = jnp.arange(x.shape[1], dtype=jnp.int32)
    for l in range(DEPTH):
        x = hybrid_layer(x, positions, norm_g[l], w_in[l], b_gate[l], conv_dw[l], conv_b[l],
                         conv_ln_g[l], conv_ln_b[l], ret_decay[l], q_norm_g[l], k_norm_g[l],
                         attn_sink[l], w_conv_out[l], w_ret_out[l], w_attn_out[l], w_out[l])
    return x
```

```python
import os
import numpy as np
import concourse.bass as bass
import concourse.mybir as mybir
from concourse.bass_utils import run_bass_kernel_spmd
from contextlib import ExitStack

F32 = mybir.dt.float32
BF16 = mybir.dt.bfloat16
ALU = mybir.AluOpType
AF = mybir.ActivationFunctionType
AX = mybir.AxisListType

ENG = ['tensor', 'vector', 'scalar', 'gpsimd', 'sync']
EPOCH = 20000
ND = 8
PE, V, A, G, S = 'tensor', 'vector', 'scalar', 'gpsimd', 'sync'


class U:
    __slots__ = ('w', 'rs')

    def __init__(s):
        s.w = None
        s.rs = {}


class KB:
    def __init__(s, nc, stack):
        s.nc = nc
        s.st = stack
        s.cnt = {e: 0 for e in ENG}
        s.nsem = 0
        s.sem = {e: s.new_sem(f'e_{e}') for e in ENG}
        s.hist = {e: [] for e in ENG}
        s.waited = {e: {} for e in ENG}
        s.dsem = {}
        s.dtarget = {}
        s.dcount = {}
        s.n_inst = 0

    def new_sem(s, name):
        s.nsem += 1
        return s.st.enter_context(s.nc.semaphore(f'{name}_{s.nsem}'))

    def _waits(s, engine, r, w):
        deps = {}

        def add(tok):
            key = id(tok[0])
            if key not in deps or deps[key][1] < tok[1]:
                deps[key] = tok
        for u in r:
            if u.w is not None:
                add(u.w)
        for u in w:
            if u.w is not None:
                add(u.w)
            for tok in u.rs.values():
                add(tok)
        waits = []
        wd = s.waited[engine]
        for key, (sem, val, src) in deps.items():
            if engine == PE and src == PE:
                continue
            if wd.get(key, 0) >= val:
                continue
            wd[key] = val
            waits.append((sem, val))
        return waits

    def _emit(s, ename, waits, fn, inc):
        e = getattr(s.nc, ename)
        for sem, val in waits:
            e.wait_ge(sem, val)
        if fn is None:
            return
        ins = fn(e)
        if inc[1] is None:
            ins.then_inc(inc[0])
        else:
            ins.then_inc(inc[0], inc[1])
        s.n_inst += 1

    def op(s, engine, fn, r=(), w=()):
        waits = s._waits(engine, r, w)
        if s.cnt[engine] >= EPOCH:
            s.hist[engine].append((s.sem[engine], s.cnt[engine]))
            s.sem[engine] = s.new_sem(f'e_{engine}')
            s.cnt[engine] = 0
        s.cnt[engine] += 1
        sem = s.sem[engine]
        tok = (sem, s.cnt[engine], engine)
        s._emit(engine, waits, fn, (sem, 1))
        for u in r:
            u.rs[id(sem)] = tok
        for u in w:
            u.w = tok
            u.rs = {}
        return tok

    def dma(s, q, out, in_, r=(), w=(), **kw):
        waits = s._waits(q, r, w)
        if q not in s.dsem:
            s.dsem[q] = [s.new_sem(f'd_{q}{i}') for i in range(ND)]
            s.dtarget[q] = [0] * ND
            s.dcount[q] = 0
        i = s.dcount[q] % ND
        s.dcount[q] += 1
        sem = s.dsem[q][i]
        prev = s.dtarget[q][i]
        if prev > 0 and s.waited[q].get(id(sem), 0) < prev:
            s.waited[q][id(sem)] = prev
            waits.append((sem, prev))
        tgt = prev + 16
        s.dtarget[q][i] = tgt
        tok = (sem, tgt, 'dma')
        s._emit(q, waits, lambda e: e.dma_start(out=out, in_=in_, **kw), (sem, 16))
        for u in r:
            u.rs[id(sem)] = tok
        for u in w:
            u.w = tok
            u.rs = {}
        return tok

    def custom(s, engine, fn, inc_sem, inc_val, r=(), w=()):
        waits = s._waits(engine, r, w)
        tok = (inc_sem, inc_val, 'custom')
        s._emit(engine, waits, fn, (inc_sem, None))
        for u in r:
            u.rs[id(inc_sem)] = tok
        for u in w:
            u.w = tok
            u.rs = {}
        return tok

    def barrier(s):
        toks = []
        for e in ENG:
            for sem, c in s.hist[e]:
                toks.append((sem, c))
            if s.cnt[e] > 0:
                toks.append((s.sem[e], s.cnt[e]))
        for q in s.dsem:
            for i in range(ND):
                if s.dtarget[q][i] > 0:
                    toks.append((s.dsem[q][i], s.dtarget[q][i]))
        for e in ENG:
            wd = s.waited[e]
            waits = []
            for sem, val in toks:
                if wd.get(id(sem), 0) >= val:
                    continue
                wd[id(sem)] = val
                waits.append((sem, val))
            s._emit(e, waits, None, None)


D = 1024
T = 2048
TE = 2304
NT = 16
INW = 14848
OFF_CGLU, OFF_CGATE = 0, 2048
OFF_RQ, OFF_RK, OFF_RV, OFF_RG = 3072, 4096, 5120, 7168
OFF_AQ, OFF_AK, OFF_AV, OFF_AG = 9216, 10240, 10496, 10752
OFF_GL = 11776
EPS = 1e-6


def build(NL=2, dbg=False, phases='CTRF'):
    nc = bass.Bass("TRN2", target_bir_lowering=False)

    def din(name, shape):
        return nc.dram_tensor(name, shape, F32, kind="ExternalInput").ap()

    x_ext = din("x_ext", [TE, D])
    norm_g = din("norm_g", [2, D])
    w_in = din("w_in", [2, D, INW])
    b_gate = din("b_gate", [2, 3, D])
    conv_dw = din("conv_dw", [2, 31, D])
    conv_b = din("conv_b", [2, 1, D])
    conv_ln_g = din("conv_ln_g", [2, 1, D])
    conv_ln_b = din("conv_ln_b", [2, 1, D])
    ret_decay = din("ret_decay", [2, 8])
    q_norm_g = din("q_norm_g", [2, 64])
    k_norm_g = din("k_norm_g", [2, 64])
    attn_sink = din("attn_sink", [2, 16])
    w_conv_out = din("w_conv_out", [2, D, D])
    w_ret_out = din("w_ret_out", [2, 2 * D, D])
    w_attn_out = din("w_attn_out", [2, D, D])
    w_out = din("w_out", [2, D, D])
    c_cosr = din("c_cosr", [128, T])
    c_sinr = din("c_sinr", [128, T])
    c_cosa = din("c_cosa", [128, 18, 8])
    c_sina = din("c_sina", [128, 18, 8])
    c_mask = din("c_mask", [128, 512])
    c_ident = din("c_ident", [128, 128])
    c_eseg = din("c_eseg", [128, 32])
    c_ek = din("c_ek", [128, 2])
    c_eq = din("c_eq", [128, 256])
    c_em = din("c_em", [128, 512])
    c_dist = din("c_dist", [128, 8])
    c_sel = din("c_sel", [128, 8])

    y_out = nc.dram_tensor("y_out", [T, D], F32, kind="ExternalOutput").ap()
    okind = "ExternalOutput" if dbg else "Internal"
    OC = nc.dram_tensor("OC", [8, 128, T], BF16, kind=okind).ap()
    OA = nc.dram_tensor("OA", [16, 64, T], BF16, kind=okind).ap()
    OR = nc.dram_tensor("OR", [16, 128, T], BF16, kind=okind).ap()
    x1e = nc.dram_tensor("x1e", [TE, D], F32, kind=okind).ap()
    fs_bounce = nc.dram_tensor("fs_bounce", [512, 512], F32)
    fs_gath = nc.dram_tensor("fs_gath", [2048, 512], F32)
    h_bounce = nc.dram_tensor("h_bounce", [256, D], F32)
    h_gath = nc.dram_tensor("h_gath", [1024, D], F32)
    u_OC, u_OA, u_OR, u_x1e, u_fsb, u_fsg, u_hb, u_hg, u_yout = [U() for _ in range(9)]
    RG = [[0, 1, 2, 3], [4, 5, 6, 7]]

    with ExitStack() as st0:
        k = KB(nc, st0)

        ncount = [0]

        def sb(st, name, shape, dt):
            ncount[0] += 1
            return st.enter_context(nc.sbuf_tensor(f"{name}_{ncount[0]}", shape, dt))

        PF = [st0.enter_context(nc.psum_tensor(f"pf{i}", [128, 512], F32)) for i in range(7)]
        uPF = [U() for _ in range(7)]
        PT = st0.enter_context(nc.psum_tensor("ptb", [128, 8, 128], BF16))
        uPT = U()
        hT = sb(st0, "hT", [128, 8, TE], BF16)
        uhT = [U() for _ in range(18)]
        ident = sb(st0, "ident", [128, 128], F32); identb = sb(st0, "identb", [128, 128], BF16)
        onesf = sb(st0, "onesf", [128, 128], F32); onesb = sb(st0, "onesb", [128, 128], BF16)
        maskb = sb(st0, "maskb", [128, 512], BF16)
        eseg = sb(st0, "eseg", [128, 32], F32); ek = sb(st0, "ek", [128, 2], F32)
        eq = sb(st0, "eq", [128, 256], F32); em = sb(st0, "em", [128, 512], F32)
        dist = sb(st0, "dist", [128, 8], F32); sel = sb(st0, "sel", [128, 8], F32)
        cosa = sb(st0, "cosa", [128, 18, 8], F32); sina = sb(st0, "sina", [128, 18, 8], F32)
        lg = sb(st0, "lg", [128, 8], F32)
        qg = sb(st0, "qg", [128, 64], F32); kg = sb(st0, "kg", [128, 64], F32)
        snk = sb(st0, "snk", [128, 16], F32); snke = sb(st0, "snke", [128, 16], F32)
        negc = sb(st0, "negc", [128, 1], F32); tmpc = sb(st0, "tmpc", [128, 4], F32)
        g2 = sb(st0, "g2", [128, 128], F32)
        prm = sb(st0, "prm", [37, D], F32)
        prT = sb(st0, "prT", [128, 8, 37], F32)
        uC = U()
        uL = U()

        for (t, src) in ((ident, c_ident), (eseg, c_eseg), (ek, c_ek), (eq, c_eq), (em, c_em),
                         (dist, c_dist), (sel, c_sel), (cosa, c_cosa), (sina, c_sina)):
            k.dma(S, t[:], src, w=[uC])
        k.dma(G, identb[:], c_ident, w=[uC])
        k.dma(G, maskb[:], c_mask, w=[uC])
        k.op(V, lambda e: e.memset(onesf[:], 1.0 / 1024.0), w=[uC])
        k.op(V, lambda e: e.memset(onesb[:], 1.0), w=[uC])
        k.barrier()

        def phase_T(l):
            with ExitStack() as st:
                Wt = sb(st, "Wt", [128, 8, 2560], BF16); uWt = U()
                for j in range(5):
                    k.dma(G, Wt[:, :, j * 512:(j + 1) * 512], w_in[l, :, OFF_AQ + j * 512:OFF_AQ + (j + 1) * 512].rearrange("(k p) c -> p k c", p=128), w=[uWt])
                qTg = sb(st, "qTg", [64, 16, 512], BF16); uqT = [U() for _ in range(4)]
                kT = sb(st, "kTres", [64, 4, TE], BF16); ukT = [U() for _ in range(18)]
                Vr = sb(st, "Vres", [128, 18, 256], BF16); uVr = [U() for _ in range(18)]
                sq = sb(st, "sq", [128, 1280], F32); usq = U()
                ssa = sb(st, "ssa", [128, 20], F32); rsa = sb(st, "rsa", [128, 20], F32); urs = U()
                qn = sb(st, "qn", [128, 1280], F32); uqn = U()
                rt = [sb(st, f"rt{i}", [128, 20, 8], F32) for i in range(4)]; urt = [U() for _ in range(4)]
                qb = sb(st, "qb", [128, 1024], BF16); uqb = U()
                kb = sb(st, "kb", [128, 256], BF16); ukb = U()
                sq3 = sq[:].rearrange("p (h d) -> p h d", d=64)
                qn3 = qn[:].rearrange("p (h d) -> p h d", d=64)

                def norm_rot(t, h0, h1):
                    nh = h1 - h0
                    k.op(V, lambda e: e.tensor_reduce(out=ssa[:, h0:h1], in_=sq3[:, h0:h1, :], axis=AX.X, op=ALU.add), r=[usq], w=[urs])
                    k.op(V, lambda e: e.tensor_scalar(out=rsa[:, h0:h1], in0=ssa[:, h0:h1], scalar1=1.0 / 64.0, scalar2=EPS, op0=ALU.mult, op1=ALU.add), r=[urs], w=[urs])
                    k.op(A, lambda e: e.activation(out=rsa[:, h0:h1], in_=rsa[:, h0:h1], func=AF.Sqrt), r=[urs], w=[urs])
                    k.op(V, lambda e: e.reciprocal(out=rsa[:, h0:h1], in_=rsa[:, h0:h1]), r=[urs], w=[urs])
                    if h0 == 0:
                        for half in range(2):
                            k.op(V, lambda e, half=half: e.tensor_tensor(out=qn3[:, half * 8:(half + 1) * 8, :], in0=PF[half][:].rearrange("p (h d) -> p h d", d=64), in1=rsa[:, half * 8:(half + 1) * 8].unsqueeze(2).broadcast_to([128, 8, 64]), op=ALU.mult), r=[uPF[half], urs], w=[uqn])
                        k.op(G, lambda e: e.tensor_tensor(out=qn3[:, 0:16, :], in0=qn3[:, 0:16, :], in1=qg[:].unsqueeze(1).broadcast_to([128, 16, 64]), op=ALU.mult), r=[uqn, uL], w=[uqn])
                    else:
                        k.op(V, lambda e: e.tensor_tensor(out=qn3[:, 16:20, :], in0=PF[2][:, 0:256].rearrange("p (h d) -> p h d", d=64), in1=rsa[:, 16:20].unsqueeze(2).broadcast_to([128, 4, 64]), op=ALU.mult), r=[uPF[2], urs], w=[uqn])
                        k.op(G, lambda e: e.tensor_tensor(out=qn3[:, 16:20, :], in0=qn3[:, 16:20, :], in1=kg[:].unsqueeze(1).broadcast_to([128, 4, 64]), op=ALU.mult), r=[uqn, uL], w=[uqn])
                    cb = cosa[:, t, :].unsqueeze(1).broadcast_to([128, nh, 8])
                    sbb = sina[:, t, :].unsqueeze(1).broadcast_to([128, nh, 8])
                    x1 = qn3[:, h0:h1, 0:8]
                    x2 = qn3[:, h0:h1, 8:16]
                    k.op(V, lambda e: e.tensor_tensor(out=rt[0][:, h0:h1, :], in0=x1, in1=cb, op=ALU.mult), r=[uqn, uC], w=[urt[0]])
                    k.op(G, lambda e: e.tensor_tensor(out=rt[1][:, h0:h1, :], in0=x2, in1=sbb, op=ALU.mult), r=[uqn, uC], w=[urt[1]])
                    k.op(V, lambda e: e.tensor_tensor(out=rt[2][:, h0:h1, :], in0=x2, in1=cb, op=ALU.mult), r=[uqn, uC], w=[urt[2]])
                    k.op(G, lambda e: e.tensor_tensor(out=rt[3][:, h0:h1, :], in0=x1, in1=sbb, op=ALU.mult), r=[uqn, uC], w=[urt[3]])
                    k.op(V, lambda e: e.tensor_tensor(out=x1, in0=rt[0][:, h0:h1, :], in1=rt[1][:, h0:h1, :], op=ALU.subtract), r=[urt[0], urt[1], urt[2], urt[3]], w=[uqn])
                    k.op(G, lambda e: e.tensor_tensor(out=x2, in0=rt[2][:, h0:h1, :], in1=rt[3][:, h0:h1, :], op=ALU.add), r=[urt[2], urt[3]], w=[uqn])

                for t in range(18):
                    for kk in range(8):
                        k.op(PE, lambda e, kk=kk, t=t: e.matmul(PF[2][:], lhsT=hT[:, kk, t * 128:(t + 1) * 128], rhs=Wt[:, kk, 1024:1536], start=(kk == 0), stop=(kk == 7)), r=[uWt, uhT[t]], w=[uPF[2]])
                    k.op(A, lambda e: e.activation(out=sq[:, 1024:1280], in_=PF[2][:, 0:256], func=AF.Square), r=[uPF[2]], w=[usq])
                    norm_rot(t, 16, 20)
                    k.op(A, lambda e: e.activation(out=kb[:], in_=qn[:, 1024:1280], func=AF.Copy), r=[uqn], w=[ukb])
                    k.op(A, lambda e, t=t: e.activation(out=Vr[:, t, :], in_=PF[2][:, 256:512], func=AF.Copy), r=[uPF[2]], w=[uVr[t]])
                    for g in range(4):
                        k.op(PE, lambda e, g=g: e.transpose(out=PT[0:64, g, :], in_=kb[:, g * 64:(g + 1) * 64], identity=identb[:]), r=[ukb, uC], w=[uPT])
                    k.op(V, lambda e, t=t: e.tensor_copy(out=kT[:, :, t * 128:(t + 1) * 128], in_=PT[0:64, 0:4, :]), r=[uPT], w=[ukT[t]])
                if dbg == 'T1':
                    k.barrier()
                    return
                gT = sb(st, "gT", [64, 16, 512], BF16); ugT = U()
                pt = [sb(st, f"pt{i}", [128, 512], BF16) for i in range(3)]; upt = [U() for _ in range(3)]
                den = sb(st, "den", [64, 512], F32); uden = U()
                rec = sb(st, "rec", [64, 512], F32); urec = U()
                on = sb(st, "on", [64, 512], F32); uon = U()
                oaT = sb(st, "oaT", [64, 16, 512], BF16); uoaT = U()
                for tg in range(4):
                    e0 = 128 + tg * 512
                    hr = uhT[e0 // 128:(e0 + 512) // 128]
                    for nn in range(4):
                        t = tg * 4 + nn + 1
                        for half in range(2):
                            for kk in range(8):
                                k.op(PE, lambda e, half=half, kk=kk, t=t: e.matmul(PF[half][:], lhsT=hT[:, kk, t * 128:(t + 1) * 128], rhs=Wt[:, kk, half * 512:(half + 1) * 512], start=(kk == 0), stop=(kk == 7)), r=[uWt, uhT[t]], w=[uPF[half]])
                            k.op(A, lambda e, half=half: e.activation(out=sq[:, half * 512:(half + 1) * 512], in_=PF[half][:], func=AF.Square), r=[uPF[half]], w=[usq])
                        norm_rot(t, 0, 16)
                        k.op(A, lambda e: e.activation(out=qb[:], in_=qn[:, 0:1024], func=AF.Copy), r=[uqn], w=[uqb])
                        for rr in range(2):
                            for j in range(8):
                                hh = rr * 8 + j
                                k.op(PE, lambda e, j=j, hh=hh: e.transpose(out=PT[0:64, j, :], in_=qb[:, hh * 64:(hh + 1) * 64], identity=identb[:]), r=[uqb, uC], w=[uPT])
                            k.op(V, lambda e, rr=rr, nn=nn: e.tensor_copy(out=qTg[:, rr * 8:(rr + 1) * 8, nn * 128:(nn + 1) * 128], in_=PT[0:64, :, :]), r=[uPT], w=[uqT[nn]])
                    for hh in range(16):
                        ps, ups = PF[hh % 2], uPF[hh % 2]
                        for kk in range(8):
                            k.op(PE, lambda e, ps=ps, hh=hh, kk=kk, e0=e0: e.matmul(ps[0:64, :], lhsT=Wt[:, kk, 1536 + hh * 64:1536 + (hh + 1) * 64], rhs=hT[:, kk, e0:e0 + 512], start=(kk == 0), stop=(kk == 7)), r=[uWt] + hr, w=[ups])
                        k.op(A, lambda e, ps=ps, hh=hh: e.activation(out=gT[:, hh, :], in_=ps[0:64, :], func=AF.Silu), r=[ups], w=[ugT])
                    for nn in range(4):
                        n = tg * 4 + nn
                        t = n + 1
                        for g in range(4):
                            for mi, m in enumerate((t - 1, t, t + 1)):
                                k.op(PE, lambda e, mi=mi, m=m, g=g, nn=nn: e.matmul(PF[2 + mi][:], lhsT=kT[:, g, m * 128:(m + 1) * 128], rhs=qTg[:, 4 * g:4 * g + 4, nn * 128:(nn + 1) * 128], start=True, stop=True), r=[ukT[m], uqT[nn]], w=[uPF[2 + mi]])
                                k.op(A, lambda e, mi=mi: e.activation(out=pt[mi][:], in_=PF[2 + mi][:], func=AF.Exp, bias=negc[:, 0:1], scale=0.125), r=[uPF[2 + mi], uL], w=[upt[mi]])
                            mL = 256 if t == 1 else 0
                            mR = 384 if t == 16 else 128
                            k.op(V, lambda e, mL=mL: e.tensor_tensor(out=pt[0][:].rearrange("p (a i) -> p a i", a=4), in0=pt[0][:].rearrange("p (a i) -> p a i", a=4), in1=maskb[:, mL:mL + 128].unsqueeze(1).broadcast_to([128, 4, 128]), op=ALU.mult), r=[upt[0], uC], w=[upt[0]])
                            k.op(G, lambda e, mR=mR: e.tensor_tensor(out=pt[2][:].rearrange("p (a i) -> p a i", a=4), in0=pt[2][:].rearrange("p (a i) -> p a i", a=4), in1=maskb[:, mR:mR + 128].unsqueeze(1).broadcast_to([128, 4, 128]), op=ALU.mult), r=[upt[2], uC], w=[upt[2]])
                            for mi, m in enumerate((t - 1, t, t + 1)):
                                k.op(PE, lambda e, mi=mi, m=m, g=g: e.matmul(PF[5][0:64, :], lhsT=Vr[:, m, g * 64:(g + 1) * 64], rhs=pt[mi][:], start=(mi == 0), stop=(mi == 2)), r=[uVr[m], upt[mi]], w=[uPF[5]])
                            for mi, m in enumerate((t - 1, t, t + 1)):
                                k.op(PE, lambda e, mi=mi: e.matmul(PF[6][0:64, :], lhsT=onesb[:, 0:64], rhs=pt[mi][:], start=(mi == 0), stop=(mi == 2)), r=[uC, upt[mi]], w=[uPF[6]])
                            for ei in range(4):
                                hd = 4 * g + ei
                                k.op(V, lambda e, ei=ei, hd=hd: e.tensor_scalar_add(out=den[:, ei * 128:(ei + 1) * 128], in0=PF[6][0:64, ei * 128:(ei + 1) * 128], scalar1=snke[0:64, hd:hd + 1]), r=[uPF[6], uL], w=[uden])
                            k.op(V, lambda e: e.reciprocal(out=rec[:], in_=den[:]), r=[uden], w=[urec])
                            k.op(V, lambda e: e.tensor_tensor(out=on[:], in0=PF[5][0:64, :], in1=rec[:], op=ALU.mult), r=[uPF[5], urec], w=[uon])
                            k.op(G, lambda e, g=g, nn=nn: e.tensor_tensor(out=oaT[:, 4 * g:4 * g + 4, nn * 128:(nn + 1) * 128], in0=on[:].rearrange("p (a i) -> p a i", a=4), in1=gT[:, 4 * g:4 * g + 4, nn * 128:(nn + 1) * 128], op=ALU.mult), r=[uon, ugT], w=[uoaT])
                    k.dma(S, OA[:, :, tg * 512:(tg + 1) * 512].rearrange("h d t -> d h t"), oaT[:], r=[uoaT], w=[u_OA])
                k.barrier()

        def phase_R(l):
            with ExitStack() as st:
                Wr = sb(st, "Wr", [128, 8, 2048], BF16); uWr = U()
                qT = sb(st, "rqT", [128, 2, T], BF16); uq = [U() for _ in range(4)]
                kT = sb(st, "rkT", [128, 2, T], BF16); ukk = [U() for _ in range(4)]
                Vv = sb(st, "rV", [128, 16, 512], BF16); uV = [U() for _ in range(16)]
                kt = sb(st, "rkt", [128, 16, 256], BF16); ukt = [U() for _ in range(16)]
                Pp = sb(st, "rP", [128, 16, 512], F32); uP = [U() for _ in range(16)]
                cs = sb(st, "rcs", [128, 2, 512], F32); ucs = U()
                tq = [sb(st, f"rtq{i}", [128, 512], F32) for i in range(4)]; utq = [U() for _ in range(4)]
                Sf = sb(st, "Sf", [128, 2, 512], F32); Sb = sb(st, "Sb", [128, 2, 512], F32); uSf = U(); uSb = U()
                Sfb = sb(st, "Sfb", [128, 2, 512], BF16); Sbb = sb(st, "Sbb", [128, 2, 512], BF16); uSfb = U(); uSbb = U()
                fsum = sb(st, "fsum", [128, 4, 512], F32); ufs = U()
                DTm = sb(st, "DTm", [128, 128], F32); dq = sb(st, "dq", [128, 256], F32); dsg = sb(st, "dsg", [128, 32], F32)
                dk = sb(st, "dk", [128, 2], F32); gC = sb(st, "gC", [128, 2], F32); wr = sb(st, "wr", [128, 8], F32)
                tmpD = sb(st, "tmpD", [128, 256], F32); uH = U()
                kfs = sb(st, "kfs", [128, 256], BF16); kbs = sb(st, "kbs", [128, 256], BF16); ukfs = U(); ukbs = U()
                qd = sb(st, "qd", [128, 2, 128], BF16); uqd = U()
                kdc = sb(st, "kdc", [128, 256], BF16); ukdc = U()
                sd = sb(st, "sd", [128, 128], BF16); usd = U()
                oo = sb(st, "oo", [128, 512], F32); uoo = U()
                onn = sb(st, "onn", [128, 512], F32); uonn = U()
                sgg = sb(st, "sgg", [128, 512], F32); usgg = U()
                og = sb(st, "og", [128, 512], BF16); uog = U()
                bst = sb(st, "bst", [128, 6], F32); mv = sb(st, "mv", [128, 2], F32); ubn = U()
                orT = sb(st, "orT", [128, 4, 512], BF16); uorT = U()
                RSTOP = int(os.environ.get("RSTOP", "0"))

                class _Stop(Exception):
                    pass

                def ck(n_):
                    if RSTOP == n_:
                        raise _Stop()
                try:
                  for h in range(4):
                      wl = [(1024, OFF_RV + h * 512), (1536, OFF_RG + h * 512)]
                      if h % 2 == 0:
                          wl = [(0, OFF_RQ + h * 256), (512, OFF_RK + h * 256)] + wl
                      for (dst, off) in wl:
                          k.dma(G, Wr[:, :, dst:dst + 512], w_in[l, :, off:off + 512].rearrange("(k p) c -> p k c", p=128), w=[uWr])
                      qc0 = (h % 2) * 256
                      kc0 = 512 + (h % 2) * 256
                      lgf = lg[:, h:h + 1]
                      lgb = lg[:, 4 + h:5 + h]
                      hc = dict(r=[uL, uC, uH], w=[uH])
                      k.op(A, lambda e: e.activation(out=dsg[:, 0:16], in_=eseg[:, 0:16], func=AF.Exp, scale=lgf), **hc)
                      k.op(A, lambda e: e.activation(out=dsg[:, 16:32], in_=eseg[:, 16:32], func=AF.Exp, scale=lgb), **hc)
                      k.op(A, lambda e: e.activation(out=dk[:, 0:1], in_=ek[:, 0:1], func=AF.Exp, scale=lgf), **hc)
                      k.op(A, lambda e: e.activation(out=dk[:, 1:2], in_=ek[:, 1:2], func=AF.Exp, scale=lgb), **hc)
                      k.op(A, lambda e: e.activation(out=dq[:, 0:128], in_=eq[:, 0:128], func=AF.Exp, scale=lgf), **hc)
                      k.op(A, lambda e: e.activation(out=dq[:, 128:256], in_=eq[:, 128:256], func=AF.Exp, scale=lgb), **hc)
                      k.op(V, lambda e: e.tensor_scalar_mul(out=dq[:], in0=dq[:], scalar1=1.0 / 16.0), **hc)
                      k.op(A, lambda e: e.activation(out=gC[:, 0:1], in_=lgf, func=AF.Exp, scale=128.0), **hc)
                      k.op(A, lambda e: e.activation(out=gC[:, 1:2], in_=lgb, func=AF.Exp, scale=128.0), **hc)
                      k.op(A, lambda e: e.activation(out=wr[:, 0:4], in_=dist[:, 0:4], func=AF.Exp, scale=lgf), **hc)
                      k.op(A, lambda e: e.activation(out=wr[:, 4:8], in_=dist[:, 4:8], func=AF.Exp, scale=lgb), **hc)
                      k.op(A, lambda e: e.activation(out=tmpD[:, 0:128], in_=em[:, 0:128], func=AF.Exp, scale=lgf), **hc)
                      k.op(A, lambda e: e.activation(out=tmpD[:, 128:256], in_=em[:, 128:256], func=AF.Exp, scale=lgb), **hc)
                      k.op(V, lambda e: e.tensor_tensor(out=tmpD[:], in0=tmpD[:], in1=em[:, 256:512], op=ALU.mult), **hc)
                      k.op(V, lambda e: e.tensor_tensor(out=DTm[:], in0=tmpD[:, 0:128], in1=tmpD[:, 128:256], op=ALU.add), **hc)
                      k.op(V, lambda e: e.tensor_scalar_mul(out=DTm[:], in0=DTm[:], scalar1=1.0 / 16.0), **hc)
                      ck(1)
                      for tg in range(4):
                          e0 = 128 + tg * 512
                          hr = uhT[e0 // 128:(e0 + 512) // 128]
                          k.dma(S, cs[:, 0, :], c_cosr[:, tg * 512:(tg + 1) * 512], w=[ucs])
                          k.dma(S, cs[:, 1, :], c_sinr[:, tg * 512:(tg + 1) * 512], w=[ucs])
                          for (col0, dstT, ud) in ((qc0, qT, uq[tg]), (kc0, kT, ukk[tg])):
                              for dc in range(2):
                                  for kk in range(8):
                                      k.op(PE, lambda e, dc=dc, kk=kk, col0=col0, e0=e0: e.matmul(PF[dc][:], lhsT=Wr[:, kk, col0 + dc * 128:col0 + (dc + 1) * 128], rhs=hT[:, kk, e0:e0 + 512], start=(kk == 0), stop=(kk == 7)), r=[uWr] + hr, w=[uPF[dc]])
                              k.op(V, lambda e: e.tensor_tensor(out=tq[0][:], in0=PF[0][:], in1=cs[:, 0, :], op=ALU.mult), r=[uPF[0], ucs], w=[utq[0]])
                              k.op(V, lambda e: e.tensor_tensor(out=tq[1][:], in0=PF[1][:], in1=cs[:, 1, :], op=ALU.mult), r=[uPF[1], ucs], w=[utq[1]])
                              k.op(V, lambda e: e.tensor_tensor(out=tq[2][:], in0=PF[1][:], in1=cs[:, 0, :], op=ALU.mult), r=[uPF[1], ucs], w=[utq[2]])
                              k.op(V, lambda e: e.tensor_tensor(out=tq[3][:], in0=PF[0][:], in1=cs[:, 1, :], op=ALU.mult), r=[uPF[0], ucs], w=[utq[3]])
                              k.op(V, lambda e, dstT=dstT, tg=tg: e.tensor_tensor(out=dstT[:, 0, tg * 512:(tg + 1) * 512], in0=tq[0][:], in1=tq[1][:], op=ALU.subtract), r=[utq[0], utq[1]], w=[ud])
                              k.op(G, lambda e, dstT=dstT, tg=tg: e.tensor_tensor(out=dstT[:, 1, tg * 512:(tg + 1) * 512], in0=tq[2][:], in1=tq[3][:], op=ALU.add), r=[utq[2], utq[3]], w=[ud])
                          ck(2)
                          for nn in range(4):
                              n = tg * 4 + nn
                              t = n + 1
                              for kk in range(8):
                                  k.op(PE, lambda e, kk=kk, t=t: e.matmul(PF[2][:], lhsT=hT[:, kk, t * 128:(t + 1) * 128], rhs=Wr[:, kk, 1024:1536], start=(kk == 0), stop=(kk == 7)), r=[uWr, uhT[t]], w=[uPF[2]])
                              k.op(A, lambda e, n=n: e.activation(out=Vv[:, n, :], in_=PF[2][:], func=AF.Copy), r=[uPF[2]], w=[uV[n]])
                              ck(3)
                              for dc in range(2):
                                  k.op(PE, lambda e, dc=dc, n=n: e.transpose(out=PT[:, dc, :], in_=kT[:, dc, n * 128:(n + 1) * 128], identity=identb[:]), r=[ukk[tg], uC], w=[uPT])
                              ptv = PT[:, 0:2, :]
                              k.op(A, lambda e, n=n, ptv=ptv: e.activation(out=kfs[:].rearrange("p (a d) -> p a d", a=2), in_=ptv, func=AF.Copy, scale=dsg[:, n:n + 1]), r=[uPT, uH], w=[ukfs])
                              k.op(A, lambda e, n=n, ptv=ptv: e.activation(out=kbs[:].rearrange("p (a d) -> p a d", a=2), in_=ptv, func=AF.Copy, scale=dsg[:, 16 + n:17 + n]), r=[uPT, uH], w=[ukbs])
                              k.op(A, lambda e, n=n, ptv=ptv: e.activation(out=kt[:, n, :].rearrange("p (a d) -> p a d", a=2), in_=ptv, func=AF.Copy), r=[uPT], w=[ukt[n]])
                              ck(4)
                              for dc in range(2):
                                  k.op(PE, lambda e, dc=dc, n=n: e.matmul(PF[3 + dc][:], lhsT=kfs[:, dc * 128:(dc + 1) * 128], rhs=Vv[:, n, :], start=(n == 0), stop=(n == 15)), r=[ukfs, uV[n]], w=[uPF[3 + dc]])
                                  k.op(PE, lambda e, dc=dc, n=n: e.matmul(PF[5 + dc][:], lhsT=kbs[:, dc * 128:(dc + 1) * 128], rhs=Vv[:, n, :], start=(n == 0), stop=(n == 15)), r=[ukbs, uV[n]], w=[uPF[5 + dc]])
                      if dbg == 'R0':
                          k.barrier()
                          return
                      for j in range(4):
                          k.op(A, lambda e, j=j: e.activation(out=fsum[:, j, :], in_=PF[3 + j][:], func=AF.Copy), r=[uPF[3 + j]], w=[ufs])
                      k.dma(S, fs_bounce.ap().rearrange("(j p) v -> p j v", p=128), fsum[:], r=[ufs], w=[u_fsb])
                      ccs = k.new_sem("cc")
                      k.custom(G, lambda e: e.collective_compute("AllGather", ALU.bypass, replica_groups=RG, ins=[fs_bounce.ap().opt()], outs=[fs_gath.ap().opt()]), ccs, 1, r=[u_fsb], w=[u_fsg])
                      for r_ in range(4):
                          k.dma(S, fsum[:], fs_gath.ap()[r_ * 512:(r_ + 1) * 512, :].rearrange("(j p) v -> p j v", p=128), r=[u_fsg], w=[ufs])
                          for dirn, (Sx, uSx) in enumerate(((Sf, uSf), (Sb, uSb))):
                              for dc in range(2):
                                  wcol = wr[:, dirn * 4 + r_:dirn * 4 + r_ + 1]
                                  if r_ == 0:
                                      k.op(V, lambda e, Sx=Sx, dc=dc, dirn=dirn, wcol=wcol: e.tensor_scalar_mul(out=Sx[:, dc, :], in0=fsum[:, dirn * 2 + dc, :], scalar1=wcol), r=[ufs, uH], w=[uSx])
                                  else:
                                      k.op(V, lambda e, Sx=Sx, dc=dc, dirn=dirn, wcol=wcol: e.scalar_tensor_tensor(out=Sx[:, dc, :], in0=fsum[:, dirn * 2 + dc, :], scalar=wcol, in1=Sx[:, dc, :], op0=ALU.mult, op1=ALU.add), r=[ufs, uH, uSx], w=[uSx])
                      k.op(A, lambda e: e.activation(out=Sfb[:], in_=Sf[:], func=AF.Copy), r=[uSf], w=[uSfb])
                      k.op(A, lambda e: e.activation(out=Sbb[:], in_=Sb[:], func=AF.Copy), r=[uSb], w=[uSbb])
                      if dbg == 'RAG':
                          k.barrier()
                          return
                      for n in range(15, -1, -1):
                          tg = n // 4
                          k.op(V, lambda e, n=n: e.tensor_tensor(out=qd[:], in0=qT[:, :, n * 128:(n + 1) * 128], in1=dq[:, 128:256].unsqueeze(1).broadcast_to([128, 2, 128]), op=ALU.mult), r=[uq[tg], uH], w=[uqd])
                          for dc in range(2):
                              k.op(PE, lambda e, dc=dc: e.matmul(PF[0][:], lhsT=qd[:, dc, :], rhs=Sbb[:, dc, :], start=(dc == 0), stop=(dc == 1)), r=[uqd, uSbb], w=[uPF[0]])
                          k.op(A, lambda e, n=n: e.activation(out=Pp[:, n, :], in_=PF[0][:], func=AF.Copy), r=[uPF[0]], w=[uP[n]])
                          k.op(A, lambda e, n=n: e.activation(out=kdc[:], in_=kt[:, n, :], func=AF.Copy, scale=dk[:, 1:2]), r=[ukt[n], uH], w=[ukdc])
                          for dc in range(2):
                              k.op(PE, lambda e, dc=dc, n=n: e.matmul(PF[1 + dc][:], lhsT=kdc[:, dc * 128:(dc + 1) * 128], rhs=Vv[:, n, :], start=True, stop=True), r=[ukdc, uV[n]], w=[uPF[1 + dc]])
                              k.op(V, lambda e, dc=dc: e.scalar_tensor_tensor(out=Sb[:, dc, :], in0=Sb[:, dc, :], scalar=gC[:, 1:2], in1=PF[1 + dc][:], op0=ALU.mult, op1=ALU.add), r=[uPF[1 + dc], uH, uSb], w=[uSb])
                          k.op(A, lambda e: e.activation(out=Sbb[:], in_=Sb[:], func=AF.Copy), r=[uSb], w=[uSbb])
                      if dbg == 'R1':
                          k.barrier()
                          return
                      for n in range(16):
                          tg = n // 4
                          t = n + 1
                          for dc in range(2):
                              k.op(PE, lambda e, dc=dc, n=n: e.matmul(PF[3][:, 0:128], lhsT=kT[:, dc, n * 128:(n + 1) * 128], rhs=qT[:, dc, n * 128:(n + 1) * 128], start=(dc == 0), stop=(dc == 1)), r=[ukk[tg], uq[tg]], w=[uPF[3]])
                          k.op(V, lambda e: e.tensor_tensor(out=sd[:], in0=PF[3][:, 0:128], in1=DTm[:], op=ALU.mult), r=[uPF[3], uH], w=[usd])
                          k.op(G, lambda e, n=n: e.tensor_tensor(out=qd[:], in0=qT[:, :, n * 128:(n + 1) * 128], in1=dq[:, 0:128].unsqueeze(1).broadcast_to([128, 2, 128]), op=ALU.mult), r=[uq[tg], uH], w=[uqd])
                          k.op(PE, lambda e, n=n: e.matmul(PF[4][:], lhsT=sd[:], rhs=Vv[:, n, :], start=True, stop=False), r=[usd, uV[n]], w=[uPF[4]])
                          for dc in range(2):
                              k.op(PE, lambda e, dc=dc: e.matmul(PF[4][:], lhsT=qd[:, dc, :], rhs=Sfb[:, dc, :], start=False, stop=(dc == 1)), r=[uqd, uSfb], w=[uPF[4]])
                          k.op(V, lambda e, n=n: e.tensor_tensor(out=oo[:], in0=PF[4][:], in1=Pp[:, n, :], op=ALU.add), r=[uPF[4], uP[n]], w=[uoo])
                          k.op(V, lambda e: e.bn_stats(out=bst[:], in_=oo[:]), r=[uoo], w=[ubn])
                          k.op(V, lambda e: e.bn_aggr(out=mv[:], in_=bst[:]), r=[ubn], w=[ubn])
                          k.op(V, lambda e: e.tensor_scalar_add(out=mv[:, 1:2], in0=mv[:, 1:2], scalar1=EPS), r=[ubn], w=[ubn])
                          k.op(A, lambda e: e.activation(out=mv[:, 1:2], in_=mv[:, 1:2], func=AF.Sqrt), r=[ubn], w=[ubn])
                          k.op(V, lambda e: e.reciprocal(out=mv[:, 1:2], in_=mv[:, 1:2]), r=[ubn], w=[ubn])
                          k.op(V, lambda e: e.tensor_scalar(out=onn[:], in0=oo[:], scalar1=mv[:, 0:1], scalar2=mv[:, 1:2], op0=ALU.subtract, op1=ALU.mult), r=[uoo, ubn], w=[uonn])
                          for kk in range(8):
                              k.op(PE, lambda e, kk=kk, t=t: e.matmul(PF[5][:], lhsT=hT[:, kk, t * 128:(t + 1) * 128], rhs=Wr[:, kk, 1536:2048], start=(kk == 0), stop=(kk == 7)), r=[uWr, uhT[t]], w=[uPF[5]])
                          k.op(A, lambda e: e.activation(out=sgg[:], in_=PF[5][:], func=AF.Silu), r=[uPF[5]], w=[usgg])
                          k.op(G, lambda e: e.tensor_tensor(out=og[:], in0=onn[:], in1=sgg[:], op=ALU.mult), r=[uonn, usgg], w=[uog])
                          for vc in range(4):
                              k.op(PE, lambda e, vc=vc: e.transpose(out=PT[:, vc, :], in_=og[:, vc * 128:(vc + 1) * 128], identity=identb[:]), r=[uog, uC], w=[uPT])
                          k.op(A, lambda e, n=n: e.activation(out=orT[:, :, (n % 4) * 128:(n % 4 + 1) * 128], in_=PT[:, 0:4, :], func=AF.Copy), r=[uPT], w=[uorT])
                          if n % 4 == 3:
                              k.dma(S, OR[h * 4:(h + 1) * 4, :, tg * 512:(tg + 1) * 512].rearrange("c p t -> p c t"), orT[:], r=[uorT], w=[u_OR])
                          k.op(A, lambda e, n=n: e.activation(out=kdc[:], in_=kt[:, n, :], func=AF.Copy, scale=dk[:, 0:1]), r=[ukt[n], uH], w=[ukdc])
                          for dc in range(2):
                              k.op(PE, lambda e, dc=dc, n=n: e.matmul(PF[1 + dc][:], lhsT=kdc[:, dc * 128:(dc + 1) * 128], rhs=Vv[:, n, :], start=True, stop=True), r=[ukdc, uV[n]], w=[uPF[1 + dc]])
                              k.op(V, lambda e, dc=dc: e.scalar_tensor_tensor(out=Sf[:, dc, :], in0=Sf[:, dc, :], scalar=gC[:, 0:1], in1=PF[1 + dc][:], op0=ALU.mult, op1=ALU.add), r=[uPF[1 + dc], uH, uSf], w=[uSf])
                          k.op(A, lambda e: e.activation(out=Sfb[:], in_=Sf[:], func=AF.Copy), r=[uSf], w=[uSfb])

                except _Stop:
                    pass
                k.barrier()

        def phase_F(l, xsrc, last):
            with ExitStack() as st:
                mg = sb(st, "mg", [128, 8, T], F32); umg = [U() for _ in range(4)]
                sgm = sb(st, "sgm", [128, 512], F32); usgm = U()
                tmpm = sb(st, "tmpm", [128, 512], F32); utmpm = U()
                for bi, (nk, kp) in enumerate(((8, 128), (16, 128), (16, 64))):
                    with ExitStack() as st2:
                        Wb = sb(st2, f"Wb{bi}", [kp, nk, D], BF16); uWb = U()
                        Wg = sb(st2, f"Wg{bi}", [128, 8, D], BF16); uWg = U()
                        ob = sb(st2, f"ob{bi}", [kp, nk, 512], BF16); uob = U()
                        src_w = (w_conv_out, w_ret_out, w_attn_out)[bi]
                        if bi == 2:
                            view = src_w[l].rearrange("(h d) c -> d h c", d=64)
                        else:
                            view = src_w[l].rearrange("(k p) c -> p k c", p=128)
                        for j in range(0, nk, 4):
                            k.dma(G, Wb[:, j:j + 4, :], view[:, j:j + 4, :], w=[uWb])
                        for j in range(2):
                            c0 = OFF_GL + bi * 1024 + j * 512
                            k.dma(G, Wg[:, :, j * 512:(j + 1) * 512], w_in[l, :, c0:c0 + 512].rearrange("(k p) c -> p k c", p=128), w=[uWg])
                        osrc = (OC, OR, OA)[bi]
                        uos = (u_OC, u_OR, u_OA)[bi]
                        for tg in range(4):
                            e0 = 128 + tg * 512
                            hr = uhT[e0 // 128:(e0 + 512) // 128]
                            k.dma(S, ob[:], osrc[:, :, tg * 512:(tg + 1) * 512].rearrange("c p t -> p c t"), r=[uos], w=[uob])
                            for m in range(8):
                                py, upy = PF[m % 3], uPF[m % 3]
                                pg, upg = PF[3 + m % 3], uPF[3 + m % 3]
                                for j in range(nk):
                                    k.op(PE, lambda e, py=py, j=j, m=m: e.matmul(py[:], lhsT=Wb[:, j, m * 128:(m + 1) * 128], rhs=ob[:, j, :], start=(j == 0), stop=(j == nk - 1)), r=[uWb, uob], w=[upy])
                                for kk in range(8):
                                    k.op(PE, lambda e, pg=pg, kk=kk, m=m, e0=e0: e.matmul(pg[:], lhsT=Wg[:, kk, m * 128:(m + 1) * 128], rhs=hT[:, kk, e0:e0 + 512], start=(kk == 0), stop=(kk == 7)), r=[uWg] + hr, w=[upg])
                                k.op(A, lambda e, pg=pg, m=m, bi=bi: e.activation(out=sgm[:], in_=pg[:], func=AF.Sigmoid, bias=prT[:, m, 34 + bi:35 + bi], scale=1.0), r=[upg, uL], w=[usgm])
                                if bi == 0:
                                    k.op(V, lambda e, py=py, m=m, tg=tg: e.tensor_tensor(out=mg[:, m, tg * 512:(tg + 1) * 512], in0=py[:], in1=sgm[:], op=ALU.mult), r=[upy, usgm], w=[umg[tg]])
                                else:
                                    k.op(V, lambda e, py=py: e.tensor_tensor(out=tmpm[:], in0=py[:], in1=sgm[:], op=ALU.mult), r=[upy, usgm], w=[utmpm])
                                    k.op(G, lambda e, m=m, tg=tg: e.tensor_tensor(out=mg[:, m, tg * 512:(tg + 1) * 512], in0=mg[:, m, tg * 512:(tg + 1) * 512], in1=tmpm[:], op=ALU.add), r=[utmpm, umg[tg]], w=[umg[tg]])
                        k.barrier()
                with ExitStack() as st2:
                    Wo = sb(st2, "Wo", [128, 8, D], BF16); uWo = U()
                    for j in range(2):
                        k.dma(G, Wo[:, j * 4:(j + 1) * 4, :], w_out[l].rearrange("(k p) c -> p k c", p=128)[:, j * 4:(j + 1) * 4, :], w=[uWo])
                    mb = sb(st2, "mb", [128, 8, 128], BF16); umb = U()
                    xt = [sb(st2, f"fxt{i}", [128, D], F32) for i in range(2)]; uxt = [U(), U()]
                    xn = [sb(st2, f"fxn{i}", [128, D], F32) for i in range(2)]; uxn = [U(), U()]
                    for n in range(16):
                        b = n % 2
                        k.op(A, lambda e, n=n: e.activation(out=mb[:], in_=mg[:, :, n * 128:(n + 1) * 128], func=AF.Copy), r=[umg[n // 4]], w=[umb])
                        k.dma(S, xt[b][:], xsrc[128 + n * 128:128 + (n + 1) * 128, :], r=[u_x1e], w=[uxt[b]])
                        for half in range(2):
                            for kk in range(8):
                                k.op(PE, lambda e, half=half, kk=kk: e.matmul(PF[half][:], lhsT=mb[:, kk, :], rhs=Wo[:, kk, half * 512:(half + 1) * 512], start=(kk == 0), stop=(kk == 7)), r=[umb, uWo], w=[uPF[half]])
                            k.op(V, lambda e, half=half, b=b: e.tensor_tensor(out=xn[b][:, half * 512:(half + 1) * 512], in0=PF[half][:], in1=xt[b][:, half * 512:(half + 1) * 512], op=ALU.add), r=[uPF[half], uxt[b]], w=[uxn[b]])
                        if last:
                            k.dma(S, y_out[n * 128:(n + 1) * 128, :], xn[b][:], r=[uxn[b]], w=[u_yout])
                        else:
                            k.dma(S, x1e[128 + n * 128:128 + (n + 1) * 128, :], xn[b][:], r=[uxn[b]], w=[u_x1e])
                            if n == 0:
                                k.dma(S, h_bounce.ap()[0:128, :], xn[b][:], r=[uxn[b]], w=[u_hb])
                            if n == 15:
                                k.dma(S, h_bounce.ap()[128:256, :], xn[b][:], r=[uxn[b]], w=[u_hb])
                    k.barrier()
            if not last:
                ccs = k.new_sem("cch")
                k.custom(G, lambda e: e.collective_compute("AllGather", ALU.bypass, replica_groups=RG, ins=[h_bounce.ap().opt()], outs=[h_gath.ap().opt()]), ccs, 1, r=[u_hb], w=[u_hg])
                with ExitStack() as st2:
                    hb = sb(st2, "hb", [128, 2, D], F32); uhb = U()
                    accL = sb(st2, "accL", [128, D], F32); accR = sb(st2, "accR", [128, D], F32); uaL = U(); uaR = U()
                    for r_ in range(4):
                        k.dma(S, hb[:], h_gath.ap()[r_ * 256:(r_ + 1) * 256, :].rearrange("(a p) f -> p a f", p=128), r=[u_hg], w=[uhb])
                        for (acc, ua, a_, sc) in ((accL, uaL, 1, r_), (accR, uaR, 0, 4 + r_)):
                            if r_ == 0:
                                k.op(V, lambda e, acc=acc, a_=a_, sc=sc: e.tensor_scalar_mul(out=acc[:], in0=hb[:, a_, :], scalar1=sel[:, sc:sc + 1]), r=[uhb, uC], w=[ua])
                            else:
                                k.op(V, lambda e, acc=acc, a_=a_, sc=sc: e.scalar_tensor_tensor(out=acc[:], in0=hb[:, a_, :], scalar=sel[:, sc:sc + 1], in1=acc[:], op0=ALU.mult, op1=ALU.add), r=[uhb, uC, ua], w=[ua])
                    k.dma(S, x1e[0:128, :], accL[:], r=[uaL], w=[u_x1e])
                    k.dma(S, x1e[TE - 128:TE, :], accR[:], r=[uaR], w=[u_x1e])
                    k.barrier()

        for l in range(NL):
            xsrc = x_ext if l == 0 else x1e
            last = (l == NL - 1)
            k.dma(S, lg[:], ret_decay[l:l + 1, :].partition_broadcast(128), w=[uL])
            k.dma(S, qg[:], q_norm_g[l:l + 1, :].partition_broadcast(128), w=[uL])
            k.dma(S, kg[:], k_norm_g[l:l + 1, :].partition_broadcast(128), w=[uL])
            k.dma(S, snk[:], attn_sink[l:l + 1, :].partition_broadcast(128), w=[uL])
            k.dma(S, prm[0:31, :], conv_dw[l], w=[uL])
            k.dma(S, prm[31:32, :], conv_b[l], w=[uL])
            k.dma(S, prm[32:33, :], conv_ln_g[l], w=[uL])
            k.dma(S, prm[33:34, :], conv_ln_b[l], w=[uL])
            k.dma(S, prm[34:37, :], b_gate[l], w=[uL])
            k.op(A, lambda e: e.activation(out=lg[:], in_=lg[:], func=AF.Exp), r=[uL], w=[uL])
            k.op(V, lambda e: e.tensor_scalar_mul(out=lg[:], in0=lg[:], scalar1=-1.0), r=[uL], w=[uL])
            k.op(V, lambda e: e.tensor_tensor(out=g2[:, 0:64], in0=qg[:], in1=qg[:], op=ALU.mult), r=[uL], w=[uL])
            k.op(V, lambda e: e.tensor_tensor(out=g2[:, 64:128], in0=kg[:], in1=kg[:], op=ALU.mult), r=[uL], w=[uL])
            k.op(V, lambda e: e.tensor_reduce(out=tmpc[:, 0:1], in_=g2[:, 0:64], axis=AX.X, op=ALU.max), r=[uL], w=[uL])
            k.op(V, lambda e: e.tensor_reduce(out=tmpc[:, 1:2], in_=g2[:, 64:128], axis=AX.X, op=ALU.max), r=[uL], w=[uL])
            k.op(V, lambda e: e.tensor_tensor(out=tmpc[:, 2:3], in0=tmpc[:, 0:1], in1=tmpc[:, 1:2], op=ALU.mult), r=[uL], w=[uL])
            k.op(A, lambda e: e.activation(out=tmpc[:, 3:4], in_=tmpc[:, 2:3], func=AF.Sqrt), r=[uL], w=[uL])
            k.op(V, lambda e: e.tensor_scalar_mul(out=negc[:], in0=tmpc[:, 3:4], scalar1=-8.0), r=[uL], w=[uL])
            k.op(A, lambda e: e.activation(out=snke[:], in_=snk[:], func=AF.Exp, bias=negc[:, 0:1], scale=1.0), r=[uL], w=[uL])
            for c in range(8):
                k.op(PE, lambda e, c=c: e.transpose(out=PF[0][:, c * 37:(c + 1) * 37], in_=prm[:, c * 128:(c + 1) * 128], identity=ident[0:37, 0:37]), r=[uL, uC], w=[uPF[0]])
            k.op(V, lambda e: e.tensor_copy(out=prT[:].rearrange("p c j -> p (c j)"), in_=PF[0][:, 0:296]), r=[uPF[0]], w=[uL])
            k.barrier()

            with ExitStack() as st:
                g_bc = sb(st, "g_bc", [128, D], F32); ug = U()
                k.dma(S, g_bc[:], norm_g[l:l + 1, :].partition_broadcast(128), w=[ug])
                xt = [sb(st, f"xt{i}", [128, D], F32) for i in range(2)]; uxt = [U(), U()]
                junk = sb(st, "junk", [128, D], BF16); ujunk = U()
                xs = [sb(st, f"xs{i}", [128, D], BF16) for i in range(2)]; uxs = [U(), U()]
                ssq = [sb(st, f"ssq{i}", [128, 2], F32) for i in range(2)]; ussq = [U(), U()]
                for t in range(18):
                    b = t % 2
                    k.dma(S, xt[b][:], xsrc[t * 128:(t + 1) * 128, :], r=[u_x1e], w=[uxt[b]])
                    k.op(V, lambda e, b=b: e.memset(ssq[b][:], 0.0), w=[ussq[b]])
                    k.op(A, lambda e, b=b: e.activation(out=junk[:], in_=xt[b][:], func=AF.Square, accum_out=ssq[b][:, 0:1]), r=[uxt[b]], w=[ujunk, ussq[b]])
                    k.op(V, lambda e, b=b: e.tensor_scalar(out=ssq[b][:, 1:2], in0=ssq[b][:, 0:1], scalar1=1.0 / D, scalar2=EPS, op0=ALU.mult, op1=ALU.add), r=[ussq[b]], w=[ussq[b]])
                    k.op(A, lambda e, b=b: e.activation(out=ssq[b][:, 1:2], in_=ssq[b][:, 1:2], func=AF.Sqrt), r=[ussq[b]], w=[ussq[b]])
                    k.op(V, lambda e, b=b: e.reciprocal(out=ssq[b][:, 1:2], in_=ssq[b][:, 1:2]), r=[ussq[b]], w=[ussq[b]])
                    k.op(V, lambda e, b=b: e.scalar_tensor_tensor(out=xs[b][:], in0=xt[b][:], scalar=ssq[b][:, 1:2], in1=g_bc[:], op0=ALU.mult, op1=ALU.mult), r=[uxt[b], ussq[b], ug], w=[uxs[b]])
                    for c in range(8):
                        k.op(PE, lambda e, b=b, c=c: e.transpose(out=PT[:, c, :], in_=xs[b][:, c * 128:(c + 1) * 128], identity=identb[:]), r=[uxs[b], uC], w=[uPT])
                    k.op(A, lambda e, t=t: e.activation(out=hT[:, :, t * 128:(t + 1) * 128], in_=PT[:], func=AF.Copy), r=[uPT], w=[uhT[t]])
                k.barrier()

            if 'C' in phases:
                with ExitStack() as st:
                    Wc = sb(st, "Wc", [128, 8, 3072], BF16); uWc = U()
                    for j in range(6):
                        k.dma(G, Wc[:, :, j * 512:(j + 1) * 512], w_in[l, :, j * 512:(j + 1) * 512].rearrange("(k p) c -> p k c", p=128), w=[uWc])
                    sig = [sb(st, f"sig{i}", [128, 544], F32) for i in range(2)]; usig = [U(), U()]
                    vv = [sb(st, f"vv{i}", [128, 544], F32) for i in range(2)]; uvv = [U(), U()]
                    acc1 = [sb(st, f"acc1{i}", [128, 512], F32) for i in range(2)]; uacc1 = [U(), U()]
                    acc2 = [sb(st, f"acc2{i}", [128, 512], F32) for i in range(2)]; uacc2 = [U(), U()]
                    tmpk = [sb(st, f"tmpk{i}", [128, 512], F32) for i in range(4)]; utmpk = [U() for _ in range(4)]
                    yy = sb(st, "yy", [128, 8, 512], F32); uyy = [U() for _ in range(8)]
                    ysq = [sb(st, f"ysq{i}", [128, 512], F32) for i in range(2)]; uysq = [U(), U()]
                    mean = sb(st, "mean", [128, 512], F32); msq = sb(st, "msq", [128, 512], F32)
                    rstd = sb(st, "rstd", [128, 512], F32); ustat = U()
                    t1 = [sb(st, f"t1{i}", [128, 512], F32) for i in range(2)]; ut1 = [U(), U()]
                    t2 = [sb(st, f"t2{i}", [128, 512], F32) for i in range(2)]; ut2 = [U(), U()]
                    s1 = [sb(st, f"s1{i}", [128, 512], F32) for i in range(2)]; us1 = [U(), U()]
                    sg = [sb(st, f"sg{i}", [128, 512], F32) for i in range(2)]; usg = [U(), U()]
                    ocT = sb(st, "ocT", [128, 8, 512], BF16); uocT = U()
                    pTl = PF[4]; uTl = [U(), U()]
                    pST, pSQ, uST, uSQ = PF[5], PF[6], uPF[5], uPF[6]
                    it = 0
                    tk = 0
                    for tg in range(4):
                        e0 = 128 + tg * 512
                        hr = uhT[(e0 - 16) // 128:(e0 + 528 + 127) // 128]
                        for c in range(8):
                            b = it % 2
                            it += 1
                            pA, uA, pB, uB = PF[b], uPF[b], PF[2 + b], uPF[2 + b]
                            tb = b * 64
                            for (ps, ups, col0, to) in ((pA, uA, c * 128, tb), (pB, uB, 1024 + c * 128, tb + 32)):
                                for kk in range(8):
                                    k.op(PE, lambda e, ps=ps, kk=kk, col0=col0: e.matmul(ps[:, 0:512], lhsT=Wc[:, kk, col0:col0 + 128], rhs=hT[:, kk, e0 - 16:e0 + 496], start=(kk == 0), stop=(kk == 7)), r=[uWc] + hr, w=[ups])
                                for kk in range(8):
                                    k.op(PE, lambda e, kk=kk, col0=col0, to=to: e.matmul(pTl[:, to:to + 32], lhsT=Wc[:, kk, col0:col0 + 128], rhs=hT[:, kk, e0 + 496:e0 + 528], start=(kk == 0), stop=(kk == 7)), r=[uWc] + hr, w=[uTl[b]])
                            k.op(A, lambda e, b=b, pB=pB: e.activation(out=sig[b][:, 0:512], in_=pB[:, 0:512], func=AF.Sigmoid), r=[uB], w=[usig[b]])
                            k.op(A, lambda e, b=b, tb=tb: e.activation(out=sig[b][:, 512:544], in_=pTl[:, tb + 32:tb + 64], func=AF.Sigmoid), r=[uTl[b]], w=[usig[b]])
                            k.op(V, lambda e, b=b, pA=pA: e.tensor_tensor(out=vv[b][:, 0:512], in0=pA[:, 0:512], in1=sig[b][:, 0:512], op=ALU.mult), r=[uA, usig[b]], w=[uvv[b]])
                            k.op(V, lambda e, b=b, tb=tb: e.tensor_tensor(out=vv[b][:, 512:544], in0=pTl[:, tb:tb + 32], in1=sig[b][:, 512:544], op=ALU.mult), r=[uTl[b], usig[b]], w=[uvv[b]])
                            k.op(V, lambda e, c=c, b=b: e.tensor_scalar(out=acc1[b][:], in0=vv[b][:, 1:513], scalar1=prT[:, c, 0:1], scalar2=prT[:, c, 31:32], op0=ALU.mult, op1=ALU.add), r=[uvv[b], uL], w=[uacc1[b]])
                            for kt in range(1, 16):
                                k.op(V, lambda e, c=c, kt=kt, b=b: e.scalar_tensor_tensor(out=acc1[b][:], in0=vv[b][:, kt + 1:kt + 513], scalar=prT[:, c, kt:kt + 1], in1=acc1[b][:], op0=ALU.mult, op1=ALU.add), r=[uvv[b], uL, uacc1[b]], w=[uacc1[b]])
                            k.op(A, lambda e, c=c, b=b: e.activation(out=acc2[b][:], in_=vv[b][:, 17:529], func=AF.Copy, scale=prT[:, c, 16:17]), r=[uvv[b], uL], w=[uacc2[b]])
                            for kt in range(17, 31):
                                j = tk % 4
                                tk += 1
                                k.op(A, lambda e, c=c, kt=kt, j=j, b=b: e.activation(out=tmpk[j][:], in_=vv[b][:, kt + 1:kt + 513], func=AF.Copy, scale=prT[:, c, kt:kt + 1]), r=[uvv[b], uL], w=[utmpk[j]])
                                k.op(G, lambda e, j=j, b=b: e.tensor_tensor(out=acc2[b][:], in0=acc2[b][:], in1=tmpk[j][:], op=ALU.add), r=[utmpk[j], uacc2[b]], w=[uacc2[b]])
                            k.op(G, lambda e, c=c, b=b: e.tensor_tensor(out=yy[:, c, :], in0=acc1[b][:], in1=acc2[b][:], op=ALU.add), r=[uacc1[b], uacc2[b]], w=[uyy[c]])
                            k.op(A, lambda e, c=c, b=b: e.activation(out=ysq[b][:], in_=yy[:, c, :], func=AF.Square), r=[uyy[c]], w=[uysq[b]])
                            k.op(PE, lambda e, c=c: e.matmul(pST[:], lhsT=onesf[:], rhs=yy[:, c, :], start=(c == 0), stop=(c == 7)), r=[uyy[c], uC], w=[uST])
                            k.op(PE, lambda e, c=c, b=b: e.matmul(pSQ[:], lhsT=onesf[:], rhs=ysq[b][:], start=(c == 0), stop=(c == 7)), r=[uysq[b], uC], w=[uSQ])
                        k.op(V, lambda e: e.tensor_copy(out=mean[:], in_=pST[:]), r=[uST], w=[ustat])
                        k.op(G, lambda e: e.tensor_tensor(out=msq[:], in0=mean[:], in1=mean[:], op=ALU.mult), r=[ustat], w=[ustat])
                        k.op(V, lambda e: e.tensor_tensor(out=rstd[:], in0=pSQ[:], in1=msq[:], op=ALU.subtract), r=[uSQ, ustat], w=[ustat])
                        k.op(V, lambda e: e.tensor_scalar_add(out=rstd[:], in0=rstd[:], scalar1=EPS), r=[ustat], w=[ustat])
                        k.op(A, lambda e: e.activation(out=rstd[:], in_=rstd[:], func=AF.Sqrt), r=[ustat], w=[ustat])
                        k.op(V, lambda e: e.reciprocal(out=rstd[:], in_=rstd[:]), r=[ustat], w=[ustat])
                        for c in range(8):
                            b = c % 2
                            pG, uG = PF[b], uPF[b]
                            for kk in range(8):
                                k.op(PE, lambda e, c=c, kk=kk, pG=pG: e.matmul(pG[:], lhsT=Wc[:, kk, 2048 + c * 128:2048 + (c + 1) * 128], rhs=hT[:, kk, e0:e0 + 512], start=(kk == 0), stop=(kk == 7)), r=[uWc] + hr, w=[uG])
                            k.op(V, lambda e, c=c, b=b: e.tensor_tensor(out=t1[b][:], in0=yy[:, c, :], in1=mean[:], op=ALU.subtract), r=[uyy[c], ustat], w=[ut1[b]])
                            k.op(G, lambda e, b=b: e.tensor_tensor(out=t2[b][:], in0=t1[b][:], in1=rstd[:], op=ALU.mult), r=[ut1[b], ustat], w=[ut2[b]])
                            k.op(A, lambda e, c=c, b=b: e.activation(out=s1[b][:], in_=t2[b][:], func=AF.Silu, bias=prT[:, c, 33:34], scale=prT[:, c, 32:33]), r=[ut2[b], uL], w=[us1[b]])
                            k.op(A, lambda e, b=b, pG=pG: e.activation(out=sg[b][:], in_=pG[:], func=AF.Silu), r=[uG], w=[usg[b]])
                            k.op(V, lambda e, c=c, b=b: e.tensor_tensor(out=ocT[:, c, :], in0=s1[b][:], in1=sg[b][:], op=ALU.mult), r=[us1[b], usg[b]], w=[uocT])
                        k.dma(S, OC[:, :, tg * 512:(tg + 1) * 512].rearrange("c p t -> p c t"), ocT[:], r=[uocT], w=[u_OC])
                    k.barrier()
            if 'T' in phases:
                phase_T(l)
            if 'R' in phases:
                phase_R(l)
            if 'F' in phases:
                phase_F(l, xsrc, last)
        k.barrier()
    return nc


def make_consts(core):
    c = core % 4
    seg = c * T
    p = np.arange(128)
    tt = np.arange(T)
    inv = 10000.0 ** (-np.arange(128, dtype=np.float64) / 128.0)
    ang = inv[:, None] * (seg + tt)[None, :].astype(np.float64)
    cosr = np.cos(ang).astype(np.float32)
    sinr = np.sin(ang).astype(np.float32)
    inva = 500000.0 ** (-np.arange(8, dtype=np.float64) / 8.0)
    pos = (seg - 128 + np.arange(18)[None, :] * 128 + p[:, None]).astype(np.float64)
    anga = pos[:, :, None] * inva[None, None, :]
    cosa = np.cos(anga).astype(np.float32)
    sina = np.sin(anga).astype(np.float32)
    j = p[:, None]
    i = p[None, :]
    maskL = (j >= i).astype(np.float32)
    maskR = (j <= i).astype(np.float32)
    mask = np.concatenate([maskL, maskR, maskL * (1.0 if c > 0 else 0.0), maskR * (1.0 if c < 3 else 0.0)], axis=1)
    eseg = np.zeros((128, 32), np.float32)
    for n in range(16):
        eseg[:, n] = 2047 - (n * 128 + p)
        eseg[:, 16 + n] = n * 128 + p
    ek = np.stack([127 - p, p], axis=1).astype(np.float32)
    eq = np.concatenate([np.tile((np.arange(128) + 1)[None, :], (128, 1)), np.tile((128 - np.arange(128))[None, :], (128, 1))], axis=1).astype(np.float32)
    Ef = np.maximum(i - j, 0); Eb = np.maximum(j - i, 0)
    Mf = (i >= j); Mb = (j > i)
    em = np.concatenate([Ef, Eb, Mf, Mb], axis=1).astype(np.float32)
    BIG = 1.0e6
    dist = np.zeros((128, 8), np.float32)
    sel = np.zeros((128, 8), np.float32)
    for r in range(4):
        dist[:, r] = T * (c - r - 1) if r < c else BIG
        dist[:, 4 + r] = T * (r - c - 1) if r > c else BIG
        sel[:, r] = 1.0 if r == c - 1 else 0.0
        sel[:, 4 + r] = 1.0 if r == c + 1 else 0.0
    return dict(c_cosr=cosr, c_sinr=sinr, c_cosa=cosa, c_sina=sina, c_mask=mask,
                c_ident=np.eye(128, dtype=np.float32), c_eseg=eseg, c_ek=ek, c_eq=eq, c_em=em,
                c_dist=dist, c_sel=sel)


def make_in_maps(inputs):
    x = np.asarray(inputs['x'], np.float32)
    shared = dict(
        norm_g=np.asarray(inputs['norm_g'], np.float32),
        w_in=np.asarray(inputs['w_in'], np.float32),
        b_gate=np.asarray(inputs['b_gate'], np.float32).reshape(2, 3, D),
        conv_dw=np.asarray(inputs['conv_dw'], np.float32),
        conv_b=np.asarray(inputs['conv_b'], np.float32).reshape(2, 1, D),
        conv_ln_g=np.asarray(inputs['conv_ln_g'], np.float32).reshape(2, 1, D),
        conv_ln_b=np.asarray(inputs['conv_ln_b'], np.float32).reshape(2, 1, D),
        ret_decay=np.asarray(inputs['ret_decay'], np.float32).reshape(2, 8),
        q_norm_g=np.asarray(inputs['q_norm_g'], np.float32),
        k_norm_g=np.asarray(inputs['k_norm_g'], np.float32),
        attn_sink=np.asarray(inputs['attn_sink'], np.float32),
        w_conv_out=np.asarray(inputs['w_conv_out'], np.float32),
        w_ret_out=np.asarray(inputs['w_ret_out'], np.float32),
        w_attn_out=np.asarray(inputs['w_attn_out'], np.float32),
        w_out=np.asarray(inputs['w_out'], np.float32),
    )
    in_maps = []
    for core in range(8):
        b, c = core // 4, core % 4
        xe = np.zeros((TE, D), np.float32)
        lo = c * T - 128
        hi = c * T + T + 128
        slo, shi = max(lo, 0), min(hi, 4 * T)
        xe[slo - lo:shi - lo] = x[b, slo:shi]
        m = dict(shared)
        m['x_ext'] = xe
        m.update(make_consts(core))
        in_maps.append(m)
    return in_maps


_NC = None


def kernel(**inputs):
    global _NC
    if _NC is None:
        _NC = build(2)
    in_maps = make_in_maps(inputs)
    res = run_bass_kernel_spmd(_NC, in_maps, core_ids=list(range(8)))
    out = np.zeros((2, 4 * T, D), np.float32)
    for core in range(8):
        b, c = core // 4, core % 4
        out[b, c * T:(c + 1) * T] = res.results[core]["y_out"]
    return out
```

```python
import os
import numpy as np
import concourse.bass as bass
import concourse.mybir as mybir
from concourse.bass_utils import run_bass_kernel_spmd
from contextlib import ExitStack

F32 = mybir.dt.float32
BF16 = mybir.dt.bfloat16
ALU = mybir.AluOpType
AF = mybir.ActivationFunctionType
AX = mybir.AxisListType

ENG = ['tensor', 'vector', 'scalar', 'gpsimd', 'sync']
EPOCH = 20000
ND = 8
PE, V, A, G, S = 'tensor', 'vector', 'scalar', 'gpsimd', 'sync'


class U:
    __slots__ = ('w', 'rs')

    def __init__(s):
        s.w = None
        s.rs = {}


class KB:
    def __init__(s, nc, stack):
        s.nc = nc
        s.st = stack
        s.cnt = {e: 0 for e in ENG}
        s.nsem = 0
        s.sem = {e: s.new_sem(f'e_{e}') for e in ENG}
        s.hist = {e: [] for e in ENG}
        s.waited = {e: {} for e in ENG}
        s.dsem = {}
        s.dtarget = {}
        s.dcount = {}
        s.n_inst = 0

    def new_sem(s, name):
        s.nsem += 1
        return s.st.enter_context(s.nc.semaphore(f'{name}_{s.nsem}'))

    def _waits(s, engine, r, w):
        deps = {}

        def add(tok):
            key = id(tok[0])
            if key not in deps or deps[key][1] < tok[1]:
                deps[key] = tok
        for u in r:
            if u.w is not None:
                add(u.w)
        for u in w:
            if u.w is not None:
                add(u.w)
            for tok in u.rs.values():
                add(tok)
        waits = []
        wd = s.waited[engine]
        for key, (sem, val, src) in deps.items():
            if engine == PE and src == PE:
                continue
            if wd.get(key, 0) >= val:
                continue
            wd[key] = val
            waits.append((sem, val))
        return waits

    def _emit(s, ename, waits, fn, inc):
        e = getattr(s.nc, ename)
        for sem, val in waits:
            e.wait_ge(sem, val)
        if fn is None:
            return
        ins = fn(e)
        if inc[1] is None:
            ins.then_inc(inc[0])
        else:
            ins.then_inc(inc[0], inc[1])
        s.n_inst += 1

    def op(s, engine, fn, r=(), w=()):
        waits = s._waits(engine, r, w)
        if s.cnt[engine] >= EPOCH:
            s.hist[engine].append((s.sem[engine], s.cnt[engine]))
            s.sem[engine] = s.new_sem(f'e_{engine}')
            s.cnt[engine] = 0
        s.cnt[engine] += 1
        sem = s.sem[engine]
        tok = (sem, s.cnt[engine], engine)
        s._emit(engine, waits, fn, (sem, 1))
        for u in r:
            u.rs[id(sem)] = tok
        for u in w:
            u.w = tok
            u.rs = {}
        return tok

    def dma(s, q, out, in_, r=(), w=(), **kw):
        waits = s._waits(q, r, w)
        if q not in s.dsem:
            s.dsem[q] = [s.new_sem(f'd_{q}{i}') for i in range(ND)]
            s.dtarget[q] = [0] * ND
            s.dcount[q] = 0
        i = s.dcount[q] % ND
        s.dcount[q] += 1
        sem = s.dsem[q][i]
        prev = s.dtarget[q][i]
        if prev > 0 and s.waited[q].get(id(sem), 0) < prev:
            s.waited[q][id(sem)] = prev
            waits.append((sem, prev))
        tgt = prev + 16
        s.dtarget[q][i] = tgt
        tok = (sem, tgt, 'dma')
        s._emit(q, waits, lambda e: e.dma_start(out=out, in_=in_, **kw), (sem, 16))
        for u in r:
            u.rs[id(sem)] = tok
        for u in w:
            u.w = tok
            u.rs = {}
        return tok

    def custom(s, engine, fn, inc_sem, inc_val, r=(), w=()):
        waits = s._waits(engine, r, w)
        tok = (inc_sem, inc_val, 'custom')
        s._emit(engine, waits, fn, (inc_sem, None))
        for u in r:
            u.rs[id(inc_sem)] = tok
        for u in w:
            u.w = tok
            u.rs = {}
        return tok

    def barrier(s):
        toks = []
        for e in ENG:
            for sem, c in s.hist[e]:
                toks.append((sem, c))
            if s.cnt[e] > 0:
                toks.append((s.sem[e], s.cnt[e]))
        for q in s.dsem:
            for i in range(ND):
                if s.dtarget[q][i] > 0:
                    toks.append((s.dsem[q][i], s.dtarget[q][i]))
        for e in ENG:
            wd = s.waited[e]
            waits = []
            for sem, val in toks:
                if wd.get(id(sem), 0) >= val:
                    continue
                wd[id(sem)] = val
                waits.append((sem, val))
            s._emit(e, waits, None, None)


D = 1024
T = 2048
TE = 2304
NT = 16
INW = 14848
OFF_CGLU, OFF_CGATE = 0, 2048
OFF_RQ, OFF_RK, OFF_RV, OFF_RG = 3072, 4096, 5120, 7168
OFF_AQ, OFF_AK, OFF_AV, OFF_AG = 9216, 10240, 10496, 10752
OFF_GL = 11776
EPS = 1e-6


def build(NL=2, dbg=False, phases='CTRF'):
    nc = bass.Bass("TRN2", target_bir_lowering=False)

    def din(name, shape):
        return nc.dram_tensor(name, shape, F32, kind="ExternalInput").ap()

    x_ext = din("x_ext", [TE, D])
    norm_g = din("norm_g", [2, D])
    w_in = din("w_in", [2, D, INW])
    b_gate = din("b_gate", [2, 3, D])
    conv_dw = din("conv_dw", [2, 31, D])
    conv_b = din("conv_b", [2, 1, D])
    conv_ln_g = din("conv_ln_g", [2, 1, D])
    conv_ln_b = din("conv_ln_b", [2, 1, D])
    ret_decay = din("ret_decay", [2, 8])
    q_norm_g = din("q_norm_g", [2, 64])
    k_norm_g = din("k_norm_g", [2, 64])
    attn_sink = din("attn_sink", [2, 16])
    w_conv_out = din("w_conv_out", [2, D, D])
    w_ret_out = din("w_ret_out", [2, 2 * D, D])
    w_attn_out = din("w_attn_out", [2, D, D])
    w_out = din("w_out", [2, D, D])
    c_cosr = din("c_cosr", [128, T])
    c_sinr = din("c_sinr", [128, T])
    c_cosa = din("c_cosa", [128, 18, 8])
    c_sina = din("c_sina", [128, 18, 8])
    c_mask = din("c_mask", [128, 512])
    c_ident = din("c_ident", [128, 128])
    c_eseg = din("c_eseg", [128, 32])
    c_ek = din("c_ek", [128, 2])
    c_eq = din("c_eq", [128, 256])
    c_em = din("c_em", [128, 512])
    c_dist = din("c_dist", [128, 8])
    c_sel = din("c_sel", [128, 8])

    y_out = nc.dram_tensor("y_out", [T, D], F32, kind="ExternalOutput").ap()
    okind = "ExternalOutput" if dbg else "Internal"
    OC = nc.dram_tensor("OC", [8, 128, T], BF16, kind=okind).ap()
    OA = nc.dram_tensor("OA", [16, 64, T], BF16, kind=okind).ap()
    OR = nc.dram_tensor("OR", [16, 128, T], BF16, kind=okind).ap()
    x1e = nc.dram_tensor("x1e", [TE, D], F32, kind=okind).ap()
    fs_bounce = nc.dram_tensor("fs_bounce", [512, 512], F32)
    fs_gath = nc.dram_tensor("fs_gath", [2048, 512], F32)
    h_bounce = nc.dram_tensor("h_bounce", [256, D], F32)
    h_gath = nc.dram_tensor("h_gath", [1024, D], F32)
    u_OC, u_OA, u_OR, u_x1e, u_fsb, u_fsg, u_hb, u_hg, u_yout = [U() for _ in range(9)]
    RG = [[0, 1, 2, 3], [4, 5, 6, 7]]

    with ExitStack() as st0:
        k = KB(nc, st0)

        ncount = [0]

        def sb(st, name, shape, dt):
            ncount[0] += 1
            return st.enter_context(nc.sbuf_tensor(f"{name}_{ncount[0]}", shape, dt))

        PF = [st0.enter_context(nc.psum_tensor(f"pf{i}", [128, 512], F32)) for i in range(7)]
        uPF = [U() for _ in range(7)]
        PT = st0.enter_context(nc.psum_tensor("ptb", [128, 8, 128], BF16))
        uPT = U()
        hT = sb(st0, "hT", [128, 8, TE], BF16)
        uhT = [U() for _ in range(18)]
        ident = sb(st0, "ident", [128, 128], F32); identb = sb(st0, "identb", [128, 128], BF16)
        onesf = sb(st0, "onesf", [128, 128], F32); onesb = sb(st0, "onesb", [128, 128], BF16)
        maskb = sb(st0, "maskb", [128, 512], BF16)
        eseg = sb(st0, "eseg", [128, 32], F32); ek = sb(st0, "ek", [128, 2], F32)
        eq = sb(st0, "eq", [128, 256], F32); em = sb(st0, "em", [128, 512], F32)
        dist = sb(st0, "dist", [128, 8], F32); sel = sb(st0, "sel", [128, 8], F32)
        cosa = sb(st0, "cosa", [128, 18, 8], F32); sina = sb(st0, "sina", [128, 18, 8], F32)
        lg = sb(st0, "lg", [128, 8], F32)
        qg = sb(st0, "qg", [128, 64], F32); kg = sb(st0, "kg", [128, 64], F32)
        snk = sb(st0, "snk", [128, 16], F32); snke = sb(st0, "snke", [128, 16], F32)
        negc = sb(st0, "negc", [128, 1], F32); tmpc = sb(st0, "tmpc", [128, 4], F32)
        g2 = sb(st0, "g2", [128, 128], F32)
        prm = sb(st0, "prm", [37, D], F32)
        prT = sb(st0, "prT", [128, 8, 37], F32)
        uC = U()
        uL = U()

        for (t, src) in ((ident, c_ident), (eseg, c_eseg), (ek, c_ek), (eq, c_eq), (em, c_em),
                         (dist, c_dist), (sel, c_sel), (cosa, c_cosa), (sina, c_sina)):
            k.dma(S, t[:], src, w=[uC])
        k.dma(G, identb[:], c_ident, w=[uC])
        k.dma(G, maskb[:], c_mask, w=[uC])
        k.op(V, lambda e: e.memset(onesf[:], 1.0 / 1024.0), w=[uC])
        k.op(V, lambda e: e.memset(onesb[:], 1.0), w=[uC])
        k.barrier()

        def phase_T(l):
            with ExitStack() as st:
                Wt = sb(st, "Wt", [128, 8, 2560], BF16); uWt = U()
                for j in range(5):
                    k.dma(G, Wt[:, :, j * 512:(j + 1) * 512], w_in[l, :, OFF_AQ + j * 512:OFF_AQ + (j + 1) * 512].rearrange("(k p) c -> p k c", p=128), w=[uWt])
                qTg = sb(st, "qTg", [64, 16, 512], BF16); uqT = [U() for _ in range(4)]
                kT = sb(st, "kTres", [64, 4, TE], BF16); ukT = [U() for _ in range(18)]
                Vr = sb(st, "Vres", [128, 18, 256], BF16); uVr = [U() for _ in range(18)]
                sq = sb(st, "sq", [128, 1280], F32); usq = U()
                ssa = sb(st, "ssa", [128, 20], F32); rsa = sb(st, "rsa", [128, 20], F32); urs = U()
                qn = sb(st, "qn", [128, 1280], F32); uqn = U()
                rt = [sb(st, f"rt{i}", [128, 20, 8], F32) for i in range(4)]; urt = [U() for _ in range(4)]
                qb = sb(st, "qb", [128, 1024], BF16); uqb = U()
                kb = sb(st, "kb", [128, 256], BF16); ukb = U()
                sq3 = sq[:].rearrange("p (h d) -> p h d", d=64)
                qn3 = qn[:].rearrange("p (h d) -> p h d", d=64)

                def norm_rot(t, h0, h1):
                    nh = h1 - h0
                    k.op(V, lambda e: e.tensor_reduce(out=ssa[:, h0:h1], in_=sq3[:, h0:h1, :], axis=AX.X, op=ALU.add), r=[usq], w=[urs])
                    k.op(V, lambda e: e.tensor_scalar(out=rsa[:, h0:h1], in0=ssa[:, h0:h1], scalar1=1.0 / 64.0, scalar2=EPS, op0=ALU.mult, op1=ALU.add), r=[urs], w=[urs])
                    k.op(A, lambda e: e.activation(out=rsa[:, h0:h1], in_=rsa[:, h0:h1], func=AF.Sqrt), r=[urs], w=[urs])
                    k.op(V, lambda e: e.reciprocal(out=rsa[:, h0:h1], in_=rsa[:, h0:h1]), r=[urs], w=[urs])
                    if h0 == 0:
                        for half in range(2):
                            k.op(V, lambda e, half=half: e.tensor_tensor(out=qn3[:, half * 8:(half + 1) * 8, :], in0=PF[half][:].rearrange("p (h d) -> p h d", d=64), in1=rsa[:, half * 8:(half + 1) * 8].unsqueeze(2).broadcast_to([128, 8, 64]), op=ALU.mult), r=[uPF[half], urs], w=[uqn])
                        k.op(G, lambda e: e.tensor_tensor(out=qn3[:, 0:16, :], in0=qn3[:, 0:16, :], in1=qg[:].unsqueeze(1).broadcast_to([128, 16, 64]), op=ALU.mult), r=[uqn, uL], w=[uqn])
                    else:
                        k.op(V, lambda e: e.tensor_tensor(out=qn3[:, 16:20, :], in0=PF[2][:, 0:256].rearrange("p (h d) -> p h d", d=64), in1=rsa[:, 16:20].unsqueeze(2).broadcast_to([128, 4, 64]), op=ALU.mult), r=[uPF[2], urs], w=[uqn])
                        k.op(G, lambda e: e.tensor_tensor(out=qn3[:, 16:20, :], in0=qn3[:, 16:20, :], in1=kg[:].unsqueeze(1).broadcast_to([128, 4, 64]), op=ALU.mult), r=[uqn, uL], w=[uqn])
                    cb = cosa[:, t, :].unsqueeze(1).broadcast_to([128, nh, 8])
                    sbb = sina[:, t, :].unsqueeze(1).broadcast_to([128, nh, 8])
                    x1 = qn3[:, h0:h1, 0:8]
                    x2 = qn3[:, h0:h1, 8:16]
                    k.op(V, lambda e: e.tensor_tensor(out=rt[0][:, h0:h1, :], in0=x1, in1=cb, op=ALU.mult), r=[uqn, uC], w=[urt[0]])
                    k.op(G, lambda e: e.tensor_tensor(out=rt[1][:, h0:h1, :], in0=x2, in1=sbb, op=ALU.mult), r=[uqn, uC], w=[urt[1]])
                    k.op(V, lambda e: e.tensor_tensor(out=rt[2][:, h0:h1, :], in0=x2, in1=cb, op=ALU.mult), r=[uqn, uC], w=[urt[2]])
                    k.op(G, lambda e: e.tensor_tensor(out=rt[3][:, h0:h1, :], in0=x1, in1=sbb, op=ALU.mult), r=[uqn, uC], w=[urt[3]])
                    k.op(V, lambda e: e.tensor_tensor(out=x1, in0=rt[0][:, h0:h1, :], in1=rt[1][:, h0:h1, :], op=ALU.subtract), r=[urt[0], urt[1], urt[2], urt[3]], w=[uqn])
                    k.op(G, lambda e: e.tensor_tensor(out=x2, in0=rt[2][:, h0:h1, :], in1=rt[3][:, h0:h1, :], op=ALU.add), r=[urt[2], urt[3]], w=[uqn])

                for t in range(18):
                    for kk in range(8):
                        k.op(PE, lambda e, kk=kk, t=t: e.matmul(PF[2][:], lhsT=hT[:, kk, t * 128:(t + 1) * 128], rhs=Wt[:, kk, 1024:1536], start=(kk == 0), stop=(kk == 7)), r=[uWt, uhT[t]], w=[uPF[2]])
                    k.op(A, lambda e: e.activation(out=sq[:, 1024:1280], in_=PF[2][:, 0:256], func=AF.Square), r=[uPF[2]], w=[usq])
                    norm_rot(t, 16, 20)
                    k.op(A, lambda e: e.activation(out=kb[:], in_=qn[:, 1024:1280], func=AF.Copy), r=[uqn], w=[ukb])
                    k.op(A, lambda e, t=t: e.activation(out=Vr[:, t, :], in_=PF[2][:, 256:512], func=AF.Copy), r=[uPF[2]], w=[uVr[t]])
                    for g in range(4):
                        k.op(PE, lambda e, g=g: e.transpose(out=PT[0:64, g, :], in_=kb[:, g * 64:(g + 1) * 64], identity=identb[:]), r=[ukb, uC], w=[uPT])
                    k.op(V, lambda e, t=t: e.tensor_copy(out=kT[:, :, t * 128:(t + 1) * 128], in_=PT[0:64, 0:4, :]), r=[uPT], w=[ukT[t]])
                if dbg == 'T1':
                    k.barrier()
                    return
                gT = sb(st, "gT", [64, 16, 512], BF16); ugT = U()
                pt = [sb(st, f"pt{i}", [128, 512], BF16) for i in range(3)]; upt = [U() for _ in range(3)]
                den = sb(st, "den", [64, 512], F32); uden = U()
                rec = sb(st, "rec", [64, 512], F32); urec = U()
                on = sb(st, "on", [64, 512], F32); uon = U()
                oaT = sb(st, "oaT", [64, 16, 512], BF16); uoaT = U()
                for tg in range(4):
                    e0 = 128 + tg * 512
                    hr = uhT[e0 // 128:(e0 + 512) // 128]
                    for nn in range(4):
                        t = tg * 4 + nn + 1
                        for half in range(2):
                            for kk in range(8):
                                k.op(PE, lambda e, half=half, kk=kk, t=t: e.matmul(PF[half][:], lhsT=hT[:, kk, t * 128:(t + 1) * 128], rhs=Wt[:, kk, half * 512:(half + 1) * 512], start=(kk == 0), stop=(kk == 7)), r=[uWt, uhT[t]], w=[uPF[half]])
                            k.op(A, lambda e, half=half: e.activation(out=sq[:, half * 512:(half + 1) * 512], in_=PF[half][:], func=AF.Square), r=[uPF[half]], w=[usq])
                        norm_rot(t, 0, 16)
                        k.op(A, lambda e: e.activation(out=qb[:], in_=qn[:, 0:1024], func=AF.Copy), r=[uqn], w=[uqb])
                        for rr in range(2):
                            for j in range(8):
                                hh = rr * 8 + j
                                k.op(PE, lambda e, j=j, hh=hh: e.transpose(out=PT[0:64, j, :], in_=qb[:, hh * 64:(hh + 1) * 64], identity=identb[:]), r=[uqb, uC], w=[uPT])
                            k.op(V, lambda e, rr=rr, nn=nn: e.tensor_copy(out=qTg[:, rr * 8:(rr + 1) * 8, nn * 128:(nn + 1) * 128], in_=PT[0:64, :, :]), r=[uPT], w=[uqT[nn]])
                    for hh in range(16):
                        ps, ups = PF[hh % 2], uPF[hh % 2]
                        for kk in range(8):
                            k.op(PE, lambda e, ps=ps, hh=hh, kk=kk, e0=e0: e.matmul(ps[0:64, :], lhsT=Wt[:, kk, 1536 + hh * 64:1536 + (hh + 1) * 64], rhs=hT[:, kk, e0:e0 + 512], start=(kk == 0), stop=(kk == 7)), r=[uWt] + hr, w=[ups])
                        k.op(A, lambda e, ps=ps, hh=hh: e.activation(out=gT[:, hh, :], in_=ps[0:64, :], func=AF.Silu), r=[ups], w=[ugT])
                    for nn in range(4):
                        n = tg * 4 + nn
                        t = n + 1
                        for g in range(4):
                            for mi, m in enumerate((t - 1, t, t + 1)):
                                k.op(PE, lambda e, mi=mi, m=m, g=g, nn=nn: e.matmul(PF[2 + mi][:], lhsT=kT[:, g, m * 128:(m + 1) * 128], rhs=qTg[:, 4 * g:4 * g + 4, nn * 128:(nn + 1) * 128], start=True, stop=True), r=[ukT[m], uqT[nn]], w=[uPF[2 + mi]])
                                k.op(A, lambda e, mi=mi: e.activation(out=pt[mi][:], in_=PF[2 + mi][:], func=AF.Exp, bias=negc[:, 0:1], scale=0.125), r=[uPF[2 + mi], uL], w=[upt[mi]])
                            mL = 256 if t == 1 else 0
                            mR = 384 if t == 16 else 128
                            k.op(V, lambda e, mL=mL: e.tensor_tensor(out=pt[0][:].rearrange("p (a i) -> p a i", a=4), in0=pt[0][:].rearrange("p (a i) -> p a i", a=4), in1=maskb[:, mL:mL + 128].unsqueeze(1).broadcast_to([128, 4, 128]), op=ALU.mult), r=[upt[0], uC], w=[upt[0]])
                            k.op(G, lambda e, mR=mR: e.tensor_tensor(out=pt[2][:].rearrange("p (a i) -> p a i", a=4), in0=pt[2][:].rearrange("p (a i) -> p a i", a=4), in1=maskb[:, mR:mR + 128].unsqueeze(1).broadcast_to([128, 4, 128]), op=ALU.mult), r=[upt[2], uC], w=[upt[2]])
                            for mi, m in enumerate((t - 1, t, t + 1)):
                                k.op(PE, lambda e, mi=mi, m=m, g=g: e.matmul(PF[5][0:64, :], lhsT=Vr[:, m, g * 64:(g + 1) * 64], rhs=pt[mi][:], start=(mi == 0), stop=(mi == 2)), r=[uVr[m], upt[mi]], w=[uPF[5]])
                            for mi, m in enumerate((t - 1, t, t + 1)):
                                k.op(PE, lambda e, mi=mi: e.matmul(PF[6][0:64, :], lhsT=onesb[:, 0:64], rhs=pt[mi][:], start=(mi == 0), stop=(mi == 2)), r=[uC, upt[mi]], w=[uPF[6]])
                            for ei in range(4):
                                hd = 4 * g + ei
                                k.op(V, lambda e, ei=ei, hd=hd: e.tensor_scalar_add(out=den[:, ei * 128:(ei + 1) * 128], in0=PF[6][0:64, ei * 128:(ei + 1) * 128], scalar1=snke[0:64, hd:hd + 1]), r=[uPF[6], uL], w=[uden])
                            k.op(V, lambda e: e.reciprocal(out=rec[:], in_=den[:]), r=[uden], w=[urec])
                            k.op(V, lambda e: e.tensor_tensor(out=on[:], in0=PF[5][0:64, :], in1=rec[:], op=ALU.mult), r=[uPF[5], urec], w=[uon])
                            k.op(G, lambda e, g=g, nn=nn: e.tensor_tensor(out=oaT[:, 4 * g:4 * g + 4, nn * 128:(nn + 1) * 128], in0=on[:].rearrange("p (a i) -> p a i", a=4), in1=gT[:, 4 * g:4 * g + 4, nn * 128:(nn + 1) * 128], op=ALU.mult), r=[uon, ugT], w=[uoaT])
                    k.dma(S, OA[:, :, tg * 512:(tg + 1) * 512].rearrange("h d t -> d h t"), oaT[:], r=[uoaT], w=[u_OA])
                k.barrier()

        def phase_R(l):
            with ExitStack() as st:
                Wr = sb(st, "Wr", [128, 8, 2048], BF16); uWr = U()
                qT = sb(st, "rqT", [128, 2, T], BF16); uq = [U() for _ in range(4)]
                kT = sb(st, "rkT", [128, 2, T], BF16); ukk = [U() for _ in range(4)]
                Vv = sb(st, "rV", [128, 16, 512], BF16); uV = [U() for _ in range(16)]
                kt = sb(st, "rkt", [128, 16, 256], BF16); ukt = [U() for _ in range(16)]
                Pp = sb(st, "rP", [128, 16, 512], F32); uP = [U() for _ in range(16)]
                cs = sb(st, "rcs", [128, 2, 512], F32); ucs = U()
                tq = [sb(st, f"rtq{i}", [128, 512], F32) for i in range(4)]; utq = [U() for _ in range(4)]
                Sf = sb(st, "Sf", [128, 2, 512], F32); Sb = sb(st, "Sb", [128, 2, 512], F32); uSf = U(); uSb = U()
                Sfb = sb(st, "Sfb", [128, 2, 512], BF16); Sbb = sb(st, "Sbb", [128, 2, 512], BF16); uSfb = U(); uSbb = U()
                fsum = sb(st, "fsum", [128, 4, 512], F32); ufs = U()
                DTm = sb(st, "DTm", [128, 128], F32); dq = sb(st, "dq", [128, 256], F32); dsg = sb(st, "dsg", [128, 32], F32)
                dk = sb(st, "dk", [128, 2], F32); gC = sb(st, "gC", [128, 2], F32); wr = sb(st, "wr", [128, 8], F32)
                tmpD = sb(st, "tmpD", [128, 256], F32); uH = U()
                kfs = sb(st, "kfs", [128, 256], BF16); kbs = sb(st, "kbs", [128, 256], BF16); ukfs = U(); ukbs = U()
                qd = sb(st, "qd", [128, 2, 128], BF16); uqd = U()
                kdc = sb(st, "kdc", [128, 256], BF16); ukdc = U()
                sd = sb(st, "sd", [128, 128], BF16); usd = U()
                oo = sb(st, "oo", [128, 512], F32); uoo = U()
                onn = sb(st, "onn", [128, 512], F32); uonn = U()
                sgg = sb(st, "sgg", [128, 512], F32); usgg = U()
                og = sb(st, "og", [128, 512], BF16); uog = U()
                bst = sb(st, "bst", [128, 6], F32); mv = sb(st, "mv", [128, 2], F32); ubn = U()
                orT = sb(st, "orT", [128, 4, 512], BF16); uorT = U()
                RSTOP = int(os.environ.get("RSTOP", "0"))

                class _Stop(Exception):
                    pass

                def ck(n_):
                    if RSTOP == n_:
                        raise _Stop()
                try:
                  for h in range(4):
                      wl = [(1024, OFF_RV + h * 512), (1536, OFF_RG + h * 512)]
                      if h % 2 == 0:
                          wl = [(0, OFF_RQ + h * 256), (512, OFF_RK + h * 256)] + wl
                      for (dst, off) in wl:
                          k.dma(G, Wr[:, :, dst:dst + 512], w_in[l, :, off:off + 512].rearrange("(k p) c -> p k c", p=128), w=[uWr])
                      qc0 = (h % 2) * 256
                      kc0 = 512 + (h % 2) * 256
                      lgf = lg[:, h:h + 1]
                      lgb = lg[:, 4 + h:5 + h]
                      hc = dict(r=[uL, uC, uH], w=[uH])
                      k.op(A, lambda e: e.activation(out=dsg[:, 0:16], in_=eseg[:, 0:16], func=AF.Exp, scale=lgf), **hc)
                      k.op(A, lambda e: e.activation(out=dsg[:, 16:32], in_=eseg[:, 16:32], func=AF.Exp, scale=lgb), **hc)
                      k.op(A, lambda e: e.activation(out=dk[:, 0:1], in_=ek[:, 0:1], func=AF.Exp, scale=lgf), **hc)
                      k.op(A, lambda e: e.activation(out=dk[:, 1:2], in_=ek[:, 1:2], func=AF.Exp, scale=lgb), **hc)
                      k.op(A, lambda e: e.activation(out=dq[:, 0:128], in_=eq[:, 0:128], func=AF.Exp, scale=lgf), **hc)
                      k.op(A, lambda e: e.activation(out=dq[:, 128:256], in_=eq[:, 128:256], func=AF.Exp, scale=lgb), **hc)
                      k.op(V, lambda e: e.tensor_scalar_mul(out=dq[:], in0=dq[:], scalar1=1.0 / 16.0), **hc)
                      k.op(A, lambda e: e.activation(out=gC[:, 0:1], in_=lgf, func=AF.Exp, scale=128.0), **hc)
                      k.op(A, lambda e: e.activation(out=gC[:, 1:2], in_=lgb, func=AF.Exp, scale=128.0), **hc)
                      k.op(A, lambda e: e.activation(out=wr[:, 0:4], in_=dist[:, 0:4], func=AF.Exp, scale=lgf), **hc)
                      k.op(A, lambda e: e.activation(out=wr[:, 4:8], in_=dist[:, 4:8], func=AF.Exp, scale=lgb), **hc)
                      k.op(A, lambda e: e.activation(out=tmpD[:, 0:128], in_=em[:, 0:128], func=AF.Exp, scale=lgf), **hc)
                      k.op(A, lambda e: e.activation(out=tmpD[:, 128:256], in_=em[:, 128:256], func=AF.Exp, scale=lgb), **hc)
                      k.op(V, lambda e: e.tensor_tensor(out=tmpD[:], in0=tmpD[:], in1=em[:, 256:512], op=ALU.mult), **hc)
                      k.op(V, lambda e: e.tensor_tensor(out=DTm[:], in0=tmpD[:, 0:128], in1=tmpD[:, 128:256], op=ALU.add), **hc)
                      k.op(V, lambda e: e.tensor_scalar_mul(out=DTm[:], in0=DTm[:], scalar1=1.0 / 16.0), **hc)
                      ck(1)
                      for tg in range(4):
                          e0 = 128 + tg * 512
                          hr = uhT[e0 // 128:(e0 + 512) // 128]
                          k.dma(S, cs[:, 0, :], c_cosr[:, tg * 512:(tg + 1) * 512], w=[ucs])
                          k.dma(S, cs[:, 1, :], c_sinr[:, tg * 512:(tg + 1) * 512], w=[ucs])
                          for (col0, dstT, ud) in ((qc0, qT, uq[tg]), (kc0, kT, ukk[tg])):
                              for dc in range(2):
                                  for kk in range(8):
                                      k.op(PE, lambda e, dc=dc, kk=kk, col0=col0, e0=e0: e.matmul(PF[dc][:], lhsT=Wr[:, kk, col0 + dc * 128:col0 + (dc + 1) * 128], rhs=hT[:, kk, e0:e0 + 512], start=(kk == 0), stop=(kk == 7)), r=[uWr] + hr, w=[uPF[dc]])
                              k.op(V, lambda e: e.tensor_tensor(out=tq[0][:], in0=PF[0][:], in1=cs[:, 0, :], op=ALU.mult), r=[uPF[0], ucs], w=[utq[0]])
                              k.op(V, lambda e: e.tensor_tensor(out=tq[1][:], in0=PF[1][:], in1=cs[:, 1, :], op=ALU.mult), r=[uPF[1], ucs], w=[utq[1]])
                              k.op(V, lambda e: e.tensor_tensor(out=tq[2][:], in0=PF[1][:], in1=cs[:, 0, :], op=ALU.mult), r=[uPF[1], ucs], w=[utq[2]])
                              k.op(V, lambda e: e.tensor_tensor(out=tq[3][:], in0=PF[0][:], in1=cs[:, 1, :], op=ALU.mult), r=[uPF[0], ucs], w=[utq[3]])
                              k.op(V, lambda e, dstT=dstT, tg=tg: e.tensor_tensor(out=dstT[:, 0, tg * 512:(tg + 1) * 512], in0=tq[0][:], in1=tq[1][:], op=ALU.subtract), r=[utq[0], utq[1]], w=[ud])
                              k.op(G, lambda e, dstT=dstT, tg=tg: e.tensor_tensor(out=dstT[:, 1, tg * 512:(tg + 1) * 512], in0=tq[2][:], in1=tq[3][:], op=ALU.add), r=[utq[2], utq[3]], w=[ud])
                          ck(2)
                          for nn in range(4):
                              n = tg * 4 + nn
                              t = n + 1
                              for kk in range(8):
                                  k.op(PE, lambda e, kk=kk, t=t: e.matmul(PF[2][:], lhsT=hT[:, kk, t * 128:(t + 1) * 128], rhs=Wr[:, kk, 1024:1536], start=(kk == 0), stop=(kk == 7)), r=[uWr, uhT[t]], w=[uPF[2]])
                              k.op(A, lambda e, n=n: e.activation(out=Vv[:, n, :], in_=PF[2][:], func=AF.Copy), r=[uPF[2]], w=[uV[n]])
                              ck(3)
                              for dc in range(2):
                                  k.op(PE, lambda e, dc=dc, n=n: e.transpose(out=PT[:, dc, :], in_=kT[:, dc, n * 128:(n + 1) * 128], identity=identb[:]), r=[ukk[tg], uC], w=[uPT])
                              ptv = PT[:, 0:2, :]
                              k.op(A, lambda e, n=n, ptv=ptv: e.activation(out=kfs[:].rearrange("p (a d) -> p a d", a=2), in_=ptv, func=AF.Copy, scale=dsg[:, n:n + 1]), r=[uPT, uH], w=[ukfs])
                              k.op(A, lambda e, n=n, ptv=ptv: e.activation(out=kbs[:].rearrange("p (a d) -> p a d", a=2), in_=ptv, func=AF.Copy, scale=dsg[:, 16 + n:17 + n]), r=[uPT, uH], w=[ukbs])
                              k.op(A, lambda e, n=n, ptv=ptv: e.activation(out=kt[:, n, :].rearrange("p (a d) -> p a d", a=2), in_=ptv, func=AF.Copy), r=[uPT], w=[ukt[n]])
                              ck(4)
                              for dc in range(2):
                                  k.op(PE, lambda e, dc=dc, n=n: e.matmul(PF[3 + dc][:], lhsT=kfs[:, dc * 128:(dc + 1) * 128], rhs=Vv[:, n, :], start=(n == 0), stop=(n == 15)), r=[ukfs, uV[n]], w=[uPF[3 + dc]])
                                  k.op(PE, lambda e, dc=dc, n=n: e.matmul(PF[5 + dc][:], lhsT=kbs[:, dc * 128:(dc + 1) * 128], rhs=Vv[:, n, :], start=(n == 0), stop=(n == 15)), r=[ukbs, uV[n]], w=[uPF[5 + dc]])
                      if dbg == 'R0':
                          k.barrier()
                          return
                      for j in range(4):
                          k.op(A, lambda e, j=j: e.activation(out=fsum[:, j, :], in_=PF[3 + j][:], func=AF.Copy), r=[uPF[3 + j]], w=[ufs])
                      k.dma(S, fs_bounce.ap().rearrange("(j p) v -> p j v", p=128), fsum[:], r=[ufs], w=[u_fsb])
                      ccs = k.new_sem("cc")
                      k.custom(G, lambda e: e.collective_compute("AllGather", ALU.bypass, replica_groups=RG, ins=[fs_bounce.ap().opt()], outs=[fs_gath.ap().opt()]), ccs, 1, r=[u_fsb], w=[u_fsg])
                      for r_ in range(4):
                          k.dma(S, fsum[:], fs_gath.ap()[r_ * 512:(r_ + 1) * 512, :].rearrange("(j p) v -> p j v", p=128), r=[u_fsg], w=[ufs])
                          for dirn, (Sx, uSx) in enumerate(((Sf, uSf), (Sb, uSb))):
                              for dc in range(2):
                                  wcol = wr[:, dirn * 4 + r_:dirn * 4 + r_ + 1]
                                  if r_ == 0:
                                      k.op(V, lambda e, Sx=Sx, dc=dc, dirn=dirn, wcol=wcol: e.tensor_scalar_mul(out=Sx[:, dc, :], in0=fsum[:, dirn * 2 + dc, :], scalar1=wcol), r=[ufs, uH], w=[uSx])
                                  else:
                                      k.op(V, lambda e, Sx=Sx, dc=dc, dirn=dirn, wcol=wcol: e.scalar_tensor_tensor(out=Sx[:, dc, :], in0=fsum[:, dirn * 2 + dc, :], scalar=wcol, in1=Sx[:, dc, :], op0=ALU.mult, op1=ALU.add), r=[ufs, uH, uSx], w=[uSx])
                      k.op(A, lambda e: e.activation(out=Sfb[:], in_=Sf[:], func=AF.Copy), r=[uSf], w=[uSfb])
                      k.op(A, lambda e: e.activation(out=Sbb[:], in_=Sb[:], func=AF.Copy), r=[uSb], w=[uSbb])
                      if dbg == 'RAG':
                          k.barrier()
                          return
                      for n in range(15, -1, -1):
                          tg = n // 4
                          k.op(V, lambda e, n=n: e.tensor_tensor(out=qd[:], in0=qT[:, :, n * 128:(n + 1) * 128], in1=dq[:, 128:256].unsqueeze(1).broadcast_to([128, 2, 128]), op=ALU.mult), r=[uq[tg], uH], w=[uqd])
                          for dc in range(2):
                              k.op(PE, lambda e, dc=dc: e.matmul(PF[0][:], lhsT=qd[:, dc, :], rhs=Sbb[:, dc, :], start=(dc == 0), stop=(dc == 1)), r=[uqd, uSbb], w=[uPF[0]])
                          k.op(A, lambda e, n=n: e.activation(out=Pp[:, n, :], in_=PF[0][:], func=AF.Copy), r=[uPF[0]], w=[uP[n]])
                          k.op(A, lambda e, n=n: e.activation(out=kdc[:], in_=kt[:, n, :], func=AF.Copy, scale=dk[:, 1:2]), r=[ukt[n], uH], w=[ukdc])
                          for dc in range(2):
                              k.op(PE, lambda e, dc=dc, n=n: e.matmul(PF[1 + dc][:], lhsT=kdc[:, dc * 128:(dc + 1) * 128], rhs=Vv[:, n, :], start=True, stop=True), r=[ukdc, uV[n]], w=[uPF[1 + dc]])
                              k.op(V, lambda e, dc=dc: e.scalar_tensor_tensor(out=Sb[:, dc, :], in0=Sb[:, dc, :], scalar=gC[:, 1:2], in1=PF[1 + dc][:], op0=ALU.mult, op1=ALU.add), r=[uPF[1 + dc], uH, uSb], w=[uSb])
                          k.op(A, lambda e: e.activation(out=Sbb[:], in_=Sb[:], func=AF.Copy), r=[uSb], w=[uSbb])
                      if dbg == 'R1':
                          k.barrier()
                          return
                      for n in range(16):
                          tg = n // 4
                          t = n + 1
                          for dc in range(2):
                              k.op(PE, lambda e, dc=dc, n=n: e.matmul(PF[3][:, 0:128], lhsT=kT[:, dc, n * 128:(n + 1) * 128], rhs=qT[:, dc, n * 128:(n + 1) * 128], start=(dc == 0), stop=(dc == 1)), r=[ukk[tg], uq[tg]], w=[uPF[3]])
                          k.op(V, lambda e: e.tensor_tensor(out=sd[:], in0=PF[3][:, 0:128], in1=DTm[:], op=ALU.mult), r=[uPF[3], uH], w=[usd])
                          k.op(G, lambda e, n=n: e.tensor_tensor(out=qd[:], in0=qT[:, :, n * 128:(n + 1) * 128], in1=dq[:, 0:128].unsqueeze(1).broadcast_to([128, 2, 128]), op=ALU.mult), r=[uq[tg], uH], w=[uqd])
                          k.op(PE, lambda e, n=n: e.matmul(PF[4][:], lhsT=sd[:], rhs=Vv[:, n, :], start=True, stop=False), r=[usd, uV[n]], w=[uPF[4]])
                          for dc in range(2):
                              k.op(PE, lambda e, dc=dc: e.matmul(PF[4][:], lhsT=qd[:, dc, :], rhs=Sfb[:, dc, :], start=False, stop=(dc == 1)), r=[uqd, uSfb], w=[uPF[4]])
                          k.op(V, lambda e, n=n: e.tensor_tensor(out=oo[:], in0=PF[4][:], in1=Pp[:, n, :], op=ALU.add), r=[uPF[4], uP[n]], w=[uoo])
                          k.op(V, lambda e: e.bn_stats(out=bst[:], in_=oo[:]), r=[uoo], w=[ubn])
                          k.op(V, lambda e: e.bn_aggr(out=mv[:], in_=bst[:]), r=[ubn], w=[ubn])
                          k.op(V, lambda e: e.tensor_scalar_add(out=mv[:, 1:2], in0=mv[:, 1:2], scalar1=EPS), r=[ubn], w=[ubn])
                          k.op(A, lambda e: e.activation(out=mv[:, 1:2], in_=mv[:, 1:2], func=AF.Sqrt), r=[ubn], w=[ubn])
                          k.op(V, lambda e: e.reciprocal(out=mv[:, 1:2], in_=mv[:, 1:2]), r=[ubn], w=[ubn])
                          k.op(V, lambda e: e.tensor_scalar(out=onn[:], in0=oo[:], scalar1=mv[:, 0:1], scalar2=mv[:, 1:2], op0=ALU.subtract, op1=ALU.mult), r=[uoo, ubn], w=[uonn])
                          for kk in range(8):
                              k.op(PE, lambda e, kk=kk, t=t: e.matmul(PF[5][:], lhsT=hT[:, kk, t * 128:(t + 1) * 128], rhs=Wr[:, kk, 1536:2048], start=(kk == 0), stop=(kk == 7)), r=[uWr, uhT[t]], w=[uPF[5]])
                          k.op(A, lambda e: e.activation(out=sgg[:], in_=PF[5][:], func=AF.Silu), r=[uPF[5]], w=[usgg])
                          k.op(G, lambda e: e.tensor_tensor(out=og[:], in0=onn[:], in1=sgg[:], op=ALU.mult), r=[uonn, usgg], w=[uog])
                          for vc in range(4):
                              k.op(PE, lambda e, vc=vc: e.transpose(out=PT[:, vc, :], in_=og[:, vc * 128:(vc + 1) * 128], identity=identb[:]), r=[uog, uC], w=[uPT])
                          k.op(A, lambda e, n=n: e.activation(out=orT[:, :, (n % 4) * 128:(n % 4 + 1) * 128], in_=PT[:, 0:4, :], func=AF.Copy), r=[uPT], w=[uorT])
                          if n % 4 == 3:
                              k.dma(S, OR[h * 4:(h + 1) * 4, :, tg * 512:(tg + 1) * 512].rearrange("c p t -> p c t"), orT[:], r=[uorT], w=[u_OR])
                          k.op(A, lambda e, n=n: e.activation(out=kdc[:], in_=kt[:, n, :], func=AF.Copy, scale=dk[:, 0:1]), r=[ukt[n], uH], w=[ukdc])
                          for dc in range(2):
                              k.op(PE, lambda e, dc=dc, n=n: e.matmul(PF[1 + dc][:], lhsT=kdc[:, dc * 128:(dc + 1) * 128], rhs=Vv[:, n, :], start=True, stop=True), r=[ukdc, uV[n]], w=[uPF[1 + dc]])
                              k.op(V, lambda e, dc=dc: e.scalar_tensor_tensor(out=Sf[:, dc, :], in0=Sf[:, dc, :], scalar=gC[:, 0:1], in1=PF[1 + dc][:], op0=ALU.mult, op1=ALU.add), r=[uPF[1 + dc], uH, uSf], w=[uSf])
                          k.op(A, lambda e: e.activation(out=Sfb[:], in_=Sf[:], func=AF.Copy), r=[uSf], w=[uSfb])

                except _Stop:
                    pass
                k.barrier()

        def phase_F(l, xsrc, last):
            with ExitStack() as st:
                mg = sb(st, "mg", [128, 8, T], F32); umg = [U() for _ in range(4)]
                sgm = sb(st, "sgm", [128, 512], F32); usgm = U()
                tmpm = sb(st, "tmpm", [128, 512], F32); utmpm = U()
                for bi, (nk, kp) in enumerate(((8, 128), (16, 128), (16, 64))):
                    with ExitStack() as st2:
                        Wb = sb(st2, f"Wb{bi}", [kp, nk, D], BF16); uWb = U()
                        Wg = sb(st2, f"Wg{bi}", [128, 8, D], BF16); uWg = U()
                        ob = sb(st2, f"ob{bi}", [kp, nk, 512], BF16); uob = U()
                        src_w = (w_conv_out, w_ret_out, w_attn_out)[bi]
                        if bi == 2:
                            view = src_w[l].rearrange("(h d) c -> d h c", d=64)
                        else:
                            view = src_w[l].rearrange("(k p) c -> p k c", p=128)
                        for j in range(0, nk, 4):
                            k.dma(G, Wb[:, j:j + 4, :], view[:, j:j + 4, :], w=[uWb])
                        for j in range(2):
                            c0 = OFF_GL + bi * 1024 + j * 512
                            k.dma(G, Wg[:, :, j * 512:(j + 1) * 512], w_in[l, :, c0:c0 + 512].rearrange("(k p) c -> p k c", p=128), w=[uWg])
                        osrc = (OC, OR, OA)[bi]
                        uos = (u_OC, u_OR, u_OA)[bi]
                        for tg in range(4):
                            e0 = 128 + tg * 512
                            hr = uhT[e0 // 128:(e0 + 512) // 128]
                            k.dma(S, ob[:], osrc[:, :, tg * 512:(tg + 1) * 512].rearrange("c p t -> p c t"), r=[uos], w=[uob])
                            for m in range(8):
                                py, upy = PF[m % 3], uPF[m % 3]
                                pg, upg = PF[3 + m % 3], uPF[3 + m % 3]
                                for j in range(nk):
                                    k.op(PE, lambda e, py=py, j=j, m=m: e.matmul(py[:], lhsT=Wb[:, j, m * 128:(m + 1) * 128], rhs=ob[:, j, :], start=(j == 0), stop=(j == nk - 1)), r=[uWb, uob], w=[upy])
                                for kk in range(8):
                                    k.op(PE, lambda e, pg=pg, kk=kk, m=m, e0=e0: e.matmul(pg[:], lhsT=Wg[:, kk, m * 128:(m + 1) * 128], rhs=hT[:, kk, e0:e0 + 512], start=(kk == 0), stop=(kk == 7)), r=[uWg] + hr, w=[upg])
                                k.op(A, lambda e, pg=pg, m=m, bi=bi: e.activation(out=sgm[:], in_=pg[:], func=AF.Sigmoid, bias=prT[:, m, 34 + bi:35 + bi], scale=1.0), r=[upg, uL], w=[usgm])
                                if bi == 0:
                                    k.op(V, lambda e, py=py, m=m, tg=tg: e.tensor_tensor(out=mg[:, m, tg * 512:(tg + 1) * 512], in0=py[:], in1=sgm[:], op=ALU.mult), r=[upy, usgm], w=[umg[tg]])
                                else:
                                    k.op(V, lambda e, py=py: e.tensor_tensor(out=tmpm[:], in0=py[:], in1=sgm[:], op=ALU.mult), r=[upy, usgm], w=[utmpm])
                                    k.op(G, lambda e, m=m, tg=tg: e.tensor_tensor(out=mg[:, m, tg * 512:(tg + 1) * 512], in0=mg[:, m, tg * 512:(tg + 1) * 512], in1=tmpm[:], op=ALU.add), r=[utmpm, umg[tg]], w=[umg[tg]])
                        k.barrier()
                with ExitStack() as st2:
                    Wo = sb(st2, "Wo", [128, 8, D], BF16); uWo = U()
                    for j in range(2):
                        k.dma(G, Wo[:, j * 4:(j + 1) * 4, :], w_out[l].rearrange("(k p) c -> p k c", p=128)[:, j * 4:(j + 1) * 4, :], w=[uWo])
                    mb = sb(st2, "mb", [128, 8, 128], BF16); umb = U()
                    xt = [sb(st2, f"fxt{i}", [128, D], F32) for i in range(2)]; uxt = [U(), U()]
                    xn = [sb(st2, f"fxn{i}", [128, D], F32) for i in range(2)]; uxn = [U(), U()]
                    for n in range(16):
                        b = n % 2
                        k.op(A, lambda e, n=n: e.activation(out=mb[:], in_=mg[:, :, n * 128:(n + 1) * 128], func=AF.Copy), r=[umg[n // 4]], w=[umb])
                        k.dma(S, xt[b][:], xsrc[128 + n * 128:128 + (n + 1) * 128, :], r=[u_x1e], w=[uxt[b]])
                        for half in range(2):
                            for kk in range(8):
                                k.op(PE, lambda e, half=half, kk=kk: e.matmul(PF[half][:], lhsT=mb[:, kk, :], rhs=Wo[:, kk, half * 512:(half + 1) * 512], start=(kk == 0), stop=(kk == 7)), r=[umb, uWo], w=[uPF[half]])
                            k.op(V, lambda e, half=half, b=b: e.tensor_tensor(out=xn[b][:, half * 512:(half + 1) * 512], in0=PF[half][:], in1=xt[b][:, half * 512:(half + 1) * 512], op=ALU.add), r=[uPF[half], uxt[b]], w=[uxn[b]])
                        if last:
                            k.dma(S, y_out[n * 128:(n + 1) * 128, :], xn[b][:], r=[uxn[b]], w=[u_yout])
                        else:
                            k.dma(S, x1e[128 + n * 128:128 + (n + 1) * 128, :], xn[b][:], r=[uxn[b]], w=[u_x1e])
                            if n == 0:
                                k.dma(S, h_bounce.ap()[0:128, :], xn[b][:], r=[uxn[b]], w=[u_hb])
                            if n == 15:
                                k.dma(S, h_bounce.ap()[128:256, :], xn[b][:], r=[uxn[b]], w=[u_hb])
                    k.barrier()
            if not last:
                ccs = k.new_sem("cch")
                k.custom(G, lambda e: e.collective_compute("AllGather", ALU.bypass, replica_groups=RG, ins=[h_bounce.ap().opt()], outs=[h_gath.ap().opt()]), ccs, 1, r=[u_hb], w=[u_hg])
                with ExitStack() as st2:
                    hb = sb(st2, "hb", [128, 2, D], F32); uhb = U()
                    accL = sb(st2, "accL", [128, D], F32); accR = sb(st2, "accR", [128, D], F32); uaL = U(); uaR = U()
                    for r_ in range(4):
                        k.dma(S, hb[:], h_gath.ap()[r_ * 256:(r_ + 1) * 256, :].rearrange("(a p) f -> p a f", p=128), r=[u_hg], w=[uhb])
                        for (acc, ua, a_, sc) in ((accL, uaL, 1, r_), (accR, uaR, 0, 4 + r_)):
                            if r_ == 0:
                                k.op(V, lambda e, acc=acc, a_=a_, sc=sc: e.tensor_scalar_mul(out=acc[:], in0=hb[:, a_, :], scalar1=sel[:, sc:sc + 1]), r=[uhb, uC], w=[ua])
                            else:
                                k.op(V, lambda e, acc=acc, a_=a_, sc=sc: e.scalar_tensor_tensor(out=acc[:], in0=hb[:, a_, :], scalar=sel[:, sc:sc + 1], in1=acc[:], op0=ALU.mult, op1=ALU.add), r=[uhb, uC, ua], w=[ua])
                    k.dma(S, x1e[0:128, :], accL[:], r=[uaL], w=[u_x1e])
                    k.dma(S, x1e[TE - 128:TE, :], accR[:], r=[uaR], w=[u_x1e])
                    k.barrier()

        for l in range(NL):
            xsrc = x_ext if l == 0 else x1e
            last = (l == NL - 1)
            k.dma(S, lg[:], ret_decay[l:l + 1, :].partition_broadcast(128), w=[uL])
            k.dma(S, qg[:], q_norm_g[l:l + 1, :].partition_broadcast(128), w=[uL])
            k.dma(S, kg[:], k_norm_g[l:l + 1, :].partition_broadcast(128), w=[uL])
            k.dma(S, snk[:], attn_sink[l:l + 1, :].partition_broadcast(128), w=[uL])
            k.dma(S, prm[0:31, :], conv_dw[l], w=[uL])
            k.dma(S, prm[31:32, :], conv_b[l], w=[uL])
            k.dma(S, prm[32:33, :], conv_ln_g[l], w=[uL])
            k.dma(S, prm[33:34, :], conv_ln_b[l], w=[uL])
            k.dma(S, prm[34:37, :], b_gate[l], w=[uL])
            k.op(A, lambda e: e.activation(out=lg[:], in_=lg[:], func=AF.Exp), r=[uL], w=[uL])
            k.op(V, lambda e: e.tensor_scalar_mul(out=lg[:], in0=lg[:], scalar1=-1.0), r=[uL], w=[uL])
            k.op(V, lambda e: e.tensor_tensor(out=g2[:, 0:64], in0=qg[:], in1=qg[:], op=ALU.mult), r=[uL], w=[uL])
            k.op(V, lambda e: e.tensor_tensor(out=g2[:, 64:128], in0=kg[:], in1=kg[:], op=ALU.mult), r=[uL], w=[uL])
            k.op(V, lambda e: e.tensor_reduce(out=tmpc[:, 0:1], in_=g2[:, 0:64], axis=AX.X, op=ALU.max), r=[uL], w=[uL])
            k.op(V, lambda e: e.tensor_reduce(out=tmpc[:, 1:2], in_=g2[:, 64:128], axis=AX.X, op=ALU.max), r=[uL], w=[uL])
            k.op(V, lambda e: e.tensor_tensor(out=tmpc[:, 2:3], in0=tmpc[:, 0:1], in1=tmpc[:, 1:2], op=ALU.mult), r=[uL], w=[uL])
            k.op(A, lambda e: e.activation(out=tmpc[:, 3:4], in_=tmpc[:, 2:3], func=AF.Sqrt), r=[uL], w=[uL])
            k.op(V, lambda e: e.tensor_scalar_mul(out=negc[:], in0=tmpc[:, 3:4], scalar1=-8.0), r=[uL], w=[uL])
            k.op(A, lambda e: e.activation(out=snke[:], in_=snk[:], func=AF.Exp, bias=negc[:, 0:1], scale=1.0), r=[uL], w=[uL])
            for c in range(8):
                k.op(PE, lambda e, c=c: e.transpose(out=PF[0][:, c * 37:(c + 1) * 37], in_=prm[:, c * 128:(c + 1) * 128], identity=ident[0:37, 0:37]), r=[uL, uC], w=[uPF[0]])
            k.op(V, lambda e: e.tensor_copy(out=prT[:].rearrange("p c j -> p (c j)"), in_=PF[0][:, 0:296]), r=[uPF[0]], w=[uL])
            k.barrier()

            with ExitStack() as st:
                g_bc = sb(st, "g_bc", [128, D], F32); ug = U()
                k.dma(S, g_bc[:], norm_g[l:l + 1, :].partition_broadcast(128), w=[ug])
                xt = [sb(st, f"xt{i}", [128, D], F32) for i in range(2)]; uxt = [U(), U()]
                junk = sb(st, "junk", [128, D], BF16); ujunk = U()
                xs = [sb(st, f"xs{i}", [128, D], BF16) for i in range(2)]; uxs = [U(), U()]
                ssq = [sb(st, f"ssq{i}", [128, 2], F32) for i in range(2)]; ussq = [U(), U()]
                for t in range(18):
                    b = t % 2
                    k.dma(S, xt[b][:], xsrc[t * 128:(t + 1) * 128, :], r=[u_x1e], w=[uxt[b]])
                    k.op(V, lambda e, b=b: e.memset(ssq[b][:], 0.0), w=[ussq[b]])
                    k.op(A, lambda e, b=b: e.activation(out=junk[:], in_=xt[b][:], func=AF.Square, accum_out=ssq[b][:, 0:1]), r=[uxt[b]], w=[ujunk, ussq[b]])
                    k.op(V, lambda e, b=b: e.tensor_scalar(out=ssq[b][:, 1:2], in0=ssq[b][:, 0:1], scalar1=1.0 / D, scalar2=EPS, op0=ALU.mult, op1=ALU.add), r=[ussq[b]], w=[ussq[b]])
                    k.op(A, lambda e, b=b: e.activation(out=ssq[b][:, 1:2], in_=ssq[b][:, 1:2], func=AF.Sqrt), r=[ussq[b]], w=[ussq[b]])
                    k.op(V, lambda e, b=b: e.reciprocal(out=ssq[b][:, 1:2], in_=ssq[b][:, 1:2]), r=[ussq[b]], w=[ussq[b]])
                    k.op(V, lambda e, b=b: e.scalar_tensor_tensor(out=xs[b][:], in0=xt[b][:], scalar=ssq[b][:, 1:2], in1=g_bc[:], op0=ALU.mult, op1=ALU.mult), r=[uxt[b], ussq[b], ug], w=[uxs[b]])
                    for c in range(8):
                        k.op(PE, lambda e, b=b, c=c: e.transpose(out=PT[:, c, :], in_=xs[b][:, c * 128:(c + 1) * 128], identity=identb[:]), r=[uxs[b], uC], w=[uPT])
                    k.op(A, lambda e, t=t: e.activation(out=hT[:, :, t * 128:(t + 1) * 128], in_=PT[:], func=AF.Copy), r=[uPT], w=[uhT[t]])
                k.barrier()

            if 'C' in phases:
                with ExitStack() as st:
                    Wc = sb(st, "Wc", [128, 8, 3072], BF16); uWc = U()
                    for j in range(6):
                        k.dma(G, Wc[:, :, j * 512:(j + 1) * 512], w_in[l, :, j * 512:(j + 1) * 512].rearrange("(k p) c -> p k c", p=128), w=[uWc])
                    sig = [sb(st, f"sig{i}", [128, 544], F32) for i in range(2)]; usig = [U(), U()]
                    vv = [sb(st, f"vv{i}", [128, 544], F32) for i in range(2)]; uvv = [U(), U()]
                    acc1 = [sb(st, f"acc1{i}", [128, 512], F32) for i in range(2)]; uacc1 = [U(), U()]
                    acc2 = [sb(st, f"acc2{i}", [128, 512], F32) for i in range(2)]; uacc2 = [U(), U()]
                    tmpk = [sb(st, f"tmpk{i}", [128, 512], F32) for i in range(4)]; utmpk = [U() for _ in range(4)]
                    yy = sb(st, "yy", [128, 8, 512], F32); uyy = [U() for _ in range(8)]
                    ysq = [sb(st, f"ysq{i}", [128, 512], F32) for i in range(2)]; uysq = [U(), U()]
                    mean = sb(st, "mean", [128, 512], F32); msq = sb(st, "msq", [128, 512], F32)
                    rstd = sb(st, "rstd", [128, 512], F32); ustat = U()
                    t1 = [sb(st, f"t1{i}", [128, 512], F32) for i in range(2)]; ut1 = [U(), U()]
                    t2 = [sb(st, f"t2{i}", [128, 512], F32) for i in range(2)]; ut2 = [U(), U()]
                    s1 = [sb(st, f"s1{i}", [128, 512], F32) for i in range(2)]; us1 = [U(), U()]
                    sg = [sb(st, f"sg{i}", [128, 512], F32) for i in range(2)]; usg = [U(), U()]
                    ocT = sb(st, "ocT", [128, 8, 512], BF16); uocT = U()
                    pTls = [PF[4], PF[5]]; uTl = [uPF[4], uPF[5]]
                    pST, pSQ, uST, uSQ = PF[6], PF[4], uPF[6], uPF[4]
                    it = 0
                    tk = 0

                    def proj(tg, c, b):
                        e0 = 128 + tg * 512
                        hr = uhT[(e0 - 16) // 128:(e0 + 528 + 127) // 128]
                        pA, uA, pB, uB = PF[b], uPF[b], PF[2 + b], uPF[2 + b]
                        tb = 0
                        pTl = pTls[b]
                        for (ps, ups, col0, to) in ((pA, uA, c * 128, tb), (pB, uB, 1024 + c * 128, tb + 32)):
                            for kk in range(8):
                                k.op(PE, lambda e, ps=ps, kk=kk, col0=col0: e.matmul(ps[:, 0:512], lhsT=Wc[:, kk, col0:col0 + 128], rhs=hT[:, kk, e0 - 16:e0 + 496], start=(kk == 0), stop=(kk == 7)), r=[uWc] + hr, w=[ups])
                            for kk in range(8):
                                k.op(PE, lambda e, kk=kk, col0=col0, to=to, pTl=pTl: e.matmul(pTl[:, to:to + 32], lhsT=Wc[:, kk, col0:col0 + 128], rhs=hT[:, kk, e0 + 496:e0 + 528], start=(kk == 0), stop=(kk == 7)), r=[uWc] + hr, w=[uTl[b]])
                    for tg in range(4):
                        e0 = 128 + tg * 512
                        hr = uhT[(e0 - 16) // 128:(e0 + 528 + 127) // 128]
                        proj(tg, 0, it % 2)
                        for c in range(8):
                            b = it % 2
                            it += 1
                            pA, uA, pB, uB = PF[b], uPF[b], PF[2 + b], uPF[2 + b]
                            tb = 0
                            pTl = pTls[b]
                            if c < 7:
                                proj(tg, c + 1, it % 2)
                            k.op(A, lambda e, b=b, pB=pB: e.activation(out=sig[b][:, 0:512], in_=pB[:, 0:512], func=AF.Sigmoid), r=[uB], w=[usig[b]])
                            k.op(A, lambda e, b=b, tb=tb, pTl=pTl: e.activation(out=sig[b][:, 512:544], in_=pTl[:, tb + 32:tb + 64], func=AF.Sigmoid), r=[uTl[b]], w=[usig[b]])
                            k.op(V, lambda e, b=b, pA=pA: e.tensor_tensor(out=vv[b][:, 0:512], in0=pA[:, 0:512], in1=sig[b][:, 0:512], op=ALU.mult), r=[uA, usig[b]], w=[uvv[b]])
                            k.op(V, lambda e, b=b, tb=tb, pTl=pTl: e.tensor_tensor(out=vv[b][:, 512:544], in0=pTl[:, tb:tb + 32], in1=sig[b][:, 512:544], op=ALU.mult), r=[uTl[b], usig[b]], w=[uvv[b]])
                            k.op(V, lambda e, c=c, b=b: e.tensor_scalar(out=acc1[b][:], in0=vv[b][:, 1:513], scalar1=prT[:, c, 0:1], scalar2=prT[:, c, 31:32], op0=ALU.mult, op1=ALU.add), r=[uvv[b], uL], w=[uacc1[b]])
                            for kt in range(1, 16):
                                k.op(V, lambda e, c=c, kt=kt, b=b: e.scalar_tensor_tensor(out=acc1[b][:], in0=vv[b][:, kt + 1:kt + 513], scalar=prT[:, c, kt:kt + 1], in1=acc1[b][:], op0=ALU.mult, op1=ALU.add), r=[uvv[b], uL, uacc1[b]], w=[uacc1[b]])
                            k.op(A, lambda e, c=c, b=b: e.activation(out=acc2[b][:], in_=vv[b][:, 17:529], func=AF.Copy, scale=prT[:, c, 16:17]), r=[uvv[b], uL], w=[uacc2[b]])
                            for kt in range(17, 31):
                                j = tk % 4
                                tk += 1
                                k.op(A, lambda e, c=c, kt=kt, j=j, b=b: e.activation(out=tmpk[j][:], in_=vv[b][:, kt + 1:kt + 513], func=AF.Copy, scale=prT[:, c, kt:kt + 1]), r=[uvv[b], uL], w=[utmpk[j]])
                                k.op(G, lambda e, j=j, b=b: e.tensor_tensor(out=acc2[b][:], in0=acc2[b][:], in1=tmpk[j][:], op=ALU.add), r=[utmpk[j], uacc2[b]], w=[uacc2[b]])
                            k.op(G, lambda e, c=c, b=b: e.tensor_tensor(out=yy[:, c, :], in0=acc1[b][:], in1=acc2[b][:], op=ALU.add), r=[uacc1[b], uacc2[b]], w=[uyy[c]])
                        for c in range(8):
                            b = c % 2
                            k.op(A, lambda e, c=c, b=b: e.activation(out=ysq[b][:], in_=yy[:, c, :], func=AF.Square), r=[uyy[c]], w=[uysq[b]])
                            k.op(PE, lambda e, c=c: e.matmul(pST[:], lhsT=onesf[:], rhs=yy[:, c, :], start=(c == 0), stop=(c == 7)), r=[uyy[c], uC], w=[uST])
                            k.op(PE, lambda e, c=c, b=b: e.matmul(pSQ[:], lhsT=onesf[:], rhs=ysq[b][:], start=(c == 0), stop=(c == 7)), r=[uysq[b], uC], w=[uSQ])
                        k.op(V, lambda e: e.tensor_copy(out=mean[:], in_=pST[:]), r=[uST], w=[ustat])
                        k.op(G, lambda e: e.tensor_tensor(out=msq[:], in0=mean[:], in1=mean[:], op=ALU.mult), r=[ustat], w=[ustat])
                        k.op(V, lambda e: e.tensor_tensor(out=rstd[:], in0=pSQ[:], in1=msq[:], op=ALU.subtract), r=[uSQ, ustat], w=[ustat])
                        k.op(V, lambda e: e.tensor_scalar_add(out=rstd[:], in0=rstd[:], scalar1=EPS), r=[ustat], w=[ustat])
                        k.op(A, lambda e: e.activation(out=rstd[:], in_=rstd[:], func=AF.Sqrt), r=[ustat], w=[ustat])
                        k.op(V, lambda e: e.reciprocal(out=rstd[:], in_=rstd[:]), r=[ustat], w=[ustat])
                        for c in range(8):
                            b = c % 2
                            pG, uG = PF[b], uPF[b]
                            for kk in range(8):
                                k.op(PE, lambda e, c=c, kk=kk, pG=pG: e.matmul(pG[:], lhsT=Wc[:, kk, 2048 + c * 128:2048 + (c + 1) * 128], rhs=hT[:, kk, e0:e0 + 512], start=(kk == 0), stop=(kk == 7)), r=[uWc] + hr, w=[uG])
                            k.op(V, lambda e, c=c, b=b: e.tensor_tensor(out=t1[b][:], in0=yy[:, c, :], in1=mean[:], op=ALU.subtract), r=[uyy[c], ustat], w=[ut1[b]])
                            k.op(G, lambda e, b=b: e.tensor_tensor(out=t2[b][:], in0=t1[b][:], in1=rstd[:], op=ALU.mult), r=[ut1[b], ustat], w=[ut2[b]])
                            k.op(A, lambda e, c=c, b=b: e.activation(out=s1[b][:], in_=t2[b][:], func=AF.Silu, bias=prT[:, c, 33:34], scale=prT[:, c, 32:33]), r=[ut2[b], uL], w=[us1[b]])
                            k.op(A, lambda e, b=b, pG=pG: e.activation(out=sg[b][:], in_=pG[:], func=AF.Silu), r=[uG], w=[usg[b]])
                            k.op(V, lambda e, c=c, b=b: e.tensor_tensor(out=ocT[:, c, :], in0=s1[b][:], in1=sg[b][:], op=ALU.mult), r=[us1[b], usg[b]], w=[uocT])
                        k.dma(S, OC[:, :, tg * 512:(tg + 1) * 512].rearrange("c p t -> p c t"), ocT[:], r=[uocT], w=[u_OC])
                    k.barrier()
            if 'T' in phases:
                phase_T(l)
            if 'R' in phases:
                phase_R(l)
            if 'F' in phases:
                phase_F(l, xsrc, last)
        k.barrier()
    return nc


def make_consts(core):
    c = core % 4
    seg = c * T
    p = np.arange(128)
    tt = np.arange(T)
    inv = 10000.0 ** (-np.arange(128, dtype=np.float64) / 128.0)
    ang = inv[:, None] * (seg + tt)[None, :].astype(np.float64)
    cosr = np.cos(ang).astype(np.float32)
    sinr = np.sin(ang).astype(np.float32)
    inva = 500000.0 ** (-np.arange(8, dtype=np.float64) / 8.0)
    pos = (seg - 128 + np.arange(18)[None, :] * 128 + p[:, None]).astype(np.float64)
    anga = pos[:, :, None] * inva[None, None, :]
    cosa = np.cos(anga).astype(np.float32)
    sina = np.sin(anga).astype(np.float32)
    j = p[:, None]
    i = p[None, :]
    maskL = (j >= i).astype(np.float32)
    maskR = (j <= i).astype(np.float32)
    mask = np.concatenate([maskL, maskR, maskL * (1.0 if c > 0 else 0.0), maskR * (1.0 if c < 3 else 0.0)], axis=1)
    eseg = np.zeros((128, 32), np.float32)
    for n in range(16):
        eseg[:, n] = 2047 - (n * 128 + p)
        eseg[:, 16 + n] = n * 128 + p
    ek = np.stack([127 - p, p], axis=1).astype(np.float32)
    eq = np.concatenate([np.tile((np.arange(128) + 1)[None, :], (128, 1)), np.tile((128 - np.arange(128))[None, :], (128, 1))], axis=1).astype(np.float32)
    Ef = np.maximum(i - j, 0); Eb = np.maximum(j - i, 0)
    Mf = (i >= j); Mb = (j > i)
    em = np.concatenate([Ef, Eb, Mf, Mb], axis=1).astype(np.float32)
    BIG = 1.0e6
    dist = np.zeros((128, 8), np.float32)
    sel = np.zeros((128, 8), np.float32)
    for r in range(4):
        dist[:, r] = T * (c - r - 1) if r < c else BIG
        dist[:, 4 + r] = T * (r - c - 1) if r > c else BIG
        sel[:, r] = 1.0 if r == c - 1 else 0.0
        sel[:, 4 + r] = 1.0 if r == c + 1 else 0.0
    return dict(c_cosr=cosr, c_sinr=sinr, c_cosa=cosa, c_sina=sina, c_mask=mask,
                c_ident=np.eye(128, dtype=np.float32), c_eseg=eseg, c_ek=ek, c_eq=eq, c_em=em,
                c_dist=dist, c_sel=sel)


def make_in_maps(inputs):
    x = np.asarray(inputs['x'], np.float32)
    shared = dict(
        norm_g=np.asarray(inputs['norm_g'], np.float32),
        w_in=np.asarray(inputs['w_in'], np.float32),
        b_gate=np.asarray(inputs['b_gate'], np.float32).reshape(2, 3, D),
        conv_dw=np.asarray(inputs['conv_dw'], np.float32),
        conv_b=np.asarray(inputs['conv_b'], np.float32).reshape(2, 1, D),
        conv_ln_g=np.asarray(inputs['conv_ln_g'], np.float32).reshape(2, 1, D),
        conv_ln_b=np.asarray(inputs['conv_ln_b'], np.float32).reshape(2, 1, D),
        ret_decay=np.asarray(inputs['ret_decay'], np.float32).reshape(2, 8),
        q_norm_g=np.asarray(inputs['q_norm_g'], np.float32),
        k_norm_g=np.asarray(inputs['k_norm_g'], np.float32),
        attn_sink=np.asarray(inputs['attn_sink'], np.float32),
        w_conv_out=np.asarray(inputs['w_conv_out'], np.float32),
        w_ret_out=np.asarray(inputs['w_ret_out'], np.float32),
        w_attn_out=np.asarray(inputs['w_attn_out'], np.float32),
        w_out=np.asarray(inputs['w_out'], np.float32),
    )
    in_maps = []
    for core in range(8):
        b, c = core // 4, core % 4
        xe = np.zeros((TE, D), np.float32)
        lo = c * T - 128
        hi = c * T + T + 128
        slo, shi = max(lo, 0), min(hi, 4 * T)
        xe[slo - lo:shi - lo] = x[b, slo:shi]
        m = dict(shared)
        m['x_ext'] = xe
        m.update(make_consts(core))
        in_maps.append(m)
    return in_maps


_NC = None


def kernel(**inputs):
    global _NC
    if _NC is None:
        _NC = build(2)
    in_maps = make_in_maps(inputs)
    res = run_bass_kernel_spmd(_NC, in_maps, core_ids=list(range(8)))
    out = np.zeros((2, 4 * T, D), np.float32)
    for core in range(8):
        b, c = core // 4, core % 4
        out[b, c * T:(c + 1) * T] = res.results[core]["y_out"]
    return out
```

```python
import os
import numpy as np
import concourse.bass as bass
import concourse.mybir as mybir
from concourse.bass_utils import run_bass_kernel_spmd
from contextlib import ExitStack

F32 = mybir.dt.float32
BF16 = mybir.dt.bfloat16
ALU = mybir.AluOpType
AF = mybir.ActivationFunctionType
AX = mybir.AxisListType

ENG = ['tensor', 'vector', 'scalar', 'gpsimd', 'sync']
EPOCH = 20000
ND = 8
PE, V, A, G, S = 'tensor', 'vector', 'scalar', 'gpsimd', 'sync'


class U:
    __slots__ = ('w', 'rs')

    def __init__(s):
        s.w = None
        s.rs = {}


class KB:
    def __init__(s, nc, stack):
        s.nc = nc
        s.st = stack
        s.cnt = {e: 0 for e in ENG}
        s.nsem = 0
        s.sem = {e: s.new_sem(f'e_{e}') for e in ENG}
        s.hist = {e: [] for e in ENG}
        s.waited = {e: {} for e in ENG}
        s.dsem = {}
        s.dtarget = {}
        s.dcount = {}
        s.n_inst = 0

    def new_sem(s, name):
        s.nsem += 1
        return s.st.enter_context(s.nc.semaphore(f'{name}_{s.nsem}'))

    def _waits(s, engine, r, w):
        deps = {}

        def add(tok):
            key = id(tok[0])
            if key not in deps or deps[key][1] < tok[1]:
                deps[key] = tok
        for u in r:
            if u.w is not None:
                add(u.w)
        for u in w:
            if u.w is not None:
                add(u.w)
            for tok in u.rs.values():
                add(tok)
        waits = []
        wd = s.waited[engine]
        for key, (sem, val, src) in deps.items():
            if engine == PE and src == PE:
                continue
            if wd.get(key, 0) >= val:
                continue
            wd[key] = val
            waits.append((sem, val))
        return waits

    def _emit(s, ename, waits, fn, inc):
        e = getattr(s.nc, ename)
        for sem, val in waits:
            e.wait_ge(sem, val)
        if fn is None:
            return
        ins = fn(e)
        if inc[1] is None:
            ins.then_inc(inc[0])
        else:
            ins.then_inc(inc[0], inc[1])
        s.n_inst += 1

    def op(s, engine, fn, r=(), w=()):
        waits = s._waits(engine, r, w)
        if s.cnt[engine] >= EPOCH:
            s.hist[engine].append((s.sem[engine], s.cnt[engine]))
            s.sem[engine] = s.new_sem(f'e_{engine}')
            s.cnt[engine] = 0
        s.cnt[engine] += 1
        sem = s.sem[engine]
        tok = (sem, s.cnt[engine], engine)
        s._emit(engine, waits, fn, (sem, 1))
        for u in r:
            u.rs[id(sem)] = tok
        for u in w:
            u.w = tok
            u.rs = {}
        return tok

    def dma(s, q, out, in_, r=(), w=(), **kw):
        waits = s._waits(q, r, w)
        if q not in s.dsem:
            s.dsem[q] = [s.new_sem(f'd_{q}{i}') for i in range(ND)]
            s.dtarget[q] = [0] * ND
            s.dcount[q] = 0
        i = s.dcount[q] % ND
        s.dcount[q] += 1
        sem = s.dsem[q][i]
        prev = s.dtarget[q][i]
        if prev > 0 and s.waited[q].get(id(sem), 0) < prev:
            s.waited[q][id(sem)] = prev
            waits.append((sem, prev))
        tgt = prev + 16
        s.dtarget[q][i] = tgt
        tok = (sem, tgt, 'dma')
        s._emit(q, waits, lambda e: e.dma_start(out=out, in_=in_, **kw), (sem, 16))
        for u in r:
            u.rs[id(sem)] = tok
        for u in w:
            u.w = tok
            u.rs = {}
        return tok

    def custom(s, engine, fn, inc_sem, inc_val, r=(), w=()):
        waits = s._waits(engine, r, w)
        tok = (inc_sem, inc_val, 'custom')
        s._emit(engine, waits, fn, (inc_sem, None))
        for u in r:
            u.rs[id(inc_sem)] = tok
        for u in w:
            u.w = tok
            u.rs = {}
        return tok

    def barrier(s):
        toks = []
        for e in ENG:
            for sem, c in s.hist[e]:
                toks.append((sem, c))
            if s.cnt[e] > 0:
                toks.append((s.sem[e], s.cnt[e]))
        for q in s.dsem:
            for i in range(ND):
                if s.dtarget[q][i] > 0:
                    toks.append((s.dsem[q][i], s.dtarget[q][i]))
        for e in ENG:
            wd = s.waited[e]
            waits = []
            for sem, val in toks:
                if wd.get(id(sem), 0) >= val:
                    continue
                wd[id(sem)] = val
                waits.append((sem, val))
            s._emit(e, waits, None, None)


D = 1024
T = 2048
TE = 2304
NT = 16
INW = 14848
OFF_CGLU, OFF_CGATE = 0, 2048
OFF_RQ, OFF_RK, OFF_RV, OFF_RG = 3072, 4096, 5120, 7168
OFF_AQ, OFF_AK, OFF_AV, OFF_AG = 9216, 10240, 10496, 10752
OFF_GL = 11776
EPS = 1e-6


def build(NL=2, dbg=False, phases='CTRF'):
    nc = bass.Bass("TRN2", target_bir_lowering=False)

    def din(name, shape):
        return nc.dram_tensor(name, shape, F32, kind="ExternalInput").ap()

    x_ext = din("x_ext", [TE, D])
    norm_g = din("norm_g", [2, D])
    w_in = din("w_in", [2, D, INW])
    b_gate = din("b_gate", [2, 3, D])
    conv_dw = din("conv_dw", [2, 31, D])
    conv_b = din("conv_b", [2, 1, D])
    conv_ln_g = din("conv_ln_g", [2, 1, D])
    conv_ln_b = din("conv_ln_b", [2, 1, D])
    ret_decay = din("ret_decay", [2, 8])
    q_norm_g = din("q_norm_g", [2, 64])
    k_norm_g = din("k_norm_g", [2, 64])
    attn_sink = din("attn_sink", [2, 16])
    w_conv_out = din("w_conv_out", [2, D, D])
    w_ret_out = din("w_ret_out", [2, 2 * D, D])
    w_attn_out = din("w_attn_out", [2, D, D])
    w_out = din("w_out", [2, D, D])
    c_cosr = din("c_cosr", [128, T])
    c_sinr = din("c_sinr", [128, T])
    c_cosa = din("c_cosa", [128, 18, 8])
    c_sina = din("c_sina", [128, 18, 8])
    c_mask = din("c_mask", [128, 512])
    c_ident = din("c_ident", [128, 128])
    c_eseg = din("c_eseg", [128, 32])
    c_ek = din("c_ek", [128, 2])
    c_eq = din("c_eq", [128, 256])
    c_em = din("c_em", [128, 512])
    c_dist = din("c_dist", [128, 8])
    c_sel = din("c_sel", [128, 8])

    y_out = nc.dram_tensor("y_out", [T, D], F32, kind="ExternalOutput").ap()
    okind = "ExternalOutput" if dbg else "Internal"
    OC = nc.dram_tensor("OC", [8, 128, T], BF16, kind=okind).ap()
    OA = nc.dram_tensor("OA", [16, 64, T], BF16, kind=okind).ap()
    OR = nc.dram_tensor("OR", [16, 128, T], BF16, kind=okind).ap()
    x1e = nc.dram_tensor("x1e", [TE, D], F32, kind=okind).ap()
    fs_bounce = nc.dram_tensor("fs_bounce", [512, 512], F32)
    fs_gath = nc.dram_tensor("fs_gath", [2048, 512], F32)
    h_bounce = nc.dram_tensor("h_bounce", [256, D], F32)
    h_gath = nc.dram_tensor("h_gath", [1024, D], F32)
    u_OC, u_OA, u_OR, u_x1e, u_fsb, u_fsg, u_hb, u_hg, u_yout = [U() for _ in range(9)]
    RG = [[0, 1, 2, 3], [4, 5, 6, 7]]

    with ExitStack() as st0:
        k = KB(nc, st0)

        ncount = [0]

        def sb(st, name, shape, dt):
            ncount[0] += 1
            return st.enter_context(nc.sbuf_tensor(f"{name}_{ncount[0]}", shape, dt))

        PF = [st0.enter_context(nc.psum_tensor(f"pf{i}", [128, 512], F32)) for i in range(7)]
        uPF = [U() for _ in range(7)]
        PT = st0.enter_context(nc.psum_tensor("ptb", [128, 8, 128], BF16))
        uPT = U()
        hT = sb(st0, "hT", [128, 8, TE], BF16)
        uhT = [U() for _ in range(18)]
        ident = sb(st0, "ident", [128, 128], F32); identb = sb(st0, "identb", [128, 128], BF16)
        onesf = sb(st0, "onesf", [128, 128], F32); onesb = sb(st0, "onesb", [128, 128], BF16)
        maskb = sb(st0, "maskb", [128, 512], BF16)
        eseg = sb(st0, "eseg", [128, 32], F32); ek = sb(st0, "ek", [128, 2], F32)
        eq = sb(st0, "eq", [128, 256], F32); em = sb(st0, "em", [128, 512], F32)
        dist = sb(st0, "dist", [128, 8], F32); sel = sb(st0, "sel", [128, 8], F32)
        cosa = sb(st0, "cosa", [128, 18, 8], F32); sina = sb(st0, "sina", [128, 18, 8], F32)
        lg = sb(st0, "lg", [128, 8], F32)
        qg = sb(st0, "qg", [128, 64], F32); kg = sb(st0, "kg", [128, 64], F32)
        snk = sb(st0, "snk", [128, 16], F32); snke = sb(st0, "snke", [128, 16], F32)
        negc = sb(st0, "negc", [128, 1], F32); tmpc = sb(st0, "tmpc", [128, 4], F32)
        g2 = sb(st0, "g2", [128, 128], F32)
        prm = sb(st0, "prm", [37, D], F32)
        prT = sb(st0, "prT", [128, 8, 37], F32)
        uC = U()
        uL = U()

        for (t, src) in ((ident, c_ident), (eseg, c_eseg), (ek, c_ek), (eq, c_eq), (em, c_em),
                         (dist, c_dist), (sel, c_sel), (cosa, c_cosa), (sina, c_sina)):
            k.dma(S, t[:], src, w=[uC])
        k.dma(G, identb[:], c_ident, w=[uC])
        k.dma(G, maskb[:], c_mask, w=[uC])
        k.op(V, lambda e: e.memset(onesf[:], 1.0 / 1024.0), w=[uC])
        k.op(V, lambda e: e.memset(onesb[:], 1.0), w=[uC])
        k.barrier()

        def phase_T(l):
            with ExitStack() as st:
                Wt = sb(st, "Wt", [128, 8, 2560], BF16); uWt = U()
                for j in range(5):
                    k.dma(G, Wt[:, :, j * 512:(j + 1) * 512], w_in[l, :, OFF_AQ + j * 512:OFF_AQ + (j + 1) * 512].rearrange("(k p) c -> p k c", p=128), w=[uWt])
                qTg = sb(st, "qTg", [64, 16, 512], BF16); uqT = [U() for _ in range(4)]
                kT = sb(st, "kTres", [64, 4, TE], BF16); ukT = [U() for _ in range(18)]
                Vr = sb(st, "Vres", [128, 18, 256], BF16); uVr = [U() for _ in range(18)]
                sq = sb(st, "sq", [128, 1280], F32); usq = U()
                ssa = sb(st, "ssa", [128, 20], F32); rsa = sb(st, "rsa", [128, 20], F32); urs = U()
                qn = sb(st, "qn", [128, 1280], F32); uqn = U()
                rt = [sb(st, f"rt{i}", [128, 20, 8], F32) for i in range(4)]; urt = [U() for _ in range(4)]
                qb = sb(st, "qb", [128, 1024], BF16); uqb = U()
                kb = sb(st, "kb", [128, 256], BF16); ukb = U()
                sq3 = sq[:].rearrange("p (h d) -> p h d", d=64)
                qn3 = qn[:].rearrange("p (h d) -> p h d", d=64)

                def norm_rot(t, h0, h1):
                    nh = h1 - h0
                    k.op(V, lambda e: e.tensor_reduce(out=ssa[:, h0:h1], in_=sq3[:, h0:h1, :], axis=AX.X, op=ALU.add), r=[usq], w=[urs])
                    k.op(V, lambda e: e.tensor_scalar(out=rsa[:, h0:h1], in0=ssa[:, h0:h1], scalar1=1.0 / 64.0, scalar2=EPS, op0=ALU.mult, op1=ALU.add), r=[urs], w=[urs])
                    k.op(A, lambda e: e.activation(out=rsa[:, h0:h1], in_=rsa[:, h0:h1], func=AF.Sqrt), r=[urs], w=[urs])
                    k.op(V, lambda e: e.reciprocal(out=rsa[:, h0:h1], in_=rsa[:, h0:h1]), r=[urs], w=[urs])
                    if h0 == 0:
                        for half in range(2):
                            k.op(V, lambda e, half=half: e.tensor_tensor(out=qn3[:, half * 8:(half + 1) * 8, :], in0=PF[half][:].rearrange("p (h d) -> p h d", d=64), in1=rsa[:, half * 8:(half + 1) * 8].unsqueeze(2).broadcast_to([128, 8, 64]), op=ALU.mult), r=[uPF[half], urs], w=[uqn])
                        k.op(G, lambda e: e.tensor_tensor(out=qn3[:, 0:16, :], in0=qn3[:, 0:16, :], in1=qg[:].unsqueeze(1).broadcast_to([128, 16, 64]), op=ALU.mult), r=[uqn, uL], w=[uqn])
                    else:
                        k.op(V, lambda e: e.tensor_tensor(out=qn3[:, 16:20, :], in0=PF[2][:, 0:256].rearrange("p (h d) -> p h d", d=64), in1=rsa[:, 16:20].unsqueeze(2).broadcast_to([128, 4, 64]), op=ALU.mult), r=[uPF[2], urs], w=[uqn])
                        k.op(G, lambda e: e.tensor_tensor(out=qn3[:, 16:20, :], in0=qn3[:, 16:20, :], in1=kg[:].unsqueeze(1).broadcast_to([128, 4, 64]), op=ALU.mult), r=[uqn, uL], w=[uqn])
                    cb = cosa[:, t, :].unsqueeze(1).broadcast_to([128, nh, 8])
                    sbb = sina[:, t, :].unsqueeze(1).broadcast_to([128, nh, 8])
                    x1 = qn3[:, h0:h1, 0:8]
                    x2 = qn3[:, h0:h1, 8:16]
                    k.op(V, lambda e: e.tensor_tensor(out=rt[0][:, h0:h1, :], in0=x1, in1=cb, op=ALU.mult), r=[uqn, uC], w=[urt[0]])
                    k.op(G, lambda e: e.tensor_tensor(out=rt[1][:, h0:h1, :], in0=x2, in1=sbb, op=ALU.mult), r=[uqn, uC], w=[urt[1]])
                    k.op(V, lambda e: e.tensor_tensor(out=rt[2][:, h0:h1, :], in0=x2, in1=cb, op=ALU.mult), r=[uqn, uC], w=[urt[2]])
                    k.op(G, lambda e: e.tensor_tensor(out=rt[3][:, h0:h1, :], in0=x1, in1=sbb, op=ALU.mult), r=[uqn, uC], w=[urt[3]])
                    k.op(V, lambda e: e.tensor_tensor(out=x1, in0=rt[0][:, h0:h1, :], in1=rt[1][:, h0:h1, :], op=ALU.subtract), r=[urt[0], urt[1], urt[2], urt[3]], w=[uqn])
                    k.op(G, lambda e: e.tensor_tensor(out=x2, in0=rt[2][:, h0:h1, :], in1=rt[3][:, h0:h1, :], op=ALU.add), r=[urt[2], urt[3]], w=[uqn])

                for t in range(18):
                    for kk in range(8):
                        k.op(PE, lambda e, kk=kk, t=t: e.matmul(PF[2][:], lhsT=hT[:, kk, t * 128:(t + 1) * 128], rhs=Wt[:, kk, 1024:1536], start=(kk == 0), stop=(kk == 7)), r=[uWt, uhT[t]], w=[uPF[2]])
                    k.op(A, lambda e: e.activation(out=sq[:, 1024:1280], in_=PF[2][:, 0:256], func=AF.Square), r=[uPF[2]], w=[usq])
                    norm_rot(t, 16, 20)
                    k.op(A, lambda e: e.activation(out=kb[:], in_=qn[:, 1024:1280], func=AF.Copy), r=[uqn], w=[ukb])
                    k.op(A, lambda e, t=t: e.activation(out=Vr[:, t, :], in_=PF[2][:, 256:512], func=AF.Copy), r=[uPF[2]], w=[uVr[t]])
                    for g in range(4):
                        k.op(PE, lambda e, g=g: e.transpose(out=PT[0:64, g, :], in_=kb[:, g * 64:(g + 1) * 64], identity=identb[:]), r=[ukb, uC], w=[uPT])
                    k.op(V, lambda e, t=t: e.tensor_copy(out=kT[:, :, t * 128:(t + 1) * 128], in_=PT[0:64, 0:4, :]), r=[uPT], w=[ukT[t]])
                if dbg == 'T1':
                    k.barrier()
                    return
                gT = sb(st, "gT", [64, 16, 512], BF16); ugT = U()
                pt = [sb(st, f"pt{i}", [128, 512], BF16) for i in range(3)]; upt = [U() for _ in range(3)]
                den = sb(st, "den", [64, 512], F32); uden = U()
                rec = sb(st, "rec", [64, 512], F32); urec = U()
                on = sb(st, "on", [64, 512], F32); uon = U()
                oaT = sb(st, "oaT", [64, 16, 512], BF16); uoaT = U()
                for tg in range(4):
                    e0 = 128 + tg * 512
                    hr = uhT[e0 // 128:(e0 + 512) // 128]
                    for nn in range(4):
                        t = tg * 4 + nn + 1
                        for half in range(2):
                            for kk in range(8):
                                k.op(PE, lambda e, half=half, kk=kk, t=t: e.matmul(PF[half][:], lhsT=hT[:, kk, t * 128:(t + 1) * 128], rhs=Wt[:, kk, half * 512:(half + 1) * 512], start=(kk == 0), stop=(kk == 7)), r=[uWt, uhT[t]], w=[uPF[half]])
                            k.op(A, lambda e, half=half: e.activation(out=sq[:, half * 512:(half + 1) * 512], in_=PF[half][:], func=AF.Square), r=[uPF[half]], w=[usq])
                        norm_rot(t, 0, 16)
                        k.op(A, lambda e: e.activation(out=qb[:], in_=qn[:, 0:1024], func=AF.Copy), r=[uqn], w=[uqb])
                        for rr in range(2):
                            for j in range(8):
                                hh = rr * 8 + j
                                k.op(PE, lambda e, j=j, hh=hh: e.transpose(out=PT[0:64, j, :], in_=qb[:, hh * 64:(hh + 1) * 64], identity=identb[:]), r=[uqb, uC], w=[uPT])
                            k.op(V, lambda e, rr=rr, nn=nn: e.tensor_copy(out=qTg[:, rr * 8:(rr + 1) * 8, nn * 128:(nn + 1) * 128], in_=PT[0:64, :, :]), r=[uPT], w=[uqT[nn]])
                    for hh in range(16):
                        ps, ups = PF[hh % 2], uPF[hh % 2]
                        for kk in range(8):
                            k.op(PE, lambda e, ps=ps, hh=hh, kk=kk, e0=e0: e.matmul(ps[0:64, :], lhsT=Wt[:, kk, 1536 + hh * 64:1536 + (hh + 1) * 64], rhs=hT[:, kk, e0:e0 + 512], start=(kk == 0), stop=(kk == 7)), r=[uWt] + hr, w=[ups])
                        k.op(A, lambda e, ps=ps, hh=hh: e.activation(out=gT[:, hh, :], in_=ps[0:64, :], func=AF.Silu), r=[ups], w=[ugT])
                    for nn in range(4):
                        n = tg * 4 + nn
                        t = n + 1
                        for g in range(4):
                            for mi, m in enumerate((t - 1, t, t + 1)):
                                k.op(PE, lambda e, mi=mi, m=m, g=g, nn=nn: e.matmul(PF[2 + mi][:], lhsT=kT[:, g, m * 128:(m + 1) * 128], rhs=qTg[:, 4 * g:4 * g + 4, nn * 128:(nn + 1) * 128], start=True, stop=True), r=[ukT[m], uqT[nn]], w=[uPF[2 + mi]])
                                k.op(A, lambda e, mi=mi: e.activation(out=pt[mi][:], in_=PF[2 + mi][:], func=AF.Exp, bias=negc[:, 0:1], scale=0.125), r=[uPF[2 + mi], uL], w=[upt[mi]])
                            mL = 256 if t == 1 else 0
                            mR = 384 if t == 16 else 128
                            k.op(V, lambda e, mL=mL: e.tensor_tensor(out=pt[0][:].rearrange("p (a i) -> p a i", a=4), in0=pt[0][:].rearrange("p (a i) -> p a i", a=4), in1=maskb[:, mL:mL + 128].unsqueeze(1).broadcast_to([128, 4, 128]), op=ALU.mult), r=[upt[0], uC], w=[upt[0]])
                            k.op(G, lambda e, mR=mR: e.tensor_tensor(out=pt[2][:].rearrange("p (a i) -> p a i", a=4), in0=pt[2][:].rearrange("p (a i) -> p a i", a=4), in1=maskb[:, mR:mR + 128].unsqueeze(1).broadcast_to([128, 4, 128]), op=ALU.mult), r=[upt[2], uC], w=[upt[2]])
                            for mi, m in enumerate((t - 1, t, t + 1)):
                                k.op(PE, lambda e, mi=mi, m=m, g=g: e.matmul(PF[5][0:64, :], lhsT=Vr[:, m, g * 64:(g + 1) * 64], rhs=pt[mi][:], start=(mi == 0), stop=(mi == 2)), r=[uVr[m], upt[mi]], w=[uPF[5]])
                            for mi, m in enumerate((t - 1, t, t + 1)):
                                k.op(PE, lambda e, mi=mi: e.matmul(PF[6][0:64, :], lhsT=onesb[:, 0:64], rhs=pt[mi][:], start=(mi == 0), stop=(mi == 2)), r=[uC, upt[mi]], w=[uPF[6]])
                            for ei in range(4):
                                hd = 4 * g + ei
                                k.op(V, lambda e, ei=ei, hd=hd: e.tensor_scalar_add(out=den[:, ei * 128:(ei + 1) * 128], in0=PF[6][0:64, ei * 128:(ei + 1) * 128], scalar1=snke[0:64, hd:hd + 1]), r=[uPF[6], uL], w=[uden])
                            k.op(V, lambda e: e.reciprocal(out=rec[:], in_=den[:]), r=[uden], w=[urec])
                            k.op(V, lambda e: e.tensor_tensor(out=on[:], in0=PF[5][0:64, :], in1=rec[:], op=ALU.mult), r=[uPF[5], urec], w=[uon])
                            k.op(G, lambda e, g=g, nn=nn: e.tensor_tensor(out=oaT[:, 4 * g:4 * g + 4, nn * 128:(nn + 1) * 128], in0=on[:].rearrange("p (a i) -> p a i", a=4), in1=gT[:, 4 * g:4 * g + 4, nn * 128:(nn + 1) * 128], op=ALU.mult), r=[uon, ugT], w=[uoaT])
                    k.dma(S, OA[:, :, tg * 512:(tg + 1) * 512].rearrange("h d t -> d h t"), oaT[:], r=[uoaT], w=[u_OA])
                k.barrier()

        def phase_R(l):
            with ExitStack() as st:
                Wr = sb(st, "Wr", [128, 8, 2048], BF16); uWr = U()
                qT = sb(st, "rqT", [128, 2, T], BF16); uq = [U() for _ in range(4)]
                kT = sb(st, "rkT", [128, 2, T], BF16); ukk = [U() for _ in range(4)]
                Vv = sb(st, "rV", [128, 16, 512], BF16); uV = [U() for _ in range(16)]
                kt = sb(st, "rkt", [128, 16, 256], BF16); ukt = [U() for _ in range(16)]
                Pp = sb(st, "rP", [128, 16, 512], F32); uP = [U() for _ in range(16)]
                cs = sb(st, "rcs", [128, 2, 512], F32); ucs = U()
                tq = [sb(st, f"rtq{i}", [128, 512], F32) for i in range(4)]; utq = [U() for _ in range(4)]
                Sf = sb(st, "Sf", [128, 2, 512], F32); Sb = sb(st, "Sb", [128, 2, 512], F32); uSf = U(); uSb = U()
                Sfb = sb(st, "Sfb", [128, 2, 512], BF16); Sbb = sb(st, "Sbb", [128, 2, 512], BF16); uSfb = U(); uSbb = U()
                fsum = sb(st, "fsum", [128, 4, 512], F32); ufs = U()
                DTm = sb(st, "DTm", [128, 128], F32); dq = sb(st, "dq", [128, 256], F32); dsg = sb(st, "dsg", [128, 32], F32)
                dk = sb(st, "dk", [128, 2], F32); gC = sb(st, "gC", [128, 2], F32); wr = sb(st, "wr", [128, 8], F32)
                tmpD = sb(st, "tmpD", [128, 256], F32); uH = U()
                kfs = sb(st, "kfs", [128, 256], BF16); kbs = sb(st, "kbs", [128, 256], BF16); ukfs = U(); ukbs = U()
                qd = sb(st, "qd", [128, 2, 128], BF16); uqd = U()
                kdc = sb(st, "kdc", [128, 256], BF16); ukdc = U()
                sd = sb(st, "sd", [128, 128], BF16); usd = U()
                oo = sb(st, "oo", [128, 512], F32); uoo = U()
                onn = sb(st, "onn", [128, 512], F32); uonn = U()
                sgg = sb(st, "sgg", [128, 512], F32); usgg = U()
                og = sb(st, "og", [128, 512], BF16); uog = U()
                og_b = sb(st, "og_b", [128, 512], BF16); uog_b = U()
                bst = sb(st, "bst", [128, 6], F32); mv = sb(st, "mv", [128, 2], F32); ubn = U()
                orT = sb(st, "orT", [128, 4, 512], BF16); uorT = U()
                RSTOP = int(os.environ.get("RSTOP", "0"))

                class _Stop(Exception):
                    pass

                def ck(n_):
                    if RSTOP == n_:
                        raise _Stop()
                try:
                  for h in range(4):
                      wl = [(1024, OFF_RV + h * 512), (1536, OFF_RG + h * 512)]
                      if h % 2 == 0:
                          wl = [(0, OFF_RQ + h * 256), (512, OFF_RK + h * 256)] + wl
                      for (dst, off) in wl:
                          k.dma(G, Wr[:, :, dst:dst + 512], w_in[l, :, off:off + 512].rearrange("(k p) c -> p k c", p=128), w=[uWr])
                      qc0 = (h % 2) * 256
                      kc0 = 512 + (h % 2) * 256
                      lgf = lg[:, h:h + 1]
                      lgb = lg[:, 4 + h:5 + h]
                      hc = dict(r=[uL, uC, uH], w=[uH])
                      k.op(A, lambda e: e.activation(out=dsg[:, 0:16], in_=eseg[:, 0:16], func=AF.Exp, scale=lgf), **hc)
                      k.op(A, lambda e: e.activation(out=dsg[:, 16:32], in_=eseg[:, 16:32], func=AF.Exp, scale=lgb), **hc)
                      k.op(A, lambda e: e.activation(out=dk[:, 0:1], in_=ek[:, 0:1], func=AF.Exp, scale=lgf), **hc)
                      k.op(A, lambda e: e.activation(out=dk[:, 1:2], in_=ek[:, 1:2], func=AF.Exp, scale=lgb), **hc)
                      k.op(A, lambda e: e.activation(out=dq[:, 0:128], in_=eq[:, 0:128], func=AF.Exp, scale=lgf), **hc)
                      k.op(A, lambda e: e.activation(out=dq[:, 128:256], in_=eq[:, 128:256], func=AF.Exp, scale=lgb), **hc)
                      k.op(V, lambda e: e.tensor_scalar_mul(out=dq[:], in0=dq[:], scalar1=1.0 / 16.0), **hc)
                      k.op(A, lambda e: e.activation(out=gC[:, 0:1], in_=lgf, func=AF.Exp, scale=128.0), **hc)
                      k.op(A, lambda e: e.activation(out=gC[:, 1:2], in_=lgb, func=AF.Exp, scale=128.0), **hc)
                      k.op(A, lambda e: e.activation(out=wr[:, 0:4], in_=dist[:, 0:4], func=AF.Exp, scale=lgf), **hc)
                      k.op(A, lambda e: e.activation(out=wr[:, 4:8], in_=dist[:, 4:8], func=AF.Exp, scale=lgb), **hc)
                      k.op(A, lambda e: e.activation(out=tmpD[:, 0:128], in_=em[:, 0:128], func=AF.Exp, scale=lgf), **hc)
                      k.op(A, lambda e: e.activation(out=tmpD[:, 128:256], in_=em[:, 128:256], func=AF.Exp, scale=lgb), **hc)
                      k.op(V, lambda e: e.tensor_tensor(out=tmpD[:], in0=tmpD[:], in1=em[:, 256:512], op=ALU.mult), **hc)
                      k.op(V, lambda e: e.tensor_tensor(out=DTm[:], in0=tmpD[:, 0:128], in1=tmpD[:, 128:256], op=ALU.add), **hc)
                      k.op(V, lambda e: e.tensor_scalar_mul(out=DTm[:], in0=DTm[:], scalar1=1.0 / 16.0), **hc)
                      ck(1)
                      for tg in range(4):
                          e0 = 128 + tg * 512
                          hr = uhT[e0 // 128:(e0 + 512) // 128]
                          k.dma(S, cs[:, 0, :], c_cosr[:, tg * 512:(tg + 1) * 512], w=[ucs])
                          k.dma(S, cs[:, 1, :], c_sinr[:, tg * 512:(tg + 1) * 512], w=[ucs])
                          for (col0, dstT, ud) in ((qc0, qT, uq[tg]), (kc0, kT, ukk[tg])):
                              for dc in range(2):
                                  for kk in range(8):
                                      k.op(PE, lambda e, dc=dc, kk=kk, col0=col0, e0=e0: e.matmul(PF[dc][:], lhsT=Wr[:, kk, col0 + dc * 128:col0 + (dc + 1) * 128], rhs=hT[:, kk, e0:e0 + 512], start=(kk == 0), stop=(kk == 7)), r=[uWr] + hr, w=[uPF[dc]])
                              k.op(V, lambda e: e.tensor_tensor(out=tq[0][:], in0=PF[0][:], in1=cs[:, 0, :], op=ALU.mult), r=[uPF[0], ucs], w=[utq[0]])
                              k.op(V, lambda e: e.tensor_tensor(out=tq[1][:], in0=PF[1][:], in1=cs[:, 1, :], op=ALU.mult), r=[uPF[1], ucs], w=[utq[1]])
                              k.op(V, lambda e: e.tensor_tensor(out=tq[2][:], in0=PF[1][:], in1=cs[:, 0, :], op=ALU.mult), r=[uPF[1], ucs], w=[utq[2]])
                              k.op(V, lambda e: e.tensor_tensor(out=tq[3][:], in0=PF[0][:], in1=cs[:, 1, :], op=ALU.mult), r=[uPF[0], ucs], w=[utq[3]])
                              k.op(V, lambda e, dstT=dstT, tg=tg: e.tensor_tensor(out=dstT[:, 0, tg * 512:(tg + 1) * 512], in0=tq[0][:], in1=tq[1][:], op=ALU.subtract), r=[utq[0], utq[1]], w=[ud])
                              k.op(G, lambda e, dstT=dstT, tg=tg: e.tensor_tensor(out=dstT[:, 1, tg * 512:(tg + 1) * 512], in0=tq[2][:], in1=tq[3][:], op=ALU.add), r=[utq[2], utq[3]], w=[ud])
                          ck(2)
                          for nn in range(4):
                              n = tg * 4 + nn
                              t = n + 1
                              for kk in range(8):
                                  k.op(PE, lambda e, kk=kk, t=t: e.matmul(PF[2][:], lhsT=hT[:, kk, t * 128:(t + 1) * 128], rhs=Wr[:, kk, 1024:1536], start=(kk == 0), stop=(kk == 7)), r=[uWr, uhT[t]], w=[uPF[2]])
                              k.op(A, lambda e, n=n: e.activation(out=Vv[:, n, :], in_=PF[2][:], func=AF.Copy), r=[uPF[2]], w=[uV[n]])
                              ck(3)
                              for dc in range(2):
                                  k.op(PE, lambda e, dc=dc, n=n: e.transpose(out=PT[:, dc, :], in_=kT[:, dc, n * 128:(n + 1) * 128], identity=identb[:]), r=[ukk[tg], uC], w=[uPT])
                              ptv = PT[:, 0:2, :]
                              k.op(A, lambda e, n=n, ptv=ptv: e.activation(out=kfs[:].rearrange("p (a d) -> p a d", a=2), in_=ptv, func=AF.Copy, scale=dsg[:, n:n + 1]), r=[uPT, uH], w=[ukfs])
                              k.op(A, lambda e, n=n, ptv=ptv: e.activation(out=kbs[:].rearrange("p (a d) -> p a d", a=2), in_=ptv, func=AF.Copy, scale=dsg[:, 16 + n:17 + n]), r=[uPT, uH], w=[ukbs])
                              k.op(A, lambda e, n=n, ptv=ptv: e.activation(out=kt[:, n, :].rearrange("p (a d) -> p a d", a=2), in_=ptv, func=AF.Copy), r=[uPT], w=[ukt[n]])
                              ck(4)
                              for dc in range(2):
                                  k.op(PE, lambda e, dc=dc, n=n: e.matmul(PF[3 + dc][:], lhsT=kfs[:, dc * 128:(dc + 1) * 128], rhs=Vv[:, n, :], start=(n == 0), stop=(n == 15)), r=[ukfs, uV[n]], w=[uPF[3 + dc]])
                                  k.op(PE, lambda e, dc=dc, n=n: e.matmul(PF[5 + dc][:], lhsT=kbs[:, dc * 128:(dc + 1) * 128], rhs=Vv[:, n, :], start=(n == 0), stop=(n == 15)), r=[ukbs, uV[n]], w=[uPF[5 + dc]])
                      if dbg == 'R0':
                          k.barrier()
                          return
                      for j in range(4):
                          k.op(A, lambda e, j=j: e.activation(out=fsum[:, j, :], in_=PF[3 + j][:], func=AF.Copy), r=[uPF[3 + j]], w=[ufs])
                      k.dma(S, fs_bounce.ap().rearrange("(j p) v -> p j v", p=128), fsum[:], r=[ufs], w=[u_fsb])
                      ccs = k.new_sem("cc")
                      k.custom(G, lambda e: e.collective_compute("AllGather", ALU.bypass, replica_groups=RG, ins=[fs_bounce.ap().opt()], outs=[fs_gath.ap().opt()]), ccs, 1, r=[u_fsb], w=[u_fsg])
                      for r_ in range(4):
                          k.dma(S, fsum[:], fs_gath.ap()[r_ * 512:(r_ + 1) * 512, :].rearrange("(j p) v -> p j v", p=128), r=[u_fsg], w=[ufs])
                          for dirn, (Sx, uSx) in enumerate(((Sf, uSf), (Sb, uSb))):
                              for dc in range(2):
                                  wcol = wr[:, dirn * 4 + r_:dirn * 4 + r_ + 1]
                                  if r_ == 0:
                                      k.op(V, lambda e, Sx=Sx, dc=dc, dirn=dirn, wcol=wcol: e.tensor_scalar_mul(out=Sx[:, dc, :], in0=fsum[:, dirn * 2 + dc, :], scalar1=wcol), r=[ufs, uH], w=[uSx])
                                  else:
                                      k.op(V, lambda e, Sx=Sx, dc=dc, dirn=dirn, wcol=wcol: e.scalar_tensor_tensor(out=Sx[:, dc, :], in0=fsum[:, dirn * 2 + dc, :], scalar=wcol, in1=Sx[:, dc, :], op0=ALU.mult, op1=ALU.add), r=[ufs, uH, uSx], w=[uSx])
                      k.op(A, lambda e: e.activation(out=Sfb[:], in_=Sf[:], func=AF.Copy), r=[uSf], w=[uSfb])
                      k.op(A, lambda e: e.activation(out=Sbb[:], in_=Sb[:], func=AF.Copy), r=[uSb], w=[uSbb])
                      if dbg == 'RAG':
                          k.barrier()
                          return
                      for n in range(15, -1, -1):
                          tg = n // 4
                          k.op(A, lambda e, n=n: e.activation(out=kdc[:], in_=kt[:, n, :], func=AF.Copy, scale=dk[:, 1:2]), r=[ukt[n], uH], w=[ukdc])
                          for dc in range(2):
                              k.op(PE, lambda e, dc=dc, n=n: e.matmul(PF[1 + dc][:], lhsT=kdc[:, dc * 128:(dc + 1) * 128], rhs=Vv[:, n, :], start=True, stop=True), r=[ukdc, uV[n]], w=[uPF[1 + dc]])
                          k.op(V, lambda e, n=n: e.tensor_tensor(out=qd[:], in0=qT[:, :, n * 128:(n + 1) * 128], in1=dq[:, 128:256].unsqueeze(1).broadcast_to([128, 2, 128]), op=ALU.mult), r=[uq[tg], uH], w=[uqd])
                          for dc in range(2):
                              k.op(PE, lambda e, dc=dc: e.matmul(PF[0][:], lhsT=qd[:, dc, :], rhs=Sbb[:, dc, :], start=(dc == 0), stop=(dc == 1)), r=[uqd, uSbb], w=[uPF[0]])
                          for dc in range(2):
                              k.op(V, lambda e, dc=dc: e.scalar_tensor_tensor(out=Sb[:, dc, :], in0=Sb[:, dc, :], scalar=gC[:, 1:2], in1=PF[1 + dc][:], op0=ALU.mult, op1=ALU.add), r=[uPF[1 + dc], uH, uSb], w=[uSb])
                          k.op(A, lambda e: e.activation(out=Sbb[:], in_=Sb[:], func=AF.Copy), r=[uSb], w=[uSbb])
                          k.op(A, lambda e, n=n: e.activation(out=Pp[:, n, :], in_=PF[0][:], func=AF.Copy), r=[uPF[0]], w=[uP[n]])
                      Sfb2 = [Sfb, Sbb]
                      uSfb2 = [uSfb, uSbb]
                      og2 = [og, og_b]
                      uog2 = [uog, uog_b]

                      def finish(n):
                          tg = n // 4
                          for vc in range(4):
                              k.op(PE, lambda e, vc=vc, n=n: e.transpose(out=PT[:, vc, :], in_=og2[n % 2][:, vc * 128:(vc + 1) * 128], identity=identb[:]), r=[uog2[n % 2], uC], w=[uPT])
                          k.op(A, lambda e, n=n: e.activation(out=orT[:, :, (n % 4) * 128:(n % 4 + 1) * 128], in_=PT[:, 0:4, :], func=AF.Copy), r=[uPT], w=[uorT])
                          if n % 4 == 3:
                              k.dma(S, OR[h * 4:(h + 1) * 4, :, tg * 512:(tg + 1) * 512].rearrange("c p t -> p c t"), orT[:], r=[uorT], w=[u_OR])
                      for n in range(16):
                          tg = n // 4
                          t = n + 1
                          cur, nxt = n % 2, (n + 1) % 2
                          k.op(A, lambda e, n=n: e.activation(out=kdc[:], in_=kt[:, n, :], func=AF.Copy, scale=dk[:, 0:1]), r=[ukt[n], uH], w=[ukdc])
                          for dc in range(2):
                              k.op(PE, lambda e, dc=dc, n=n: e.matmul(PF[1 + dc][:], lhsT=kdc[:, dc * 128:(dc + 1) * 128], rhs=Vv[:, n, :], start=True, stop=True), r=[ukdc, uV[n]], w=[uPF[1 + dc]])
                          for kk in range(8):
                              k.op(PE, lambda e, kk=kk, t=t: e.matmul(PF[5][:], lhsT=hT[:, kk, t * 128:(t + 1) * 128], rhs=Wr[:, kk, 1536:2048], start=(kk == 0), stop=(kk == 7)), r=[uWr, uhT[t]], w=[uPF[5]])
                          k.op(A, lambda e: e.activation(out=sgg[:], in_=PF[5][:], func=AF.Silu), r=[uPF[5]], w=[usgg])
                          for dc in range(2):
                              k.op(PE, lambda e, dc=dc, n=n: e.matmul(PF[3][:, 0:128], lhsT=kT[:, dc, n * 128:(n + 1) * 128], rhs=qT[:, dc, n * 128:(n + 1) * 128], start=(dc == 0), stop=(dc == 1)), r=[ukk[tg], uq[tg]], w=[uPF[3]])
                          k.op(V, lambda e: e.tensor_tensor(out=sd[:], in0=PF[3][:, 0:128], in1=DTm[:], op=ALU.mult), r=[uPF[3], uH], w=[usd])
                          k.op(G, lambda e, n=n: e.tensor_tensor(out=qd[:], in0=qT[:, :, n * 128:(n + 1) * 128], in1=dq[:, 0:128].unsqueeze(1).broadcast_to([128, 2, 128]), op=ALU.mult), r=[uq[tg], uH], w=[uqd])
                          k.op(PE, lambda e, n=n: e.matmul(PF[4][:], lhsT=sd[:], rhs=Vv[:, n, :], start=True, stop=False), r=[usd, uV[n]], w=[uPF[4]])
                          for dc in range(2):
                              k.op(PE, lambda e, dc=dc, cur=cur: e.matmul(PF[4][:], lhsT=qd[:, dc, :], rhs=Sfb2[cur][:, dc, :], start=False, stop=(dc == 1)), r=[uqd, uSfb2[cur]], w=[uPF[4]])
                          if n > 0:
                              finish(n - 1)
                          for dc in range(2):
                              k.op(V, lambda e, dc=dc: e.scalar_tensor_tensor(out=Sf[:, dc, :], in0=Sf[:, dc, :], scalar=gC[:, 0:1], in1=PF[1 + dc][:], op0=ALU.mult, op1=ALU.add), r=[uPF[1 + dc], uH, uSf], w=[uSf])
                          k.op(A, lambda e, nxt=nxt: e.activation(out=Sfb2[nxt][:], in_=Sf[:], func=AF.Copy), r=[uSf], w=[uSfb2[nxt]])
                          k.op(V, lambda e, n=n: e.tensor_tensor(out=oo[:], in0=PF[4][:], in1=Pp[:, n, :], op=ALU.add), r=[uPF[4], uP[n]], w=[uoo])
                          k.op(V, lambda e: e.bn_stats(out=bst[:], in_=oo[:]), r=[uoo], w=[ubn])
                          k.op(V, lambda e: e.bn_aggr(out=mv[:], in_=bst[:]), r=[ubn], w=[ubn])
                          k.op(V, lambda e: e.tensor_scalar_add(out=mv[:, 1:2], in0=mv[:, 1:2], scalar1=EPS), r=[ubn], w=[ubn])
                          k.op(A, lambda e: e.activation(out=mv[:, 1:2], in_=mv[:, 1:2], func=AF.Sqrt), r=[ubn], w=[ubn])
                          k.op(V, lambda e: e.reciprocal(out=mv[:, 1:2], in_=mv[:, 1:2]), r=[ubn], w=[ubn])
                          k.op(V, lambda e: e.tensor_scalar(out=onn[:], in0=oo[:], scalar1=mv[:, 0:1], scalar2=mv[:, 1:2], op0=ALU.subtract, op1=ALU.mult), r=[uoo, ubn], w=[uonn])
                          k.op(G, lambda e, cur=cur: e.tensor_tensor(out=og2[cur][:], in0=onn[:], in1=sgg[:], op=ALU.mult), r=[uonn, usgg], w=[uog2[cur]])
                      finish(15)
                except _Stop:
                    pass
                k.barrier()

        def phase_F(l, xsrc, last):
            with ExitStack() as st:
                mg = sb(st, "mg", [128, 8, T], F32); umg = [U() for _ in range(4)]
                sgm = sb(st, "sgm", [128, 512], F32); usgm = U()
                tmpm = sb(st, "tmpm", [128, 512], F32); utmpm = U()
                for bi, (nk, kp) in enumerate(((8, 128), (16, 128), (16, 64))):
                    with ExitStack() as st2:
                        Wb = sb(st2, f"Wb{bi}", [kp, nk, D], BF16); uWb = U()
                        Wg = sb(st2, f"Wg{bi}", [128, 8, D], BF16); uWg = U()
                        ob = sb(st2, f"ob{bi}", [kp, nk, 512], BF16); uob = U()
                        src_w = (w_conv_out, w_ret_out, w_attn_out)[bi]
                        if bi == 2:
                            view = src_w[l].rearrange("(h d) c -> d h c", d=64)
                        else:
                            view = src_w[l].rearrange("(k p) c -> p k c", p=128)
                        for j in range(0, nk, 4):
                            k.dma(G, Wb[:, j:j + 4, :], view[:, j:j + 4, :], w=[uWb])
                        for j in range(2):
                            c0 = OFF_GL + bi * 1024 + j * 512
                            k.dma(G, Wg[:, :, j * 512:(j + 1) * 512], w_in[l, :, c0:c0 + 512].rearrange("(k p) c -> p k c", p=128), w=[uWg])
                        osrc = (OC, OR, OA)[bi]
                        uos = (u_OC, u_OR, u_OA)[bi]
                        for tg in range(4):
                            e0 = 128 + tg * 512
                            hr = uhT[e0 // 128:(e0 + 512) // 128]
                            k.dma(S, ob[:], osrc[:, :, tg * 512:(tg + 1) * 512].rearrange("c p t -> p c t"), r=[uos], w=[uob])
                            for m in range(8):
                                py, upy = PF[m % 3], uPF[m % 3]
                                pg, upg = PF[3 + m % 3], uPF[3 + m % 3]
                                for j in range(nk):
                                    k.op(PE, lambda e, py=py, j=j, m=m: e.matmul(py[:], lhsT=Wb[:, j, m * 128:(m + 1) * 128], rhs=ob[:, j, :], start=(j == 0), stop=(j == nk - 1)), r=[uWb, uob], w=[upy])
                                for kk in range(8):
                                    k.op(PE, lambda e, pg=pg, kk=kk, m=m, e0=e0: e.matmul(pg[:], lhsT=Wg[:, kk, m * 128:(m + 1) * 128], rhs=hT[:, kk, e0:e0 + 512], start=(kk == 0), stop=(kk == 7)), r=[uWg] + hr, w=[upg])
                                k.op(A, lambda e, pg=pg, m=m, bi=bi: e.activation(out=sgm[:], in_=pg[:], func=AF.Sigmoid, bias=prT[:, m, 34 + bi:35 + bi], scale=1.0), r=[upg, uL], w=[usgm])
                                if bi == 0:
                                    k.op(V, lambda e, py=py, m=m, tg=tg: e.tensor_tensor(out=mg[:, m, tg * 512:(tg + 1) * 512], in0=py[:], in1=sgm[:], op=ALU.mult), r=[upy, usgm], w=[umg[tg]])
                                else:
                                    k.op(V, lambda e, py=py: e.tensor_tensor(out=tmpm[:], in0=py[:], in1=sgm[:], op=ALU.mult), r=[upy, usgm], w=[utmpm])
                                    k.op(G, lambda e, m=m, tg=tg: e.tensor_tensor(out=mg[:, m, tg * 512:(tg + 1) * 512], in0=mg[:, m, tg * 512:(tg + 1) * 512], in1=tmpm[:], op=ALU.add), r=[utmpm, umg[tg]], w=[umg[tg]])
                        k.barrier()
                with ExitStack() as st2:
                    Wo = sb(st2, "Wo", [128, 8, D], BF16); uWo = U()
                    for j in range(2):
                        k.dma(G, Wo[:, j * 4:(j + 1) * 4, :], w_out[l].rearrange("(k p) c -> p k c", p=128)[:, j * 4:(j + 1) * 4, :], w=[uWo])
                    mb = sb(st2, "mb", [128, 8, 128], BF16); umb = U()
                    xt = [sb(st2, f"fxt{i}", [128, D], F32) for i in range(2)]; uxt = [U(), U()]
                    xn = [sb(st2, f"fxn{i}", [128, D], F32) for i in range(2)]; uxn = [U(), U()]
                    for n in range(16):
                        b = n % 2
                        k.op(A, lambda e, n=n: e.activation(out=mb[:], in_=mg[:, :, n * 128:(n + 1) * 128], func=AF.Copy), r=[umg[n // 4]], w=[umb])
                        k.dma(S, xt[b][:], xsrc[128 + n * 128:128 + (n + 1) * 128, :], r=[u_x1e], w=[uxt[b]])
                        for half in range(2):
                            for kk in range(8):
                                k.op(PE, lambda e, half=half, kk=kk: e.matmul(PF[half][:], lhsT=mb[:, kk, :], rhs=Wo[:, kk, half * 512:(half + 1) * 512], start=(kk == 0), stop=(kk == 7)), r=[umb, uWo], w=[uPF[half]])
                            k.op(V, lambda e, half=half, b=b: e.tensor_tensor(out=xn[b][:, half * 512:(half + 1) * 512], in0=PF[half][:], in1=xt[b][:, half * 512:(half + 1) * 512], op=ALU.add), r=[uPF[half], uxt[b]], w=[uxn[b]])
                        if last:
                            k.dma(S, y_out[n * 128:(n + 1) * 128, :], xn[b][:], r=[uxn[b]], w=[u_yout])
                        else:
                            k.dma(S, x1e[128 + n * 128:128 + (n + 1) * 128, :], xn[b][:], r=[uxn[b]], w=[u_x1e])
                            if n == 0:
                                k.dma(S, h_bounce.ap()[0:128, :], xn[b][:], r=[uxn[b]], w=[u_hb])
                            if n == 15:
                                k.dma(S, h_bounce.ap()[128:256, :], xn[b][:], r=[uxn[b]], w=[u_hb])
                    k.barrier()
            if not last:
                ccs = k.new_sem("cch")
                k.custom(G, lambda e: e.collective_compute("AllGather", ALU.bypass, replica_groups=RG, ins=[h_bounce.ap().opt()], outs=[h_gath.ap().opt()]), ccs, 1, r=[u_hb], w=[u_hg])
                with ExitStack() as st2:
                    hb = sb(st2, "hb", [128, 2, D], F32); uhb = U()
                    accL = sb(st2, "accL", [128, D], F32); accR = sb(st2, "accR", [128, D], F32); uaL = U(); uaR = U()
                    for r_ in range(4):
                        k.dma(S, hb[:], h_gath.ap()[r_ * 256:(r_ + 1) * 256, :].rearrange("(a p) f -> p a f", p=128), r=[u_hg], w=[uhb])
                        for (acc, ua, a_, sc) in ((accL, uaL, 1, r_), (accR, uaR, 0, 4 + r_)):
                            if r_ == 0:
                                k.op(V, lambda e, acc=acc, a_=a_, sc=sc: e.tensor_scalar_mul(out=acc[:], in0=hb[:, a_, :], scalar1=sel[:, sc:sc + 1]), r=[uhb, uC], w=[ua])
                            else:
                                k.op(V, lambda e, acc=acc, a_=a_, sc=sc: e.scalar_tensor_tensor(out=acc[:], in0=hb[:, a_, :], scalar=sel[:, sc:sc + 1], in1=acc[:], op0=ALU.mult, op1=ALU.add), r=[uhb, uC, ua], w=[ua])
                    k.dma(S, x1e[0:128, :], accL[:], r=[uaL], w=[u_x1e])
                    k.dma(S, x1e[TE - 128:TE, :], accR[:], r=[uaR], w=[u_x1e])
                    k.barrier()

        for l in range(NL):
            xsrc = x_ext if l == 0 else x1e
            last = (l == NL - 1)
            k.dma(S, lg[:], ret_decay[l:l + 1, :].partition_broadcast(128), w=[uL])
            k.dma(S, qg[:], q_norm_g[l:l + 1, :].partition_broadcast(128), w=[uL])
            k.dma(S, kg[:], k_norm_g[l:l + 1, :].partition_broadcast(128), w=[uL])
            k.dma(S, snk[:], attn_sink[l:l + 1, :].partition_broadcast(128), w=[uL])
            k.dma(S, prm[0:31, :], conv_dw[l], w=[uL])
            k.dma(S, prm[31:32, :], conv_b[l], w=[uL])
            k.dma(S, prm[32:33, :], conv_ln_g[l], w=[uL])
            k.dma(S, prm[33:34, :], conv_ln_b[l], w=[uL])
            k.dma(S, prm[34:37, :], b_gate[l], w=[uL])
            k.op(A, lambda e: e.activation(out=lg[:], in_=lg[:], func=AF.Exp), r=[uL], w=[uL])
            k.op(V, lambda e: e.tensor_scalar_mul(out=lg[:], in0=lg[:], scalar1=-1.0), r=[uL], w=[uL])
            k.op(V, lambda e: e.tensor_tensor(out=g2[:, 0:64], in0=qg[:], in1=qg[:], op=ALU.mult), r=[uL], w=[uL])
            k.op(V, lambda e: e.tensor_tensor(out=g2[:, 64:128], in0=kg[:], in1=kg[:], op=ALU.mult), r=[uL], w=[uL])
            k.op(V, lambda e: e.tensor_reduce(out=tmpc[:, 0:1], in_=g2[:, 0:64], axis=AX.X, op=ALU.max), r=[uL], w=[uL])
            k.op(V, lambda e: e.tensor_reduce(out=tmpc[:, 1:2], in_=g2[:, 64:128], axis=AX.X, op=ALU.max), r=[uL], w=[uL])
            k.op(V, lambda e: e.tensor_tensor(out=tmpc[:, 2:3], in0=tmpc[:, 0:1], in1=tmpc[:, 1:2], op=ALU.mult), r=[uL], w=[uL])
            k.op(A, lambda e: e.activation(out=tmpc[:, 3:4], in_=tmpc[:, 2:3], func=AF.Sqrt), r=[uL], w=[uL])
            k.op(V, lambda e: e.tensor_scalar_mul(out=negc[:], in0=tmpc[:, 3:4], scalar1=-8.0), r=[uL], w=[uL])
            k.op(A, lambda e: e.activation(out=snke[:], in_=snk[:], func=AF.Exp, bias=negc[:, 0:1], scale=1.0), r=[uL], w=[uL])
            for c in range(8):
                k.op(PE, lambda e, c=c: e.transpose(out=PF[0][:, c * 37:(c + 1) * 37], in_=prm[:, c * 128:(c + 1) * 128], identity=ident[0:37, 0:37]), r=[uL, uC], w=[uPF[0]])
            k.op(V, lambda e: e.tensor_copy(out=prT[:].rearrange("p c j -> p (c j)"), in_=PF[0][:, 0:296]), r=[uPF[0]], w=[uL])
            k.barrier()

            with ExitStack() as st:
                g_bc = sb(st, "g_bc", [128, D], F32); ug = U()
                k.dma(S, g_bc[:], norm_g[l:l + 1, :].partition_broadcast(128), w=[ug])
                xt = [sb(st, f"xt{i}", [128, D], F32) for i in range(2)]; uxt = [U(), U()]
                junk = sb(st, "junk", [128, D], BF16); ujunk = U()
                xs = [sb(st, f"xs{i}", [128, D], BF16) for i in range(2)]; uxs = [U(), U()]
                ssq = [sb(st, f"ssq{i}", [128, 2], F32) for i in range(2)]; ussq = [U(), U()]
                for t in range(18):
                    b = t % 2
                    k.dma(S, xt[b][:], xsrc[t * 128:(t + 1) * 128, :], r=[u_x1e], w=[uxt[b]])
                    k.op(V, lambda e, b=b: e.memset(ssq[b][:], 0.0), w=[ussq[b]])
                    k.op(A, lambda e, b=b: e.activation(out=junk[:], in_=xt[b][:], func=AF.Square, accum_out=ssq[b][:, 0:1]), r=[uxt[b]], w=[ujunk, ussq[b]])
                    k.op(V, lambda e, b=b: e.tensor_scalar(out=ssq[b][:, 1:2], in0=ssq[b][:, 0:1], scalar1=1.0 / D, scalar2=EPS, op0=ALU.mult, op1=ALU.add), r=[ussq[b]], w=[ussq[b]])
                    k.op(A, lambda e, b=b: e.activation(out=ssq[b][:, 1:2], in_=ssq[b][:, 1:2], func=AF.Sqrt), r=[ussq[b]], w=[ussq[b]])
                    k.op(V, lambda e, b=b: e.reciprocal(out=ssq[b][:, 1:2], in_=ssq[b][:, 1:2]), r=[ussq[b]], w=[ussq[b]])
                    k.op(V, lambda e, b=b: e.scalar_tensor_tensor(out=xs[b][:], in0=xt[b][:], scalar=ssq[b][:, 1:2], in1=g_bc[:], op0=ALU.mult, op1=ALU.mult), r=[uxt[b], ussq[b], ug], w=[uxs[b]])
                    for c in range(8):
                        k.op(PE, lambda e, b=b, c=c: e.transpose(out=PT[:, c, :], in_=xs[b][:, c * 128:(c + 1) * 128], identity=identb[:]), r=[uxs[b], uC], w=[uPT])
                    k.op(A, lambda e, t=t: e.activation(out=hT[:, :, t * 128:(t + 1) * 128], in_=PT[:], func=AF.Copy), r=[uPT], w=[uhT[t]])
                k.barrier()

            if 'C' in phases:
                with ExitStack() as st:
                    Wc = sb(st, "Wc", [128, 8, 3072], BF16); uWc = U()
                    for j in range(6):
                        k.dma(G, Wc[:, :, j * 512:(j + 1) * 512], w_in[l, :, j * 512:(j + 1) * 512].rearrange("(k p) c -> p k c", p=128), w=[uWc])
                    sig = [sb(st, f"sig{i}", [128, 544], F32) for i in range(2)]; usig = [U(), U()]
                    vv = [sb(st, f"vv{i}", [128, 544], F32) for i in range(2)]; uvv = [U(), U()]
                    acc1 = [sb(st, f"acc1{i}", [128, 512], F32) for i in range(2)]; uacc1 = [U(), U()]
                    acc2 = [sb(st, f"acc2{i}", [128, 512], F32) for i in range(2)]; uacc2 = [U(), U()]
                    tmpk = [sb(st, f"tmpk{i}", [128, 512], F32) for i in range(4)]; utmpk = [U() for _ in range(4)]
                    yy = sb(st, "yy", [128, 8, 512], F32); uyy = [U() for _ in range(8)]
                    ysq = [sb(st, f"ysq{i}", [128, 512], F32) for i in range(2)]; uysq = [U(), U()]
                    mean = sb(st, "mean", [128, 512], F32); msq = sb(st, "msq", [128, 512], F32)
                    rstd = sb(st, "rstd", [128, 512], F32); ustat = U()
                    t1 = [sb(st, f"t1{i}", [128, 512], F32) for i in range(2)]; ut1 = [U(), U()]
                    t2 = [sb(st, f"t2{i}", [128, 512], F32) for i in range(2)]; ut2 = [U(), U()]
                    s1 = [sb(st, f"s1{i}", [128, 512], F32) for i in range(2)]; us1 = [U(), U()]
                    sg = [sb(st, f"sg{i}", [128, 512], F32) for i in range(2)]; usg = [U(), U()]
                    ocT = sb(st, "ocT", [128, 8, 512], BF16); uocT = U()
                    pTls = [PF[4], PF[5]]; uTl = [uPF[4], uPF[5]]
                    pST, pSQ, uST, uSQ = PF[6], PF[4], uPF[6], uPF[4]
                    it = 0
                    tk = 0

                    def proj(tg, c, b):
                        e0 = 128 + tg * 512
                        hr = uhT[(e0 - 16) // 128:(e0 + 528 + 127) // 128]
                        pA, uA, pB, uB = PF[b], uPF[b], PF[2 + b], uPF[2 + b]
                        tb = 0
                        pTl = pTls[b]
                        for (ps, ups, col0, to) in ((pA, uA, c * 128, tb), (pB, uB, 1024 + c * 128, tb + 32)):
                            for kk in range(8):
                                k.op(PE, lambda e, ps=ps, kk=kk, col0=col0: e.matmul(ps[:, 0:512], lhsT=Wc[:, kk, col0:col0 + 128], rhs=hT[:, kk, e0 - 16:e0 + 496], start=(kk == 0), stop=(kk == 7)), r=[uWc] + hr, w=[ups])
                            for kk in range(8):
                                k.op(PE, lambda e, kk=kk, col0=col0, to=to, pTl=pTl: e.matmul(pTl[:, to:to + 32], lhsT=Wc[:, kk, col0:col0 + 128], rhs=hT[:, kk, e0 + 496:e0 + 528], start=(kk == 0), stop=(kk == 7)), r=[uWc] + hr, w=[uTl[b]])
                    for tg in range(4):
                        e0 = 128 + tg * 512
                        hr = uhT[(e0 - 16) // 128:(e0 + 528 + 127) // 128]
                        proj(tg, 0, it % 2)
                        for c in range(8):
                            b = it % 2
                            it += 1
                            pA, uA, pB, uB = PF[b], uPF[b], PF[2 + b], uPF[2 + b]
                            tb = 0
                            pTl = pTls[b]
                            if c < 7:
                                proj(tg, c + 1, it % 2)
                            k.op(A, lambda e, b=b, pB=pB: e.activation(out=sig[b][:, 0:512], in_=pB[:, 0:512], func=AF.Sigmoid), r=[uB], w=[usig[b]])
                            k.op(A, lambda e, b=b, tb=tb, pTl=pTl: e.activation(out=sig[b][:, 512:544], in_=pTl[:, tb + 32:tb + 64], func=AF.Sigmoid), r=[uTl[b]], w=[usig[b]])
                            k.op(V, lambda e, b=b, pA=pA: e.tensor_tensor(out=vv[b][:, 0:512], in0=pA[:, 0:512], in1=sig[b][:, 0:512], op=ALU.mult), r=[uA, usig[b]], w=[uvv[b]])
                            k.op(V, lambda e, b=b, tb=tb, pTl=pTl: e.tensor_tensor(out=vv[b][:, 512:544], in0=pTl[:, tb:tb + 32], in1=sig[b][:, 512:544], op=ALU.mult), r=[uTl[b], usig[b]], w=[uvv[b]])
                            k.op(V, lambda e, c=c, b=b: e.tensor_scalar(out=acc1[b][:], in0=vv[b][:, 1:513], scalar1=prT[:, c, 0:1], scalar2=prT[:, c, 31:32], op0=ALU.mult, op1=ALU.add), r=[uvv[b], uL], w=[uacc1[b]])
                            for kt in range(1, 16):
                                k.op(V, lambda e, c=c, kt=kt, b=b: e.scalar_tensor_tensor(out=acc1[b][:], in0=vv[b][:, kt + 1:kt + 513], scalar=prT[:, c, kt:kt + 1], in1=acc1[b][:], op0=ALU.mult, op1=ALU.add), r=[uvv[b], uL, uacc1[b]], w=[uacc1[b]])
                            k.op(A, lambda e, c=c, b=b: e.activation(out=acc2[b][:], in_=vv[b][:, 17:529], func=AF.Copy, scale=prT[:, c, 16:17]), r=[uvv[b], uL], w=[uacc2[b]])
                            for kt in range(17, 31):
                                j = tk % 4
                                tk += 1
                                k.op(A, lambda e, c=c, kt=kt, j=j, b=b: e.activation(out=tmpk[j][:], in_=vv[b][:, kt + 1:kt + 513], func=AF.Copy, scale=prT[:, c, kt:kt + 1]), r=[uvv[b], uL], w=[utmpk[j]])
                                k.op(G, lambda e, j=j, b=b: e.tensor_tensor(out=acc2[b][:], in0=acc2[b][:], in1=tmpk[j][:], op=ALU.add), r=[utmpk[j], uacc2[b]], w=[uacc2[b]])
                            k.op(G, lambda e, c=c, b=b: e.tensor_tensor(out=yy[:, c, :], in0=acc1[b][:], in1=acc2[b][:], op=ALU.add), r=[uacc1[b], uacc2[b]], w=[uyy[c]])
                        for c in range(8):
                            b = c % 2
                            k.op(A, lambda e, c=c, b=b: e.activation(out=ysq[b][:], in_=yy[:, c, :], func=AF.Square), r=[uyy[c]], w=[uysq[b]])
                            k.op(PE, lambda e, c=c: e.matmul(pST[:], lhsT=onesf[:], rhs=yy[:, c, :], start=(c == 0), stop=(c == 7)), r=[uyy[c], uC], w=[uST])
                            k.op(PE, lambda e, c=c, b=b: e.matmul(pSQ[:], lhsT=onesf[:], rhs=ysq[b][:], start=(c == 0), stop=(c == 7)), r=[uysq[b], uC], w=[uSQ])
                        k.op(V, lambda e: e.tensor_copy(out=mean[:], in_=pST[:]), r=[uST], w=[ustat])
                        k.op(G, lambda e: e.tensor_tensor(out=msq[:], in0=mean[:], in1=mean[:], op=ALU.mult), r=[ustat], w=[ustat])
                        k.op(V, lambda e: e.tensor_tensor(out=rstd[:], in0=pSQ[:], in1=msq[:], op=ALU.subtract), r=[uSQ, ustat], w=[ustat])
                        k.op(V, lambda e: e.tensor_scalar_add(out=rstd[:], in0=rstd[:], scalar1=EPS), r=[ustat], w=[ustat])
                        k.op(A, lambda e: e.activation(out=rstd[:], in_=rstd[:], func=AF.Sqrt), r=[ustat], w=[ustat])
                        k.op(V, lambda e: e.reciprocal(out=rstd[:], in_=rstd[:]), r=[ustat], w=[ustat])
                        for c in range(8):
                            b = c % 2
                            pG, uG = PF[b], uPF[b]
                            for kk in range(8):
                                k.op(PE, lambda e, c=c, kk=kk, pG=pG: e.matmul(pG[:], lhsT=Wc[:, kk, 2048 + c * 128:2048 + (c + 1) * 128], rhs=hT[:, kk, e0:e0 + 512], start=(kk == 0), stop=(kk == 7)), r=[uWc] + hr, w=[uG])
                            k.op(V, lambda e, c=c, b=b: e.tensor_tensor(out=t1[b][:], in0=yy[:, c, :], in1=mean[:], op=ALU.subtract), r=[uyy[c], ustat], w=[ut1[b]])
                            k.op(G, lambda e, b=b: e.tensor_tensor(out=t2[b][:], in0=t1[b][:], in1=rstd[:], op=ALU.mult), r=[ut1[b], ustat], w=[ut2[b]])
                            k.op(A, lambda e, c=c, b=b: e.activation(out=s1[b][:], in_=t2[b][:], func=AF.Silu, bias=prT[:, c, 33:34], scale=prT[:, c, 32:33]), r=[ut2[b], uL], w=[us1[b]])
                            k.op(A, lambda e, b=b, pG=pG: e.activation(out=sg[b][:], in_=pG[:], func=AF.Silu), r=[uG], w=[usg[b]])
                            k.op(V, lambda e, c=c, b=b: e.tensor_tensor(out=ocT[:, c, :], in0=s1[b][:], in1=sg[b][:], op=ALU.mult), r=[us1[b], usg[b]], w=[uocT])
                        k.dma(S, OC[:, :, tg * 512:(tg + 1) * 512].rearrange("c p t -> p c t"), ocT[:], r=[uocT], w=[u_OC])
                    k.barrier()
            if 'T' in phases:
                phase_T(l)
            if 'R' in phases:
                phase_R(l)
            if 'F' in phases:
                phase_F(l, xsrc, last)
        k.barrier()
    return nc


def make_consts(core):
    c = core % 4
    seg = c * T
    p = np.arange(128)
    tt = np.arange(T)
    inv = 10000.0 ** (-np.arange(128, dtype=np.float64) / 128.0)
    ang = inv[:, None] * (seg + tt)[None, :].astype(np.float64)
    cosr = np.cos(ang).astype(np.float32)
    sinr = np.sin(ang).astype(np.float32)
    inva = 500000.0 ** (-np.arange(8, dtype=np.float64) / 8.0)
    pos = (seg - 128 + np.arange(18)[None, :] * 128 + p[:, None]).astype(np.float64)
    anga = pos[:, :, None] * inva[None, None, :]
    cosa = np.cos(anga).astype(np.float32)
    sina = np.sin(anga).astype(np.float32)
    j = p[:, None]
    i = p[None, :]
    maskL = (j >= i).astype(np.float32)
    maskR = (j <= i).astype(np.float32)
    mask = np.concatenate([maskL, maskR, maskL * (1.0 if c > 0 else 0.0), maskR * (1.0 if c < 3 else 0.0)], axis=1)
    eseg = np.zeros((128, 32), np.float32)
    for n in range(16):
        eseg[:, n] = 2047 - (n * 128 + p)
        eseg[:, 16 + n] = n * 128 + p
    ek = np.stack([127 - p, p], axis=1).astype(np.float32)
    eq = np.concatenate([np.tile((np.arange(128) + 1)[None, :], (128, 1)), np.tile((128 - np.arange(128))[None, :], (128, 1))], axis=1).astype(np.float32)
    Ef = np.maximum(i - j, 0); Eb = np.maximum(j - i, 0)
    Mf = (i >= j); Mb = (j > i)
    em = np.concatenate([Ef, Eb, Mf, Mb], axis=1).astype(np.float32)
    BIG = 1.0e6
    dist = np.zeros((128, 8), np.float32)
    sel = np.zeros((128, 8), np.float32)
    for r in range(4):
        dist[:, r] = T * (c - r - 1) if r < c else BIG
        dist[:, 4 + r] = T * (r - c - 1) if r > c else BIG
        sel[:, r] = 1.0 if r == c - 1 else 0.0
        sel[:, 4 + r] = 1.0 if r == c + 1 else 0.0
    return dict(c_cosr=cosr, c_sinr=sinr, c_cosa=cosa, c_sina=sina, c_mask=mask,
                c_ident=np.eye(128, dtype=np.float32), c_eseg=eseg, c_ek=ek, c_eq=eq, c_em=em,
                c_dist=dist, c_sel=sel)


def make_in_maps(inputs):
    x = np.asarray(inputs['x'], np.float32)
    shared = dict(
        norm_g=np.asarray(inputs['norm_g'], np.float32),
        w_in=np.asarray(inputs['w_in'], np.float32),
        b_gate=np.asarray(inputs['b_gate'], np.float32).reshape(2, 3, D),
        conv_dw=np.asarray(inputs['conv_dw'], np.float32),
        conv_b=np.asarray(inputs['conv_b'], np.float32).reshape(2, 1, D),
        conv_ln_g=np.asarray(inputs['conv_ln_g'], np.float32).reshape(2, 1, D),
        conv_ln_b=np.asarray(inputs['conv_ln_b'], np.float32).reshape(2, 1, D),
        ret_decay=np.asarray(inputs['ret_decay'], np.float32).reshape(2, 8),
        q_norm_g=np.asarray(inputs['q_norm_g'], np.float32),
        k_norm_g=np.asarray(inputs['k_norm_g'], np.float32),
        attn_sink=np.asarray(inputs['attn_sink'], np.float32),
        w_conv_out=np.asarray(inputs['w_conv_out'], np.float32),
        w_ret_out=np.asarray(inputs['w_ret_out'], np.float32),
        w_attn_out=np.asarray(inputs['w_attn_out'], np.float32),
        w_out=np.asarray(inputs['w_out'], np.float32),
    )
    in_maps = []
    for core in range(8):
        b, c = core // 4, core % 4
        xe = np.zeros((TE, D), np.float32)
        lo = c * T - 128
        hi = c * T + T + 128
        slo, shi = max(lo, 0), min(hi, 4 * T)
        xe[slo - lo:shi - lo] = x[b, slo:shi]
        m = dict(shared)
        m['x_ext'] = xe
        m.update(make_consts(core))
        in_maps.append(m)
    return in_maps


_NC = None


def kernel(**inputs):
    global _NC
    if _NC is None:
        _NC = build(2)
    in_maps = make_in_maps(inputs)
    res = run_bass_kernel_spmd(_NC, in_maps, core_ids=list(range(8)))
    out = np.zeros((2, 4 * T, D), np.float32)
    for core in range(8):
        b, c = core // 4, core % 4
        out[b, c * T:(c + 1) * T] = res.results[core]["y_out"]
    return out
```

```python
import os
import numpy as np
import concourse.bass as bass
import concourse.mybir as mybir
from concourse.bass_utils import run_bass_kernel_spmd
from contextlib import ExitStack

F32 = mybir.dt.float32
BF16 = mybir.dt.bfloat16
ALU = mybir.AluOpType
AF = mybir.ActivationFunctionType
AX = mybir.AxisListType

ENG = ['tensor', 'vector', 'scalar', 'gpsimd', 'sync']
EPOCH = 20000
ND = 8
PE, V, A, G, S = 'tensor', 'vector', 'scalar', 'gpsimd', 'sync'


class U:
    __slots__ = ('w', 'rs')

    def __init__(s):
        s.w = None
        s.rs = {}


class KB:
    def __init__(s, nc, stack):
        s.nc = nc
        s.st = stack
        s.cnt = {e: 0 for e in ENG}
        s.nsem = 0
        s.sem = {e: s.new_sem(f'e_{e}') for e in ENG}
        s.hist = {e: [] for e in ENG}
        s.waited = {e: {} for e in ENG}
        s.dsem = {}
        s.dtarget = {}
        s.dcount = {}
        s.n_inst = 0

    def new_sem(s, name):
        s.nsem += 1
        return s.st.enter_context(s.nc.semaphore(f'{name}_{s.nsem}'))

    def _waits(s, engine, r, w):
        deps = {}

        def add(tok):
            key = id(tok[0])
            if key not in deps or deps[key][1] < tok[1]:
                deps[key] = tok
        for u in r:
            if u.w is not None:
                add(u.w)
        for u in w:
            if u.w is not None:
                add(u.w)
            for tok in u.rs.values():
                add(tok)
        waits = []
        wd = s.waited[engine]
        for key, (sem, val, src) in deps.items():
            if engine == PE and src == PE:
                continue
            if wd.get(key, 0) >= val:
                continue
            wd[key] = val
            waits.append((sem, val))
        return waits

    def _emit(s, ename, waits, fn, inc):
        e = getattr(s.nc, ename)
        for sem, val in waits:
            e.wait_ge(sem, val)
        if fn is None:
            return
        ins = fn(e)
        if inc[1] is None:
            ins.then_inc(inc[0])
        else:
            ins.then_inc(inc[0], inc[1])
        s.n_inst += 1

    def op(s, engine, fn, r=(), w=()):
        waits = s._waits(engine, r, w)
        if s.cnt[engine] >= EPOCH:
            s.hist[engine].append((s.sem[engine], s.cnt[engine]))
            s.sem[engine] = s.new_sem(f'e_{engine}')
            s.cnt[engine] = 0
        s.cnt[engine] += 1
        sem = s.sem[engine]
        tok = (sem, s.cnt[engine], engine)
        s._emit(engine, waits, fn, (sem, 1))
        for u in r:
            u.rs[id(sem)] = tok
        for u in w:
            u.w = tok
            u.rs = {}
        return tok

    def dma(s, q, out, in_, r=(), w=(), **kw):
        waits = s._waits(q, r, w)
        if q not in s.dsem:
            s.dsem[q] = [s.new_sem(f'd_{q}{i}') for i in range(ND)]
            s.dtarget[q] = [0] * ND
            s.dcount[q] = 0
        i = s.dcount[q] % ND
        s.dcount[q] += 1
        sem = s.dsem[q][i]
        prev = s.dtarget[q][i]
        if prev > 0 and s.waited[q].get(id(sem), 0) < prev:
            s.waited[q][id(sem)] = prev
            waits.append((sem, prev))
        tgt = prev + 16
        s.dtarget[q][i] = tgt
        tok = (sem, tgt, 'dma')
        s._emit(q, waits, lambda e: e.dma_start(out=out, in_=in_, **kw), (sem, 16))
        for u in r:
            u.rs[id(sem)] = tok
        for u in w:
            u.w = tok
            u.rs = {}
        return tok

    def custom(s, engine, fn, inc_sem, inc_val, r=(), w=()):
        waits = s._waits(engine, r, w)
        tok = (inc_sem, inc_val, 'custom')
        s._emit(engine, waits, fn, (inc_sem, None))
        for u in r:
            u.rs[id(inc_sem)] = tok
        for u in w:
            u.w = tok
            u.rs = {}
        return tok

    def barrier(s):
        toks = []
        for e in ENG:
            for sem, c in s.hist[e]:
                toks.append((sem, c))
            if s.cnt[e] > 0:
                toks.append((s.sem[e], s.cnt[e]))
        for q in s.dsem:
            for i in range(ND):
                if s.dtarget[q][i] > 0:
                    toks.append((s.dsem[q][i], s.dtarget[q][i]))
        for e in ENG:
            wd = s.waited[e]
            waits = []
            for sem, val in toks:
                if wd.get(id(sem), 0) >= val:
                    continue
                wd[id(sem)] = val
                waits.append((sem, val))
            s._emit(e, waits, None, None)


D = 1024
T = 2048
TE = 2304
NT = 16
INW = 14848
OFF_CGLU, OFF_CGATE = 0, 2048
OFF_RQ, OFF_RK, OFF_RV, OFF_RG = 3072, 4096, 5120, 7168
OFF_AQ, OFF_AK, OFF_AV, OFF_AG = 9216, 10240, 10496, 10752
OFF_GL = 11776
EPS = 1e-6


def build(NL=2, dbg=False, phases='CTRF'):
    nc = bass.Bass("TRN2", target_bir_lowering=False)

    def din(name, shape):
        return nc.dram_tensor(name, shape, F32, kind="ExternalInput").ap()

    x_ext = din("x_ext", [TE, D])
    norm_g = din("norm_g", [2, D])
    w_in = din("w_in", [2, D, INW])
    b_gate = din("b_gate", [2, 3, D])
    conv_dw = din("conv_dw", [2, 31, D])
    conv_b = din("conv_b", [2, 1, D])
    conv_ln_g = din("conv_ln_g", [2, 1, D])
    conv_ln_b = din("conv_ln_b", [2, 1, D])
    ret_decay = din("ret_decay", [2, 8])
    q_norm_g = din("q_norm_g", [2, 64])
    k_norm_g = din("k_norm_g", [2, 64])
    attn_sink = din("attn_sink", [2, 16])
    w_conv_out = din("w_conv_out", [2, D, D])
    w_ret_out = din("w_ret_out", [2, 2 * D, D])
    w_attn_out = din("w_attn_out", [2, D, D])
    w_out = din("w_out", [2, D, D])
    c_cosr = din("c_cosr", [128, T])
    c_sinr = din("c_sinr", [128, T])
    c_cosa = din("c_cosa", [128, 18, 8])
    c_sina = din("c_sina", [128, 18, 8])
    c_mask = din("c_mask", [128, 512])
    c_ident = din("c_ident", [128, 128])
    c_eseg = din("c_eseg", [128, 32])
    c_ek = din("c_ek", [128, 2])
    c_eq = din("c_eq", [128, 256])
    c_em = din("c_em", [128, 512])
    c_dist = din("c_dist", [128, 8])
    c_sel = din("c_sel", [128, 8])

    y_out = nc.dram_tensor("y_out", [T, D], F32, kind="ExternalOutput").ap()
    okind = "ExternalOutput" if dbg else "Internal"
    OC = nc.dram_tensor("OC", [8, 128, T], BF16, kind=okind).ap()
    OA = nc.dram_tensor("OA", [16, 64, T], BF16, kind=okind).ap()
    OR = nc.dram_tensor("OR", [16, 128, T], BF16, kind=okind).ap()
    x1e = nc.dram_tensor("x1e", [TE, D], F32, kind=okind).ap()
    fs_bounce = nc.dram_tensor("fs_bounce", [512, 512], F32)
    fs_gath = nc.dram_tensor("fs_gath", [2048, 512], F32)
    h_bounce = nc.dram_tensor("h_bounce", [256, D], F32)
    h_gath = nc.dram_tensor("h_gath", [1024, D], F32)
    u_OC, u_OA, u_OR, u_x1e, u_fsb, u_fsg, u_hb, u_hg, u_yout = [U() for _ in range(9)]
    RG = [[0, 1, 2, 3], [4, 5, 6, 7]]

    with ExitStack() as st0:
        k = KB(nc, st0)

        ncount = [0]

        def sb(st, name, shape, dt):
            ncount[0] += 1
            return st.enter_context(nc.sbuf_tensor(f"{name}_{ncount[0]}", shape, dt))

        PF = [st0.enter_context(nc.psum_tensor(f"pf{i}", [128, 512], F32)) for i in range(7)]
        uPF = [U() for _ in range(7)]
        PT = st0.enter_context(nc.psum_tensor("ptb", [128, 8, 128], BF16))
        uPT = U()
        hT = sb(st0, "hT", [128, 8, TE], BF16)
        uhT = [U() for _ in range(18)]
        ident = sb(st0, "ident", [128, 128], F32); identb = sb(st0, "identb", [128, 128], BF16)
        onesf = sb(st0, "onesf", [128, 128], F32); onesb = sb(st0, "onesb", [128, 128], BF16)
        maskb = sb(st0, "maskb", [128, 512], BF16)
        eseg = sb(st0, "eseg", [128, 32], F32); ek = sb(st0, "ek", [128, 2], F32)
        eq = sb(st0, "eq", [128, 256], F32); em = sb(st0, "em", [128, 512], F32)
        dist = sb(st0, "dist", [128, 8], F32); sel = sb(st0, "sel", [128, 8], F32)
        cosa = sb(st0, "cosa", [128, 18, 8], F32); sina = sb(st0, "sina", [128, 18, 8], F32)
        lg = sb(st0, "lg", [128, 8], F32)
        qg = sb(st0, "qg", [128, 64], F32); kg = sb(st0, "kg", [128, 64], F32)
        snk = sb(st0, "snk", [128, 16], F32); snke = sb(st0, "snke", [128, 16], F32)
        negc = sb(st0, "negc", [128, 1], F32); tmpc = sb(st0, "tmpc", [128, 4], F32)
        g2 = sb(st0, "g2", [128, 128], F32)
        prm = sb(st0, "prm", [37, D], F32)
        prT = sb(st0, "prT", [128, 8, 37], F32)
        uC = U()
        uL = U()

        for (t, src) in ((ident, c_ident), (eseg, c_eseg), (ek, c_ek), (eq, c_eq), (em, c_em),
                         (dist, c_dist), (sel, c_sel), (cosa, c_cosa), (sina, c_sina)):
            k.dma(S, t[:], src, w=[uC])
        k.dma(G, identb[:], c_ident, w=[uC])
        k.dma(G, maskb[:], c_mask, w=[uC])
        k.op(V, lambda e: e.memset(onesf[:], 1.0 / 1024.0), w=[uC])
        k.op(V, lambda e: e.memset(onesb[:], 1.0), w=[uC])
        k.barrier()

        def phase_T(l):
            with ExitStack() as st:
                Wt = sb(st, "Wt", [128, 8, 2560], BF16); uWt = U()
                for j in range(5):
                    k.dma(G, Wt[:, :, j * 512:(j + 1) * 512], w_in[l, :, OFF_AQ + j * 512:OFF_AQ + (j + 1) * 512].rearrange("(k p) c -> p k c", p=128), w=[uWt])
                qTg = sb(st, "qTg", [64, 16, 512], BF16); uqT = [U() for _ in range(4)]
                kT = sb(st, "kTres", [64, 4, TE], BF16); ukT = [U() for _ in range(18)]
                Vr = sb(st, "Vres", [128, 18, 256], BF16); uVr = [U() for _ in range(18)]
                sq = sb(st, "sq", [128, 1280], F32); usq = U()
                ssa = sb(st, "ssa", [128, 20], F32); rsa = sb(st, "rsa", [128, 20], F32); urs = U()
                qn = sb(st, "qn", [128, 1280], F32); uqn = U()
                rt = [sb(st, f"rt{i}", [128, 20, 8], F32) for i in range(4)]; urt = [U() for _ in range(4)]
                qb = sb(st, "qb", [128, 1024], BF16); uqb = U()
                kb = sb(st, "kb", [128, 256], BF16); ukb = U()
                sq3 = sq[:].rearrange("p (h d) -> p h d", d=64)
                qn3 = qn[:].rearrange("p (h d) -> p h d", d=64)

                def norm_rot(t, h0, h1):
                    nh = h1 - h0
                    k.op(V, lambda e: e.tensor_reduce(out=ssa[:, h0:h1], in_=sq3[:, h0:h1, :], axis=AX.X, op=ALU.add), r=[usq], w=[urs])
                    k.op(V, lambda e: e.tensor_scalar(out=rsa[:, h0:h1], in0=ssa[:, h0:h1], scalar1=1.0 / 64.0, scalar2=EPS, op0=ALU.mult, op1=ALU.add), r=[urs], w=[urs])
                    k.op(A, lambda e: e.activation(out=rsa[:, h0:h1], in_=rsa[:, h0:h1], func=AF.Sqrt), r=[urs], w=[urs])
                    k.op(V, lambda e: e.reciprocal(out=rsa[:, h0:h1], in_=rsa[:, h0:h1]), r=[urs], w=[urs])
                    if h0 == 0:
                        for half in range(2):
                            k.op(V, lambda e, half=half: e.tensor_tensor(out=qn3[:, half * 8:(half + 1) * 8, :], in0=PF[half][:].rearrange("p (h d) -> p h d", d=64), in1=rsa[:, half * 8:(half + 1) * 8].unsqueeze(2).broadcast_to([128, 8, 64]), op=ALU.mult), r=[uPF[half], urs], w=[uqn])
                        k.op(G, lambda e: e.tensor_tensor(out=qn3[:, 0:16, :], in0=qn3[:, 0:16, :], in1=qg[:].unsqueeze(1).broadcast_to([128, 16, 64]), op=ALU.mult), r=[uqn, uL], w=[uqn])
                    else:
                        k.op(V, lambda e: e.tensor_tensor(out=qn3[:, 16:20, :], in0=PF[2][:, 0:256].rearrange("p (h d) -> p h d", d=64), in1=rsa[:, 16:20].unsqueeze(2).broadcast_to([128, 4, 64]), op=ALU.mult), r=[uPF[2], urs], w=[uqn])
                        k.op(G, lambda e: e.tensor_tensor(out=qn3[:, 16:20, :], in0=qn3[:, 16:20, :], in1=kg[:].unsqueeze(1).broadcast_to([128, 4, 64]), op=ALU.mult), r=[uqn, uL], w=[uqn])
                    cb = cosa[:, t, :].unsqueeze(1).broadcast_to([128, nh, 8])
                    sbb = sina[:, t, :].unsqueeze(1).broadcast_to([128, nh, 8])
                    x1 = qn3[:, h0:h1, 0:8]
                    x2 = qn3[:, h0:h1, 8:16]
                    k.op(V, lambda e: e.tensor_tensor(out=rt[0][:, h0:h1, :], in0=x1, in1=cb, op=ALU.mult), r=[uqn, uC], w=[urt[0]])
                    k.op(G, lambda e: e.tensor_tensor(out=rt[1][:, h0:h1, :], in0=x2, in1=sbb, op=ALU.mult), r=[uqn, uC], w=[urt[1]])
                    k.op(V, lambda e: e.tensor_tensor(out=rt[2][:, h0:h1, :], in0=x2, in1=cb, op=ALU.mult), r=[uqn, uC], w=[urt[2]])
                    k.op(G, lambda e: e.tensor_tensor(out=rt[3][:, h0:h1, :], in0=x1, in1=sbb, op=ALU.mult), r=[uqn, uC], w=[urt[3]])
                    k.op(V, lambda e: e.tensor_tensor(out=x1, in0=rt[0][:, h0:h1, :], in1=rt[1][:, h0:h1, :], op=ALU.subtract), r=[urt[0], urt[1], urt[2], urt[3]], w=[uqn])
                    k.op(G, lambda e: e.tensor_tensor(out=x2, in0=rt[2][:, h0:h1, :], in1=rt[3][:, h0:h1, :], op=ALU.add), r=[urt[2], urt[3]], w=[uqn])

                for t in range(18):
                    for kk in range(8):
                        k.op(PE, lambda e, kk=kk, t=t: e.matmul(PF[2][:], lhsT=hT[:, kk, t * 128:(t + 1) * 128], rhs=Wt[:, kk, 1024:1536], start=(kk == 0), stop=(kk == 7)), r=[uWt, uhT[t]], w=[uPF[2]])
                    k.op(A, lambda e: e.activation(out=sq[:, 1024:1280], in_=PF[2][:, 0:256], func=AF.Square), r=[uPF[2]], w=[usq])
                    norm_rot(t, 16, 20)
                    k.op(A, lambda e: e.activation(out=kb[:], in_=qn[:, 1024:1280], func=AF.Copy), r=[uqn], w=[ukb])
                    k.op(A, lambda e, t=t: e.activation(out=Vr[:, t, :], in_=PF[2][:, 256:512], func=AF.Copy), r=[uPF[2]], w=[uVr[t]])
                    for g in range(4):
                        k.op(PE, lambda e, g=g: e.transpose(out=PT[0:64, g, :], in_=kb[:, g * 64:(g + 1) * 64], identity=identb[:]), r=[ukb, uC], w=[uPT])
                    k.op(V, lambda e, t=t: e.tensor_copy(out=kT[:, :, t * 128:(t + 1) * 128], in_=PT[0:64, 0:4, :]), r=[uPT], w=[ukT[t]])
                if dbg == 'T1':
                    k.barrier()
                    return
                gT = sb(st, "gT", [64, 16, 512], BF16); ugT = U()
                pt = [sb(st, f"pt{i}", [128, 512], BF16) for i in range(3)]; upt = [U() for _ in range(3)]
                den = sb(st, "den", [64, 512], F32); uden = U()
                rec = sb(st, "rec", [64, 512], F32); urec = U()
                on = sb(st, "on", [64, 512], F32); uon = U()
                oaT = sb(st, "oaT", [64, 16, 512], BF16); uoaT = U()
                for tg in range(4):
                    e0 = 128 + tg * 512
                    hr = uhT[e0 // 128:(e0 + 512) // 128]
                    for nn in range(4):
                        t = tg * 4 + nn + 1
                        for half in range(2):
                            for kk in range(8):
                                k.op(PE, lambda e, half=half, kk=kk, t=t: e.matmul(PF[half][:], lhsT=hT[:, kk, t * 128:(t + 1) * 128], rhs=Wt[:, kk, half * 512:(half + 1) * 512], start=(kk == 0), stop=(kk == 7)), r=[uWt, uhT[t]], w=[uPF[half]])
                            k.op(A, lambda e, half=half: e.activation(out=sq[:, half * 512:(half + 1) * 512], in_=PF[half][:], func=AF.Square), r=[uPF[half]], w=[usq])
                        norm_rot(t, 0, 16)
                        k.op(A, lambda e: e.activation(out=qb[:], in_=qn[:, 0:1024], func=AF.Copy), r=[uqn], w=[uqb])
                        for rr in range(2):
                            for j in range(8):
                                hh = rr * 8 + j
                                k.op(PE, lambda e, j=j, hh=hh: e.transpose(out=PT[0:64, j, :], in_=qb[:, hh * 64:(hh + 1) * 64], identity=identb[:]), r=[uqb, uC], w=[uPT])
                            k.op(V, lambda e, rr=rr, nn=nn: e.tensor_copy(out=qTg[:, rr * 8:(rr + 1) * 8, nn * 128:(nn + 1) * 128], in_=PT[0:64, :, :]), r=[uPT], w=[uqT[nn]])
                    for hh in range(16):
                        ps, ups = PF[hh % 2], uPF[hh % 2]
                        for kk in range(8):
                            k.op(PE, lambda e, ps=ps, hh=hh, kk=kk, e0=e0: e.matmul(ps[0:64, :], lhsT=Wt[:, kk, 1536 + hh * 64:1536 + (hh + 1) * 64], rhs=hT[:, kk, e0:e0 + 512], start=(kk == 0), stop=(kk == 7)), r=[uWt] + hr, w=[ups])
                        k.op(A, lambda e, ps=ps, hh=hh: e.activation(out=gT[:, hh, :], in_=ps[0:64, :], func=AF.Silu), r=[ups], w=[ugT])
                    for nn in range(4):
                        n = tg * 4 + nn
                        t = n + 1
                        for g in range(4):
                            for mi, m in enumerate((t - 1, t, t + 1)):
                                k.op(PE, lambda e, mi=mi, m=m, g=g, nn=nn: e.matmul(PF[2 + mi][:], lhsT=kT[:, g, m * 128:(m + 1) * 128], rhs=qTg[:, 4 * g:4 * g + 4, nn * 128:(nn + 1) * 128], start=True, stop=True), r=[ukT[m], uqT[nn]], w=[uPF[2 + mi]])
                                k.op(A, lambda e, mi=mi: e.activation(out=pt[mi][:], in_=PF[2 + mi][:], func=AF.Exp, bias=negc[:, 0:1], scale=0.125), r=[uPF[2 + mi], uL], w=[upt[mi]])
                            mL = 256 if t == 1 else 0
                            mR = 384 if t == 16 else 128
                            k.op(V, lambda e, mL=mL: e.tensor_tensor(out=pt[0][:].rearrange("p (a i) -> p a i", a=4), in0=pt[0][:].rearrange("p (a i) -> p a i", a=4), in1=maskb[:, mL:mL + 128].unsqueeze(1).broadcast_to([128, 4, 128]), op=ALU.mult), r=[upt[0], uC], w=[upt[0]])
                            k.op(G, lambda e, mR=mR: e.tensor_tensor(out=pt[2][:].rearrange("p (a i) -> p a i", a=4), in0=pt[2][:].rearrange("p (a i) -> p a i", a=4), in1=maskb[:, mR:mR + 128].unsqueeze(1).broadcast_to([128, 4, 128]), op=ALU.mult), r=[upt[2], uC], w=[upt[2]])
                            for mi, m in enumerate((t - 1, t, t + 1)):
                                k.op(PE, lambda e, mi=mi, m=m, g=g: e.matmul(PF[5][0:64, :], lhsT=Vr[:, m, g * 64:(g + 1) * 64], rhs=pt[mi][:], start=(mi == 0), stop=(mi == 2)), r=[uVr[m], upt[mi]], w=[uPF[5]])
                            for mi, m in enumerate((t - 1, t, t + 1)):
                                k.op(PE, lambda e, mi=mi: e.matmul(PF[6][0:64, :], lhsT=onesb[:, 0:64], rhs=pt[mi][:], start=(mi == 0), stop=(mi == 2)), r=[uC, upt[mi]], w=[uPF[6]])
                            for ei in range(4):
                                hd = 4 * g + ei
                                k.op(V, lambda e, ei=ei, hd=hd: e.tensor_scalar_add(out=den[:, ei * 128:(ei + 1) * 128], in0=PF[6][0:64, ei * 128:(ei + 1) * 128], scalar1=snke[0:64, hd:hd + 1]), r=[uPF[6], uL], w=[uden])
                            k.op(V, lambda e: e.reciprocal(out=rec[:], in_=den[:]), r=[uden], w=[urec])
                            k.op(V, lambda e: e.tensor_tensor(out=on[:], in0=PF[5][0:64, :], in1=rec[:], op=ALU.mult), r=[uPF[5], urec], w=[uon])
                            k.op(G, lambda e, g=g, nn=nn: e.tensor_tensor(out=oaT[:, 4 * g:4 * g + 4, nn * 128:(nn + 1) * 128], in0=on[:].rearrange("p (a i) -> p a i", a=4), in1=gT[:, 4 * g:4 * g + 4, nn * 128:(nn + 1) * 128], op=ALU.mult), r=[uon, ugT], w=[uoaT])
                    k.dma(S, OA[:, :, tg * 512:(tg + 1) * 512].rearrange("h d t -> d h t"), oaT[:], r=[uoaT], w=[u_OA])
                k.barrier()

        def phase_R(l):
            with ExitStack() as st:
                Wr = sb(st, "Wr", [128, 8, 2048], BF16); uWr = U()
                qT = sb(st, "rqT", [128, 2, T], BF16); uq = [U() for _ in range(4)]
                kT = sb(st, "rkT", [128, 2, T], BF16); ukk = [U() for _ in range(4)]
                Vv = sb(st, "rV", [128, 16, 512], BF16); uV = [U() for _ in range(16)]
                kt = sb(st, "rkt", [128, 16, 256], BF16); ukt = [U() for _ in range(16)]
                Pp = sb(st, "rP", [128, 16, 512], F32); uP = [U() for _ in range(16)]
                cs = sb(st, "rcs", [128, 2, 512], F32); ucs = U()
                tq = [sb(st, f"rtq{i}", [128, 512], F32) for i in range(4)]; utq = [U() for _ in range(4)]
                Sf = sb(st, "Sf", [128, 2, 512], F32); Sb = sb(st, "Sb", [128, 2, 512], F32); uSf = U(); uSb = U()
                Sfb = sb(st, "Sfb", [128, 2, 512], BF16); Sbb = sb(st, "Sbb", [128, 2, 512], BF16); uSfb = U(); uSbb = U()
                fsum = sb(st, "fsum", [128, 4, 512], F32); ufs = U()
                DTm = sb(st, "DTm", [128, 128], F32); dq = sb(st, "dq", [128, 256], F32); dsg = sb(st, "dsg", [128, 32], F32)
                dk = sb(st, "dk", [128, 2], F32); gC = sb(st, "gC", [128, 2], F32); wr = sb(st, "wr", [128, 8], F32)
                tmpD = sb(st, "tmpD", [128, 256], F32); uH = U()
                kfs = sb(st, "kfs", [128, 256], BF16); kbs = sb(st, "kbs", [128, 256], BF16); ukfs = U(); ukbs = U()
                qd = sb(st, "qd", [128, 2, 128], BF16); uqd = U()
                kdc = sb(st, "kdc", [128, 256], BF16); ukdc = U()
                sd = sb(st, "sd", [128, 128], BF16); usd = U()
                oo = sb(st, "oo", [128, 512], F32); uoo = U()
                onn = sb(st, "onn", [128, 512], F32); uonn = U()
                sgg = sb(st, "sgg", [128, 512], F32); usgg = U()
                og = sb(st, "og", [128, 512], BF16); uog = U()
                og_b = sb(st, "og_b", [128, 512], BF16); uog_b = U()
                bst = sb(st, "bst", [128, 6], F32); mv = sb(st, "mv", [128, 2], F32); ubn = U()
                orT = sb(st, "orT", [128, 4, 512], BF16); uorT = U()
                RSTOP = int(os.environ.get("RSTOP", "0"))

                class _Stop(Exception):
                    pass

                def ck(n_):
                    if RSTOP == n_:
                        raise _Stop()
                try:
                  for h in range(4):
                      wl = [(1024, OFF_RV + h * 512), (1536, OFF_RG + h * 512)]
                      if h % 2 == 0:
                          wl = [(0, OFF_RQ + h * 256), (512, OFF_RK + h * 256)] + wl
                      for (dst, off) in wl:
                          k.dma(G, Wr[:, :, dst:dst + 512], w_in[l, :, off:off + 512].rearrange("(k p) c -> p k c", p=128), w=[uWr])
                      qc0 = (h % 2) * 256
                      kc0 = 512 + (h % 2) * 256
                      lgf = lg[:, h:h + 1]
                      lgb = lg[:, 4 + h:5 + h]
                      hc = dict(r=[uL, uC, uH], w=[uH])
                      k.op(A, lambda e: e.activation(out=dsg[:, 0:16], in_=eseg[:, 0:16], func=AF.Exp, scale=lgf), **hc)
                      k.op(A, lambda e: e.activation(out=dsg[:, 16:32], in_=eseg[:, 16:32], func=AF.Exp, scale=lgb), **hc)
                      k.op(A, lambda e: e.activation(out=dk[:, 0:1], in_=ek[:, 0:1], func=AF.Exp, scale=lgf), **hc)
                      k.op(A, lambda e: e.activation(out=dk[:, 1:2], in_=ek[:, 1:2], func=AF.Exp, scale=lgb), **hc)
                      k.op(A, lambda e: e.activation(out=dq[:, 0:128], in_=eq[:, 0:128], func=AF.Exp, scale=lgf), **hc)
                      k.op(A, lambda e: e.activation(out=dq[:, 128:256], in_=eq[:, 128:256], func=AF.Exp, scale=lgb), **hc)
                      k.op(V, lambda e: e.tensor_scalar_mul(out=dq[:], in0=dq[:], scalar1=1.0 / 16.0), **hc)
                      k.op(A, lambda e: e.activation(out=gC[:, 0:1], in_=lgf, func=AF.Exp, scale=128.0), **hc)
                      k.op(A, lambda e: e.activation(out=gC[:, 1:2], in_=lgb, func=AF.Exp, scale=128.0), **hc)
                      k.op(A, lambda e: e.activation(out=wr[:, 0:4], in_=dist[:, 0:4], func=AF.Exp, scale=lgf), **hc)
                      k.op(A, lambda e: e.activation(out=wr[:, 4:8], in_=dist[:, 4:8], func=AF.Exp, scale=lgb), **hc)
                      k.op(A, lambda e: e.activation(out=tmpD[:, 0:128], in_=em[:, 0:128], func=AF.Exp, scale=lgf), **hc)
                      k.op(A, lambda e: e.activation(out=tmpD[:, 128:256], in_=em[:, 128:256], func=AF.Exp, scale=lgb), **hc)
                      k.op(V, lambda e: e.tensor_tensor(out=tmpD[:], in0=tmpD[:], in1=em[:, 256:512], op=ALU.mult), **hc)
                      k.op(V, lambda e: e.tensor_tensor(out=DTm[:], in0=tmpD[:, 0:128], in1=tmpD[:, 128:256], op=ALU.add), **hc)
                      k.op(V, lambda e: e.tensor_scalar_mul(out=DTm[:], in0=DTm[:], scalar1=1.0 / 16.0), **hc)
                      ck(1)
                      for tg in range(4):
                          e0 = 128 + tg * 512
                          hr = uhT[e0 // 128:(e0 + 512) // 128]
                          k.dma(S, cs[:, 0, :], c_cosr[:, tg * 512:(tg + 1) * 512], w=[ucs])
                          k.dma(S, cs[:, 1, :], c_sinr[:, tg * 512:(tg + 1) * 512], w=[ucs])
                          for (col0, dstT, ud) in ((qc0, qT, uq[tg]), (kc0, kT, ukk[tg])):
                              for dc in range(2):
                                  for kk in range(8):
                                      k.op(PE, lambda e, dc=dc, kk=kk, col0=col0, e0=e0: e.matmul(PF[dc][:], lhsT=Wr[:, kk, col0 + dc * 128:col0 + (dc + 1) * 128], rhs=hT[:, kk, e0:e0 + 512], start=(kk == 0), stop=(kk == 7)), r=[uWr] + hr, w=[uPF[dc]])
                              k.op(V, lambda e: e.tensor_tensor(out=tq[0][:], in0=PF[0][:], in1=cs[:, 0, :], op=ALU.mult), r=[uPF[0], ucs], w=[utq[0]])
                              k.op(V, lambda e: e.tensor_tensor(out=tq[1][:], in0=PF[1][:], in1=cs[:, 1, :], op=ALU.mult), r=[uPF[1], ucs], w=[utq[1]])
                              k.op(V, lambda e: e.tensor_tensor(out=tq[2][:], in0=PF[1][:], in1=cs[:, 0, :], op=ALU.mult), r=[uPF[1], ucs], w=[utq[2]])
                              k.op(V, lambda e: e.tensor_tensor(out=tq[3][:], in0=PF[0][:], in1=cs[:, 1, :], op=ALU.mult), r=[uPF[0], ucs], w=[utq[3]])
                              k.op(V, lambda e, dstT=dstT, tg=tg: e.tensor_tensor(out=dstT[:, 0, tg * 512:(tg + 1) * 512], in0=tq[0][:], in1=tq[1][:], op=ALU.subtract), r=[utq[0], utq[1]], w=[ud])
                              k.op(G, lambda e, dstT=dstT, tg=tg: e.tensor_tensor(out=dstT[:, 1, tg * 512:(tg + 1) * 512], in0=tq[2][:], in1=tq[3][:], op=ALU.add), r=[utq[2], utq[3]], w=[ud])
                          ck(2)
                          for nn in range(4):
                              n = tg * 4 + nn
                              t = n + 1
                              for kk in range(8):
                                  k.op(PE, lambda e, kk=kk, t=t: e.matmul(PF[2][:], lhsT=hT[:, kk, t * 128:(t + 1) * 128], rhs=Wr[:, kk, 1024:1536], start=(kk == 0), stop=(kk == 7)), r=[uWr, uhT[t]], w=[uPF[2]])
                              k.op(A, lambda e, n=n: e.activation(out=Vv[:, n, :], in_=PF[2][:], func=AF.Copy), r=[uPF[2]], w=[uV[n]])
                              ck(3)
                              for dc in range(2):
                                  k.op(PE, lambda e, dc=dc, n=n: e.transpose(out=PT[:, dc, :], in_=kT[:, dc, n * 128:(n + 1) * 128], identity=identb[:]), r=[ukk[tg], uC], w=[uPT])
                              ptv = PT[:, 0:2, :]
                              k.op(A, lambda e, n=n, ptv=ptv: e.activation(out=kfs[:].rearrange("p (a d) -> p a d", a=2), in_=ptv, func=AF.Copy, scale=dsg[:, n:n + 1]), r=[uPT, uH], w=[ukfs])
                              k.op(A, lambda e, n=n, ptv=ptv: e.activation(out=kbs[:].rearrange("p (a d) -> p a d", a=2), in_=ptv, func=AF.Copy, scale=dsg[:, 16 + n:17 + n]), r=[uPT, uH], w=[ukbs])
                              k.op(A, lambda e, n=n, ptv=ptv: e.activation(out=kt[:, n, :].rearrange("p (a d) -> p a d", a=2), in_=ptv, func=AF.Copy), r=[uPT], w=[ukt[n]])
                              ck(4)
                              for dc in range(2):
                                  k.op(PE, lambda e, dc=dc, n=n: e.matmul(PF[3 + dc][:], lhsT=kfs[:, dc * 128:(dc + 1) * 128], rhs=Vv[:, n, :], start=(n == 0), stop=(n == 15)), r=[ukfs, uV[n]], w=[uPF[3 + dc]])
                                  k.op(PE, lambda e, dc=dc, n=n: e.matmul(PF[5 + dc][:], lhsT=kbs[:, dc * 128:(dc + 1) * 128], rhs=Vv[:, n, :], start=(n == 0), stop=(n == 15)), r=[ukbs, uV[n]], w=[uPF[5 + dc]])
                      if dbg == 'R0':
                          k.barrier()
                          return
                      for j in range(4):
                          k.op(A, lambda e, j=j: e.activation(out=fsum[:, j, :], in_=PF[3 + j][:], func=AF.Copy), r=[uPF[3 + j]], w=[ufs])
                      k.dma(S, fs_bounce.ap().rearrange("(j p) v -> p j v", p=128), fsum[:], r=[ufs], w=[u_fsb])
                      ccs = k.new_sem("cc")
                      k.custom(G, lambda e: e.collective_compute("AllGather", ALU.bypass, replica_groups=RG, ins=[fs_bounce.ap().opt()], outs=[fs_gath.ap().opt()]), ccs, 1, r=[u_fsb], w=[u_fsg])
                      for r_ in range(4):
                          k.dma(S, fsum[:], fs_gath.ap()[r_ * 512:(r_ + 1) * 512, :].rearrange("(j p) v -> p j v", p=128), r=[u_fsg], w=[ufs])
                          for dirn, (Sx, uSx) in enumerate(((Sf, uSf), (Sb, uSb))):
                              for dc in range(2):
                                  wcol = wr[:, dirn * 4 + r_:dirn * 4 + r_ + 1]
                                  if r_ == 0:
                                      k.op(V, lambda e, Sx=Sx, dc=dc, dirn=dirn, wcol=wcol: e.tensor_scalar_mul(out=Sx[:, dc, :], in0=fsum[:, dirn * 2 + dc, :], scalar1=wcol), r=[ufs, uH], w=[uSx])
                                  else:
                                      k.op(V, lambda e, Sx=Sx, dc=dc, dirn=dirn, wcol=wcol: e.scalar_tensor_tensor(out=Sx[:, dc, :], in0=fsum[:, dirn * 2 + dc, :], scalar=wcol, in1=Sx[:, dc, :], op0=ALU.mult, op1=ALU.add), r=[ufs, uH, uSx], w=[uSx])
                      k.op(A, lambda e: e.activation(out=Sfb[:], in_=Sf[:], func=AF.Copy), r=[uSf], w=[uSfb])
                      k.op(A, lambda e: e.activation(out=Sbb[:], in_=Sb[:], func=AF.Copy), r=[uSb], w=[uSbb])
                      if dbg == 'RAG':
                          k.barrier()
                          return
                      for n in range(15, -1, -1):
                          tg = n // 4
                          k.op(A, lambda e, n=n: e.activation(out=kdc[:], in_=kt[:, n, :], func=AF.Copy, scale=dk[:, 1:2]), r=[ukt[n], uH], w=[ukdc])
                          for dc in range(2):
                              k.op(PE, lambda e, dc=dc, n=n: e.matmul(PF[1 + dc][:], lhsT=kdc[:, dc * 128:(dc + 1) * 128], rhs=Vv[:, n, :], start=True, stop=True), r=[ukdc, uV[n]], w=[uPF[1 + dc]])
                          k.op(V, lambda e, n=n: e.tensor_tensor(out=qd[:], in0=qT[:, :, n * 128:(n + 1) * 128], in1=dq[:, 128:256].unsqueeze(1).broadcast_to([128, 2, 128]), op=ALU.mult), r=[uq[tg], uH], w=[uqd])
                          for dc in range(2):
                              k.op(PE, lambda e, dc=dc: e.matmul(PF[0][:], lhsT=qd[:, dc, :], rhs=Sbb[:, dc, :], start=(dc == 0), stop=(dc == 1)), r=[uqd, uSbb], w=[uPF[0]])
                          for dc in range(2):
                              k.op(V, lambda e, dc=dc: e.scalar_tensor_tensor(out=Sb[:, dc, :], in0=Sb[:, dc, :], scalar=gC[:, 1:2], in1=PF[1 + dc][:], op0=ALU.mult, op1=ALU.add), r=[uPF[1 + dc], uH, uSb], w=[uSb])
                          k.op(A, lambda e: e.activation(out=Sbb[:], in_=Sb[:], func=AF.Copy), r=[uSb], w=[uSbb])
                          k.op(A, lambda e, n=n: e.activation(out=Pp[:, n, :], in_=PF[0][:], func=AF.Copy), r=[uPF[0]], w=[uP[n]])
                      Sfb2 = [Sfb, Sbb]
                      uSfb2 = [uSfb, uSbb]
                      og2 = [og, og_b]
                      uog2 = [uog, uog_b]

                      def finish(n):
                          tg = n // 4
                          for vc in range(4):
                              k.op(PE, lambda e, vc=vc, n=n: e.transpose(out=PT[:, vc, :], in_=og2[n % 2][:, vc * 128:(vc + 1) * 128], identity=identb[:]), r=[uog2[n % 2], uC], w=[uPT])
                          k.op(A, lambda e, n=n: e.activation(out=orT[:, :, (n % 4) * 128:(n % 4 + 1) * 128], in_=PT[:, 0:4, :], func=AF.Copy), r=[uPT], w=[uorT])
                          if n % 4 == 3:
                              k.dma(S, OR[h * 4:(h + 1) * 4, :, tg * 512:(tg + 1) * 512].rearrange("c p t -> p c t"), orT[:], r=[uorT], w=[u_OR])
                      for n in range(16):
                          tg = n // 4
                          t = n + 1
                          cur, nxt = n % 2, (n + 1) % 2
                          k.op(A, lambda e, n=n: e.activation(out=kdc[:], in_=kt[:, n, :], func=AF.Copy, scale=dk[:, 0:1]), r=[ukt[n], uH], w=[ukdc])
                          for dc in range(2):
                              k.op(PE, lambda e, dc=dc, n=n: e.matmul(PF[1 + dc][:], lhsT=kdc[:, dc * 128:(dc + 1) * 128], rhs=Vv[:, n, :], start=True, stop=True), r=[ukdc, uV[n]], w=[uPF[1 + dc]])
                          for kk in range(8):
                              k.op(PE, lambda e, kk=kk, t=t: e.matmul(PF[5][:], lhsT=hT[:, kk, t * 128:(t + 1) * 128], rhs=Wr[:, kk, 1536:2048], start=(kk == 0), stop=(kk == 7)), r=[uWr, uhT[t]], w=[uPF[5]])
                          k.op(A, lambda e: e.activation(out=sgg[:], in_=PF[5][:], func=AF.Silu), r=[uPF[5]], w=[usgg])
                          for dc in range(2):
                              k.op(PE, lambda e, dc=dc, n=n: e.matmul(PF[3][:, 0:128], lhsT=kT[:, dc, n * 128:(n + 1) * 128], rhs=qT[:, dc, n * 128:(n + 1) * 128], start=(dc == 0), stop=(dc == 1)), r=[ukk[tg], uq[tg]], w=[uPF[3]])
                          k.op(V, lambda e: e.tensor_tensor(out=sd[:], in0=PF[3][:, 0:128], in1=DTm[:], op=ALU.mult), r=[uPF[3], uH], w=[usd])
                          k.op(G, lambda e, n=n: e.tensor_tensor(out=qd[:], in0=qT[:, :, n * 128:(n + 1) * 128], in1=dq[:, 0:128].unsqueeze(1).broadcast_to([128, 2, 128]), op=ALU.mult), r=[uq[tg], uH], w=[uqd])
                          k.op(PE, lambda e, n=n: e.matmul(PF[4][:], lhsT=sd[:], rhs=Vv[:, n, :], start=True, stop=False), r=[usd, uV[n]], w=[uPF[4]])
                          for dc in range(2):
                              k.op(PE, lambda e, dc=dc, cur=cur: e.matmul(PF[4][:], lhsT=qd[:, dc, :], rhs=Sfb2[cur][:, dc, :], start=False, stop=(dc == 1)), r=[uqd, uSfb2[cur]], w=[uPF[4]])
                          if n > 0:
                              finish(n - 1)
                          for dc in range(2):
                              k.op(V, lambda e, dc=dc: e.scalar_tensor_tensor(out=Sf[:, dc, :], in0=Sf[:, dc, :], scalar=gC[:, 0:1], in1=PF[1 + dc][:], op0=ALU.mult, op1=ALU.add), r=[uPF[1 + dc], uH, uSf], w=[uSf])
                          k.op(A, lambda e, nxt=nxt: e.activation(out=Sfb2[nxt][:], in_=Sf[:], func=AF.Copy), r=[uSf], w=[uSfb2[nxt]])
                          k.op(V, lambda e, n=n: e.tensor_tensor(out=oo[:], in0=PF[4][:], in1=Pp[:, n, :], op=ALU.add), r=[uPF[4], uP[n]], w=[uoo])
                          k.op(V, lambda e: e.bn_stats(out=bst[:], in_=oo[:]), r=[uoo], w=[ubn])
                          k.op(V, lambda e: e.bn_aggr(out=mv[:], in_=bst[:]), r=[ubn], w=[ubn])
                          k.op(V, lambda e: e.tensor_scalar_add(out=mv[:, 1:2], in0=mv[:, 1:2], scalar1=EPS), r=[ubn], w=[ubn])
                          k.op(A, lambda e: e.activation(out=mv[:, 1:2], in_=mv[:, 1:2], func=AF.Sqrt), r=[ubn], w=[ubn])
                          k.op(V, lambda e: e.reciprocal(out=mv[:, 1:2], in_=mv[:, 1:2]), r=[ubn], w=[ubn])
                          k.op(V, lambda e: e.tensor_scalar(out=onn[:], in0=oo[:], scalar1=mv[:, 0:1], scalar2=mv[:, 1:2], op0=ALU.subtract, op1=ALU.mult), r=[uoo, ubn], w=[uonn])
                          k.op(G, lambda e, cur=cur: e.tensor_tensor(out=og2[cur][:], in0=onn[:], in1=sgg[:], op=ALU.mult), r=[uonn, usgg], w=[uog2[cur]])
                      finish(15)
                except _Stop:
                    pass
                k.barrier()

        def phase_F(l, xsrc, last):
            with ExitStack() as st:
                mg = sb(st, "mg", [128, 8, T], F32); umg = [U() for _ in range(4)]
                sgm = sb(st, "sgm", [128, 512], F32); usgm = U()
                tmpm = sb(st, "tmpm", [128, 512], F32); utmpm = U()
                for bi, (nk, kp) in enumerate(((8, 128), (16, 128), (16, 64))):
                    with ExitStack() as st2:
                        Wb = sb(st2, f"Wb{bi}", [kp, nk, D], BF16); uWb = U()
                        Wg = sb(st2, f"Wg{bi}", [128, 8, D], BF16); uWg = U()
                        ob = sb(st2, f"ob{bi}", [kp, nk, 512], BF16); uob = U()
                        src_w = (w_conv_out, w_ret_out, w_attn_out)[bi]
                        if bi == 2:
                            view = src_w[l].rearrange("(h d) c -> d h c", d=64)
                        else:
                            view = src_w[l].rearrange("(k p) c -> p k c", p=128)
                        for j in range(0, nk, 4):
                            k.dma(G, Wb[:, j:j + 4, :], view[:, j:j + 4, :], w=[uWb])
                        for j in range(2):
                            c0 = OFF_GL + bi * 1024 + j * 512
                            k.dma(G, Wg[:, :, j * 512:(j + 1) * 512], w_in[l, :, c0:c0 + 512].rearrange("(k p) c -> p k c", p=128), w=[uWg])
                        osrc = (OC, OR, OA)[bi]
                        uos = (u_OC, u_OR, u_OA)[bi]
                        for tg in range(4):
                            e0 = 128 + tg * 512
                            hr = uhT[e0 // 128:(e0 + 512) // 128]
                            k.dma(S, ob[:], osrc[:, :, tg * 512:(tg + 1) * 512].rearrange("c p t -> p c t"), r=[uos], w=[uob])
                            for m in range(8):
                                py, upy = PF[m % 3], uPF[m % 3]
                                pg, upg = PF[3 + m % 3], uPF[3 + m % 3]
                                for j in range(nk):
                                    k.op(PE, lambda e, py=py, j=j, m=m: e.matmul(py[:], lhsT=Wb[:, j, m * 128:(m + 1) * 128], rhs=ob[:, j, :], start=(j == 0), stop=(j == nk - 1)), r=[uWb, uob], w=[upy])
                                for kk in range(8):
                                    k.op(PE, lambda e, pg=pg, kk=kk, m=m, e0=e0: e.matmul(pg[:], lhsT=Wg[:, kk, m * 128:(m + 1) * 128], rhs=hT[:, kk, e0:e0 + 512], start=(kk == 0), stop=(kk == 7)), r=[uWg] + hr, w=[upg])
                                k.op(A, lambda e, pg=pg, m=m, bi=bi: e.activation(out=sgm[:], in_=pg[:], func=AF.Sigmoid, bias=prT[:, m, 34 + bi:35 + bi], scale=1.0), r=[upg, uL], w=[usgm])
                                if bi == 0:
                                    k.op(V, lambda e, py=py, m=m, tg=tg: e.tensor_tensor(out=mg[:, m, tg * 512:(tg + 1) * 512], in0=py[:], in1=sgm[:], op=ALU.mult), r=[upy, usgm], w=[umg[tg]])
                                else:
                                    k.op(V, lambda e, py=py: e.tensor_tensor(out=tmpm[:], in0=py[:], in1=sgm[:], op=ALU.mult), r=[upy, usgm], w=[utmpm])
                                    k.op(G, lambda e, m=m, tg=tg: e.tensor_tensor(out=mg[:, m, tg * 512:(tg + 1) * 512], in0=mg[:, m, tg * 512:(tg + 1) * 512], in1=tmpm[:], op=ALU.add), r=[utmpm, umg[tg]], w=[umg[tg]])
                        k.barrier()
                with ExitStack() as st2:
                    Wo = sb(st2, "Wo", [128, 8, D], BF16); uWo = U()
                    for j in range(2):
                        k.dma(G, Wo[:, j * 4:(j + 1) * 4, :], w_out[l].rearrange("(k p) c -> p k c", p=128)[:, j * 4:(j + 1) * 4, :], w=[uWo])
                    mb = sb(st2, "mb", [128, 8, 128], BF16); umb = U()
                    xt = [sb(st2, f"fxt{i}", [128, D], F32) for i in range(2)]; uxt = [U(), U()]
                    xn = [sb(st2, f"fxn{i}", [128, D], F32) for i in range(2)]; uxn = [U(), U()]
                    for n in range(16):
                        b = n % 2
                        k.op(A, lambda e, n=n: e.activation(out=mb[:], in_=mg[:, :, n * 128:(n + 1) * 128], func=AF.Copy), r=[umg[n // 4]], w=[umb])
                        k.dma(S, xt[b][:], xsrc[128 + n * 128:128 + (n + 1) * 128, :], r=[u_x1e], w=[uxt[b]])
                        for half in range(2):
                            for kk in range(8):
                                k.op(PE, lambda e, half=half, kk=kk: e.matmul(PF[half][:], lhsT=mb[:, kk, :], rhs=Wo[:, kk, half * 512:(half + 1) * 512], start=(kk == 0), stop=(kk == 7)), r=[umb, uWo], w=[uPF[half]])
                            k.op(V, lambda e, half=half, b=b: e.tensor_tensor(out=xn[b][:, half * 512:(half + 1) * 512], in0=PF[half][:], in1=xt[b][:, half * 512:(half + 1) * 512], op=ALU.add), r=[uPF[half], uxt[b]], w=[uxn[b]])
                        if last:
                            k.dma(S, y_out[n * 128:(n + 1) * 128, :], xn[b][:], r=[uxn[b]], w=[u_yout])
                        else:
                            k.dma(S, x1e[128 + n * 128:128 + (n + 1) * 128, :], xn[b][:], r=[uxn[b]], w=[u_x1e])
                            if n == 0:
                                k.dma(S, h_bounce.ap()[0:128, :], xn[b][:], r=[uxn[b]], w=[u_hb])
                            if n == 15:
                                k.dma(S, h_bounce.ap()[128:256, :], xn[b][:], r=[uxn[b]], w=[u_hb])
                    k.barrier()
            if not last:
                ccs = k.new_sem("cch")
                k.custom(G, lambda e: e.collective_compute("AllGather", ALU.bypass, replica_groups=RG, ins=[h_bounce.ap().opt()], outs=[h_gath.ap().opt()]), ccs, 1, r=[u_hb], w=[u_hg])
                with ExitStack() as st2:
                    hb = sb(st2, "hb", [128, 2, D], F32); uhb = U()
                    accL = sb(st2, "accL", [128, D], F32); accR = sb(st2, "accR", [128, D], F32); uaL = U(); uaR = U()
                    for r_ in range(4):
                        k.dma(S, hb[:], h_gath.ap()[r_ * 256:(r_ + 1) * 256, :].rearrange("(a p) f -> p a f", p=128), r=[u_hg], w=[uhb])
                        for (acc, ua, a_, sc) in ((accL, uaL, 1, r_), (accR, uaR, 0, 4 + r_)):
                            if r_ == 0:
                                k.op(V, lambda e, acc=acc, a_=a_, sc=sc: e.tensor_scalar_mul(out=acc[:], in0=hb[:, a_, :], scalar1=sel[:, sc:sc + 1]), r=[uhb, uC], w=[ua])
                            else:
                                k.op(V, lambda e, acc=acc, a_=a_, sc=sc: e.scalar_tensor_tensor(out=acc[:], in0=hb[:, a_, :], scalar=sel[:, sc:sc + 1], in1=acc[:], op0=ALU.mult, op1=ALU.add), r=[uhb, uC, ua], w=[ua])
                    k.dma(S, x1e[0:128, :], accL[:], r=[uaL], w=[u_x1e])
                    k.dma(S, x1e[TE - 128:TE, :], accR[:], r=[uaR], w=[u_x1e])
                    k.barrier()

        for l in range(NL):
            xsrc = x_ext if l == 0 else x1e
            last = (l == NL - 1)
            k.dma(S, lg[:], ret_decay[l:l + 1, :].partition_broadcast(128), w=[uL])
            k.dma(S, qg[:], q_norm_g[l:l + 1, :].partition_broadcast(128), w=[uL])
            k.dma(S, kg[:], k_norm_g[l:l + 1, :].partition_broadcast(128), w=[uL])
            k.dma(S, snk[:], attn_sink[l:l + 1, :].partition_broadcast(128), w=[uL])
            k.dma(S, prm[0:31, :], conv_dw[l], w=[uL])
            k.dma(S, prm[31:32, :], conv_b[l], w=[uL])
            k.dma(S, prm[32:33, :], conv_ln_g[l], w=[uL])
            k.dma(S, prm[33:34, :], conv_ln_b[l], w=[uL])
            k.dma(S, prm[34:37, :], b_gate[l], w=[uL])
            k.op(A, lambda e: e.activation(out=lg[:], in_=lg[:], func=AF.Exp), r=[uL], w=[uL])
            k.op(V, lambda e: e.tensor_scalar_mul(out=lg[:], in0=lg[:], scalar1=-1.0), r=[uL], w=[uL])
            k.op(V, lambda e: e.tensor_tensor(out=g2[:, 0:64], in0=qg[:], in1=qg[:], op=ALU.mult), r=[uL], w=[uL])
            k.op(V, lambda e: e.tensor_tensor(out=g2[:, 64:128], in0=kg[:], in1=kg[:], op=ALU.mult), r=[uL], w=[uL])
            k.op(V, lambda e: e.tensor_reduce(out=tmpc[:, 0:1], in_=g2[:, 0:64], axis=AX.X, op=ALU.max), r=[uL], w=[uL])
            k.op(V, lambda e: e.tensor_reduce(out=tmpc[:, 1:2], in_=g2[:, 64:128], axis=AX.X, op=ALU.max), r=[uL], w=[uL])
            k.op(V, lambda e: e.tensor_tensor(out=tmpc[:, 2:3], in0=tmpc[:, 0:1], in1=tmpc[:, 1:2], op=ALU.mult), r=[uL], w=[uL])
            k.op(A, lambda e: e.activation(out=tmpc[:, 3:4], in_=tmpc[:, 2:3], func=AF.Sqrt), r=[uL], w=[uL])
            k.op(V, lambda e: e.tensor_scalar_mul(out=negc[:], in0=tmpc[:, 3:4], scalar1=-8.0), r=[uL], w=[uL])
            k.op(A, lambda e: e.activation(out=snke[:], in_=snk[:], func=AF.Exp, bias=negc[:, 0:1], scale=1.0), r=[uL], w=[uL])
            for c in range(8):
                k.op(PE, lambda e, c=c: e.transpose(out=PF[0][:, c * 37:(c + 1) * 37], in_=prm[:, c * 128:(c + 1) * 128], identity=ident[0:37, 0:37]), r=[uL, uC], w=[uPF[0]])
            k.op(V, lambda e: e.tensor_copy(out=prT[:].rearrange("p c j -> p (c j)"), in_=PF[0][:, 0:296]), r=[uPF[0]], w=[uL])
            k.barrier()

            with ExitStack() as st:
                g_bc = sb(st, "g_bc", [128, D], F32); ug = U()
                k.dma(S, g_bc[:], norm_g[l:l + 1, :].partition_broadcast(128), w=[ug])
                xt = [sb(st, f"xt{i}", [128, D], F32) for i in range(2)]; uxt = [U(), U()]
                junk = [sb(st, f"junk{i}", [128, D], BF16) for i in range(2)]; ujunk = [U(), U()]
                xs = [sb(st, f"xs{i}", [128, D], BF16) for i in range(2)]; uxs = [U(), U()]
                ssq = [sb(st, f"ssq{i}", [128, 2], F32) for i in range(2)]; ussq = [U(), U()]
                def a_s1(t):
                    b = t % 2
                    k.dma(S, xt[b][:], xsrc[t * 128:(t + 1) * 128, :], r=[u_x1e], w=[uxt[b]])
                    k.op(V, lambda e, b=b: e.memset(ssq[b][:], 0.0), w=[ussq[b]])
                    k.op(A, lambda e, b=b: e.activation(out=junk[b][:], in_=xt[b][:], func=AF.Square, accum_out=ssq[b][:, 0:1]), r=[uxt[b]], w=[ujunk[b], ussq[b]])
                    k.op(V, lambda e, b=b: e.tensor_scalar(out=ssq[b][:, 1:2], in0=ssq[b][:, 0:1], scalar1=1.0 / D, scalar2=EPS, op0=ALU.mult, op1=ALU.add), r=[ussq[b]], w=[ussq[b]])
                    k.op(A, lambda e, b=b: e.activation(out=ssq[b][:, 1:2], in_=ssq[b][:, 1:2], func=AF.Sqrt), r=[ussq[b]], w=[ussq[b]])
                    k.op(V, lambda e, b=b: e.reciprocal(out=ssq[b][:, 1:2], in_=ssq[b][:, 1:2]), r=[ussq[b]], w=[ussq[b]])
                    k.op(V, lambda e, b=b: e.scalar_tensor_tensor(out=xs[b][:], in0=xt[b][:], scalar=ssq[b][:, 1:2], in1=g_bc[:], op0=ALU.mult, op1=ALU.mult), r=[uxt[b], ussq[b], ug], w=[uxs[b]])

                def a_s2(t):
                    b = t % 2
                    for c in range(8):
                        k.op(PE, lambda e, b=b, c=c: e.transpose(out=PT[:, c, :], in_=xs[b][:, c * 128:(c + 1) * 128], identity=identb[:]), r=[uxs[b], uC], w=[uPT])
                    k.op(A, lambda e, t=t: e.activation(out=hT[:, :, t * 128:(t + 1) * 128], in_=PT[:], func=AF.Copy), r=[uPT], w=[uhT[t]])
                a_s1(0)
                for t in range(18):
                    if t < 17:
                        a_s1(t + 1)
                    a_s2(t)
                k.barrier()

            if 'C' in phases:
                with ExitStack() as st:
                    Wc = sb(st, "Wc", [128, 8, 3072], BF16); uWc = U()
                    for j in range(6):
                        k.dma(G, Wc[:, :, j * 512:(j + 1) * 512], w_in[l, :, j * 512:(j + 1) * 512].rearrange("(k p) c -> p k c", p=128), w=[uWc])
                    sig = [sb(st, f"sig{i}", [128, 544], F32) for i in range(2)]; usig = [U(), U()]
                    vv = [sb(st, f"vv{i}", [128, 544], F32) for i in range(2)]; uvv = [U(), U()]
                    acc1 = [sb(st, f"acc1{i}", [128, 512], F32) for i in range(2)]; uacc1 = [U(), U()]
                    acc2 = [sb(st, f"acc2{i}", [128, 512], F32) for i in range(2)]; uacc2 = [U(), U()]
                    tmpk = [sb(st, f"tmpk{i}", [128, 512], F32) for i in range(4)]; utmpk = [U() for _ in range(4)]
                    yy = sb(st, "yy", [128, 8, 512], F32); uyy = [U() for _ in range(8)]
                    ysq = [sb(st, f"ysq{i}", [128, 512], F32) for i in range(2)]; uysq = [U(), U()]
                    mean = sb(st, "mean", [128, 512], F32); msq = sb(st, "msq", [128, 512], F32)
                    rstd = sb(st, "rstd", [128, 512], F32); ustat = U()
                    t1 = [sb(st, f"t1{i}", [128, 512], F32) for i in range(2)]; ut1 = [U(), U()]
                    t2 = [sb(st, f"t2{i}", [128, 512], F32) for i in range(2)]; ut2 = [U(), U()]
                    s1 = [sb(st, f"s1{i}", [128, 512], F32) for i in range(2)]; us1 = [U(), U()]
                    sg = [sb(st, f"sg{i}", [128, 512], F32) for i in range(2)]; usg = [U(), U()]
                    ocT = sb(st, "ocT", [128, 8, 512], BF16); uocT = U()
                    pTls = [PF[4], PF[5]]; uTl = [uPF[4], uPF[5]]
                    pST, pSQ, uST, uSQ = PF[6], PF[4], uPF[6], uPF[4]
                    it = 0
                    tk = 0

                    def proj(tg, c, b):
                        e0 = 128 + tg * 512
                        hr = uhT[(e0 - 16) // 128:(e0 + 528 + 127) // 128]
                        pA, uA, pB, uB = PF[b], uPF[b], PF[2 + b], uPF[2 + b]
                        tb = 0
                        pTl = pTls[b]
                        for (ps, ups, col0, to) in ((pA, uA, c * 128, tb), (pB, uB, 1024 + c * 128, tb + 32)):
                            for kk in range(8):
                                k.op(PE, lambda e, ps=ps, kk=kk, col0=col0: e.matmul(ps[:, 0:512], lhsT=Wc[:, kk, col0:col0 + 128], rhs=hT[:, kk, e0 - 16:e0 + 496], start=(kk == 0), stop=(kk == 7)), r=[uWc] + hr, w=[ups])
                            for kk in range(8):
                                k.op(PE, lambda e, kk=kk, col0=col0, to=to, pTl=pTl: e.matmul(pTl[:, to:to + 32], lhsT=Wc[:, kk, col0:col0 + 128], rhs=hT[:, kk, e0 + 496:e0 + 528], start=(kk == 0), stop=(kk == 7)), r=[uWc] + hr, w=[uTl[b]])
                    for tg in range(4):
                        e0 = 128 + tg * 512
                        hr = uhT[(e0 - 16) // 128:(e0 + 528 + 127) // 128]
                        proj(tg, 0, it % 2)
                        for c in range(8):
                            b = it % 2
                            it += 1
                            pA, uA, pB, uB = PF[b], uPF[b], PF[2 + b], uPF[2 + b]
                            tb = 0
                            pTl = pTls[b]
                            if c < 7:
                                proj(tg, c + 1, it % 2)
                            k.op(A, lambda e, b=b, pB=pB: e.activation(out=sig[b][:, 0:512], in_=pB[:, 0:512], func=AF.Sigmoid), r=[uB], w=[usig[b]])
                            k.op(A, lambda e, b=b, tb=tb, pTl=pTl: e.activation(out=sig[b][:, 512:544], in_=pTl[:, tb + 32:tb + 64], func=AF.Sigmoid), r=[uTl[b]], w=[usig[b]])
                            k.op(V, lambda e, b=b, pA=pA: e.tensor_tensor(out=vv[b][:, 0:512], in0=pA[:, 0:512], in1=sig[b][:, 0:512], op=ALU.mult), r=[uA, usig[b]], w=[uvv[b]])
                            k.op(V, lambda e, b=b, tb=tb, pTl=pTl: e.tensor_tensor(out=vv[b][:, 512:544], in0=pTl[:, tb:tb + 32], in1=sig[b][:, 512:544], op=ALU.mult), r=[uTl[b], usig[b]], w=[uvv[b]])
                            k.op(V, lambda e, c=c, b=b: e.tensor_scalar(out=acc1[b][:], in0=vv[b][:, 1:513], scalar1=prT[:, c, 0:1], scalar2=prT[:, c, 31:32], op0=ALU.mult, op1=ALU.add), r=[uvv[b], uL], w=[uacc1[b]])
                            for kt in range(1, 16):
                                k.op(V, lambda e, c=c, kt=kt, b=b: e.scalar_tensor_tensor(out=acc1[b][:], in0=vv[b][:, kt + 1:kt + 513], scalar=prT[:, c, kt:kt + 1], in1=acc1[b][:], op0=ALU.mult, op1=ALU.add), r=[uvv[b], uL, uacc1[b]], w=[uacc1[b]])
                            k.op(A, lambda e, c=c, b=b: e.activation(out=acc2[b][:], in_=vv[b][:, 17:529], func=AF.Copy, scale=prT[:, c, 16:17]), r=[uvv[b], uL], w=[uacc2[b]])
                            for kt in range(17, 31):
                                j = tk % 4
                                tk += 1
                                k.op(A, lambda e, c=c, kt=kt, j=j, b=b: e.activation(out=tmpk[j][:], in_=vv[b][:, kt + 1:kt + 513], func=AF.Copy, scale=prT[:, c, kt:kt + 1]), r=[uvv[b], uL], w=[utmpk[j]])
                                k.op(G, lambda e, j=j, b=b: e.tensor_tensor(out=acc2[b][:], in0=acc2[b][:], in1=tmpk[j][:], op=ALU.add), r=[utmpk[j], uacc2[b]], w=[uacc2[b]])
                            k.op(G, lambda e, c=c, b=b: e.tensor_tensor(out=yy[:, c, :], in0=acc1[b][:], in1=acc2[b][:], op=ALU.add), r=[uacc1[b], uacc2[b]], w=[uyy[c]])
                        for c in range(8):
                            b = c % 2
                            k.op(A, lambda e, c=c, b=b: e.activation(out=ysq[b][:], in_=yy[:, c, :], func=AF.Square), r=[uyy[c]], w=[uysq[b]])
                            k.op(PE, lambda e, c=c: e.matmul(pST[:], lhsT=onesf[:], rhs=yy[:, c, :], start=(c == 0), stop=(c == 7)), r=[uyy[c], uC], w=[uST])
                            k.op(PE, lambda e, c=c, b=b: e.matmul(pSQ[:], lhsT=onesf[:], rhs=ysq[b][:], start=(c == 0), stop=(c == 7)), r=[uysq[b], uC], w=[uSQ])
                        k.op(V, lambda e: e.tensor_copy(out=mean[:], in_=pST[:]), r=[uST], w=[ustat])
                        k.op(G, lambda e: e.tensor_tensor(out=msq[:], in0=mean[:], in1=mean[:], op=ALU.mult), r=[ustat], w=[ustat])
                        k.op(V, lambda e: e.tensor_tensor(out=rstd[:], in0=pSQ[:], in1=msq[:], op=ALU.subtract), r=[uSQ, ustat], w=[ustat])
                        k.op(V, lambda e: e.tensor_scalar_add(out=rstd[:], in0=rstd[:], scalar1=EPS), r=[ustat], w=[ustat])
                        k.op(A, lambda e: e.activation(out=rstd[:], in_=rstd[:], func=AF.Sqrt), r=[ustat], w=[ustat])
                        k.op(V, lambda e: e.reciprocal(out=rstd[:], in_=rstd[:]), r=[ustat], w=[ustat])
                        for c in range(8):
                            b = c % 2
                            pG, uG = PF[b], uPF[b]
                            for kk in range(8):
                                k.op(PE, lambda e, c=c, kk=kk, pG=pG: e.matmul(pG[:], lhsT=Wc[:, kk, 2048 + c * 128:2048 + (c + 1) * 128], rhs=hT[:, kk, e0:e0 + 512], start=(kk == 0), stop=(kk == 7)), r=[uWc] + hr, w=[uG])
                            k.op(V, lambda e, c=c, b=b: e.tensor_tensor(out=t1[b][:], in0=yy[:, c, :], in1=mean[:], op=ALU.subtract), r=[uyy[c], ustat], w=[ut1[b]])
                            k.op(G, lambda e, b=b: e.tensor_tensor(out=t2[b][:], in0=t1[b][:], in1=rstd[:], op=ALU.mult), r=[ut1[b], ustat], w=[ut2[b]])
                            k.op(A, lambda e, c=c, b=b: e.activation(out=s1[b][:], in_=t2[b][:], func=AF.Silu, bias=prT[:, c, 33:34], scale=prT[:, c, 32:33]), r=[ut2[b], uL], w=[us1[b]])
                            k.op(A, lambda e, b=b, pG=pG: e.activation(out=sg[b][:], in_=pG[:], func=AF.Silu), r=[uG], w=[usg[b]])
                            k.op(V, lambda e, c=c, b=b: e.tensor_tensor(out=ocT[:, c, :], in0=s1[b][:], in1=sg[b][:], op=ALU.mult), r=[us1[b], usg[b]], w=[uocT])
                        k.dma(S, OC[:, :, tg * 512:(tg + 1) * 512].rearrange("c p t -> p c t"), ocT[:], r=[uocT], w=[u_OC])
                    k.barrier()
            if 'T' in phases:
                phase_T(l)
            if 'R' in phases:
                phase_R(l)
            if 'F' in phases:
                phase_F(l, xsrc, last)
        k.barrier()
    return nc


def make_consts(core):
    c = core % 4
    seg = c * T
    p = np.arange(128)
    tt = np.arange(T)
    inv = 10000.0 ** (-np.arange(128, dtype=np.float64) / 128.0)
    ang = inv[:, None] * (seg + tt)[None, :].astype(np.float64)
    cosr = np.cos(ang).astype(np.float32)
    sinr = np.sin(ang).astype(np.float32)
    inva = 500000.0 ** (-np.arange(8, dtype=np.float64) / 8.0)
    pos = (seg - 128 + np.arange(18)[None, :] * 128 + p[:, None]).astype(np.float64)
    anga = pos[:, :, None] * inva[None, None, :]
    cosa = np.cos(anga).astype(np.float32)
    sina = np.sin(anga).astype(np.float32)
    j = p[:, None]
    i = p[None, :]
    maskL = (j >= i).astype(np.float32)
    maskR = (j <= i).astype(np.float32)
    mask = np.concatenate([maskL, maskR, maskL * (1.0 if c > 0 else 0.0), maskR * (1.0 if c < 3 else 0.0)], axis=1)
    eseg = np.zeros((128, 32), np.float32)
    for n in range(16):
        eseg[:, n] = 2047 - (n * 128 + p)
        eseg[:, 16 + n] = n * 128 + p
    ek = np.stack([127 - p, p], axis=1).astype(np.float32)
    eq = np.concatenate([np.tile((np.arange(128) + 1)[None, :], (128, 1)), np.tile((128 - np.arange(128))[None, :], (128, 1))], axis=1).astype(np.float32)
    Ef = np.maximum(i - j, 0); Eb = np.maximum(j - i, 0)
    Mf = (i >= j); Mb = (j > i)
    em = np.concatenate([Ef, Eb, Mf, Mb], axis=1).astype(np.float32)
    BIG = 1.0e6
    dist = np.zeros((128, 8), np.float32)
    sel = np.zeros((128, 8), np.float32)
    for r in range(4):
        dist[:, r] = T * (c - r - 1) if r < c else BIG
        dist[:, 4 + r] = T * (r - c - 1) if r > c else BIG
        sel[:, r] = 1.0 if r == c - 1 else 0.0
        sel[:, 4 + r] = 1.0 if r == c + 1 else 0.0
    return dict(c_cosr=cosr, c_sinr=sinr, c_cosa=cosa, c_sina=sina, c_mask=mask,
                c_ident=np.eye(128, dtype=np.float32), c_eseg=eseg, c_ek=ek, c_eq=eq, c_em=em,
                c_dist=dist, c_sel=sel)


def make_in_maps(inputs):
    x = np.asarray(inputs['x'], np.float32)
    shared = dict(
        norm_g=np.asarray(inputs['norm_g'], np.float32),
        w_in=np.asarray(inputs['w_in'], np.float32),
        b_gate=np.asarray(inputs['b_gate'], np.float32).reshape(2, 3, D),
        conv_dw=np.asarray(inputs['conv_dw'], np.float32),
        conv_b=np.asarray(inputs['conv_b'], np.float32).reshape(2, 1, D),
        conv_ln_g=np.asarray(inputs['conv_ln_g'], np.float32).reshape(2, 1, D),
        conv_ln_b=np.asarray(inputs['conv_ln_b'], np.float32).reshape(2, 1, D),
        ret_decay=np.asarray(inputs['ret_decay'], np.float32).reshape(2, 8),
        q_norm_g=np.asarray(inputs['q_norm_g'], np.float32),
        k_norm_g=np.asarray(inputs['k_norm_g'], np.float32),
        attn_sink=np.asarray(inputs['attn_sink'], np.float32),
        w_conv_out=np.asarray(inputs['w_conv_out'], np.float32),
        w_ret_out=np.asarray(inputs['w_ret_out'], np.float32),
        w_attn_out=np.asarray(inputs['w_attn_out'], np.float32),
        w_out=np.asarray(inputs['w_out'], np.float32),
    )
    in_maps = []
    for core in range(8):
        b, c = core // 4, core % 4
        xe = np.zeros((TE, D), np.float32)
        lo = c * T - 128
        hi = c * T + T + 128
        slo, shi = max(lo, 0), min(hi, 4 * T)
        xe[slo - lo:shi - lo] = x[b, slo:shi]
        m = dict(shared)
        m['x_ext'] = xe
        m.update(make_consts(core))
        in_maps.append(m)
    return in_maps


_NC = None


def kernel(**inputs):
    global _NC
    if _NC is None:
        _NC = build(2)
    in_maps = make_in_maps(inputs)
    res = run_bass_kernel_spmd(_NC, in_maps, core_ids=list(range(8)))
    out = np.zeros((2, 4 * T, D), np.float32)
    for core in range(8):
        b, c = core // 4, core % 4
        out[b, c * T:(c + 1) * T] = res.results[core]["y_out"]
    return out
```

```python
import os
import numpy as np
import concourse.bass as bass
import concourse.mybir as mybir
from concourse.bass_utils import run_bass_kernel_spmd
from contextlib import ExitStack

F32 = mybir.dt.float32
BF16 = mybir.dt.bfloat16
ALU = mybir.AluOpType
AF = mybir.ActivationFunctionType
AX = mybir.AxisListType

ENG = ['tensor', 'vector', 'scalar', 'gpsimd', 'sync']
EPOCH = 20000
ND = 8
PE, V, A, G, S = 'tensor', 'vector', 'scalar', 'gpsimd', 'sync'


class U:
    __slots__ = ('w', 'rs')

    def __init__(s):
        s.w = None
        s.rs = {}


class KB:
    def __init__(s, nc, stack):
        s.nc = nc
        s.st = stack
        s.cnt = {e: 0 for e in ENG}
        s.nsem = 0
        s.sem = {e: s.new_sem(f'e_{e}') for e in ENG}
        s.hist = {e: [] for e in ENG}
        s.waited = {e: {} for e in ENG}
        s.dsem = {}
        s.dtarget = {}
        s.dcount = {}
        s.n_inst = 0

    def new_sem(s, name):
        s.nsem += 1
        return s.st.enter_context(s.nc.semaphore(f'{name}_{s.nsem}'))

    def _waits(s, engine, r, w):
        deps = {}

        def add(tok):
            key = id(tok[0])
            if key not in deps or deps[key][1] < tok[1]:
                deps[key] = tok
        for u in r:
            if u.w is not None:
                add(u.w)
        for u in w:
            if u.w is not None:
                add(u.w)
            for tok in u.rs.values():
                add(tok)
        waits = []
        wd = s.waited[engine]
        for key, (sem, val, src) in deps.items():
            if engine == PE and src == PE:
                continue
            if wd.get(key, 0) >= val:
                continue
            wd[key] = val
            waits.append((sem, val))
        return waits

    def _emit(s, ename, waits, fn, inc):
        e = getattr(s.nc, ename)
        for sem, val in waits:
            e.wait_ge(sem, val)
        if fn is None:
            return
        ins = fn(e)
        if inc[1] is None:
            ins.then_inc(inc[0])
        else:
            ins.then_inc(inc[0], inc[1])
        s.n_inst += 1

    def op(s, engine, fn, r=(), w=()):
        waits = s._waits(engine, r, w)
        if s.cnt[engine] >= EPOCH:
            s.hist[engine].append((s.sem[engine], s.cnt[engine]))
            s.sem[engine] = s.new_sem(f'e_{engine}')
            s.cnt[engine] = 0
        s.cnt[engine] += 1
        sem = s.sem[engine]
        tok = (sem, s.cnt[engine], engine)
        s._emit(engine, waits, fn, (sem, 1))
        for u in r:
            u.rs[id(sem)] = tok
        for u in w:
            u.w = tok
            u.rs = {}
        return tok

    def dma(s, q, out, in_, r=(), w=(), **kw):
        waits = s._waits(q, r, w)
        if q not in s.dsem:
            s.dsem[q] = [s.new_sem(f'd_{q}{i}') for i in range(ND)]
            s.dtarget[q] = [0] * ND
            s.dcount[q] = 0
        i = s.dcount[q] % ND
        s.dcount[q] += 1
        sem = s.dsem[q][i]
        prev = s.dtarget[q][i]
        if prev > 0 and s.waited[q].get(id(sem), 0) < prev:
            s.waited[q][id(sem)] = prev
            waits.append((sem, prev))
        tgt = prev + 16
        s.dtarget[q][i] = tgt
        tok = (sem, tgt, 'dma')
        s._emit(q, waits, lambda e: e.dma_start(out=out, in_=in_, **kw), (sem, 16))
        for u in r:
            u.rs[id(sem)] = tok
        for u in w:
            u.w = tok
            u.rs = {}
        return tok

    def custom(s, engine, fn, inc_sem, inc_val, r=(), w=()):
        waits = s._waits(engine, r, w)
        tok = (inc_sem, inc_val, 'custom')
        s._emit(engine, waits, fn, (inc_sem, None))
        for u in r:
            u.rs[id(inc_sem)] = tok
        for u in w:
            u.w = tok
            u.rs = {}
        return tok

    def barrier(s):
        toks = []
        for e in ENG:
            for sem, c in s.hist[e]:
                toks.append((sem, c))
            if s.cnt[e] > 0:
                toks.append((s.sem[e], s.cnt[e]))
        for q in s.dsem:
            for i in range(ND):
                if s.dtarget[q][i] > 0:
                    toks.append((s.dsem[q][i], s.dtarget[q][i]))
        for e in ENG:
            wd = s.waited[e]
            waits = []
            for sem, val in toks:
                if wd.get(id(sem), 0) >= val:
                    continue
                wd[id(sem)] = val
                waits.append((sem, val))
            s._emit(e, waits, None, None)


D = 1024
T = 2048
TE = 2304
NT = 16
INW = 14848
OFF_CGLU, OFF_CGATE = 0, 2048
OFF_RQ, OFF_RK, OFF_RV, OFF_RG = 3072, 4096, 5120, 7168
OFF_AQ, OFF_AK, OFF_AV, OFF_AG = 9216, 10240, 10496, 10752
OFF_GL = 11776
EPS = 1e-6


def build(NL=2, dbg=False, phases='CTRF'):
    nc = bass.Bass("TRN2", target_bir_lowering=False)

    def din(name, shape):
        return nc.dram_tensor(name, shape, F32, kind="ExternalInput").ap()

    x_ext = din("x_ext", [TE, D])
    norm_g = din("norm_g", [2, D])
    w_in = din("w_in", [2, D, INW])
    b_gate = din("b_gate", [2, 3, D])
    conv_dw = din("conv_dw", [2, 31, D])
    conv_b = din("conv_b", [2, 1, D])
    conv_ln_g = din("conv_ln_g", [2, 1, D])
    conv_ln_b = din("conv_ln_b", [2, 1, D])
    ret_decay = din("ret_decay", [2, 8])
    q_norm_g = din("q_norm_g", [2, 64])
    k_norm_g = din("k_norm_g", [2, 64])
    attn_sink = din("attn_sink", [2, 16])
    w_conv_out = din("w_conv_out", [2, D, D])
    w_ret_out = din("w_ret_out", [2, 2 * D, D])
    w_attn_out = din("w_attn_out", [2, D, D])
    w_out = din("w_out", [2, D, D])
    c_cosr = din("c_cosr", [128, T])
    c_sinr = din("c_sinr", [128, T])
    c_cosa = din("c_cosa", [128, 18, 8])
    c_sina = din("c_sina", [128, 18, 8])
    c_mask = din("c_mask", [128, 512])
    c_ident = din("c_ident", [128, 128])
    c_eseg = din("c_eseg", [128, 32])
    c_ek = din("c_ek", [128, 2])
    c_eq = din("c_eq", [128, 256])
    c_em = din("c_em", [128, 512])
    c_dist = din("c_dist", [128, 8])
    c_sel = din("c_sel", [128, 8])

    y_out = nc.dram_tensor("y_out", [T, D], F32, kind="ExternalOutput").ap()
    okind = "ExternalOutput" if dbg else "Internal"
    OC = nc.dram_tensor("OC", [8, 128, T], BF16, kind=okind).ap()
    OA = nc.dram_tensor("OA", [16, 64, T], BF16, kind=okind).ap()
    OR = nc.dram_tensor("OR", [16, 128, T], BF16, kind=okind).ap()
    x1e = nc.dram_tensor("x1e", [TE, D], F32, kind=okind).ap()
    fs_bounce = nc.dram_tensor("fs_bounce", [512, 512], F32)
    fs_gath = nc.dram_tensor("fs_gath", [2048, 512], F32)
    h_bounce = nc.dram_tensor("h_bounce", [256, D], F32)
    h_gath = nc.dram_tensor("h_gath", [1024, D], F32)
    u_OC, u_OA, u_OR, u_x1e, u_fsb, u_fsg, u_hb, u_hg, u_yout = [U() for _ in range(9)]
    RG = [[0, 1, 2, 3], [4, 5, 6, 7]]

    with ExitStack() as st0:
        k = KB(nc, st0)

        ncount = [0]

        def sb(st, name, shape, dt):
            ncount[0] += 1
            return st.enter_context(nc.sbuf_tensor(f"{name}_{ncount[0]}", shape, dt))

        PF = [st0.enter_context(nc.psum_tensor(f"pf{i}", [128, 512], F32)) for i in range(7)]
        uPF = [U() for _ in range(7)]
        PT = st0.enter_context(nc.psum_tensor("ptb", [128, 8, 128], BF16))
        uPT = U()
        hT = sb(st0, "hT", [128, 8, TE], BF16)
        uhT = [U() for _ in range(18)]
        ident = sb(st0, "ident", [128, 128], F32); identb = sb(st0, "identb", [128, 128], BF16)
        onesf = sb(st0, "onesf", [128, 128], F32); onesb = sb(st0, "onesb", [128, 128], BF16)
        maskb = sb(st0, "maskb", [128, 512], BF16)
        eseg = sb(st0, "eseg", [128, 32], F32); ek = sb(st0, "ek", [128, 2], F32)
        eq = sb(st0, "eq", [128, 256], F32); em = sb(st0, "em", [128, 512], F32)
        dist = sb(st0, "dist", [128, 8], F32); sel = sb(st0, "sel", [128, 8], F32)
        cosa = sb(st0, "cosa", [128, 18, 8], F32); sina = sb(st0, "sina", [128, 18, 8], F32)
        lg = sb(st0, "lg", [128, 8], F32)
        qg = sb(st0, "qg", [128, 64], F32); kg = sb(st0, "kg", [128, 64], F32)
        snk = sb(st0, "snk", [128, 16], F32); snke = sb(st0, "snke", [128, 16], F32)
        negc = sb(st0, "negc", [128, 1], F32); tmpc = sb(st0, "tmpc", [128, 4], F32)
        g2 = sb(st0, "g2", [128, 128], F32)
        prm = sb(st0, "prm", [37, D], F32)
        prT = sb(st0, "prT", [128, 8, 37], F32)
        uC = U()
        uL = U()

        for (t, src) in ((ident, c_ident), (eseg, c_eseg), (ek, c_ek), (eq, c_eq), (em, c_em),
                         (dist, c_dist), (sel, c_sel), (cosa, c_cosa), (sina, c_sina)):
            k.dma(S, t[:], src, w=[uC])
        k.dma(G, identb[:], c_ident, w=[uC])
        k.dma(G, maskb[:], c_mask, w=[uC])
        k.op(V, lambda e: e.memset(onesf[:], 1.0 / 1024.0), w=[uC])
        k.op(V, lambda e: e.memset(onesb[:], 1.0), w=[uC])
        k.barrier()

        def phase_T(l):
            with ExitStack() as st:
                Wt = sb(st, "Wt", [128, 8, 2560], BF16); uWt = U()
                for j in range(5):
                    k.dma(G, Wt[:, :, j * 512:(j + 1) * 512], w_in[l, :, OFF_AQ + j * 512:OFF_AQ + (j + 1) * 512].rearrange("(k p) c -> p k c", p=128), w=[uWt])
                qTg = sb(st, "qTg", [64, 16, 512], BF16); uqT = [U() for _ in range(4)]
                kT = sb(st, "kTres", [64, 4, TE], BF16); ukT = [U() for _ in range(18)]
                Vr = sb(st, "Vres", [128, 18, 256], BF16); uVr = [U() for _ in range(18)]
                sq = sb(st, "sq", [128, 1280], F32); usq = U()
                ssa = sb(st, "ssa", [128, 20], F32); rsa = sb(st, "rsa", [128, 20], F32); urs = U()
                qn = sb(st, "qn", [128, 1280], F32); uqn = U()
                rt = [sb(st, f"rt{i}", [128, 20, 8], F32) for i in range(4)]; urt = [U() for _ in range(4)]
                qb = sb(st, "qb", [128, 1024], BF16); uqb = U()
                kb = sb(st, "kb", [128, 256], BF16); ukb = U()
                sq3 = sq[:].rearrange("p (h d) -> p h d", d=64)
                qn3 = qn[:].rearrange("p (h d) -> p h d", d=64)

                def norm_rot(t, h0, h1):
                    nh = h1 - h0
                    k.op(V, lambda e: e.tensor_reduce(out=ssa[:, h0:h1], in_=sq3[:, h0:h1, :], axis=AX.X, op=ALU.add), r=[usq], w=[urs])
                    k.op(V, lambda e: e.tensor_scalar(out=rsa[:, h0:h1], in0=ssa[:, h0:h1], scalar1=1.0 / 64.0, scalar2=EPS, op0=ALU.mult, op1=ALU.add), r=[urs], w=[urs])
                    k.op(A, lambda e: e.activation(out=rsa[:, h0:h1], in_=rsa[:, h0:h1], func=AF.Sqrt), r=[urs], w=[urs])
                    k.op(V, lambda e: e.reciprocal(out=rsa[:, h0:h1], in_=rsa[:, h0:h1]), r=[urs], w=[urs])
                    if h0 == 0:
                        for half in range(2):
                            k.op(V, lambda e, half=half: e.tensor_tensor(out=qn3[:, half * 8:(half + 1) * 8, :], in0=PF[half][:].rearrange("p (h d) -> p h d", d=64), in1=rsa[:, half * 8:(half + 1) * 8].unsqueeze(2).broadcast_to([128, 8, 64]), op=ALU.mult), r=[uPF[half], urs], w=[uqn])
                        k.op(G, lambda e: e.tensor_tensor(out=qn3[:, 0:16, :], in0=qn3[:, 0:16, :], in1=qg[:].unsqueeze(1).broadcast_to([128, 16, 64]), op=ALU.mult), r=[uqn, uL], w=[uqn])
                    else:
                        k.op(V, lambda e: e.tensor_tensor(out=qn3[:, 16:20, :], in0=PF[2][:, 0:256].rearrange("p (h d) -> p h d", d=64), in1=rsa[:, 16:20].unsqueeze(2).broadcast_to([128, 4, 64]), op=ALU.mult), r=[uPF[2], urs], w=[uqn])
                        k.op(G, lambda e: e.tensor_tensor(out=qn3[:, 16:20, :], in0=qn3[:, 16:20, :], in1=kg[:].unsqueeze(1).broadcast_to([128, 4, 64]), op=ALU.mult), r=[uqn, uL], w=[uqn])
                    cb = cosa[:, t, :].unsqueeze(1).broadcast_to([128, nh, 8])
                    sbb = sina[:, t, :].unsqueeze(1).broadcast_to([128, nh, 8])
                    x1 = qn3[:, h0:h1, 0:8]
                    x2 = qn3[:, h0:h1, 8:16]
                    k.op(V, lambda e: e.tensor_tensor(out=rt[0][:, h0:h1, :], in0=x1, in1=cb, op=ALU.mult), r=[uqn, uC], w=[urt[0]])
                    k.op(G, lambda e: e.tensor_tensor(out=rt[1][:, h0:h1, :], in0=x2, in1=sbb, op=ALU.mult), r=[uqn, uC], w=[urt[1]])
                    k.op(V, lambda e: e.tensor_tensor(out=rt[2][:, h0:h1, :], in0=x2, in1=cb, op=ALU.mult), r=[uqn, uC], w=[urt[2]])
                    k.op(G, lambda e: e.tensor_tensor(out=rt[3][:, h0:h1, :], in0=x1, in1=sbb, op=ALU.mult), r=[uqn, uC], w=[urt[3]])
                    k.op(V, lambda e: e.tensor_tensor(out=x1, in0=rt[0][:, h0:h1, :], in1=rt[1][:, h0:h1, :], op=ALU.subtract), r=[urt[0], urt[1], urt[2], urt[3]], w=[uqn])
                    k.op(G, lambda e: e.tensor_tensor(out=x2, in0=rt[2][:, h0:h1, :], in1=rt[3][:, h0:h1, :], op=ALU.add), r=[urt[2], urt[3]], w=[uqn])

                for t in range(18):
                    for kk in range(8):
                        k.op(PE, lambda e, kk=kk, t=t: e.matmul(PF[2][:], lhsT=hT[:, kk, t * 128:(t + 1) * 128], rhs=Wt[:, kk, 1024:1536], start=(kk == 0), stop=(kk == 7)), r=[uWt, uhT[t]], w=[uPF[2]])
                    k.op(A, lambda e: e.activation(out=sq[:, 1024:1280], in_=PF[2][:, 0:256], func=AF.Square), r=[uPF[2]], w=[usq])
                    norm_rot(t, 16, 20)
                    k.op(A, lambda e: e.activation(out=kb[:], in_=qn[:, 1024:1280], func=AF.Copy), r=[uqn], w=[ukb])
                    k.op(A, lambda e, t=t: e.activation(out=Vr[:, t, :], in_=PF[2][:, 256:512], func=AF.Copy), r=[uPF[2]], w=[uVr[t]])
                    for g in range(4):
                        k.op(PE, lambda e, g=g: e.transpose(out=PT[0:64, g, :], in_=kb[:, g * 64:(g + 1) * 64], identity=identb[:]), r=[ukb, uC], w=[uPT])
                    k.op(V, lambda e, t=t: e.tensor_copy(out=kT[:, :, t * 128:(t + 1) * 128], in_=PT[0:64, 0:4, :]), r=[uPT], w=[ukT[t]])
                if dbg == 'T1':
                    k.barrier()
                    return
                gT = sb(st, "gT", [64, 16, 512], BF16); ugT = U()
                pt = [sb(st, f"pt{i}", [128, 512], BF16) for i in range(3)]; upt = [U() for _ in range(3)]
                den = sb(st, "den", [64, 512], F32); uden = U()
                rec = sb(st, "rec", [64, 512], F32); urec = U()
                on = sb(st, "on", [64, 512], F32); uon = U()
                on_b = sb(st, "on_b", [64, 512], F32); uon_b = U()
                on2 = [on, on_b]; uon2 = [uon, uon_b]
                pt_b = [sb(st, f"ptb{i}", [128, 512], BF16) for i in range(3)]; upt_b = [U() for _ in range(3)]
                pt2 = [pt, pt_b]; upt2 = [upt, upt_b]
                oaT = sb(st, "oaT", [64, 16, 512], BF16); uoaT = U()
                for tg in range(4):
                    e0 = 128 + tg * 512
                    hr = uhT[e0 // 128:(e0 + 512) // 128]
                    for nn in range(4):
                        t = tg * 4 + nn + 1
                        for half in range(2):
                            for kk in range(8):
                                k.op(PE, lambda e, half=half, kk=kk, t=t: e.matmul(PF[half][:], lhsT=hT[:, kk, t * 128:(t + 1) * 128], rhs=Wt[:, kk, half * 512:(half + 1) * 512], start=(kk == 0), stop=(kk == 7)), r=[uWt, uhT[t]], w=[uPF[half]])
                            k.op(A, lambda e, half=half: e.activation(out=sq[:, half * 512:(half + 1) * 512], in_=PF[half][:], func=AF.Square), r=[uPF[half]], w=[usq])
                        norm_rot(t, 0, 16)
                        k.op(A, lambda e: e.activation(out=qb[:], in_=qn[:, 0:1024], func=AF.Copy), r=[uqn], w=[uqb])
                        for rr in range(2):
                            for j in range(8):
                                hh = rr * 8 + j
                                k.op(PE, lambda e, j=j, hh=hh: e.transpose(out=PT[0:64, j, :], in_=qb[:, hh * 64:(hh + 1) * 64], identity=identb[:]), r=[uqb, uC], w=[uPT])
                            k.op(V, lambda e, rr=rr, nn=nn: e.tensor_copy(out=qTg[:, rr * 8:(rr + 1) * 8, nn * 128:(nn + 1) * 128], in_=PT[0:64, :, :]), r=[uPT], w=[uqT[nn]])
                    for hh in range(16):
                        ps, ups = PF[hh % 2], uPF[hh % 2]
                        for kk in range(8):
                            k.op(PE, lambda e, ps=ps, hh=hh, kk=kk, e0=e0: e.matmul(ps[0:64, :], lhsT=Wt[:, kk, 1536 + hh * 64:1536 + (hh + 1) * 64], rhs=hT[:, kk, e0:e0 + 512], start=(kk == 0), stop=(kk == 7)), r=[uWt] + hr, w=[ups])
                        k.op(A, lambda e, ps=ps, hh=hh: e.activation(out=gT[:, hh, :], in_=ps[0:64, :], func=AF.Silu), r=[ups], w=[ugT])
                    def t2_front(idx, nn, g):
                        n = tg * 4 + nn
                        t = n + 1
                        ptc, uptc = pt2[idx % 2], upt2[idx % 2]
                        for mi, m in enumerate((t - 1, t, t + 1)):
                            k.op(PE, lambda e, mi=mi, m=m: e.matmul(PF[2 + mi][:], lhsT=kT[:, g, m * 128:(m + 1) * 128], rhs=qTg[:, 4 * g:4 * g + 4, nn * 128:(nn + 1) * 128], start=True, stop=True), r=[ukT[m], uqT[nn]], w=[uPF[2 + mi]])
                            k.op(A, lambda e, mi=mi: e.activation(out=ptc[mi][:], in_=PF[2 + mi][:], func=AF.Exp, bias=negc[:, 0:1], scale=0.125), r=[uPF[2 + mi], uL], w=[uptc[mi]])
                        mL = 256 if t == 1 else 0
                        mR = 384 if t == 16 else 128
                        k.op(V, lambda e: e.tensor_tensor(out=ptc[0][:].rearrange("p (a i) -> p a i", a=4), in0=ptc[0][:].rearrange("p (a i) -> p a i", a=4), in1=maskb[:, mL:mL + 128].unsqueeze(1).broadcast_to([128, 4, 128]), op=ALU.mult), r=[uptc[0], uC], w=[uptc[0]])
                        k.op(G, lambda e: e.tensor_tensor(out=ptc[2][:].rearrange("p (a i) -> p a i", a=4), in0=ptc[2][:].rearrange("p (a i) -> p a i", a=4), in1=maskb[:, mR:mR + 128].unsqueeze(1).broadcast_to([128, 4, 128]), op=ALU.mult), r=[uptc[2], uC], w=[uptc[2]])

                    def t2_back(idx, nn, g):
                        n = tg * 4 + nn
                        t = n + 1
                        ptc, uptc = pt2[idx % 2], upt2[idx % 2]
                        onc, uonc = on2[idx % 2], uon2[idx % 2]
                        for mi, m in enumerate((t - 1, t, t + 1)):
                            k.op(PE, lambda e, mi=mi, m=m: e.matmul(PF[5][0:64, :], lhsT=Vr[:, m, g * 64:(g + 1) * 64], rhs=ptc[mi][:], start=(mi == 0), stop=(mi == 2)), r=[uVr[m], uptc[mi]], w=[uPF[5]])
                        for mi, m in enumerate((t - 1, t, t + 1)):
                            k.op(PE, lambda e, mi=mi: e.matmul(PF[6][0:64, :], lhsT=onesb[:, 0:64], rhs=ptc[mi][:], start=(mi == 0), stop=(mi == 2)), r=[uC, uptc[mi]], w=[uPF[6]])
                        for ei in range(4):
                            hd = 4 * g + ei
                            k.op(A, lambda e, ei=ei, hd=hd: e.activation(out=den[:, ei * 128:(ei + 1) * 128], in_=PF[6][0:64, ei * 128:(ei + 1) * 128], func=AF.Ln, bias=snke[0:64, hd:hd + 1], scale=1.0), r=[uPF[6], uL], w=[uden])
                        k.op(A, lambda e: e.activation(out=rec[:], in_=den[:], func=AF.Exp, scale=-1.0), r=[uden], w=[urec])
                        k.op(V, lambda e: e.tensor_tensor(out=onc[:], in0=PF[5][0:64, :], in1=rec[:], op=ALU.mult), r=[uPF[5], urec], w=[uonc])
                        k.op(G, lambda e: e.tensor_tensor(out=oaT[:, 4 * g:4 * g + 4, nn * 128:(nn + 1) * 128], in0=onc[:].rearrange("p (a i) -> p a i", a=4), in1=gT[:, 4 * g:4 * g + 4, nn * 128:(nn + 1) * 128], op=ALU.mult), r=[uonc, ugT], w=[uoaT])
                    its = [(nn, g) for nn in range(4) for g in range(4)]
                    for idx, (nn, g) in enumerate(its):
                        t2_front(idx, nn, g)
                        if idx > 0:
                            t2_back(idx - 1, *its[idx - 1])
                    t2_back(len(its) - 1, *its[-1])
                    k.dma(S, OA[:, :, tg * 512:(tg + 1) * 512].rearrange("h d t -> d h t"), oaT[:], r=[uoaT], w=[u_OA])
                k.barrier()

        def phase_R(l):
            with ExitStack() as st:
                Wr = sb(st, "Wr", [128, 8, 2048], BF16); uWr = U()
                qT = sb(st, "rqT", [128, 2, T], BF16); uq = [U() for _ in range(4)]
                kT = sb(st, "rkT", [128, 2, T], BF16); ukk = [U() for _ in range(4)]
                Vv = sb(st, "rV", [128, 16, 512], BF16); uV = [U() for _ in range(16)]
                kt = sb(st, "rkt", [128, 16, 256], BF16); ukt = [U() for _ in range(16)]
                Pp = sb(st, "rP", [128, 16, 512], F32); uP = [U() for _ in range(16)]
                cs = sb(st, "rcs", [128, 2, 512], F32); ucs = U()
                tq = [sb(st, f"rtq{i}", [128, 512], F32) for i in range(4)]; utq = [U() for _ in range(4)]
                Sf = sb(st, "Sf", [128, 2, 512], F32); Sb = sb(st, "Sb", [128, 2, 512], F32); uSf = U(); uSb = U()
                Sfb = sb(st, "Sfb", [128, 2, 512], BF16); Sbb = sb(st, "Sbb", [128, 2, 512], BF16); uSfb = U(); uSbb = U()
                fsum = sb(st, "fsum", [128, 4, 512], F32); ufs = U()
                DTm = sb(st, "DTm", [128, 128], F32); dq = sb(st, "dq", [128, 256], F32); dsg = sb(st, "dsg", [128, 32], F32)
                dk = sb(st, "dk", [128, 2], F32); gC = sb(st, "gC", [128, 2], F32); wr = sb(st, "wr", [128, 8], F32)
                tmpD = sb(st, "tmpD", [128, 256], F32); uH = U()
                kfs = sb(st, "kfs", [128, 256], BF16); kbs = sb(st, "kbs", [128, 256], BF16); ukfs = U(); ukbs = U()
                qd = sb(st, "qd", [128, 2, 128], BF16); uqd = U()
                kdc = sb(st, "kdc", [128, 256], BF16); ukdc = U()
                sd = sb(st, "sd", [128, 128], BF16); usd = U()
                oo = sb(st, "oo", [128, 512], F32); uoo = U()
                onn = sb(st, "onn", [128, 512], F32); uonn = U()
                sgg = sb(st, "sgg", [128, 512], F32); usgg = U()
                og = sb(st, "og", [128, 512], BF16); uog = U()
                og_b = sb(st, "og_b", [128, 512], BF16); uog_b = U()
                bst = sb(st, "bst", [128, 6], F32); mv = sb(st, "mv", [128, 2], F32); ubn = U()
                orT = sb(st, "orT", [128, 4, 512], BF16); uorT = U()
                RSTOP = int(os.environ.get("RSTOP", "0"))

                class _Stop(Exception):
                    pass

                def ck(n_):
                    if RSTOP == n_:
                        raise _Stop()
                try:
                  for h in range(4):
                      wl = [(1024, OFF_RV + h * 512), (1536, OFF_RG + h * 512)]
                      if h % 2 == 0:
                          wl = [(0, OFF_RQ + h * 256), (512, OFF_RK + h * 256)] + wl
                      for (dst, off) in wl:
                          k.dma(G, Wr[:, :, dst:dst + 512], w_in[l, :, off:off + 512].rearrange("(k p) c -> p k c", p=128), w=[uWr])
                      qc0 = (h % 2) * 256
                      kc0 = 512 + (h % 2) * 256
                      lgf = lg[:, h:h + 1]
                      lgb = lg[:, 4 + h:5 + h]
                      hc = dict(r=[uL, uC, uH], w=[uH])
                      k.op(A, lambda e: e.activation(out=dsg[:, 0:16], in_=eseg[:, 0:16], func=AF.Exp, scale=lgf), **hc)
                      k.op(A, lambda e: e.activation(out=dsg[:, 16:32], in_=eseg[:, 16:32], func=AF.Exp, scale=lgb), **hc)
                      k.op(A, lambda e: e.activation(out=dk[:, 0:1], in_=ek[:, 0:1], func=AF.Exp, scale=lgf), **hc)
                      k.op(A, lambda e: e.activation(out=dk[:, 1:2], in_=ek[:, 1:2], func=AF.Exp, scale=lgb), **hc)
                      k.op(A, lambda e: e.activation(out=dq[:, 0:128], in_=eq[:, 0:128], func=AF.Exp, scale=lgf), **hc)
                      k.op(A, lambda e: e.activation(out=dq[:, 128:256], in_=eq[:, 128:256], func=AF.Exp, scale=lgb), **hc)
                      k.op(V, lambda e: e.tensor_scalar_mul(out=dq[:], in0=dq[:], scalar1=1.0 / 16.0), **hc)
                      k.op(A, lambda e: e.activation(out=gC[:, 0:1], in_=lgf, func=AF.Exp, scale=128.0), **hc)
                      k.op(A, lambda e: e.activation(out=gC[:, 1:2], in_=lgb, func=AF.Exp, scale=128.0), **hc)
                      k.op(A, lambda e: e.activation(out=wr[:, 0:4], in_=dist[:, 0:4], func=AF.Exp, scale=lgf), **hc)
                      k.op(A, lambda e: e.activation(out=wr[:, 4:8], in_=dist[:, 4:8], func=AF.Exp, scale=lgb), **hc)
                      k.op(A, lambda e: e.activation(out=tmpD[:, 0:128], in_=em[:, 0:128], func=AF.Exp, scale=lgf), **hc)
                      k.op(A, lambda e: e.activation(out=tmpD[:, 128:256], in_=em[:, 128:256], func=AF.Exp, scale=lgb), **hc)
                      k.op(V, lambda e: e.tensor_tensor(out=tmpD[:], in0=tmpD[:], in1=em[:, 256:512], op=ALU.mult), **hc)
                      k.op(V, lambda e: e.tensor_tensor(out=DTm[:], in0=tmpD[:, 0:128], in1=tmpD[:, 128:256], op=ALU.add), **hc)
                      k.op(V, lambda e: e.tensor_scalar_mul(out=DTm[:], in0=DTm[:], scalar1=1.0 / 16.0), **hc)
                      ck(1)
                      for tg in range(4):
                          e0 = 128 + tg * 512
                          hr = uhT[e0 // 128:(e0 + 512) // 128]
                          k.dma(S, cs[:, 0, :], c_cosr[:, tg * 512:(tg + 1) * 512], w=[ucs])
                          k.dma(S, cs[:, 1, :], c_sinr[:, tg * 512:(tg + 1) * 512], w=[ucs])
                          for (col0, dstT, ud) in ((qc0, qT, uq[tg]), (kc0, kT, ukk[tg])):
                              for dc in range(2):
                                  for kk in range(8):
                                      k.op(PE, lambda e, dc=dc, kk=kk, col0=col0, e0=e0: e.matmul(PF[dc][:], lhsT=Wr[:, kk, col0 + dc * 128:col0 + (dc + 1) * 128], rhs=hT[:, kk, e0:e0 + 512], start=(kk == 0), stop=(kk == 7)), r=[uWr] + hr, w=[uPF[dc]])
                              k.op(V, lambda e: e.tensor_tensor(out=tq[0][:], in0=PF[0][:], in1=cs[:, 0, :], op=ALU.mult), r=[uPF[0], ucs], w=[utq[0]])
                              k.op(V, lambda e: e.tensor_tensor(out=tq[1][:], in0=PF[1][:], in1=cs[:, 1, :], op=ALU.mult), r=[uPF[1], ucs], w=[utq[1]])
                              k.op(V, lambda e: e.tensor_tensor(out=tq[2][:], in0=PF[1][:], in1=cs[:, 0, :], op=ALU.mult), r=[uPF[1], ucs], w=[utq[2]])
                              k.op(V, lambda e: e.tensor_tensor(out=tq[3][:], in0=PF[0][:], in1=cs[:, 1, :], op=ALU.mult), r=[uPF[0], ucs], w=[utq[3]])
                              k.op(V, lambda e, dstT=dstT, tg=tg: e.tensor_tensor(out=dstT[:, 0, tg * 512:(tg + 1) * 512], in0=tq[0][:], in1=tq[1][:], op=ALU.subtract), r=[utq[0], utq[1]], w=[ud])
                              k.op(G, lambda e, dstT=dstT, tg=tg: e.tensor_tensor(out=dstT[:, 1, tg * 512:(tg + 1) * 512], in0=tq[2][:], in1=tq[3][:], op=ALU.add), r=[utq[2], utq[3]], w=[ud])
                          ck(2)
                          for nn in range(4):
                              n = tg * 4 + nn
                              t = n + 1
                              for kk in range(8):
                                  k.op(PE, lambda e, kk=kk, t=t: e.matmul(PF[2][:], lhsT=hT[:, kk, t * 128:(t + 1) * 128], rhs=Wr[:, kk, 1024:1536], start=(kk == 0), stop=(kk == 7)), r=[uWr, uhT[t]], w=[uPF[2]])
                              k.op(A, lambda e, n=n: e.activation(out=Vv[:, n, :], in_=PF[2][:], func=AF.Copy), r=[uPF[2]], w=[uV[n]])
                              ck(3)
                              for dc in range(2):
                                  k.op(PE, lambda e, dc=dc, n=n: e.transpose(out=PT[:, dc, :], in_=kT[:, dc, n * 128:(n + 1) * 128], identity=identb[:]), r=[ukk[tg], uC], w=[uPT])
                              ptv = PT[:, 0:2, :]
                              k.op(A, lambda e, n=n, ptv=ptv: e.activation(out=kfs[:].rearrange("p (a d) -> p a d", a=2), in_=ptv, func=AF.Copy, scale=dsg[:, n:n + 1]), r=[uPT, uH], w=[ukfs])
                              k.op(A, lambda e, n=n, ptv=ptv: e.activation(out=kbs[:].rearrange("p (a d) -> p a d", a=2), in_=ptv, func=AF.Copy, scale=dsg[:, 16 + n:17 + n]), r=[uPT, uH], w=[ukbs])
                              k.op(A, lambda e, n=n, ptv=ptv: e.activation(out=kt[:, n, :].rearrange("p (a d) -> p a d", a=2), in_=ptv, func=AF.Copy), r=[uPT], w=[ukt[n]])
                              ck(4)
                              for dc in range(2):
                                  k.op(PE, lambda e, dc=dc, n=n: e.matmul(PF[3 + dc][:], lhsT=kfs[:, dc * 128:(dc + 1) * 128], rhs=Vv[:, n, :], start=(n == 0), stop=(n == 15)), r=[ukfs, uV[n]], w=[uPF[3 + dc]])
                                  k.op(PE, lambda e, dc=dc, n=n: e.matmul(PF[5 + dc][:], lhsT=kbs[:, dc * 128:(dc + 1) * 128], rhs=Vv[:, n, :], start=(n == 0), stop=(n == 15)), r=[ukbs, uV[n]], w=[uPF[5 + dc]])
                      if dbg == 'R0':
                          k.barrier()
                          return
                      for j in range(4):
                          k.op(A, lambda e, j=j: e.activation(out=fsum[:, j, :], in_=PF[3 + j][:], func=AF.Copy), r=[uPF[3 + j]], w=[ufs])
                      k.dma(S, fs_bounce.ap().rearrange("(j p) v -> p j v", p=128), fsum[:], r=[ufs], w=[u_fsb])
                      ccs = k.new_sem("cc")
                      k.custom(G, lambda e: e.collective_compute("AllGather", ALU.bypass, replica_groups=RG, ins=[fs_bounce.ap().opt()], outs=[fs_gath.ap().opt()]), ccs, 1, r=[u_fsb], w=[u_fsg])
                      for r_ in range(4):
                          k.dma(S, fsum[:], fs_gath.ap()[r_ * 512:(r_ + 1) * 512, :].rearrange("(j p) v -> p j v", p=128), r=[u_fsg], w=[ufs])
                          for dirn, (Sx, uSx) in enumerate(((Sf, uSf), (Sb, uSb))):
                              for dc in range(2):
                                  wcol = wr[:, dirn * 4 + r_:dirn * 4 + r_ + 1]
                                  if r_ == 0:
                                      k.op(V, lambda e, Sx=Sx, dc=dc, dirn=dirn, wcol=wcol: e.tensor_scalar_mul(out=Sx[:, dc, :], in0=fsum[:, dirn * 2 + dc, :], scalar1=wcol), r=[ufs, uH], w=[uSx])
                                  else:
                                      k.op(V, lambda e, Sx=Sx, dc=dc, dirn=dirn, wcol=wcol: e.scalar_tensor_tensor(out=Sx[:, dc, :], in0=fsum[:, dirn * 2 + dc, :], scalar=wcol, in1=Sx[:, dc, :], op0=ALU.mult, op1=ALU.add), r=[ufs, uH, uSx], w=[uSx])
                      k.op(A, lambda e: e.activation(out=Sfb[:], in_=Sf[:], func=AF.Copy), r=[uSf], w=[uSfb])
                      k.op(A, lambda e: e.activation(out=Sbb[:], in_=Sb[:], func=AF.Copy), r=[uSb], w=[uSbb])
                      if dbg == 'RAG':
                          k.barrier()
                          return
                      for n in range(15, -1, -1):
                          tg = n // 4
                          k.op(A, lambda e, n=n: e.activation(out=kdc[:], in_=kt[:, n, :], func=AF.Copy, scale=dk[:, 1:2]), r=[ukt[n], uH], w=[ukdc])
                          for dc in range(2):
                              k.op(PE, lambda e, dc=dc, n=n: e.matmul(PF[1 + dc][:], lhsT=kdc[:, dc * 128:(dc + 1) * 128], rhs=Vv[:, n, :], start=True, stop=True), r=[ukdc, uV[n]], w=[uPF[1 + dc]])
                          k.op(V, lambda e, n=n: e.tensor_tensor(out=qd[:], in0=qT[:, :, n * 128:(n + 1) * 128], in1=dq[:, 128:256].unsqueeze(1).broadcast_to([128, 2, 128]), op=ALU.mult), r=[uq[tg], uH], w=[uqd])
                          for dc in range(2):
                              k.op(PE, lambda e, dc=dc: e.matmul(PF[0][:], lhsT=qd[:, dc, :], rhs=Sbb[:, dc, :], start=(dc == 0), stop=(dc == 1)), r=[uqd, uSbb], w=[uPF[0]])
                          for dc in range(2):
                              k.op(V, lambda e, dc=dc: e.scalar_tensor_tensor(out=Sb[:, dc, :], in0=Sb[:, dc, :], scalar=gC[:, 1:2], in1=PF[1 + dc][:], op0=ALU.mult, op1=ALU.add), r=[uPF[1 + dc], uH, uSb], w=[uSb])
                          k.op(A, lambda e: e.activation(out=Sbb[:], in_=Sb[:], func=AF.Copy), r=[uSb], w=[uSbb])
                          k.op(A, lambda e, n=n: e.activation(out=Pp[:, n, :], in_=PF[0][:], func=AF.Copy), r=[uPF[0]], w=[uP[n]])
                      Sfb2 = [Sfb, Sbb]
                      uSfb2 = [uSfb, uSbb]
                      og2 = [og, og_b]
                      uog2 = [uog, uog_b]

                      def finish(n):
                          tg = n // 4
                          for vc in range(4):
                              k.op(PE, lambda e, vc=vc, n=n: e.transpose(out=PT[:, vc, :], in_=og2[n % 2][:, vc * 128:(vc + 1) * 128], identity=identb[:]), r=[uog2[n % 2], uC], w=[uPT])
                          k.op(A, lambda e, n=n: e.activation(out=orT[:, :, (n % 4) * 128:(n % 4 + 1) * 128], in_=PT[:, 0:4, :], func=AF.Copy), r=[uPT], w=[uorT])
                          if n % 4 == 3:
                              k.dma(S, OR[h * 4:(h + 1) * 4, :, tg * 512:(tg + 1) * 512].rearrange("c p t -> p c t"), orT[:], r=[uorT], w=[u_OR])
                      for n in range(16):
                          tg = n // 4
                          t = n + 1
                          cur, nxt = n % 2, (n + 1) % 2
                          k.op(A, lambda e, n=n: e.activation(out=kdc[:], in_=kt[:, n, :], func=AF.Copy, scale=dk[:, 0:1]), r=[ukt[n], uH], w=[ukdc])
                          for dc in range(2):
                              k.op(PE, lambda e, dc=dc, n=n: e.matmul(PF[1 + dc][:], lhsT=kdc[:, dc * 128:(dc + 1) * 128], rhs=Vv[:, n, :], start=True, stop=True), r=[ukdc, uV[n]], w=[uPF[1 + dc]])
                          for kk in range(8):
                              k.op(PE, lambda e, kk=kk, t=t: e.matmul(PF[5][:], lhsT=hT[:, kk, t * 128:(t + 1) * 128], rhs=Wr[:, kk, 1536:2048], start=(kk == 0), stop=(kk == 7)), r=[uWr, uhT[t]], w=[uPF[5]])
                          k.op(A, lambda e: e.activation(out=sgg[:], in_=PF[5][:], func=AF.Silu), r=[uPF[5]], w=[usgg])
                          for dc in range(2):
                              k.op(PE, lambda e, dc=dc, n=n: e.matmul(PF[3][:, 0:128], lhsT=kT[:, dc, n * 128:(n + 1) * 128], rhs=qT[:, dc, n * 128:(n + 1) * 128], start=(dc == 0), stop=(dc == 1)), r=[ukk[tg], uq[tg]], w=[uPF[3]])
                          k.op(V, lambda e: e.tensor_tensor(out=sd[:], in0=PF[3][:, 0:128], in1=DTm[:], op=ALU.mult), r=[uPF[3], uH], w=[usd])
                          k.op(G, lambda e, n=n: e.tensor_tensor(out=qd[:], in0=qT[:, :, n * 128:(n + 1) * 128], in1=dq[:, 0:128].unsqueeze(1).broadcast_to([128, 2, 128]), op=ALU.mult), r=[uq[tg], uH], w=[uqd])
                          k.op(PE, lambda e, n=n: e.matmul(PF[4][:], lhsT=sd[:], rhs=Vv[:, n, :], start=True, stop=False), r=[usd, uV[n]], w=[uPF[4]])
                          for dc in range(2):
                              k.op(PE, lambda e, dc=dc, cur=cur: e.matmul(PF[4][:], lhsT=qd[:, dc, :], rhs=Sfb2[cur][:, dc, :], start=False, stop=(dc == 1)), r=[uqd, uSfb2[cur]], w=[uPF[4]])
                          if n > 0:
                              finish(n - 1)
                          for dc in range(2):
                              k.op(V, lambda e, dc=dc: e.scalar_tensor_tensor(out=Sf[:, dc, :], in0=Sf[:, dc, :], scalar=gC[:, 0:1], in1=PF[1 + dc][:], op0=ALU.mult, op1=ALU.add), r=[uPF[1 + dc], uH, uSf], w=[uSf])
                          k.op(A, lambda e, nxt=nxt: e.activation(out=Sfb2[nxt][:], in_=Sf[:], func=AF.Copy), r=[uSf], w=[uSfb2[nxt]])
                          k.op(V, lambda e, n=n: e.tensor_tensor(out=oo[:], in0=PF[4][:], in1=Pp[:, n, :], op=ALU.add), r=[uPF[4], uP[n]], w=[uoo])
                          k.op(V, lambda e: e.bn_stats(out=bst[:], in_=oo[:]), r=[uoo], w=[ubn])
                          k.op(V, lambda e: e.bn_aggr(out=mv[:], in_=bst[:]), r=[ubn], w=[ubn])
                          k.op(V, lambda e: e.tensor_scalar_add(out=mv[:, 1:2], in0=mv[:, 1:2], scalar1=EPS), r=[ubn], w=[ubn])
                          k.op(A, lambda e: e.activation(out=mv[:, 1:2], in_=mv[:, 1:2], func=AF.Sqrt), r=[ubn], w=[ubn])
                          k.op(V, lambda e: e.reciprocal(out=mv[:, 1:2], in_=mv[:, 1:2]), r=[ubn], w=[ubn])
                          k.op(V, lambda e: e.tensor_scalar(out=onn[:], in0=oo[:], scalar1=mv[:, 0:1], scalar2=mv[:, 1:2], op0=ALU.subtract, op1=ALU.mult), r=[uoo, ubn], w=[uonn])
                          k.op(G, lambda e, cur=cur: e.tensor_tensor(out=og2[cur][:], in0=onn[:], in1=sgg[:], op=ALU.mult), r=[uonn, usgg], w=[uog2[cur]])
                      finish(15)
                except _Stop:
                    pass
                k.barrier()

        def phase_F(l, xsrc, last):
            with ExitStack() as st:
                mg = sb(st, "mg", [128, 8, T], F32); umg = [U() for _ in range(4)]
                sgm = sb(st, "sgm", [128, 512], F32); usgm = U()
                tmpm = sb(st, "tmpm", [128, 512], F32); utmpm = U()
                for bi, (nk, kp) in enumerate(((8, 128), (16, 128), (16, 64))):
                    with ExitStack() as st2:
                        Wb = sb(st2, f"Wb{bi}", [kp, nk, D], BF16); uWb = U()
                        Wg = sb(st2, f"Wg{bi}", [128, 8, D], BF16); uWg = U()
                        ob = sb(st2, f"ob{bi}", [kp, nk, 512], BF16); uob = U()
                        src_w = (w_conv_out, w_ret_out, w_attn_out)[bi]
                        if bi == 2:
                            view = src_w[l].rearrange("(h d) c -> d h c", d=64)
                        else:
                            view = src_w[l].rearrange("(k p) c -> p k c", p=128)
                        for j in range(0, nk, 4):
                            k.dma(G, Wb[:, j:j + 4, :], view[:, j:j + 4, :], w=[uWb])
                        for j in range(2):
                            c0 = OFF_GL + bi * 1024 + j * 512
                            k.dma(G, Wg[:, :, j * 512:(j + 1) * 512], w_in[l, :, c0:c0 + 512].rearrange("(k p) c -> p k c", p=128), w=[uWg])
                        osrc = (OC, OR, OA)[bi]
                        uos = (u_OC, u_OR, u_OA)[bi]
                        for tg in range(4):
                            e0 = 128 + tg * 512
                            hr = uhT[e0 // 128:(e0 + 512) // 128]
                            k.dma(S, ob[:], osrc[:, :, tg * 512:(tg + 1) * 512].rearrange("c p t -> p c t"), r=[uos], w=[uob])
                            for m in range(8):
                                py, upy = PF[m % 3], uPF[m % 3]
                                pg, upg = PF[3 + m % 3], uPF[3 + m % 3]
                                for j in range(nk):
                                    k.op(PE, lambda e, py=py, j=j, m=m: e.matmul(py[:], lhsT=Wb[:, j, m * 128:(m + 1) * 128], rhs=ob[:, j, :], start=(j == 0), stop=(j == nk - 1)), r=[uWb, uob], w=[upy])
                                for kk in range(8):
                                    k.op(PE, lambda e, pg=pg, kk=kk, m=m, e0=e0: e.matmul(pg[:], lhsT=Wg[:, kk, m * 128:(m + 1) * 128], rhs=hT[:, kk, e0:e0 + 512], start=(kk == 0), stop=(kk == 7)), r=[uWg] + hr, w=[upg])
                                k.op(A, lambda e, pg=pg, m=m, bi=bi: e.activation(out=sgm[:], in_=pg[:], func=AF.Sigmoid, bias=prT[:, m, 34 + bi:35 + bi], scale=1.0), r=[upg, uL], w=[usgm])
                                if bi == 0:
                                    k.op(V, lambda e, py=py, m=m, tg=tg: e.tensor_tensor(out=mg[:, m, tg * 512:(tg + 1) * 512], in0=py[:], in1=sgm[:], op=ALU.mult), r=[upy, usgm], w=[umg[tg]])
                                else:
                                    k.op(V, lambda e, py=py: e.tensor_tensor(out=tmpm[:], in0=py[:], in1=sgm[:], op=ALU.mult), r=[upy, usgm], w=[utmpm])
                                    k.op(G, lambda e, m=m, tg=tg: e.tensor_tensor(out=mg[:, m, tg * 512:(tg + 1) * 512], in0=mg[:, m, tg * 512:(tg + 1) * 512], in1=tmpm[:], op=ALU.add), r=[utmpm, umg[tg]], w=[umg[tg]])
                        k.barrier()
                with ExitStack() as st2:
                    Wo = sb(st2, "Wo", [128, 8, D], BF16); uWo = U()
                    for j in range(2):
                        k.dma(G, Wo[:, j * 4:(j + 1) * 4, :], w_out[l].rearrange("(k p) c -> p k c", p=128)[:, j * 4:(j + 1) * 4, :], w=[uWo])
                    mb = sb(st2, "mb", [128, 8, 128], BF16); umb = U()
                    xt = [sb(st2, f"fxt{i}", [128, D], F32) for i in range(2)]; uxt = [U(), U()]
                    xn = [sb(st2, f"fxn{i}", [128, D], F32) for i in range(2)]; uxn = [U(), U()]
                    for n in range(16):
                        b = n % 2
                        k.op(A, lambda e, n=n: e.activation(out=mb[:], in_=mg[:, :, n * 128:(n + 1) * 128], func=AF.Copy), r=[umg[n // 4]], w=[umb])
                        k.dma(S, xt[b][:], xsrc[128 + n * 128:128 + (n + 1) * 128, :], r=[u_x1e], w=[uxt[b]])
                        for half in range(2):
                            for kk in range(8):
                                k.op(PE, lambda e, half=half, kk=kk: e.matmul(PF[half][:], lhsT=mb[:, kk, :], rhs=Wo[:, kk, half * 512:(half + 1) * 512], start=(kk == 0), stop=(kk == 7)), r=[umb, uWo], w=[uPF[half]])
                            k.op(V, lambda e, half=half, b=b: e.tensor_tensor(out=xn[b][:, half * 512:(half + 1) * 512], in0=PF[half][:], in1=xt[b][:, half * 512:(half + 1) * 512], op=ALU.add), r=[uPF[half], uxt[b]], w=[uxn[b]])
                        if last:
                            k.dma(S, y_out[n * 128:(n + 1) * 128, :], xn[b][:], r=[uxn[b]], w=[u_yout])
                        else:
                            k.dma(S, x1e[128 + n * 128:128 + (n + 1) * 128, :], xn[b][:], r=[uxn[b]], w=[u_x1e])
                            if n == 0:
                                k.dma(S, h_bounce.ap()[0:128, :], xn[b][:], r=[uxn[b]], w=[u_hb])
                            if n == 15:
                                k.dma(S, h_bounce.ap()[128:256, :], xn[b][:], r=[uxn[b]], w=[u_hb])
                    k.barrier()
            if not last:
                ccs = k.new_sem("cch")
                k.custom(G, lambda e: e.collective_compute("AllGather", ALU.bypass, replica_groups=RG, ins=[h_bounce.ap().opt()], outs=[h_gath.ap().opt()]), ccs, 1, r=[u_hb], w=[u_hg])
                with ExitStack() as st2:
                    hb = sb(st2, "hb", [128, 2, D], F32); uhb = U()
                    accL = sb(st2, "accL", [128, D], F32); accR = sb(st2, "accR", [128, D], F32); uaL = U(); uaR = U()
                    for r_ in range(4):
                        k.dma(S, hb[:], h_gath.ap()[r_ * 256:(r_ + 1) * 256, :].rearrange("(a p) f -> p a f", p=128), r=[u_hg], w=[uhb])
                        for (acc, ua, a_, sc) in ((accL, uaL, 1, r_), (accR, uaR, 0, 4 + r_)):
                            if r_ == 0:
                                k.op(V, lambda e, acc=acc, a_=a_, sc=sc: e.tensor_scalar_mul(out=acc[:], in0=hb[:, a_, :], scalar1=sel[:, sc:sc + 1]), r=[uhb, uC], w=[ua])
                            else:
                                k.op(V, lambda e, acc=acc, a_=a_, sc=sc: e.scalar_tensor_tensor(out=acc[:], in0=hb[:, a_, :], scalar=sel[:, sc:sc + 1], in1=acc[:], op0=ALU.mult, op1=ALU.add), r=[uhb, uC, ua], w=[ua])
                    k.dma(S, x1e[0:128, :], accL[:], r=[uaL], w=[u_x1e])
                    k.dma(S, x1e[TE - 128:TE, :], accR[:], r=[uaR], w=[u_x1e])
                    k.barrier()

        for l in range(NL):
            xsrc = x_ext if l == 0 else x1e
            last = (l == NL - 1)
            k.dma(S, lg[:], ret_decay[l:l + 1, :].partition_broadcast(128), w=[uL])
            k.dma(S, qg[:], q_norm_g[l:l + 1, :].partition_broadcast(128), w=[uL])
            k.dma(S, kg[:], k_norm_g[l:l + 1, :].partition_broadcast(128), w=[uL])
            k.dma(S, snk[:], attn_sink[l:l + 1, :].partition_broadcast(128), w=[uL])
            k.dma(S, prm[0:31, :], conv_dw[l], w=[uL])
            k.dma(S, prm[31:32, :], conv_b[l], w=[uL])
            k.dma(S, prm[32:33, :], conv_ln_g[l], w=[uL])
            k.dma(S, prm[33:34, :], conv_ln_b[l], w=[uL])
            k.dma(S, prm[34:37, :], b_gate[l], w=[uL])
            k.op(A, lambda e: e.activation(out=lg[:], in_=lg[:], func=AF.Exp), r=[uL], w=[uL])
            k.op(V, lambda e: e.tensor_scalar_mul(out=lg[:], in0=lg[:], scalar1=-1.0), r=[uL], w=[uL])
            k.op(V, lambda e: e.tensor_tensor(out=g2[:, 0:64], in0=qg[:], in1=qg[:], op=ALU.mult), r=[uL], w=[uL])
            k.op(V, lambda e: e.tensor_tensor(out=g2[:, 64:128], in0=kg[:], in1=kg[:], op=ALU.mult), r=[uL], w=[uL])
            k.op(V, lambda e: e.tensor_reduce(out=tmpc[:, 0:1], in_=g2[:, 0:64], axis=AX.X, op=ALU.max), r=[uL], w=[uL])
            k.op(V, lambda e: e.tensor_reduce(out=tmpc[:, 1:2], in_=g2[:, 64:128], axis=AX.X, op=ALU.max), r=[uL], w=[uL])
            k.op(V, lambda e: e.tensor_tensor(out=tmpc[:, 2:3], in0=tmpc[:, 0:1], in1=tmpc[:, 1:2], op=ALU.mult), r=[uL], w=[uL])
            k.op(A, lambda e: e.activation(out=tmpc[:, 3:4], in_=tmpc[:, 2:3], func=AF.Sqrt), r=[uL], w=[uL])
            k.op(V, lambda e: e.tensor_scalar_mul(out=negc[:], in0=tmpc[:, 3:4], scalar1=-8.0), r=[uL], w=[uL])
            k.op(A, lambda e: e.activation(out=snke[:], in_=snk[:], func=AF.Exp, bias=negc[:, 0:1], scale=1.0), r=[uL], w=[uL])
            for c in range(8):
                k.op(PE, lambda e, c=c: e.transpose(out=PF[0][:, c * 37:(c + 1) * 37], in_=prm[:, c * 128:(c + 1) * 128], identity=ident[0:37, 0:37]), r=[uL, uC], w=[uPF[0]])
            k.op(V, lambda e: e.tensor_copy(out=prT[:].rearrange("p c j -> p (c j)"), in_=PF[0][:, 0:296]), r=[uPF[0]], w=[uL])
            k.barrier()

            with ExitStack() as st:
                g_bc = sb(st, "g_bc", [128, D], F32); ug = U()
                k.dma(S, g_bc[:], norm_g[l:l + 1, :].partition_broadcast(128), w=[ug])
                xt = [sb(st, f"xt{i}", [128, D], F32) for i in range(2)]; uxt = [U(), U()]
                junk = [sb(st, f"junk{i}", [128, D], BF16) for i in range(2)]; ujunk = [U(), U()]
                xs = [sb(st, f"xs{i}", [128, D], BF16) for i in range(2)]; uxs = [U(), U()]
                ssq = [sb(st, f"ssq{i}", [128, 2], F32) for i in range(2)]; ussq = [U(), U()]
                def a_s1(t):
                    b = t % 2
                    k.dma(S, xt[b][:], xsrc[t * 128:(t + 1) * 128, :], r=[u_x1e], w=[uxt[b]])
                    k.op(V, lambda e, b=b: e.memset(ssq[b][:], 0.0), w=[ussq[b]])
                    k.op(A, lambda e, b=b: e.activation(out=junk[b][:], in_=xt[b][:], func=AF.Square, accum_out=ssq[b][:, 0:1]), r=[uxt[b]], w=[ujunk[b], ussq[b]])
                    k.op(V, lambda e, b=b: e.tensor_scalar(out=ssq[b][:, 1:2], in0=ssq[b][:, 0:1], scalar1=1.0 / D, scalar2=EPS, op0=ALU.mult, op1=ALU.add), r=[ussq[b]], w=[ussq[b]])
                    k.op(A, lambda e, b=b: e.activation(out=ssq[b][:, 1:2], in_=ssq[b][:, 1:2], func=AF.Sqrt), r=[ussq[b]], w=[ussq[b]])
                    k.op(V, lambda e, b=b: e.reciprocal(out=ssq[b][:, 1:2], in_=ssq[b][:, 1:2]), r=[ussq[b]], w=[ussq[b]])
                    k.op(V, lambda e, b=b: e.scalar_tensor_tensor(out=xs[b][:], in0=xt[b][:], scalar=ssq[b][:, 1:2], in1=g_bc[:], op0=ALU.mult, op1=ALU.mult), r=[uxt[b], ussq[b], ug], w=[uxs[b]])

                def a_s2(t):
                    b = t % 2
                    for c in range(8):
                        k.op(PE, lambda e, b=b, c=c: e.transpose(out=PT[:, c, :], in_=xs[b][:, c * 128:(c + 1) * 128], identity=identb[:]), r=[uxs[b], uC], w=[uPT])
                    k.op(A, lambda e, t=t: e.activation(out=hT[:, :, t * 128:(t + 1) * 128], in_=PT[:], func=AF.Copy), r=[uPT], w=[uhT[t]])
                a_s1(0)
                for t in range(18):
                    if t < 17:
                        a_s1(t + 1)
                    a_s2(t)
                k.barrier()

            if 'C' in phases:
                with ExitStack() as st:
                    Wc = sb(st, "Wc", [128, 8, 3072], BF16); uWc = U()
                    for j in range(6):
                        k.dma(G, Wc[:, :, j * 512:(j + 1) * 512], w_in[l, :, j * 512:(j + 1) * 512].rearrange("(k p) c -> p k c", p=128), w=[uWc])
                    sig = [sb(st, f"sig{i}", [128, 544], F32) for i in range(2)]; usig = [U(), U()]
                    vv = [sb(st, f"vv{i}", [128, 544], F32) for i in range(2)]; uvv = [U(), U()]
                    acc1 = [sb(st, f"acc1{i}", [128, 512], F32) for i in range(2)]; uacc1 = [U(), U()]
                    acc2 = [sb(st, f"acc2{i}", [128, 512], F32) for i in range(2)]; uacc2 = [U(), U()]
                    tmpk = [sb(st, f"tmpk{i}", [128, 512], F32) for i in range(4)]; utmpk = [U() for _ in range(4)]
                    yy = sb(st, "yy", [128, 8, 512], F32); uyy = [U() for _ in range(8)]
                    ysq = [sb(st, f"ysq{i}", [128, 512], F32) for i in range(2)]; uysq = [U(), U()]
                    mean = sb(st, "mean", [128, 512], F32); msq = sb(st, "msq", [128, 512], F32)
                    rstd = sb(st, "rstd", [128, 512], F32); ustat = U()
                    t1 = [sb(st, f"t1{i}", [128, 512], F32) for i in range(2)]; ut1 = [U(), U()]
                    t2 = [sb(st, f"t2{i}", [128, 512], F32) for i in range(2)]; ut2 = [U(), U()]
                    s1 = [sb(st, f"s1{i}", [128, 512], F32) for i in range(2)]; us1 = [U(), U()]
                    sg = [sb(st, f"sg{i}", [128, 512], F32) for i in range(2)]; usg = [U(), U()]
                    ocT = sb(st, "ocT", [128, 8, 512], BF16); uocT = U()
                    pTls = [PF[4], PF[5]]; uTl = [uPF[4], uPF[5]]
                    pST, pSQ, uST, uSQ = PF[6], PF[4], uPF[6], uPF[4]
                    it = 0
                    tk = 0

                    def proj(tg, c, b):
                        e0 = 128 + tg * 512
                        hr = uhT[(e0 - 16) // 128:(e0 + 528 + 127) // 128]
                        pA, uA, pB, uB = PF[b], uPF[b], PF[2 + b], uPF[2 + b]
                        tb = 0
                        pTl = pTls[b]
                        for (ps, ups, col0, to) in ((pA, uA, c * 128, tb), (pB, uB, 1024 + c * 128, tb + 32)):
                            for kk in range(8):
                                k.op(PE, lambda e, ps=ps, kk=kk, col0=col0: e.matmul(ps[:, 0:512], lhsT=Wc[:, kk, col0:col0 + 128], rhs=hT[:, kk, e0 - 16:e0 + 496], start=(kk == 0), stop=(kk == 7)), r=[uWc] + hr, w=[ups])
                            for kk in range(8):
                                k.op(PE, lambda e, kk=kk, col0=col0, to=to, pTl=pTl: e.matmul(pTl[:, to:to + 32], lhsT=Wc[:, kk, col0:col0 + 128], rhs=hT[:, kk, e0 + 496:e0 + 528], start=(kk == 0), stop=(kk == 7)), r=[uWc] + hr, w=[uTl[b]])
                    for tg in range(4):
                        e0 = 128 + tg * 512
                        hr = uhT[(e0 - 16) // 128:(e0 + 528 + 127) // 128]
                        proj(tg, 0, it % 2)
                        for c in range(8):
                            b = it % 2
                            it += 1
                            pA, uA, pB, uB = PF[b], uPF[b], PF[2 + b], uPF[2 + b]
                            tb = 0
                            pTl = pTls[b]
                            if c < 7:
                                proj(tg, c + 1, it % 2)
                            k.op(A, lambda e, b=b, pB=pB: e.activation(out=sig[b][:, 0:512], in_=pB[:, 0:512], func=AF.Sigmoid), r=[uB], w=[usig[b]])
                            k.op(A, lambda e, b=b, tb=tb, pTl=pTl: e.activation(out=sig[b][:, 512:544], in_=pTl[:, tb + 32:tb + 64], func=AF.Sigmoid), r=[uTl[b]], w=[usig[b]])
                            k.op(V, lambda e, b=b, pA=pA: e.tensor_tensor(out=vv[b][:, 0:512], in0=pA[:, 0:512], in1=sig[b][:, 0:512], op=ALU.mult), r=[uA, usig[b]], w=[uvv[b]])
                            k.op(V, lambda e, b=b, tb=tb, pTl=pTl: e.tensor_tensor(out=vv[b][:, 512:544], in0=pTl[:, tb:tb + 32], in1=sig[b][:, 512:544], op=ALU.mult), r=[uTl[b], usig[b]], w=[uvv[b]])
                            k.op(V, lambda e, c=c, b=b: e.tensor_scalar(out=acc1[b][:], in0=vv[b][:, 1:513], scalar1=prT[:, c, 0:1], scalar2=prT[:, c, 31:32], op0=ALU.mult, op1=ALU.add), r=[uvv[b], uL], w=[uacc1[b]])
                            for kt in range(1, 16):
                                k.op(V, lambda e, c=c, kt=kt, b=b: e.scalar_tensor_tensor(out=acc1[b][:], in0=vv[b][:, kt + 1:kt + 513], scalar=prT[:, c, kt:kt + 1], in1=acc1[b][:], op0=ALU.mult, op1=ALU.add), r=[uvv[b], uL, uacc1[b]], w=[uacc1[b]])
                            k.op(A, lambda e, c=c, b=b: e.activation(out=acc2[b][:], in_=vv[b][:, 17:529], func=AF.Copy, scale=prT[:, c, 16:17]), r=[uvv[b], uL], w=[uacc2[b]])
                            for kt in range(17, 31):
                                j = tk % 4
                                tk += 1
                                k.op(A, lambda e, c=c, kt=kt, j=j, b=b: e.activation(out=tmpk[j][:], in_=vv[b][:, kt + 1:kt + 513], func=AF.Copy, scale=prT[:, c, kt:kt + 1]), r=[uvv[b], uL], w=[utmpk[j]])
                                k.op(G, lambda e, j=j, b=b: e.tensor_tensor(out=acc2[b][:], in0=acc2[b][:], in1=tmpk[j][:], op=ALU.add), r=[utmpk[j], uacc2[b]], w=[uacc2[b]])
                            k.op(G, lambda e, c=c, b=b: e.tensor_tensor(out=yy[:, c, :], in0=acc1[b][:], in1=acc2[b][:], op=ALU.add), r=[uacc1[b], uacc2[b]], w=[uyy[c]])
                        for c in range(8):
                            b = c % 2
                            k.op(A, lambda e, c=c, b=b: e.activation(out=ysq[b][:], in_=yy[:, c, :], func=AF.Square), r=[uyy[c]], w=[uysq[b]])
                            k.op(PE, lambda e, c=c: e.matmul(pST[:], lhsT=onesf[:], rhs=yy[:, c, :], start=(c == 0), stop=(c == 7)), r=[uyy[c], uC], w=[uST])
                            k.op(PE, lambda e, c=c, b=b: e.matmul(pSQ[:], lhsT=onesf[:], rhs=ysq[b][:], start=(c == 0), stop=(c == 7)), r=[uysq[b], uC], w=[uSQ])
                        k.op(V, lambda e: e.tensor_copy(out=mean[:], in_=pST[:]), r=[uST], w=[ustat])
                        k.op(G, lambda e: e.tensor_tensor(out=msq[:], in0=mean[:], in1=mean[:], op=ALU.mult), r=[ustat], w=[ustat])
                        k.op(V, lambda e: e.tensor_tensor(out=rstd[:], in0=pSQ[:], in1=msq[:], op=ALU.subtract), r=[uSQ, ustat], w=[ustat])
                        k.op(V, lambda e: e.tensor_scalar_add(out=rstd[:], in0=rstd[:], scalar1=EPS), r=[ustat], w=[ustat])
                        k.op(A, lambda e: e.activation(out=rstd[:], in_=rstd[:], func=AF.Sqrt), r=[ustat], w=[ustat])
                        k.op(V, lambda e: e.reciprocal(out=rstd[:], in_=rstd[:]), r=[ustat], w=[ustat])
                        for c in range(8):
                            b = c % 2
                            pG, uG = PF[b], uPF[b]
                            for kk in range(8):
                                k.op(PE, lambda e, c=c, kk=kk, pG=pG: e.matmul(pG[:], lhsT=Wc[:, kk, 2048 + c * 128:2048 + (c + 1) * 128], rhs=hT[:, kk, e0:e0 + 512], start=(kk == 0), stop=(kk == 7)), r=[uWc] + hr, w=[uG])
                            k.op(V, lambda e, c=c, b=b: e.tensor_tensor(out=t1[b][:], in0=yy[:, c, :], in1=mean[:], op=ALU.subtract), r=[uyy[c], ustat], w=[ut1[b]])
                            k.op(G, lambda e, b=b: e.tensor_tensor(out=t2[b][:], in0=t1[b][:], in1=rstd[:], op=ALU.mult), r=[ut1[b], ustat], w=[ut2[b]])
                            k.op(A, lambda e, c=c, b=b: e.activation(out=s1[b][:], in_=t2[b][:], func=AF.Silu, bias=prT[:, c, 33:34], scale=prT[:, c, 32:33]), r=[ut2[b], uL], w=[us1[b]])
                            k.op(A, lambda e, b=b, pG=pG: e.activation(out=sg[b][:], in_=pG[:], func=AF.Silu), r=[uG], w=[usg[b]])
                            k.op(V, lambda e, c=c, b=b: e.tensor_tensor(out=ocT[:, c, :], in0=s1[b][:], in1=sg[b][:], op=ALU.mult), r=[us1[b], usg[b]], w=[uocT])
                        k.dma(S, OC[:, :, tg * 512:(tg + 1) * 512].rearrange("c p t -> p c t"), ocT[:], r=[uocT], w=[u_OC])
                    k.barrier()
            if 'T' in phases:
                phase_T(l)
            if 'R' in phases:
                phase_R(l)
            if 'F' in phases:
                phase_F(l, xsrc, last)
        k.barrier()
    return nc


def make_consts(core):
    c = core % 4
    seg = c * T
    p = np.arange(128)
    tt = np.arange(T)
    inv = 10000.0 ** (-np.arange(128, dtype=np.float64) / 128.0)
    ang = inv[:, None] * (seg + tt)[None, :].astype(np.float64)
    cosr = np.cos(ang).astype(np.float32)
    sinr = np.sin(ang).astype(np.float32)
    inva = 500000.0 ** (-np.arange(8, dtype=np.float64) / 8.0)
    pos = (seg - 128 + np.arange(18)[None, :] * 128 + p[:, None]).astype(np.float64)
    anga = pos[:, :, None] * inva[None, None, :]
    cosa = np.cos(anga).astype(np.float32)
    sina = np.sin(anga).astype(np.float32)
    j = p[:, None]
    i = p[None, :]
    maskL = (j >= i).astype(np.float32)
    maskR = (j <= i).astype(np.float32)
    mask = np.concatenate([maskL, maskR, maskL * (1.0 if c > 0 else 0.0), maskR * (1.0 if c < 3 else 0.0)], axis=1)
    eseg = np.zeros((128, 32), np.float32)
    for n in range(16):
        eseg[:, n] = 2047 - (n * 128 + p)
        eseg[:, 16 + n] = n * 128 + p
    ek = np.stack([127 - p, p], axis=1).astype(np.float32)
    eq = np.concatenate([np.tile((np.arange(128) + 1)[None, :], (128, 1)), np.tile((128 - np.arange(128))[None, :], (128, 1))], axis=1).astype(np.float32)
    Ef = np.maximum(i - j, 0); Eb = np.maximum(j - i, 0)
    Mf = (i >= j); Mb = (j > i)
    em = np.concatenate([Ef, Eb, Mf, Mb], axis=1).astype(np.float32)
    BIG = 1.0e6
    dist = np.zeros((128, 8), np.float32)
    sel = np.zeros((128, 8), np.float32)
    for r in range(4):
        dist[:, r] = T * (c - r - 1) if r < c else BIG
        dist[:, 4 + r] = T * (r - c - 1) if r > c else BIG
        sel[:, r] = 1.0 if r == c - 1 else 0.0
        sel[:, 4 + r] = 1.0 if r == c + 1 else 0.0
    return dict(c_cosr=cosr, c_sinr=sinr, c_cosa=cosa, c_sina=sina, c_mask=mask,
                c_ident=np.eye(128, dtype=np.float32), c_eseg=eseg, c_ek=ek, c_eq=eq, c_em=em,
                c_dist=dist, c_sel=sel)


def make_in_maps(inputs):
    x = np.asarray(inputs['x'], np.float32)
    shared = dict(
        norm_g=np.asarray(inputs['norm_g'], np.float32),
        w_in=np.asarray(inputs['w_in'], np.float32),
        b_gate=np.asarray(inputs['b_gate'], np.float32).reshape(2, 3, D),
        conv_dw=np.asarray(inputs['conv_dw'], np.float32),
        conv_b=np.asarray(inputs['conv_b'], np.float32).reshape(2, 1, D),
        conv_ln_g=np.asarray(inputs['conv_ln_g'], np.float32).reshape(2, 1, D),
        conv_ln_b=np.asarray(inputs['conv_ln_b'], np.float32).reshape(2, 1, D),
        ret_decay=np.asarray(inputs['ret_decay'], np.float32).reshape(2, 8),
        q_norm_g=np.asarray(inputs['q_norm_g'], np.float32),
        k_norm_g=np.asarray(inputs['k_norm_g'], np.float32),
        attn_sink=np.asarray(inputs['attn_sink'], np.float32),
        w_conv_out=np.asarray(inputs['w_conv_out'], np.float32),
        w_ret_out=np.asarray(inputs['w_ret_out'], np.float32),
        w_attn_out=np.asarray(inputs['w_attn_out'], np.float32),
        w_out=np.asarray(inputs['w_out'], np.float32),
    )
    in_maps = []
    for core in range(8):
        b, c = core // 4, core % 4
        xe = np.zeros((TE, D), np.float32)
        lo = c * T - 128
        hi = c * T + T + 128
        slo, shi = max(lo, 0), min(hi, 4 * T)
        xe[slo - lo:shi - lo] = x[b, slo:shi]
        m = dict(shared)
        m['x_ext'] = xe
        m.update(make_consts(core))
        in_maps.append(m)
    return in_maps


_NC = None


def kernel(**inputs):
    global _NC
    if _NC is None:
        _NC = build(2)
    in_maps = make_in_maps(inputs)
    res = run_bass_kernel_spmd(_NC, in_maps, core_ids=list(range(8)))
    out = np.zeros((2, 4 * T, D), np.float32)
    for core in range(8):
        b, c = core // 4, core % 4
        out[b, c * T:(c + 1) * T] = res.results[core]["y_out"]
    return out
```

```python
import os
import numpy as np
import concourse.bass as bass
import concourse.mybir as mybir
from concourse.bass_utils import run_bass_kernel_spmd
from contextlib import ExitStack

F32 = mybir.dt.float32
BF16 = mybir.dt.bfloat16
ALU = mybir.AluOpType
AF = mybir.ActivationFunctionType
AX = mybir.AxisListType

ENG = ['tensor', 'vector', 'scalar', 'gpsimd', 'sync']
EPOCH = 20000
ND = 8
PE, V, A, G, S = 'tensor', 'vector', 'scalar', 'gpsimd', 'sync'


class U:
    __slots__ = ('w', 'rs')

    def __init__(s):
        s.w = None
        s.rs = {}


class KB:
    def __init__(s, nc, stack):
        s.nc = nc
        s.st = stack
        s.cnt = {e: 0 for e in ENG}
        s.nsem = 0
        s.sem = {e: s.new_sem(f'e_{e}') for e in ENG}
        s.hist = {e: [] for e in ENG}
        s.waited = {e: {} for e in ENG}
        s.dsem = {}
        s.dtarget = {}
        s.dcount = {}
        s.n_inst = 0

    def new_sem(s, name):
        s.nsem += 1
        return s.st.enter_context(s.nc.semaphore(f'{name}_{s.nsem}'))

    def _waits(s, engine, r, w):
        deps = {}

        def add(tok):
            key = id(tok[0])
            if key not in deps or deps[key][1] < tok[1]:
                deps[key] = tok
        for u in r:
            if u.w is not None:
                add(u.w)
        for u in w:
            if u.w is not None:
                add(u.w)
            for tok in u.rs.values():
                add(tok)
        waits = []
        wd = s.waited[engine]
        for key, (sem, val, src) in deps.items():
            if engine == PE and src == PE:
                continue
            if wd.get(key, 0) >= val:
                continue
            wd[key] = val
            waits.append((sem, val))
        return waits

    def _emit(s, ename, waits, fn, inc):
        e = getattr(s.nc, ename)
        for sem, val in waits:
            e.wait_ge(sem, val)
        if fn is None:
            return
        ins = fn(e)
        if inc[1] is None:
            ins.then_inc(inc[0])
        else:
            ins.then_inc(inc[0], inc[1])
        s.n_inst += 1

    def op(s, engine, fn, r=(), w=()):
        waits = s._waits(engine, r, w)
        if s.cnt[engine] >= EPOCH:
            s.hist[engine].append((s.sem[engine], s.cnt[engine]))
            s.sem[engine] = s.new_sem(f'e_{engine}')
            s.cnt[engine] = 0
        s.cnt[engine] += 1
        sem = s.sem[engine]
        tok = (sem, s.cnt[engine], engine)
        s._emit(engine, waits, fn, (sem, 1))
        for u in r:
            u.rs[id(sem)] = tok
        for u in w:
            u.w = tok
            u.rs = {}
        return tok

    def dma(s, q, out, in_, r=(), w=(), **kw):
        waits = s._waits(q, r, w)
        if q not in s.dsem:
            s.dsem[q] = [s.new_sem(f'd_{q}{i}') for i in range(ND)]
            s.dtarget[q] = [0] * ND
            s.dcount[q] = 0
        i = s.dcount[q] % ND
        s.dcount[q] += 1
        sem = s.dsem[q][i]
        prev = s.dtarget[q][i]
        if prev > 0 and s.waited[q].get(id(sem), 0) < prev:
            s.waited[q][id(sem)] = prev
            waits.append((sem, prev))
        tgt = prev + 16
        s.dtarget[q][i] = tgt
        tok = (sem, tgt, 'dma')
        s._emit(q, waits, lambda e: e.dma_start(out=out, in_=in_, **kw), (sem, 16))
        for u in r:
            u.rs[id(sem)] = tok
        for u in w:
            u.w = tok
            u.rs = {}
        return tok

    def custom(s, engine, fn, inc_sem, inc_val, r=(), w=()):
        waits = s._waits(engine, r, w)
        tok = (inc_sem, inc_val, 'custom')
        s._emit(engine, waits, fn, (inc_sem, None))
        for u in r:
            u.rs[id(inc_sem)] = tok
        for u in w:
            u.w = tok
            u.rs = {}
        return tok

    def barrier(s):
        toks = []
        for e in ENG:
            for sem, c in s.hist[e]:
                toks.append((sem, c))
            if s.cnt[e] > 0:
                toks.append((s.sem[e], s.cnt[e]))
        for q in s.dsem:
            for i in range(ND):
                if s.dtarget[q][i] > 0:
                    toks.append((s.dsem[q][i], s.dtarget[q][i]))
        for e in ENG:
            wd = s.waited[e]
            waits = []
            for sem, val in toks:
                if wd.get(id(sem), 0) >= val:
                    continue
                wd[id(sem)] = val
                waits.append((sem, val))
            s._emit(e, waits, None, None)


D = 1024
T = 2048
TE = 2304
NT = 16
INW = 14848
OFF_CGLU, OFF_CGATE = 0, 2048
OFF_RQ, OFF_RK, OFF_RV, OFF_RG = 3072, 4096, 5120, 7168
OFF_AQ, OFF_AK, OFF_AV, OFF_AG = 9216, 10240, 10496, 10752
OFF_GL = 11776
EPS = 1e-6


def build(NL=2, dbg=False, phases='CTRF'):
    nc = bass.Bass("TRN2", target_bir_lowering=False)

    def din(name, shape):
        return nc.dram_tensor(name, shape, F32, kind="ExternalInput").ap()

    x_ext = din("x_ext", [TE, D])
    norm_g = din("norm_g", [2, D])
    w_in = din("w_in", [2, D, INW])
    b_gate = din("b_gate", [2, 3, D])
    conv_dw = din("conv_dw", [2, 31, D])
    conv_b = din("conv_b", [2, 1, D])
    conv_ln_g = din("conv_ln_g", [2, 1, D])
    conv_ln_b = din("conv_ln_b", [2, 1, D])
    ret_decay = din("ret_decay", [2, 8])
    q_norm_g = din("q_norm_g", [2, 64])
    k_norm_g = din("k_norm_g", [2, 64])
    attn_sink = din("attn_sink", [2, 16])
    w_conv_out = din("w_conv_out", [2, D, D])
    w_ret_out = din("w_ret_out", [2, 2 * D, D])
    w_attn_out = din("w_attn_out", [2, D, D])
    w_out = din("w_out", [2, D, D])
    c_cosr = din("c_cosr", [128, T])
    c_sinr = din("c_sinr", [128, T])
    c_cosa = din("c_cosa", [128, 18, 8])
    c_sina = din("c_sina", [128, 18, 8])
    c_mask = din("c_mask", [128, 512])
    c_ident = din("c_ident", [128, 128])
    c_eseg = din("c_eseg", [128, 32])
    c_ek = din("c_ek", [128, 2])
    c_eq = din("c_eq", [128, 256])
    c_em = din("c_em", [128, 512])
    c_dist = din("c_dist", [128, 8])
    c_sel = din("c_sel", [128, 8])
    c_e128 = din("c_e128", [128, 16])

    y_out = nc.dram_tensor("y_out", [T, D], F32, kind="ExternalOutput").ap()
    okind = "ExternalOutput" if dbg else "Internal"
    OC = nc.dram_tensor("OC", [8, 128, T], BF16, kind=okind).ap()
    OA = nc.dram_tensor("OA", [16, 64, T], BF16, kind=okind).ap()
    OR = nc.dram_tensor("OR", [16, 128, T], BF16, kind=okind).ap()
    x1e = nc.dram_tensor("x1e", [TE, D], F32, kind=okind).ap()
    fs_bounce = nc.dram_tensor("fs_bounce", [512, 512], F32)
    fs_gath = nc.dram_tensor("fs_gath", [2048, 512], F32)
    h_bounce = nc.dram_tensor("h_bounce", [256, D], F32)
    h_gath = nc.dram_tensor("h_gath", [1024, D], F32)
    u_OC, u_OA, u_OR, u_x1e, u_fsb, u_fsg, u_hb, u_hg, u_yout = [U() for _ in range(9)]
    RG = [[0, 1, 2, 3], [4, 5, 6, 7]]

    with ExitStack() as st0:
        k = KB(nc, st0)

        ncount = [0]

        def sb(st, name, shape, dt):
            ncount[0] += 1
            return st.enter_context(nc.sbuf_tensor(f"{name}_{ncount[0]}", shape, dt))

        PF = [st0.enter_context(nc.psum_tensor(f"pf{i}", [128, 512], F32)) for i in range(7)]
        uPF = [U() for _ in range(7)]
        PT = st0.enter_context(nc.psum_tensor("ptb", [128, 8, 128], BF16))
        uPT = U()
        hT = sb(st0, "hT", [128, 8, TE], BF16)
        uhT = [U() for _ in range(18)]
        ident = sb(st0, "ident", [128, 128], F32); identb = sb(st0, "identb", [128, 128], BF16)
        onesf = sb(st0, "onesf", [128, 128], F32); onesb = sb(st0, "onesb", [128, 128], BF16)
        maskb = sb(st0, "maskb", [128, 512], BF16)
        eseg = sb(st0, "eseg", [128, 32], F32); ek = sb(st0, "ek", [128, 2], F32)
        eq = sb(st0, "eq", [128, 256], F32); em = sb(st0, "em", [128, 512], F32)
        dist = sb(st0, "dist", [128, 8], F32); sel = sb(st0, "sel", [128, 8], F32)
        e128 = sb(st0, "e128", [128, 16], F32)
        cosa = sb(st0, "cosa", [128, 18, 8], F32); sina = sb(st0, "sina", [128, 18, 8], F32)
        lg = sb(st0, "lg", [128, 8], F32)
        qg = sb(st0, "qg", [128, 64], F32); kg = sb(st0, "kg", [128, 64], F32)
        snk = sb(st0, "snk", [128, 16], F32); snke = sb(st0, "snke", [128, 16], F32)
        negc = sb(st0, "negc", [128, 1], F32); tmpc = sb(st0, "tmpc", [128, 4], F32)
        g2 = sb(st0, "g2", [128, 128], F32)
        prm = sb(st0, "prm", [37, D], F32)
        prT = sb(st0, "prT", [128, 8, 37], F32)
        uC = U()
        uL = U()

        for (t, src) in ((ident, c_ident), (eseg, c_eseg), (ek, c_ek), (eq, c_eq), (em, c_em),
                         (dist, c_dist), (sel, c_sel), (cosa, c_cosa), (sina, c_sina), (e128, c_e128)):
            k.dma(S, t[:], src, w=[uC])
        k.dma(G, identb[:], c_ident, w=[uC])
        k.dma(G, maskb[:], c_mask, w=[uC])
        k.op(V, lambda e: e.memset(onesf[:], 1.0 / 1024.0), w=[uC])
        k.op(V, lambda e: e.memset(onesb[:], 1.0), w=[uC])
        k.barrier()

        def phase_T(l):
            with ExitStack() as st:
                Wt = sb(st, "Wt", [128, 8, 2560], BF16); uWt = U()
                for j in range(5):
                    k.dma(G, Wt[:, :, j * 512:(j + 1) * 512], w_in[l, :, OFF_AQ + j * 512:OFF_AQ + (j + 1) * 512].rearrange("(k p) c -> p k c", p=128), w=[uWt])
                qTg = sb(st, "qTg", [64, 16, 512], BF16); uqT = [U() for _ in range(4)]
                kT = sb(st, "kTres", [64, 4, TE], BF16); ukT = [U() for _ in range(18)]
                Vr = sb(st, "Vres", [128, 18, 256], BF16); uVr = [U() for _ in range(18)]
                sq = sb(st, "sq", [128, 1280], F32); usq = U()
                ssa = sb(st, "ssa", [128, 20], F32); rsa = sb(st, "rsa", [128, 20], F32); urs = U()
                qn = sb(st, "qn", [128, 1280], F32); uqn = U()
                rt = [sb(st, f"rt{i}", [128, 20, 8], F32) for i in range(4)]; urt = [U() for _ in range(4)]
                qb = sb(st, "qb", [128, 1024], BF16); uqb = U()
                kb = sb(st, "kb", [128, 256], BF16); ukb = U()
                sq3 = sq[:].rearrange("p (h d) -> p h d", d=64)
                qn3 = qn[:].rearrange("p (h d) -> p h d", d=64)

                def norm_rot(t, h0, h1):
                    nh = h1 - h0
                    k.op(V, lambda e: e.tensor_reduce(out=ssa[:, h0:h1], in_=sq3[:, h0:h1, :], axis=AX.X, op=ALU.add), r=[usq], w=[urs])
                    k.op(V, lambda e: e.tensor_scalar(out=rsa[:, h0:h1], in0=ssa[:, h0:h1], scalar1=1.0 / 64.0, scalar2=EPS, op0=ALU.mult, op1=ALU.add), r=[urs], w=[urs])
                    k.op(A, lambda e: e.activation(out=rsa[:, h0:h1], in_=rsa[:, h0:h1], func=AF.Sqrt), r=[urs], w=[urs])
                    k.op(V, lambda e: e.reciprocal(out=rsa[:, h0:h1], in_=rsa[:, h0:h1]), r=[urs], w=[urs])
                    if h0 == 0:
                        for half in range(2):
                            k.op(V, lambda e, half=half: e.tensor_tensor(out=qn3[:, half * 8:(half + 1) * 8, :], in0=PF[half][:].rearrange("p (h d) -> p h d", d=64), in1=rsa[:, half * 8:(half + 1) * 8].unsqueeze(2).broadcast_to([128, 8, 64]), op=ALU.mult), r=[uPF[half], urs], w=[uqn])
                        k.op(G, lambda e: e.tensor_tensor(out=qn3[:, 0:16, :], in0=qn3[:, 0:16, :], in1=qg[:].unsqueeze(1).broadcast_to([128, 16, 64]), op=ALU.mult), r=[uqn, uL], w=[uqn])
                    else:
                        k.op(V, lambda e: e.tensor_tensor(out=qn3[:, 16:20, :], in0=PF[2][:, 0:256].rearrange("p (h d) -> p h d", d=64), in1=rsa[:, 16:20].unsqueeze(2).broadcast_to([128, 4, 64]), op=ALU.mult), r=[uPF[2], urs], w=[uqn])
                        k.op(G, lambda e: e.tensor_tensor(out=qn3[:, 16:20, :], in0=qn3[:, 16:20, :], in1=kg[:].unsqueeze(1).broadcast_to([128, 4, 64]), op=ALU.mult), r=[uqn, uL], w=[uqn])
                    cb = cosa[:, t, :].unsqueeze(1).broadcast_to([128, nh, 8])
                    sbb = sina[:, t, :].unsqueeze(1).broadcast_to([128, nh, 8])
                    x1 = qn3[:, h0:h1, 0:8]
                    x2 = qn3[:, h0:h1, 8:16]
                    k.op(V, lambda e: e.tensor_tensor(out=rt[0][:, h0:h1, :], in0=x1, in1=cb, op=ALU.mult), r=[uqn, uC], w=[urt[0]])
                    k.op(G, lambda e: e.tensor_tensor(out=rt[1][:, h0:h1, :], in0=x2, in1=sbb, op=ALU.mult), r=[uqn, uC], w=[urt[1]])
                    k.op(V, lambda e: e.tensor_tensor(out=rt[2][:, h0:h1, :], in0=x2, in1=cb, op=ALU.mult), r=[uqn, uC], w=[urt[2]])
                    k.op(G, lambda e: e.tensor_tensor(out=rt[3][:, h0:h1, :], in0=x1, in1=sbb, op=ALU.mult), r=[uqn, uC], w=[urt[3]])
                    k.op(V, lambda e: e.tensor_tensor(out=x1, in0=rt[0][:, h0:h1, :], in1=rt[1][:, h0:h1, :], op=ALU.subtract), r=[urt[0], urt[1], urt[2], urt[3]], w=[uqn])
                    k.op(G, lambda e: e.tensor_tensor(out=x2, in0=rt[2][:, h0:h1, :], in1=rt[3][:, h0:h1, :], op=ALU.add), r=[urt[2], urt[3]], w=[uqn])

                for t in range(18):
                    for kk in range(8):
                        k.op(PE, lambda e, kk=kk, t=t: e.matmul(PF[2][:], lhsT=hT[:, kk, t * 128:(t + 1) * 128], rhs=Wt[:, kk, 1024:1536], start=(kk == 0), stop=(kk == 7)), r=[uWt, uhT[t]], w=[uPF[2]])
                    k.op(A, lambda e: e.activation(out=sq[:, 1024:1280], in_=PF[2][:, 0:256], func=AF.Square), r=[uPF[2]], w=[usq])
                    norm_rot(t, 16, 20)
                    k.op(A, lambda e: e.activation(out=kb[:], in_=qn[:, 1024:1280], func=AF.Copy), r=[uqn], w=[ukb])
                    k.op(A, lambda e, t=t: e.activation(out=Vr[:, t, :], in_=PF[2][:, 256:512], func=AF.Copy), r=[uPF[2]], w=[uVr[t]])
                    for g in range(4):
                        k.op(PE, lambda e, g=g: e.transpose(out=PT[0:64, g, :], in_=kb[:, g * 64:(g + 1) * 64], identity=identb[:]), r=[ukb, uC], w=[uPT])
                    k.op(V, lambda e, t=t: e.tensor_copy(out=kT[:, :, t * 128:(t + 1) * 128], in_=PT[0:64, 0:4, :]), r=[uPT], w=[ukT[t]])
                if dbg == 'T1':
                    k.barrier()
                    return
                gT = sb(st, "gT", [64, 16, 512], BF16); ugT = U()
                pt = [sb(st, f"pt{i}", [128, 512], BF16) for i in range(3)]; upt = [U() for _ in range(3)]
                den = sb(st, "den", [64, 512], F32); uden = U()
                rec = sb(st, "rec", [64, 512], F32); urec = U()
                on = sb(st, "on", [64, 512], F32); uon = U()
                on_b = sb(st, "on_b", [64, 512], F32); uon_b = U()
                on2 = [on, on_b]; uon2 = [uon, uon_b]
                pt_b = [sb(st, f"ptb{i}", [128, 512], BF16) for i in range(3)]; upt_b = [U() for _ in range(3)]
                pt2 = [pt, pt_b]; upt2 = [upt, upt_b]
                oaT = sb(st, "oaT", [64, 16, 512], BF16); uoaT = U()
                for tg in range(4):
                    e0 = 128 + tg * 512
                    hr = uhT[e0 // 128:(e0 + 512) // 128]
                    for nn in range(4):
                        t = tg * 4 + nn + 1
                        for half in range(2):
                            for kk in range(8):
                                k.op(PE, lambda e, half=half, kk=kk, t=t: e.matmul(PF[half][:], lhsT=hT[:, kk, t * 128:(t + 1) * 128], rhs=Wt[:, kk, half * 512:(half + 1) * 512], start=(kk == 0), stop=(kk == 7)), r=[uWt, uhT[t]], w=[uPF[half]])
                            k.op(A, lambda e, half=half: e.activation(out=sq[:, half * 512:(half + 1) * 512], in_=PF[half][:], func=AF.Square), r=[uPF[half]], w=[usq])
                        norm_rot(t, 0, 16)
                        k.op(A, lambda e: e.activation(out=qb[:], in_=qn[:, 0:1024], func=AF.Copy), r=[uqn], w=[uqb])
                        for rr in range(2):
                            for j in range(8):
                                hh = rr * 8 + j
                                k.op(PE, lambda e, j=j, hh=hh: e.transpose(out=PT[0:64, j, :], in_=qb[:, hh * 64:(hh + 1) * 64], identity=identb[:]), r=[uqb, uC], w=[uPT])
                            k.op(V, lambda e, rr=rr, nn=nn: e.tensor_copy(out=qTg[:, rr * 8:(rr + 1) * 8, nn * 128:(nn + 1) * 128], in_=PT[0:64, :, :]), r=[uPT], w=[uqT[nn]])
                    for hh in range(16):
                        ps, ups = PF[hh % 2], uPF[hh % 2]
                        for kk in range(8):
                            k.op(PE, lambda e, ps=ps, hh=hh, kk=kk, e0=e0: e.matmul(ps[0:64, :], lhsT=Wt[:, kk, 1536 + hh * 64:1536 + (hh + 1) * 64], rhs=hT[:, kk, e0:e0 + 512], start=(kk == 0), stop=(kk == 7)), r=[uWt] + hr, w=[ups])
                        k.op(A, lambda e, ps=ps, hh=hh: e.activation(out=gT[:, hh, :], in_=ps[0:64, :], func=AF.Silu), r=[ups], w=[ugT])
                    def t2_front(idx, nn, g):
                        n = tg * 4 + nn
                        t = n + 1
                        ptc, uptc = pt2[idx % 2], upt2[idx % 2]
                        for mi, m in enumerate((t - 1, t, t + 1)):
                            k.op(PE, lambda e, mi=mi, m=m: e.matmul(PF[2 + mi][:], lhsT=kT[:, g, m * 128:(m + 1) * 128], rhs=qTg[:, 4 * g:4 * g + 4, nn * 128:(nn + 1) * 128], start=True, stop=True), r=[ukT[m], uqT[nn]], w=[uPF[2 + mi]])
                            k.op(A, lambda e, mi=mi: e.activation(out=ptc[mi][:], in_=PF[2 + mi][:], func=AF.Exp, bias=negc[:, 0:1], scale=0.125), r=[uPF[2 + mi], uL], w=[uptc[mi]])
                        mL = 256 if t == 1 else 0
                        mR = 384 if t == 16 else 128
                        k.op(V, lambda e: e.tensor_tensor(out=ptc[0][:].rearrange("p (a i) -> p a i", a=4), in0=ptc[0][:].rearrange("p (a i) -> p a i", a=4), in1=maskb[:, mL:mL + 128].unsqueeze(1).broadcast_to([128, 4, 128]), op=ALU.mult), r=[uptc[0], uC], w=[uptc[0]])
                        k.op(G, lambda e: e.tensor_tensor(out=ptc[2][:].rearrange("p (a i) -> p a i", a=4), in0=ptc[2][:].rearrange("p (a i) -> p a i", a=4), in1=maskb[:, mR:mR + 128].unsqueeze(1).broadcast_to([128, 4, 128]), op=ALU.mult), r=[uptc[2], uC], w=[uptc[2]])

                    def t2_back(idx, nn, g):
                        n = tg * 4 + nn
                        t = n + 1
                        ptc, uptc = pt2[idx % 2], upt2[idx % 2]
                        onc, uonc = on2[idx % 2], uon2[idx % 2]
                        for mi, m in enumerate((t - 1, t, t + 1)):
                            k.op(PE, lambda e, mi=mi, m=m: e.matmul(PF[5][0:64, :], lhsT=Vr[:, m, g * 64:(g + 1) * 64], rhs=ptc[mi][:], start=(mi == 0), stop=(mi == 2)), r=[uVr[m], uptc[mi]], w=[uPF[5]])
                        for mi, m in enumerate((t - 1, t, t + 1)):
                            k.op(PE, lambda e, mi=mi: e.matmul(PF[6][0:64, :], lhsT=onesb[:, 0:64], rhs=ptc[mi][:], start=(mi == 0), stop=(mi == 2)), r=[uC, uptc[mi]], w=[uPF[6]])
                        for ei in range(4):
                            hd = 4 * g + ei
                            k.op(A, lambda e, ei=ei, hd=hd: e.activation(out=den[:, ei * 128:(ei + 1) * 128], in_=PF[6][0:64, ei * 128:(ei + 1) * 128], func=AF.Ln, bias=snke[0:64, hd:hd + 1], scale=1.0), r=[uPF[6], uL], w=[uden])
                        k.op(A, lambda e: e.activation(out=rec[:], in_=den[:], func=AF.Exp, scale=-1.0), r=[uden], w=[urec])
                        k.op(V, lambda e: e.tensor_tensor(out=onc[:], in0=PF[5][0:64, :], in1=rec[:], op=ALU.mult), r=[uPF[5], urec], w=[uonc])
                        k.op(G, lambda e: e.tensor_tensor(out=oaT[:, 4 * g:4 * g + 4, nn * 128:(nn + 1) * 128], in0=onc[:].rearrange("p (a i) -> p a i", a=4), in1=gT[:, 4 * g:4 * g + 4, nn * 128:(nn + 1) * 128], op=ALU.mult), r=[uonc, ugT], w=[uoaT])
                    its = [(nn, g) for nn in range(4) for g in range(4)]
                    for idx, (nn, g) in enumerate(its):
                        t2_front(idx, nn, g)
                        if idx > 0:
                            t2_back(idx - 1, *its[idx - 1])
                    t2_back(len(its) - 1, *its[-1])
                    k.dma(S, OA[:, :, tg * 512:(tg + 1) * 512].rearrange("h d t -> d h t"), oaT[:], r=[uoaT], w=[u_OA])
                k.barrier()

        def phase_R(l):
            with ExitStack() as st:
                Wr = sb(st, "Wr", [128, 8, 2048], BF16); uWr = U()
                qT = sb(st, "rqT", [128, 2, T], BF16); uq = [U() for _ in range(4)]
                kT = sb(st, "rkT", [128, 2, T], BF16); ukk = [U() for _ in range(4)]
                Vv = sb(st, "rV", [128, 16, 512], BF16); uV = [U() for _ in range(16)]
                kt = sb(st, "rkt", [128, 16, 256], BF16); ukt = [U() for _ in range(16)]
                Pp = sb(st, "rP", [128, 16, 512], F32); uP = [U() for _ in range(16)]
                cs = sb(st, "rcs", [128, 2, 512], F32); ucs = U()
                tq = [sb(st, f"rtq{i}", [128, 512], F32) for i in range(4)]; utq = [U() for _ in range(4)]
                Sf = sb(st, "Sf", [128, 2, 512], F32); Sb = sb(st, "Sb", [128, 2, 512], F32); uSf = U(); uSb = U()
                Sfb = sb(st, "Sfb", [128, 2, 512], BF16); Sbb = sb(st, "Sbb", [128, 2, 512], BF16); uSfb = U(); uSbb = U()
                fsum = sb(st, "fsum", [128, 4, 512], F32); ufs = U()
                DTm = sb(st, "DTm", [128, 128], F32); dq = sb(st, "dq", [128, 256], F32); dsg = sb(st, "dsg", [128, 32], F32)
                dk = sb(st, "dk", [128, 2], F32); gC = sb(st, "gC", [128, 2], F32); wr = sb(st, "wr", [128, 8], F32)
                tmpD = sb(st, "tmpD", [128, 256], F32); uH = U()
                kfs = sb(st, "kfs", [128, 256], BF16); kbs = sb(st, "kbs", [128, 256], BF16); ukfs = U(); ukbs = U()
                qd = sb(st, "qd", [128, 2, 128], BF16); uqd = U()
                kdc = sb(st, "kdc", [128, 256], BF16); ukdc = U()
                sd = sb(st, "sd", [128, 128], BF16); usd = U()
                oo = sb(st, "oo", [128, 512], F32); uoo = U()
                onn = sb(st, "onn", [128, 512], F32); uonn = U()
                sgg = sb(st, "sgg", [128, 512], F32); usgg = U()
                og = sb(st, "og", [128, 512], BF16); uog = U()
                og_b = sb(st, "og_b", [128, 512], BF16); uog_b = U()
                qd_b = sb(st, "qd_b", [128, 2, 128], BF16); uqd_b = U()
                gn = sb(st, "gn", [128, 32], F32)
                bst = sb(st, "bst", [128, 6], F32); mv = sb(st, "mv", [128, 2], F32); ubn = U()
                orT = sb(st, "orT", [128, 4, 512], BF16); uorT = U()
                RSTOP = int(os.environ.get("RSTOP", "0"))

                class _Stop(Exception):
                    pass

                def ck(n_):
                    if RSTOP == n_:
                        raise _Stop()
                try:
                  for h in range(4):
                      wl = [(1024, OFF_RV + h * 512), (1536, OFF_RG + h * 512)]
                      if h % 2 == 0:
                          wl = [(0, OFF_RQ + h * 256), (512, OFF_RK + h * 256)] + wl
                      for (dst, off) in wl:
                          k.dma(G, Wr[:, :, dst:dst + 512], w_in[l, :, off:off + 512].rearrange("(k p) c -> p k c", p=128), w=[uWr])
                      qc0 = (h % 2) * 256
                      kc0 = 512 + (h % 2) * 256
                      lgf = lg[:, h:h + 1]
                      lgb = lg[:, 4 + h:5 + h]
                      hc = dict(r=[uL, uC, uH], w=[uH])
                      k.op(A, lambda e: e.activation(out=dsg[:, 0:16], in_=eseg[:, 0:16], func=AF.Exp, scale=lgf), **hc)
                      k.op(A, lambda e: e.activation(out=dsg[:, 16:32], in_=eseg[:, 16:32], func=AF.Exp, scale=lgb), **hc)
                      k.op(A, lambda e: e.activation(out=dk[:, 0:1], in_=ek[:, 0:1], func=AF.Exp, scale=lgf), **hc)
                      k.op(A, lambda e: e.activation(out=dk[:, 1:2], in_=ek[:, 1:2], func=AF.Exp, scale=lgb), **hc)
                      k.op(A, lambda e: e.activation(out=dq[:, 0:128], in_=eq[:, 0:128], func=AF.Exp, scale=lgf), **hc)
                      k.op(A, lambda e: e.activation(out=dq[:, 128:256], in_=eq[:, 128:256], func=AF.Exp, scale=lgb), **hc)
                      k.op(V, lambda e: e.tensor_scalar_mul(out=dq[:], in0=dq[:], scalar1=1.0 / 16.0), **hc)
                      k.op(A, lambda e: e.activation(out=gn[:, 0:16], in_=e128[:], func=AF.Exp, scale=lgf), **hc)
                      k.op(A, lambda e: e.activation(out=gn[:, 16:32], in_=e128[:], func=AF.Exp, scale=lgb), **hc)
                      k.op(A, lambda e: e.activation(out=gC[:, 0:1], in_=lgf, func=AF.Exp, scale=128.0), **hc)
                      k.op(A, lambda e: e.activation(out=gC[:, 1:2], in_=lgb, func=AF.Exp, scale=128.0), **hc)
                      k.op(A, lambda e: e.activation(out=wr[:, 0:4], in_=dist[:, 0:4], func=AF.Exp, scale=lgf), **hc)
                      k.op(A, lambda e: e.activation(out=wr[:, 4:8], in_=dist[:, 4:8], func=AF.Exp, scale=lgb), **hc)
                      k.op(A, lambda e: e.activation(out=tmpD[:, 0:128], in_=em[:, 0:128], func=AF.Exp, scale=lgf), **hc)
                      k.op(A, lambda e: e.activation(out=tmpD[:, 128:256], in_=em[:, 128:256], func=AF.Exp, scale=lgb), **hc)
                      k.op(V, lambda e: e.tensor_tensor(out=tmpD[:], in0=tmpD[:], in1=em[:, 256:512], op=ALU.mult), **hc)
                      k.op(V, lambda e: e.tensor_tensor(out=DTm[:], in0=tmpD[:, 0:128], in1=tmpD[:, 128:256], op=ALU.add), **hc)
                      k.op(V, lambda e: e.tensor_scalar_mul(out=DTm[:], in0=DTm[:], scalar1=1.0 / 16.0), **hc)
                      ck(1)
                      for tg in range(4):
                          e0 = 128 + tg * 512
                          hr = uhT[e0 // 128:(e0 + 512) // 128]
                          k.dma(S, cs[:, 0, :], c_cosr[:, tg * 512:(tg + 1) * 512], w=[ucs])
                          k.dma(S, cs[:, 1, :], c_sinr[:, tg * 512:(tg + 1) * 512], w=[ucs])
                          for (col0, dstT, ud) in ((qc0, qT, uq[tg]), (kc0, kT, ukk[tg])):
                              for dc in range(2):
                                  for kk in range(8):
                                      k.op(PE, lambda e, dc=dc, kk=kk, col0=col0, e0=e0: e.matmul(PF[dc][:], lhsT=Wr[:, kk, col0 + dc * 128:col0 + (dc + 1) * 128], rhs=hT[:, kk, e0:e0 + 512], start=(kk == 0), stop=(kk == 7)), r=[uWr] + hr, w=[uPF[dc]])
                              k.op(V, lambda e: e.tensor_tensor(out=tq[0][:], in0=PF[0][:], in1=cs[:, 0, :], op=ALU.mult), r=[uPF[0], ucs], w=[utq[0]])
                              k.op(V, lambda e: e.tensor_tensor(out=tq[1][:], in0=PF[1][:], in1=cs[:, 1, :], op=ALU.mult), r=[uPF[1], ucs], w=[utq[1]])
                              k.op(V, lambda e: e.tensor_tensor(out=tq[2][:], in0=PF[1][:], in1=cs[:, 0, :], op=ALU.mult), r=[uPF[1], ucs], w=[utq[2]])
                              k.op(V, lambda e: e.tensor_tensor(out=tq[3][:], in0=PF[0][:], in1=cs[:, 1, :], op=ALU.mult), r=[uPF[0], ucs], w=[utq[3]])
                              k.op(V, lambda e, dstT=dstT, tg=tg: e.tensor_tensor(out=dstT[:, 0, tg * 512:(tg + 1) * 512], in0=tq[0][:], in1=tq[1][:], op=ALU.subtract), r=[utq[0], utq[1]], w=[ud])
                              k.op(G, lambda e, dstT=dstT, tg=tg: e.tensor_tensor(out=dstT[:, 1, tg * 512:(tg + 1) * 512], in0=tq[2][:], in1=tq[3][:], op=ALU.add), r=[utq[2], utq[3]], w=[ud])
                          ck(2)
                          for nn in range(4):
                              n = tg * 4 + nn
                              t = n + 1
                              for kk in range(8):
                                  k.op(PE, lambda e, kk=kk, t=t: e.matmul(PF[2][:], lhsT=hT[:, kk, t * 128:(t + 1) * 128], rhs=Wr[:, kk, 1024:1536], start=(kk == 0), stop=(kk == 7)), r=[uWr, uhT[t]], w=[uPF[2]])
                              k.op(A, lambda e, n=n: e.activation(out=Vv[:, n, :], in_=PF[2][:], func=AF.Copy), r=[uPF[2]], w=[uV[n]])
                              ck(3)
                              for dc in range(2):
                                  k.op(PE, lambda e, dc=dc, n=n: e.transpose(out=PT[:, dc, :], in_=kT[:, dc, n * 128:(n + 1) * 128], identity=identb[:]), r=[ukk[tg], uC], w=[uPT])
                              ptv = PT[:, 0:2, :]
                              k.op(A, lambda e, n=n, ptv=ptv: e.activation(out=kfs[:].rearrange("p (a d) -> p a d", a=2), in_=ptv, func=AF.Copy, scale=dsg[:, n:n + 1]), r=[uPT, uH], w=[ukfs])
                              k.op(A, lambda e, n=n, ptv=ptv: e.activation(out=kbs[:].rearrange("p (a d) -> p a d", a=2), in_=ptv, func=AF.Copy, scale=dsg[:, 16 + n:17 + n]), r=[uPT, uH], w=[ukbs])
                              k.op(A, lambda e, n=n, ptv=ptv: e.activation(out=kt[:, n, :].rearrange("p (a d) -> p a d", a=2), in_=ptv, func=AF.Copy), r=[uPT], w=[ukt[n]])
                              ck(4)
                              for dc in range(2):
                                  k.op(PE, lambda e, dc=dc, n=n: e.matmul(PF[3 + dc][:], lhsT=kfs[:, dc * 128:(dc + 1) * 128], rhs=Vv[:, n, :], start=(n == 0), stop=(n == 15)), r=[ukfs, uV[n]], w=[uPF[3 + dc]])
                                  k.op(PE, lambda e, dc=dc, n=n: e.matmul(PF[5 + dc][:], lhsT=kbs[:, dc * 128:(dc + 1) * 128], rhs=Vv[:, n, :], start=(n == 0), stop=(n == 15)), r=[ukbs, uV[n]], w=[uPF[5 + dc]])
                      if dbg == 'R0':
                          k.barrier()
                          return
                      for j in range(4):
                          k.op(A, lambda e, j=j: e.activation(out=fsum[:, j, :], in_=PF[3 + j][:], func=AF.Copy), r=[uPF[3 + j]], w=[ufs])
                      k.dma(S, fs_bounce.ap().rearrange("(j p) v -> p j v", p=128), fsum[:], r=[ufs], w=[u_fsb])
                      ccs = k.new_sem("cc")
                      k.custom(G, lambda e: e.collective_compute("AllGather", ALU.bypass, replica_groups=RG, ins=[fs_bounce.ap().opt()], outs=[fs_gath.ap().opt()]), ccs, 1, r=[u_fsb], w=[u_fsg])
                      k.op(V, lambda e: e.memset(Sf[:], 0.0), r=[uSf], w=[uSf])
                      k.op(V, lambda e: e.memset(Sb[:], 0.0), r=[uSb], w=[uSb])
                      k.op(V, lambda e: e.memset(Sfb[:], 0.0), r=[uSfb], w=[uSfb])
                      k.op(V, lambda e: e.memset(Sbb[:], 0.0), r=[uSbb], w=[uSbb])
                      for n in range(15, -1, -1):
                          tg = n // 4
                          k.op(A, lambda e, n=n: e.activation(out=kdc[:], in_=kt[:, n, :], func=AF.Copy, scale=dk[:, 1:2]), r=[ukt[n], uH], w=[ukdc])
                          for dc in range(2):
                              k.op(PE, lambda e, dc=dc, n=n: e.matmul(PF[1 + dc][:], lhsT=kdc[:, dc * 128:(dc + 1) * 128], rhs=Vv[:, n, :], start=True, stop=True), r=[ukdc, uV[n]], w=[uPF[1 + dc]])
                          k.op(V, lambda e, n=n: e.tensor_tensor(out=qd[:], in0=qT[:, :, n * 128:(n + 1) * 128], in1=dq[:, 128:256].unsqueeze(1).broadcast_to([128, 2, 128]), op=ALU.mult), r=[uq[tg], uH], w=[uqd])
                          for dc in range(2):
                              k.op(PE, lambda e, dc=dc: e.matmul(PF[0][:], lhsT=qd[:, dc, :], rhs=Sbb[:, dc, :], start=(dc == 0), stop=(dc == 1)), r=[uqd, uSbb], w=[uPF[0]])
                          for dc in range(2):
                              k.op(V, lambda e, dc=dc: e.scalar_tensor_tensor(out=Sb[:, dc, :], in0=Sb[:, dc, :], scalar=gC[:, 1:2], in1=PF[1 + dc][:], op0=ALU.mult, op1=ALU.add), r=[uPF[1 + dc], uH, uSb], w=[uSb])
                          k.op(A, lambda e: e.activation(out=Sbb[:], in_=Sb[:], func=AF.Copy), r=[uSb], w=[uSbb])
                          k.op(A, lambda e, n=n: e.activation(out=Pp[:, n, :], in_=PF[0][:], func=AF.Copy), r=[uPF[0]], w=[uP[n]])
                      k.op(V, lambda e: e.memset(Sbb[:], 0.0), r=[uSbb], w=[uSbb])
                      Sfb2 = [Sfb, Sbb]
                      uSfb2 = [uSfb, uSbb]
                      for n in range(16):
                          tg = n // 4
                          cur, nxt = n % 2, (n + 1) % 2
                          k.op(A, lambda e, n=n: e.activation(out=kdc[:], in_=kt[:, n, :], func=AF.Copy, scale=dk[:, 0:1]), r=[ukt[n], uH], w=[ukdc])
                          for dc in range(2):
                              k.op(PE, lambda e, dc=dc, n=n: e.matmul(PF[1 + dc][:], lhsT=kdc[:, dc * 128:(dc + 1) * 128], rhs=Vv[:, n, :], start=True, stop=True), r=[ukdc, uV[n]], w=[uPF[1 + dc]])
                          for dc in range(2):
                              k.op(PE, lambda e, dc=dc, n=n: e.matmul(PF[3][:, 0:128], lhsT=kT[:, dc, n * 128:(n + 1) * 128], rhs=qT[:, dc, n * 128:(n + 1) * 128], start=(dc == 0), stop=(dc == 1)), r=[ukk[tg], uq[tg]], w=[uPF[3]])
                          k.op(V, lambda e: e.tensor_tensor(out=sd[:], in0=PF[3][:, 0:128], in1=DTm[:], op=ALU.mult), r=[uPF[3], uH], w=[usd])
                          k.op(G, lambda e, n=n: e.tensor_tensor(out=qd[:], in0=qT[:, :, n * 128:(n + 1) * 128], in1=dq[:, 0:128].unsqueeze(1).broadcast_to([128, 2, 128]), op=ALU.mult), r=[uq[tg], uH], w=[uqd])
                          k.op(PE, lambda e, n=n: e.matmul(PF[4][:], lhsT=sd[:], rhs=Vv[:, n, :], start=True, stop=False), r=[usd, uV[n]], w=[uPF[4]])
                          for dc in range(2):
                              k.op(PE, lambda e, dc=dc, cur=cur: e.matmul(PF[4][:], lhsT=qd[:, dc, :], rhs=Sfb2[cur][:, dc, :], start=False, stop=(dc == 1)), r=[uqd, uSfb2[cur]], w=[uPF[4]])
                          for dc in range(2):
                              k.op(V, lambda e, dc=dc: e.scalar_tensor_tensor(out=Sf[:, dc, :], in0=Sf[:, dc, :], scalar=gC[:, 0:1], in1=PF[1 + dc][:], op0=ALU.mult, op1=ALU.add), r=[uPF[1 + dc], uH, uSf], w=[uSf])
                          k.op(A, lambda e, nxt=nxt: e.activation(out=Sfb2[nxt][:], in_=Sf[:], func=AF.Copy), r=[uSf], w=[uSfb2[nxt]])
                          k.op(V, lambda e, n=n: e.tensor_tensor(out=Pp[:, n, :], in0=PF[4][:], in1=Pp[:, n, :], op=ALU.add), r=[uPF[4], uP[n]], w=[uP[n]])
                      for r_ in range(4):
                          k.dma(S, fsum[:], fs_gath.ap()[r_ * 512:(r_ + 1) * 512, :].rearrange("(j p) v -> p j v", p=128), r=[u_fsg], w=[ufs])
                          for dirn, (Sx, uSx) in enumerate(((Sf, uSf), (Sb, uSb))):
                              for dc in range(2):
                                  wcol = wr[:, dirn * 4 + r_:dirn * 4 + r_ + 1]
                                  if r_ == 0:
                                      k.op(V, lambda e, Sx=Sx, dc=dc, dirn=dirn, wcol=wcol: e.tensor_scalar_mul(out=Sx[:, dc, :], in0=fsum[:, dirn * 2 + dc, :], scalar1=wcol), r=[ufs, uH], w=[uSx])
                                  else:
                                      k.op(V, lambda e, Sx=Sx, dc=dc, dirn=dirn, wcol=wcol: e.scalar_tensor_tensor(out=Sx[:, dc, :], in0=fsum[:, dirn * 2 + dc, :], scalar=wcol, in1=Sx[:, dc, :], op0=ALU.mult, op1=ALU.add), r=[ufs, uH, uSx], w=[uSx])
                      k.op(A, lambda e: e.activation(out=Sfb[:], in_=Sf[:], func=AF.Copy), r=[uSf], w=[uSfb])
                      k.op(A, lambda e: e.activation(out=Sbb[:], in_=Sb[:], func=AF.Copy), r=[uSb], w=[uSbb])
                      og2 = [og, og_b]
                      uog2 = [uog, uog_b]
                      qd2 = [qd, qd_b]
                      uqd2 = [uqd, uqd_b]

                      def finish(n):
                          tg = n // 4
                          for vc in range(4):
                              k.op(PE, lambda e, vc=vc, n=n: e.transpose(out=PT[:, vc, :], in_=og2[n % 2][:, vc * 128:(vc + 1) * 128], identity=identb[:]), r=[uog2[n % 2], uC], w=[uPT])
                          k.op(A, lambda e, n=n: e.activation(out=orT[:, :, (n % 4) * 128:(n % 4 + 1) * 128], in_=PT[:, 0:4, :], func=AF.Copy), r=[uPT], w=[uorT])
                          if n % 4 == 3:
                              k.dma(S, OR[h * 4:(h + 1) * 4, :, tg * 512:(tg + 1) * 512].rearrange("c p t -> p c t"), orT[:], r=[uorT], w=[u_OR])
                      for n in range(16):
                          tg = n // 4
                          t = n + 1
                          cur = n % 2
                          for kk in range(8):
                              k.op(PE, lambda e, kk=kk, t=t: e.matmul(PF[5][:], lhsT=hT[:, kk, t * 128:(t + 1) * 128], rhs=Wr[:, kk, 1536:2048], start=(kk == 0), stop=(kk == 7)), r=[uWr, uhT[t]], w=[uPF[5]])
                          k.op(A, lambda e: e.activation(out=sgg[:], in_=PF[5][:], func=AF.Silu), r=[uPF[5]], w=[usgg])
                          k.op(V, lambda e, n=n: e.scalar_tensor_tensor(out=qd2[0][:], in0=qT[:, :, n * 128:(n + 1) * 128], scalar=gn[:, n:n + 1], in1=dq[:, 0:128].unsqueeze(1).broadcast_to([128, 2, 128]), op0=ALU.mult, op1=ALU.mult), r=[uq[tg], uH], w=[uqd2[0]])
                          k.op(V, lambda e, n=n: e.scalar_tensor_tensor(out=qd2[1][:], in0=qT[:, :, n * 128:(n + 1) * 128], scalar=gn[:, 16 + 15 - n:16 + 16 - n], in1=dq[:, 128:256].unsqueeze(1).broadcast_to([128, 2, 128]), op0=ALU.mult, op1=ALU.mult), r=[uq[tg], uH], w=[uqd2[1]])
                          for di, Sxb, uSxb in ((0, Sfb, uSfb), (1, Sbb, uSbb)):
                              for dc in range(2):
                                  k.op(PE, lambda e, di=di, dc=dc, Sxb=Sxb: e.matmul(PF[4][:], lhsT=qd2[di][:, dc, :], rhs=Sxb[:, dc, :], start=(di == 0 and dc == 0), stop=(di == 1 and dc == 1)), r=[uqd2[di], uSxb], w=[uPF[4]])
                          if n > 0:
                              finish(n - 1)
                          k.op(V, lambda e, n=n: e.tensor_tensor(out=oo[:], in0=PF[4][:], in1=Pp[:, n, :], op=ALU.add), r=[uPF[4], uP[n]], w=[uoo])
                          k.op(V, lambda e: e.bn_stats(out=bst[:], in_=oo[:]), r=[uoo], w=[ubn])
                          k.op(V, lambda e: e.bn_aggr(out=mv[:], in_=bst[:]), r=[ubn], w=[ubn])
                          k.op(V, lambda e: e.tensor_scalar_add(out=mv[:, 1:2], in0=mv[:, 1:2], scalar1=EPS), r=[ubn], w=[ubn])
                          k.op(A, lambda e: e.activation(out=mv[:, 1:2], in_=mv[:, 1:2], func=AF.Sqrt), r=[ubn], w=[ubn])
                          k.op(V, lambda e: e.reciprocal(out=mv[:, 1:2], in_=mv[:, 1:2]), r=[ubn], w=[ubn])
                          k.op(V, lambda e: e.tensor_scalar(out=onn[:], in0=oo[:], scalar1=mv[:, 0:1], scalar2=mv[:, 1:2], op0=ALU.subtract, op1=ALU.mult), r=[uoo, ubn], w=[uonn])
                          k.op(G, lambda e, cur=cur: e.tensor_tensor(out=og2[cur][:], in0=onn[:], in1=sgg[:], op=ALU.mult), r=[uonn, usgg], w=[uog2[cur]])
                      finish(15)
                except _Stop:
                    pass
                k.barrier()

        def phase_F(l, xsrc, last):
            with ExitStack() as st:
                mg = sb(st, "mg", [128, 8, T], F32); umg = [U() for _ in range(4)]
                sgm = sb(st, "sgm", [128, 512], F32); usgm = U()
                tmpm = sb(st, "tmpm", [128, 512], F32); utmpm = U()
                for bi, (nk, kp) in enumerate(((8, 128), (16, 128), (16, 64))):
                    with ExitStack() as st2:
                        Wb = sb(st2, f"Wb{bi}", [kp, nk, D], BF16); uWb = U()
                        Wg = sb(st2, f"Wg{bi}", [128, 8, D], BF16); uWg = U()
                        ob = sb(st2, f"ob{bi}", [kp, nk, 512], BF16); uob = U()
                        src_w = (w_conv_out, w_ret_out, w_attn_out)[bi]
                        if bi == 2:
                            view = src_w[l].rearrange("(h d) c -> d h c", d=64)
                        else:
                            view = src_w[l].rearrange("(k p) c -> p k c", p=128)
                        for j in range(0, nk, 4):
                            k.dma(G, Wb[:, j:j + 4, :], view[:, j:j + 4, :], w=[uWb])
                        for j in range(2):
                            c0 = OFF_GL + bi * 1024 + j * 512
                            k.dma(G, Wg[:, :, j * 512:(j + 1) * 512], w_in[l, :, c0:c0 + 512].rearrange("(k p) c -> p k c", p=128), w=[uWg])
                        osrc = (OC, OR, OA)[bi]
                        uos = (u_OC, u_OR, u_OA)[bi]
                        for tg in range(4):
                            e0 = 128 + tg * 512
                            hr = uhT[e0 // 128:(e0 + 512) // 128]
                            k.dma(S, ob[:], osrc[:, :, tg * 512:(tg + 1) * 512].rearrange("c p t -> p c t"), r=[uos], w=[uob])
                            for m in range(8):
                                py, upy = PF[m % 3], uPF[m % 3]
                                pg, upg = PF[3 + m % 3], uPF[3 + m % 3]
                                for j in range(nk):
                                    k.op(PE, lambda e, py=py, j=j, m=m: e.matmul(py[:], lhsT=Wb[:, j, m * 128:(m + 1) * 128], rhs=ob[:, j, :], start=(j == 0), stop=(j == nk - 1)), r=[uWb, uob], w=[upy])
                                for kk in range(8):
                                    k.op(PE, lambda e, pg=pg, kk=kk, m=m, e0=e0: e.matmul(pg[:], lhsT=Wg[:, kk, m * 128:(m + 1) * 128], rhs=hT[:, kk, e0:e0 + 512], start=(kk == 0), stop=(kk == 7)), r=[uWg] + hr, w=[upg])
                                k.op(A, lambda e, pg=pg, m=m, bi=bi: e.activation(out=sgm[:], in_=pg[:], func=AF.Sigmoid, bias=prT[:, m, 34 + bi:35 + bi], scale=1.0), r=[upg, uL], w=[usgm])
                                if bi == 0:
                                    k.op(V, lambda e, py=py, m=m, tg=tg: e.tensor_tensor(out=mg[:, m, tg * 512:(tg + 1) * 512], in0=py[:], in1=sgm[:], op=ALU.mult), r=[upy, usgm], w=[umg[tg]])
                                else:
                                    k.op(V, lambda e, py=py: e.tensor_tensor(out=tmpm[:], in0=py[:], in1=sgm[:], op=ALU.mult), r=[upy, usgm], w=[utmpm])
                                    k.op(G, lambda e, m=m, tg=tg: e.tensor_tensor(out=mg[:, m, tg * 512:(tg + 1) * 512], in0=mg[:, m, tg * 512:(tg + 1) * 512], in1=tmpm[:], op=ALU.add), r=[utmpm, umg[tg]], w=[umg[tg]])
                        k.barrier()
                with ExitStack() as st2:
                    Wo = sb(st2, "Wo", [128, 8, D], BF16); uWo = U()
                    for j in range(2):
                        k.dma(G, Wo[:, j * 4:(j + 1) * 4, :], w_out[l].rearrange("(k p) c -> p k c", p=128)[:, j * 4:(j + 1) * 4, :], w=[uWo])
                    mb = sb(st2, "mb", [128, 8, 128], BF16); umb = U()
                    xt = [sb(st2, f"fxt{i}", [128, D], F32) for i in range(2)]; uxt = [U(), U()]
                    xn = [sb(st2, f"fxn{i}", [128, D], F32) for i in range(2)]; uxn = [U(), U()]
                    for n in range(16):
                        b = n % 2
                        k.op(A, lambda e, n=n: e.activation(out=mb[:], in_=mg[:, :, n * 128:(n + 1) * 128], func=AF.Copy), r=[umg[n // 4]], w=[umb])
                        k.dma(S, xt[b][:], xsrc[128 + n * 128:128 + (n + 1) * 128, :], r=[u_x1e], w=[uxt[b]])
                        for half in range(2):
                            for kk in range(8):
                                k.op(PE, lambda e, half=half, kk=kk: e.matmul(PF[half][:], lhsT=mb[:, kk, :], rhs=Wo[:, kk, half * 512:(half + 1) * 512], start=(kk == 0), stop=(kk == 7)), r=[umb, uWo], w=[uPF[half]])
                            k.op(V, lambda e, half=half, b=b: e.tensor_tensor(out=xn[b][:, half * 512:(half + 1) * 512], in0=PF[half][:], in1=xt[b][:, half * 512:(half + 1) * 512], op=ALU.add), r=[uPF[half], uxt[b]], w=[uxn[b]])
                        if last:
                            k.dma(S, y_out[n * 128:(n + 1) * 128, :], xn[b][:], r=[uxn[b]], w=[u_yout])
                        else:
                            k.dma(S, x1e[128 + n * 128:128 + (n + 1) * 128, :], xn[b][:], r=[uxn[b]], w=[u_x1e])
                            if n == 0:
                                k.dma(S, h_bounce.ap()[0:128, :], xn[b][:], r=[uxn[b]], w=[u_hb])
                            if n == 15:
                                k.dma(S, h_bounce.ap()[128:256, :], xn[b][:], r=[uxn[b]], w=[u_hb])
                    k.barrier()
            if not last:
                ccs = k.new_sem("cch")
                k.custom(G, lambda e: e.collective_compute("AllGather", ALU.bypass, replica_groups=RG, ins=[h_bounce.ap().opt()], outs=[h_gath.ap().opt()]), ccs, 1, r=[u_hb], w=[u_hg])
                with ExitStack() as st2:
                    hb = sb(st2, "hb", [128, 2, D], F32); uhb = U()
                    accL = sb(st2, "accL", [128, D], F32); accR = sb(st2, "accR", [128, D], F32); uaL = U(); uaR = U()
                    for r_ in range(4):
                        k.dma(S, hb[:], h_gath.ap()[r_ * 256:(r_ + 1) * 256, :].rearrange("(a p) f -> p a f", p=128), r=[u_hg], w=[uhb])
                        for (acc, ua, a_, sc) in ((accL, uaL, 1, r_), (accR, uaR, 0, 4 + r_)):
                            if r_ == 0:
                                k.op(V, lambda e, acc=acc, a_=a_, sc=sc: e.tensor_scalar_mul(out=acc[:], in0=hb[:, a_, :], scalar1=sel[:, sc:sc + 1]), r=[uhb, uC], w=[ua])
                            else:
                                k.op(V, lambda e, acc=acc, a_=a_, sc=sc: e.scalar_tensor_tensor(out=acc[:], in0=hb[:, a_, :], scalar=sel[:, sc:sc + 1], in1=acc[:], op0=ALU.mult, op1=ALU.add), r=[uhb, uC, ua], w=[ua])
                    k.dma(S, x1e[0:128, :], accL[:], r=[uaL], w=[u_x1e])
                    k.dma(S, x1e[TE - 128:TE, :], accR[:], r=[uaR], w=[u_x1e])
                    k.barrier()

        for l in range(NL):
            xsrc = x_ext if l == 0 else x1e
            last = (l == NL - 1)
            k.dma(S, lg[:], ret_decay[l:l + 1, :].partition_broadcast(128), w=[uL])
            k.dma(S, qg[:], q_norm_g[l:l + 1, :].partition_broadcast(128), w=[uL])
            k.dma(S, kg[:], k_norm_g[l:l + 1, :].partition_broadcast(128), w=[uL])
            k.dma(S, snk[:], attn_sink[l:l + 1, :].partition_broadcast(128), w=[uL])
            k.dma(S, prm[0:31, :], conv_dw[l], w=[uL])
            k.dma(S, prm[31:32, :], conv_b[l], w=[uL])
            k.dma(S, prm[32:33, :], conv_ln_g[l], w=[uL])
            k.dma(S, prm[33:34, :], conv_ln_b[l], w=[uL])
            k.dma(S, prm[34:37, :], b_gate[l], w=[uL])
            k.op(A, lambda e: e.activation(out=lg[:], in_=lg[:], func=AF.Exp), r=[uL], w=[uL])
            k.op(V, lambda e: e.tensor_scalar_mul(out=lg[:], in0=lg[:], scalar1=-1.0), r=[uL], w=[uL])
            k.op(V, lambda e: e.tensor_tensor(out=g2[:, 0:64], in0=qg[:], in1=qg[:], op=ALU.mult), r=[uL], w=[uL])
            k.op(V, lambda e: e.tensor_tensor(out=g2[:, 64:128], in0=kg[:], in1=kg[:], op=ALU.mult), r=[uL], w=[uL])
            k.op(V, lambda e: e.tensor_reduce(out=tmpc[:, 0:1], in_=g2[:, 0:64], axis=AX.X, op=ALU.max), r=[uL], w=[uL])
            k.op(V, lambda e: e.tensor_reduce(out=tmpc[:, 1:2], in_=g2[:, 64:128], axis=AX.X, op=ALU.max), r=[uL], w=[uL])
            k.op(V, lambda e: e.tensor_tensor(out=tmpc[:, 2:3], in0=tmpc[:, 0:1], in1=tmpc[:, 1:2], op=ALU.mult), r=[uL], w=[uL])
            k.op(A, lambda e: e.activation(out=tmpc[:, 3:4], in_=tmpc[:, 2:3], func=AF.Sqrt), r=[uL], w=[uL])
            k.op(V, lambda e: e.tensor_scalar_mul(out=negc[:], in0=tmpc[:, 3:4], scalar1=-8.0), r=[uL], w=[uL])
            k.op(A, lambda e: e.activation(out=snke[:], in_=snk[:], func=AF.Exp, bias=negc[:, 0:1], scale=1.0), r=[uL], w=[uL])
            for c in range(8):
                k.op(PE, lambda e, c=c: e.transpose(out=PF[0][:, c * 37:(c + 1) * 37], in_=prm[:, c * 128:(c + 1) * 128], identity=ident[0:37, 0:37]), r=[uL, uC], w=[uPF[0]])
            k.op(V, lambda e: e.tensor_copy(out=prT[:].rearrange("p c j -> p (c j)"), in_=PF[0][:, 0:296]), r=[uPF[0]], w=[uL])
            k.barrier()

            with ExitStack() as st:
                g_bc = sb(st, "g_bc", [128, D], F32); ug = U()
                k.dma(S, g_bc[:], norm_g[l:l + 1, :].partition_broadcast(128), w=[ug])
                xt = [sb(st, f"xt{i}", [128, D], F32) for i in range(2)]; uxt = [U(), U()]
                junk = [sb(st, f"junk{i}", [128, D], BF16) for i in range(2)]; ujunk = [U(), U()]
                xs = [sb(st, f"xs{i}", [128, D], BF16) for i in range(2)]; uxs = [U(), U()]
                ssq = [sb(st, f"ssq{i}", [128, 2], F32) for i in range(2)]; ussq = [U(), U()]
                def a_s1(t):
                    b = t % 2
                    k.dma(S, xt[b][:], xsrc[t * 128:(t + 1) * 128, :], r=[u_x1e], w=[uxt[b]])
                    k.op(V, lambda e, b=b: e.memset(ssq[b][:], 0.0), w=[ussq[b]])
                    k.op(A, lambda e, b=b: e.activation(out=junk[b][:], in_=xt[b][:], func=AF.Square, accum_out=ssq[b][:, 0:1]), r=[uxt[b]], w=[ujunk[b], ussq[b]])
                    k.op(V, lambda e, b=b: e.tensor_scalar(out=ssq[b][:, 1:2], in0=ssq[b][:, 0:1], scalar1=1.0 / D, scalar2=EPS, op0=ALU.mult, op1=ALU.add), r=[ussq[b]], w=[ussq[b]])
                    k.op(A, lambda e, b=b: e.activation(out=ssq[b][:, 1:2], in_=ssq[b][:, 1:2], func=AF.Sqrt), r=[ussq[b]], w=[ussq[b]])
                    k.op(V, lambda e, b=b: e.reciprocal(out=ssq[b][:, 1:2], in_=ssq[b][:, 1:2]), r=[ussq[b]], w=[ussq[b]])
                    k.op(V, lambda e, b=b: e.scalar_tensor_tensor(out=xs[b][:], in0=xt[b][:], scalar=ssq[b][:, 1:2], in1=g_bc[:], op0=ALU.mult, op1=ALU.mult), r=[uxt[b], ussq[b], ug], w=[uxs[b]])

                def a_s2(t):
                    b = t % 2
                    for c in range(8):
                        k.op(PE, lambda e, b=b, c=c: e.transpose(out=PT[:, c, :], in_=xs[b][:, c * 128:(c + 1) * 128], identity=identb[:]), r=[uxs[b], uC], w=[uPT])
                    k.op(A, lambda e, t=t: e.activation(out=hT[:, :, t * 128:(t + 1) * 128], in_=PT[:], func=AF.Copy), r=[uPT], w=[uhT[t]])
                a_s1(0)
                for t in range(18):
                    if t < 17:
                        a_s1(t + 1)
                    a_s2(t)
                k.barrier()

            if 'C' in phases:
                with ExitStack() as st:
                    Wc = sb(st, "Wc", [128, 8, 3072], BF16); uWc = U()
                    for j in range(6):
                        k.dma(G, Wc[:, :, j * 512:(j + 1) * 512], w_in[l, :, j * 512:(j + 1) * 512].rearrange("(k p) c -> p k c", p=128), w=[uWc])
                    sig = [sb(st, f"sig{i}", [128, 544], F32) for i in range(2)]; usig = [U(), U()]
                    vv = [sb(st, f"vv{i}", [128, 544], F32) for i in range(2)]; uvv = [U(), U()]
                    acc1 = [sb(st, f"acc1{i}", [128, 512], F32) for i in range(2)]; uacc1 = [U(), U()]
                    acc2 = [sb(st, f"acc2{i}", [128, 512], F32) for i in range(2)]; uacc2 = [U(), U()]
                    tmpk = [sb(st, f"tmpk{i}", [128, 512], F32) for i in range(4)]; utmpk = [U() for _ in range(4)]
                    yy = sb(st, "yy", [128, 8, 512], F32); uyy = [U() for _ in range(8)]
                    ysq = [sb(st, f"ysq{i}", [128, 512], F32) for i in range(2)]; uysq = [U(), U()]
                    mean = sb(st, "mean", [128, 512], F32); msq = sb(st, "msq", [128, 512], F32)
                    rstd = sb(st, "rstd", [128, 512], F32); ustat = U()
                    t1 = [sb(st, f"t1{i}", [128, 512], F32) for i in range(2)]; ut1 = [U(), U()]
                    t2 = [sb(st, f"t2{i}", [128, 512], F32) for i in range(2)]; ut2 = [U(), U()]
                    s1 = [sb(st, f"s1{i}", [128, 512], F32) for i in range(2)]; us1 = [U(), U()]
                    sg = [sb(st, f"sg{i}", [128, 512], F32) for i in range(2)]; usg = [U(), U()]
                    ocT = sb(st, "ocT", [128, 8, 512], BF16); uocT = U()
                    pTls = [PF[4], PF[5]]; uTl = [uPF[4], uPF[5]]
                    pST, pSQ, uST, uSQ = PF[6], PF[4], uPF[6], uPF[4]
                    it = 0
                    tk = 0

                    def proj(tg, c, b):
                        e0 = 128 + tg * 512
                        hr = uhT[(e0 - 16) // 128:(e0 + 528 + 127) // 128]
                        pA, uA, pB, uB = PF[b], uPF[b], PF[2 + b], uPF[2 + b]
                        tb = 0
                        pTl = pTls[b]
                        for (ps, ups, col0, to) in ((pA, uA, c * 128, tb), (pB, uB, 1024 + c * 128, tb + 32)):
                            for kk in range(8):
                                k.op(PE, lambda e, ps=ps, kk=kk, col0=col0: e.matmul(ps[:, 0:512], lhsT=Wc[:, kk, col0:col0 + 128], rhs=hT[:, kk, e0 - 16:e0 + 496], start=(kk == 0), stop=(kk == 7)), r=[uWc] + hr, w=[ups])
                            for kk in range(8):
                                k.op(PE, lambda e, kk=kk, col0=col0, to=to, pTl=pTl: e.matmul(pTl[:, to:to + 32], lhsT=Wc[:, kk, col0:col0 + 128], rhs=hT[:, kk, e0 + 496:e0 + 528], start=(kk == 0), stop=(kk == 7)), r=[uWc] + hr, w=[uTl[b]])
                    for tg in range(4):
                        e0 = 128 + tg * 512
                        hr = uhT[(e0 - 16) // 128:(e0 + 528 + 127) // 128]
                        proj(tg, 0, it % 2)
                        for c in range(8):
                            b = it % 2
                            it += 1
                            pA, uA, pB, uB = PF[b], uPF[b], PF[2 + b], uPF[2 + b]
                            tb = 0
                            pTl = pTls[b]
                            if c < 7:
                                proj(tg, c + 1, it % 2)
                            k.op(A, lambda e, b=b, pB=pB: e.activation(out=sig[b][:, 0:512], in_=pB[:, 0:512], func=AF.Sigmoid), r=[uB], w=[usig[b]])
                            k.op(A, lambda e, b=b, tb=tb, pTl=pTl: e.activation(out=sig[b][:, 512:544], in_=pTl[:, tb + 32:tb + 64], func=AF.Sigmoid), r=[uTl[b]], w=[usig[b]])
                            k.op(V, lambda e, b=b, pA=pA: e.tensor_tensor(out=vv[b][:, 0:512], in0=pA[:, 0:512], in1=sig[b][:, 0:512], op=ALU.mult), r=[uA, usig[b]], w=[uvv[b]])
                            k.op(V, lambda e, b=b, tb=tb, pTl=pTl: e.tensor_tensor(out=vv[b][:, 512:544], in0=pTl[:, tb:tb + 32], in1=sig[b][:, 512:544], op=ALU.mult), r=[uTl[b], usig[b]], w=[uvv[b]])
                            k.op(V, lambda e, c=c, b=b: e.tensor_scalar(out=acc1[b][:], in0=vv[b][:, 1:513], scalar1=prT[:, c, 0:1], scalar2=prT[:, c, 31:32], op0=ALU.mult, op1=ALU.add), r=[uvv[b], uL], w=[uacc1[b]])
                            for kt in range(1, 16):
                                k.op(V, lambda e, c=c, kt=kt, b=b: e.scalar_tensor_tensor(out=acc1[b][:], in0=vv[b][:, kt + 1:kt + 513], scalar=prT[:, c, kt:kt + 1], in1=acc1[b][:], op0=ALU.mult, op1=ALU.add), r=[uvv[b], uL, uacc1[b]], w=[uacc1[b]])
                            k.op(A, lambda e, c=c, b=b: e.activation(out=acc2[b][:], in_=vv[b][:, 17:529], func=AF.Copy, scale=prT[:, c, 16:17]), r=[uvv[b], uL], w=[uacc2[b]])
                            for kt in range(17, 31):
                                j = tk % 4
                                tk += 1
                                k.op(A, lambda e, c=c, kt=kt, j=j, b=b: e.activation(out=tmpk[j][:], in_=vv[b][:, kt + 1:kt + 513], func=AF.Copy, scale=prT[:, c, kt:kt + 1]), r=[uvv[b], uL], w=[utmpk[j]])
                                k.op(G, lambda e, j=j, b=b: e.tensor_tensor(out=acc2[b][:], in0=acc2[b][:], in1=tmpk[j][:], op=ALU.add), r=[utmpk[j], uacc2[b]], w=[uacc2[b]])
                            k.op(G, lambda e, c=c, b=b: e.tensor_tensor(out=yy[:, c, :], in0=acc1[b][:], in1=acc2[b][:], op=ALU.add), r=[uacc1[b], uacc2[b]], w=[uyy[c]])
                        for c in range(8):
                            b = c % 2
                            k.op(A, lambda e, c=c, b=b: e.activation(out=ysq[b][:], in_=yy[:, c, :], func=AF.Square), r=[uyy[c]], w=[uysq[b]])
                            k.op(PE, lambda e, c=c: e.matmul(pST[:], lhsT=onesf[:], rhs=yy[:, c, :], start=(c == 0), stop=(c == 7)), r=[uyy[c], uC], w=[uST])
                            k.op(PE, lambda e, c=c, b=b: e.matmul(pSQ[:], lhsT=onesf[:], rhs=ysq[b][:], start=(c == 0), stop=(c == 7)), r=[uysq[b], uC], w=[uSQ])
                        k.op(V, lambda e: e.tensor_copy(out=mean[:], in_=pST[:]), r=[uST], w=[ustat])
                        k.op(G, lambda e: e.tensor_tensor(out=msq[:], in0=mean[:], in1=mean[:], op=ALU.mult), r=[ustat], w=[ustat])
                        k.op(V, lambda e: e.tensor_tensor(out=rstd[:], in0=pSQ[:], in1=msq[:], op=ALU.subtract), r=[uSQ, ustat], w=[ustat])
                        k.op(V, lambda e: e.tensor_scalar_add(out=rstd[:], in0=rstd[:], scalar1=EPS), r=[ustat], w=[ustat])
                        k.op(A, lambda e: e.activation(out=rstd[:], in_=rstd[:], func=AF.Sqrt), r=[ustat], w=[ustat])
                        k.op(V, lambda e: e.reciprocal(out=rstd[:], in_=rstd[:]), r=[ustat], w=[ustat])
                        for c in range(8):
                            b = c % 2
                            pG, uG = PF[b], uPF[b]
                            for kk in range(8):
                                k.op(PE, lambda e, c=c, kk=kk, pG=pG: e.matmul(pG[:], lhsT=Wc[:, kk, 2048 + c * 128:2048 + (c + 1) * 128], rhs=hT[:, kk, e0:e0 + 512], start=(kk == 0), stop=(kk == 7)), r=[uWc] + hr, w=[uG])
                            k.op(V, lambda e, c=c, b=b: e.tensor_tensor(out=t1[b][:], in0=yy[:, c, :], in1=mean[:], op=ALU.subtract), r=[uyy[c], ustat], w=[ut1[b]])
                            k.op(G, lambda e, b=b: e.tensor_tensor(out=t2[b][:], in0=t1[b][:], in1=rstd[:], op=ALU.mult), r=[ut1[b], ustat], w=[ut2[b]])
                            k.op(A, lambda e, c=c, b=b: e.activation(out=s1[b][:], in_=t2[b][:], func=AF.Silu, bias=prT[:, c, 33:34], scale=prT[:, c, 32:33]), r=[ut2[b], uL], w=[us1[b]])
                            k.op(A, lambda e, b=b, pG=pG: e.activation(out=sg[b][:], in_=pG[:], func=AF.Silu), r=[uG], w=[usg[b]])
                            k.op(V, lambda e, c=c, b=b: e.tensor_tensor(out=ocT[:, c, :], in0=s1[b][:], in1=sg[b][:], op=ALU.mult), r=[us1[b], usg[b]], w=[uocT])
                        k.dma(S, OC[:, :, tg * 512:(tg + 1) * 512].rearrange("c p t -> p c t"), ocT[:], r=[uocT], w=[u_OC])
                    k.barrier()
            if 'T' in phases:
                phase_T(l)
            if 'R' in phases:
                phase_R(l)
            if 'F' in phases:
                phase_F(l, xsrc, last)
        k.barrier()
    return nc


def make_consts(core):
    c = core % 4
    seg = c * T
    p = np.arange(128)
    tt = np.arange(T)
    inv = 10000.0 ** (-np.arange(128, dtype=np.float64) / 128.0)
    ang = inv[:, None] * (seg + tt)[None, :].astype(np.float64)
    cosr = np.cos(ang).astype(np.float32)
    sinr = np.sin(ang).astype(np.float32)
    inva = 500000.0 ** (-np.arange(8, dtype=np.float64) / 8.0)
    pos = (seg - 128 + np.arange(18)[None, :] * 128 + p[:, None]).astype(np.float64)
    anga = pos[:, :, None] * inva[None, None, :]
    cosa = np.cos(anga).astype(np.float32)
    sina = np.sin(anga).astype(np.float32)
    j = p[:, None]
    i = p[None, :]
    maskL = (j >= i).astype(np.float32)
    maskR = (j <= i).astype(np.float32)
    mask = np.concatenate([maskL, maskR, maskL * (1.0 if c > 0 else 0.0), maskR * (1.0 if c < 3 else 0.0)], axis=1)
    eseg = np.zeros((128, 32), np.float32)
    for n in range(16):
        eseg[:, n] = 2047 - (n * 128 + p)
        eseg[:, 16 + n] = n * 128 + p
    ek = np.stack([127 - p, p], axis=1).astype(np.float32)
    eq = np.concatenate([np.tile((np.arange(128) + 1)[None, :], (128, 1)), np.tile((128 - np.arange(128))[None, :], (128, 1))], axis=1).astype(np.float32)
    Ef = np.maximum(i - j, 0); Eb = np.maximum(j - i, 0)
    Mf = (i >= j); Mb = (j > i)
    em = np.concatenate([Ef, Eb, Mf, Mb], axis=1).astype(np.float32)
    BIG = 1.0e6
    dist = np.zeros((128, 8), np.float32)
    sel = np.zeros((128, 8), np.float32)
    for r in range(4):
        dist[:, r] = T * (c - r - 1) if r < c else BIG
        dist[:, 4 + r] = T * (r - c - 1) if r > c else BIG
        sel[:, r] = 1.0 if r == c - 1 else 0.0
        sel[:, 4 + r] = 1.0 if r == c + 1 else 0.0
    return dict(c_cosr=cosr, c_sinr=sinr, c_cosa=cosa, c_sina=sina, c_mask=mask,
                c_ident=np.eye(128, dtype=np.float32), c_eseg=eseg, c_ek=ek, c_eq=eq, c_em=em,
                c_dist=dist, c_sel=sel, c_e128=np.tile((128.0 * np.arange(16, dtype=np.float32))[None, :], (128, 1)))


def make_in_maps(inputs):
    x = np.asarray(inputs['x'], np.float32)
    shared = dict(
        norm_g=np.asarray(inputs['norm_g'], np.float32),
        w_in=np.asarray(inputs['w_in'], np.float32),
        b_gate=np.asarray(inputs['b_gate'], np.float32).reshape(2, 3, D),
        conv_dw=np.asarray(inputs['conv_dw'], np.float32),
        conv_b=np.asarray(inputs['conv_b'], np.float32).reshape(2, 1, D),
        conv_ln_g=np.asarray(inputs['conv_ln_g'], np.float32).reshape(2, 1, D),
        conv_ln_b=np.asarray(inputs['conv_ln_b'], np.float32).reshape(2, 1, D),
        ret_decay=np.asarray(inputs['ret_decay'], np.float32).reshape(2, 8),
        q_norm_g=np.asarray(inputs['q_norm_g'], np.float32),
        k_norm_g=np.asarray(inputs['k_norm_g'], np.float32),
        attn_sink=np.asarray(inputs['attn_sink'], np.float32),
        w_conv_out=np.asarray(inputs['w_conv_out'], np.float32),
        w_ret_out=np.asarray(inputs['w_ret_out'], np.float32),
        w_attn_out=np.asarray(inputs['w_attn_out'], np.float32),
        w_out=np.asarray(inputs['w_out'], np.float32),
    )
    in_maps = []
    for core in range(8):
        b, c = core // 4, core % 4
        xe = np.zeros((TE, D), np.float32)
        lo = c * T - 128
        hi = c * T + T + 128
        slo, shi = max(lo, 0), min(hi, 4 * T)
        xe[slo - lo:shi - lo] = x[b, slo:shi]
        m = dict(shared)
        m['x_ext'] = xe
        m.update(make_consts(core))
        in_maps.append(m)
    return in_maps


_NC = None


def kernel(**inputs):
    global _NC
    if _NC is None:
        _NC = build(2)
    in_maps = make_in_maps(inputs)
    res = run_bass_kernel_spmd(_NC, in_maps, core_ids=list(range(8)))
    out = np.zeros((2, 4 * T, D), np.float32)
    for core in range(8):
        b, c = core // 4, core % 4
        out[b, c * T:(c + 1) * T] = res.results[core]["y_out"]
    return out
```

```python
import os
import numpy as np
import concourse.bass as bass
import concourse.mybir as mybir
from concourse.bass_utils import run_bass_kernel_spmd
from contextlib import ExitStack

F32 = mybir.dt.float32
BF16 = mybir.dt.bfloat16
ALU = mybir.AluOpType
AF = mybir.ActivationFunctionType
AX = mybir.AxisListType

ENG = ['tensor', 'vector', 'scalar', 'gpsimd', 'sync']
EPOCH = 20000
ND = 8
PE, V, A, G, S = 'tensor', 'vector', 'scalar', 'gpsimd', 'sync'


class U:
    __slots__ = ('w', 'rs')

    def __init__(s):
        s.w = None
        s.rs = {}


class KB:
    def __init__(s, nc, stack):
        s.nc = nc
        s.st = stack
        s.cnt = {e: 0 for e in ENG}
        s.nsem = 0
        s.sem = {e: s.new_sem(f'e_{e}') for e in ENG}
        s.hist = {e: [] for e in ENG}
        s.waited = {e: {} for e in ENG}
        s.dsem = {}
        s.dtarget = {}
        s.dcount = {}
        s.n_inst = 0

    def new_sem(s, name):
        s.nsem += 1
        return s.st.enter_context(s.nc.semaphore(f'{name}_{s.nsem}'))

    def _waits(s, engine, r, w):
        deps = {}

        def add(tok):
            key = id(tok[0])
            if key not in deps or deps[key][1] < tok[1]:
                deps[key] = tok
        for u in r:
            if u.w is not None:
                add(u.w)
        for u in w:
            if u.w is not None:
                add(u.w)
            for tok in u.rs.values():
                add(tok)
        waits = []
        wd = s.waited[engine]
        for key, (sem, val, src) in deps.items():
            if engine == PE and src == PE:
                continue
            if wd.get(key, 0) >= val:
                continue
            wd[key] = val
            waits.append((sem, val))
        return waits

    def _emit(s, ename, waits, fn, inc):
        e = getattr(s.nc, ename)
        for sem, val in waits:
            e.wait_ge(sem, val)
        if fn is None:
            return
        ins = fn(e)
        if inc[1] is None:
            ins.then_inc(inc[0])
        else:
            ins.then_inc(inc[0], inc[1])
        s.n_inst += 1

    def op(s, engine, fn, r=(), w=()):
        waits = s._waits(engine, r, w)
        if s.cnt[engine] >= EPOCH:
            s.hist[engine].append((s.sem[engine], s.cnt[engine]))
            s.sem[engine] = s.new_sem(f'e_{engine}')
            s.cnt[engine] = 0
        s.cnt[engine] += 1
        sem = s.sem[engine]
        tok = (sem, s.cnt[engine], engine)
        s._emit(engine, waits, fn, (sem, 1))
        for u in r:
            u.rs[id(sem)] = tok
        for u in w:
            u.w = tok
            u.rs = {}
        return tok

    def dma(s, q, out, in_, r=(), w=(), **kw):
        waits = s._waits(q, r, w)
        if q not in s.dsem:
            s.dsem[q] = [s.new_sem(f'd_{q}{i}') for i in range(ND)]
            s.dtarget[q] = [0] * ND
            s.dcount[q] = 0
        i = s.dcount[q] % ND
        s.dcount[q] += 1
        sem = s.dsem[q][i]
        prev = s.dtarget[q][i]
        if prev > 0 and s.waited[q].get(id(sem), 0) < prev:
            s.waited[q][id(sem)] = prev
            waits.append((sem, prev))
        tgt = prev + 16
        s.dtarget[q][i] = tgt
        tok = (sem, tgt, 'dma')
        s._emit(q, waits, lambda e: e.dma_start(out=out, in_=in_, **kw), (sem, 16))
        for u in r:
            u.rs[id(sem)] = tok
        for u in w:
            u.w = tok
            u.rs = {}
        return tok

    def custom(s, engine, fn, inc_sem, inc_val, r=(), w=()):
        waits = s._waits(engine, r, w)
        tok = (inc_sem, inc_val, 'custom')
        s._emit(engine, waits, fn, (inc_sem, None))
        for u in r:
            u.rs[id(inc_sem)] = tok
        for u in w:
            u.w = tok
            u.rs = {}
        return tok

    def barrier(s):
        toks = []
        for e in ENG:
            for sem, c in s.hist[e]:
                toks.append((sem, c))
            if s.cnt[e] > 0:
                toks.append((s.sem[e], s.cnt[e]))
        for q in s.dsem:
            for i in range(ND):
                if s.dtarget[q][i] > 0:
                    toks.append((s.dsem[q][i], s.dtarget[q][i]))
        for e in ENG:
            wd = s.waited[e]
            waits = []
            for sem, val in toks:
                if wd.get(id(sem), 0) >= val:
                    continue
                wd[id(sem)] = val
                waits.append((sem, val))
            s._emit(e, waits, None, None)


D = 1024
T = 2048
TE = 2304
NT = 16
INW = 14848
OFF_CGLU, OFF_CGATE = 0, 2048
OFF_RQ, OFF_RK, OFF_RV, OFF_RG = 3072, 4096, 5120, 7168
OFF_AQ, OFF_AK, OFF_AV, OFF_AG = 9216, 10240, 10496, 10752
OFF_GL = 11776
EPS = 1e-6


def build(NL=2, dbg=False, phases='CTRF'):
    nc = bass.Bass("TRN2", target_bir_lowering=False)

    def din(name, shape):
        return nc.dram_tensor(name, shape, F32, kind="ExternalInput").ap()

    x_ext = din("x_ext", [TE, D])
    norm_g = din("norm_g", [2, D])
    w_in = din("w_in", [2, D, INW])
    b_gate = din("b_gate", [2, 3, D])
    conv_dw = din("conv_dw", [2, 31, D])
    conv_b = din("conv_b", [2, 1, D])
    conv_ln_g = din("conv_ln_g", [2, 1, D])
    conv_ln_b = din("conv_ln_b", [2, 1, D])
    ret_decay = din("ret_decay", [2, 8])
    q_norm_g = din("q_norm_g", [2, 64])
    k_norm_g = din("k_norm_g", [2, 64])
    attn_sink = din("attn_sink", [2, 16])
    w_conv_out = din("w_conv_out", [2, D, D])
    w_ret_out = din("w_ret_out", [2, 2 * D, D])
    w_attn_out = din("w_attn_out", [2, D, D])
    w_out = din("w_out", [2, D, D])
    c_cosr = din("c_cosr", [128, T])
    c_sinr = din("c_sinr", [128, T])
    c_cosa = din("c_cosa", [128, 18, 8])
    c_sina = din("c_sina", [128, 18, 8])
    c_mask = din("c_mask", [128, 512])
    c_ident = din("c_ident", [128, 128])
    c_eseg = din("c_eseg", [128, 32])
    c_ek = din("c_ek", [128, 2])
    c_eq = din("c_eq", [128, 256])
    c_em = din("c_em", [128, 512])
    c_dist = din("c_dist", [128, 8])
    c_sel = din("c_sel", [128, 8])
    c_e128 = din("c_e128", [128, 16])

    y_out = nc.dram_tensor("y_out", [T, D], F32, kind="ExternalOutput").ap()
    okind = "ExternalOutput" if dbg else "Internal"
    OC = nc.dram_tensor("OC", [8, 128, T], BF16, kind=okind).ap()
    OA = nc.dram_tensor("OA", [16, 64, T], BF16, kind=okind).ap()
    OR = nc.dram_tensor("OR", [16, 128, T], BF16, kind=okind).ap()
    x1e = nc.dram_tensor("x1e", [TE, D], F32, kind=okind).ap()
    fs_bounce = nc.dram_tensor("fs_bounce", [512, 512], F32)
    fs_gath = nc.dram_tensor("fs_gath", [2048, 512], F32)
    h_bounce = nc.dram_tensor("h_bounce", [256, D], F32)
    h_gath = nc.dram_tensor("h_gath", [1024, D], F32)
    u_OC, u_OA, u_OR, u_x1e, u_fsb, u_fsg, u_hb, u_hg, u_yout = [U() for _ in range(9)]
    RG = [[0, 1, 2, 3], [4, 5, 6, 7]]

    with ExitStack() as st0:
        k = KB(nc, st0)

        ncount = [0]

        def sb(st, name, shape, dt):
            ncount[0] += 1
            return st.enter_context(nc.sbuf_tensor(f"{name}_{ncount[0]}", shape, dt))

        PF = [st0.enter_context(nc.psum_tensor(f"pf{i}", [128, 512], F32)) for i in range(7)]
        uPF = [U() for _ in range(7)]
        PT = st0.enter_context(nc.psum_tensor("ptb", [128, 8, 128], BF16))
        uPT = U()
        hT = sb(st0, "hT", [128, 8, TE], BF16)
        uhT = [U() for _ in range(18)]
        ident = sb(st0, "ident", [128, 128], F32); identb = sb(st0, "identb", [128, 128], BF16)
        onesf = sb(st0, "onesf", [128, 128], F32); onesb = sb(st0, "onesb", [128, 128], BF16)
        maskb = sb(st0, "maskb", [128, 512], BF16)
        eseg = sb(st0, "eseg", [128, 32], F32); ek = sb(st0, "ek", [128, 2], F32)
        eq = sb(st0, "eq", [128, 256], F32); em = sb(st0, "em", [128, 512], F32)
        dist = sb(st0, "dist", [128, 8], F32); sel = sb(st0, "sel", [128, 8], F32)
        e128 = sb(st0, "e128", [128, 16], F32)
        cosa = sb(st0, "cosa", [128, 18, 8], F32); sina = sb(st0, "sina", [128, 18, 8], F32)
        lg = sb(st0, "lg", [128, 8], F32)
        qg = sb(st0, "qg", [128, 64], F32); kg = sb(st0, "kg", [128, 64], F32)
        snk = sb(st0, "snk", [128, 16], F32); snke = sb(st0, "snke", [128, 16], F32)
        negc = sb(st0, "negc", [128, 1], F32); tmpc = sb(st0, "tmpc", [128, 4], F32)
        g2 = sb(st0, "g2", [128, 128], F32)
        prm = sb(st0, "prm", [37, D], F32)
        prT = sb(st0, "prT", [128, 8, 37], F32)
        uC = U()
        uL = U()

        for (t, src) in ((ident, c_ident), (eseg, c_eseg), (ek, c_ek), (eq, c_eq), (em, c_em),
                         (dist, c_dist), (sel, c_sel), (cosa, c_cosa), (sina, c_sina), (e128, c_e128)):
            k.dma(S, t[:], src, w=[uC])
        k.dma(G, identb[:], c_ident, w=[uC])
        k.dma(G, maskb[:], c_mask, w=[uC])
        k.op(V, lambda e: e.memset(onesf[:], 1.0 / 1024.0), w=[uC])
        k.op(V, lambda e: e.memset(onesb[:], 1.0), w=[uC])
        k.barrier()

        def phase_T(l):
            with ExitStack() as st:
                Wt = sb(st, "Wt", [128, 8, 2560], BF16); uWt = U()
                for j in range(5):
                    k.dma(G, Wt[:, :, j * 512:(j + 1) * 512], w_in[l, :, OFF_AQ + j * 512:OFF_AQ + (j + 1) * 512].rearrange("(k p) c -> p k c", p=128), w=[uWt])
                qTg = sb(st, "qTg", [64, 16, 512], BF16); uqT = [U() for _ in range(4)]
                kT = sb(st, "kTres", [64, 4, TE], BF16); ukT = [U() for _ in range(18)]
                Vr = sb(st, "Vres", [128, 18, 256], BF16); uVr = [U() for _ in range(18)]
                sq = [sb(st, f"sq{i}", [128, 1280], F32) for i in range(2)]; usq = [U(), U()]
                ssa = [sb(st, f"ssa{i}", [128, 20], F32) for i in range(2)]; rsa = [sb(st, f"rsa{i}", [128, 20], F32) for i in range(2)]; urs = [U(), U()]
                qn = [sb(st, f"qn{i}", [128, 1280], F32) for i in range(2)]; uqn = [U(), U()]
                rt = [[sb(st, f"rt{p}{i}", [128, 20, 8], F32) for i in range(4)] for p in range(2)]; urt = [[U() for _ in range(4)] for _ in range(2)]
                qb = [sb(st, f"qb{i}", [128, 1024], BF16) for i in range(2)]; uqb = [U(), U()]
                kb = [sb(st, f"kb{i}", [128, 256], BF16) for i in range(2)]; ukb = [U(), U()]
                sq3 = [sq[p][:].rearrange("p (h d) -> p h d", d=64) for p in range(2)]
                qn3 = [qn[p][:].rearrange("p (h d) -> p h d", d=64) for p in range(2)]

                def norm_rot(t, h0, h1, p, banks):
                    nh = h1 - h0
                    k.op(V, lambda e: e.tensor_reduce(out=ssa[p][:, h0:h1], in_=sq3[p][:, h0:h1, :], axis=AX.X, op=ALU.add), r=[usq[p]], w=[urs[p]])
                    k.op(V, lambda e: e.tensor_scalar(out=rsa[p][:, h0:h1], in0=ssa[p][:, h0:h1], scalar1=1.0 / 64.0, scalar2=EPS, op0=ALU.mult, op1=ALU.add), r=[urs[p]], w=[urs[p]])
                    k.op(A, lambda e: e.activation(out=rsa[p][:, h0:h1], in_=rsa[p][:, h0:h1], func=AF.Sqrt), r=[urs[p]], w=[urs[p]])
                    k.op(V, lambda e: e.reciprocal(out=rsa[p][:, h0:h1], in_=rsa[p][:, h0:h1]), r=[urs[p]], w=[urs[p]])
                    if h0 == 0:
                        for half in range(2):
                            bk, ubk = banks[half]
                            k.op(V, lambda e, half=half, bk=bk: e.tensor_tensor(out=qn3[p][:, half * 8:(half + 1) * 8, :], in0=bk[:].rearrange("p (h d) -> p h d", d=64), in1=rsa[p][:, half * 8:(half + 1) * 8].unsqueeze(2).broadcast_to([128, 8, 64]), op=ALU.mult), r=[ubk, urs[p]], w=[uqn[p]])
                        k.op(G, lambda e: e.tensor_tensor(out=qn3[p][:, 0:16, :], in0=qn3[p][:, 0:16, :], in1=qg[:].unsqueeze(1).broadcast_to([128, 16, 64]), op=ALU.mult), r=[uqn[p], uL], w=[uqn[p]])
                    else:
                        bk, ubk = banks[0]
                        k.op(V, lambda e, bk=bk: e.tensor_tensor(out=qn3[p][:, 16:20, :], in0=bk[:, 0:256].rearrange("p (h d) -> p h d", d=64), in1=rsa[p][:, 16:20].unsqueeze(2).broadcast_to([128, 4, 64]), op=ALU.mult), r=[ubk, urs[p]], w=[uqn[p]])
                        k.op(G, lambda e: e.tensor_tensor(out=qn3[p][:, 16:20, :], in0=qn3[p][:, 16:20, :], in1=kg[:].unsqueeze(1).broadcast_to([128, 4, 64]), op=ALU.mult), r=[uqn[p], uL], w=[uqn[p]])
                    cb = cosa[:, t, :].unsqueeze(1).broadcast_to([128, nh, 8])
                    sbb = sina[:, t, :].unsqueeze(1).broadcast_to([128, nh, 8])
                    x1 = qn3[p][:, h0:h1, 0:8]
                    x2 = qn3[p][:, h0:h1, 8:16]
                    r_, ur_ = rt[p], urt[p]
                    k.op(V, lambda e: e.tensor_tensor(out=r_[0][:, h0:h1, :], in0=x1, in1=cb, op=ALU.mult), r=[uqn[p], uC], w=[ur_[0]])
                    k.op(G, lambda e: e.tensor_tensor(out=r_[1][:, h0:h1, :], in0=x2, in1=sbb, op=ALU.mult), r=[uqn[p], uC], w=[ur_[1]])
                    k.op(V, lambda e: e.tensor_tensor(out=r_[2][:, h0:h1, :], in0=x2, in1=cb, op=ALU.mult), r=[uqn[p], uC], w=[ur_[2]])
                    k.op(G, lambda e: e.tensor_tensor(out=r_[3][:, h0:h1, :], in0=x1, in1=sbb, op=ALU.mult), r=[uqn[p], uC], w=[ur_[3]])
                    k.op(V, lambda e: e.tensor_tensor(out=x1, in0=r_[0][:, h0:h1, :], in1=r_[1][:, h0:h1, :], op=ALU.subtract), r=[ur_[0], ur_[1], ur_[2], ur_[3]], w=[uqn[p]])
                    k.op(G, lambda e: e.tensor_tensor(out=x2, in0=r_[2][:, h0:h1, :], in1=r_[3][:, h0:h1, :], op=ALU.add), r=[ur_[2], ur_[3]], w=[uqn[p]])

                def kv_s1(t):
                    p = t % 2
                    bk, ubk = PF[2 + p], uPF[2 + p]
                    for kk in range(8):
                        k.op(PE, lambda e, kk=kk: e.matmul(bk[:], lhsT=hT[:, kk, t * 128:(t + 1) * 128], rhs=Wt[:, kk, 1024:1536], start=(kk == 0), stop=(kk == 7)), r=[uWt, uhT[t]], w=[ubk])
                    k.op(A, lambda e: e.activation(out=sq[p][:, 1024:1280], in_=bk[:, 0:256], func=AF.Square), r=[ubk], w=[usq[p]])
                    norm_rot(t, 16, 20, p, [(bk, ubk)])
                    k.op(A, lambda e: e.activation(out=kb[p][:], in_=qn[p][:, 1024:1280], func=AF.Copy), r=[uqn[p]], w=[ukb[p]])
                    k.op(A, lambda e: e.activation(out=Vr[:, t, :], in_=bk[:, 256:512], func=AF.Copy), r=[ubk], w=[uVr[t]])

                def kv_s2(t):
                    p = t % 2
                    for g in range(4):
                        k.op(PE, lambda e, g=g: e.transpose(out=PT[0:64, g, :], in_=kb[p][:, g * 64:(g + 1) * 64], identity=identb[:]), r=[ukb[p], uC], w=[uPT])
                    k.op(V, lambda e: e.tensor_copy(out=kT[:, :, t * 128:(t + 1) * 128], in_=PT[0:64, 0:4, :]), r=[uPT], w=[ukT[t]])
                kv_s1(0)
                for t in range(18):
                    if t < 17:
                        kv_s1(t + 1)
                    kv_s2(t)
                if dbg == 'T1':
                    k.barrier()
                    return
                gT = sb(st, "gT", [64, 16, 512], BF16); ugT = U()
                pt = [sb(st, f"pt{i}", [128, 512], BF16) for i in range(3)]; upt = [U() for _ in range(3)]
                den = sb(st, "den", [64, 512], F32); uden = U()
                rec = sb(st, "rec", [64, 512], F32); urec = U()
                on = sb(st, "on", [64, 512], F32); uon = U()
                on2 = [on, on]; uon2 = [uon, uon]
                pt_b = [sb(st, f"ptb{i}", [128, 512], BF16) for i in range(3)]; upt_b = [U() for _ in range(3)]
                pt2 = [pt, pt_b]; upt2 = [upt, upt_b]
                oaT = sb(st, "oaT", [64, 16, 512], BF16); uoaT = U()
                for tg in range(4):
                    e0 = 128 + tg * 512
                    hr = uhT[e0 // 128:(e0 + 512) // 128]
                    def q_s1(nn):
                        p = nn % 2
                        t = tg * 4 + nn + 1
                        banks = [(PF[2 * p], uPF[2 * p]), (PF[2 * p + 1], uPF[2 * p + 1])]
                        for half in range(2):
                            bk, ubk = banks[half]
                            for kk in range(8):
                                k.op(PE, lambda e, half=half, kk=kk, bk=bk: e.matmul(bk[:], lhsT=hT[:, kk, t * 128:(t + 1) * 128], rhs=Wt[:, kk, half * 512:(half + 1) * 512], start=(kk == 0), stop=(kk == 7)), r=[uWt, uhT[t]], w=[ubk])
                            k.op(A, lambda e, half=half, bk=bk: e.activation(out=sq[p][:, half * 512:(half + 1) * 512], in_=bk[:], func=AF.Square), r=[ubk], w=[usq[p]])
                        norm_rot(t, 0, 16, p, banks)
                        k.op(A, lambda e: e.activation(out=qb[p][:], in_=qn[p][:, 0:1024], func=AF.Copy), r=[uqn[p]], w=[uqb[p]])

                    def q_s2(nn):
                        p = nn % 2
                        for rr in range(2):
                            for j in range(8):
                                hh = rr * 8 + j
                                k.op(PE, lambda e, j=j, hh=hh: e.transpose(out=PT[0:64, j, :], in_=qb[p][:, hh * 64:(hh + 1) * 64], identity=identb[:]), r=[uqb[p], uC], w=[uPT])
                            k.op(V, lambda e, rr=rr: e.tensor_copy(out=qTg[:, rr * 8:(rr + 1) * 8, nn * 128:(nn + 1) * 128], in_=PT[0:64, :, :]), r=[uPT], w=[uqT[nn]])
                    q_s1(0)
                    for nn in range(4):
                        if nn < 3:
                            q_s1(nn + 1)
                        q_s2(nn)
                    for hh in range(16):
                        ps, ups = PF[hh % 2], uPF[hh % 2]
                        for kk in range(8):
                            k.op(PE, lambda e, ps=ps, hh=hh, kk=kk, e0=e0: e.matmul(ps[0:64, :], lhsT=Wt[:, kk, 1536 + hh * 64:1536 + (hh + 1) * 64], rhs=hT[:, kk, e0:e0 + 512], start=(kk == 0), stop=(kk == 7)), r=[uWt] + hr, w=[ups])
                        k.op(A, lambda e, ps=ps, hh=hh: e.activation(out=gT[:, hh, :], in_=ps[0:64, :], func=AF.Silu), r=[ups], w=[ugT])
                    def t2_front(idx, nn, g):
                        n = tg * 4 + nn
                        t = n + 1
                        ptc, uptc = pt2[idx % 2], upt2[idx % 2]
                        for mi, m in enumerate((t - 1, t, t + 1)):
                            k.op(PE, lambda e, mi=mi, m=m: e.matmul(PF[2 + mi][:], lhsT=kT[:, g, m * 128:(m + 1) * 128], rhs=qTg[:, 4 * g:4 * g + 4, nn * 128:(nn + 1) * 128], start=True, stop=True), r=[ukT[m], uqT[nn]], w=[uPF[2 + mi]])
                            k.op(A, lambda e, mi=mi: e.activation(out=ptc[mi][:], in_=PF[2 + mi][:], func=AF.Exp, bias=negc[:, 0:1], scale=0.125), r=[uPF[2 + mi], uL], w=[uptc[mi]])
                        mL = 256 if t == 1 else 0
                        mR = 384 if t == 16 else 128
                        k.op(V, lambda e: e.tensor_tensor(out=ptc[0][:].rearrange("p (a i) -> p a i", a=4), in0=ptc[0][:].rearrange("p (a i) -> p a i", a=4), in1=maskb[:, mL:mL + 128].unsqueeze(1).broadcast_to([128, 4, 128]), op=ALU.mult), r=[uptc[0], uC], w=[uptc[0]])
                        k.op(G, lambda e: e.tensor_tensor(out=ptc[2][:].rearrange("p (a i) -> p a i", a=4), in0=ptc[2][:].rearrange("p (a i) -> p a i", a=4), in1=maskb[:, mR:mR + 128].unsqueeze(1).broadcast_to([128, 4, 128]), op=ALU.mult), r=[uptc[2], uC], w=[uptc[2]])

                    def t2_back(idx, nn, g):
                        n = tg * 4 + nn
                        t = n + 1
                        ptc, uptc = pt2[idx % 2], upt2[idx % 2]
                        onc, uonc = on2[idx % 2], uon2[idx % 2]
                        for mi, m in enumerate((t - 1, t, t + 1)):
                            k.op(PE, lambda e, mi=mi, m=m: e.matmul(PF[5][0:64, :], lhsT=Vr[:, m, g * 64:(g + 1) * 64], rhs=ptc[mi][:], start=(mi == 0), stop=(mi == 2)), r=[uVr[m], uptc[mi]], w=[uPF[5]])
                        for mi, m in enumerate((t - 1, t, t + 1)):
                            k.op(PE, lambda e, mi=mi: e.matmul(PF[6][0:64, :], lhsT=onesb[:, 0:64], rhs=ptc[mi][:], start=(mi == 0), stop=(mi == 2)), r=[uC, uptc[mi]], w=[uPF[6]])
                        for ei in range(4):
                            hd = 4 * g + ei
                            k.op(A, lambda e, ei=ei, hd=hd: e.activation(out=den[:, ei * 128:(ei + 1) * 128], in_=PF[6][0:64, ei * 128:(ei + 1) * 128], func=AF.Ln, bias=snke[0:64, hd:hd + 1], scale=1.0), r=[uPF[6], uL], w=[uden])
                        k.op(A, lambda e: e.activation(out=rec[:], in_=den[:], func=AF.Exp, scale=-1.0), r=[uden], w=[urec])
                        k.op(V, lambda e: e.tensor_tensor(out=onc[:], in0=PF[5][0:64, :], in1=rec[:], op=ALU.mult), r=[uPF[5], urec], w=[uonc])
                        k.op(G, lambda e: e.tensor_tensor(out=oaT[:, 4 * g:4 * g + 4, nn * 128:(nn + 1) * 128], in0=onc[:].rearrange("p (a i) -> p a i", a=4), in1=gT[:, 4 * g:4 * g + 4, nn * 128:(nn + 1) * 128], op=ALU.mult), r=[uonc, ugT], w=[uoaT])
                    its = [(nn, g) for nn in range(4) for g in range(4)]
                    for idx, (nn, g) in enumerate(its):
                        t2_front(idx, nn, g)
                        if idx > 0:
                            t2_back(idx - 1, *its[idx - 1])
                    t2_back(len(its) - 1, *its[-1])
                    k.dma(S, OA[:, :, tg * 512:(tg + 1) * 512].rearrange("h d t -> d h t"), oaT[:], r=[uoaT], w=[u_OA])
                k.barrier()

        def phase_R(l):
            with ExitStack() as st:
                Wr = sb(st, "Wr", [128, 8, 2048], BF16); uWr = U()
                qT = sb(st, "rqT", [128, 2, T], BF16); uq = [U() for _ in range(4)]
                kT = sb(st, "rkT", [128, 2, T], BF16); ukk = [U() for _ in range(4)]
                Vv = sb(st, "rV", [128, 16, 512], BF16); uV = [U() for _ in range(16)]
                kt = sb(st, "rkt", [128, 16, 256], BF16); ukt = [U() for _ in range(16)]
                Pp = sb(st, "rP", [128, 16, 512], F32); uP = [U() for _ in range(16)]
                cs = sb(st, "rcs", [128, 2, 512], F32); ucs = U()
                tq = [sb(st, f"rtq{i}", [128, 512], F32) for i in range(4)]; utq = [U() for _ in range(4)]
                Sf = sb(st, "Sf", [128, 2, 512], F32); Sb = sb(st, "Sb", [128, 2, 512], F32); uSf = U(); uSb = U()
                Sfb = sb(st, "Sfb", [128, 2, 512], BF16); Sbb = sb(st, "Sbb", [128, 2, 512], BF16); uSfb = U(); uSbb = U()
                fsum = sb(st, "fsum", [128, 4, 512], F32); ufs = U()
                DTm = sb(st, "DTm", [128, 128], F32); dq = sb(st, "dq", [128, 256], F32); dsg = sb(st, "dsg", [128, 32], F32)
                dk = sb(st, "dk", [128, 2], F32); gC = sb(st, "gC", [128, 2], F32); wr = sb(st, "wr", [128, 8], F32)
                tmpD = sb(st, "tmpD", [128, 256], F32); uH = U()
                kfs = sb(st, "kfs", [128, 256], BF16); kbs = sb(st, "kbs", [128, 256], BF16); ukfs = U(); ukbs = U()
                qd = sb(st, "qd", [128, 2, 128], BF16); uqd = U()
                kdc = sb(st, "kdc", [128, 256], BF16); ukdc = U()
                sd = sb(st, "sd", [128, 128], BF16); usd = U()
                oo = sb(st, "oo", [128, 512], F32); uoo = U()
                onn = sb(st, "onn", [128, 512], F32); uonn = U()
                sgg = sb(st, "sgg", [128, 512], F32); usgg = U()
                og = sb(st, "og", [128, 512], BF16); uog = U()
                og_b = sb(st, "og_b", [128, 512], BF16); uog_b = U()
                qd_b = sb(st, "qd_b", [128, 2, 128], BF16); uqd_b = U()
                gn = sb(st, "gn", [128, 32], F32)
                bst = sb(st, "bst", [128, 6], F32); mv = sb(st, "mv", [128, 2], F32); ubn = U()
                orT = sb(st, "orT", [128, 4, 512], BF16); uorT = U()
                RSTOP = int(os.environ.get("RSTOP", "0"))

                class _Stop(Exception):
                    pass

                def ck(n_):
                    if RSTOP == n_:
                        raise _Stop()
                try:
                  for h in range(4):
                      wl = [(1024, OFF_RV + h * 512), (1536, OFF_RG + h * 512)]
                      if h % 2 == 0:
                          wl = [(0, OFF_RQ + h * 256), (512, OFF_RK + h * 256)] + wl
                      for (dst, off) in wl:
                          k.dma(G, Wr[:, :, dst:dst + 512], w_in[l, :, off:off + 512].rearrange("(k p) c -> p k c", p=128), w=[uWr])
                      qc0 = (h % 2) * 256
                      kc0 = 512 + (h % 2) * 256
                      lgf = lg[:, h:h + 1]
                      lgb = lg[:, 4 + h:5 + h]
                      hc = dict(r=[uL, uC, uH], w=[uH])
                      k.op(A, lambda e: e.activation(out=dsg[:, 0:16], in_=eseg[:, 0:16], func=AF.Exp, scale=lgf), **hc)
                      k.op(A, lambda e: e.activation(out=dsg[:, 16:32], in_=eseg[:, 16:32], func=AF.Exp, scale=lgb), **hc)
                      k.op(A, lambda e: e.activation(out=dk[:, 0:1], in_=ek[:, 0:1], func=AF.Exp, scale=lgf), **hc)
                      k.op(A, lambda e: e.activation(out=dk[:, 1:2], in_=ek[:, 1:2], func=AF.Exp, scale=lgb), **hc)
                      k.op(A, lambda e: e.activation(out=dq[:, 0:128], in_=eq[:, 0:128], func=AF.Exp, scale=lgf), **hc)
                      k.op(A, lambda e: e.activation(out=dq[:, 128:256], in_=eq[:, 128:256], func=AF.Exp, scale=lgb), **hc)
                      k.op(V, lambda e: e.tensor_scalar_mul(out=dq[:], in0=dq[:], scalar1=1.0 / 16.0), **hc)
                      k.op(A, lambda e: e.activation(out=gn[:, 0:16], in_=e128[:], func=AF.Exp, scale=lgf), **hc)
                      k.op(A, lambda e: e.activation(out=gn[:, 16:32], in_=e128[:], func=AF.Exp, scale=lgb), **hc)
                      k.op(A, lambda e: e.activation(out=gC[:, 0:1], in_=lgf, func=AF.Exp, scale=128.0), **hc)
                      k.op(A, lambda e: e.activation(out=gC[:, 1:2], in_=lgb, func=AF.Exp, scale=128.0), **hc)
                      k.op(A, lambda e: e.activation(out=wr[:, 0:4], in_=dist[:, 0:4], func=AF.Exp, scale=lgf), **hc)
                      k.op(A, lambda e: e.activation(out=wr[:, 4:8], in_=dist[:, 4:8], func=AF.Exp, scale=lgb), **hc)
                      k.op(A, lambda e: e.activation(out=tmpD[:, 0:128], in_=em[:, 0:128], func=AF.Exp, scale=lgf), **hc)
                      k.op(A, lambda e: e.activation(out=tmpD[:, 128:256], in_=em[:, 128:256], func=AF.Exp, scale=lgb), **hc)
                      k.op(V, lambda e: e.tensor_tensor(out=tmpD[:], in0=tmpD[:], in1=em[:, 256:512], op=ALU.mult), **hc)
                      k.op(V, lambda e: e.tensor_tensor(out=DTm[:], in0=tmpD[:, 0:128], in1=tmpD[:, 128:256], op=ALU.add), **hc)
                      k.op(V, lambda e: e.tensor_scalar_mul(out=DTm[:], in0=DTm[:], scalar1=1.0 / 16.0), **hc)
                      ck(1)
                      for tg in range(4):
                          e0 = 128 + tg * 512
                          hr = uhT[e0 // 128:(e0 + 512) // 128]
                          k.dma(S, cs[:, 0, :], c_cosr[:, tg * 512:(tg + 1) * 512], w=[ucs])
                          k.dma(S, cs[:, 1, :], c_sinr[:, tg * 512:(tg + 1) * 512], w=[ucs])
                          for (col0, dstT, ud) in ((qc0, qT, uq[tg]), (kc0, kT, ukk[tg])):
                              for dc in range(2):
                                  for kk in range(8):
                                      k.op(PE, lambda e, dc=dc, kk=kk, col0=col0, e0=e0: e.matmul(PF[dc][:], lhsT=Wr[:, kk, col0 + dc * 128:col0 + (dc + 1) * 128], rhs=hT[:, kk, e0:e0 + 512], start=(kk == 0), stop=(kk == 7)), r=[uWr] + hr, w=[uPF[dc]])
                              k.op(V, lambda e: e.tensor_tensor(out=tq[0][:], in0=PF[0][:], in1=cs[:, 0, :], op=ALU.mult), r=[uPF[0], ucs], w=[utq[0]])
                              k.op(V, lambda e: e.tensor_tensor(out=tq[1][:], in0=PF[1][:], in1=cs[:, 1, :], op=ALU.mult), r=[uPF[1], ucs], w=[utq[1]])
                              k.op(V, lambda e: e.tensor_tensor(out=tq[2][:], in0=PF[1][:], in1=cs[:, 0, :], op=ALU.mult), r=[uPF[1], ucs], w=[utq[2]])
                              k.op(V, lambda e: e.tensor_tensor(out=tq[3][:], in0=PF[0][:], in1=cs[:, 1, :], op=ALU.mult), r=[uPF[0], ucs], w=[utq[3]])
                              k.op(V, lambda e, dstT=dstT, tg=tg: e.tensor_tensor(out=dstT[:, 0, tg * 512:(tg + 1) * 512], in0=tq[0][:], in1=tq[1][:], op=ALU.subtract), r=[utq[0], utq[1]], w=[ud])
                              k.op(G, lambda e, dstT=dstT, tg=tg: e.tensor_tensor(out=dstT[:, 1, tg * 512:(tg + 1) * 512], in0=tq[2][:], in1=tq[3][:], op=ALU.add), r=[utq[2], utq[3]], w=[ud])
                          ck(2)
                          for nn in range(4):
                              n = tg * 4 + nn
                              t = n + 1
                              for kk in range(8):
                                  k.op(PE, lambda e, kk=kk, t=t: e.matmul(PF[2][:], lhsT=hT[:, kk, t * 128:(t + 1) * 128], rhs=Wr[:, kk, 1024:1536], start=(kk == 0), stop=(kk == 7)), r=[uWr, uhT[t]], w=[uPF[2]])
                              k.op(A, lambda e, n=n: e.activation(out=Vv[:, n, :], in_=PF[2][:], func=AF.Copy), r=[uPF[2]], w=[uV[n]])
                              ck(3)
                              for dc in range(2):
                                  k.op(PE, lambda e, dc=dc, n=n: e.transpose(out=PT[:, dc, :], in_=kT[:, dc, n * 128:(n + 1) * 128], identity=identb[:]), r=[ukk[tg], uC], w=[uPT])
                              ptv = PT[:, 0:2, :]
                              k.op(A, lambda e, n=n, ptv=ptv: e.activation(out=kfs[:].rearrange("p (a d) -> p a d", a=2), in_=ptv, func=AF.Copy, scale=dsg[:, n:n + 1]), r=[uPT, uH], w=[ukfs])
                              k.op(A, lambda e, n=n, ptv=ptv: e.activation(out=kbs[:].rearrange("p (a d) -> p a d", a=2), in_=ptv, func=AF.Copy, scale=dsg[:, 16 + n:17 + n]), r=[uPT, uH], w=[ukbs])
                              k.op(A, lambda e, n=n, ptv=ptv: e.activation(out=kt[:, n, :].rearrange("p (a d) -> p a d", a=2), in_=ptv, func=AF.Copy), r=[uPT], w=[ukt[n]])
                              ck(4)
                              for dc in range(2):
                                  k.op(PE, lambda e, dc=dc, n=n: e.matmul(PF[3 + dc][:], lhsT=kfs[:, dc * 128:(dc + 1) * 128], rhs=Vv[:, n, :], start=(n == 0), stop=(n == 15)), r=[ukfs, uV[n]], w=[uPF[3 + dc]])
                                  k.op(PE, lambda e, dc=dc, n=n: e.matmul(PF[5 + dc][:], lhsT=kbs[:, dc * 128:(dc + 1) * 128], rhs=Vv[:, n, :], start=(n == 0), stop=(n == 15)), r=[ukbs, uV[n]], w=[uPF[5 + dc]])
                      if dbg == 'R0':
                          k.barrier()
                          return
                      for j in range(4):
                          k.op(A, lambda e, j=j: e.activation(out=fsum[:, j, :], in_=PF[3 + j][:], func=AF.Copy), r=[uPF[3 + j]], w=[ufs])
                      k.dma(S, fs_bounce.ap().rearrange("(j p) v -> p j v", p=128), fsum[:], r=[ufs], w=[u_fsb])
                      ccs = k.new_sem("cc")
                      k.custom(G, lambda e: e.collective_compute("AllGather", ALU.bypass, replica_groups=RG, ins=[fs_bounce.ap().opt()], outs=[fs_gath.ap().opt()]), ccs, 1, r=[u_fsb], w=[u_fsg])
                      k.op(V, lambda e: e.memset(Sf[:], 0.0), r=[uSf], w=[uSf])
                      k.op(V, lambda e: e.memset(Sb[:], 0.0), r=[uSb], w=[uSb])
                      k.op(V, lambda e: e.memset(Sfb[:], 0.0), r=[uSfb], w=[uSfb])
                      k.op(V, lambda e: e.memset(Sbb[:], 0.0), r=[uSbb], w=[uSbb])
                      for n in range(15, -1, -1):
                          tg = n // 4
                          k.op(A, lambda e, n=n: e.activation(out=kdc[:], in_=kt[:, n, :], func=AF.Copy, scale=dk[:, 1:2]), r=[ukt[n], uH], w=[ukdc])
                          for dc in range(2):
                              k.op(PE, lambda e, dc=dc, n=n: e.matmul(PF[1 + dc][:], lhsT=kdc[:, dc * 128:(dc + 1) * 128], rhs=Vv[:, n, :], start=True, stop=True), r=[ukdc, uV[n]], w=[uPF[1 + dc]])
                          k.op(V, lambda e, n=n: e.tensor_tensor(out=qd[:], in0=qT[:, :, n * 128:(n + 1) * 128], in1=dq[:, 128:256].unsqueeze(1).broadcast_to([128, 2, 128]), op=ALU.mult), r=[uq[tg], uH], w=[uqd])
                          for dc in range(2):
                              k.op(PE, lambda e, dc=dc: e.matmul(PF[0][:], lhsT=qd[:, dc, :], rhs=Sbb[:, dc, :], start=(dc == 0), stop=(dc == 1)), r=[uqd, uSbb], w=[uPF[0]])
                          for dc in range(2):
                              k.op(V, lambda e, dc=dc: e.scalar_tensor_tensor(out=Sb[:, dc, :], in0=Sb[:, dc, :], scalar=gC[:, 1:2], in1=PF[1 + dc][:], op0=ALU.mult, op1=ALU.add), r=[uPF[1 + dc], uH, uSb], w=[uSb])
                          k.op(A, lambda e: e.activation(out=Sbb[:], in_=Sb[:], func=AF.Copy), r=[uSb], w=[uSbb])
                          k.op(A, lambda e, n=n: e.activation(out=Pp[:, n, :], in_=PF[0][:], func=AF.Copy), r=[uPF[0]], w=[uP[n]])
                      k.op(V, lambda e: e.memset(Sbb[:], 0.0), r=[uSbb], w=[uSbb])
                      Sfb2 = [Sfb, Sbb]
                      uSfb2 = [uSfb, uSbb]
                      for n in range(16):
                          tg = n // 4
                          cur, nxt = n % 2, (n + 1) % 2
                          k.op(A, lambda e, n=n: e.activation(out=kdc[:], in_=kt[:, n, :], func=AF.Copy, scale=dk[:, 0:1]), r=[ukt[n], uH], w=[ukdc])
                          for dc in range(2):
                              k.op(PE, lambda e, dc=dc, n=n: e.matmul(PF[1 + dc][:], lhsT=kdc[:, dc * 128:(dc + 1) * 128], rhs=Vv[:, n, :], start=True, stop=True), r=[ukdc, uV[n]], w=[uPF[1 + dc]])
                          for dc in range(2):
                              k.op(PE, lambda e, dc=dc, n=n: e.matmul(PF[3][:, 0:128], lhsT=kT[:, dc, n * 128:(n + 1) * 128], rhs=qT[:, dc, n * 128:(n + 1) * 128], start=(dc == 0), stop=(dc == 1)), r=[ukk[tg], uq[tg]], w=[uPF[3]])
                          k.op(V, lambda e: e.tensor_tensor(out=sd[:], in0=PF[3][:, 0:128], in1=DTm[:], op=ALU.mult), r=[uPF[3], uH], w=[usd])
                          k.op(G, lambda e, n=n: e.tensor_tensor(out=qd[:], in0=qT[:, :, n * 128:(n + 1) * 128], in1=dq[:, 0:128].unsqueeze(1).broadcast_to([128, 2, 128]), op=ALU.mult), r=[uq[tg], uH], w=[uqd])
                          k.op(PE, lambda e, n=n: e.matmul(PF[4][:], lhsT=sd[:], rhs=Vv[:, n, :], start=True, stop=False), r=[usd, uV[n]], w=[uPF[4]])
                          for dc in range(2):
                              k.op(PE, lambda e, dc=dc, cur=cur: e.matmul(PF[4][:], lhsT=qd[:, dc, :], rhs=Sfb2[cur][:, dc, :], start=False, stop=(dc == 1)), r=[uqd, uSfb2[cur]], w=[uPF[4]])
                          for dc in range(2):
                              k.op(V, lambda e, dc=dc: e.scalar_tensor_tensor(out=Sf[:, dc, :], in0=Sf[:, dc, :], scalar=gC[:, 0:1], in1=PF[1 + dc][:], op0=ALU.mult, op1=ALU.add), r=[uPF[1 + dc], uH, uSf], w=[uSf])
                          k.op(A, lambda e, nxt=nxt: e.activation(out=Sfb2[nxt][:], in_=Sf[:], func=AF.Copy), r=[uSf], w=[uSfb2[nxt]])
                          k.op(V, lambda e, n=n: e.tensor_tensor(out=Pp[:, n, :], in0=PF[4][:], in1=Pp[:, n, :], op=ALU.add), r=[uPF[4], uP[n]], w=[uP[n]])
                      for r_ in range(4):
                          k.dma(S, fsum[:], fs_gath.ap()[r_ * 512:(r_ + 1) * 512, :].rearrange("(j p) v -> p j v", p=128), r=[u_fsg], w=[ufs])
                          for dirn, (Sx, uSx) in enumerate(((Sf, uSf), (Sb, uSb))):
                              for dc in range(2):
                                  wcol = wr[:, dirn * 4 + r_:dirn * 4 + r_ + 1]
                                  if r_ == 0:
                                      k.op(V, lambda e, Sx=Sx, dc=dc, dirn=dirn, wcol=wcol: e.tensor_scalar_mul(out=Sx[:, dc, :], in0=fsum[:, dirn * 2 + dc, :], scalar1=wcol), r=[ufs, uH], w=[uSx])
                                  else:
                                      k.op(V, lambda e, Sx=Sx, dc=dc, dirn=dirn, wcol=wcol: e.scalar_tensor_tensor(out=Sx[:, dc, :], in0=fsum[:, dirn * 2 + dc, :], scalar=wcol, in1=Sx[:, dc, :], op0=ALU.mult, op1=ALU.add), r=[ufs, uH, uSx], w=[uSx])
                      k.op(A, lambda e: e.activation(out=Sfb[:], in_=Sf[:], func=AF.Copy), r=[uSf], w=[uSfb])
                      k.op(A, lambda e: e.activation(out=Sbb[:], in_=Sb[:], func=AF.Copy), r=[uSb], w=[uSbb])
                      og2 = [og, og_b]
                      uog2 = [uog, uog_b]
                      qd2 = [qd, qd_b]
                      uqd2 = [uqd, uqd_b]

                      def finish(n):
                          tg = n // 4
                          for vc in range(4):
                              k.op(PE, lambda e, vc=vc, n=n: e.transpose(out=PT[:, vc, :], in_=og2[n % 2][:, vc * 128:(vc + 1) * 128], identity=identb[:]), r=[uog2[n % 2], uC], w=[uPT])
                          k.op(A, lambda e, n=n: e.activation(out=orT[:, :, (n % 4) * 128:(n % 4 + 1) * 128], in_=PT[:, 0:4, :], func=AF.Copy), r=[uPT], w=[uorT])
                          if n % 4 == 3:
                              k.dma(S, OR[h * 4:(h + 1) * 4, :, tg * 512:(tg + 1) * 512].rearrange("c p t -> p c t"), orT[:], r=[uorT], w=[u_OR])
                      for n in range(16):
                          tg = n // 4
                          t = n + 1
                          cur = n % 2
                          for kk in range(8):
                              k.op(PE, lambda e, kk=kk, t=t: e.matmul(PF[5][:], lhsT=hT[:, kk, t * 128:(t + 1) * 128], rhs=Wr[:, kk, 1536:2048], start=(kk == 0), stop=(kk == 7)), r=[uWr, uhT[t]], w=[uPF[5]])
                          k.op(A, lambda e: e.activation(out=sgg[:], in_=PF[5][:], func=AF.Silu), r=[uPF[5]], w=[usgg])
                          k.op(V, lambda e, n=n: e.scalar_tensor_tensor(out=qd2[0][:], in0=qT[:, :, n * 128:(n + 1) * 128], scalar=gn[:, n:n + 1], in1=dq[:, 0:128].unsqueeze(1).broadcast_to([128, 2, 128]), op0=ALU.mult, op1=ALU.mult), r=[uq[tg], uH], w=[uqd2[0]])
                          k.op(V, lambda e, n=n: e.scalar_tensor_tensor(out=qd2[1][:], in0=qT[:, :, n * 128:(n + 1) * 128], scalar=gn[:, 16 + 15 - n:16 + 16 - n], in1=dq[:, 128:256].unsqueeze(1).broadcast_to([128, 2, 128]), op0=ALU.mult, op1=ALU.mult), r=[uq[tg], uH], w=[uqd2[1]])
                          for di, Sxb, uSxb in ((0, Sfb, uSfb), (1, Sbb, uSbb)):
                              for dc in range(2):
                                  k.op(PE, lambda e, di=di, dc=dc, Sxb=Sxb: e.matmul(PF[4][:], lhsT=qd2[di][:, dc, :], rhs=Sxb[:, dc, :], start=(di == 0 and dc == 0), stop=(di == 1 and dc == 1)), r=[uqd2[di], uSxb], w=[uPF[4]])
                          if n > 0:
                              finish(n - 1)
                          k.op(V, lambda e, n=n: e.tensor_tensor(out=oo[:], in0=PF[4][:], in1=Pp[:, n, :], op=ALU.add), r=[uPF[4], uP[n]], w=[uoo])
                          k.op(V, lambda e: e.bn_stats(out=bst[:], in_=oo[:]), r=[uoo], w=[ubn])
                          k.op(V, lambda e: e.bn_aggr(out=mv[:], in_=bst[:]), r=[ubn], w=[ubn])
                          k.op(V, lambda e: e.tensor_scalar_add(out=mv[:, 1:2], in0=mv[:, 1:2], scalar1=EPS), r=[ubn], w=[ubn])
                          k.op(A, lambda e: e.activation(out=mv[:, 1:2], in_=mv[:, 1:2], func=AF.Sqrt), r=[ubn], w=[ubn])
                          k.op(V, lambda e: e.reciprocal(out=mv[:, 1:2], in_=mv[:, 1:2]), r=[ubn], w=[ubn])
                          k.op(V, lambda e: e.tensor_scalar(out=onn[:], in0=oo[:], scalar1=mv[:, 0:1], scalar2=mv[:, 1:2], op0=ALU.subtract, op1=ALU.mult), r=[uoo, ubn], w=[uonn])
                          k.op(G, lambda e, cur=cur: e.tensor_tensor(out=og2[cur][:], in0=onn[:], in1=sgg[:], op=ALU.mult), r=[uonn, usgg], w=[uog2[cur]])
                      finish(15)
                except _Stop:
                    pass
                k.barrier()

        def phase_F(l, xsrc, last):
            with ExitStack() as st:
                mg = sb(st, "mg", [128, 8, T], F32); umg = [U() for _ in range(4)]
                sgm = sb(st, "sgm", [128, 512], F32); usgm = U()
                tmpm = sb(st, "tmpm", [128, 512], F32); utmpm = U()
                for bi, (nk, kp) in enumerate(((8, 128), (16, 128), (16, 64))):
                    with ExitStack() as st2:
                        Wb = sb(st2, f"Wb{bi}", [kp, nk, D], BF16); uWb = U()
                        Wg = sb(st2, f"Wg{bi}", [128, 8, D], BF16); uWg = U()
                        ob = sb(st2, f"ob{bi}", [kp, nk, 512], BF16); uob = U()
                        src_w = (w_conv_out, w_ret_out, w_attn_out)[bi]
                        if bi == 2:
                            view = src_w[l].rearrange("(h d) c -> d h c", d=64)
                        else:
                            view = src_w[l].rearrange("(k p) c -> p k c", p=128)
                        for j in range(0, nk, 4):
                            k.dma(G, Wb[:, j:j + 4, :], view[:, j:j + 4, :], w=[uWb])
                        for j in range(2):
                            c0 = OFF_GL + bi * 1024 + j * 512
                            k.dma(G, Wg[:, :, j * 512:(j + 1) * 512], w_in[l, :, c0:c0 + 512].rearrange("(k p) c -> p k c", p=128), w=[uWg])
                        osrc = (OC, OR, OA)[bi]
                        uos = (u_OC, u_OR, u_OA)[bi]
                        for tg in range(4):
                            e0 = 128 + tg * 512
                            hr = uhT[e0 // 128:(e0 + 512) // 128]
                            k.dma(S, ob[:], osrc[:, :, tg * 512:(tg + 1) * 512].rearrange("c p t -> p c t"), r=[uos], w=[uob])
                            for m in range(8):
                                py, upy = PF[m % 3], uPF[m % 3]
                                pg, upg = PF[3 + m % 3], uPF[3 + m % 3]
                                for j in range(nk):
                                    k.op(PE, lambda e, py=py, j=j, m=m: e.matmul(py[:], lhsT=Wb[:, j, m * 128:(m + 1) * 128], rhs=ob[:, j, :], start=(j == 0), stop=(j == nk - 1)), r=[uWb, uob], w=[upy])
                                for kk in range(8):
                                    k.op(PE, lambda e, pg=pg, kk=kk, m=m, e0=e0: e.matmul(pg[:], lhsT=Wg[:, kk, m * 128:(m + 1) * 128], rhs=hT[:, kk, e0:e0 + 512], start=(kk == 0), stop=(kk == 7)), r=[uWg] + hr, w=[upg])
                                k.op(A, lambda e, pg=pg, m=m, bi=bi: e.activation(out=sgm[:], in_=pg[:], func=AF.Sigmoid, bias=prT[:, m, 34 + bi:35 + bi], scale=1.0), r=[upg, uL], w=[usgm])
                                if bi == 0:
                                    k.op(V, lambda e, py=py, m=m, tg=tg: e.tensor_tensor(out=mg[:, m, tg * 512:(tg + 1) * 512], in0=py[:], in1=sgm[:], op=ALU.mult), r=[upy, usgm], w=[umg[tg]])
                                else:
                                    k.op(V, lambda e, py=py: e.tensor_tensor(out=tmpm[:], in0=py[:], in1=sgm[:], op=ALU.mult), r=[upy, usgm], w=[utmpm])
                                    k.op(G, lambda e, m=m, tg=tg: e.tensor_tensor(out=mg[:, m, tg * 512:(tg + 1) * 512], in0=mg[:, m, tg * 512:(tg + 1) * 512], in1=tmpm[:], op=ALU.add), r=[utmpm, umg[tg]], w=[umg[tg]])
                        k.barrier()
                with ExitStack() as st2:
                    Wo = sb(st2, "Wo", [128, 8, D], BF16); uWo = U()
                    for j in range(2):
                        k.dma(G, Wo[:, j * 4:(j + 1) * 4, :], w_out[l].rearrange("(k p) c -> p k c", p=128)[:, j * 4:(j + 1) * 4, :], w=[uWo])
                    mb = sb(st2, "mb", [128, 8, 128], BF16); umb = U()
                    xt = [sb(st2, f"fxt{i}", [128, D], F32) for i in range(2)]; uxt = [U(), U()]
                    xn = [sb(st2, f"fxn{i}", [128, D], F32) for i in range(2)]; uxn = [U(), U()]
                    for n in range(16):
                        b = n % 2
                        k.op(A, lambda e, n=n: e.activation(out=mb[:], in_=mg[:, :, n * 128:(n + 1) * 128], func=AF.Copy), r=[umg[n // 4]], w=[umb])
                        k.dma(S, xt[b][:], xsrc[128 + n * 128:128 + (n + 1) * 128, :], r=[u_x1e], w=[uxt[b]])
                        for half in range(2):
                            for kk in range(8):
                                k.op(PE, lambda e, half=half, kk=kk: e.matmul(PF[half][:], lhsT=mb[:, kk, :], rhs=Wo[:, kk, half * 512:(half + 1) * 512], start=(kk == 0), stop=(kk == 7)), r=[umb, uWo], w=[uPF[half]])
                            k.op(V, lambda e, half=half, b=b: e.tensor_tensor(out=xn[b][:, half * 512:(half + 1) * 512], in0=PF[half][:], in1=xt[b][:, half * 512:(half + 1) * 512], op=ALU.add), r=[uPF[half], uxt[b]], w=[uxn[b]])
                        if last:
                            k.dma(S, y_out[n * 128:(n + 1) * 128, :], xn[b][:], r=[uxn[b]], w=[u_yout])
                        else:
                            k.dma(S, x1e[128 + n * 128:128 + (n + 1) * 128, :], xn[b][:], r=[uxn[b]], w=[u_x1e])
                            if n == 0:
                                k.dma(S, h_bounce.ap()[0:128, :], xn[b][:], r=[uxn[b]], w=[u_hb])
                            if n == 15:
                                k.dma(S, h_bounce.ap()[128:256, :], xn[b][:], r=[uxn[b]], w=[u_hb])
                    k.barrier()
            if not last:
                ccs = k.new_sem("cch")
                k.custom(G, lambda e: e.collective_compute("AllGather", ALU.bypass, replica_groups=RG, ins=[h_bounce.ap().opt()], outs=[h_gath.ap().opt()]), ccs, 1, r=[u_hb], w=[u_hg])
                with ExitStack() as st2:
                    hb = sb(st2, "hb", [128, 2, D], F32); uhb = U()
                    accL = sb(st2, "accL", [128, D], F32); accR = sb(st2, "accR", [128, D], F32); uaL = U(); uaR = U()
                    for r_ in range(4):
                        k.dma(S, hb[:], h_gath.ap()[r_ * 256:(r_ + 1) * 256, :].rearrange("(a p) f -> p a f", p=128), r=[u_hg], w=[uhb])
                        for (acc, ua, a_, sc) in ((accL, uaL, 1, r_), (accR, uaR, 0, 4 + r_)):
                            if r_ == 0:
                                k.op(V, lambda e, acc=acc, a_=a_, sc=sc: e.tensor_scalar_mul(out=acc[:], in0=hb[:, a_, :], scalar1=sel[:, sc:sc + 1]), r=[uhb, uC], w=[ua])
                            else:
                                k.op(V, lambda e, acc=acc, a_=a_, sc=sc: e.scalar_tensor_tensor(out=acc[:], in0=hb[:, a_, :], scalar=sel[:, sc:sc + 1], in1=acc[:], op0=ALU.mult, op1=ALU.add), r=[uhb, uC, ua], w=[ua])
                    k.dma(S, x1e[0:128, :], accL[:], r=[uaL], w=[u_x1e])
                    k.dma(S, x1e[TE - 128:TE, :], accR[:], r=[uaR], w=[u_x1e])
                    k.barrier()

        for l in range(NL):
            xsrc = x_ext if l == 0 else x1e
            last = (l == NL - 1)
            k.dma(S, lg[:], ret_decay[l:l + 1, :].partition_broadcast(128), w=[uL])
            k.dma(S, qg[:], q_norm_g[l:l + 1, :].partition_broadcast(128), w=[uL])
            k.dma(S, kg[:], k_norm_g[l:l + 1, :].partition_broadcast(128), w=[uL])
            k.dma(S, snk[:], attn_sink[l:l + 1, :].partition_broadcast(128), w=[uL])
            k.dma(S, prm[0:31, :], conv_dw[l], w=[uL])
            k.dma(S, prm[31:32, :], conv_b[l], w=[uL])
            k.dma(S, prm[32:33, :], conv_ln_g[l], w=[uL])
            k.dma(S, prm[33:34, :], conv_ln_b[l], w=[uL])
            k.dma(S, prm[34:37, :], b_gate[l], w=[uL])
            k.op(A, lambda e: e.activation(out=lg[:], in_=lg[:], func=AF.Exp), r=[uL], w=[uL])
            k.op(V, lambda e: e.tensor_scalar_mul(out=lg[:], in0=lg[:], scalar1=-1.0), r=[uL], w=[uL])
            k.op(V, lambda e: e.tensor_tensor(out=g2[:, 0:64], in0=qg[:], in1=qg[:], op=ALU.mult), r=[uL], w=[uL])
            k.op(V, lambda e: e.tensor_tensor(out=g2[:, 64:128], in0=kg[:], in1=kg[:], op=ALU.mult), r=[uL], w=[uL])
            k.op(V, lambda e: e.tensor_reduce(out=tmpc[:, 0:1], in_=g2[:, 0:64], axis=AX.X, op=ALU.max), r=[uL], w=[uL])
            k.op(V, lambda e: e.tensor_reduce(out=tmpc[:, 1:2], in_=g2[:, 64:128], axis=AX.X, op=ALU.max), r=[uL], w=[uL])
            k.op(V, lambda e: e.tensor_tensor(out=tmpc[:, 2:3], in0=tmpc[:, 0:1], in1=tmpc[:, 1:2], op=ALU.mult), r=[uL], w=[uL])
            k.op(A, lambda e: e.activation(out=tmpc[:, 3:4], in_=tmpc[:, 2:3], func=AF.Sqrt), r=[uL], w=[uL])
            k.op(V, lambda e: e.tensor_scalar_mul(out=negc[:], in0=tmpc[:, 3:4], scalar1=-8.0), r=[uL], w=[uL])
            k.op(A, lambda e: e.activation(out=snke[:], in_=snk[:], func=AF.Exp, bias=negc[:, 0:1], scale=1.0), r=[uL], w=[uL])
            for c in range(8):
                k.op(PE, lambda e, c=c: e.transpose(out=PF[0][:, c * 37:(c + 1) * 37], in_=prm[:, c * 128:(c + 1) * 128], identity=ident[0:37, 0:37]), r=[uL, uC], w=[uPF[0]])
            k.op(V, lambda e: e.tensor_copy(out=prT[:].rearrange("p c j -> p (c j)"), in_=PF[0][:, 0:296]), r=[uPF[0]], w=[uL])
            k.barrier()

            with ExitStack() as st:
                g_bc = sb(st, "g_bc", [128, D], F32); ug = U()
                k.dma(S, g_bc[:], norm_g[l:l + 1, :].partition_broadcast(128), w=[ug])
                xt = [sb(st, f"xt{i}", [128, D], F32) for i in range(2)]; uxt = [U(), U()]
                junk = [sb(st, f"junk{i}", [128, D], BF16) for i in range(2)]; ujunk = [U(), U()]
                xs = [sb(st, f"xs{i}", [128, D], BF16) for i in range(2)]; uxs = [U(), U()]
                ssq = [sb(st, f"ssq{i}", [128, 2], F32) for i in range(2)]; ussq = [U(), U()]
                def a_s1(t):
                    b = t % 2
                    k.dma(S, xt[b][:], xsrc[t * 128:(t + 1) * 128, :], r=[u_x1e], w=[uxt[b]])
                    k.op(V, lambda e, b=b: e.memset(ssq[b][:], 0.0), w=[ussq[b]])
                    k.op(A, lambda e, b=b: e.activation(out=junk[b][:], in_=xt[b][:], func=AF.Square, accum_out=ssq[b][:, 0:1]), r=[uxt[b]], w=[ujunk[b], ussq[b]])
                    k.op(V, lambda e, b=b: e.tensor_scalar(out=ssq[b][:, 1:2], in0=ssq[b][:, 0:1], scalar1=1.0 / D, scalar2=EPS, op0=ALU.mult, op1=ALU.add), r=[ussq[b]], w=[ussq[b]])
                    k.op(A, lambda e, b=b: e.activation(out=ssq[b][:, 1:2], in_=ssq[b][:, 1:2], func=AF.Sqrt), r=[ussq[b]], w=[ussq[b]])
                    k.op(V, lambda e, b=b: e.reciprocal(out=ssq[b][:, 1:2], in_=ssq[b][:, 1:2]), r=[ussq[b]], w=[ussq[b]])
                    k.op(V, lambda e, b=b: e.scalar_tensor_tensor(out=xs[b][:], in0=xt[b][:], scalar=ssq[b][:, 1:2], in1=g_bc[:], op0=ALU.mult, op1=ALU.mult), r=[uxt[b], ussq[b], ug], w=[uxs[b]])

                def a_s2(t):
                    b = t % 2
                    for c in range(8):
                        k.op(PE, lambda e, b=b, c=c: e.transpose(out=PT[:, c, :], in_=xs[b][:, c * 128:(c + 1) * 128], identity=identb[:]), r=[uxs[b], uC], w=[uPT])
                    k.op(A, lambda e, t=t: e.activation(out=hT[:, :, t * 128:(t + 1) * 128], in_=PT[:], func=AF.Copy), r=[uPT], w=[uhT[t]])
                a_s1(0)
                for t in range(18):
                    if t < 17:
                        a_s1(t + 1)
                    a_s2(t)
                k.barrier()

            if 'C' in phases:
                with ExitStack() as st:
                    Wc = sb(st, "Wc", [128, 8, 3072], BF16); uWc = U()
                    for j in range(6):
                        k.dma(G, Wc[:, :, j * 512:(j + 1) * 512], w_in[l, :, j * 512:(j + 1) * 512].rearrange("(k p) c -> p k c", p=128), w=[uWc])
                    sig = [sb(st, f"sig{i}", [128, 544], F32) for i in range(2)]; usig = [U(), U()]
                    vv = [sb(st, f"vv{i}", [128, 544], F32) for i in range(2)]; uvv = [U(), U()]
                    acc1 = [sb(st, f"acc1{i}", [128, 512], F32) for i in range(2)]; uacc1 = [U(), U()]
                    acc2 = [sb(st, f"acc2{i}", [128, 512], F32) for i in range(2)]; uacc2 = [U(), U()]
                    tmpk = [sb(st, f"tmpk{i}", [128, 512], F32) for i in range(4)]; utmpk = [U() for _ in range(4)]
                    yy = sb(st, "yy", [128, 8, 512], F32); uyy = [U() for _ in range(8)]
                    ysq = [sb(st, f"ysq{i}", [128, 512], F32) for i in range(2)]; uysq = [U(), U()]
                    mean = sb(st, "mean", [128, 512], F32); msq = sb(st, "msq", [128, 512], F32)
                    rstd = sb(st, "rstd", [128, 512], F32); ustat = U()
                    t1 = [sb(st, f"t1{i}", [128, 512], F32) for i in range(2)]; ut1 = [U(), U()]
                    t2 = [sb(st, f"t2{i}", [128, 512], F32) for i in range(2)]; ut2 = [U(), U()]
                    s1 = [sb(st, f"s1{i}", [128, 512], F32) for i in range(2)]; us1 = [U(), U()]
                    sg = [sb(st, f"sg{i}", [128, 512], F32) for i in range(2)]; usg = [U(), U()]
                    ocT = sb(st, "ocT", [128, 8, 512], BF16); uocT = U()
                    pTls = [PF[4], PF[5]]; uTl = [uPF[4], uPF[5]]
                    pST, pSQ, uST, uSQ = PF[6], PF[4], uPF[6], uPF[4]
                    it = 0
                    tk = 0

                    def proj(tg, c, b):
                        e0 = 128 + tg * 512
                        hr = uhT[(e0 - 16) // 128:(e0 + 528 + 127) // 128]
                        pA, uA, pB, uB = PF[b], uPF[b], PF[2 + b], uPF[2 + b]
                        tb = 0
                        pTl = pTls[b]
                        for (ps, ups, col0, to) in ((pA, uA, c * 128, tb), (pB, uB, 1024 + c * 128, tb + 32)):
                            for kk in range(8):
                                k.op(PE, lambda e, ps=ps, kk=kk, col0=col0: e.matmul(ps[:, 0:512], lhsT=Wc[:, kk, col0:col0 + 128], rhs=hT[:, kk, e0 - 16:e0 + 496], start=(kk == 0), stop=(kk == 7)), r=[uWc] + hr, w=[ups])
                            for kk in range(8):
                                k.op(PE, lambda e, kk=kk, col0=col0, to=to, pTl=pTl: e.matmul(pTl[:, to:to + 32], lhsT=Wc[:, kk, col0:col0 + 128], rhs=hT[:, kk, e0 + 496:e0 + 528], start=(kk == 0), stop=(kk == 7)), r=[uWc] + hr, w=[uTl[b]])
                    for tg in range(4):
                        e0 = 128 + tg * 512
                        hr = uhT[(e0 - 16) // 128:(e0 + 528 + 127) // 128]
                        proj(tg, 0, it % 2)
                        for c in range(8):
                            b = it % 2
                            it += 1
                            pA, uA, pB, uB = PF[b], uPF[b], PF[2 + b], uPF[2 + b]
                            tb = 0
                            pTl = pTls[b]
                            if c < 7:
                                proj(tg, c + 1, it % 2)
                            k.op(A, lambda e, b=b, pB=pB: e.activation(out=sig[b][:, 0:512], in_=pB[:, 0:512], func=AF.Sigmoid), r=[uB], w=[usig[b]])
                            k.op(A, lambda e, b=b, tb=tb, pTl=pTl: e.activation(out=sig[b][:, 512:544], in_=pTl[:, tb + 32:tb + 64], func=AF.Sigmoid), r=[uTl[b]], w=[usig[b]])
                            k.op(V, lambda e, b=b, pA=pA: e.tensor_tensor(out=vv[b][:, 0:512], in0=pA[:, 0:512], in1=sig[b][:, 0:512], op=ALU.mult), r=[uA, usig[b]], w=[uvv[b]])
                            k.op(V, lambda e, b=b, tb=tb, pTl=pTl: e.tensor_tensor(out=vv[b][:, 512:544], in0=pTl[:, tb:tb + 32], in1=sig[b][:, 512:544], op=ALU.mult), r=[uTl[b], usig[b]], w=[uvv[b]])
                            k.op(V, lambda e, c=c, b=b: e.tensor_scalar(out=acc1[b][:], in0=vv[b][:, 1:513], scalar1=prT[:, c, 0:1], scalar2=prT[:, c, 31:32], op0=ALU.mult, op1=ALU.add), r=[uvv[b], uL], w=[uacc1[b]])
                            for kt in range(1, 16):
                                k.op(V, lambda e, c=c, kt=kt, b=b: e.scalar_tensor_tensor(out=acc1[b][:], in0=vv[b][:, kt + 1:kt + 513], scalar=prT[:, c, kt:kt + 1], in1=acc1[b][:], op0=ALU.mult, op1=ALU.add), r=[uvv[b], uL, uacc1[b]], w=[uacc1[b]])
                            k.op(A, lambda e, c=c, b=b: e.activation(out=acc2[b][:], in_=vv[b][:, 17:529], func=AF.Copy, scale=prT[:, c, 16:17]), r=[uvv[b], uL], w=[uacc2[b]])
                            for kt in range(17, 31):
                                j = tk % 4
                                tk += 1
                                k.op(A, lambda e, c=c, kt=kt, j=j, b=b: e.activation(out=tmpk[j][:], in_=vv[b][:, kt + 1:kt + 513], func=AF.Copy, scale=prT[:, c, kt:kt + 1]), r=[uvv[b], uL], w=[utmpk[j]])
                                k.op(G, lambda e, j=j, b=b: e.tensor_tensor(out=acc2[b][:], in0=acc2[b][:], in1=tmpk[j][:], op=ALU.add), r=[utmpk[j], uacc2[b]], w=[uacc2[b]])
                            k.op(G, lambda e, c=c, b=b: e.tensor_tensor(out=yy[:, c, :], in0=acc1[b][:], in1=acc2[b][:], op=ALU.add), r=[uacc1[b], uacc2[b]], w=[uyy[c]])
                        for c in range(8):
                            b = c % 2
                            k.op(A, lambda e, c=c, b=b: e.activation(out=ysq[b][:], in_=yy[:, c, :], func=AF.Square), r=[uyy[c]], w=[uysq[b]])
                            k.op(PE, lambda e, c=c: e.matmul(pST[:], lhsT=onesf[:], rhs=yy[:, c, :], start=(c == 0), stop=(c == 7)), r=[uyy[c], uC], w=[uST])
                            k.op(PE, lambda e, c=c, b=b: e.matmul(pSQ[:], lhsT=onesf[:], rhs=ysq[b][:], start=(c == 0), stop=(c == 7)), r=[uysq[b], uC], w=[uSQ])
                        k.op(V, lambda e: e.tensor_copy(out=mean[:], in_=pST[:]), r=[uST], w=[ustat])
                        k.op(G, lambda e: e.tensor_tensor(out=msq[:], in0=mean[:], in1=mean[:], op=ALU.mult), r=[ustat], w=[ustat])
                        k.op(V, lambda e: e.tensor_tensor(out=rstd[:], in0=pSQ[:], in1=msq[:], op=ALU.subtract), r=[uSQ, ustat], w=[ustat])
                        k.op(V, lambda e: e.tensor_scalar_add(out=rstd[:], in0=rstd[:], scalar1=EPS), r=[ustat], w=[ustat])
                        k.op(A, lambda e: e.activation(out=rstd[:], in_=rstd[:], func=AF.Sqrt), r=[ustat], w=[ustat])
                        k.op(V, lambda e: e.reciprocal(out=rstd[:], in_=rstd[:]), r=[ustat], w=[ustat])
                        for c in range(8):
                            b = c % 2
                            pG, uG = PF[b], uPF[b]
                            for kk in range(8):
                                k.op(PE, lambda e, c=c, kk=kk, pG=pG: e.matmul(pG[:], lhsT=Wc[:, kk, 2048 + c * 128:2048 + (c + 1) * 128], rhs=hT[:, kk, e0:e0 + 512], start=(kk == 0), stop=(kk == 7)), r=[uWc] + hr, w=[uG])
                            k.op(V, lambda e, c=c, b=b: e.tensor_tensor(out=t1[b][:], in0=yy[:, c, :], in1=mean[:], op=ALU.subtract), r=[uyy[c], ustat], w=[ut1[b]])
                            k.op(G, lambda e, b=b: e.tensor_tensor(out=t2[b][:], in0=t1[b][:], in1=rstd[:], op=ALU.mult), r=[ut1[b], ustat], w=[ut2[b]])
                            k.op(A, lambda e, c=c, b=b: e.activation(out=s1[b][:], in_=t2[b][:], func=AF.Silu, bias=prT[:, c, 33:34], scale=prT[:, c, 32:33]), r=[ut2[b], uL], w=[us1[b]])
                            k.op(A, lambda e, b=b, pG=pG: e.activation(out=sg[b][:], in_=pG[:], func=AF.Silu), r=[uG], w=[usg[b]])
                            k.op(V, lambda e, c=c, b=b: e.tensor_tensor(out=ocT[:, c, :], in0=s1[b][:], in1=sg[b][:], op=ALU.mult), r=[us1[b], usg[b]], w=[uocT])
                        k.dma(S, OC[:, :, tg * 512:(tg + 1) * 512].rearrange("c p t -> p c t"), ocT[:], r=[uocT], w=[u_OC])
                    k.barrier()
            if 'T' in phases:
                phase_T(l)
            if 'R' in phases:
                phase_R(l)
            if 'F' in phases:
                phase_F(l, xsrc, last)
        k.barrier()
    return nc


def make_consts(core):
    c = core % 4
    seg = c * T
    p = np.arange(128)
    tt = np.arange(T)
    inv = 10000.0 ** (-np.arange(128, dtype=np.float64) / 128.0)
    ang = inv[:, None] * (seg + tt)[None, :].astype(np.float64)
    cosr = np.cos(ang).astype(np.float32)
    sinr = np.sin(ang).astype(np.float32)
    inva = 500000.0 ** (-np.arange(8, dtype=np.float64) / 8.0)
    pos = (seg - 128 + np.arange(18)[None, :] * 128 + p[:, None]).astype(np.float64)
    anga = pos[:, :, None] * inva[None, None, :]
    cosa = np.cos(anga).astype(np.float32)
    sina = np.sin(anga).astype(np.float32)
    j = p[:, None]
    i = p[None, :]
    maskL = (j >= i).astype(np.float32)
    maskR = (j <= i).astype(np.float32)
    mask = np.concatenate([maskL, maskR, maskL * (1.0 if c > 0 else 0.0), maskR * (1.0 if c < 3 else 0.0)], axis=1)
    eseg = np.zeros((128, 32), np.float32)
    for n in range(16):
        eseg[:, n] = 2047 - (n * 128 + p)
        eseg[:, 16 + n] = n * 128 + p
    ek = np.stack([127 - p, p], axis=1).astype(np.float32)
    eq = np.concatenate([np.tile((np.arange(128) + 1)[None, :], (128, 1)), np.tile((128 - np.arange(128))[None, :], (128, 1))], axis=1).astype(np.float32)
    Ef = np.maximum(i - j, 0); Eb = np.maximum(j - i, 0)
    Mf = (i >= j); Mb = (j > i)
    em = np.concatenate([Ef, Eb, Mf, Mb], axis=1).astype(np.float32)
    BIG = 1.0e6
    dist = np.zeros((128, 8), np.float32)
    sel = np.zeros((128, 8), np.float32)
    for r in range(4):
        dist[:, r] = T * (c - r - 1) if r < c else BIG
        dist[:, 4 + r] = T * (r - c - 1) if r > c else BIG
        sel[:, r] = 1.0 if r == c - 1 else 0.0
        sel[:, 4 + r] = 1.0 if r == c + 1 else 0.0
    return dict(c_cosr=cosr, c_sinr=sinr, c_cosa=cosa, c_sina=sina, c_mask=mask,
                c_ident=np.eye(128, dtype=np.float32), c_eseg=eseg, c_ek=ek, c_eq=eq, c_em=em,
                c_dist=dist, c_sel=sel, c_e128=np.tile((128.0 * np.arange(16, dtype=np.float32))[None, :], (128, 1)))


def make_in_maps(inputs):
    x = np.asarray(inputs['x'], np.float32)
    shared = dict(
        norm_g=np.asarray(inputs['norm_g'], np.float32),
        w_in=np.asarray(inputs['w_in'], np.float32),
        b_gate=np.asarray(inputs['b_gate'], np.float32).reshape(2, 3, D),
        conv_dw=np.asarray(inputs['conv_dw'], np.float32),
        conv_b=np.asarray(inputs['conv_b'], np.float32).reshape(2, 1, D),
        conv_ln_g=np.asarray(inputs['conv_ln_g'], np.float32).reshape(2, 1, D),
        conv_ln_b=np.asarray(inputs['conv_ln_b'], np.float32).reshape(2, 1, D),
        ret_decay=np.asarray(inputs['ret_decay'], np.float32).reshape(2, 8),
        q_norm_g=np.asarray(inputs['q_norm_g'], np.float32),
        k_norm_g=np.asarray(inputs['k_norm_g'], np.float32),
        attn_sink=np.asarray(inputs['attn_sink'], np.float32),
        w_conv_out=np.asarray(inputs['w_conv_out'], np.float32),
        w_ret_out=np.asarray(inputs['w_ret_out'], np.float32),
        w_attn_out=np.asarray(inputs['w_attn_out'], np.float32),
        w_out=np.asarray(inputs['w_out'], np.float32),
    )
    in_maps = []
    for core in range(8):
        b, c = core // 4, core % 4
        xe = np.zeros((TE, D), np.float32)
        lo = c * T - 128
        hi = c * T + T + 128
        slo, shi = max(lo, 0), min(hi, 4 * T)
        xe[slo - lo:shi - lo] = x[b, slo:shi]
        m = dict(shared)
        m['x_ext'] = xe
        m.update(make_consts(core))
        in_maps.append(m)
    return in_maps


_NC = None


def kernel(**inputs):
    global _NC
    if _NC is None:
        _NC = build(2)
    in_maps = make_in_maps(inputs)
    res = run_bass_kernel_spmd(_NC, in_maps, core_ids=list(range(8)))
    out = np.zeros((2, 4 * T, D), np.float32)
    for core in range(8):
        b, c = core // 4, core % 4
        out[b, c * T:(c + 1) * T] = res.results[core]["y_out"]
    return out
```

```python
import os
import numpy as np
import concourse.bass as bass
import concourse.mybir as mybir
from concourse.bass_utils import run_bass_kernel_spmd
from contextlib import ExitStack

F32 = mybir.dt.float32
BF16 = mybir.dt.bfloat16
ALU = mybir.AluOpType
AF = mybir.ActivationFunctionType
AX = mybir.AxisListType

ENG = ['tensor', 'vector', 'scalar', 'gpsimd', 'sync']
EPOCH = 20000
ND = 8
PE, V, A, G, S = 'tensor', 'vector', 'scalar', 'gpsimd', 'sync'


class U:
    __slots__ = ('w', 'rs')

    def __init__(s):
        s.w = None
        s.rs = {}


class KB:
    def __init__(s, nc, stack):
        s.nc = nc
        s.st = stack
        s.cnt = {e: 0 for e in ENG}
        s.nsem = 0
        s.sem = {e: s.new_sem(f'e_{e}') for e in ENG}
        s.hist = {e: [] for e in ENG}
        s.waited = {e: {} for e in ENG}
        s.dsem = {}
        s.dtarget = {}
        s.dcount = {}
        s.n_inst = 0

    def new_sem(s, name):
        s.nsem += 1
        return s.st.enter_context(s.nc.semaphore(f'{name}_{s.nsem}'))

    def _waits(s, engine, r, w):
        deps = {}

        def add(tok):
            key = id(tok[0])
            if key not in deps or deps[key][1] < tok[1]:
                deps[key] = tok
        for u in r:
            if u.w is not None:
                add(u.w)
        for u in w:
            if u.w is not None:
                add(u.w)
            for tok in u.rs.values():
                add(tok)
        waits = []
        wd = s.waited[engine]
        for key, (sem, val, src) in deps.items():
            if engine == PE and src == PE:
                continue
            if wd.get(key, 0) >= val:
                continue
            wd[key] = val
            waits.append((sem, val))
        return waits

    def _emit(s, ename, waits, fn, inc):
        e = getattr(s.nc, ename)
        for sem, val in waits:
            e.wait_ge(sem, val)
        if fn is None:
            return
        ins = fn(e)
        if inc[1] is None:
            ins.then_inc(inc[0])
        else:
            ins.then_inc(inc[0], inc[1])
        s.n_inst += 1

    def op(s, engine, fn, r=(), w=()):
        waits = s._waits(engine, r, w)
        if s.cnt[engine] >= EPOCH:
            s.hist[engine].append((s.sem[engine], s.cnt[engine]))
            s.sem[engine] = s.new_sem(f'e_{engine}')
            s.cnt[engine] = 0
        s.cnt[engine] += 1
        sem = s.sem[engine]
        tok = (sem, s.cnt[engine], engine)
        s._emit(engine, waits, fn, (sem, 1))
        for u in r:
            u.rs[id(sem)] = tok
        for u in w:
            u.w = tok
            u.rs = {}
        return tok

    def dma(s, q, out, in_, r=(), w=(), **kw):
        waits = s._waits(q, r, w)
        if q not in s.dsem:
            s.dsem[q] = [s.new_sem(f'd_{q}{i}') for i in range(ND)]
            s.dtarget[q] = [0] * ND
            s.dcount[q] = 0
        i = s.dcount[q] % ND
        s.dcount[q] += 1
        sem = s.dsem[q][i]
        prev = s.dtarget[q][i]
        if prev > 0 and s.waited[q].get(id(sem), 0) < prev:
            s.waited[q][id(sem)] = prev
            waits.append((sem, prev))
        tgt = prev + 16
        s.dtarget[q][i] = tgt
        tok = (sem, tgt, 'dma')
        s._emit(q, waits, lambda e: e.dma_start(out=out, in_=in_, **kw), (sem, 16))
        for u in r:
            u.rs[id(sem)] = tok
        for u in w:
            u.w = tok
            u.rs = {}
        return tok

    def custom(s, engine, fn, inc_sem, inc_val, r=(), w=()):
        waits = s._waits(engine, r, w)
        tok = (inc_sem, inc_val, 'custom')
        s._emit(engine, waits, fn, (inc_sem, None))
        for u in r:
            u.rs[id(inc_sem)] = tok
        for u in w:
            u.w = tok
            u.rs = {}
        return tok

    def barrier(s):
        toks = []
        for e in ENG:
            for sem, c in s.hist[e]:
                toks.append((sem, c))
            if s.cnt[e] > 0:
                toks.append((s.sem[e], s.cnt[e]))
        for q in s.dsem:
            for i in range(ND):
                if s.dtarget[q][i] > 0:
                    toks.append((s.dsem[q][i], s.dtarget[q][i]))
        for e in ENG:
            wd = s.waited[e]
            waits = []
            for sem, val in toks:
                if wd.get(id(sem), 0) >= val:
                    continue
                wd[id(sem)] = val
                waits.append((sem, val))
            s._emit(e, waits, None, None)


D = 1024
T = 2048
TE = 2304
NT = 16
INW = 14848
OFF_CGLU, OFF_CGATE = 0, 2048
OFF_RQ, OFF_RK, OFF_RV, OFF_RG = 3072, 4096, 5120, 7168
OFF_AQ, OFF_AK, OFF_AV, OFF_AG = 9216, 10240, 10496, 10752
OFF_GL = 11776
EPS = 1e-6


def build(NL=2, dbg=False, phases='CTRF'):
    nc = bass.Bass("TRN2", target_bir_lowering=False)

    def din(name, shape):
        return nc.dram_tensor(name, shape, F32, kind="ExternalInput").ap()

    x_ext = din("x_ext", [TE, D])
    norm_g = din("norm_g", [2, D])
    w_in = din("w_in", [2, D, INW])
    b_gate = din("b_gate", [2, 3, D])
    conv_dw = din("conv_dw", [2, 31, D])
    conv_b = din("conv_b", [2, 1, D])
    conv_ln_g = din("conv_ln_g", [2, 1, D])
    conv_ln_b = din("conv_ln_b", [2, 1, D])
    ret_decay = din("ret_decay", [2, 8])
    q_norm_g = din("q_norm_g", [2, 64])
    k_norm_g = din("k_norm_g", [2, 64])
    attn_sink = din("attn_sink", [2, 16])
    w_conv_out = din("w_conv_out", [2, D, D])
    w_ret_out = din("w_ret_out", [2, 2 * D, D])
    w_attn_out = din("w_attn_out", [2, D, D])
    w_out = din("w_out", [2, D, D])
    c_cosr = din("c_cosr", [128, T])
    c_sinr = din("c_sinr", [128, T])
    c_cosa = din("c_cosa", [128, 18, 8])
    c_sina = din("c_sina", [128, 18, 8])
    c_mask = din("c_mask", [128, 512])
    c_ident = din("c_ident", [128, 128])
    c_eseg = din("c_eseg", [128, 32])
    c_ek = din("c_ek", [128, 2])
    c_eq = din("c_eq", [128, 256])
    c_em = din("c_em", [128, 512])
    c_dist = din("c_dist", [128, 8])
    c_sel = din("c_sel", [128, 8])
    c_e128 = din("c_e128", [128, 16])

    y_out = nc.dram_tensor("y_out", [T, D], F32, kind="ExternalOutput").ap()
    okind = "ExternalOutput" if dbg else "Internal"
    OC = nc.dram_tensor("OC", [8, 128, T], BF16, kind=okind).ap()
    OA = nc.dram_tensor("OA", [16, 64, T], BF16, kind=okind).ap()
    OR = nc.dram_tensor("OR", [16, 128, T], BF16, kind=okind).ap()
    x1e = nc.dram_tensor("x1e", [TE, D], F32, kind=okind).ap()
    fs_bounce = nc.dram_tensor("fs_bounce", [512, 512], F32)
    fs_gath = nc.dram_tensor("fs_gath", [2048, 512], F32)
    h_bounce = nc.dram_tensor("h_bounce", [256, D], F32)
    h_gath = nc.dram_tensor("h_gath", [1024, D], F32)
    u_OC, u_OA, u_OR, u_x1e, u_fsb, u_fsg, u_hb, u_hg, u_yout = [U() for _ in range(9)]
    RG = [[0, 1, 2, 3], [4, 5, 6, 7]]

    with ExitStack() as st0:
        k = KB(nc, st0)

        ncount = [0]

        def sb(st, name, shape, dt):
            ncount[0] += 1
            return st.enter_context(nc.sbuf_tensor(f"{name}_{ncount[0]}", shape, dt))

        PF = [st0.enter_context(nc.psum_tensor(f"pf{i}", [128, 512], F32)) for i in range(7)]
        uPF = [U() for _ in range(7)]
        PT = st0.enter_context(nc.psum_tensor("ptb", [128, 8, 128], BF16))
        uPT = U()
        hT = sb(st0, "hT", [128, 8, TE], BF16)
        uhT = [U() for _ in range(18)]
        ident = sb(st0, "ident", [128, 128], F32); identb = sb(st0, "identb", [128, 128], BF16)
        onesf = sb(st0, "onesf", [128, 128], F32); onesb = sb(st0, "onesb", [128, 128], BF16)
        maskb = sb(st0, "maskb", [128, 512], BF16)
        eseg = sb(st0, "eseg", [128, 32], F32); ek = sb(st0, "ek", [128, 2], F32)
        eq = sb(st0, "eq", [128, 256], F32); em = sb(st0, "em", [128, 512], F32)
        dist = sb(st0, "dist", [128, 8], F32); sel = sb(st0, "sel", [128, 8], F32)
        e128 = sb(st0, "e128", [128, 16], F32)
        cosa = sb(st0, "cosa", [128, 18, 8], F32); sina = sb(st0, "sina", [128, 18, 8], F32)
        lg = sb(st0, "lg", [128, 8], F32)
        qg = sb(st0, "qg", [128, 64], F32); kg = sb(st0, "kg", [128, 64], F32)
        snk = sb(st0, "snk", [128, 16], F32); snke = sb(st0, "snke", [128, 16], F32)
        negc = sb(st0, "negc", [128, 1], F32); tmpc = sb(st0, "tmpc", [128, 4], F32)
        g2 = sb(st0, "g2", [128, 128], F32)
        prm = sb(st0, "prm", [37, D], F32)
        prT = sb(st0, "prT", [128, 8, 37], F32)
        uC = U()
        uL = U()

        for (t, src) in ((ident, c_ident), (eseg, c_eseg), (ek, c_ek), (eq, c_eq), (em, c_em),
                         (dist, c_dist), (sel, c_sel), (cosa, c_cosa), (sina, c_sina), (e128, c_e128)):
            k.dma(S, t[:], src, w=[uC])
        k.dma(G, identb[:], c_ident, w=[uC])
        k.dma(G, maskb[:], c_mask, w=[uC])
        k.op(V, lambda e: e.memset(onesf[:], 1.0 / 1024.0), w=[uC])
        k.op(V, lambda e: e.memset(onesb[:], 1.0), w=[uC])
        k.barrier()

        def phase_T(l, Wt, uWt):
            with ExitStack() as st:
                qTg = sb(st, "qTg", [64, 16, 512], BF16); uqT = [U() for _ in range(4)]
                kT = sb(st, "kTres", [64, 4, TE], BF16); ukT = [U() for _ in range(18)]
                Vr = sb(st, "Vres", [128, 18, 256], BF16); uVr = [U() for _ in range(18)]
                sq = [sb(st, f"sq{i}", [128, 1280], F32) for i in range(2)]; usq = [U(), U()]
                ssa = [sb(st, f"ssa{i}", [128, 20], F32) for i in range(2)]; rsa = [sb(st, f"rsa{i}", [128, 20], F32) for i in range(2)]; urs = [U(), U()]
                qn = [sb(st, f"qn{i}", [128, 1280], F32) for i in range(2)]; uqn = [U(), U()]
                rt = [[sb(st, f"rt{p}{i}", [128, 20, 8], F32) for i in range(4)] for p in range(2)]; urt = [[U() for _ in range(4)] for _ in range(2)]
                qb = [sb(st, f"qb{i}", [128, 1024], BF16) for i in range(2)]; uqb = [U(), U()]
                kb = [sb(st, f"kb{i}", [128, 256], BF16) for i in range(2)]; ukb = [U(), U()]
                sq3 = [sq[p][:].rearrange("p (h d) -> p h d", d=64) for p in range(2)]
                qn3 = [qn[p][:].rearrange("p (h d) -> p h d", d=64) for p in range(2)]

                def norm_rot(t, h0, h1, p, banks):
                    nh = h1 - h0
                    k.op(V, lambda e: e.tensor_reduce(out=ssa[p][:, h0:h1], in_=sq3[p][:, h0:h1, :], axis=AX.X, op=ALU.add), r=[usq[p]], w=[urs[p]])
                    k.op(V, lambda e: e.tensor_scalar(out=rsa[p][:, h0:h1], in0=ssa[p][:, h0:h1], scalar1=1.0 / 64.0, scalar2=EPS, op0=ALU.mult, op1=ALU.add), r=[urs[p]], w=[urs[p]])
                    k.op(A, lambda e: e.activation(out=rsa[p][:, h0:h1], in_=rsa[p][:, h0:h1], func=AF.Sqrt), r=[urs[p]], w=[urs[p]])
                    k.op(V, lambda e: e.reciprocal(out=rsa[p][:, h0:h1], in_=rsa[p][:, h0:h1]), r=[urs[p]], w=[urs[p]])
                    if h0 == 0:
                        for half in range(2):
                            bk, ubk = banks[half]
                            k.op(V, lambda e, half=half, bk=bk: e.tensor_tensor(out=qn3[p][:, half * 8:(half + 1) * 8, :], in0=bk[:].rearrange("p (h d) -> p h d", d=64), in1=rsa[p][:, half * 8:(half + 1) * 8].unsqueeze(2).broadcast_to([128, 8, 64]), op=ALU.mult), r=[ubk, urs[p]], w=[uqn[p]])
                        k.op(G, lambda e: e.tensor_tensor(out=qn3[p][:, 0:16, :], in0=qn3[p][:, 0:16, :], in1=qg[:].unsqueeze(1).broadcast_to([128, 16, 64]), op=ALU.mult), r=[uqn[p], uL], w=[uqn[p]])
                    else:
                        bk, ubk = banks[0]
                        k.op(V, lambda e, bk=bk: e.tensor_tensor(out=qn3[p][:, 16:20, :], in0=bk[:, 0:256].rearrange("p (h d) -> p h d", d=64), in1=rsa[p][:, 16:20].unsqueeze(2).broadcast_to([128, 4, 64]), op=ALU.mult), r=[ubk, urs[p]], w=[uqn[p]])
                        k.op(G, lambda e: e.tensor_tensor(out=qn3[p][:, 16:20, :], in0=qn3[p][:, 16:20, :], in1=kg[:].unsqueeze(1).broadcast_to([128, 4, 64]), op=ALU.mult), r=[uqn[p], uL], w=[uqn[p]])
                    cb = cosa[:, t, :].unsqueeze(1).broadcast_to([128, nh, 8])
                    sbb = sina[:, t, :].unsqueeze(1).broadcast_to([128, nh, 8])
                    x1 = qn3[p][:, h0:h1, 0:8]
                    x2 = qn3[p][:, h0:h1, 8:16]
                    r_, ur_ = rt[p], urt[p]
                    k.op(V, lambda e: e.tensor_tensor(out=r_[0][:, h0:h1, :], in0=x1, in1=cb, op=ALU.mult), r=[uqn[p], uC], w=[ur_[0]])
                    k.op(G, lambda e: e.tensor_tensor(out=r_[1][:, h0:h1, :], in0=x2, in1=sbb, op=ALU.mult), r=[uqn[p], uC], w=[ur_[1]])
                    k.op(V, lambda e: e.tensor_tensor(out=r_[2][:, h0:h1, :], in0=x2, in1=cb, op=ALU.mult), r=[uqn[p], uC], w=[ur_[2]])
                    k.op(G, lambda e: e.tensor_tensor(out=r_[3][:, h0:h1, :], in0=x1, in1=sbb, op=ALU.mult), r=[uqn[p], uC], w=[ur_[3]])
                    k.op(V, lambda e: e.tensor_tensor(out=x1, in0=r_[0][:, h0:h1, :], in1=r_[1][:, h0:h1, :], op=ALU.subtract), r=[ur_[0], ur_[1], ur_[2], ur_[3]], w=[uqn[p]])
                    k.op(G, lambda e: e.tensor_tensor(out=x2, in0=r_[2][:, h0:h1, :], in1=r_[3][:, h0:h1, :], op=ALU.add), r=[ur_[2], ur_[3]], w=[uqn[p]])

                def kv_s1(t):
                    p = t % 2
                    bk, ubk = PF[2 + p], uPF[2 + p]
                    for kk in range(8):
                        k.op(PE, lambda e, kk=kk: e.matmul(bk[:], lhsT=hT[:, kk, t * 128:(t + 1) * 128], rhs=Wt[:, kk, 1024:1536], start=(kk == 0), stop=(kk == 7)), r=[uWt, uhT[t]], w=[ubk])
                    k.op(A, lambda e: e.activation(out=sq[p][:, 1024:1280], in_=bk[:, 0:256], func=AF.Square), r=[ubk], w=[usq[p]])
                    norm_rot(t, 16, 20, p, [(bk, ubk)])
                    k.op(A, lambda e: e.activation(out=kb[p][:], in_=qn[p][:, 1024:1280], func=AF.Copy), r=[uqn[p]], w=[ukb[p]])
                    k.op(A, lambda e: e.activation(out=Vr[:, t, :], in_=bk[:, 256:512], func=AF.Copy), r=[ubk], w=[uVr[t]])

                def kv_s2(t):
                    p = t % 2
                    for g in range(4):
                        k.op(PE, lambda e, g=g: e.transpose(out=PT[0:64, g, :], in_=kb[p][:, g * 64:(g + 1) * 64], identity=identb[:]), r=[ukb[p], uC], w=[uPT])
                    k.op(V, lambda e: e.tensor_copy(out=kT[:, :, t * 128:(t + 1) * 128], in_=PT[0:64, 0:4, :]), r=[uPT], w=[ukT[t]])
                kv_s1(0)
                for t in range(18):
                    if t < 17:
                        kv_s1(t + 1)
                    kv_s2(t)
                if dbg == 'T1':
                    k.barrier()
                    return
                gT = sb(st, "gT", [64, 16, 512], BF16); ugT = U()
                pt = [sb(st, f"pt{i}", [128, 512], BF16) for i in range(3)]; upt = [U() for _ in range(3)]
                den = sb(st, "den", [64, 512], F32); uden = U()
                rec = sb(st, "rec", [64, 512], F32); urec = U()
                on = sb(st, "on", [64, 512], F32); uon = U()
                on2 = [on, on]; uon2 = [uon, uon]
                pt_b = [sb(st, f"ptb{i}", [128, 512], BF16) for i in range(3)]; upt_b = [U() for _ in range(3)]
                pt2 = [pt, pt_b]; upt2 = [upt, upt_b]
                oaT = sb(st, "oaT", [64, 16, 512], BF16); uoaT = U()
                for tg in range(4):
                    e0 = 128 + tg * 512
                    hr = uhT[e0 // 128:(e0 + 512) // 128]
                    def q_s1(nn):
                        p = nn % 2
                        t = tg * 4 + nn + 1
                        banks = [(PF[2 * p], uPF[2 * p]), (PF[2 * p + 1], uPF[2 * p + 1])]
                        for half in range(2):
                            bk, ubk = banks[half]
                            for kk in range(8):
                                k.op(PE, lambda e, half=half, kk=kk, bk=bk: e.matmul(bk[:], lhsT=hT[:, kk, t * 128:(t + 1) * 128], rhs=Wt[:, kk, half * 512:(half + 1) * 512], start=(kk == 0), stop=(kk == 7)), r=[uWt, uhT[t]], w=[ubk])
                            k.op(A, lambda e, half=half, bk=bk: e.activation(out=sq[p][:, half * 512:(half + 1) * 512], in_=bk[:], func=AF.Square), r=[ubk], w=[usq[p]])
                        norm_rot(t, 0, 16, p, banks)
                        k.op(A, lambda e: e.activation(out=qb[p][:], in_=qn[p][:, 0:1024], func=AF.Copy), r=[uqn[p]], w=[uqb[p]])

                    def q_s2(nn):
                        p = nn % 2
                        for rr in range(2):
                            for j in range(8):
                                hh = rr * 8 + j
                                k.op(PE, lambda e, j=j, hh=hh: e.transpose(out=PT[0:64, j, :], in_=qb[p][:, hh * 64:(hh + 1) * 64], identity=identb[:]), r=[uqb[p], uC], w=[uPT])
                            k.op(V, lambda e, rr=rr: e.tensor_copy(out=qTg[:, rr * 8:(rr + 1) * 8, nn * 128:(nn + 1) * 128], in_=PT[0:64, :, :]), r=[uPT], w=[uqT[nn]])
                    q_s1(0)
                    for nn in range(4):
                        if nn < 3:
                            q_s1(nn + 1)
                        q_s2(nn)
                    for hh in range(16):
                        ps, ups = PF[hh % 2], uPF[hh % 2]
                        for kk in range(8):
                            k.op(PE, lambda e, ps=ps, hh=hh, kk=kk, e0=e0: e.matmul(ps[0:64, :], lhsT=Wt[:, kk, 1536 + hh * 64:1536 + (hh + 1) * 64], rhs=hT[:, kk, e0:e0 + 512], start=(kk == 0), stop=(kk == 7)), r=[uWt] + hr, w=[ups])
                        k.op(A, lambda e, ps=ps, hh=hh: e.activation(out=gT[:, hh, :], in_=ps[0:64, :], func=AF.Silu), r=[ups], w=[ugT])
                    def t2_front(idx, nn, g):
                        n = tg * 4 + nn
                        t = n + 1
                        ptc, uptc = pt2[idx % 2], upt2[idx % 2]
                        for mi, m in enumerate((t - 1, t, t + 1)):
                            k.op(PE, lambda e, mi=mi, m=m: e.matmul(PF[2 + mi][:], lhsT=kT[:, g, m * 128:(m + 1) * 128], rhs=qTg[:, 4 * g:4 * g + 4, nn * 128:(nn + 1) * 128], start=True, stop=True), r=[ukT[m], uqT[nn]], w=[uPF[2 + mi]])
                            k.op(A, lambda e, mi=mi: e.activation(out=ptc[mi][:], in_=PF[2 + mi][:], func=AF.Exp, bias=negc[:, 0:1], scale=0.125), r=[uPF[2 + mi], uL], w=[uptc[mi]])
                        mL = 256 if t == 1 else 0
                        mR = 384 if t == 16 else 128
                        k.op(V, lambda e: e.tensor_tensor(out=ptc[0][:].rearrange("p (a i) -> p a i", a=4), in0=ptc[0][:].rearrange("p (a i) -> p a i", a=4), in1=maskb[:, mL:mL + 128].unsqueeze(1).broadcast_to([128, 4, 128]), op=ALU.mult), r=[uptc[0], uC], w=[uptc[0]])
                        k.op(G, lambda e: e.tensor_tensor(out=ptc[2][:].rearrange("p (a i) -> p a i", a=4), in0=ptc[2][:].rearrange("p (a i) -> p a i", a=4), in1=maskb[:, mR:mR + 128].unsqueeze(1).broadcast_to([128, 4, 128]), op=ALU.mult), r=[uptc[2], uC], w=[uptc[2]])

                    def t2_back(idx, nn, g):
                        n = tg * 4 + nn
                        t = n + 1
                        ptc, uptc = pt2[idx % 2], upt2[idx % 2]
                        onc, uonc = on2[idx % 2], uon2[idx % 2]
                        for mi, m in enumerate((t - 1, t, t + 1)):
                            k.op(PE, lambda e, mi=mi, m=m: e.matmul(PF[5][0:64, :], lhsT=Vr[:, m, g * 64:(g + 1) * 64], rhs=ptc[mi][:], start=(mi == 0), stop=(mi == 2)), r=[uVr[m], uptc[mi]], w=[uPF[5]])
                        for mi, m in enumerate((t - 1, t, t + 1)):
                            k.op(PE, lambda e, mi=mi: e.matmul(PF[6][0:64, :], lhsT=onesb[:, 0:64], rhs=ptc[mi][:], start=(mi == 0), stop=(mi == 2)), r=[uC, uptc[mi]], w=[uPF[6]])
                        for ei in range(4):
                            hd = 4 * g + ei
                            k.op(A, lambda e, ei=ei, hd=hd: e.activation(out=den[:, ei * 128:(ei + 1) * 128], in_=PF[6][0:64, ei * 128:(ei + 1) * 128], func=AF.Ln, bias=snke[0:64, hd:hd + 1], scale=1.0), r=[uPF[6], uL], w=[uden])
                        k.op(A, lambda e: e.activation(out=rec[:], in_=den[:], func=AF.Exp, scale=-1.0), r=[uden], w=[urec])
                        k.op(V, lambda e: e.tensor_tensor(out=onc[:], in0=PF[5][0:64, :], in1=rec[:], op=ALU.mult), r=[uPF[5], urec], w=[uonc])
                        k.op(G, lambda e: e.tensor_tensor(out=oaT[:, 4 * g:4 * g + 4, nn * 128:(nn + 1) * 128], in0=onc[:].rearrange("p (a i) -> p a i", a=4), in1=gT[:, 4 * g:4 * g + 4, nn * 128:(nn + 1) * 128], op=ALU.mult), r=[uonc, ugT], w=[uoaT])
                    its = [(nn, g) for nn in range(4) for g in range(4)]
                    for idx, (nn, g) in enumerate(its):
                        t2_front(idx, nn, g)
                        if idx > 0:
                            t2_back(idx - 1, *its[idx - 1])
                    t2_back(len(its) - 1, *its[-1])
                    k.dma(S, OA[:, :, tg * 512:(tg + 1) * 512].rearrange("h d t -> d h t"), oaT[:], r=[uoaT], w=[u_OA])
                k.barrier()

        def phase_R(l):
            with ExitStack() as st:
                Wr = sb(st, "Wr", [128, 8, 2048], BF16); uWr = U()
                qT = sb(st, "rqT", [128, 2, T], BF16); uq = [U() for _ in range(4)]
                kT = sb(st, "rkT", [128, 2, T], BF16); ukk = [U() for _ in range(4)]
                Vv = sb(st, "rV", [128, 16, 512], BF16); uV = [U() for _ in range(16)]
                kt = sb(st, "rkt", [128, 16, 256], BF16); ukt = [U() for _ in range(16)]
                Pp = sb(st, "rP", [128, 16, 512], F32); uP = [U() for _ in range(16)]
                cs = sb(st, "rcs", [128, 2, 512], F32); ucs = U()
                tq = [sb(st, f"rtq{i}", [128, 512], F32) for i in range(4)]; utq = [U() for _ in range(4)]
                Sf = sb(st, "Sf", [128, 2, 512], F32); Sb = sb(st, "Sb", [128, 2, 512], F32); uSf = U(); uSb = U()
                Sfb = sb(st, "Sfb", [128, 2, 512], BF16); Sbb = sb(st, "Sbb", [128, 2, 512], BF16); uSfb = U(); uSbb = U()
                fsum = sb(st, "fsum", [128, 4, 512], F32); ufs = U()
                DTm = sb(st, "DTm", [128, 128], F32); dq = sb(st, "dq", [128, 256], F32); dsg = sb(st, "dsg", [128, 32], F32)
                dk = sb(st, "dk", [128, 2], F32); gC = sb(st, "gC", [128, 2], F32); wr = sb(st, "wr", [128, 8], F32)
                tmpD = sb(st, "tmpD", [128, 256], F32); uH = U()
                kfs = sb(st, "kfs", [128, 256], BF16); kbs = sb(st, "kbs", [128, 256], BF16); ukfs = U(); ukbs = U()
                qd = sb(st, "qd", [128, 2, 128], BF16); uqd = U()
                kdc = sb(st, "kdc", [128, 256], BF16); ukdc = U()
                sd = sb(st, "sd", [128, 128], BF16); usd = U()
                oo = sb(st, "oo", [128, 512], F32); uoo = U()
                onn = sb(st, "onn", [128, 512], F32); uonn = U()
                sgg = sb(st, "sgg", [128, 512], F32); usgg = U()
                og = sb(st, "og", [128, 512], BF16); uog = U()
                og_b = sb(st, "og_b", [128, 512], BF16); uog_b = U()
                qd_b = sb(st, "qd_b", [128, 2, 128], BF16); uqd_b = U()
                gn = sb(st, "gn", [128, 32], F32)
                bst = sb(st, "bst", [128, 6], F32); mv = sb(st, "mv", [128, 2], F32); ubn = U()
                orT = sb(st, "orT", [128, 4, 512], BF16); uorT = U()
                RSTOP = int(os.environ.get("RSTOP", "0"))

                class _Stop(Exception):
                    pass

                def ck(n_):
                    if RSTOP == n_:
                        raise _Stop()
                try:
                  for h in range(4):
                      wl = [(1024, OFF_RV + h * 512), (1536, OFF_RG + h * 512)]
                      if h % 2 == 0:
                          wl = [(0, OFF_RQ + h * 256), (512, OFF_RK + h * 256)] + wl
                      for (dst, off) in wl:
                          k.dma(G, Wr[:, :, dst:dst + 512], w_in[l, :, off:off + 512].rearrange("(k p) c -> p k c", p=128), w=[uWr])
                      qc0 = (h % 2) * 256
                      kc0 = 512 + (h % 2) * 256
                      lgf = lg[:, h:h + 1]
                      lgb = lg[:, 4 + h:5 + h]
                      hc = dict(r=[uL, uC, uH], w=[uH])
                      k.op(A, lambda e: e.activation(out=dsg[:, 0:16], in_=eseg[:, 0:16], func=AF.Exp, scale=lgf), **hc)
                      k.op(A, lambda e: e.activation(out=dsg[:, 16:32], in_=eseg[:, 16:32], func=AF.Exp, scale=lgb), **hc)
                      k.op(A, lambda e: e.activation(out=dk[:, 0:1], in_=ek[:, 0:1], func=AF.Exp, scale=lgf), **hc)
                      k.op(A, lambda e: e.activation(out=dk[:, 1:2], in_=ek[:, 1:2], func=AF.Exp, scale=lgb), **hc)
                      k.op(A, lambda e: e.activation(out=dq[:, 0:128], in_=eq[:, 0:128], func=AF.Exp, scale=lgf), **hc)
                      k.op(A, lambda e: e.activation(out=dq[:, 128:256], in_=eq[:, 128:256], func=AF.Exp, scale=lgb), **hc)
                      k.op(V, lambda e: e.tensor_scalar_mul(out=dq[:], in0=dq[:], scalar1=1.0 / 16.0), **hc)
                      k.op(A, lambda e: e.activation(out=gn[:, 0:16], in_=e128[:], func=AF.Exp, scale=lgf), **hc)
                      k.op(A, lambda e: e.activation(out=gn[:, 16:32], in_=e128[:], func=AF.Exp, scale=lgb), **hc)
                      k.op(A, lambda e: e.activation(out=gC[:, 0:1], in_=lgf, func=AF.Exp, scale=128.0), **hc)
                      k.op(A, lambda e: e.activation(out=gC[:, 1:2], in_=lgb, func=AF.Exp, scale=128.0), **hc)
                      k.op(A, lambda e: e.activation(out=wr[:, 0:4], in_=dist[:, 0:4], func=AF.Exp, scale=lgf), **hc)
                      k.op(A, lambda e: e.activation(out=wr[:, 4:8], in_=dist[:, 4:8], func=AF.Exp, scale=lgb), **hc)
                      k.op(A, lambda e: e.activation(out=tmpD[:, 0:128], in_=em[:, 0:128], func=AF.Exp, scale=lgf), **hc)
                      k.op(A, lambda e: e.activation(out=tmpD[:, 128:256], in_=em[:, 128:256], func=AF.Exp, scale=lgb), **hc)
                      k.op(V, lambda e: e.tensor_tensor(out=tmpD[:], in0=tmpD[:], in1=em[:, 256:512], op=ALU.mult), **hc)
                      k.op(V, lambda e: e.tensor_tensor(out=DTm[:], in0=tmpD[:, 0:128], in1=tmpD[:, 128:256], op=ALU.add), **hc)
                      k.op(V, lambda e: e.tensor_scalar_mul(out=DTm[:], in0=DTm[:], scalar1=1.0 / 16.0), **hc)
                      ck(1)
                      for tg in range(4):
                          e0 = 128 + tg * 512
                          hr = uhT[e0 // 128:(e0 + 512) // 128]
                          k.dma(S, cs[:, 0, :], c_cosr[:, tg * 512:(tg + 1) * 512], w=[ucs])
                          k.dma(S, cs[:, 1, :], c_sinr[:, tg * 512:(tg + 1) * 512], w=[ucs])
                          for (col0, dstT, ud) in ((qc0, qT, uq[tg]), (kc0, kT, ukk[tg])):
                              for dc in range(2):
                                  for kk in range(8):
                                      k.op(PE, lambda e, dc=dc, kk=kk, col0=col0, e0=e0: e.matmul(PF[dc][:], lhsT=Wr[:, kk, col0 + dc * 128:col0 + (dc + 1) * 128], rhs=hT[:, kk, e0:e0 + 512], start=(kk == 0), stop=(kk == 7)), r=[uWr] + hr, w=[uPF[dc]])
                              k.op(V, lambda e: e.tensor_tensor(out=tq[0][:], in0=PF[0][:], in1=cs[:, 0, :], op=ALU.mult), r=[uPF[0], ucs], w=[utq[0]])
                              k.op(V, lambda e: e.tensor_tensor(out=tq[1][:], in0=PF[1][:], in1=cs[:, 1, :], op=ALU.mult), r=[uPF[1], ucs], w=[utq[1]])
                              k.op(V, lambda e: e.tensor_tensor(out=tq[2][:], in0=PF[1][:], in1=cs[:, 0, :], op=ALU.mult), r=[uPF[1], ucs], w=[utq[2]])
                              k.op(V, lambda e: e.tensor_tensor(out=tq[3][:], in0=PF[0][:], in1=cs[:, 1, :], op=ALU.mult), r=[uPF[0], ucs], w=[utq[3]])
                              k.op(V, lambda e, dstT=dstT, tg=tg: e.tensor_tensor(out=dstT[:, 0, tg * 512:(tg + 1) * 512], in0=tq[0][:], in1=tq[1][:], op=ALU.subtract), r=[utq[0], utq[1]], w=[ud])
                              k.op(G, lambda e, dstT=dstT, tg=tg: e.tensor_tensor(out=dstT[:, 1, tg * 512:(tg + 1) * 512], in0=tq[2][:], in1=tq[3][:], op=ALU.add), r=[utq[2], utq[3]], w=[ud])
                          ck(2)
                          for nn in range(4):
                              n = tg * 4 + nn
                              t = n + 1
                              for kk in range(8):
                                  k.op(PE, lambda e, kk=kk, t=t: e.matmul(PF[2][:], lhsT=hT[:, kk, t * 128:(t + 1) * 128], rhs=Wr[:, kk, 1024:1536], start=(kk == 0), stop=(kk == 7)), r=[uWr, uhT[t]], w=[uPF[2]])
                              k.op(A, lambda e, n=n: e.activation(out=Vv[:, n, :], in_=PF[2][:], func=AF.Copy), r=[uPF[2]], w=[uV[n]])
                              ck(3)
                              for dc in range(2):
                                  k.op(PE, lambda e, dc=dc, n=n: e.transpose(out=PT[:, dc, :], in_=kT[:, dc, n * 128:(n + 1) * 128], identity=identb[:]), r=[ukk[tg], uC], w=[uPT])
                              ptv = PT[:, 0:2, :]
                              k.op(A, lambda e, n=n, ptv=ptv: e.activation(out=kfs[:].rearrange("p (a d) -> p a d", a=2), in_=ptv, func=AF.Copy, scale=dsg[:, n:n + 1]), r=[uPT, uH], w=[ukfs])
                              k.op(A, lambda e, n=n, ptv=ptv: e.activation(out=kbs[:].rearrange("p (a d) -> p a d", a=2), in_=ptv, func=AF.Copy, scale=dsg[:, 16 + n:17 + n]), r=[uPT, uH], w=[ukbs])
                              k.op(A, lambda e, n=n, ptv=ptv: e.activation(out=kt[:, n, :].rearrange("p (a d) -> p a d", a=2), in_=ptv, func=AF.Copy), r=[uPT], w=[ukt[n]])
                              ck(4)
                              for dc in range(2):
                                  k.op(PE, lambda e, dc=dc, n=n: e.matmul(PF[3 + dc][:], lhsT=kfs[:, dc * 128:(dc + 1) * 128], rhs=Vv[:, n, :], start=(n == 0), stop=(n == 15)), r=[ukfs, uV[n]], w=[uPF[3 + dc]])
                                  k.op(PE, lambda e, dc=dc, n=n: e.matmul(PF[5 + dc][:], lhsT=kbs[:, dc * 128:(dc + 1) * 128], rhs=Vv[:, n, :], start=(n == 0), stop=(n == 15)), r=[ukbs, uV[n]], w=[uPF[5 + dc]])
                      if dbg == 'R0':
                          k.barrier()
                          return
                      for j in range(4):
                          k.op(A, lambda e, j=j: e.activation(out=fsum[:, j, :], in_=PF[3 + j][:], func=AF.Copy), r=[uPF[3 + j]], w=[ufs])
                      k.dma(S, fs_bounce.ap().rearrange("(j p) v -> p j v", p=128), fsum[:], r=[ufs], w=[u_fsb])
                      ccs = k.new_sem("cc")
                      k.custom(G, lambda e: e.collective_compute("AllGather", ALU.bypass, replica_groups=RG, ins=[fs_bounce.ap().opt()], outs=[fs_gath.ap().opt()]), ccs, 1, r=[u_fsb], w=[u_fsg])
                      k.op(V, lambda e: e.memset(Sf[:], 0.0), r=[uSf], w=[uSf])
                      k.op(V, lambda e: e.memset(Sb[:], 0.0), r=[uSb], w=[uSb])
                      k.op(V, lambda e: e.memset(Sfb[:], 0.0), r=[uSfb], w=[uSfb])
                      k.op(V, lambda e: e.memset(Sbb[:], 0.0), r=[uSbb], w=[uSbb])
                      for n in range(15, -1, -1):
                          tg = n // 4
                          k.op(A, lambda e, n=n: e.activation(out=kdc[:], in_=kt[:, n, :], func=AF.Copy, scale=dk[:, 1:2]), r=[ukt[n], uH], w=[ukdc])
                          for dc in range(2):
                              k.op(PE, lambda e, dc=dc, n=n: e.matmul(PF[1 + dc][:], lhsT=kdc[:, dc * 128:(dc + 1) * 128], rhs=Vv[:, n, :], start=True, stop=True), r=[ukdc, uV[n]], w=[uPF[1 + dc]])
                          k.op(V, lambda e, n=n: e.tensor_tensor(out=qd[:], in0=qT[:, :, n * 128:(n + 1) * 128], in1=dq[:, 128:256].unsqueeze(1).broadcast_to([128, 2, 128]), op=ALU.mult), r=[uq[tg], uH], w=[uqd])
                          for dc in range(2):
                              k.op(PE, lambda e, dc=dc: e.matmul(PF[0][:], lhsT=qd[:, dc, :], rhs=Sbb[:, dc, :], start=(dc == 0), stop=(dc == 1)), r=[uqd, uSbb], w=[uPF[0]])
                          for dc in range(2):
                              k.op(V, lambda e, dc=dc: e.scalar_tensor_tensor(out=Sb[:, dc, :], in0=Sb[:, dc, :], scalar=gC[:, 1:2], in1=PF[1 + dc][:], op0=ALU.mult, op1=ALU.add), r=[uPF[1 + dc], uH, uSb], w=[uSb])
                          k.op(A, lambda e: e.activation(out=Sbb[:], in_=Sb[:], func=AF.Copy), r=[uSb], w=[uSbb])
                          k.op(A, lambda e, n=n: e.activation(out=Pp[:, n, :], in_=PF[0][:], func=AF.Copy), r=[uPF[0]], w=[uP[n]])
                      k.op(V, lambda e: e.memset(Sbb[:], 0.0), r=[uSbb], w=[uSbb])
                      Sfb2 = [Sfb, Sbb]
                      uSfb2 = [uSfb, uSbb]
                      for n in range(16):
                          tg = n // 4
                          cur, nxt = n % 2, (n + 1) % 2
                          k.op(A, lambda e, n=n: e.activation(out=kdc[:], in_=kt[:, n, :], func=AF.Copy, scale=dk[:, 0:1]), r=[ukt[n], uH], w=[ukdc])
                          for dc in range(2):
                              k.op(PE, lambda e, dc=dc, n=n: e.matmul(PF[1 + dc][:], lhsT=kdc[:, dc * 128:(dc + 1) * 128], rhs=Vv[:, n, :], start=True, stop=True), r=[ukdc, uV[n]], w=[uPF[1 + dc]])
                          for dc in range(2):
                              k.op(PE, lambda e, dc=dc, n=n: e.matmul(PF[3][:, 0:128], lhsT=kT[:, dc, n * 128:(n + 1) * 128], rhs=qT[:, dc, n * 128:(n + 1) * 128], start=(dc == 0), stop=(dc == 1)), r=[ukk[tg], uq[tg]], w=[uPF[3]])
                          k.op(V, lambda e: e.tensor_tensor(out=sd[:], in0=PF[3][:, 0:128], in1=DTm[:], op=ALU.mult), r=[uPF[3], uH], w=[usd])
                          k.op(G, lambda e, n=n: e.tensor_tensor(out=qd[:], in0=qT[:, :, n * 128:(n + 1) * 128], in1=dq[:, 0:128].unsqueeze(1).broadcast_to([128, 2, 128]), op=ALU.mult), r=[uq[tg], uH], w=[uqd])
                          k.op(PE, lambda e, n=n: e.matmul(PF[4][:], lhsT=sd[:], rhs=Vv[:, n, :], start=True, stop=False), r=[usd, uV[n]], w=[uPF[4]])
                          for dc in range(2):
                              k.op(PE, lambda e, dc=dc, cur=cur: e.matmul(PF[4][:], lhsT=qd[:, dc, :], rhs=Sfb2[cur][:, dc, :], start=False, stop=(dc == 1)), r=[uqd, uSfb2[cur]], w=[uPF[4]])
                          for dc in range(2):
                              k.op(V, lambda e, dc=dc: e.scalar_tensor_tensor(out=Sf[:, dc, :], in0=Sf[:, dc, :], scalar=gC[:, 0:1], in1=PF[1 + dc][:], op0=ALU.mult, op1=ALU.add), r=[uPF[1 + dc], uH, uSf], w=[uSf])
                          k.op(A, lambda e, nxt=nxt: e.activation(out=Sfb2[nxt][:], in_=Sf[:], func=AF.Copy), r=[uSf], w=[uSfb2[nxt]])
                          k.op(V, lambda e, n=n: e.tensor_tensor(out=Pp[:, n, :], in0=PF[4][:], in1=Pp[:, n, :], op=ALU.add), r=[uPF[4], uP[n]], w=[uP[n]])
                      for r_ in range(4):
                          k.dma(S, fsum[:], fs_gath.ap()[r_ * 512:(r_ + 1) * 512, :].rearrange("(j p) v -> p j v", p=128), r=[u_fsg], w=[ufs])
                          for dirn, (Sx, uSx) in enumerate(((Sf, uSf), (Sb, uSb))):
                              for dc in range(2):
                                  wcol = wr[:, dirn * 4 + r_:dirn * 4 + r_ + 1]
                                  if r_ == 0:
                                      k.op(V, lambda e, Sx=Sx, dc=dc, dirn=dirn, wcol=wcol: e.tensor_scalar_mul(out=Sx[:, dc, :], in0=fsum[:, dirn * 2 + dc, :], scalar1=wcol), r=[ufs, uH], w=[uSx])
                                  else:
                                      k.op(V, lambda e, Sx=Sx, dc=dc, dirn=dirn, wcol=wcol: e.scalar_tensor_tensor(out=Sx[:, dc, :], in0=fsum[:, dirn * 2 + dc, :], scalar=wcol, in1=Sx[:, dc, :], op0=ALU.mult, op1=ALU.add), r=[ufs, uH, uSx], w=[uSx])
                      k.op(A, lambda e: e.activation(out=Sfb[:], in_=Sf[:], func=AF.Copy), r=[uSf], w=[uSfb])
                      k.op(A, lambda e: e.activation(out=Sbb[:], in_=Sb[:], func=AF.Copy), r=[uSb], w=[uSbb])
                      og2 = [og, og_b]
                      uog2 = [uog, uog_b]
                      qd2 = [qd, qd_b]
                      uqd2 = [uqd, uqd_b]

                      def finish(n):
                          tg = n // 4
                          for vc in range(4):
                              k.op(PE, lambda e, vc=vc, n=n: e.transpose(out=PT[:, vc, :], in_=og2[n % 2][:, vc * 128:(vc + 1) * 128], identity=identb[:]), r=[uog2[n % 2], uC], w=[uPT])
                          k.op(A, lambda e, n=n: e.activation(out=orT[:, :, (n % 4) * 128:(n % 4 + 1) * 128], in_=PT[:, 0:4, :], func=AF.Copy), r=[uPT], w=[uorT])
                          if n % 4 == 3:
                              k.dma(S, OR[h * 4:(h + 1) * 4, :, tg * 512:(tg + 1) * 512].rearrange("c p t -> p c t"), orT[:], r=[uorT], w=[u_OR])
                      for n in range(16):
                          tg = n // 4
                          t = n + 1
                          cur = n % 2
                          for kk in range(8):
                              k.op(PE, lambda e, kk=kk, t=t: e.matmul(PF[5][:], lhsT=hT[:, kk, t * 128:(t + 1) * 128], rhs=Wr[:, kk, 1536:2048], start=(kk == 0), stop=(kk == 7)), r=[uWr, uhT[t]], w=[uPF[5]])
                          k.op(A, lambda e: e.activation(out=sgg[:], in_=PF[5][:], func=AF.Silu), r=[uPF[5]], w=[usgg])
                          k.op(V, lambda e, n=n: e.scalar_tensor_tensor(out=qd2[0][:], in0=qT[:, :, n * 128:(n + 1) * 128], scalar=gn[:, n:n + 1], in1=dq[:, 0:128].unsqueeze(1).broadcast_to([128, 2, 128]), op0=ALU.mult, op1=ALU.mult), r=[uq[tg], uH], w=[uqd2[0]])
                          k.op(V, lambda e, n=n: e.scalar_tensor_tensor(out=qd2[1][:], in0=qT[:, :, n * 128:(n + 1) * 128], scalar=gn[:, 16 + 15 - n:16 + 16 - n], in1=dq[:, 128:256].unsqueeze(1).broadcast_to([128, 2, 128]), op0=ALU.mult, op1=ALU.mult), r=[uq[tg], uH], w=[uqd2[1]])
                          for di, Sxb, uSxb in ((0, Sfb, uSfb), (1, Sbb, uSbb)):
                              for dc in range(2):
                                  k.op(PE, lambda e, di=di, dc=dc, Sxb=Sxb: e.matmul(PF[4][:], lhsT=qd2[di][:, dc, :], rhs=Sxb[:, dc, :], start=(di == 0 and dc == 0), stop=(di == 1 and dc == 1)), r=[uqd2[di], uSxb], w=[uPF[4]])
                          if n > 0:
                              finish(n - 1)
                          k.op(V, lambda e, n=n: e.tensor_tensor(out=oo[:], in0=PF[4][:], in1=Pp[:, n, :], op=ALU.add), r=[uPF[4], uP[n]], w=[uoo])
                          k.op(V, lambda e: e.bn_stats(out=bst[:], in_=oo[:]), r=[uoo], w=[ubn])
                          k.op(V, lambda e: e.bn_aggr(out=mv[:], in_=bst[:]), r=[ubn], w=[ubn])
                          k.op(V, lambda e: e.tensor_scalar_add(out=mv[:, 1:2], in0=mv[:, 1:2], scalar1=EPS), r=[ubn], w=[ubn])
                          k.op(A, lambda e: e.activation(out=mv[:, 1:2], in_=mv[:, 1:2], func=AF.Sqrt), r=[ubn], w=[ubn])
                          k.op(V, lambda e: e.reciprocal(out=mv[:, 1:2], in_=mv[:, 1:2]), r=[ubn], w=[ubn])
                          k.op(V, lambda e: e.tensor_scalar(out=onn[:], in0=oo[:], scalar1=mv[:, 0:1], scalar2=mv[:, 1:2], op0=ALU.subtract, op1=ALU.mult), r=[uoo, ubn], w=[uonn])
                          k.op(G, lambda e, cur=cur: e.tensor_tensor(out=og2[cur][:], in0=onn[:], in1=sgg[:], op=ALU.mult), r=[uonn, usgg], w=[uog2[cur]])
                      finish(15)
                except _Stop:
                    pass
                k.barrier()

        def phase_F(l, xsrc, last):
            with ExitStack() as st:
                mg = sb(st, "mg", [128, 8, T], F32); umg = [U() for _ in range(4)]
                sgm = sb(st, "sgm", [128, 512], F32); usgm = U()
                tmpm = sb(st, "tmpm", [128, 512], F32); utmpm = U()
                for bi, (nk, kp) in enumerate(((8, 128), (16, 128), (16, 64))):
                    with ExitStack() as st2:
                        Wb = sb(st2, f"Wb{bi}", [kp, nk, D], BF16); uWb = U()
                        Wg = sb(st2, f"Wg{bi}", [128, 8, D], BF16); uWg = U()
                        ob = sb(st2, f"ob{bi}", [kp, nk, 512], BF16); uob = U()
                        src_w = (w_conv_out, w_ret_out, w_attn_out)[bi]
                        if bi == 2:
                            view = src_w[l].rearrange("(h d) c -> d h c", d=64)
                        else:
                            view = src_w[l].rearrange("(k p) c -> p k c", p=128)
                        for j in range(0, nk, 4):
                            k.dma(G, Wb[:, j:j + 4, :], view[:, j:j + 4, :], w=[uWb])
                        for j in range(2):
                            c0 = OFF_GL + bi * 1024 + j * 512
                            k.dma(G, Wg[:, :, j * 512:(j + 1) * 512], w_in[l, :, c0:c0 + 512].rearrange("(k p) c -> p k c", p=128), w=[uWg])
                        osrc = (OC, OR, OA)[bi]
                        uos = (u_OC, u_OR, u_OA)[bi]
                        for tg in range(4):
                            e0 = 128 + tg * 512
                            hr = uhT[e0 // 128:(e0 + 512) // 128]
                            k.dma(S, ob[:], osrc[:, :, tg * 512:(tg + 1) * 512].rearrange("c p t -> p c t"), r=[uos], w=[uob])
                            for m in range(8):
                                py, upy = PF[m % 3], uPF[m % 3]
                                pg, upg = PF[3 + m % 3], uPF[3 + m % 3]
                                for j in range(nk):
                                    k.op(PE, lambda e, py=py, j=j, m=m: e.matmul(py[:], lhsT=Wb[:, j, m * 128:(m + 1) * 128], rhs=ob[:, j, :], start=(j == 0), stop=(j == nk - 1)), r=[uWb, uob], w=[upy])
                                for kk in range(8):
                                    k.op(PE, lambda e, pg=pg, kk=kk, m=m, e0=e0: e.matmul(pg[:], lhsT=Wg[:, kk, m * 128:(m + 1) * 128], rhs=hT[:, kk, e0:e0 + 512], start=(kk == 0), stop=(kk == 7)), r=[uWg] + hr, w=[upg])
                                k.op(A, lambda e, pg=pg, m=m, bi=bi: e.activation(out=sgm[:], in_=pg[:], func=AF.Sigmoid, bias=prT[:, m, 34 + bi:35 + bi], scale=1.0), r=[upg, uL], w=[usgm])
                                if bi == 0:
                                    k.op(V, lambda e, py=py, m=m, tg=tg: e.tensor_tensor(out=mg[:, m, tg * 512:(tg + 1) * 512], in0=py[:], in1=sgm[:], op=ALU.mult), r=[upy, usgm], w=[umg[tg]])
                                else:
                                    k.op(V, lambda e, py=py: e.tensor_tensor(out=tmpm[:], in0=py[:], in1=sgm[:], op=ALU.mult), r=[upy, usgm], w=[utmpm])
                                    k.op(G, lambda e, m=m, tg=tg: e.tensor_tensor(out=mg[:, m, tg * 512:(tg + 1) * 512], in0=mg[:, m, tg * 512:(tg + 1) * 512], in1=tmpm[:], op=ALU.add), r=[utmpm, umg[tg]], w=[umg[tg]])
                        k.barrier()
                with ExitStack() as st2:
                    Wo = sb(st2, "Wo", [128, 8, D], BF16); uWo = U()
                    for j in range(2):
                        k.dma(G, Wo[:, j * 4:(j + 1) * 4, :], w_out[l].rearrange("(k p) c -> p k c", p=128)[:, j * 4:(j + 1) * 4, :], w=[uWo])
                    mb = sb(st2, "mb", [128, 8, 128], BF16); umb = U()
                    xt = [sb(st2, f"fxt{i}", [128, D], F32) for i in range(2)]; uxt = [U(), U()]
                    xn = [sb(st2, f"fxn{i}", [128, D], F32) for i in range(2)]; uxn = [U(), U()]
                    for n in range(16):
                        b = n % 2
                        k.op(A, lambda e, n=n: e.activation(out=mb[:], in_=mg[:, :, n * 128:(n + 1) * 128], func=AF.Copy), r=[umg[n // 4]], w=[umb])
                        k.dma(S, xt[b][:], xsrc[128 + n * 128:128 + (n + 1) * 128, :], r=[u_x1e], w=[uxt[b]])
                        for half in range(2):
                            for kk in range(8):
                                k.op(PE, lambda e, half=half, kk=kk: e.matmul(PF[half][:], lhsT=mb[:, kk, :], rhs=Wo[:, kk, half * 512:(half + 1) * 512], start=(kk == 0), stop=(kk == 7)), r=[umb, uWo], w=[uPF[half]])
                            k.op(V, lambda e, half=half, b=b: e.tensor_tensor(out=xn[b][:, half * 512:(half + 1) * 512], in0=PF[half][:], in1=xt[b][:, half * 512:(half + 1) * 512], op=ALU.add), r=[uPF[half], uxt[b]], w=[uxn[b]])
                        if last:
                            k.dma(S, y_out[n * 128:(n + 1) * 128, :], xn[b][:], r=[uxn[b]], w=[u_yout])
                        else:
                            k.dma(S, x1e[128 + n * 128:128 + (n + 1) * 128, :], xn[b][:], r=[uxn[b]], w=[u_x1e])
                            if n == 0:
                                k.dma(S, h_bounce.ap()[0:128, :], xn[b][:], r=[uxn[b]], w=[u_hb])
                            if n == 15:
                                k.dma(S, h_bounce.ap()[128:256, :], xn[b][:], r=[uxn[b]], w=[u_hb])
                    k.barrier()
            if not last:
                ccs = k.new_sem("cch")
                k.custom(G, lambda e: e.collective_compute("AllGather", ALU.bypass, replica_groups=RG, ins=[h_bounce.ap().opt()], outs=[h_gath.ap().opt()]), ccs, 1, r=[u_hb], w=[u_hg])
                with ExitStack() as st2:
                    hb = sb(st2, "hb", [128, 2, D], F32); uhb = U()
                    accL = sb(st2, "accL", [128, D], F32); accR = sb(st2, "accR", [128, D], F32); uaL = U(); uaR = U()
                    for r_ in range(4):
                        k.dma(S, hb[:], h_gath.ap()[r_ * 256:(r_ + 1) * 256, :].rearrange("(a p) f -> p a f", p=128), r=[u_hg], w=[uhb])
                        for (acc, ua, a_, sc) in ((accL, uaL, 1, r_), (accR, uaR, 0, 4 + r_)):
                            if r_ == 0:
                                k.op(V, lambda e, acc=acc, a_=a_, sc=sc: e.tensor_scalar_mul(out=acc[:], in0=hb[:, a_, :], scalar1=sel[:, sc:sc + 1]), r=[uhb, uC], w=[ua])
                            else:
                                k.op(V, lambda e, acc=acc, a_=a_, sc=sc: e.scalar_tensor_tensor(out=acc[:], in0=hb[:, a_, :], scalar=sel[:, sc:sc + 1], in1=acc[:], op0=ALU.mult, op1=ALU.add), r=[uhb, uC, ua], w=[ua])
                    k.dma(S, x1e[0:128, :], accL[:], r=[uaL], w=[u_x1e])
                    k.dma(S, x1e[TE - 128:TE, :], accR[:], r=[uaR], w=[u_x1e])
                    k.barrier()

        for l in range(NL):
            xsrc = x_ext if l == 0 else x1e
            last = (l == NL - 1)
            k.dma(S, lg[:], ret_decay[l:l + 1, :].partition_broadcast(128), w=[uL])
            k.dma(S, qg[:], q_norm_g[l:l + 1, :].partition_broadcast(128), w=[uL])
            k.dma(S, kg[:], k_norm_g[l:l + 1, :].partition_broadcast(128), w=[uL])
            k.dma(S, snk[:], attn_sink[l:l + 1, :].partition_broadcast(128), w=[uL])
            k.dma(S, prm[0:31, :], conv_dw[l], w=[uL])
            k.dma(S, prm[31:32, :], conv_b[l], w=[uL])
            k.dma(S, prm[32:33, :], conv_ln_g[l], w=[uL])
            k.dma(S, prm[33:34, :], conv_ln_b[l], w=[uL])
            k.dma(S, prm[34:37, :], b_gate[l], w=[uL])
            k.op(A, lambda e: e.activation(out=lg[:], in_=lg[:], func=AF.Exp), r=[uL], w=[uL])
            k.op(V, lambda e: e.tensor_scalar_mul(out=lg[:], in0=lg[:], scalar1=-1.0), r=[uL], w=[uL])
            k.op(V, lambda e: e.tensor_tensor(out=g2[:, 0:64], in0=qg[:], in1=qg[:], op=ALU.mult), r=[uL], w=[uL])
            k.op(V, lambda e: e.tensor_tensor(out=g2[:, 64:128], in0=kg[:], in1=kg[:], op=ALU.mult), r=[uL], w=[uL])
            k.op(V, lambda e: e.tensor_reduce(out=tmpc[:, 0:1], in_=g2[:, 0:64], axis=AX.X, op=ALU.max), r=[uL], w=[uL])
            k.op(V, lambda e: e.tensor_reduce(out=tmpc[:, 1:2], in_=g2[:, 64:128], axis=AX.X, op=ALU.max), r=[uL], w=[uL])
            k.op(V, lambda e: e.tensor_tensor(out=tmpc[:, 2:3], in0=tmpc[:, 0:1], in1=tmpc[:, 1:2], op=ALU.mult), r=[uL], w=[uL])
            k.op(A, lambda e: e.activation(out=tmpc[:, 3:4], in_=tmpc[:, 2:3], func=AF.Sqrt), r=[uL], w=[uL])
            k.op(V, lambda e: e.tensor_scalar_mul(out=negc[:], in0=tmpc[:, 3:4], scalar1=-8.0), r=[uL], w=[uL])
            k.op(A, lambda e: e.activation(out=snke[:], in_=snk[:], func=AF.Exp, bias=negc[:, 0:1], scale=1.0), r=[uL], w=[uL])
            for c in range(8):
                k.op(PE, lambda e, c=c: e.transpose(out=PF[0][:, c * 37:(c + 1) * 37], in_=prm[:, c * 128:(c + 1) * 128], identity=ident[0:37, 0:37]), r=[uL, uC], w=[uPF[0]])
            k.op(V, lambda e: e.tensor_copy(out=prT[:].rearrange("p c j -> p (c j)"), in_=PF[0][:, 0:296]), r=[uPF[0]], w=[uL])
            k.barrier()

            stW = ExitStack()
            Wt_pre = sb(stW, "Wt", [128, 8, 2560], BF16); uWt_pre = U()
            stWc = ExitStack()
            Wc = sb(stWc, "Wc", [128, 8, 3072], BF16); uWc = U()
            if 'C' in phases:
                for j in range(6):
                    k.dma(G, Wc[:, :, j * 512:(j + 1) * 512], w_in[l, :, j * 512:(j + 1) * 512].rearrange("(k p) c -> p k c", p=128), w=[uWc])
            with ExitStack() as st:
                g_bc = sb(st, "g_bc", [128, D], F32); ug = U()
                k.dma(S, g_bc[:], norm_g[l:l + 1, :].partition_broadcast(128), w=[ug])
                xt = [sb(st, f"xt{i}", [128, D], F32) for i in range(2)]; uxt = [U(), U()]
                junk = [sb(st, f"junk{i}", [128, D], BF16) for i in range(2)]; ujunk = [U(), U()]
                xs = [sb(st, f"xs{i}", [128, D], BF16) for i in range(2)]; uxs = [U(), U()]
                ssq = [sb(st, f"ssq{i}", [128, 2], F32) for i in range(2)]; ussq = [U(), U()]
                def a_s1(t):
                    b = t % 2
                    k.dma(S, xt[b][:], xsrc[t * 128:(t + 1) * 128, :], r=[u_x1e], w=[uxt[b]])
                    k.op(V, lambda e, b=b: e.memset(ssq[b][:], 0.0), w=[ussq[b]])
                    k.op(A, lambda e, b=b: e.activation(out=junk[b][:], in_=xt[b][:], func=AF.Square, accum_out=ssq[b][:, 0:1]), r=[uxt[b]], w=[ujunk[b], ussq[b]])
                    k.op(V, lambda e, b=b: e.tensor_scalar(out=ssq[b][:, 1:2], in0=ssq[b][:, 0:1], scalar1=1.0 / D, scalar2=EPS, op0=ALU.mult, op1=ALU.add), r=[ussq[b]], w=[ussq[b]])
                    k.op(A, lambda e, b=b: e.activation(out=ssq[b][:, 1:2], in_=ssq[b][:, 1:2], func=AF.Sqrt), r=[ussq[b]], w=[ussq[b]])
                    k.op(V, lambda e, b=b: e.reciprocal(out=ssq[b][:, 1:2], in_=ssq[b][:, 1:2]), r=[ussq[b]], w=[ussq[b]])
                    k.op(V, lambda e, b=b: e.scalar_tensor_tensor(out=xs[b][:], in0=xt[b][:], scalar=ssq[b][:, 1:2], in1=g_bc[:], op0=ALU.mult, op1=ALU.mult), r=[uxt[b], ussq[b], ug], w=[uxs[b]])

                def a_s2(t):
                    b = t % 2
                    for c in range(8):
                        k.op(PE, lambda e, b=b, c=c: e.transpose(out=PT[:, c, :], in_=xs[b][:, c * 128:(c + 1) * 128], identity=identb[:]), r=[uxs[b], uC], w=[uPT])
                    k.op(A, lambda e, t=t: e.activation(out=hT[:, :, t * 128:(t + 1) * 128], in_=PT[:], func=AF.Copy), r=[uPT], w=[uhT[t]])
                a_s1(0)
                for t in range(18):
                    if t < 17:
                        a_s1(t + 1)
                    a_s2(t)
                k.barrier()

            if 'C' in phases:
                with ExitStack() as st:
                    if 'T' in phases:
                        for j in range(5):
                            k.dma(G, Wt_pre[:, :, j * 512:(j + 1) * 512], w_in[l, :, OFF_AQ + j * 512:OFF_AQ + (j + 1) * 512].rearrange("(k p) c -> p k c", p=128), w=[uWt_pre])
                    sig = [sb(st, f"sig{i}", [128, 544], F32) for i in range(2)]; usig = [U(), U()]
                    vv = [sb(st, f"vv{i}", [128, 544], F32) for i in range(2)]; uvv = [U(), U()]
                    acc1 = [sb(st, f"acc1{i}", [128, 512], F32) for i in range(2)]; uacc1 = [U(), U()]
                    acc2 = [sb(st, f"acc2{i}", [128, 512], F32) for i in range(2)]; uacc2 = [U(), U()]
                    tmpk = [sb(st, f"tmpk{i}", [128, 512], F32) for i in range(4)]; utmpk = [U() for _ in range(4)]
                    yy = sb(st, "yy", [128, 8, 512], F32); uyy = [U() for _ in range(8)]
                    ysq = [sb(st, f"ysq{i}", [128, 512], F32) for i in range(2)]; uysq = [U(), U()]
                    mean = sb(st, "mean", [128, 512], F32); msq = sb(st, "msq", [128, 512], F32)
                    rstd = sb(st, "rstd", [128, 512], F32); ustat = U()
                    t1 = [sb(st, f"t1{i}", [128, 512], F32) for i in range(2)]; ut1 = [U(), U()]
                    t2 = t1; ut2 = ut1
                    s1 = t1; us1 = ut1
                    sg = [sb(st, f"sg{i}", [128, 512], F32) for i in range(2)]; usg = [U(), U()]
                    ocT = sb(st, "ocT", [128, 8, 512], BF16); uocT = U()
                    pTls = [PF[4], PF[5]]; uTl = [uPF[4], uPF[5]]
                    pST, pSQ, uST, uSQ = PF[6], PF[4], uPF[6], uPF[4]
                    it = 0
                    tk = 0

                    def proj(tg, c, b):
                        e0 = 128 + tg * 512
                        hr = uhT[(e0 - 16) // 128:(e0 + 528 + 127) // 128]
                        pA, uA, pB, uB = PF[b], uPF[b], PF[2 + b], uPF[2 + b]
                        tb = 0
                        pTl = pTls[b]
                        for (ps, ups, col0, to) in ((pA, uA, c * 128, tb), (pB, uB, 1024 + c * 128, tb + 32)):
                            for kk in range(8):
                                k.op(PE, lambda e, ps=ps, kk=kk, col0=col0: e.matmul(ps[:, 0:512], lhsT=Wc[:, kk, col0:col0 + 128], rhs=hT[:, kk, e0 - 16:e0 + 496], start=(kk == 0), stop=(kk == 7)), r=[uWc] + hr, w=[ups])
                            for kk in range(8):
                                k.op(PE, lambda e, kk=kk, col0=col0, to=to, pTl=pTl: e.matmul(pTl[:, to:to + 32], lhsT=Wc[:, kk, col0:col0 + 128], rhs=hT[:, kk, e0 + 496:e0 + 528], start=(kk == 0), stop=(kk == 7)), r=[uWc] + hr, w=[uTl[b]])
                    for tg in range(4):
                        e0 = 128 + tg * 512
                        hr = uhT[(e0 - 16) // 128:(e0 + 528 + 127) // 128]
                        proj(tg, 0, it % 2)
                        for c in range(8):
                            b = it % 2
                            it += 1
                            pA, uA, pB, uB = PF[b], uPF[b], PF[2 + b], uPF[2 + b]
                            tb = 0
                            pTl = pTls[b]
                            if c < 7:
                                proj(tg, c + 1, it % 2)
                            k.op(A, lambda e, b=b, pB=pB: e.activation(out=sig[b][:, 0:512], in_=pB[:, 0:512], func=AF.Sigmoid), r=[uB], w=[usig[b]])
                            k.op(A, lambda e, b=b, tb=tb, pTl=pTl: e.activation(out=sig[b][:, 512:544], in_=pTl[:, tb + 32:tb + 64], func=AF.Sigmoid), r=[uTl[b]], w=[usig[b]])
                            k.op(V, lambda e, b=b, pA=pA: e.tensor_tensor(out=vv[b][:, 0:512], in0=pA[:, 0:512], in1=sig[b][:, 0:512], op=ALU.mult), r=[uA, usig[b]], w=[uvv[b]])
                            k.op(V, lambda e, b=b, tb=tb, pTl=pTl: e.tensor_tensor(out=vv[b][:, 512:544], in0=pTl[:, tb:tb + 32], in1=sig[b][:, 512:544], op=ALU.mult), r=[uTl[b], usig[b]], w=[uvv[b]])
                            k.op(V, lambda e, c=c, b=b: e.tensor_scalar(out=acc1[b][:], in0=vv[b][:, 1:513], scalar1=prT[:, c, 0:1], scalar2=prT[:, c, 31:32], op0=ALU.mult, op1=ALU.add), r=[uvv[b], uL], w=[uacc1[b]])
                            for kt in range(1, 16):
                                k.op(V, lambda e, c=c, kt=kt, b=b: e.scalar_tensor_tensor(out=acc1[b][:], in0=vv[b][:, kt + 1:kt + 513], scalar=prT[:, c, kt:kt + 1], in1=acc1[b][:], op0=ALU.mult, op1=ALU.add), r=[uvv[b], uL, uacc1[b]], w=[uacc1[b]])
                            k.op(A, lambda e, c=c, b=b: e.activation(out=acc2[b][:], in_=vv[b][:, 17:529], func=AF.Copy, scale=prT[:, c, 16:17]), r=[uvv[b], uL], w=[uacc2[b]])
                            for kt in range(17, 31):
                                j = tk % 4
                                tk += 1
                                k.op(A, lambda e, c=c, kt=kt, j=j, b=b: e.activation(out=tmpk[j][:], in_=vv[b][:, kt + 1:kt + 513], func=AF.Copy, scale=prT[:, c, kt:kt + 1]), r=[uvv[b], uL], w=[utmpk[j]])
                                k.op(G, lambda e, j=j, b=b: e.tensor_tensor(out=acc2[b][:], in0=acc2[b][:], in1=tmpk[j][:], op=ALU.add), r=[utmpk[j], uacc2[b]], w=[uacc2[b]])
                            k.op(G, lambda e, c=c, b=b: e.tensor_tensor(out=yy[:, c, :], in0=acc1[b][:], in1=acc2[b][:], op=ALU.add), r=[uacc1[b], uacc2[b]], w=[uyy[c]])
                        for c in range(8):
                            b = c % 2
                            k.op(A, lambda e, c=c, b=b: e.activation(out=ysq[b][:], in_=yy[:, c, :], func=AF.Square), r=[uyy[c]], w=[uysq[b]])
                            k.op(PE, lambda e, c=c: e.matmul(pST[:], lhsT=onesf[:], rhs=yy[:, c, :], start=(c == 0), stop=(c == 7)), r=[uyy[c], uC], w=[uST])
                            k.op(PE, lambda e, c=c, b=b: e.matmul(pSQ[:], lhsT=onesf[:], rhs=ysq[b][:], start=(c == 0), stop=(c == 7)), r=[uysq[b], uC], w=[uSQ])
                        k.op(V, lambda e: e.tensor_copy(out=mean[:], in_=pST[:]), r=[uST], w=[ustat])
                        k.op(G, lambda e: e.tensor_tensor(out=msq[:], in0=mean[:], in1=mean[:], op=ALU.mult), r=[ustat], w=[ustat])
                        k.op(V, lambda e: e.tensor_tensor(out=rstd[:], in0=pSQ[:], in1=msq[:], op=ALU.subtract), r=[uSQ, ustat], w=[ustat])
                        k.op(V, lambda e: e.tensor_scalar_add(out=rstd[:], in0=rstd[:], scalar1=EPS), r=[ustat], w=[ustat])
                        k.op(A, lambda e: e.activation(out=rstd[:], in_=rstd[:], func=AF.Sqrt), r=[ustat], w=[ustat])
                        k.op(V, lambda e: e.reciprocal(out=rstd[:], in_=rstd[:]), r=[ustat], w=[ustat])
                        for c in range(8):
                            b = c % 2
                            pG, uG = PF[b], uPF[b]
                            for kk in range(8):
                                k.op(PE, lambda e, c=c, kk=kk, pG=pG: e.matmul(pG[:], lhsT=Wc[:, kk, 2048 + c * 128:2048 + (c + 1) * 128], rhs=hT[:, kk, e0:e0 + 512], start=(kk == 0), stop=(kk == 7)), r=[uWc] + hr, w=[uG])
                            k.op(V, lambda e, c=c, b=b: e.tensor_tensor(out=t1[b][:], in0=yy[:, c, :], in1=mean[:], op=ALU.subtract), r=[uyy[c], ustat], w=[ut1[b]])
                            k.op(G, lambda e, b=b: e.tensor_tensor(out=t2[b][:], in0=t1[b][:], in1=rstd[:], op=ALU.mult), r=[ut1[b], ustat], w=[ut2[b]])
                            k.op(A, lambda e, c=c, b=b: e.activation(out=s1[b][:], in_=t2[b][:], func=AF.Silu, bias=prT[:, c, 33:34], scale=prT[:, c, 32:33]), r=[ut2[b], uL], w=[us1[b]])
                            k.op(A, lambda e, b=b, pG=pG: e.activation(out=sg[b][:], in_=pG[:], func=AF.Silu), r=[uG], w=[usg[b]])
                            k.op(V, lambda e, c=c, b=b: e.tensor_tensor(out=ocT[:, c, :], in0=s1[b][:], in1=sg[b][:], op=ALU.mult), r=[us1[b], usg[b]], w=[uocT])
                        k.dma(S, OC[:, :, tg * 512:(tg + 1) * 512].rearrange("c p t -> p c t"), ocT[:], r=[uocT], w=[u_OC])
                    k.barrier()
            stWc.close()
            if 'T' in phases:
                if 'C' not in phases:
                    for j in range(5):
                        k.dma(G, Wt_pre[:, :, j * 512:(j + 1) * 512], w_in[l, :, OFF_AQ + j * 512:OFF_AQ + (j + 1) * 512].rearrange("(k p) c -> p k c", p=128), w=[uWt_pre])
                phase_T(l, Wt_pre, uWt_pre)
            stW.close()
            if 'R' in phases:
                phase_R(l)
            if 'F' in phases:
                phase_F(l, xsrc, last)
        k.barrier()
    return nc


def make_consts(core):
    c = core % 4
    seg = c * T
    p = np.arange(128)
    tt = np.arange(T)
    inv = 10000.0 ** (-np.arange(128, dtype=np.float64) / 128.0)
    ang = inv[:, None] * (seg + tt)[None, :].astype(np.float64)
    cosr = np.cos(ang).astype(np.float32)
    sinr = np.sin(ang).astype(np.float32)
    inva = 500000.0 ** (-np.arange(8, dtype=np.float64) / 8.0)
    pos = (seg - 128 + np.arange(18)[None, :] * 128 + p[:, None]).astype(np.float64)
    anga = pos[:, :, None] * inva[None, None, :]
    cosa = np.cos(anga).astype(np.float32)
    sina = np.sin(anga).astype(np.float32)
    j = p[:, None]
    i = p[None, :]
    maskL = (j >= i).astype(np.float32)
    maskR = (j <= i).astype(np.float32)
    mask = np.concatenate([maskL, maskR, maskL * (1.0 if c > 0 else 0.0), maskR * (1.0 if c < 3 else 0.0)], axis=1)
    eseg = np.zeros((128, 32), np.float32)
    for n in range(16):
        eseg[:, n] = 2047 - (n * 128 + p)
        eseg[:, 16 + n] = n * 128 + p
    ek = np.stack([127 - p, p], axis=1).astype(np.float32)
    eq = np.concatenate([np.tile((np.arange(128) + 1)[None, :], (128, 1)), np.tile((128 - np.arange(128))[None, :], (128, 1))], axis=1).astype(np.float32)
    Ef = np.maximum(i - j, 0); Eb = np.maximum(j - i, 0)
    Mf = (i >= j); Mb = (j > i)
    em = np.concatenate([Ef, Eb, Mf, Mb], axis=1).astype(np.float32)
    BIG = 1.0e6
    dist = np.zeros((128, 8), np.float32)
    sel = np.zeros((128, 8), np.float32)
    for r in range(4):
        dist[:, r] = T * (c - r - 1) if r < c else BIG
        dist[:, 4 + r] = T * (r - c - 1) if r > c else BIG
        sel[:, r] = 1.0 if r == c - 1 else 0.0
        sel[:, 4 + r] = 1.0 if r == c + 1 else 0.0
    return dict(c_cosr=cosr, c_sinr=sinr, c_cosa=cosa, c_sina=sina, c_mask=mask,
                c_ident=np.eye(128, dtype=np.float32), c_eseg=eseg, c_ek=ek, c_eq=eq, c_em=em,
                c_dist=dist, c_sel=sel, c_e128=np.tile((128.0 * np.arange(16, dtype=np.float32))[None, :], (128, 1)))


def make_in_maps(inputs):
    x = np.asarray(inputs['x'], np.float32)
    shared = dict(
        norm_g=np.asarray(inputs['norm_g'], np.float32),
        w_in=np.asarray(inputs['w_in'], np.float32),
        b_gate=np.asarray(inputs['b_gate'], np.float32).reshape(2, 3, D),
        conv_dw=np.asarray(inputs['conv_dw'], np.float32),
        conv_b=np.asarray(inputs['conv_b'], np.float32).reshape(2, 1, D),
        conv_ln_g=np.asarray(inputs['conv_ln_g'], np.float32).reshape(2, 1, D),
        conv_ln_b=np.asarray(inputs['conv_ln_b'], np.float32).reshape(2, 1, D),
        ret_decay=np.asarray(inputs['ret_decay'], np.float32).reshape(2, 8),
        q_norm_g=np.asarray(inputs['q_norm_g'], np.float32),
        k_norm_g=np.asarray(inputs['k_norm_g'], np.float32),
        attn_sink=np.asarray(inputs['attn_sink'], np.float32),
        w_conv_out=np.asarray(inputs['w_conv_out'], np.float32),
        w_ret_out=np.asarray(inputs['w_ret_out'], np.float32),
        w_attn_out=np.asarray(inputs['w_attn_out'], np.float32),
        w_out=np.asarray(inputs['w_out'], np.float32),
    )
    in_maps = []
    for core in range(8):
        b, c = core // 4, core % 4
        xe = np.zeros((TE, D), np.float32)
        lo = c * T - 128
        hi = c * T + T + 128
        slo, shi = max(lo, 0), min(hi, 4 * T)
        xe[slo - lo:shi - lo] = x[b, slo:shi]
        m = dict(shared)
        m['x_ext'] = xe
        m.update(make_consts(core))
        in_maps.append(m)
    return in_maps


_NC = None


def kernel(**inputs):
    global _NC
    if _NC is None:
        _NC = build(2)
    in_maps = make_in_maps(inputs)
    res = run_bass_kernel_spmd(_NC, in_maps, core_ids=list(range(8)))
    out = np.zeros((2, 4 * T, D), np.float32)
    for core in range(8):
        b, c = core // 4, core % 4
        out[b, c * T:(c + 1) * T] = res.results[core]["y_out"]
    return out
```

```python
import os
import numpy as np
import concourse.bass as bass
import concourse.mybir as mybir
from concourse.bass_utils import run_bass_kernel_spmd
from contextlib import ExitStack

F32 = mybir.dt.float32
BF16 = mybir.dt.bfloat16
ALU = mybir.AluOpType
AF = mybir.ActivationFunctionType
AX = mybir.AxisListType

ENG = ['tensor', 'vector', 'scalar', 'gpsimd', 'sync']
EPOCH = 20000
ND = 8
PE, V, A, G, S = 'tensor', 'vector', 'scalar', 'gpsimd', 'sync'


class U:
    __slots__ = ('w', 'rs')

    def __init__(s):
        s.w = None
        s.rs = {}


class KB:
    def __init__(s, nc, stack):
        s.nc = nc
        s.st = stack
        s.cnt = {e: 0 for e in ENG}
        s.nsem = 0
        s.sem = {e: s.new_sem(f'e_{e}') for e in ENG}
        s.hist = {e: [] for e in ENG}
        s.waited = {e: {} for e in ENG}
        s.dsem = {}
        s.dtarget = {}
        s.dcount = {}
        s.n_inst = 0

    def new_sem(s, name):
        s.nsem += 1
        return s.st.enter_context(s.nc.semaphore(f'{name}_{s.nsem}'))

    def _waits(s, engine, r, w):
        deps = {}

        def add(tok):
            key = id(tok[0])
            if key not in deps or deps[key][1] < tok[1]:
                deps[key] = tok
        for u in r:
            if u.w is not None:
                add(u.w)
        for u in w:
            if u.w is not None:
                add(u.w)
            for tok in u.rs.values():
                add(tok)
        waits = []
        wd = s.waited[engine]
        for key, (sem, val, src) in deps.items():
            if engine == PE and src == PE:
                continue
            if wd.get(key, 0) >= val:
                continue
            wd[key] = val
            waits.append((sem, val))
        return waits

    def _emit(s, ename, waits, fn, inc):
        e = getattr(s.nc, ename)
        for sem, val in waits:
            e.wait_ge(sem, val)
        if fn is None:
            return
        ins = fn(e)
        if inc[1] is None:
            ins.then_inc(inc[0])
        else:
            ins.then_inc(inc[0], inc[1])
        s.n_inst += 1

    def op(s, engine, fn, r=(), w=()):
        waits = s._waits(engine, r, w)
        if s.cnt[engine] >= EPOCH:
            s.hist[engine].append((s.sem[engine], s.cnt[engine]))
            s.sem[engine] = s.new_sem(f'e_{engine}')
            s.cnt[engine] = 0
        s.cnt[engine] += 1
        sem = s.sem[engine]
        tok = (sem, s.cnt[engine], engine)
        s._emit(engine, waits, fn, (sem, 1))
        for u in r:
            u.rs[id(sem)] = tok
        for u in w:
            u.w = tok
            u.rs = {}
        return tok

    def dma(s, q, out, in_, r=(), w=(), **kw):
        waits = s._waits(q, r, w)
        if q not in s.dsem:
            s.dsem[q] = [s.new_sem(f'd_{q}{i}') for i in range(ND)]
            s.dtarget[q] = [0] * ND
            s.dcount[q] = 0
        i = s.dcount[q] % ND
        s.dcount[q] += 1
        sem = s.dsem[q][i]
        prev = s.dtarget[q][i]
        if prev > 0 and s.waited[q].get(id(sem), 0) < prev:
            s.waited[q][id(sem)] = prev
            waits.append((sem, prev))
        tgt = prev + 16
        s.dtarget[q][i] = tgt
        tok = (sem, tgt, 'dma')
        s._emit(q, waits, lambda e: e.dma_start(out=out, in_=in_, **kw), (sem, 16))
        for u in r:
            u.rs[id(sem)] = tok
        for u in w:
            u.w = tok
            u.rs = {}
        return tok

    def custom(s, engine, fn, inc_sem, inc_val, r=(), w=()):
        waits = s._waits(engine, r, w)
        tok = (inc_sem, inc_val, 'custom')
        s._emit(engine, waits, fn, (inc_sem, None))
        for u in r:
            u.rs[id(inc_sem)] = tok
        for u in w:
            u.w = tok
            u.rs = {}
        return tok

    def barrier(s):
        toks = []
        for e in ENG:
            for sem, c in s.hist[e]:
                toks.append((sem, c))
            if s.cnt[e] > 0:
                toks.append((s.sem[e], s.cnt[e]))
        for q in s.dsem:
            for i in range(ND):
                if s.dtarget[q][i] > 0:
                    toks.append((s.dsem[q][i], s.dtarget[q][i]))
        for e in ENG:
            wd = s.waited[e]
            waits = []
            for sem, val in toks:
                if wd.get(id(sem), 0) >= val:
                    continue
                wd[id(sem)] = val
                waits.append((sem, val))
            s._emit(e, waits, None, None)


D = 1024
T = 2048
TE = 2304
NT = 16
INW = 14848
OFF_CGLU, OFF_CGATE = 0, 2048
OFF_RQ, OFF_RK, OFF_RV, OFF_RG = 3072, 4096, 5120, 7168
OFF_AQ, OFF_AK, OFF_AV, OFF_AG = 9216, 10240, 10496, 10752
OFF_GL = 11776
EPS = 1e-6


def build(NL=2, dbg=False, phases='CTRF'):
    nc = bass.Bass("TRN2", target_bir_lowering=False)

    def din(name, shape):
        return nc.dram_tensor(name, shape, F32, kind="ExternalInput").ap()

    x_ext = din("x_ext", [TE, D])
    norm_g = din("norm_g", [2, D])
    w_in = din("w_in", [2, D, INW])
    b_gate = din("b_gate", [2, 3, D])
    conv_dw = din("conv_dw", [2, 31, D])
    conv_b = din("conv_b", [2, 1, D])
    conv_ln_g = din("conv_ln_g", [2, 1, D])
    conv_ln_b = din("conv_ln_b", [2, 1, D])
    ret_decay = din("ret_decay", [2, 8])
    q_norm_g = din("q_norm_g", [2, 64])
    k_norm_g = din("k_norm_g", [2, 64])
    attn_sink = din("attn_sink", [2, 16])
    w_conv_out = din("w_conv_out", [2, D, D])
    w_ret_out = din("w_ret_out", [2, 2 * D, D])
    w_attn_out = din("w_attn_out", [2, D, D])
    w_out = din("w_out", [2, D, D])
    c_cosr = din("c_cosr", [128, T])
    c_sinr = din("c_sinr", [128, T])
    c_cosa = din("c_cosa", [128, 18, 8])
    c_sina = din("c_sina", [128, 18, 8])
    c_mask = din("c_mask", [128, 512])
    c_ident = din("c_ident", [128, 128])
    c_eseg = din("c_eseg", [128, 32])
    c_ek = din("c_ek", [128, 2])
    c_eq = din("c_eq", [128, 256])
    c_em = din("c_em", [128, 512])
    c_dist = din("c_dist", [128, 8])
    c_sel = din("c_sel", [128, 8])
    c_e128 = din("c_e128", [128, 16])

    y_out = nc.dram_tensor("y_out", [T, D], F32, kind="ExternalOutput").ap()
    okind = "ExternalOutput" if dbg else "Internal"
    OC = nc.dram_tensor("OC", [8, 128, T], BF16, kind=okind).ap()
    OA = nc.dram_tensor("OA", [16, 64, T], BF16, kind=okind).ap()
    OR = nc.dram_tensor("OR", [16, 128, T], BF16, kind=okind).ap()
    x1e = nc.dram_tensor("x1e", [TE, D], F32, kind=okind).ap()
    fs_bounce = nc.dram_tensor("fs_bounce", [512, 512], F32)
    fs_gath = nc.dram_tensor("fs_gath", [2048, 512], F32)
    h_bounce = nc.dram_tensor("h_bounce", [256, D], F32)
    h_gath = nc.dram_tensor("h_gath", [1024, D], F32)
    u_OC, u_OA, u_OR, u_x1e, u_fsb, u_fsg, u_hb, u_hg, u_yout = [U() for _ in range(9)]
    RG = [[0, 1, 2, 3], [4, 5, 6, 7]]

    with ExitStack() as st0:
        k = KB(nc, st0)

        ncount = [0]

        def sb(st, name, shape, dt):
            ncount[0] += 1
            return st.enter_context(nc.sbuf_tensor(f"{name}_{ncount[0]}", shape, dt))

        PF = [st0.enter_context(nc.psum_tensor(f"pf{i}", [128, 512], F32)) for i in range(7)]
        uPF = [U() for _ in range(7)]
        PT = st0.enter_context(nc.psum_tensor("ptb", [128, 8, 128], BF16))
        uPT = U()
        hT = sb(st0, "hT", [128, 8, TE], BF16)
        uhT = [U() for _ in range(18)]
        ident = sb(st0, "ident", [128, 128], F32); identb = sb(st0, "identb", [128, 128], BF16)
        onesf = sb(st0, "onesf", [128, 128], F32); onesb = sb(st0, "onesb", [128, 128], BF16)
        maskb = sb(st0, "maskb", [128, 512], BF16)
        eseg = sb(st0, "eseg", [128, 32], F32); ek = sb(st0, "ek", [128, 2], F32)
        eq = sb(st0, "eq", [128, 256], F32); em = sb(st0, "em", [128, 512], F32)
        dist = sb(st0, "dist", [128, 8], F32); sel = sb(st0, "sel", [128, 8], F32)
        e128 = sb(st0, "e128", [128, 16], F32)
        cosa = sb(st0, "cosa", [128, 18, 8], F32); sina = sb(st0, "sina", [128, 18, 8], F32)
        lg = sb(st0, "lg", [128, 8], F32)
        qg = sb(st0, "qg", [128, 64], F32); kg = sb(st0, "kg", [128, 64], F32)
        snk = sb(st0, "snk", [128, 16], F32); snke = sb(st0, "snke", [128, 16], F32)
        negc = sb(st0, "negc", [128, 1], F32); tmpc = sb(st0, "tmpc", [128, 4], F32)
        g2 = sb(st0, "g2", [128, 128], F32)
        prm = sb(st0, "prm", [37, D], F32)
        prT = sb(st0, "prT", [128, 8, 37], F32)
        uC = U()
        uL = U()

        for (t, src) in ((ident, c_ident), (eseg, c_eseg), (ek, c_ek), (eq, c_eq), (em, c_em),
                         (dist, c_dist), (sel, c_sel), (cosa, c_cosa), (sina, c_sina), (e128, c_e128)):
            k.dma(S, t[:], src, w=[uC])
        k.dma(G, identb[:], c_ident, w=[uC])
        k.dma(G, maskb[:], c_mask, w=[uC])
        k.op(V, lambda e: e.memset(onesf[:], 1.0 / 1024.0), w=[uC])
        k.op(V, lambda e: e.memset(onesb[:], 1.0), w=[uC])
        k.barrier()

        def phase_T(l, Wt, uWt):
            with ExitStack() as st:
                qTg = sb(st, "qTg", [64, 16, 512], BF16); uqT = [U() for _ in range(4)]
                kT = sb(st, "kTres", [64, 4, TE], BF16); ukT = [U() for _ in range(18)]
                Vr = sb(st, "Vres", [128, 18, 256], BF16); uVr = [U() for _ in range(18)]
                sq = [sb(st, f"sq{i}", [128, 1280], F32) for i in range(2)]; usq = [U(), U()]
                ssa = [sb(st, f"ssa{i}", [128, 20], F32) for i in range(2)]; rsa = [sb(st, f"rsa{i}", [128, 20], F32) for i in range(2)]; urs = [U(), U()]
                qn = [sb(st, f"qn{i}", [128, 1280], F32) for i in range(2)]; uqn = [U(), U()]
                rt = [[sb(st, f"rt{p}{i}", [128, 20, 8], F32) for i in range(4)] for p in range(2)]; urt = [[U() for _ in range(4)] for _ in range(2)]
                qb = [sb(st, f"qb{i}", [128, 1024], BF16) for i in range(2)]; uqb = [U(), U()]
                kb = [sb(st, f"kb{i}", [128, 256], BF16) for i in range(2)]; ukb = [U(), U()]
                sq3 = [sq[p][:].rearrange("p (h d) -> p h d", d=64) for p in range(2)]
                qn3 = [qn[p][:].rearrange("p (h d) -> p h d", d=64) for p in range(2)]

                def norm_rot(t, h0, h1, p, banks):
                    nh = h1 - h0
                    k.op(V, lambda e: e.tensor_reduce(out=ssa[p][:, h0:h1], in_=sq3[p][:, h0:h1, :], axis=AX.X, op=ALU.add), r=[usq[p]], w=[urs[p]])
                    k.op(V, lambda e: e.tensor_scalar(out=rsa[p][:, h0:h1], in0=ssa[p][:, h0:h1], scalar1=1.0 / 64.0, scalar2=EPS, op0=ALU.mult, op1=ALU.add), r=[urs[p]], w=[urs[p]])
                    k.op(A, lambda e: e.activation(out=rsa[p][:, h0:h1], in_=rsa[p][:, h0:h1], func=AF.Sqrt), r=[urs[p]], w=[urs[p]])
                    k.op(V, lambda e: e.reciprocal(out=rsa[p][:, h0:h1], in_=rsa[p][:, h0:h1]), r=[urs[p]], w=[urs[p]])
                    if h0 == 0:
                        for half in range(2):
                            bk, ubk = banks[half]
                            k.op(V, lambda e, half=half, bk=bk: e.tensor_tensor(out=qn3[p][:, half * 8:(half + 1) * 8, :], in0=bk[:].rearrange("p (h d) -> p h d", d=64), in1=rsa[p][:, half * 8:(half + 1) * 8].unsqueeze(2).broadcast_to([128, 8, 64]), op=ALU.mult), r=[ubk, urs[p]], w=[uqn[p]])
                        k.op(G, lambda e: e.tensor_tensor(out=qn3[p][:, 0:16, :], in0=qn3[p][:, 0:16, :], in1=qg[:].unsqueeze(1).broadcast_to([128, 16, 64]), op=ALU.mult), r=[uqn[p], uL], w=[uqn[p]])
                    else:
                        bk, ubk = banks[0]
                        k.op(V, lambda e, bk=bk: e.tensor_tensor(out=qn3[p][:, 16:20, :], in0=bk[:, 0:256].rearrange("p (h d) -> p h d", d=64), in1=rsa[p][:, 16:20].unsqueeze(2).broadcast_to([128, 4, 64]), op=ALU.mult), r=[ubk, urs[p]], w=[uqn[p]])
                        k.op(G, lambda e: e.tensor_tensor(out=qn3[p][:, 16:20, :], in0=qn3[p][:, 16:20, :], in1=kg[:].unsqueeze(1).broadcast_to([128, 4, 64]), op=ALU.mult), r=[uqn[p], uL], w=[uqn[p]])
                    cb = cosa[:, t, :].unsqueeze(1).broadcast_to([128, nh, 8])
                    sbb = sina[:, t, :].unsqueeze(1).broadcast_to([128, nh, 8])
                    x1 = qn3[p][:, h0:h1, 0:8]
                    x2 = qn3[p][:, h0:h1, 8:16]
                    r_, ur_ = rt[p], urt[p]
                    k.op(V, lambda e: e.tensor_tensor(out=r_[0][:, h0:h1, :], in0=x1, in1=cb, op=ALU.mult), r=[uqn[p], uC], w=[ur_[0]])
                    k.op(G, lambda e: e.tensor_tensor(out=r_[1][:, h0:h1, :], in0=x2, in1=sbb, op=ALU.mult), r=[uqn[p], uC], w=[ur_[1]])
                    k.op(V, lambda e: e.tensor_tensor(out=r_[2][:, h0:h1, :], in0=x2, in1=cb, op=ALU.mult), r=[uqn[p], uC], w=[ur_[2]])
                    k.op(G, lambda e: e.tensor_tensor(out=r_[3][:, h0:h1, :], in0=x1, in1=sbb, op=ALU.mult), r=[uqn[p], uC], w=[ur_[3]])
                    k.op(V, lambda e: e.tensor_tensor(out=x1, in0=r_[0][:, h0:h1, :], in1=r_[1][:, h0:h1, :], op=ALU.subtract), r=[ur_[0], ur_[1], ur_[2], ur_[3]], w=[uqn[p]])
                    k.op(G, lambda e: e.tensor_tensor(out=x2, in0=r_[2][:, h0:h1, :], in1=r_[3][:, h0:h1, :], op=ALU.add), r=[ur_[2], ur_[3]], w=[uqn[p]])

                def kv_s1(t):
                    p = t % 2
                    bk, ubk = PF[2 + p], uPF[2 + p]
                    for kk in range(8):
                        k.op(PE, lambda e, kk=kk: e.matmul(bk[:], lhsT=hT[:, kk, t * 128:(t + 1) * 128], rhs=Wt[:, kk, 1024:1536], start=(kk == 0), stop=(kk == 7)), r=[uWt, uhT[t]], w=[ubk])
                    k.op(A, lambda e: e.activation(out=sq[p][:, 1024:1280], in_=bk[:, 0:256], func=AF.Square), r=[ubk], w=[usq[p]])
                    norm_rot(t, 16, 20, p, [(bk, ubk)])
                    k.op(A, lambda e: e.activation(out=kb[p][:], in_=qn[p][:, 1024:1280], func=AF.Copy), r=[uqn[p]], w=[ukb[p]])
                    k.op(A, lambda e: e.activation(out=Vr[:, t, :], in_=bk[:, 256:512], func=AF.Copy), r=[ubk], w=[uVr[t]])

                def kv_s2(t):
                    p = t % 2
                    for g in range(4):
                        k.op(PE, lambda e, g=g: e.transpose(out=PT[0:64, g, :], in_=kb[p][:, g * 64:(g + 1) * 64], identity=identb[:]), r=[ukb[p], uC], w=[uPT])
                    k.op(V, lambda e: e.tensor_copy(out=kT[:, :, t * 128:(t + 1) * 128], in_=PT[0:64, 0:4, :]), r=[uPT], w=[ukT[t]])
                kv_s1(0)
                for t in range(18):
                    if t < 17:
                        kv_s1(t + 1)
                    kv_s2(t)
                if dbg == 'T1':
                    k.barrier()
                    return
                gT = sb(st, "gT", [64, 16, 512], BF16); ugT = U()
                pt = [sb(st, f"pt{i}", [128, 512], BF16) for i in range(3)]; upt = [U() for _ in range(3)]
                den = sb(st, "den", [64, 512], F32); uden = U()
                rec = sb(st, "rec", [64, 512], F32); urec = U()
                on = sb(st, "on", [64, 512], F32); uon = U()
                on2 = [on, on]; uon2 = [uon, uon]
                pt_b = [sb(st, f"ptb{i}", [128, 512], BF16) for i in range(3)]; upt_b = [U() for _ in range(3)]
                pt2 = [pt, pt_b]; upt2 = [upt, upt_b]
                oaT = sb(st, "oaT", [64, 16, 512], BF16); uoaT = U()
                for tg in range(4):
                    e0 = 128 + tg * 512
                    hr = uhT[e0 // 128:(e0 + 512) // 128]
                    def q_s1(nn):
                        p = nn % 2
                        t = tg * 4 + nn + 1
                        banks = [(PF[2 * p], uPF[2 * p]), (PF[2 * p + 1], uPF[2 * p + 1])]
                        for half in range(2):
                            bk, ubk = banks[half]
                            for kk in range(8):
                                k.op(PE, lambda e, half=half, kk=kk, bk=bk: e.matmul(bk[:], lhsT=hT[:, kk, t * 128:(t + 1) * 128], rhs=Wt[:, kk, half * 512:(half + 1) * 512], start=(kk == 0), stop=(kk == 7)), r=[uWt, uhT[t]], w=[ubk])
                            k.op(A, lambda e, half=half, bk=bk: e.activation(out=sq[p][:, half * 512:(half + 1) * 512], in_=bk[:], func=AF.Square), r=[ubk], w=[usq[p]])
                        norm_rot(t, 0, 16, p, banks)
                        k.op(A, lambda e: e.activation(out=qb[p][:], in_=qn[p][:, 0:1024], func=AF.Copy), r=[uqn[p]], w=[uqb[p]])

                    def q_s2(nn):
                        p = nn % 2
                        for rr in range(2):
                            for j in range(8):
                                hh = rr * 8 + j
                                k.op(PE, lambda e, j=j, hh=hh: e.transpose(out=PT[0:64, j, :], in_=qb[p][:, hh * 64:(hh + 1) * 64], identity=identb[:]), r=[uqb[p], uC], w=[uPT])
                            k.op(V, lambda e, rr=rr: e.tensor_copy(out=qTg[:, rr * 8:(rr + 1) * 8, nn * 128:(nn + 1) * 128], in_=PT[0:64, :, :]), r=[uPT], w=[uqT[nn]])
                    q_s1(0)
                    for nn in range(4):
                        if nn < 3:
                            q_s1(nn + 1)
                        q_s2(nn)
                    for hh in range(16):
                        ps, ups = PF[hh % 2], uPF[hh % 2]
                        for kk in range(8):
                            k.op(PE, lambda e, ps=ps, hh=hh, kk=kk, e0=e0: e.matmul(ps[0:64, :], lhsT=Wt[:, kk, 1536 + hh * 64:1536 + (hh + 1) * 64], rhs=hT[:, kk, e0:e0 + 512], start=(kk == 0), stop=(kk == 7)), r=[uWt] + hr, w=[ups])
                        k.op(A, lambda e, ps=ps, hh=hh: e.activation(out=gT[:, hh, :], in_=ps[0:64, :], func=AF.Silu), r=[ups], w=[ugT])
                    def t2_front(idx, nn, g):
                        n = tg * 4 + nn
                        t = n + 1
                        ptc, uptc = pt2[idx % 2], upt2[idx % 2]
                        for mi, m in enumerate((t - 1, t, t + 1)):
                            k.op(PE, lambda e, mi=mi, m=m: e.matmul(PF[2 + mi][:], lhsT=kT[:, g, m * 128:(m + 1) * 128], rhs=qTg[:, 4 * g:4 * g + 4, nn * 128:(nn + 1) * 128], start=True, stop=True), r=[ukT[m], uqT[nn]], w=[uPF[2 + mi]])
                            k.op(A, lambda e, mi=mi: e.activation(out=ptc[mi][:], in_=PF[2 + mi][:], func=AF.Exp, bias=negc[:, 0:1], scale=0.125), r=[uPF[2 + mi], uL], w=[uptc[mi]])
                        mL = 256 if t == 1 else 0
                        mR = 384 if t == 16 else 128
                        k.op(V, lambda e: e.tensor_tensor(out=ptc[0][:].rearrange("p (a i) -> p a i", a=4), in0=ptc[0][:].rearrange("p (a i) -> p a i", a=4), in1=maskb[:, mL:mL + 128].unsqueeze(1).broadcast_to([128, 4, 128]), op=ALU.mult), r=[uptc[0], uC], w=[uptc[0]])
                        k.op(G, lambda e: e.tensor_tensor(out=ptc[2][:].rearrange("p (a i) -> p a i", a=4), in0=ptc[2][:].rearrange("p (a i) -> p a i", a=4), in1=maskb[:, mR:mR + 128].unsqueeze(1).broadcast_to([128, 4, 128]), op=ALU.mult), r=[uptc[2], uC], w=[uptc[2]])

                    def t2_back(idx, nn, g):
                        n = tg * 4 + nn
                        t = n + 1
                        ptc, uptc = pt2[idx % 2], upt2[idx % 2]
                        onc, uonc = on2[idx % 2], uon2[idx % 2]
                        for mi, m in enumerate((t - 1, t, t + 1)):
                            k.op(PE, lambda e, mi=mi, m=m: e.matmul(PF[5][0:64, :], lhsT=Vr[:, m, g * 64:(g + 1) * 64], rhs=ptc[mi][:], start=(mi == 0), stop=(mi == 2)), r=[uVr[m], uptc[mi]], w=[uPF[5]])
                        for mi, m in enumerate((t - 1, t, t + 1)):
                            k.op(PE, lambda e, mi=mi: e.matmul(PF[6][0:64, :], lhsT=onesb[:, 0:64], rhs=ptc[mi][:], start=(mi == 0), stop=(mi == 2)), r=[uC, uptc[mi]], w=[uPF[6]])
                        for ei in range(4):
                            hd = 4 * g + ei
                            k.op(A, lambda e, ei=ei, hd=hd: e.activation(out=den[:, ei * 128:(ei + 1) * 128], in_=PF[6][0:64, ei * 128:(ei + 1) * 128], func=AF.Ln, bias=snke[0:64, hd:hd + 1], scale=1.0), r=[uPF[6], uL], w=[uden])
                        k.op(A, lambda e: e.activation(out=rec[:], in_=den[:], func=AF.Exp, scale=-1.0), r=[uden], w=[urec])
                        k.op(V, lambda e: e.tensor_tensor(out=onc[:], in0=PF[5][0:64, :], in1=rec[:], op=ALU.mult), r=[uPF[5], urec], w=[uonc])
                        k.op(G, lambda e: e.tensor_tensor(out=oaT[:, 4 * g:4 * g + 4, nn * 128:(nn + 1) * 128], in0=onc[:].rearrange("p (a i) -> p a i", a=4), in1=gT[:, 4 * g:4 * g + 4, nn * 128:(nn + 1) * 128], op=ALU.mult), r=[uonc, ugT], w=[uoaT])
                    its = [(nn, g) for nn in range(4) for g in range(4)]
                    for idx, (nn, g) in enumerate(its):
                        t2_front(idx, nn, g)
                        if idx > 0:
                            t2_back(idx - 1, *its[idx - 1])
                    t2_back(len(its) - 1, *its[-1])
                    k.dma(S, OA[:, :, tg * 512:(tg + 1) * 512].rearrange("h d t -> d h t"), oaT[:], r=[uoaT], w=[u_OA])
                k.barrier()

        def phase_R(l):
            with ExitStack() as st:
                Wr = sb(st, "Wr", [128, 8, 2048], BF16); uWr = U()
                qT = sb(st, "rqT", [128, 2, T], BF16); uq = [U() for _ in range(4)]
                kT = sb(st, "rkT", [128, 2, T], BF16); ukk = [U() for _ in range(4)]
                Vv = sb(st, "rV", [128, 16, 512], BF16); uV = [U() for _ in range(16)]
                kt = sb(st, "rkt", [128, 16, 256], BF16); ukt = [U() for _ in range(16)]
                Pp = sb(st, "rP", [128, 16, 512], F32); uP = [U() for _ in range(16)]
                cs = sb(st, "rcs", [128, 2, 512], F32); ucs = U()
                tq = [sb(st, f"rtq{i}", [128, 512], F32) for i in range(4)]; utq = [U() for _ in range(4)]
                Sf = sb(st, "Sf", [128, 2, 512], F32); Sb = sb(st, "Sb", [128, 2, 512], F32); uSf = U(); uSb = U()
                Sfb = sb(st, "Sfb", [128, 2, 512], BF16); Sbb = sb(st, "Sbb", [128, 2, 512], BF16); uSfb = U(); uSbb = U()
                fsum = sb(st, "fsum", [128, 4, 512], F32); ufs = U()
                DTm = sb(st, "DTm", [128, 128], F32); dq = sb(st, "dq", [128, 256], F32); dsg = sb(st, "dsg", [128, 32], F32)
                dk = sb(st, "dk", [128, 2], F32); gC = sb(st, "gC", [128, 2], F32); wr = sb(st, "wr", [128, 8], F32)
                tmpD = sb(st, "tmpD", [128, 256], F32); uH = U()
                kfs = sb(st, "kfs", [128, 256], BF16); kbs = sb(st, "kbs", [128, 256], BF16); ukfs = U(); ukbs = U()
                qd = sb(st, "qd", [128, 2, 128], BF16); uqd = U()
                kdc = sb(st, "kdc", [128, 256], BF16); ukdc = U()
                sd = sb(st, "sd", [128, 128], BF16); usd = U()
                oo = sb(st, "oo", [128, 512], F32); uoo = U()
                onn = sb(st, "onn", [128, 512], F32); uonn = U()
                sgg = sb(st, "sgg", [128, 512], F32); usgg = U()
                og = sb(st, "og", [128, 512], BF16); uog = U()
                og_b = sb(st, "og_b", [128, 512], BF16); uog_b = U()
                qd_b = sb(st, "qd_b", [128, 2, 128], BF16); uqd_b = U()
                qd_c = sb(st, "qd_c", [128, 2, 128], BF16); uqd_c = U()
                qd_d = sb(st, "qd_d", [128, 2, 128], BF16); uqd_d = U()
                oo_b, uoo_b = tq[0], utq[0]
                onn_b, uonn_b = tq[1], utq[1]
                sgg_b, usgg_b = tq[2], utq[2]
                bst_b = sb(st, "bst_b", [128, 6], F32); mv_b = sb(st, "mv_b", [128, 2], F32); ubn_b = U()
                gn = sb(st, "gn", [128, 32], F32)
                bst = sb(st, "bst", [128, 6], F32); mv = sb(st, "mv", [128, 2], F32); ubn = U()
                orT = sb(st, "orT", [128, 4, 512], BF16); uorT = U()
                RSTOP = int(os.environ.get("RSTOP", "0"))

                class _Stop(Exception):
                    pass

                def ck(n_):
                    if RSTOP == n_:
                        raise _Stop()
                try:
                  for h in range(4):
                      wl = [(1024, OFF_RV + h * 512), (1536, OFF_RG + h * 512)]
                      if h % 2 == 0:
                          wl = [(0, OFF_RQ + h * 256), (512, OFF_RK + h * 256)] + wl
                      for (dst, off) in wl:
                          k.dma(G, Wr[:, :, dst:dst + 512], w_in[l, :, off:off + 512].rearrange("(k p) c -> p k c", p=128), w=[uWr])
                      qc0 = (h % 2) * 256
                      kc0 = 512 + (h % 2) * 256
                      lgf = lg[:, h:h + 1]
                      lgb = lg[:, 4 + h:5 + h]
                      hc = dict(r=[uL, uC, uH], w=[uH])
                      k.op(A, lambda e: e.activation(out=dsg[:, 0:16], in_=eseg[:, 0:16], func=AF.Exp, scale=lgf), **hc)
                      k.op(A, lambda e: e.activation(out=dsg[:, 16:32], in_=eseg[:, 16:32], func=AF.Exp, scale=lgb), **hc)
                      k.op(A, lambda e: e.activation(out=dk[:, 0:1], in_=ek[:, 0:1], func=AF.Exp, scale=lgf), **hc)
                      k.op(A, lambda e: e.activation(out=dk[:, 1:2], in_=ek[:, 1:2], func=AF.Exp, scale=lgb), **hc)
                      k.op(A, lambda e: e.activation(out=dq[:, 0:128], in_=eq[:, 0:128], func=AF.Exp, scale=lgf), **hc)
                      k.op(A, lambda e: e.activation(out=dq[:, 128:256], in_=eq[:, 128:256], func=AF.Exp, scale=lgb), **hc)
                      k.op(V, lambda e: e.tensor_scalar_mul(out=dq[:], in0=dq[:], scalar1=1.0 / 16.0), **hc)
                      k.op(A, lambda e: e.activation(out=gn[:, 0:16], in_=e128[:], func=AF.Exp, scale=lgf), **hc)
                      k.op(A, lambda e: e.activation(out=gn[:, 16:32], in_=e128[:], func=AF.Exp, scale=lgb), **hc)
                      k.op(A, lambda e: e.activation(out=gC[:, 0:1], in_=lgf, func=AF.Exp, scale=128.0), **hc)
                      k.op(A, lambda e: e.activation(out=gC[:, 1:2], in_=lgb, func=AF.Exp, scale=128.0), **hc)
                      k.op(A, lambda e: e.activation(out=wr[:, 0:4], in_=dist[:, 0:4], func=AF.Exp, scale=lgf), **hc)
                      k.op(A, lambda e: e.activation(out=wr[:, 4:8], in_=dist[:, 4:8], func=AF.Exp, scale=lgb), **hc)
                      k.op(A, lambda e: e.activation(out=tmpD[:, 0:128], in_=em[:, 0:128], func=AF.Exp, scale=lgf), **hc)
                      k.op(A, lambda e: e.activation(out=tmpD[:, 128:256], in_=em[:, 128:256], func=AF.Exp, scale=lgb), **hc)
                      k.op(V, lambda e: e.tensor_tensor(out=tmpD[:], in0=tmpD[:], in1=em[:, 256:512], op=ALU.mult), **hc)
                      k.op(V, lambda e: e.tensor_tensor(out=DTm[:], in0=tmpD[:, 0:128], in1=tmpD[:, 128:256], op=ALU.add), **hc)
                      k.op(V, lambda e: e.tensor_scalar_mul(out=DTm[:], in0=DTm[:], scalar1=1.0 / 16.0), **hc)
                      ck(1)
                      def r0_proj(tg, col0, dstT, ud):
                          e0 = 128 + tg * 512
                          hr = uhT[e0 // 128:(e0 + 512) // 128]
                          for dc in range(2):
                              for kk in range(8):
                                  k.op(PE, lambda e, dc=dc, kk=kk: e.matmul(PF[dc][:], lhsT=Wr[:, kk, col0 + dc * 128:col0 + (dc + 1) * 128], rhs=hT[:, kk, e0:e0 + 512], start=(kk == 0), stop=(kk == 7)), r=[uWr] + hr, w=[uPF[dc]])
                          k.op(V, lambda e: e.tensor_tensor(out=tq[0][:], in0=PF[0][:], in1=cs[:, 0, :], op=ALU.mult), r=[uPF[0], ucs], w=[utq[0]])
                          k.op(V, lambda e: e.tensor_tensor(out=tq[1][:], in0=PF[1][:], in1=cs[:, 1, :], op=ALU.mult), r=[uPF[1], ucs], w=[utq[1]])
                          k.op(V, lambda e: e.tensor_tensor(out=tq[2][:], in0=PF[1][:], in1=cs[:, 0, :], op=ALU.mult), r=[uPF[1], ucs], w=[utq[2]])
                          k.op(V, lambda e: e.tensor_tensor(out=tq[3][:], in0=PF[0][:], in1=cs[:, 1, :], op=ALU.mult), r=[uPF[0], ucs], w=[utq[3]])
                          k.op(V, lambda e: e.tensor_tensor(out=dstT[:, 0, tg * 512:(tg + 1) * 512], in0=tq[0][:], in1=tq[1][:], op=ALU.subtract), r=[utq[0], utq[1]], w=[ud])
                          k.op(G, lambda e: e.tensor_tensor(out=dstT[:, 1, tg * 512:(tg + 1) * 512], in0=tq[2][:], in1=tq[3][:], op=ALU.add), r=[utq[2], utq[3]], w=[ud])

                      def r0_front(tg):
                          k.dma(S, cs[:, 0, :], c_cosr[:, tg * 512:(tg + 1) * 512], w=[ucs])
                          k.dma(S, cs[:, 1, :], c_sinr[:, tg * 512:(tg + 1) * 512], w=[ucs])
                          r0_proj(tg, qc0, qT, uq[tg])
                          for nn in range(4):
                              n = tg * 4 + nn
                              t = n + 1
                              for kk in range(8):
                                  k.op(PE, lambda e, kk=kk, t=t: e.matmul(PF[2][:], lhsT=hT[:, kk, t * 128:(t + 1) * 128], rhs=Wr[:, kk, 1024:1536], start=(kk == 0), stop=(kk == 7)), r=[uWr, uhT[t]], w=[uPF[2]])
                              k.op(A, lambda e, n=n: e.activation(out=Vv[:, n, :], in_=PF[2][:], func=AF.Copy), r=[uPF[2]], w=[uV[n]])
                          r0_proj(tg, kc0, kT, ukk[tg])

                      def r0_back(tg):
                          for nn in range(4):
                              n = tg * 4 + nn
                              for dc in range(2):
                                  k.op(PE, lambda e, dc=dc, n=n: e.transpose(out=PT[:, dc, :], in_=kT[:, dc, n * 128:(n + 1) * 128], identity=identb[:]), r=[ukk[tg], uC], w=[uPT])
                              ptv = PT[:, 0:2, :]
                              k.op(A, lambda e, n=n, ptv=ptv: e.activation(out=kfs[:].rearrange("p (a d) -> p a d", a=2), in_=ptv, func=AF.Copy, scale=dsg[:, n:n + 1]), r=[uPT, uH], w=[ukfs])
                              k.op(A, lambda e, n=n, ptv=ptv: e.activation(out=kbs[:].rearrange("p (a d) -> p a d", a=2), in_=ptv, func=AF.Copy, scale=dsg[:, 16 + n:17 + n]), r=[uPT, uH], w=[ukbs])
                              k.op(A, lambda e, n=n, ptv=ptv: e.activation(out=kt[:, n, :].rearrange("p (a d) -> p a d", a=2), in_=ptv, func=AF.Copy), r=[uPT], w=[ukt[n]])
                              for dc in range(2):
                                  k.op(PE, lambda e, dc=dc, n=n: e.matmul(PF[3 + dc][:], lhsT=kfs[:, dc * 128:(dc + 1) * 128], rhs=Vv[:, n, :], start=(n == 0), stop=(n == 15)), r=[ukfs, uV[n]], w=[uPF[3 + dc]])
                                  k.op(PE, lambda e, dc=dc, n=n: e.matmul(PF[5 + dc][:], lhsT=kbs[:, dc * 128:(dc + 1) * 128], rhs=Vv[:, n, :], start=(n == 0), stop=(n == 15)), r=[ukbs, uV[n]], w=[uPF[5 + dc]])
                      r0_front(0)
                      for tg in range(4):
                          if tg < 3:
                              r0_front(tg + 1)
                          r0_back(tg)
                      if dbg == 'R0':
                          k.barrier()
                          return
                      for j in range(4):
                          k.op(A, lambda e, j=j: e.activation(out=fsum[:, j, :], in_=PF[3 + j][:], func=AF.Copy), r=[uPF[3 + j]], w=[ufs])
                      k.dma(S, fs_bounce.ap().rearrange("(j p) v -> p j v", p=128), fsum[:], r=[ufs], w=[u_fsb])
                      ccs = k.new_sem("cc")
                      k.custom(G, lambda e: e.collective_compute("AllGather", ALU.bypass, replica_groups=RG, ins=[fs_bounce.ap().opt()], outs=[fs_gath.ap().opt()]), ccs, 1, r=[u_fsb], w=[u_fsg])
                      k.op(V, lambda e: e.memset(Sf[:], 0.0), r=[uSf], w=[uSf])
                      k.op(V, lambda e: e.memset(Sb[:], 0.0), r=[uSb], w=[uSb])
                      k.op(V, lambda e: e.memset(Sfb[:], 0.0), r=[uSfb], w=[uSfb])
                      k.op(V, lambda e: e.memset(Sbb[:], 0.0), r=[uSbb], w=[uSbb])
                      for n in range(15, -1, -1):
                          tg = n // 4
                          k.op(A, lambda e, n=n: e.activation(out=kdc[:], in_=kt[:, n, :], func=AF.Copy, scale=dk[:, 1:2]), r=[ukt[n], uH], w=[ukdc])
                          for dc in range(2):
                              k.op(PE, lambda e, dc=dc, n=n: e.matmul(PF[1 + dc][:], lhsT=kdc[:, dc * 128:(dc + 1) * 128], rhs=Vv[:, n, :], start=True, stop=True), r=[ukdc, uV[n]], w=[uPF[1 + dc]])
                          k.op(V, lambda e, n=n: e.tensor_tensor(out=qd[:], in0=qT[:, :, n * 128:(n + 1) * 128], in1=dq[:, 128:256].unsqueeze(1).broadcast_to([128, 2, 128]), op=ALU.mult), r=[uq[tg], uH], w=[uqd])
                          for dc in range(2):
                              k.op(PE, lambda e, dc=dc: e.matmul(PF[0][:], lhsT=qd[:, dc, :], rhs=Sbb[:, dc, :], start=(dc == 0), stop=(dc == 1)), r=[uqd, uSbb], w=[uPF[0]])
                          for dc in range(2):
                              k.op(V, lambda e, dc=dc: e.scalar_tensor_tensor(out=Sb[:, dc, :], in0=Sb[:, dc, :], scalar=gC[:, 1:2], in1=PF[1 + dc][:], op0=ALU.mult, op1=ALU.add), r=[uPF[1 + dc], uH, uSb], w=[uSb])
                          k.op(A, lambda e: e.activation(out=Sbb[:], in_=Sb[:], func=AF.Copy), r=[uSb], w=[uSbb])
                          k.op(A, lambda e, n=n: e.activation(out=Pp[:, n, :], in_=PF[0][:], func=AF.Copy), r=[uPF[0]], w=[uP[n]])
                      k.op(V, lambda e: e.memset(Sbb[:], 0.0), r=[uSbb], w=[uSbb])
                      Sfb2 = [Sfb, Sbb]
                      uSfb2 = [uSfb, uSbb]
                      for n in range(16):
                          tg = n // 4
                          cur, nxt = n % 2, (n + 1) % 2
                          k.op(A, lambda e, n=n: e.activation(out=kdc[:], in_=kt[:, n, :], func=AF.Copy, scale=dk[:, 0:1]), r=[ukt[n], uH], w=[ukdc])
                          for dc in range(2):
                              k.op(PE, lambda e, dc=dc, n=n: e.matmul(PF[1 + dc][:], lhsT=kdc[:, dc * 128:(dc + 1) * 128], rhs=Vv[:, n, :], start=True, stop=True), r=[ukdc, uV[n]], w=[uPF[1 + dc]])
                          for dc in range(2):
                              k.op(PE, lambda e, dc=dc, n=n: e.matmul(PF[3][:, 0:128], lhsT=kT[:, dc, n * 128:(n + 1) * 128], rhs=qT[:, dc, n * 128:(n + 1) * 128], start=(dc == 0), stop=(dc == 1)), r=[ukk[tg], uq[tg]], w=[uPF[3]])
                          k.op(V, lambda e: e.tensor_tensor(out=sd[:], in0=PF[3][:, 0:128], in1=DTm[:], op=ALU.mult), r=[uPF[3], uH], w=[usd])
                          k.op(G, lambda e, n=n: e.tensor_tensor(out=qd[:], in0=qT[:, :, n * 128:(n + 1) * 128], in1=dq[:, 0:128].unsqueeze(1).broadcast_to([128, 2, 128]), op=ALU.mult), r=[uq[tg], uH], w=[uqd])
                          k.op(PE, lambda e, n=n: e.matmul(PF[4][:], lhsT=sd[:], rhs=Vv[:, n, :], start=True, stop=False), r=[usd, uV[n]], w=[uPF[4]])
                          for dc in range(2):
                              k.op(PE, lambda e, dc=dc, cur=cur: e.matmul(PF[4][:], lhsT=qd[:, dc, :], rhs=Sfb2[cur][:, dc, :], start=False, stop=(dc == 1)), r=[uqd, uSfb2[cur]], w=[uPF[4]])
                          for dc in range(2):
                              k.op(V, lambda e, dc=dc: e.scalar_tensor_tensor(out=Sf[:, dc, :], in0=Sf[:, dc, :], scalar=gC[:, 0:1], in1=PF[1 + dc][:], op0=ALU.mult, op1=ALU.add), r=[uPF[1 + dc], uH, uSf], w=[uSf])
                          k.op(A, lambda e, nxt=nxt: e.activation(out=Sfb2[nxt][:], in_=Sf[:], func=AF.Copy), r=[uSf], w=[uSfb2[nxt]])
                          k.op(V, lambda e, n=n: e.tensor_tensor(out=Pp[:, n, :], in0=PF[4][:], in1=Pp[:, n, :], op=ALU.add), r=[uPF[4], uP[n]], w=[uP[n]])
                      for r_ in range(4):
                          k.dma(S, fsum[:], fs_gath.ap()[r_ * 512:(r_ + 1) * 512, :].rearrange("(j p) v -> p j v", p=128), r=[u_fsg], w=[ufs])
                          for dirn, (Sx, uSx) in enumerate(((Sf, uSf), (Sb, uSb))):
                              for dc in range(2):
                                  wcol = wr[:, dirn * 4 + r_:dirn * 4 + r_ + 1]
                                  if r_ == 0:
                                      k.op(V, lambda e, Sx=Sx, dc=dc, dirn=dirn, wcol=wcol: e.tensor_scalar_mul(out=Sx[:, dc, :], in0=fsum[:, dirn * 2 + dc, :], scalar1=wcol), r=[ufs, uH], w=[uSx])
                                  else:
                                      k.op(V, lambda e, Sx=Sx, dc=dc, dirn=dirn, wcol=wcol: e.scalar_tensor_tensor(out=Sx[:, dc, :], in0=fsum[:, dirn * 2 + dc, :], scalar=wcol, in1=Sx[:, dc, :], op0=ALU.mult, op1=ALU.add), r=[ufs, uH, uSx], w=[uSx])
                      k.op(A, lambda e: e.activation(out=Sfb[:], in_=Sf[:], func=AF.Copy), r=[uSf], w=[uSfb])
                      k.op(A, lambda e: e.activation(out=Sbb[:], in_=Sb[:], func=AF.Copy), r=[uSb], w=[uSbb])
                      og2 = [og, og_b]
                      uog2 = [uog, uog_b]
                      qd2 = [qd, qd_b]
                      uqd2 = [uqd, uqd_b]

                      def finish(n):
                          tg = n // 4
                          for vc in range(4):
                              k.op(PE, lambda e, vc=vc, n=n: e.transpose(out=PT[:, vc, :], in_=og2[n % 2][:, vc * 128:(vc + 1) * 128], identity=identb[:]), r=[uog2[n % 2], uC], w=[uPT])
                          k.op(A, lambda e, n=n: e.activation(out=orT[:, :, (n % 4) * 128:(n % 4 + 1) * 128], in_=PT[:, 0:4, :], func=AF.Copy), r=[uPT], w=[uorT])
                          if n % 4 == 3:
                              k.dma(S, OR[h * 4:(h + 1) * 4, :, tg * 512:(tg + 1) * 512].rearrange("c p t -> p c t"), orT[:], r=[uorT], w=[u_OR])
                      oo2 = [oo, oo_b]; uoo2 = [uoo, uoo_b]
                      onn2 = [onn, onn_b]; uonn2 = [uonn, uonn_b]
                      sgg2 = [sgg, sgg_b]; usgg2 = [usgg, usgg_b]
                      bst2 = [bst, bst_b]; mv2 = [mv, mv_b]; ubn2 = [ubn, ubn_b]
                      qdp = [[qd, qd_b], [qd_c, qd_d]]; uqdp = [[uqd, uqd_b], [uqd_c, uqd_d]]
                      pO = [(PF[4], uPF[4]), (PF[0], uPF[0])]
                      pGt = [(PF[5], uPF[5]), (PF[1], uPF[1])]
                      for n0 in range(0, 16, 2):
                          pair = (n0, n0 + 1)
                          for n in pair:
                              p = n % 2
                              t = n + 1
                              bk, ubk = pGt[p]
                              for kk in range(8):
                                  k.op(PE, lambda e, kk=kk, t=t, bk=bk: e.matmul(bk[:], lhsT=hT[:, kk, t * 128:(t + 1) * 128], rhs=Wr[:, kk, 1536:2048], start=(kk == 0), stop=(kk == 7)), r=[uWr, uhT[t]], w=[ubk])
                          for n in pair:
                              p = n % 2
                              bk, ubk = pGt[p]
                              k.op(A, lambda e, p=p, bk=bk: e.activation(out=sgg2[p][:], in_=bk[:], func=AF.Silu), r=[ubk], w=[usgg2[p]])
                          for n in pair:
                              p = n % 2
                              tg = n // 4
                              k.op(V, lambda e, n=n, p=p: e.scalar_tensor_tensor(out=qdp[p][0][:], in0=qT[:, :, n * 128:(n + 1) * 128], scalar=gn[:, n:n + 1], in1=dq[:, 0:128].unsqueeze(1).broadcast_to([128, 2, 128]), op0=ALU.mult, op1=ALU.mult), r=[uq[tg], uH], w=[uqdp[p][0]])
                              k.op(V, lambda e, n=n, p=p: e.scalar_tensor_tensor(out=qdp[p][1][:], in0=qT[:, :, n * 128:(n + 1) * 128], scalar=gn[:, 16 + 15 - n:16 + 16 - n], in1=dq[:, 128:256].unsqueeze(1).broadcast_to([128, 2, 128]), op0=ALU.mult, op1=ALU.mult), r=[uq[tg], uH], w=[uqdp[p][1]])
                          for n in pair:
                              p = n % 2
                              bo, ubo = pO[p]
                              for di, Sxb, uSxb in ((0, Sfb, uSfb), (1, Sbb, uSbb)):
                                  for dc in range(2):
                                      k.op(PE, lambda e, di=di, dc=dc, Sxb=Sxb, p=p, bo=bo: e.matmul(bo[:], lhsT=qdp[p][di][:, dc, :], rhs=Sxb[:, dc, :], start=(di == 0 and dc == 0), stop=(di == 1 and dc == 1)), r=[uqdp[p][di], uSxb], w=[ubo])
                          if n0 > 0:
                              finish(n0 - 2)
                              finish(n0 - 1)
                          for n in pair:
                              p = n % 2
                              bo, ubo = pO[p]
                              k.op(V, lambda e, n=n, p=p, bo=bo: e.tensor_tensor(out=oo2[p][:], in0=bo[:], in1=Pp[:, n, :], op=ALU.add), r=[ubo, uP[n]], w=[uoo2[p]])
                          for n in pair:
                              p = n % 2
                              k.op(V, lambda e, p=p: e.bn_stats(out=bst2[p][:], in_=oo2[p][:]), r=[uoo2[p]], w=[ubn2[p]])
                          for n in pair:
                              p = n % 2
                              k.op(V, lambda e, p=p: e.bn_aggr(out=mv2[p][:], in_=bst2[p][:]), r=[ubn2[p]], w=[ubn2[p]])
                          for n in pair:
                              p = n % 2
                              k.op(V, lambda e, p=p: e.tensor_scalar_add(out=mv2[p][:, 1:2], in0=mv2[p][:, 1:2], scalar1=EPS), r=[ubn2[p]], w=[ubn2[p]])
                          for n in pair:
                              p = n % 2
                              k.op(A, lambda e, p=p: e.activation(out=mv2[p][:, 1:2], in_=mv2[p][:, 1:2], func=AF.Sqrt), r=[ubn2[p]], w=[ubn2[p]])
                          for n in pair:
                              p = n % 2
                              k.op(V, lambda e, p=p: e.reciprocal(out=mv2[p][:, 1:2], in_=mv2[p][:, 1:2]), r=[ubn2[p]], w=[ubn2[p]])
                          for n in pair:
                              p = n % 2
                              k.op(V, lambda e, p=p: e.tensor_scalar(out=onn2[p][:], in0=oo2[p][:], scalar1=mv2[p][:, 0:1], scalar2=mv2[p][:, 1:2], op0=ALU.subtract, op1=ALU.mult), r=[uoo2[p], ubn2[p]], w=[uonn2[p]])
                          for n in pair:
                              p = n % 2
                              k.op(G, lambda e, p=p: e.tensor_tensor(out=og2[p][:], in0=onn2[p][:], in1=sgg2[p][:], op=ALU.mult), r=[uonn2[p], usgg2[p]], w=[uog2[p]])
                      finish(14)
                      finish(15)
                except _Stop:
                    pass
                k.barrier()

        def phase_F(l, xsrc, last):
            with ExitStack() as st:
                mg = sb(st, "mg", [128, 8, T], F32); umg = [U() for _ in range(4)]
                sgm = sb(st, "sgm", [128, 512], F32); usgm = U()
                tmpm = sb(st, "tmpm", [128, 512], F32); utmpm = U()
                for bi, (nk, kp) in enumerate(((8, 128), (16, 128), (16, 64))):
                    with ExitStack() as st2:
                        Wb = sb(st2, f"Wb{bi}", [kp, nk, D], BF16); uWb = U()
                        Wg = sb(st2, f"Wg{bi}", [128, 8, D], BF16); uWg = U()
                        ob = sb(st2, f"ob{bi}", [kp, nk, 512], BF16); uob = U()
                        src_w = (w_conv_out, w_ret_out, w_attn_out)[bi]
                        if bi == 2:
                            view = src_w[l].rearrange("(h d) c -> d h c", d=64)
                        else:
                            view = src_w[l].rearrange("(k p) c -> p k c", p=128)
                        for j in range(0, nk, 4):
                            k.dma(G, Wb[:, j:j + 4, :], view[:, j:j + 4, :], w=[uWb])
                        for j in range(2):
                            c0 = OFF_GL + bi * 1024 + j * 512
                            k.dma(G, Wg[:, :, j * 512:(j + 1) * 512], w_in[l, :, c0:c0 + 512].rearrange("(k p) c -> p k c", p=128), w=[uWg])
                        osrc = (OC, OR, OA)[bi]
                        uos = (u_OC, u_OR, u_OA)[bi]
                        for tg in range(4):
                            e0 = 128 + tg * 512
                            hr = uhT[e0 // 128:(e0 + 512) // 128]
                            k.dma(S, ob[:], osrc[:, :, tg * 512:(tg + 1) * 512].rearrange("c p t -> p c t"), r=[uos], w=[uob])
                            for m in range(8):
                                py, upy = PF[m % 3], uPF[m % 3]
                                pg, upg = PF[3 + m % 3], uPF[3 + m % 3]
                                for j in range(nk):
                                    k.op(PE, lambda e, py=py, j=j, m=m: e.matmul(py[:], lhsT=Wb[:, j, m * 128:(m + 1) * 128], rhs=ob[:, j, :], start=(j == 0), stop=(j == nk - 1)), r=[uWb, uob], w=[upy])
                                for kk in range(8):
                                    k.op(PE, lambda e, pg=pg, kk=kk, m=m, e0=e0: e.matmul(pg[:], lhsT=Wg[:, kk, m * 128:(m + 1) * 128], rhs=hT[:, kk, e0:e0 + 512], start=(kk == 0), stop=(kk == 7)), r=[uWg] + hr, w=[upg])
                                k.op(A, lambda e, pg=pg, m=m, bi=bi: e.activation(out=sgm[:], in_=pg[:], func=AF.Sigmoid, bias=prT[:, m, 34 + bi:35 + bi], scale=1.0), r=[upg, uL], w=[usgm])
                                if bi == 0:
                                    k.op(V, lambda e, py=py, m=m, tg=tg: e.tensor_tensor(out=mg[:, m, tg * 512:(tg + 1) * 512], in0=py[:], in1=sgm[:], op=ALU.mult), r=[upy, usgm], w=[umg[tg]])
                                else:
                                    k.op(V, lambda e, py=py: e.tensor_tensor(out=tmpm[:], in0=py[:], in1=sgm[:], op=ALU.mult), r=[upy, usgm], w=[utmpm])
                                    k.op(G, lambda e, m=m, tg=tg: e.tensor_tensor(out=mg[:, m, tg * 512:(tg + 1) * 512], in0=mg[:, m, tg * 512:(tg + 1) * 512], in1=tmpm[:], op=ALU.add), r=[utmpm, umg[tg]], w=[umg[tg]])
                        k.barrier()
                with ExitStack() as st2:
                    Wo = sb(st2, "Wo", [128, 8, D], BF16); uWo = U()
                    for j in range(2):
                        k.dma(G, Wo[:, j * 4:(j + 1) * 4, :], w_out[l].rearrange("(k p) c -> p k c", p=128)[:, j * 4:(j + 1) * 4, :], w=[uWo])
                    mb = sb(st2, "mb", [128, 8, 128], BF16); umb = U()
                    xt = [sb(st2, f"fxt{i}", [128, D], F32) for i in range(2)]; uxt = [U(), U()]
                    xn = [sb(st2, f"fxn{i}", [128, D], F32) for i in range(2)]; uxn = [U(), U()]
                    for n in range(16):
                        b = n % 2
                        k.op(A, lambda e, n=n: e.activation(out=mb[:], in_=mg[:, :, n * 128:(n + 1) * 128], func=AF.Copy), r=[umg[n // 4]], w=[umb])
                        k.dma(S, xt[b][:], xsrc[128 + n * 128:128 + (n + 1) * 128, :], r=[u_x1e], w=[uxt[b]])
                        for half in range(2):
                            for kk in range(8):
                                k.op(PE, lambda e, half=half, kk=kk: e.matmul(PF[half][:], lhsT=mb[:, kk, :], rhs=Wo[:, kk, half * 512:(half + 1) * 512], start=(kk == 0), stop=(kk == 7)), r=[umb, uWo], w=[uPF[half]])
                            k.op(V, lambda e, half=half, b=b: e.tensor_tensor(out=xn[b][:, half * 512:(half + 1) * 512], in0=PF[half][:], in1=xt[b][:, half * 512:(half + 1) * 512], op=ALU.add), r=[uPF[half], uxt[b]], w=[uxn[b]])
                        if last:
                            k.dma(S, y_out[n * 128:(n + 1) * 128, :], xn[b][:], r=[uxn[b]], w=[u_yout])
                        else:
                            k.dma(S, x1e[128 + n * 128:128 + (n + 1) * 128, :], xn[b][:], r=[uxn[b]], w=[u_x1e])
                            if n == 0:
                                k.dma(S, h_bounce.ap()[0:128, :], xn[b][:], r=[uxn[b]], w=[u_hb])
                            if n == 15:
                                k.dma(S, h_bounce.ap()[128:256, :], xn[b][:], r=[uxn[b]], w=[u_hb])
                    k.barrier()
            if not last:
                ccs = k.new_sem("cch")
                k.custom(G, lambda e: e.collective_compute("AllGather", ALU.bypass, replica_groups=RG, ins=[h_bounce.ap().opt()], outs=[h_gath.ap().opt()]), ccs, 1, r=[u_hb], w=[u_hg])
                with ExitStack() as st2:
                    hb = sb(st2, "hb", [128, 2, D], F32); uhb = U()
                    accL = sb(st2, "accL", [128, D], F32); accR = sb(st2, "accR", [128, D], F32); uaL = U(); uaR = U()
                    for r_ in range(4):
                        k.dma(S, hb[:], h_gath.ap()[r_ * 256:(r_ + 1) * 256, :].rearrange("(a p) f -> p a f", p=128), r=[u_hg], w=[uhb])
                        for (acc, ua, a_, sc) in ((accL, uaL, 1, r_), (accR, uaR, 0, 4 + r_)):
                            if r_ == 0:
                                k.op(V, lambda e, acc=acc, a_=a_, sc=sc: e.tensor_scalar_mul(out=acc[:], in0=hb[:, a_, :], scalar1=sel[:, sc:sc + 1]), r=[uhb, uC], w=[ua])
                            else:
                                k.op(V, lambda e, acc=acc, a_=a_, sc=sc: e.scalar_tensor_tensor(out=acc[:], in0=hb[:, a_, :], scalar=sel[:, sc:sc + 1], in1=acc[:], op0=ALU.mult, op1=ALU.add), r=[uhb, uC, ua], w=[ua])
                    k.dma(S, x1e[0:128, :], accL[:], r=[uaL], w=[u_x1e])
                    k.dma(S, x1e[TE - 128:TE, :], accR[:], r=[uaR], w=[u_x1e])
                    k.barrier()

        for l in range(NL):
            xsrc = x_ext if l == 0 else x1e
            last = (l == NL - 1)
            k.dma(S, lg[:], ret_decay[l:l + 1, :].partition_broadcast(128), w=[uL])
            k.dma(S, qg[:], q_norm_g[l:l + 1, :].partition_broadcast(128), w=[uL])
            k.dma(S, kg[:], k_norm_g[l:l + 1, :].partition_broadcast(128), w=[uL])
            k.dma(S, snk[:], attn_sink[l:l + 1, :].partition_broadcast(128), w=[uL])
            k.dma(S, prm[0:31, :], conv_dw[l], w=[uL])
            k.dma(S, prm[31:32, :], conv_b[l], w=[uL])
            k.dma(S, prm[32:33, :], conv_ln_g[l], w=[uL])
            k.dma(S, prm[33:34, :], conv_ln_b[l], w=[uL])
            k.dma(S, prm[34:37, :], b_gate[l], w=[uL])
            k.op(A, lambda e: e.activation(out=lg[:], in_=lg[:], func=AF.Exp), r=[uL], w=[uL])
            k.op(V, lambda e: e.tensor_scalar_mul(out=lg[:], in0=lg[:], scalar1=-1.0), r=[uL], w=[uL])
            k.op(V, lambda e: e.tensor_tensor(out=g2[:, 0:64], in0=qg[:], in1=qg[:], op=ALU.mult), r=[uL], w=[uL])
            k.op(V, lambda e: e.tensor_tensor(out=g2[:, 64:128], in0=kg[:], in1=kg[:], op=ALU.mult), r=[uL], w=[uL])
            k.op(V, lambda e: e.tensor_reduce(out=tmpc[:, 0:1], in_=g2[:, 0:64], axis=AX.X, op=ALU.max), r=[uL], w=[uL])
            k.op(V, lambda e: e.tensor_reduce(out=tmpc[:, 1:2], in_=g2[:, 64:128], axis=AX.X, op=ALU.max), r=[uL], w=[uL])
            k.op(V, lambda e: e.tensor_tensor(out=tmpc[:, 2:3], in0=tmpc[:, 0:1], in1=tmpc[:, 1:2], op=ALU.mult), r=[uL], w=[uL])
            k.op(A, lambda e: e.activation(out=tmpc[:, 3:4], in_=tmpc[:, 2:3], func=AF.Sqrt), r=[uL], w=[uL])
            k.op(V, lambda e: e.tensor_scalar_mul(out=negc[:], in0=tmpc[:, 3:4], scalar1=-8.0), r=[uL], w=[uL])
            k.op(A, lambda e: e.activation(out=snke[:], in_=snk[:], func=AF.Exp, bias=negc[:, 0:1], scale=1.0), r=[uL], w=[uL])
            for c in range(8):
                k.op(PE, lambda e, c=c: e.transpose(out=PF[0][:, c * 37:(c + 1) * 37], in_=prm[:, c * 128:(c + 1) * 128], identity=ident[0:37, 0:37]), r=[uL, uC], w=[uPF[0]])
            k.op(V, lambda e: e.tensor_copy(out=prT[:].rearrange("p c j -> p (c j)"), in_=PF[0][:, 0:296]), r=[uPF[0]], w=[uL])
            k.barrier()

            stW = ExitStack()
            Wt_pre = sb(stW, "Wt", [128, 8, 2560], BF16); uWt_pre = U()
            stWc = ExitStack()
            Wc = sb(stWc, "Wc", [128, 8, 3072], BF16); uWc = U()
            if 'C' in phases:
                for j in range(6):
                    k.dma(G, Wc[:, :, j * 512:(j + 1) * 512], w_in[l, :, j * 512:(j + 1) * 512].rearrange("(k p) c -> p k c", p=128), w=[uWc])
            with ExitStack() as st:
                g_bc = sb(st, "g_bc", [128, D], F32); ug = U()
                k.dma(S, g_bc[:], norm_g[l:l + 1, :].partition_broadcast(128), w=[ug])
                xt = [sb(st, f"xt{i}", [128, D], F32) for i in range(2)]; uxt = [U(), U()]
                junk = [sb(st, f"junk{i}", [128, D], BF16) for i in range(2)]; ujunk = [U(), U()]
                xs = [sb(st, f"xs{i}", [128, D], BF16) for i in range(2)]; uxs = [U(), U()]
                ssq = [sb(st, f"ssq{i}", [128, 2], F32) for i in range(2)]; ussq = [U(), U()]
                def a_s1(t):
                    b = t % 2
                    k.dma(S, xt[b][:], xsrc[t * 128:(t + 1) * 128, :], r=[u_x1e], w=[uxt[b]])
                    k.op(V, lambda e, b=b: e.memset(ssq[b][:], 0.0), w=[ussq[b]])
                    k.op(A, lambda e, b=b: e.activation(out=junk[b][:], in_=xt[b][:], func=AF.Square, accum_out=ssq[b][:, 0:1]), r=[uxt[b]], w=[ujunk[b], ussq[b]])
                    k.op(V, lambda e, b=b: e.tensor_scalar(out=ssq[b][:, 1:2], in0=ssq[b][:, 0:1], scalar1=1.0 / D, scalar2=EPS, op0=ALU.mult, op1=ALU.add), r=[ussq[b]], w=[ussq[b]])
                    k.op(A, lambda e, b=b: e.activation(out=ssq[b][:, 1:2], in_=ssq[b][:, 1:2], func=AF.Sqrt), r=[ussq[b]], w=[ussq[b]])
                    k.op(V, lambda e, b=b: e.reciprocal(out=ssq[b][:, 1:2], in_=ssq[b][:, 1:2]), r=[ussq[b]], w=[ussq[b]])
                    k.op(V, lambda e, b=b: e.scalar_tensor_tensor(out=xs[b][:], in0=xt[b][:], scalar=ssq[b][:, 1:2], in1=g_bc[:], op0=ALU.mult, op1=ALU.mult), r=[uxt[b], ussq[b], ug], w=[uxs[b]])

                def a_s2(t):
                    b = t % 2
                    for c in range(8):
                        k.op(PE, lambda e, b=b, c=c: e.transpose(out=PT[:, c, :], in_=xs[b][:, c * 128:(c + 1) * 128], identity=identb[:]), r=[uxs[b], uC], w=[uPT])
                    k.op(A, lambda e, t=t: e.activation(out=hT[:, :, t * 128:(t + 1) * 128], in_=PT[:], func=AF.Copy), r=[uPT], w=[uhT[t]])
                a_s1(0)
                for t in range(18):
                    if t < 17:
                        a_s1(t + 1)
                    a_s2(t)
                k.barrier()

            if 'C' in phases:
                with ExitStack() as st:
                    if 'T' in phases:
                        for j in range(5):
                            k.dma(G, Wt_pre[:, :, j * 512:(j + 1) * 512], w_in[l, :, OFF_AQ + j * 512:OFF_AQ + (j + 1) * 512].rearrange("(k p) c -> p k c", p=128), w=[uWt_pre])
                    sig = [sb(st, f"sig{i}", [128, 544], F32) for i in range(2)]; usig = [U(), U()]
                    vv = [sb(st, f"vv{i}", [128, 544], F32) for i in range(2)]; uvv = [U(), U()]
                    acc1 = [sb(st, f"acc1{i}", [128, 512], F32) for i in range(2)]; uacc1 = [U(), U()]
                    acc2 = [sb(st, f"acc2{i}", [128, 512], F32) for i in range(2)]; uacc2 = [U(), U()]
                    tmpk = [sb(st, f"tmpk{i}", [128, 512], F32) for i in range(4)]; utmpk = [U() for _ in range(4)]
                    yy = sb(st, "yy", [128, 8, 512], F32); uyy = [U() for _ in range(8)]
                    ysq = [sb(st, f"ysq{i}", [128, 512], F32) for i in range(2)]; uysq = [U(), U()]
                    mean = sb(st, "mean", [128, 512], F32); msq = sb(st, "msq", [128, 512], F32)
                    rstd = sb(st, "rstd", [128, 512], F32); ustat = U()
                    t1 = [sb(st, f"t1{i}", [128, 512], F32) for i in range(2)]; ut1 = [U(), U()]
                    t2 = t1; ut2 = ut1
                    s1 = t1; us1 = ut1
                    sg = [sb(st, f"sg{i}", [128, 512], F32) for i in range(2)]; usg = [U(), U()]
                    ocT = sb(st, "ocT", [128, 8, 512], BF16); uocT = U()
                    pTls = [PF[4], PF[5]]; uTl = [uPF[4], uPF[5]]
                    pST, pSQ, uST, uSQ = PF[6], PF[4], uPF[6], uPF[4]
                    it = 0
                    tk = 0

                    def proj(tg, c, b):
                        e0 = 128 + tg * 512
                        hr = uhT[(e0 - 16) // 128:(e0 + 528 + 127) // 128]
                        pA, uA, pB, uB = PF[b], uPF[b], PF[2 + b], uPF[2 + b]
                        tb = 0
                        pTl = pTls[b]
                        for (ps, ups, col0, to) in ((pA, uA, c * 128, tb), (pB, uB, 1024 + c * 128, tb + 32)):
                            for kk in range(8):
                                k.op(PE, lambda e, ps=ps, kk=kk, col0=col0: e.matmul(ps[:, 0:512], lhsT=Wc[:, kk, col0:col0 + 128], rhs=hT[:, kk, e0 - 16:e0 + 496], start=(kk == 0), stop=(kk == 7)), r=[uWc] + hr, w=[ups])
                            for kk in range(8):
                                k.op(PE, lambda e, kk=kk, col0=col0, to=to, pTl=pTl: e.matmul(pTl[:, to:to + 32], lhsT=Wc[:, kk, col0:col0 + 128], rhs=hT[:, kk, e0 + 496:e0 + 528], start=(kk == 0), stop=(kk == 7)), r=[uWc] + hr, w=[uTl[b]])
                    for tg in range(4):
                        e0 = 128 + tg * 512
                        hr = uhT[(e0 - 16) // 128:(e0 + 528 + 127) // 128]
                        proj(tg, 0, it % 2)
                        for c in range(8):
                            b = it % 2
                            it += 1
                            pA, uA, pB, uB = PF[b], uPF[b], PF[2 + b], uPF[2 + b]
                            tb = 0
                            pTl = pTls[b]
                            if c < 7:
                                proj(tg, c + 1, it % 2)
                            k.op(A, lambda e, b=b, pB=pB: e.activation(out=sig[b][:, 0:512], in_=pB[:, 0:512], func=AF.Sigmoid), r=[uB], w=[usig[b]])
                            k.op(A, lambda e, b=b, tb=tb, pTl=pTl: e.activation(out=sig[b][:, 512:544], in_=pTl[:, tb + 32:tb + 64], func=AF.Sigmoid), r=[uTl[b]], w=[usig[b]])
                            k.op(V, lambda e, b=b, pA=pA: e.tensor_tensor(out=vv[b][:, 0:512], in0=pA[:, 0:512], in1=sig[b][:, 0:512], op=ALU.mult), r=[uA, usig[b]], w=[uvv[b]])
                            k.op(V, lambda e, b=b, tb=tb, pTl=pTl: e.tensor_tensor(out=vv[b][:, 512:544], in0=pTl[:, tb:tb + 32], in1=sig[b][:, 512:544], op=ALU.mult), r=[uTl[b], usig[b]], w=[uvv[b]])
                            k.op(V, lambda e, c=c, b=b: e.tensor_scalar(out=acc1[b][:], in0=vv[b][:, 1:513], scalar1=prT[:, c, 0:1], scalar2=prT[:, c, 31:32], op0=ALU.mult, op1=ALU.add), r=[uvv[b], uL], w=[uacc1[b]])
                            for kt in range(1, 16):
                                k.op(V, lambda e, c=c, kt=kt, b=b: e.scalar_tensor_tensor(out=acc1[b][:], in0=vv[b][:, kt + 1:kt + 513], scalar=prT[:, c, kt:kt + 1], in1=acc1[b][:], op0=ALU.mult, op1=ALU.add), r=[uvv[b], uL, uacc1[b]], w=[uacc1[b]])
                            k.op(A, lambda e, c=c, b=b: e.activation(out=acc2[b][:], in_=vv[b][:, 17:529], func=AF.Copy, scale=prT[:, c, 16:17]), r=[uvv[b], uL], w=[uacc2[b]])
                            for kt in range(17, 31):
                                j = tk % 4
                                tk += 1
                                k.op(A, lambda e, c=c, kt=kt, j=j, b=b: e.activation(out=tmpk[j][:], in_=vv[b][:, kt + 1:kt + 513], func=AF.Copy, scale=prT[:, c, kt:kt + 1]), r=[uvv[b], uL], w=[utmpk[j]])
                                k.op(G, lambda e, j=j, b=b: e.tensor_tensor(out=acc2[b][:], in0=acc2[b][:], in1=tmpk[j][:], op=ALU.add), r=[utmpk[j], uacc2[b]], w=[uacc2[b]])
                            k.op(G, lambda e, c=c, b=b: e.tensor_tensor(out=yy[:, c, :], in0=acc1[b][:], in1=acc2[b][:], op=ALU.add), r=[uacc1[b], uacc2[b]], w=[uyy[c]])
                        for c in range(8):
                            b = c % 2
                            k.op(A, lambda e, c=c, b=b: e.activation(out=ysq[b][:], in_=yy[:, c, :], func=AF.Square), r=[uyy[c]], w=[uysq[b]])
                            k.op(PE, lambda e, c=c: e.matmul(pST[:], lhsT=onesf[:], rhs=yy[:, c, :], start=(c == 0), stop=(c == 7)), r=[uyy[c], uC], w=[uST])
                            k.op(PE, lambda e, c=c, b=b: e.matmul(pSQ[:], lhsT=onesf[:], rhs=ysq[b][:], start=(c == 0), stop=(c == 7)), r=[uysq[b], uC], w=[uSQ])
                        k.op(V, lambda e: e.tensor_copy(out=mean[:], in_=pST[:]), r=[uST], w=[ustat])
                        k.op(G, lambda e: e.tensor_tensor(out=msq[:], in0=mean[:], in1=mean[:], op=ALU.mult), r=[ustat], w=[ustat])
                        k.op(V, lambda e: e.tensor_tensor(out=rstd[:], in0=pSQ[:], in1=msq[:], op=ALU.subtract), r=[uSQ, ustat], w=[ustat])
                        k.op(V, lambda e: e.tensor_scalar_add(out=rstd[:], in0=rstd[:], scalar1=EPS), r=[ustat], w=[ustat])
                        k.op(A, lambda e: e.activation(out=rstd[:], in_=rstd[:], func=AF.Sqrt), r=[ustat], w=[ustat])
                        k.op(V, lambda e: e.reciprocal(out=rstd[:], in_=rstd[:]), r=[ustat], w=[ustat])
                        for c in range(8):
                            b = c % 2
                            pG, uG = PF[b], uPF[b]
                            for kk in range(8):
                                k.op(PE, lambda e, c=c, kk=kk, pG=pG: e.matmul(pG[:], lhsT=Wc[:, kk, 2048 + c * 128:2048 + (c + 1) * 128], rhs=hT[:, kk, e0:e0 + 512], start=(kk == 0), stop=(kk == 7)), r=[uWc] + hr, w=[uG])
                            k.op(V, lambda e, c=c, b=b: e.tensor_tensor(out=t1[b][:], in0=yy[:, c, :], in1=mean[:], op=ALU.subtract), r=[uyy[c], ustat], w=[ut1[b]])
                            k.op(G, lambda e, b=b: e.tensor_tensor(out=t2[b][:], in0=t1[b][:], in1=rstd[:], op=ALU.mult), r=[ut1[b], ustat], w=[ut2[b]])
                            k.op(A, lambda e, c=c, b=b: e.activation(out=s1[b][:], in_=t2[b][:], func=AF.Silu, bias=prT[:, c, 33:34], scale=prT[:, c, 32:33]), r=[ut2[b], uL], w=[us1[b]])
                            k.op(A, lambda e, b=b, pG=pG: e.activation(out=sg[b][:], in_=pG[:], func=AF.Silu), r=[uG], w=[usg[b]])
                            k.op(V, lambda e, c=c, b=b: e.tensor_tensor(out=ocT[:, c, :], in0=s1[b][:], in1=sg[b][:], op=ALU.mult), r=[us1[b], usg[b]], w=[uocT])
                        k.dma(S, OC[:, :, tg * 512:(tg + 1) * 512].rearrange("c p t -> p c t"), ocT[:], r=[uocT], w=[u_OC])
                    k.barrier()
            stWc.close()
            if 'T' in phases:
                if 'C' not in phases:
                    for j in range(5):
                        k.dma(G, Wt_pre[:, :, j * 512:(j + 1) * 512], w_in[l, :, OFF_AQ + j * 512:OFF_AQ + (j + 1) * 512].rearrange("(k p) c -> p k c", p=128), w=[uWt_pre])
                phase_T(l, Wt_pre, uWt_pre)
            stW.close()
            if 'R' in phases:
                phase_R(l)
            if 'F' in phases:
                phase_F(l, xsrc, last)
        k.barrier()
    return nc


def make_consts(core):
    c = core % 4
    seg = c * T
    p = np.arange(128)
    tt = np.arange(T)
    inv = 10000.0 ** (-np.arange(128, dtype=np.float64) / 128.0)
    ang = inv[:, None] * (seg + tt)[None, :].astype(np.float64)
    cosr = np.cos(ang).astype(np.float32)
    sinr = np.sin(ang).astype(np.float32)
    inva = 500000.0 ** (-np.arange(8, dtype=np.float64) / 8.0)
    pos = (seg - 128 + np.arange(18)[None, :] * 128 + p[:, None]).astype(np.float64)
    anga = pos[:, :, None] * inva[None, None, :]
    cosa = np.cos(anga).astype(np.float32)
    sina = np.sin(anga).astype(np.float32)
    j = p[:, None]
    i = p[None, :]
    maskL = (j >= i).astype(np.float32)
    maskR = (j <= i).astype(np.float32)
    mask = np.concatenate([maskL, maskR, maskL * (1.0 if c > 0 else 0.0), maskR * (1.0 if c < 3 else 0.0)], axis=1)
    eseg = np.zeros((128, 32), np.float32)
    for n in range(16):
        eseg[:, n] = 2047 - (n * 128 + p)
        eseg[:, 16 + n] = n * 128 + p
    ek = np.stack([127 - p, p], axis=1).astype(np.float32)
    eq = np.concatenate([np.tile((np.arange(128) + 1)[None, :], (128, 1)), np.tile((128 - np.arange(128))[None, :], (128, 1))], axis=1).astype(np.float32)
    Ef = np.maximum(i - j, 0); Eb = np.maximum(j - i, 0)
    Mf = (i >= j); Mb = (j > i)
    em = np.concatenate([Ef, Eb, Mf, Mb], axis=1).astype(np.float32)
    BIG = 1.0e6
    dist = np.zeros((128, 8), np.float32)
    sel = np.zeros((128, 8), np.float32)
    for r in range(4):
        dist[:, r] = T * (c - r - 1) if r < c else BIG
        dist[:, 4 + r] = T * (r - c - 1) if r > c else BIG
        sel[:, r] = 1.0 if r == c - 1 else 0.0
        sel[:, 4 + r] = 1.0 if r == c + 1 else 0.0
    return dict(c_cosr=cosr, c_sinr=sinr, c_cosa=cosa, c_sina=sina, c_mask=mask,
                c_ident=np.eye(128, dtype=np.float32), c_eseg=eseg, c_ek=ek, c_eq=eq, c_em=em,
                c_dist=dist, c_sel=sel, c_e128=np.tile((128.0 * np.arange(16, dtype=np.float32))[None, :], (128, 1)))


def make_in_maps(inputs):
    x = np.asarray(inputs['x'], np.float32)
    shared = dict(
        norm_g=np.asarray(inputs['norm_g'], np.float32),
        w_in=np.asarray(inputs['w_in'], np.float32),
        b_gate=np.asarray(inputs['b_gate'], np.float32).reshape(2, 3, D),
        conv_dw=np.asarray(inputs['conv_dw'], np.float32),
        conv_b=np.asarray(inputs['conv_b'], np.float32).reshape(2, 1, D),
        conv_ln_g=np.asarray(inputs['conv_ln_g'], np.float32).reshape(2, 1, D),
        conv_ln_b=np.asarray(inputs['conv_ln_b'], np.float32).reshape(2, 1, D),
        ret_decay=np.asarray(inputs['ret_decay'], np.float32).reshape(2, 8),
        q_norm_g=np.asarray(inputs['q_norm_g'], np.float32),
        k_norm_g=np.asarray(inputs['k_norm_g'], np.float32),
        attn_sink=np.asarray(inputs['attn_sink'], np.float32),
        w_conv_out=np.asarray(inputs['w_conv_out'], np.float32),
        w_ret_out=np.asarray(inputs['w_ret_out'], np.float32),
        w_attn_out=np.asarray(inputs['w_attn_out'], np.float32),
        w_out=np.asarray(inputs['w_out'], np.float32),
    )
    in_maps = []
    for core in range(8):
        b, c = core // 4, core % 4
        xe = np.zeros((TE, D), np.float32)
        lo = c * T - 128
        hi = c * T + T + 128
        slo, shi = max(lo, 0), min(hi, 4 * T)
        xe[slo - lo:shi - lo] = x[b, slo:shi]
        m = dict(shared)
        m['x_ext'] = xe
        m.update(make_consts(core))
        in_maps.append(m)
    return in_maps


_NC = None


def kernel(**inputs):
    global _NC
    if _NC is None:
        _NC = build(2)
    in_maps = make_in_maps(inputs)
    res = run_bass_kernel_spmd(_NC, in_maps, core_ids=list(range(8)))
    out = np.zeros((2, 4 * T, D), np.float32)
    for core in range(8):
        b, c = core // 4, core % 4
        out[b, c * T:(c + 1) * T] = res.results[core]["y_out"]
    return out
```

```python
import os
import numpy as np
import concourse.bass as bass
import concourse.mybir as mybir
from concourse.bass_utils import run_bass_kernel_spmd
from contextlib import ExitStack

F32 = mybir.dt.float32
BF16 = mybir.dt.bfloat16
ALU = mybir.AluOpType
AF = mybir.ActivationFunctionType
AX = mybir.AxisListType

ENG = ['tensor', 'vector', 'scalar', 'gpsimd', 'sync']
EPOCH = 20000
ND = 8
PE, V, A, G, S = 'tensor', 'vector', 'scalar', 'gpsimd', 'sync'


class U:
    __slots__ = ('w', 'rs')

    def __init__(s):
        s.w = None
        s.rs = {}


class KB:
    def __init__(s, nc, stack):
        s.nc = nc
        s.st = stack
        s.cnt = {e: 0 for e in ENG}
        s.nsem = 0
        s.sem = {e: s.new_sem(f'e_{e}') for e in ENG}
        s.hist = {e: [] for e in ENG}
        s.waited = {e: {} for e in ENG}
        s.dsem = {}
        s.dtarget = {}
        s.dcount = {}
        s.n_inst = 0

    def new_sem(s, name):
        s.nsem += 1
        return s.st.enter_context(s.nc.semaphore(f'{name}_{s.nsem}'))

    def _waits(s, engine, r, w):
        deps = {}

        def add(tok):
            key = id(tok[0])
            if key not in deps or deps[key][1] < tok[1]:
                deps[key] = tok
        for u in r:
            if u.w is not None:
                add(u.w)
        for u in w:
            if u.w is not None:
                add(u.w)
            for tok in u.rs.values():
                add(tok)
        waits = []
        wd = s.waited[engine]
        for key, (sem, val, src) in deps.items():
            if engine == PE and src == PE:
                continue
            if wd.get(key, 0) >= val:
                continue
            wd[key] = val
            waits.append((sem, val))
        return waits

    def _emit(s, ename, waits, fn, inc):
        e = getattr(s.nc, ename)
        for sem, val in waits:
            e.wait_ge(sem, val)
        if fn is None:
            return
        ins = fn(e)
        if inc[1] is None:
            ins.then_inc(inc[0])
        else:
            ins.then_inc(inc[0], inc[1])
        s.n_inst += 1

    def op(s, engine, fn, r=(), w=()):
        waits = s._waits(engine, r, w)
        if s.cnt[engine] >= EPOCH:
            s.hist[engine].append((s.sem[engine], s.cnt[engine]))
            s.sem[engine] = s.new_sem(f'e_{engine}')
            s.cnt[engine] = 0
        s.cnt[engine] += 1
        sem = s.sem[engine]
        tok = (sem, s.cnt[engine], engine)
        s._emit(engine, waits, fn, (sem, 1))
        for u in r:
            u.rs[id(sem)] = tok
        for u in w:
            u.w = tok
            u.rs = {}
        return tok

    def dma(s, q, out, in_, r=(), w=(), **kw):
        waits = s._waits(q, r, w)
        if q not in s.dsem:
            s.dsem[q] = [s.new_sem(f'd_{q}{i}') for i in range(ND)]
            s.dtarget[q] = [0] * ND
            s.dcount[q] = 0
        i = s.dcount[q] % ND
        s.dcount[q] += 1
        sem = s.dsem[q][i]
        prev = s.dtarget[q][i]
        if prev > 0 and s.waited[q].get(id(sem), 0) < prev:
            s.waited[q][id(sem)] = prev
            waits.append((sem, prev))
        tgt = prev + 16
        s.dtarget[q][i] = tgt
        tok = (sem, tgt, 'dma')
        s._emit(q, waits, lambda e: e.dma_start(out=out, in_=in_, **kw), (sem, 16))
        for u in r:
            u.rs[id(sem)] = tok
        for u in w:
            u.w = tok
            u.rs = {}
        return tok

    def custom(s, engine, fn, inc_sem, inc_val, r=(), w=()):
        waits = s._waits(engine, r, w)
        tok = (inc_sem, inc_val, 'custom')
        s._emit(engine, waits, fn, (inc_sem, None))
        for u in r:
            u.rs[id(inc_sem)] = tok
        for u in w:
            u.w = tok
            u.rs = {}
        return tok

    def barrier(s):
        toks = []
        for e in ENG:
            for sem, c in s.hist[e]:
                toks.append((sem, c))
            if s.cnt[e] > 0:
                toks.append((s.sem[e], s.cnt[e]))
        for q in s.dsem:
            for i in range(ND):
                if s.dtarget[q][i] > 0:
                    toks.append((s.dsem[q][i], s.dtarget[q][i]))
        for e in ENG:
            wd = s.waited[e]
            waits = []
            for sem, val in toks:
                if wd.get(id(sem), 0) >= val:
                    continue
                wd[id(sem)] = val
                waits.append((sem, val))
            s._emit(e, waits, None, None)


D = 1024
T = 2048
TE = 2304
NT = 16
INW = 14848
OFF_CGLU, OFF_CGATE = 0, 2048
OFF_RQ, OFF_RK, OFF_RV, OFF_RG = 3072, 4096, 5120, 7168
OFF_AQ, OFF_AK, OFF_AV, OFF_AG = 9216, 10240, 10496, 10752
OFF_GL = 11776
EPS = 1e-6


def build(NL=2, dbg=False, phases='CTRF'):
    nc = bass.Bass("TRN2", target_bir_lowering=False)

    def din(name, shape):
        return nc.dram_tensor(name, shape, F32, kind="ExternalInput").ap()

    x_ext = din("x_ext", [TE, D])
    norm_g = din("norm_g", [2, D])
    w_in = din("w_in", [2, D, INW])
    b_gate = din("b_gate", [2, 3, D])
    conv_dw = din("conv_dw", [2, 31, D])
    conv_b = din("conv_b", [2, 1, D])
    conv_ln_g = din("conv_ln_g", [2, 1, D])
    conv_ln_b = din("conv_ln_b", [2, 1, D])
    ret_decay = din("ret_decay", [2, 8])
    q_norm_g = din("q_norm_g", [2, 64])
    k_norm_g = din("k_norm_g", [2, 64])
    attn_sink = din("attn_sink", [2, 16])
    w_conv_out = din("w_conv_out", [2, D, D])
    w_ret_out = din("w_ret_out", [2, 2 * D, D])
    w_attn_out = din("w_attn_out", [2, D, D])
    w_out = din("w_out", [2, D, D])
    c_cosr = din("c_cosr", [128, T])
    c_sinr = din("c_sinr", [128, T])
    c_cosa = din("c_cosa", [128, 18, 8])
    c_sina = din("c_sina", [128, 18, 8])
    c_mask = din("c_mask", [128, 512])
    c_ident = din("c_ident", [128, 128])
    c_eseg = din("c_eseg", [128, 32])
    c_ek = din("c_ek", [128, 2])
    c_eq = din("c_eq", [128, 256])
    c_em = din("c_em", [128, 512])
    c_dist = din("c_dist", [128, 8])
    c_sel = din("c_sel", [128, 8])
    c_e128 = din("c_e128", [128, 16])

    y_out = nc.dram_tensor("y_out", [T, D], F32, kind="ExternalOutput").ap()
    okind = "ExternalOutput" if dbg else "Internal"
    OC = nc.dram_tensor("OC", [8, 128, T], BF16, kind=okind).ap()
    OA = nc.dram_tensor("OA", [16, 64, T], BF16, kind=okind).ap()
    OR = nc.dram_tensor("OR", [16, 128, T], BF16, kind=okind).ap()
    x1e = nc.dram_tensor("x1e", [TE, D], F32, kind=okind).ap()
    fs_bounce = nc.dram_tensor("fs_bounce", [512, 512], F32)
    fs_gath = nc.dram_tensor("fs_gath", [2048, 512], F32)
    h_bounce = nc.dram_tensor("h_bounce", [256, D], F32)
    h_gath = nc.dram_tensor("h_gath", [1024, D], F32)
    u_OC, u_OA, u_OR, u_x1e, u_fsb, u_fsg, u_hb, u_hg, u_yout = [U() for _ in range(9)]
    RG = [[0, 1, 2, 3], [4, 5, 6, 7]]

    with ExitStack() as st0:
        k = KB(nc, st0)

        ncount = [0]

        def sb(st, name, shape, dt):
            ncount[0] += 1
            return st.enter_context(nc.sbuf_tensor(f"{name}_{ncount[0]}", shape, dt))

        PF = [st0.enter_context(nc.psum_tensor(f"pf{i}", [128, 512], F32)) for i in range(7)]
        uPF = [U() for _ in range(7)]
        PT = st0.enter_context(nc.psum_tensor("ptb", [128, 8, 128], BF16))
        uPT = U()
        hT = sb(st0, "hT", [128, 8, TE], BF16)
        uhT = [U() for _ in range(18)]
        ident = sb(st0, "ident", [128, 128], F32); identb = sb(st0, "identb", [128, 128], BF16)
        onesf = sb(st0, "onesf", [128, 128], F32); onesb = sb(st0, "onesb", [128, 128], BF16)
        maskb = sb(st0, "maskb", [128, 512], BF16)
        eseg = sb(st0, "eseg", [128, 32], F32); ek = sb(st0, "ek", [128, 2], F32)
        eq = sb(st0, "eq", [128, 256], F32); em = sb(st0, "em", [128, 512], F32)
        dist = sb(st0, "dist", [128, 8], F32); sel = sb(st0, "sel", [128, 8], F32)
        e128 = sb(st0, "e128", [128, 16], F32)
        cosa = sb(st0, "cosa", [128, 18, 8], F32); sina = sb(st0, "sina", [128, 18, 8], F32)
        lg = sb(st0, "lg", [128, 8], F32)
        qg = sb(st0, "qg", [128, 64], F32); kg = sb(st0, "kg", [128, 64], F32)
        snk = sb(st0, "snk", [128, 16], F32); snke = sb(st0, "snke", [128, 16], F32)
        negc = sb(st0, "negc", [128, 1], F32); tmpc = sb(st0, "tmpc", [128, 4], F32)
        g2 = sb(st0, "g2", [128, 128], F32)
        prm = sb(st0, "prm", [37, D], F32)
        prT = sb(st0, "prT", [128, 8, 37], F32)
        uC = U()
        uL = U()

        for (t, src) in ((ident, c_ident), (eseg, c_eseg), (ek, c_ek), (eq, c_eq), (em, c_em),
                         (dist, c_dist), (sel, c_sel), (cosa, c_cosa), (sina, c_sina), (e128, c_e128)):
            k.dma(S, t[:], src, w=[uC])
        k.dma(G, identb[:], c_ident, w=[uC])
        k.dma(G, maskb[:], c_mask, w=[uC])
        k.op(V, lambda e: e.memset(onesf[:], 1.0 / 1024.0), w=[uC])
        k.op(V, lambda e: e.memset(onesb[:], 1.0), w=[uC])
        k.barrier()

        def phase_T(l, Wt, uWt):
            with ExitStack() as st:
                qTg = sb(st, "qTg", [64, 16, 512], BF16); uqT = [U() for _ in range(4)]
                kT = sb(st, "kTres", [64, 4, TE], BF16); ukT = [U() for _ in range(18)]
                Vr = sb(st, "Vres", [128, 18, 256], BF16); uVr = [U() for _ in range(18)]
                sq = [sb(st, f"sq{i}", [128, 1280], F32) for i in range(2)]; usq = [U(), U()]
                ssa = [sb(st, f"ssa{i}", [128, 20], F32) for i in range(2)]; rsa = [sb(st, f"rsa{i}", [128, 20], F32) for i in range(2)]; urs = [U(), U()]
                qn = [sb(st, f"qn{i}", [128, 1280], F32) for i in range(2)]; uqn = [U(), U()]
                rt = [[sb(st, f"rt{p}{i}", [128, 20, 8], F32) for i in range(4)] for p in range(2)]; urt = [[U() for _ in range(4)] for _ in range(2)]
                qb = [sb(st, f"qb{i}", [128, 1024], BF16) for i in range(2)]; uqb = [U(), U()]
                kb = [sb(st, f"kb{i}", [128, 256], BF16) for i in range(2)]; ukb = [U(), U()]
                sq3 = [sq[p][:].rearrange("p (h d) -> p h d", d=64) for p in range(2)]
                qn3 = [qn[p][:].rearrange("p (h d) -> p h d", d=64) for p in range(2)]

                def norm_rot(t, h0, h1, p, banks):
                    nh = h1 - h0
                    k.op(V, lambda e: e.tensor_reduce(out=ssa[p][:, h0:h1], in_=sq3[p][:, h0:h1, :], axis=AX.X, op=ALU.add), r=[usq[p]], w=[urs[p]])
                    k.op(V, lambda e: e.tensor_scalar(out=rsa[p][:, h0:h1], in0=ssa[p][:, h0:h1], scalar1=1.0 / 64.0, scalar2=EPS, op0=ALU.mult, op1=ALU.add), r=[urs[p]], w=[urs[p]])
                    k.op(A, lambda e: e.activation(out=rsa[p][:, h0:h1], in_=rsa[p][:, h0:h1], func=AF.Sqrt), r=[urs[p]], w=[urs[p]])
                    k.op(V, lambda e: e.reciprocal(out=rsa[p][:, h0:h1], in_=rsa[p][:, h0:h1]), r=[urs[p]], w=[urs[p]])
                    if h0 == 0:
                        for half in range(2):
                            bk, ubk = banks[half]
                            k.op(V, lambda e, half=half, bk=bk: e.tensor_tensor(out=qn3[p][:, half * 8:(half + 1) * 8, :], in0=bk[:].rearrange("p (h d) -> p h d", d=64), in1=rsa[p][:, half * 8:(half + 1) * 8].unsqueeze(2).broadcast_to([128, 8, 64]), op=ALU.mult), r=[ubk, urs[p]], w=[uqn[p]])
                        k.op(G, lambda e: e.tensor_tensor(out=qn3[p][:, 0:16, :], in0=qn3[p][:, 0:16, :], in1=qg[:].unsqueeze(1).broadcast_to([128, 16, 64]), op=ALU.mult), r=[uqn[p], uL], w=[uqn[p]])
                    else:
                        bk, ubk = banks[0]
                        k.op(V, lambda e, bk=bk: e.tensor_tensor(out=qn3[p][:, 16:20, :], in0=bk[:, 0:256].rearrange("p (h d) -> p h d", d=64), in1=rsa[p][:, 16:20].unsqueeze(2).broadcast_to([128, 4, 64]), op=ALU.mult), r=[ubk, urs[p]], w=[uqn[p]])
                        k.op(G, lambda e: e.tensor_tensor(out=qn3[p][:, 16:20, :], in0=qn3[p][:, 16:20, :], in1=kg[:].unsqueeze(1).broadcast_to([128, 4, 64]), op=ALU.mult), r=[uqn[p], uL], w=[uqn[p]])
                    cb = cosa[:, t, :].unsqueeze(1).broadcast_to([128, nh, 8])
                    sbb = sina[:, t, :].unsqueeze(1).broadcast_to([128, nh, 8])
                    x1 = qn3[p][:, h0:h1, 0:8]
                    x2 = qn3[p][:, h0:h1, 8:16]
                    r_, ur_ = rt[p], urt[p]
                    k.op(V, lambda e: e.tensor_tensor(out=r_[0][:, h0:h1, :], in0=x1, in1=cb, op=ALU.mult), r=[uqn[p], uC], w=[ur_[0]])
                    k.op(G, lambda e: e.tensor_tensor(out=r_[1][:, h0:h1, :], in0=x2, in1=sbb, op=ALU.mult), r=[uqn[p], uC], w=[ur_[1]])
                    k.op(V, lambda e: e.tensor_tensor(out=r_[2][:, h0:h1, :], in0=x2, in1=cb, op=ALU.mult), r=[uqn[p], uC], w=[ur_[2]])
                    k.op(G, lambda e: e.tensor_tensor(out=r_[3][:, h0:h1, :], in0=x1, in1=sbb, op=ALU.mult), r=[uqn[p], uC], w=[ur_[3]])
                    k.op(V, lambda e: e.tensor_tensor(out=x1, in0=r_[0][:, h0:h1, :], in1=r_[1][:, h0:h1, :], op=ALU.subtract), r=[ur_[0], ur_[1], ur_[2], ur_[3]], w=[uqn[p]])
                    k.op(G, lambda e: e.tensor_tensor(out=x2, in0=r_[2][:, h0:h1, :], in1=r_[3][:, h0:h1, :], op=ALU.add), r=[ur_[2], ur_[3]], w=[uqn[p]])

                def kv_s1(t):
                    p = t % 2
                    bk, ubk = PF[2 + p], uPF[2 + p]
                    for kk in range(8):
                        k.op(PE, lambda e, kk=kk: e.matmul(bk[:], lhsT=hT[:, kk, t * 128:(t + 1) * 128], rhs=Wt[:, kk, 1024:1536], start=(kk == 0), stop=(kk == 7)), r=[uWt, uhT[t]], w=[ubk])
                    k.op(A, lambda e: e.activation(out=sq[p][:, 1024:1280], in_=bk[:, 0:256], func=AF.Square), r=[ubk], w=[usq[p]])
                    norm_rot(t, 16, 20, p, [(bk, ubk)])
                    k.op(A, lambda e: e.activation(out=kb[p][:], in_=qn[p][:, 1024:1280], func=AF.Copy), r=[uqn[p]], w=[ukb[p]])
                    k.op(A, lambda e: e.activation(out=Vr[:, t, :], in_=bk[:, 256:512], func=AF.Copy), r=[ubk], w=[uVr[t]])

                def kv_s2(t):
                    p = t % 2
                    for g in range(4):
                        k.op(PE, lambda e, g=g: e.transpose(out=PT[0:64, g, :], in_=kb[p][:, g * 64:(g + 1) * 64], identity=identb[:]), r=[ukb[p], uC], w=[uPT])
                    k.op(V, lambda e: e.tensor_copy(out=kT[:, :, t * 128:(t + 1) * 128], in_=PT[0:64, 0:4, :]), r=[uPT], w=[ukT[t]])
                kv_s1(0)
                for t in range(18):
                    if t < 17:
                        kv_s1(t + 1)
                    kv_s2(t)
                if dbg == 'T1':
                    k.barrier()
                    return
                gT = sb(st, "gT", [64, 16, 512], BF16); ugT = U()
                pt = [sb(st, f"pt{i}", [128, 512], BF16) for i in range(3)]; upt = [U() for _ in range(3)]
                den = sb(st, "den", [64, 512], F32); uden = U()
                rec = sb(st, "rec", [64, 512], F32); urec = U()
                on = sb(st, "on", [64, 512], F32); uon = U()
                on2 = [on, on]; uon2 = [uon, uon]
                pt_b = [sb(st, f"ptb{i}", [128, 512], BF16) for i in range(3)]; upt_b = [U() for _ in range(3)]
                pt2 = [pt, pt_b]; upt2 = [upt, upt_b]
                oaT = sb(st, "oaT", [64, 16, 512], BF16); uoaT = U()
                for tg in range(4):
                    e0 = 128 + tg * 512
                    hr = uhT[e0 // 128:(e0 + 512) // 128]
                    def q_s1(nn):
                        p = nn % 2
                        t = tg * 4 + nn + 1
                        banks = [(PF[2 * p], uPF[2 * p]), (PF[2 * p + 1], uPF[2 * p + 1])]
                        for half in range(2):
                            bk, ubk = banks[half]
                            for kk in range(8):
                                k.op(PE, lambda e, half=half, kk=kk, bk=bk: e.matmul(bk[:], lhsT=hT[:, kk, t * 128:(t + 1) * 128], rhs=Wt[:, kk, half * 512:(half + 1) * 512], start=(kk == 0), stop=(kk == 7)), r=[uWt, uhT[t]], w=[ubk])
                            k.op(A, lambda e, half=half, bk=bk: e.activation(out=sq[p][:, half * 512:(half + 1) * 512], in_=bk[:], func=AF.Square), r=[ubk], w=[usq[p]])
                        norm_rot(t, 0, 16, p, banks)
                        k.op(A, lambda e: e.activation(out=qb[p][:], in_=qn[p][:, 0:1024], func=AF.Copy), r=[uqn[p]], w=[uqb[p]])

                    def q_s2(nn):
                        p = nn % 2
                        for rr in range(2):
                            for j in range(8):
                                hh = rr * 8 + j
                                k.op(PE, lambda e, j=j, hh=hh: e.transpose(out=PT[0:64, j, :], in_=qb[p][:, hh * 64:(hh + 1) * 64], identity=identb[:]), r=[uqb[p], uC], w=[uPT])
                            k.op(V, lambda e, rr=rr: e.tensor_copy(out=qTg[:, rr * 8:(rr + 1) * 8, nn * 128:(nn + 1) * 128], in_=PT[0:64, :, :]), r=[uPT], w=[uqT[nn]])
                    q_s1(0)
                    for nn in range(4):
                        if nn < 3:
                            q_s1(nn + 1)
                        q_s2(nn)
                    for hh in range(16):
                        ps, ups = PF[hh % 2], uPF[hh % 2]
                        for kk in range(8):
                            k.op(PE, lambda e, ps=ps, hh=hh, kk=kk, e0=e0: e.matmul(ps[0:64, :], lhsT=Wt[:, kk, 1536 + hh * 64:1536 + (hh + 1) * 64], rhs=hT[:, kk, e0:e0 + 512], start=(kk == 0), stop=(kk == 7)), r=[uWt] + hr, w=[ups])
                        k.op(A, lambda e, ps=ps, hh=hh: e.activation(out=gT[:, hh, :], in_=ps[0:64, :], func=AF.Silu), r=[ups], w=[ugT])
                    def t2_front(idx, nn, g):
                        n = tg * 4 + nn
                        t = n + 1
                        ptc, uptc = pt2[idx % 2], upt2[idx % 2]
                        for mi, m in enumerate((t - 1, t, t + 1)):
                            k.op(PE, lambda e, mi=mi, m=m: e.matmul(PF[2 + mi][:], lhsT=kT[:, g, m * 128:(m + 1) * 128], rhs=qTg[:, 4 * g:4 * g + 4, nn * 128:(nn + 1) * 128], start=True, stop=True), r=[ukT[m], uqT[nn]], w=[uPF[2 + mi]])
                            k.op(A, lambda e, mi=mi: e.activation(out=ptc[mi][:], in_=PF[2 + mi][:], func=AF.Exp, bias=negc[:, 0:1], scale=0.125), r=[uPF[2 + mi], uL], w=[uptc[mi]])
                        mL = 256 if t == 1 else 0
                        mR = 384 if t == 16 else 128
                        k.op(V, lambda e: e.tensor_tensor(out=ptc[0][:].rearrange("p (a i) -> p a i", a=4), in0=ptc[0][:].rearrange("p (a i) -> p a i", a=4), in1=maskb[:, mL:mL + 128].unsqueeze(1).broadcast_to([128, 4, 128]), op=ALU.mult), r=[uptc[0], uC], w=[uptc[0]])
                        k.op(G, lambda e: e.tensor_tensor(out=ptc[2][:].rearrange("p (a i) -> p a i", a=4), in0=ptc[2][:].rearrange("p (a i) -> p a i", a=4), in1=maskb[:, mR:mR + 128].unsqueeze(1).broadcast_to([128, 4, 128]), op=ALU.mult), r=[uptc[2], uC], w=[uptc[2]])

                    def t2_back(idx, nn, g):
                        n = tg * 4 + nn
                        t = n + 1
                        ptc, uptc = pt2[idx % 2], upt2[idx % 2]
                        onc, uonc = on2[idx % 2], uon2[idx % 2]
                        for mi, m in enumerate((t - 1, t, t + 1)):
                            k.op(PE, lambda e, mi=mi, m=m: e.matmul(PF[5][0:64, :], lhsT=Vr[:, m, g * 64:(g + 1) * 64], rhs=ptc[mi][:], start=(mi == 0), stop=(mi == 2)), r=[uVr[m], uptc[mi]], w=[uPF[5]])
                        for mi, m in enumerate((t - 1, t, t + 1)):
                            k.op(PE, lambda e, mi=mi: e.matmul(PF[6][0:64, :], lhsT=onesb[:, 0:64], rhs=ptc[mi][:], start=(mi == 0), stop=(mi == 2)), r=[uC, uptc[mi]], w=[uPF[6]])
                        for ei in range(4):
                            hd = 4 * g + ei
                            k.op(A, lambda e, ei=ei, hd=hd: e.activation(out=den[:, ei * 128:(ei + 1) * 128], in_=PF[6][0:64, ei * 128:(ei + 1) * 128], func=AF.Ln, bias=snke[0:64, hd:hd + 1], scale=1.0), r=[uPF[6], uL], w=[uden])
                        k.op(A, lambda e: e.activation(out=rec[:], in_=den[:], func=AF.Exp, scale=-1.0), r=[uden], w=[urec])
                        k.op(V, lambda e: e.tensor_tensor(out=onc[:], in0=PF[5][0:64, :], in1=rec[:], op=ALU.mult), r=[uPF[5], urec], w=[uonc])
                        k.op(G, lambda e: e.tensor_tensor(out=oaT[:, 4 * g:4 * g + 4, nn * 128:(nn + 1) * 128], in0=onc[:].rearrange("p (a i) -> p a i", a=4), in1=gT[:, 4 * g:4 * g + 4, nn * 128:(nn + 1) * 128], op=ALU.mult), r=[uonc, ugT], w=[uoaT])
                    its = [(nn, g) for nn in range(4) for g in range(4)]
                    for idx, (nn, g) in enumerate(its):
                        t2_front(idx, nn, g)
                        if idx > 0:
                            t2_back(idx - 1, *its[idx - 1])
                    t2_back(len(its) - 1, *its[-1])
                    k.dma(S, OA[:, :, tg * 512:(tg + 1) * 512].rearrange("h d t -> d h t"), oaT[:], r=[uoaT], w=[u_OA])
                k.barrier()

        def phase_R(l):
            with ExitStack() as st:
                Wr = sb(st, "Wr", [128, 8, 2048], BF16); uWr = U()
                qT = sb(st, "rqT", [128, 2, T], BF16); uq = [U() for _ in range(4)]
                kT = sb(st, "rkT", [128, 2, T], BF16); ukk = [U() for _ in range(4)]
                Vv = sb(st, "rV", [128, 16, 512], BF16); uV = [U() for _ in range(16)]
                kt = sb(st, "rkt", [128, 16, 256], BF16); ukt = [U() for _ in range(16)]
                Pp = sb(st, "rP", [128, 16, 512], F32); uP = [U() for _ in range(16)]
                cs = sb(st, "rcs", [128, 2, 512], F32); ucs = U()
                tq = [sb(st, f"rtq{i}", [128, 512], F32) for i in range(4)]; utq = [U() for _ in range(4)]
                Sf = sb(st, "Sf", [128, 2, 512], F32); Sb = sb(st, "Sb", [128, 2, 512], F32); uSf = U(); uSb = U()
                Sfb = sb(st, "Sfb", [128, 2, 512], BF16); Sbb = sb(st, "Sbb", [128, 2, 512], BF16); uSfb = U(); uSbb = U()
                fsum = sb(st, "fsum", [128, 4, 512], F32); ufs = U()
                DTm = sb(st, "DTm", [128, 128], F32); dq = sb(st, "dq", [128, 256], F32); dsg = sb(st, "dsg", [128, 32], F32)
                dk = sb(st, "dk", [128, 2], F32); gC = sb(st, "gC", [128, 2], F32); wr = sb(st, "wr", [128, 8], F32)
                tmpD = sb(st, "tmpD", [128, 256], F32); uH = U()
                kfs = sb(st, "kfs", [128, 256], BF16); kbs = sb(st, "kbs", [128, 256], BF16); ukfs = U(); ukbs = U()
                qd = sb(st, "qd", [128, 2, 128], BF16); uqd = U()
                kdc = sb(st, "kdc", [128, 256], BF16); ukdc = U()
                sd = sb(st, "sd", [128, 128], BF16); usd = U()
                oo = sb(st, "oo", [128, 512], F32); uoo = U()
                onn = sb(st, "onn", [128, 512], F32); uonn = U()
                sgg = sb(st, "sgg", [128, 512], F32); usgg = U()
                og = sb(st, "og", [128, 512], BF16); uog = U()
                og_b = sb(st, "og_b", [128, 512], BF16); uog_b = U()
                qd_b = sb(st, "qd_b", [128, 2, 128], BF16); uqd_b = U()
                qd_c = sb(st, "qd_c", [128, 2, 128], BF16); uqd_c = U()
                kdc_b = sb(st, "kdc_b", [128, 256], BF16); ukdc_b = U()
                Sfb_c = sb(st, "Sfb_c", [128, 2, 512], BF16); uSfb_c = U()
                qd_d = sb(st, "qd_d", [128, 2, 128], BF16); uqd_d = U()
                oo_b, uoo_b = tq[0], utq[0]
                onn_b, uonn_b = tq[1], utq[1]
                sgg_b, usgg_b = tq[2], utq[2]
                bst_b = sb(st, "bst_b", [128, 6], F32); mv_b = sb(st, "mv_b", [128, 2], F32); ubn_b = U()
                gn = sb(st, "gn", [128, 32], F32)
                bst = sb(st, "bst", [128, 6], F32); mv = sb(st, "mv", [128, 2], F32); ubn = U()
                orT = sb(st, "orT", [128, 4, 512], BF16); uorT = U()
                RSTOP = int(os.environ.get("RSTOP", "0"))

                class _Stop(Exception):
                    pass

                def ck(n_):
                    if RSTOP == n_:
                        raise _Stop()
                try:
                  for h in range(4):
                      wl = [(1024, OFF_RV + h * 512), (1536, OFF_RG + h * 512)]
                      if h % 2 == 0:
                          wl = [(0, OFF_RQ + h * 256), (512, OFF_RK + h * 256)] + wl
                      for (dst, off) in wl:
                          k.dma(G, Wr[:, :, dst:dst + 512], w_in[l, :, off:off + 512].rearrange("(k p) c -> p k c", p=128), w=[uWr])
                      qc0 = (h % 2) * 256
                      kc0 = 512 + (h % 2) * 256
                      lgf = lg[:, h:h + 1]
                      lgb = lg[:, 4 + h:5 + h]
                      hc = dict(r=[uL, uC, uH], w=[uH])
                      k.op(A, lambda e: e.activation(out=dsg[:, 0:16], in_=eseg[:, 0:16], func=AF.Exp, scale=lgf), **hc)
                      k.op(A, lambda e: e.activation(out=dsg[:, 16:32], in_=eseg[:, 16:32], func=AF.Exp, scale=lgb), **hc)
                      k.op(A, lambda e: e.activation(out=dk[:, 0:1], in_=ek[:, 0:1], func=AF.Exp, scale=lgf), **hc)
                      k.op(A, lambda e: e.activation(out=dk[:, 1:2], in_=ek[:, 1:2], func=AF.Exp, scale=lgb), **hc)
                      k.op(A, lambda e: e.activation(out=dq[:, 0:128], in_=eq[:, 0:128], func=AF.Exp, scale=lgf), **hc)
                      k.op(A, lambda e: e.activation(out=dq[:, 128:256], in_=eq[:, 128:256], func=AF.Exp, scale=lgb), **hc)
                      k.op(V, lambda e: e.tensor_scalar_mul(out=dq[:], in0=dq[:], scalar1=1.0 / 16.0), **hc)
                      k.op(A, lambda e: e.activation(out=gn[:, 0:16], in_=e128[:], func=AF.Exp, scale=lgf), **hc)
                      k.op(A, lambda e: e.activation(out=gn[:, 16:32], in_=e128[:], func=AF.Exp, scale=lgb), **hc)
                      k.op(A, lambda e: e.activation(out=gC[:, 0:1], in_=lgf, func=AF.Exp, scale=128.0), **hc)
                      k.op(A, lambda e: e.activation(out=gC[:, 1:2], in_=lgb, func=AF.Exp, scale=128.0), **hc)
                      k.op(A, lambda e: e.activation(out=wr[:, 0:4], in_=dist[:, 0:4], func=AF.Exp, scale=lgf), **hc)
                      k.op(A, lambda e: e.activation(out=wr[:, 4:8], in_=dist[:, 4:8], func=AF.Exp, scale=lgb), **hc)
                      k.op(A, lambda e: e.activation(out=tmpD[:, 0:128], in_=em[:, 0:128], func=AF.Exp, scale=lgf), **hc)
                      k.op(A, lambda e: e.activation(out=tmpD[:, 128:256], in_=em[:, 128:256], func=AF.Exp, scale=lgb), **hc)
                      k.op(V, lambda e: e.tensor_tensor(out=tmpD[:], in0=tmpD[:], in1=em[:, 256:512], op=ALU.mult), **hc)
                      k.op(V, lambda e: e.tensor_tensor(out=DTm[:], in0=tmpD[:, 0:128], in1=tmpD[:, 128:256], op=ALU.add), **hc)
                      k.op(V, lambda e: e.tensor_scalar_mul(out=DTm[:], in0=DTm[:], scalar1=1.0 / 16.0), **hc)
                      ck(1)
                      def r0_proj(tg, col0, dstT, ud):
                          e0 = 128 + tg * 512
                          hr = uhT[e0 // 128:(e0 + 512) // 128]
                          for dc in range(2):
                              for kk in range(8):
                                  k.op(PE, lambda e, dc=dc, kk=kk: e.matmul(PF[dc][:], lhsT=Wr[:, kk, col0 + dc * 128:col0 + (dc + 1) * 128], rhs=hT[:, kk, e0:e0 + 512], start=(kk == 0), stop=(kk == 7)), r=[uWr] + hr, w=[uPF[dc]])
                          k.op(V, lambda e: e.tensor_tensor(out=tq[0][:], in0=PF[0][:], in1=cs[:, 0, :], op=ALU.mult), r=[uPF[0], ucs], w=[utq[0]])
                          k.op(V, lambda e: e.tensor_tensor(out=tq[1][:], in0=PF[1][:], in1=cs[:, 1, :], op=ALU.mult), r=[uPF[1], ucs], w=[utq[1]])
                          k.op(V, lambda e: e.tensor_tensor(out=tq[2][:], in0=PF[1][:], in1=cs[:, 0, :], op=ALU.mult), r=[uPF[1], ucs], w=[utq[2]])
                          k.op(V, lambda e: e.tensor_tensor(out=tq[3][:], in0=PF[0][:], in1=cs[:, 1, :], op=ALU.mult), r=[uPF[0], ucs], w=[utq[3]])
                          k.op(V, lambda e: e.tensor_tensor(out=dstT[:, 0, tg * 512:(tg + 1) * 512], in0=tq[0][:], in1=tq[1][:], op=ALU.subtract), r=[utq[0], utq[1]], w=[ud])
                          k.op(G, lambda e: e.tensor_tensor(out=dstT[:, 1, tg * 512:(tg + 1) * 512], in0=tq[2][:], in1=tq[3][:], op=ALU.add), r=[utq[2], utq[3]], w=[ud])

                      def r0_front(tg):
                          k.dma(S, cs[:, 0, :], c_cosr[:, tg * 512:(tg + 1) * 512], w=[ucs])
                          k.dma(S, cs[:, 1, :], c_sinr[:, tg * 512:(tg + 1) * 512], w=[ucs])
                          r0_proj(tg, qc0, qT, uq[tg])
                          for nn in range(4):
                              n = tg * 4 + nn
                              t = n + 1
                              for kk in range(8):
                                  k.op(PE, lambda e, kk=kk, t=t: e.matmul(PF[2][:], lhsT=hT[:, kk, t * 128:(t + 1) * 128], rhs=Wr[:, kk, 1024:1536], start=(kk == 0), stop=(kk == 7)), r=[uWr, uhT[t]], w=[uPF[2]])
                              k.op(A, lambda e, n=n: e.activation(out=Vv[:, n, :], in_=PF[2][:], func=AF.Copy), r=[uPF[2]], w=[uV[n]])
                          r0_proj(tg, kc0, kT, ukk[tg])

                      def r0_back(tg):
                          for nn in range(4):
                              n = tg * 4 + nn
                              for dc in range(2):
                                  k.op(PE, lambda e, dc=dc, n=n: e.transpose(out=PT[:, dc, :], in_=kT[:, dc, n * 128:(n + 1) * 128], identity=identb[:]), r=[ukk[tg], uC], w=[uPT])
                              ptv = PT[:, 0:2, :]
                              k.op(A, lambda e, n=n, ptv=ptv: e.activation(out=kfs[:].rearrange("p (a d) -> p a d", a=2), in_=ptv, func=AF.Copy, scale=dsg[:, n:n + 1]), r=[uPT, uH], w=[ukfs])
                              k.op(A, lambda e, n=n, ptv=ptv: e.activation(out=kbs[:].rearrange("p (a d) -> p a d", a=2), in_=ptv, func=AF.Copy, scale=dsg[:, 16 + n:17 + n]), r=[uPT, uH], w=[ukbs])
                              k.op(A, lambda e, n=n, ptv=ptv: e.activation(out=kt[:, n, :].rearrange("p (a d) -> p a d", a=2), in_=ptv, func=AF.Copy), r=[uPT], w=[ukt[n]])
                              for dc in range(2):
                                  k.op(PE, lambda e, dc=dc, n=n: e.matmul(PF[3 + dc][:], lhsT=kfs[:, dc * 128:(dc + 1) * 128], rhs=Vv[:, n, :], start=(n == 0), stop=(n == 15)), r=[ukfs, uV[n]], w=[uPF[3 + dc]])
                                  k.op(PE, lambda e, dc=dc, n=n: e.matmul(PF[5 + dc][:], lhsT=kbs[:, dc * 128:(dc + 1) * 128], rhs=Vv[:, n, :], start=(n == 0), stop=(n == 15)), r=[ukbs, uV[n]], w=[uPF[5 + dc]])
                      r0_front(0)
                      for tg in range(4):
                          if tg < 3:
                              r0_front(tg + 1)
                          r0_back(tg)
                      if dbg == 'R0':
                          k.barrier()
                          return
                      for j in range(4):
                          k.op(A, lambda e, j=j: e.activation(out=fsum[:, j, :], in_=PF[3 + j][:], func=AF.Copy), r=[uPF[3 + j]], w=[ufs])
                      k.dma(S, fs_bounce.ap().rearrange("(j p) v -> p j v", p=128), fsum[:], r=[ufs], w=[u_fsb])
                      ccs = k.new_sem("cc")
                      k.custom(G, lambda e: e.collective_compute("AllGather", ALU.bypass, replica_groups=RG, ins=[fs_bounce.ap().opt()], outs=[fs_gath.ap().opt()]), ccs, 1, r=[u_fsb], w=[u_fsg])
                      k.op(V, lambda e: e.memset(Sf[:], 0.0), r=[uSf], w=[uSf])
                      k.op(V, lambda e: e.memset(Sb[:], 0.0), r=[uSb], w=[uSb])
                      k.op(V, lambda e: e.memset(Sfb[:], 0.0), r=[uSfb], w=[uSfb])
                      k.op(V, lambda e: e.memset(Sbb[:], 0.0), r=[uSbb], w=[uSbb])
                      SfbX = [Sfb, Sfb_c]
                      uSfbX = [uSfb, uSfb_c]
                      k.op(V, lambda e: e.memset(Sfb_c[:], 0.0), r=[uSfb_c], w=[uSfb_c])
                      for i in range(16):
                          nb, nf = 15 - i, i
                          tgb, tgf = nb // 4, nf // 4
                          cur, nxt = i % 2, (i + 1) % 2
                          k.op(A, lambda e, nb=nb: e.activation(out=kdc[:], in_=kt[:, nb, :], func=AF.Copy, scale=dk[:, 1:2]), r=[ukt[nb], uH], w=[ukdc])
                          k.op(A, lambda e, nf=nf: e.activation(out=kdc_b[:], in_=kt[:, nf, :], func=AF.Copy, scale=dk[:, 0:1]), r=[ukt[nf], uH], w=[ukdc_b])
                          for dc in range(2):
                              k.op(PE, lambda e, dc=dc, nb=nb: e.matmul(PF[1 + dc][:], lhsT=kdc[:, dc * 128:(dc + 1) * 128], rhs=Vv[:, nb, :], start=True, stop=True), r=[ukdc, uV[nb]], w=[uPF[1 + dc]])
                          for dc in range(2):
                              k.op(PE, lambda e, dc=dc, nf=nf: e.matmul(PF[5 + dc][:], lhsT=kdc_b[:, dc * 128:(dc + 1) * 128], rhs=Vv[:, nf, :], start=True, stop=True), r=[ukdc_b, uV[nf]], w=[uPF[5 + dc]])
                          k.op(V, lambda e, nb=nb: e.tensor_tensor(out=qd[:], in0=qT[:, :, nb * 128:(nb + 1) * 128], in1=dq[:, 128:256].unsqueeze(1).broadcast_to([128, 2, 128]), op=ALU.mult), r=[uq[tgb], uH], w=[uqd])
                          k.op(G, lambda e, nf=nf: e.tensor_tensor(out=qd_b[:], in0=qT[:, :, nf * 128:(nf + 1) * 128], in1=dq[:, 0:128].unsqueeze(1).broadcast_to([128, 2, 128]), op=ALU.mult), r=[uq[tgf], uH], w=[uqd_b])
                          for dc in range(2):
                              k.op(PE, lambda e, dc=dc: e.matmul(PF[0][:], lhsT=qd[:, dc, :], rhs=Sbb[:, dc, :], start=(dc == 0), stop=(dc == 1)), r=[uqd, uSbb], w=[uPF[0]])
                          for dc in range(2):
                              k.op(PE, lambda e, dc=dc, nf=nf: e.matmul(PF[3][:, 0:128], lhsT=kT[:, dc, nf * 128:(nf + 1) * 128], rhs=qT[:, dc, nf * 128:(nf + 1) * 128], start=(dc == 0), stop=(dc == 1)), r=[ukk[tgf], uq[tgf]], w=[uPF[3]])
                          k.op(V, lambda e: e.tensor_tensor(out=sd[:], in0=PF[3][:, 0:128], in1=DTm[:], op=ALU.mult), r=[uPF[3], uH], w=[usd])
                          k.op(PE, lambda e, nf=nf: e.matmul(PF[4][:], lhsT=sd[:], rhs=Vv[:, nf, :], start=True, stop=False), r=[usd, uV[nf]], w=[uPF[4]])
                          for dc in range(2):
                              k.op(PE, lambda e, dc=dc, cur=cur: e.matmul(PF[4][:], lhsT=qd_b[:, dc, :], rhs=SfbX[cur][:, dc, :], start=False, stop=(dc == 1)), r=[uqd_b, uSfbX[cur]], w=[uPF[4]])
                          for dc in range(2):
                              k.op(V, lambda e, dc=dc: e.scalar_tensor_tensor(out=Sb[:, dc, :], in0=Sb[:, dc, :], scalar=gC[:, 1:2], in1=PF[1 + dc][:], op0=ALU.mult, op1=ALU.add), r=[uPF[1 + dc], uH, uSb], w=[uSb])
                          for dc in range(2):
                              k.op(V, lambda e, dc=dc: e.scalar_tensor_tensor(out=Sf[:, dc, :], in0=Sf[:, dc, :], scalar=gC[:, 0:1], in1=PF[5 + dc][:], op0=ALU.mult, op1=ALU.add), r=[uPF[5 + dc], uH, uSf], w=[uSf])
                          k.op(A, lambda e: e.activation(out=Sbb[:], in_=Sb[:], func=AF.Copy), r=[uSb], w=[uSbb])
                          k.op(A, lambda e, nxt=nxt: e.activation(out=SfbX[nxt][:], in_=Sf[:], func=AF.Copy), r=[uSf], w=[uSfbX[nxt]])
                          if i <= 7:
                              k.op(A, lambda e, nb=nb: e.activation(out=Pp[:, nb, :], in_=PF[0][:], func=AF.Copy), r=[uPF[0]], w=[uP[nb]])
                              k.op(A, lambda e, nf=nf: e.activation(out=Pp[:, nf, :], in_=PF[4][:], func=AF.Copy), r=[uPF[4]], w=[uP[nf]])
                          else:
                              k.op(V, lambda e, nb=nb: e.tensor_tensor(out=Pp[:, nb, :], in0=PF[0][:], in1=Pp[:, nb, :], op=ALU.add), r=[uPF[0], uP[nb]], w=[uP[nb]])
                              k.op(V, lambda e, nf=nf: e.tensor_tensor(out=Pp[:, nf, :], in0=PF[4][:], in1=Pp[:, nf, :], op=ALU.add), r=[uPF[4], uP[nf]], w=[uP[nf]])
                      for r_ in range(4):
                          k.dma(S, fsum[:], fs_gath.ap()[r_ * 512:(r_ + 1) * 512, :].rearrange("(j p) v -> p j v", p=128), r=[u_fsg], w=[ufs])
                          for dirn, (Sx, uSx) in enumerate(((Sf, uSf), (Sb, uSb))):
                              for dc in range(2):
                                  wcol = wr[:, dirn * 4 + r_:dirn * 4 + r_ + 1]
                                  if r_ == 0:
                                      k.op(V, lambda e, Sx=Sx, dc=dc, dirn=dirn, wcol=wcol: e.tensor_scalar_mul(out=Sx[:, dc, :], in0=fsum[:, dirn * 2 + dc, :], scalar1=wcol), r=[ufs, uH], w=[uSx])
                                  else:
                                      k.op(V, lambda e, Sx=Sx, dc=dc, dirn=dirn, wcol=wcol: e.scalar_tensor_tensor(out=Sx[:, dc, :], in0=fsum[:, dirn * 2 + dc, :], scalar=wcol, in1=Sx[:, dc, :], op0=ALU.mult, op1=ALU.add), r=[ufs, uH, uSx], w=[uSx])
                      k.op(A, lambda e: e.activation(out=Sfb[:], in_=Sf[:], func=AF.Copy), r=[uSf], w=[uSfb])
                      k.op(A, lambda e: e.activation(out=Sbb[:], in_=Sb[:], func=AF.Copy), r=[uSb], w=[uSbb])
                      og2 = [og, og_b]
                      uog2 = [uog, uog_b]
                      qd2 = [qd, qd_b]
                      uqd2 = [uqd, uqd_b]

                      def finish(n):
                          tg = n // 4
                          for vc in range(4):
                              k.op(PE, lambda e, vc=vc, n=n: e.transpose(out=PT[:, vc, :], in_=og2[n % 2][:, vc * 128:(vc + 1) * 128], identity=identb[:]), r=[uog2[n % 2], uC], w=[uPT])
                          k.op(A, lambda e, n=n: e.activation(out=orT[:, :, (n % 4) * 128:(n % 4 + 1) * 128], in_=PT[:, 0:4, :], func=AF.Copy), r=[uPT], w=[uorT])
                          if n % 4 == 3:
                              k.dma(S, OR[h * 4:(h + 1) * 4, :, tg * 512:(tg + 1) * 512].rearrange("c p t -> p c t"), orT[:], r=[uorT], w=[u_OR])
                      oo2 = [oo, oo_b]; uoo2 = [uoo, uoo_b]
                      onn2 = [onn, onn_b]; uonn2 = [uonn, uonn_b]
                      sgg2 = [sgg, sgg_b]; usgg2 = [usgg, usgg_b]
                      bst2 = [bst, bst_b]; mv2 = [mv, mv_b]; ubn2 = [ubn, ubn_b]
                      qdp = [[qd, qd_b], [qd_c, qd_d]]; uqdp = [[uqd, uqd_b], [uqd_c, uqd_d]]
                      pO = [(PF[4], uPF[4]), (PF[0], uPF[0])]
                      pGt = [(PF[5], uPF[5]), (PF[1], uPF[1])]
                      for n0 in range(0, 16, 2):
                          pair = (n0, n0 + 1)
                          for n in pair:
                              p = n % 2
                              t = n + 1
                              bk, ubk = pGt[p]
                              for kk in range(8):
                                  k.op(PE, lambda e, kk=kk, t=t, bk=bk: e.matmul(bk[:], lhsT=hT[:, kk, t * 128:(t + 1) * 128], rhs=Wr[:, kk, 1536:2048], start=(kk == 0), stop=(kk == 7)), r=[uWr, uhT[t]], w=[ubk])
                          for n in pair:
                              p = n % 2
                              bk, ubk = pGt[p]
                              k.op(A, lambda e, p=p, bk=bk: e.activation(out=sgg2[p][:], in_=bk[:], func=AF.Silu), r=[ubk], w=[usgg2[p]])
                          for n in pair:
                              p = n % 2
                              tg = n // 4
                              k.op(V, lambda e, n=n, p=p: e.scalar_tensor_tensor(out=qdp[p][0][:], in0=qT[:, :, n * 128:(n + 1) * 128], scalar=gn[:, n:n + 1], in1=dq[:, 0:128].unsqueeze(1).broadcast_to([128, 2, 128]), op0=ALU.mult, op1=ALU.mult), r=[uq[tg], uH], w=[uqdp[p][0]])
                              k.op(V, lambda e, n=n, p=p: e.scalar_tensor_tensor(out=qdp[p][1][:], in0=qT[:, :, n * 128:(n + 1) * 128], scalar=gn[:, 16 + 15 - n:16 + 16 - n], in1=dq[:, 128:256].unsqueeze(1).broadcast_to([128, 2, 128]), op0=ALU.mult, op1=ALU.mult), r=[uq[tg], uH], w=[uqdp[p][1]])
                          for n in pair:
                              p = n % 2
                              bo, ubo = pO[p]
                              for di, Sxb, uSxb in ((0, Sfb, uSfb), (1, Sbb, uSbb)):
                                  for dc in range(2):
                                      k.op(PE, lambda e, di=di, dc=dc, Sxb=Sxb, p=p, bo=bo: e.matmul(bo[:], lhsT=qdp[p][di][:, dc, :], rhs=Sxb[:, dc, :], start=(di == 0 and dc == 0), stop=(di == 1 and dc == 1)), r=[uqdp[p][di], uSxb], w=[ubo])
                          if n0 > 0:
                              finish(n0 - 2)
                              finish(n0 - 1)
                          for n in pair:
                              p = n % 2
                              bo, ubo = pO[p]
                              k.op(V, lambda e, n=n, p=p, bo=bo: e.tensor_tensor(out=oo2[p][:], in0=bo[:], in1=Pp[:, n, :], op=ALU.add), r=[ubo, uP[n]], w=[uoo2[p]])
                          for n in pair:
                              p = n % 2
                              k.op(V, lambda e, p=p: e.bn_stats(out=bst2[p][:], in_=oo2[p][:]), r=[uoo2[p]], w=[ubn2[p]])
                          for n in pair:
                              p = n % 2
                              k.op(V, lambda e, p=p: e.bn_aggr(out=mv2[p][:], in_=bst2[p][:]), r=[ubn2[p]], w=[ubn2[p]])
                          for n in pair:
                              p = n % 2
                              k.op(V, lambda e, p=p: e.tensor_scalar_add(out=mv2[p][:, 1:2], in0=mv2[p][:, 1:2], scalar1=EPS), r=[ubn2[p]], w=[ubn2[p]])
                          for n in pair:
                              p = n % 2
                              k.op(A, lambda e, p=p: e.activation(out=mv2[p][:, 1:2], in_=mv2[p][:, 1:2], func=AF.Sqrt), r=[ubn2[p]], w=[ubn2[p]])
                          for n in pair:
                              p = n % 2
                              k.op(V, lambda e, p=p: e.reciprocal(out=mv2[p][:, 1:2], in_=mv2[p][:, 1:2]), r=[ubn2[p]], w=[ubn2[p]])
                          for n in pair:
                              p = n % 2
                              k.op(V, lambda e, p=p: e.tensor_scalar(out=onn2[p][:], in0=oo2[p][:], scalar1=mv2[p][:, 0:1], scalar2=mv2[p][:, 1:2], op0=ALU.subtract, op1=ALU.mult), r=[uoo2[p], ubn2[p]], w=[uonn2[p]])
                          for n in pair:
                              p = n % 2
                              k.op(G, lambda e, p=p: e.tensor_tensor(out=og2[p][:], in0=onn2[p][:], in1=sgg2[p][:], op=ALU.mult), r=[uonn2[p], usgg2[p]], w=[uog2[p]])
                      finish(14)
                      finish(15)
                except _Stop:
                    pass
                k.barrier()

        def phase_F(l, xsrc, last):
            with ExitStack() as st:
                mg = sb(st, "mg", [128, 8, T], F32); umg = [U() for _ in range(4)]
                sgm = sb(st, "sgm", [128, 512], F32); usgm = U()
                tmpm = sb(st, "tmpm", [128, 512], F32); utmpm = U()
                for bi, (nk, kp) in enumerate(((8, 128), (16, 128), (16, 64))):
                    with ExitStack() as st2:
                        Wb = sb(st2, f"Wb{bi}", [kp, nk, D], BF16); uWb = U()
                        Wg = sb(st2, f"Wg{bi}", [128, 8, D], BF16); uWg = U()
                        ob = sb(st2, f"ob{bi}", [kp, nk, 512], BF16); uob = U()
                        src_w = (w_conv_out, w_ret_out, w_attn_out)[bi]
                        if bi == 2:
                            view = src_w[l].rearrange("(h d) c -> d h c", d=64)
                        else:
                            view = src_w[l].rearrange("(k p) c -> p k c", p=128)
                        for j in range(0, nk, 4):
                            k.dma(G, Wb[:, j:j + 4, :], view[:, j:j + 4, :], w=[uWb])
                        for j in range(2):
                            c0 = OFF_GL + bi * 1024 + j * 512
                            k.dma(G, Wg[:, :, j * 512:(j + 1) * 512], w_in[l, :, c0:c0 + 512].rearrange("(k p) c -> p k c", p=128), w=[uWg])
                        osrc = (OC, OR, OA)[bi]
                        uos = (u_OC, u_OR, u_OA)[bi]
                        for tg in range(4):
                            e0 = 128 + tg * 512
                            hr = uhT[e0 // 128:(e0 + 512) // 128]
                            k.dma(S, ob[:], osrc[:, :, tg * 512:(tg + 1) * 512].rearrange("c p t -> p c t"), r=[uos], w=[uob])
                            for m in range(8):
                                py, upy = PF[m % 3], uPF[m % 3]
                                pg, upg = PF[3 + m % 3], uPF[3 + m % 3]
                                for j in range(nk):
                                    k.op(PE, lambda e, py=py, j=j, m=m: e.matmul(py[:], lhsT=Wb[:, j, m * 128:(m + 1) * 128], rhs=ob[:, j, :], start=(j == 0), stop=(j == nk - 1)), r=[uWb, uob], w=[upy])
                                for kk in range(8):
                                    k.op(PE, lambda e, pg=pg, kk=kk, m=m, e0=e0: e.matmul(pg[:], lhsT=Wg[:, kk, m * 128:(m + 1) * 128], rhs=hT[:, kk, e0:e0 + 512], start=(kk == 0), stop=(kk == 7)), r=[uWg] + hr, w=[upg])
                                k.op(A, lambda e, pg=pg, m=m, bi=bi: e.activation(out=sgm[:], in_=pg[:], func=AF.Sigmoid, bias=prT[:, m, 34 + bi:35 + bi], scale=1.0), r=[upg, uL], w=[usgm])
                                if bi == 0:
                                    k.op(V, lambda e, py=py, m=m, tg=tg: e.tensor_tensor(out=mg[:, m, tg * 512:(tg + 1) * 512], in0=py[:], in1=sgm[:], op=ALU.mult), r=[upy, usgm], w=[umg[tg]])
                                else:
                                    k.op(V, lambda e, py=py: e.tensor_tensor(out=tmpm[:], in0=py[:], in1=sgm[:], op=ALU.mult), r=[upy, usgm], w=[utmpm])
                                    k.op(G, lambda e, m=m, tg=tg: e.tensor_tensor(out=mg[:, m, tg * 512:(tg + 1) * 512], in0=mg[:, m, tg * 512:(tg + 1) * 512], in1=tmpm[:], op=ALU.add), r=[utmpm, umg[tg]], w=[umg[tg]])
                        k.barrier()
                with ExitStack() as st2:
                    Wo = sb(st2, "Wo", [128, 8, D], BF16); uWo = U()
                    for j in range(2):
                        k.dma(G, Wo[:, j * 4:(j + 1) * 4, :], w_out[l].rearrange("(k p) c -> p k c", p=128)[:, j * 4:(j + 1) * 4, :], w=[uWo])
                    mb = sb(st2, "mb", [128, 8, 128], BF16); umb = U()
                    xt = [sb(st2, f"fxt{i}", [128, D], F32) for i in range(2)]; uxt = [U(), U()]
                    xn = [sb(st2, f"fxn{i}", [128, D], F32) for i in range(2)]; uxn = [U(), U()]
                    for n in range(16):
                        b = n % 2
                        k.op(A, lambda e, n=n: e.activation(out=mb[:], in_=mg[:, :, n * 128:(n + 1) * 128], func=AF.Copy), r=[umg[n // 4]], w=[umb])
                        k.dma(S, xt[b][:], xsrc[128 + n * 128:128 + (n + 1) * 128, :], r=[u_x1e], w=[uxt[b]])
                        for half in range(2):
                            for kk in range(8):
                                k.op(PE, lambda e, half=half, kk=kk: e.matmul(PF[half][:], lhsT=mb[:, kk, :], rhs=Wo[:, kk, half * 512:(half + 1) * 512], start=(kk == 0), stop=(kk == 7)), r=[umb, uWo], w=[uPF[half]])
                            k.op(V, lambda e, half=half, b=b: e.tensor_tensor(out=xn[b][:, half * 512:(half + 1) * 512], in0=PF[half][:], in1=xt[b][:, half * 512:(half + 1) * 512], op=ALU.add), r=[uPF[half], uxt[b]], w=[uxn[b]])
                        if last:
                            k.dma(S, y_out[n * 128:(n + 1) * 128, :], xn[b][:], r=[uxn[b]], w=[u_yout])
                        else:
                            k.dma(S, x1e[128 + n * 128:128 + (n + 1) * 128, :], xn[b][:], r=[uxn[b]], w=[u_x1e])
                            if n == 0:
                                k.dma(S, h_bounce.ap()[0:128, :], xn[b][:], r=[uxn[b]], w=[u_hb])
                            if n == 15:
                                k.dma(S, h_bounce.ap()[128:256, :], xn[b][:], r=[uxn[b]], w=[u_hb])
                    k.barrier()
            if not last:
                ccs = k.new_sem("cch")
                k.custom(G, lambda e: e.collective_compute("AllGather", ALU.bypass, replica_groups=RG, ins=[h_bounce.ap().opt()], outs=[h_gath.ap().opt()]), ccs, 1, r=[u_hb], w=[u_hg])
                with ExitStack() as st2:
                    hb = sb(st2, "hb", [128, 2, D], F32); uhb = U()
                    accL = sb(st2, "accL", [128, D], F32); accR = sb(st2, "accR", [128, D], F32); uaL = U(); uaR = U()
                    for r_ in range(4):
                        k.dma(S, hb[:], h_gath.ap()[r_ * 256:(r_ + 1) * 256, :].rearrange("(a p) f -> p a f", p=128), r=[u_hg], w=[uhb])
                        for (acc, ua, a_, sc) in ((accL, uaL, 1, r_), (accR, uaR, 0, 4 + r_)):
                            if r_ == 0:
                                k.op(V, lambda e, acc=acc, a_=a_, sc=sc: e.tensor_scalar_mul(out=acc[:], in0=hb[:, a_, :], scalar1=sel[:, sc:sc + 1]), r=[uhb, uC], w=[ua])
                            else:
                                k.op(V, lambda e, acc=acc, a_=a_, sc=sc: e.scalar_tensor_tensor(out=acc[:], in0=hb[:, a_, :], scalar=sel[:, sc:sc + 1], in1=acc[:], op0=ALU.mult, op1=ALU.add), r=[uhb, uC, ua], w=[ua])
                    k.dma(S, x1e[0:128, :], accL[:], r=[uaL], w=[u_x1e])
                    k.dma(S, x1e[TE - 128:TE, :], accR[:], r=[uaR], w=[u_x1e])
                    k.barrier()

        for l in range(NL):
            xsrc = x_ext if l == 0 else x1e
            last = (l == NL - 1)
            k.dma(S, lg[:], ret_decay[l:l + 1, :].partition_broadcast(128), w=[uL])
            k.dma(S, qg[:], q_norm_g[l:l + 1, :].partition_broadcast(128), w=[uL])
            k.dma(S, kg[:], k_norm_g[l:l + 1, :].partition_broadcast(128), w=[uL])
            k.dma(S, snk[:], attn_sink[l:l + 1, :].partition_broadcast(128), w=[uL])
            k.dma(S, prm[0:31, :], conv_dw[l], w=[uL])
            k.dma(S, prm[31:32, :], conv_b[l], w=[uL])
            k.dma(S, prm[32:33, :], conv_ln_g[l], w=[uL])
            k.dma(S, prm[33:34, :], conv_ln_b[l], w=[uL])
            k.dma(S, prm[34:37, :], b_gate[l], w=[uL])
            k.op(A, lambda e: e.activation(out=lg[:], in_=lg[:], func=AF.Exp), r=[uL], w=[uL])
            k.op(V, lambda e: e.tensor_scalar_mul(out=lg[:], in0=lg[:], scalar1=-1.0), r=[uL], w=[uL])
            k.op(V, lambda e: e.tensor_tensor(out=g2[:, 0:64], in0=qg[:], in1=qg[:], op=ALU.mult), r=[uL], w=[uL])
            k.op(V, lambda e: e.tensor_tensor(out=g2[:, 64:128], in0=kg[:], in1=kg[:], op=ALU.mult), r=[uL], w=[uL])
            k.op(V, lambda e: e.tensor_reduce(out=tmpc[:, 0:1], in_=g2[:, 0:64], axis=AX.X, op=ALU.max), r=[uL], w=[uL])
            k.op(V, lambda e: e.tensor_reduce(out=tmpc[:, 1:2], in_=g2[:, 64:128], axis=AX.X, op=ALU.max), r=[uL], w=[uL])
            k.op(V, lambda e: e.tensor_tensor(out=tmpc[:, 2:3], in0=tmpc[:, 0:1], in1=tmpc[:, 1:2], op=ALU.mult), r=[uL], w=[uL])
            k.op(A, lambda e: e.activation(out=tmpc[:, 3:4], in_=tmpc[:, 2:3], func=AF.Sqrt), r=[uL], w=[uL])
            k.op(V, lambda e: e.tensor_scalar_mul(out=negc[:], in0=tmpc[:, 3:4], scalar1=-8.0), r=[uL], w=[uL])
            k.op(A, lambda e: e.activation(out=snke[:], in_=snk[:], func=AF.Exp, bias=negc[:, 0:1], scale=1.0), r=[uL], w=[uL])
            for c in range(8):
                k.op(PE, lambda e, c=c: e.transpose(out=PF[0][:, c * 37:(c + 1) * 37], in_=prm[:, c * 128:(c + 1) * 128], identity=ident[0:37, 0:37]), r=[uL, uC], w=[uPF[0]])
            k.op(V, lambda e: e.tensor_copy(out=prT[:].rearrange("p c j -> p (c j)"), in_=PF[0][:, 0:296]), r=[uPF[0]], w=[uL])
            k.barrier()

            stW = ExitStack()
            Wt_pre = sb(stW, "Wt", [128, 8, 2560], BF16); uWt_pre = U()
            stWc = ExitStack()
            Wc = sb(stWc, "Wc", [128, 8, 3072], BF16); uWc = U()
            if 'C' in phases:
                for j in range(6):
                    k.dma(G, Wc[:, :, j * 512:(j + 1) * 512], w_in[l, :, j * 512:(j + 1) * 512].rearrange("(k p) c -> p k c", p=128), w=[uWc])
            with ExitStack() as st:
                g_bc = sb(st, "g_bc", [128, D], F32); ug = U()
                k.dma(S, g_bc[:], norm_g[l:l + 1, :].partition_broadcast(128), w=[ug])
                xt = [sb(st, f"xt{i}", [128, D], F32) for i in range(2)]; uxt = [U(), U()]
                junk = [sb(st, f"junk{i}", [128, D], BF16) for i in range(2)]; ujunk = [U(), U()]
                xs = [sb(st, f"xs{i}", [128, D], BF16) for i in range(2)]; uxs = [U(), U()]
                ssq = [sb(st, f"ssq{i}", [128, 2], F32) for i in range(2)]; ussq = [U(), U()]
                def a_s1(t):
                    b = t % 2
                    k.dma(S, xt[b][:], xsrc[t * 128:(t + 1) * 128, :], r=[u_x1e], w=[uxt[b]])
                    k.op(V, lambda e, b=b: e.memset(ssq[b][:], 0.0), w=[ussq[b]])
                    k.op(A, lambda e, b=b: e.activation(out=junk[b][:], in_=xt[b][:], func=AF.Square, accum_out=ssq[b][:, 0:1]), r=[uxt[b]], w=[ujunk[b], ussq[b]])
                    k.op(V, lambda e, b=b: e.tensor_scalar(out=ssq[b][:, 1:2], in0=ssq[b][:, 0:1], scalar1=1.0 / D, scalar2=EPS, op0=ALU.mult, op1=ALU.add), r=[ussq[b]], w=[ussq[b]])
                    k.op(A, lambda e, b=b: e.activation(out=ssq[b][:, 1:2], in_=ssq[b][:, 1:2], func=AF.Sqrt), r=[ussq[b]], w=[ussq[b]])
                    k.op(V, lambda e, b=b: e.reciprocal(out=ssq[b][:, 1:2], in_=ssq[b][:, 1:2]), r=[ussq[b]], w=[ussq[b]])
                    k.op(V, lambda e, b=b: e.scalar_tensor_tensor(out=xs[b][:], in0=xt[b][:], scalar=ssq[b][:, 1:2], in1=g_bc[:], op0=ALU.mult, op1=ALU.mult), r=[uxt[b], ussq[b], ug], w=[uxs[b]])

                def a_s2(t):
                    b = t % 2
                    for c in range(8):
                        k.op(PE, lambda e, b=b, c=c: e.transpose(out=PT[:, c, :], in_=xs[b][:, c * 128:(c + 1) * 128], identity=identb[:]), r=[uxs[b], uC], w=[uPT])
                    k.op(A, lambda e, t=t: e.activation(out=hT[:, :, t * 128:(t + 1) * 128], in_=PT[:], func=AF.Copy), r=[uPT], w=[uhT[t]])
                a_s1(0)
                for t in range(18):
                    if t < 17:
                        a_s1(t + 1)
                    a_s2(t)
                k.barrier()

            if 'C' in phases:
                with ExitStack() as st:
                    if 'T' in phases:
                        for j in range(5):
                            k.dma(G, Wt_pre[:, :, j * 512:(j + 1) * 512], w_in[l, :, OFF_AQ + j * 512:OFF_AQ + (j + 1) * 512].rearrange("(k p) c -> p k c", p=128), w=[uWt_pre])
                    sig = [sb(st, f"sig{i}", [128, 544], F32) for i in range(2)]; usig = [U(), U()]
                    vv = [sb(st, f"vv{i}", [128, 544], F32) for i in range(2)]; uvv = [U(), U()]
                    acc1 = [sb(st, f"acc1{i}", [128, 512], F32) for i in range(2)]; uacc1 = [U(), U()]
                    acc2 = [sb(st, f"acc2{i}", [128, 512], F32) for i in range(2)]; uacc2 = [U(), U()]
                    tmpk = [sb(st, f"tmpk{i}", [128, 512], F32) for i in range(4)]; utmpk = [U() for _ in range(4)]
                    yy = sb(st, "yy", [128, 8, 512], F32); uyy = [U() for _ in range(8)]
                    ysq = [sb(st, f"ysq{i}", [128, 512], F32) for i in range(2)]; uysq = [U(), U()]
                    mean = sb(st, "mean", [128, 512], F32); msq = sb(st, "msq", [128, 512], F32)
                    rstd = sb(st, "rstd", [128, 512], F32); ustat = U()
                    t1 = [sb(st, f"t1{i}", [128, 512], F32) for i in range(2)]; ut1 = [U(), U()]
                    t2 = t1; ut2 = ut1
                    s1 = t1; us1 = ut1
                    sg = [sb(st, f"sg{i}", [128, 512], F32) for i in range(2)]; usg = [U(), U()]
                    ocT = sb(st, "ocT", [128, 8, 512], BF16); uocT = U()
                    pTls = [PF[4], PF[5]]; uTl = [uPF[4], uPF[5]]
                    pST, pSQ, uST, uSQ = PF[6], PF[4], uPF[6], uPF[4]
                    it = 0
                    tk = 0

                    def proj(tg, c, b):
                        e0 = 128 + tg * 512
                        hr = uhT[(e0 - 16) // 128:(e0 + 528 + 127) // 128]
                        pA, uA, pB, uB = PF[b], uPF[b], PF[2 + b], uPF[2 + b]
                        tb = 0
                        pTl = pTls[b]
                        for (ps, ups, col0, to) in ((pA, uA, c * 128, tb), (pB, uB, 1024 + c * 128, tb + 32)):
                            for kk in range(8):
                                k.op(PE, lambda e, ps=ps, kk=kk, col0=col0: e.matmul(ps[:, 0:512], lhsT=Wc[:, kk, col0:col0 + 128], rhs=hT[:, kk, e0 - 16:e0 + 496], start=(kk == 0), stop=(kk == 7)), r=[uWc] + hr, w=[ups])
                            for kk in range(8):
                                k.op(PE, lambda e, kk=kk, col0=col0, to=to, pTl=pTl: e.matmul(pTl[:, to:to + 32], lhsT=Wc[:, kk, col0:col0 + 128], rhs=hT[:, kk, e0 + 496:e0 + 528], start=(kk == 0), stop=(kk == 7)), r=[uWc] + hr, w=[uTl[b]])
                    for tg in range(4):
                        e0 = 128 + tg * 512
                        hr = uhT[(e0 - 16) // 128:(e0 + 528 + 127) // 128]
                        proj(tg, 0, it % 2)
                        for c in range(8):
                            b = it % 2
                            it += 1
                            pA, uA, pB, uB = PF[b], uPF[b], PF[2 + b], uPF[2 + b]
                            tb = 0
                            pTl = pTls[b]
                            if c < 7:
                                proj(tg, c + 1, it % 2)
                            k.op(A, lambda e, b=b, pB=pB: e.activation(out=sig[b][:, 0:512], in_=pB[:, 0:512], func=AF.Sigmoid), r=[uB], w=[usig[b]])
                            k.op(A, lambda e, b=b, tb=tb, pTl=pTl: e.activation(out=sig[b][:, 512:544], in_=pTl[:, tb + 32:tb + 64], func=AF.Sigmoid), r=[uTl[b]], w=[usig[b]])
                            k.op(V, lambda e, b=b, pA=pA: e.tensor_tensor(out=vv[b][:, 0:512], in0=pA[:, 0:512], in1=sig[b][:, 0:512], op=ALU.mult), r=[uA, usig[b]], w=[uvv[b]])
                            k.op(V, lambda e, b=b, tb=tb, pTl=pTl: e.tensor_tensor(out=vv[b][:, 512:544], in0=pTl[:, tb:tb + 32], in1=sig[b][:, 512:544], op=ALU.mult), r=[uTl[b], usig[b]], w=[uvv[b]])
                            k.op(V, lambda e, c=c, b=b: e.tensor_scalar(out=acc1[b][:], in0=vv[b][:, 1:513], scalar1=prT[:, c, 0:1], scalar2=prT[:, c, 31:32], op0=ALU.mult, op1=ALU.add), r=[uvv[b], uL], w=[uacc1[b]])
                            for kt in range(1, 16):
                                k.op(V, lambda e, c=c, kt=kt, b=b: e.scalar_tensor_tensor(out=acc1[b][:], in0=vv[b][:, kt + 1:kt + 513], scalar=prT[:, c, kt:kt + 1], in1=acc1[b][:], op0=ALU.mult, op1=ALU.add), r=[uvv[b], uL, uacc1[b]], w=[uacc1[b]])
                            k.op(A, lambda e, c=c, b=b: e.activation(out=acc2[b][:], in_=vv[b][:, 17:529], func=AF.Copy, scale=prT[:, c, 16:17]), r=[uvv[b], uL], w=[uacc2[b]])
                            for kt in range(17, 31):
                                j = tk % 4
                                tk += 1
                                k.op(A, lambda e, c=c, kt=kt, j=j, b=b: e.activation(out=tmpk[j][:], in_=vv[b][:, kt + 1:kt + 513], func=AF.Copy, scale=prT[:, c, kt:kt + 1]), r=[uvv[b], uL], w=[utmpk[j]])
                                k.op(G, lambda e, j=j, b=b: e.tensor_tensor(out=acc2[b][:], in0=acc2[b][:], in1=tmpk[j][:], op=ALU.add), r=[utmpk[j], uacc2[b]], w=[uacc2[b]])
                            k.op(G, lambda e, c=c, b=b: e.tensor_tensor(out=yy[:, c, :], in0=acc1[b][:], in1=acc2[b][:], op=ALU.add), r=[uacc1[b], uacc2[b]], w=[uyy[c]])
                        for c in range(8):
                            b = c % 2
                            k.op(A, lambda e, c=c, b=b: e.activation(out=ysq[b][:], in_=yy[:, c, :], func=AF.Square), r=[uyy[c]], w=[uysq[b]])
                            k.op(PE, lambda e, c=c: e.matmul(pST[:], lhsT=onesf[:], rhs=yy[:, c, :], start=(c == 0), stop=(c == 7)), r=[uyy[c], uC], w=[uST])
                            k.op(PE, lambda e, c=c, b=b: e.matmul(pSQ[:], lhsT=onesf[:], rhs=ysq[b][:], start=(c == 0), stop=(c == 7)), r=[uysq[b], uC], w=[uSQ])
                        k.op(V, lambda e: e.tensor_copy(out=mean[:], in_=pST[:]), r=[uST], w=[ustat])
                        k.op(G, lambda e: e.tensor_tensor(out=msq[:], in0=mean[:], in1=mean[:], op=ALU.mult), r=[ustat], w=[ustat])
                        k.op(V, lambda e: e.tensor_tensor(out=rstd[:], in0=pSQ[:], in1=msq[:], op=ALU.subtract), r=[uSQ, ustat], w=[ustat])
                        k.op(V, lambda e: e.tensor_scalar_add(out=rstd[:], in0=rstd[:], scalar1=EPS), r=[ustat], w=[ustat])
                        k.op(A, lambda e: e.activation(out=rstd[:], in_=rstd[:], func=AF.Sqrt), r=[ustat], w=[ustat])
                        k.op(V, lambda e: e.reciprocal(out=rstd[:], in_=rstd[:]), r=[ustat], w=[ustat])
                        for c in range(8):
                            b = c % 2
                            pG, uG = PF[b], uPF[b]
                            for kk in range(8):
                                k.op(PE, lambda e, c=c, kk=kk, pG=pG: e.matmul(pG[:], lhsT=Wc[:, kk, 2048 + c * 128:2048 + (c + 1) * 128], rhs=hT[:, kk, e0:e0 + 512], start=(kk == 0), stop=(kk == 7)), r=[uWc] + hr, w=[uG])
                            k.op(V, lambda e, c=c, b=b: e.tensor_tensor(out=t1[b][:], in0=yy[:, c, :], in1=mean[:], op=ALU.subtract), r=[uyy[c], ustat], w=[ut1[b]])
                            k.op(G, lambda e, b=b: e.tensor_tensor(out=t2[b][:], in0=t1[b][:], in1=rstd[:], op=ALU.mult), r=[ut1[b], ustat], w=[ut2[b]])
                            k.op(A, lambda e, c=c, b=b: e.activation(out=s1[b][:], in_=t2[b][:], func=AF.Silu, bias=prT[:, c, 33:34], scale=prT[:, c, 32:33]), r=[ut2[b], uL], w=[us1[b]])
                            k.op(A, lambda e, b=b, pG=pG: e.activation(out=sg[b][:], in_=pG[:], func=AF.Silu), r=[uG], w=[usg[b]])
                            k.op(V, lambda e, c=c, b=b: e.tensor_tensor(out=ocT[:, c, :], in0=s1[b][:], in1=sg[b][:], op=ALU.mult), r=[us1[b], usg[b]], w=[uocT])
                        k.dma(S, OC[:, :, tg * 512:(tg + 1) * 512].rearrange("c p t -> p c t"), ocT[:], r=[uocT], w=[u_OC])
                    k.barrier()
            stWc.close()
            if 'T' in phases:
                if 'C' not in phases:
                    for j in range(5):
                        k.dma(G, Wt_pre[:, :, j * 512:(j + 1) * 512], w_in[l, :, OFF_AQ + j * 512:OFF_AQ + (j + 1) * 512].rearrange("(k p) c -> p k c", p=128), w=[uWt_pre])
                phase_T(l, Wt_pre, uWt_pre)
            stW.close()
            if 'R' in phases:
                phase_R(l)
            if 'F' in phases:
                phase_F(l, xsrc, last)
        k.barrier()
    return nc


def make_consts(core):
    c = core % 4
    seg = c * T
    p = np.arange(128)
    tt = np.arange(T)
    inv = 10000.0 ** (-np.arange(128, dtype=np.float64) / 128.0)
    ang = inv[:, None] * (seg + tt)[None, :].astype(np.float64)
    cosr = np.cos(ang).astype(np.float32)
    sinr = np.sin(ang).astype(np.float32)
    inva = 500000.0 ** (-np.arange(8, dtype=np.float64) / 8.0)
    pos = (seg - 128 + np.arange(18)[None, :] * 128 + p[:, None]).astype(np.float64)
    anga = pos[:, :, None] * inva[None, None, :]
    cosa = np.cos(anga).astype(np.float32)
    sina = np.sin(anga).astype(np.float32)
    j = p[:, None]
    i = p[None, :]
    maskL = (j >= i).astype(np.float32)
    maskR = (j <= i).astype(np.float32)
    mask = np.concatenate([maskL, maskR, maskL * (1.0 if c > 0 else 0.0), maskR * (1.0 if c < 3 else 0.0)], axis=1)
    eseg = np.zeros((128, 32), np.float32)
    for n in range(16):
        eseg[:, n] = 2047 - (n * 128 + p)
        eseg[:, 16 + n] = n * 128 + p
    ek = np.stack([127 - p, p], axis=1).astype(np.float32)
    eq = np.concatenate([np.tile((np.arange(128) + 1)[None, :], (128, 1)), np.tile((128 - np.arange(128))[None, :], (128, 1))], axis=1).astype(np.float32)
    Ef = np.maximum(i - j, 0); Eb = np.maximum(j - i, 0)
    Mf = (i >= j); Mb = (j > i)
    em = np.concatenate([Ef, Eb, Mf, Mb], axis=1).astype(np.float32)
    BIG = 1.0e6
    dist = np.zeros((128, 8), np.float32)
    sel = np.zeros((128, 8), np.float32)
    for r in range(4):
        dist[:, r] = T * (c - r - 1) if r < c else BIG
        dist[:, 4 + r] = T * (r - c - 1) if r > c else BIG
        sel[:, r] = 1.0 if r == c - 1 else 0.0
        sel[:, 4 + r] = 1.0 if r == c + 1 else 0.0
    return dict(c_cosr=cosr, c_sinr=sinr, c_cosa=cosa, c_sina=sina, c_mask=mask,
                c_ident=np.eye(128, dtype=np.float32), c_eseg=eseg, c_ek=ek, c_eq=eq, c_em=em,
                c_dist=dist, c_sel=sel, c_e128=np.tile((128.0 * np.arange(16, dtype=np.float32))[None, :], (128, 1)))


def make_in_maps(inputs):
    x = np.asarray(inputs['x'], np.float32)
    shared = dict(
        norm_g=np.asarray(inputs['norm_g'], np.float32),
        w_in=np.asarray(inputs['w_in'], np.float32),
        b_gate=np.asarray(inputs['b_gate'], np.float32).reshape(2, 3, D),
        conv_dw=np.asarray(inputs['conv_dw'], np.float32),
        conv_b=np.asarray(inputs['conv_b'], np.float32).reshape(2, 1, D),
        conv_ln_g=np.asarray(inputs['conv_ln_g'], np.float32).reshape(2, 1, D),
        conv_ln_b=np.asarray(inputs['conv_ln_b'], np.float32).reshape(2, 1, D),
        ret_decay=np.asarray(inputs['ret_decay'], np.float32).reshape(2, 8),
        q_norm_g=np.asarray(inputs['q_norm_g'], np.float32),
        k_norm_g=np.asarray(inputs['k_norm_g'], np.float32),
        attn_sink=np.asarray(inputs['attn_sink'], np.float32),
        w_conv_out=np.asarray(inputs['w_conv_out'], np.float32),
        w_ret_out=np.asarray(inputs['w_ret_out'], np.float32),
        w_attn_out=np.asarray(inputs['w_attn_out'], np.float32),
        w_out=np.asarray(inputs['w_out'], np.float32),
    )
    in_maps = []
    for core in range(8):
        b, c = core // 4, core % 4
        xe = np.zeros((TE, D), np.float32)
        lo = c * T - 128
        hi = c * T + T + 128
        slo, shi = max(lo, 0), min(hi, 4 * T)
        xe[slo - lo:shi - lo] = x[b, slo:shi]
        m = dict(shared)
        m['x_ext'] = xe
        m.update(make_consts(core))
        in_maps.append(m)
    return in_maps


_NC = None


def kernel(**inputs):
    global _NC
    if _NC is None:
        _NC = build(2)
    in_maps = make_in_maps(inputs)
    res = run_bass_kernel_spmd(_NC, in_maps, core_ids=list(range(8)))
    out = np.zeros((2, 4 * T, D), np.float32)
    for core in range(8):
        b, c = core // 4, core % 4
        out[b, c * T:(c + 1) * T] = res.results[core]["y_out"]
    return out
```

```python
import os
import numpy as np
import concourse.bass as bass
import concourse.mybir as mybir
from concourse.bass_utils import run_bass_kernel_spmd
from contextlib import ExitStack

F32 = mybir.dt.float32
BF16 = mybir.dt.bfloat16
ALU = mybir.AluOpType
AF = mybir.ActivationFunctionType
AX = mybir.AxisListType

ENG = ['tensor', 'vector', 'scalar', 'gpsimd', 'sync']
EPOCH = 20000
ND = 8
PE, V, A, G, S = 'tensor', 'vector', 'scalar', 'gpsimd', 'sync'


class U:
    __slots__ = ('w', 'rs')

    def __init__(s):
        s.w = None
        s.rs = {}


class KB:
    def __init__(s, nc, stack):
        s.nc = nc
        s.st = stack
        s.cnt = {e: 0 for e in ENG}
        s.nsem = 0
        s.sem = {e: s.new_sem(f'e_{e}') for e in ENG}
        s.hist = {e: [] for e in ENG}
        s.waited = {e: {} for e in ENG}
        s.dsem = {}
        s.dtarget = {}
        s.dcount = {}
        s.n_inst = 0

    def new_sem(s, name):
        s.nsem += 1
        return s.st.enter_context(s.nc.semaphore(f'{name}_{s.nsem}'))

    def _waits(s, engine, r, w):
        deps = {}

        def add(tok):
            key = id(tok[0])
            if key not in deps or deps[key][1] < tok[1]:
                deps[key] = tok
        for u in r:
            if u.w is not None:
                add(u.w)
        for u in w:
            if u.w is not None:
                add(u.w)
            for tok in u.rs.values():
                add(tok)
        waits = []
        wd = s.waited[engine]
        for key, (sem, val, src) in deps.items():
            if engine == PE and src == PE:
                continue
            if wd.get(key, 0) >= val:
                continue
            wd[key] = val
            waits.append((sem, val))
        return waits

    def _emit(s, ename, waits, fn, inc):
        e = getattr(s.nc, ename)
        for sem, val in waits:
            e.wait_ge(sem, val)
        if fn is None:
            return
        ins = fn(e)
        if inc[1] is None:
            ins.then_inc(inc[0])
        else:
            ins.then_inc(inc[0], inc[1])
        s.n_inst += 1

    def op(s, engine, fn, r=(), w=()):
        waits = s._waits(engine, r, w)
        if s.cnt[engine] >= EPOCH:
            s.hist[engine].append((s.sem[engine], s.cnt[engine]))
            s.sem[engine] = s.new_sem(f'e_{engine}')
            s.cnt[engine] = 0
        s.cnt[engine] += 1
        sem = s.sem[engine]
        tok = (sem, s.cnt[engine], engine)
        s._emit(engine, waits, fn, (sem, 1))
        for u in r:
            u.rs[id(sem)] = tok
        for u in w:
            u.w = tok
            u.rs = {}
        return tok

    def dma(s, q, out, in_, r=(), w=(), **kw):
        waits = s._waits(q, r, w)
        if q not in s.dsem:
            s.dsem[q] = [s.new_sem(f'd_{q}{i}') for i in range(ND)]
            s.dtarget[q] = [0] * ND
            s.dcount[q] = 0
        i = s.dcount[q] % ND
        s.dcount[q] += 1
        sem = s.dsem[q][i]
        prev = s.dtarget[q][i]
        if prev > 0 and s.waited[q].get(id(sem), 0) < prev:
            s.waited[q][id(sem)] = prev
            waits.append((sem, prev))
        tgt = prev + 16
        s.dtarget[q][i] = tgt
        tok = (sem, tgt, 'dma')
        s._emit(q, waits, lambda e: e.dma_start(out=out, in_=in_, **kw), (sem, 16))
        for u in r:
            u.rs[id(sem)] = tok
        for u in w:
            u.w = tok
            u.rs = {}
        return tok

    def custom(s, engine, fn, inc_sem, inc_val, r=(), w=()):
        waits = s._waits(engine, r, w)
        tok = (inc_sem, inc_val, 'custom')
        s._emit(engine, waits, fn, (inc_sem, None))
        for u in r:
            u.rs[id(inc_sem)] = tok
        for u in w:
            u.w = tok
            u.rs = {}
        return tok

    def barrier(s):
        toks = []
        for e in ENG:
            for sem, c in s.hist[e]:
                toks.append((sem, c))
            if s.cnt[e] > 0:
                toks.append((s.sem[e], s.cnt[e]))
        for q in s.dsem:
            for i in range(ND):
                if s.dtarget[q][i] > 0:
                    toks.append((s.dsem[q][i], s.dtarget[q][i]))
        for e in ENG:
            wd = s.waited[e]
            waits = []
            for sem, val in toks:
                if wd.get(id(sem), 0) >= val:
                    continue
                wd[id(sem)] = val
                waits.append((sem, val))
            s._emit(e, waits, None, None)


D = 1024
T = 2048
TE = 2304
NT = 16
INW = 14848
OFF_CGLU, OFF_CGATE = 0, 2048
OFF_RQ, OFF_RK, OFF_RV, OFF_RG = 3072, 4096, 5120, 7168
OFF_AQ, OFF_AK, OFF_AV, OFF_AG = 9216, 10240, 10496, 10752
OFF_GL = 11776
EPS = 1e-6


def build(NL=2, dbg=False, phases='CTRF'):
    nc = bass.Bass("TRN2", target_bir_lowering=False)

    def din(name, shape):
        return nc.dram_tensor(name, shape, F32, kind="ExternalInput").ap()

    x_ext = din("x_ext", [TE, D])
    norm_g = din("norm_g", [2, D])
    w_in = din("w_in", [2, D, INW])
    b_gate = din("b_gate", [2, 3, D])
    conv_dw = din("conv_dw", [2, 31, D])
    conv_b = din("conv_b", [2, 1, D])
    conv_ln_g = din("conv_ln_g", [2, 1, D])
    conv_ln_b = din("conv_ln_b", [2, 1, D])
    ret_decay = din("ret_decay", [2, 8])
    q_norm_g = din("q_norm_g", [2, 64])
    k_norm_g = din("k_norm_g", [2, 64])
    attn_sink = din("attn_sink", [2, 16])
    w_conv_out = din("w_conv_out", [2, D, D])
    w_ret_out = din("w_ret_out", [2, 2 * D, D])
    w_attn_out = din("w_attn_out", [2, D, D])
    w_out = din("w_out", [2, D, D])
    c_cosr = din("c_cosr", [128, T])
    c_sinr = din("c_sinr", [128, T])
    c_cosa = din("c_cosa", [128, 18, 8])
    c_sina = din("c_sina", [128, 18, 8])
    c_mask = din("c_mask", [128, 512])
    c_ident = din("c_ident", [128, 128])
    c_eseg = din("c_eseg", [128, 32])
    c_ek = din("c_ek", [128, 2])
    c_eq = din("c_eq", [128, 256])
    c_em = din("c_em", [128, 512])
    c_dist = din("c_dist", [128, 8])
    c_sel = din("c_sel", [128, 8])
    c_e128 = din("c_e128", [128, 16])

    y_out = nc.dram_tensor("y_out", [T, D], F32, kind="ExternalOutput").ap()
    okind = "ExternalOutput" if dbg else "Internal"
    OC = nc.dram_tensor("OC", [8, 128, T], BF16, kind=okind).ap()
    OA = nc.dram_tensor("OA", [16, 64, T], BF16, kind=okind).ap()
    OR = nc.dram_tensor("OR", [16, 128, T], BF16, kind=okind).ap()
    x1e = nc.dram_tensor("x1e", [TE, D], F32, kind=okind).ap()
    fs_bounce = nc.dram_tensor("fs_bounce", [512, 512], F32)
    fs_gath = nc.dram_tensor("fs_gath", [2048, 512], F32)
    h_bounce = nc.dram_tensor("h_bounce", [256, D], F32)
    h_gath = nc.dram_tensor("h_gath", [1024, D], F32)
    u_OC, u_OA, u_OR, u_x1e, u_fsb, u_fsg, u_hb, u_hg, u_yout = [U() for _ in range(9)]
    RG = [[0, 1, 2, 3], [4, 5, 6, 7]]

    with ExitStack() as st0:
        k = KB(nc, st0)

        ncount = [0]

        def sb(st, name, shape, dt):
            ncount[0] += 1
            return st.enter_context(nc.sbuf_tensor(f"{name}_{ncount[0]}", shape, dt))

        PF = [st0.enter_context(nc.psum_tensor(f"pf{i}", [128, 512], F32)) for i in range(7)]
        uPF = [U() for _ in range(7)]
        PT = st0.enter_context(nc.psum_tensor("ptb", [128, 8, 128], BF16))
        uPT = U()
        hT = sb(st0, "hT", [128, 8, TE], BF16)
        uhT = [U() for _ in range(18)]
        ident = sb(st0, "ident", [128, 128], F32); identb = sb(st0, "identb", [128, 128], BF16)
        onesf = sb(st0, "onesf", [128, 128], F32); onesb = sb(st0, "onesb", [128, 128], BF16)
        maskb = sb(st0, "maskb", [128, 512], BF16)
        eseg = sb(st0, "eseg", [128, 32], F32); ek = sb(st0, "ek", [128, 2], F32)
        eq = sb(st0, "eq", [128, 256], F32); em = sb(st0, "em", [128, 512], F32)
        dist = sb(st0, "dist", [128, 8], F32); sel = sb(st0, "sel", [128, 8], F32)
        e128 = sb(st0, "e128", [128, 16], F32)
        cosa = sb(st0, "cosa", [128, 18, 8], F32); sina = sb(st0, "sina", [128, 18, 8], F32)
        lg = sb(st0, "lg", [128, 8], F32)
        qg = sb(st0, "qg", [128, 64], F32); kg = sb(st0, "kg", [128, 64], F32)
        snk = sb(st0, "snk", [128, 16], F32); snke = sb(st0, "snke", [128, 16], F32)
        negc = sb(st0, "negc", [128, 1], F32); tmpc = sb(st0, "tmpc", [128, 4], F32)
        g2 = sb(st0, "g2", [128, 128], F32)
        prm = sb(st0, "prm", [37, D], F32)
        prT = sb(st0, "prT", [128, 8, 37], F32)
        uC = U()
        uL = U()

        for (t, src) in ((ident, c_ident), (eseg, c_eseg), (ek, c_ek), (eq, c_eq), (em, c_em),
                         (dist, c_dist), (sel, c_sel), (cosa, c_cosa), (sina, c_sina), (e128, c_e128)):
            k.dma(S, t[:], src, w=[uC])
        k.dma(G, identb[:], c_ident, w=[uC])
        k.dma(G, maskb[:], c_mask, w=[uC])
        k.op(V, lambda e: e.memset(onesf[:], 1.0 / 1024.0), w=[uC])
        k.op(V, lambda e: e.memset(onesb[:], 1.0), w=[uC])
        k.barrier()

        def phase_T(l, Wt, uWt):
            with ExitStack() as st:
                qTg = sb(st, "qTg", [64, 16, 512], BF16); uqT = [U() for _ in range(4)]
                kT = sb(st, "kTres", [64, 4, TE], BF16); ukT = [U() for _ in range(18)]
                Vr = sb(st, "Vres", [128, 18, 256], BF16); uVr = [U() for _ in range(18)]
                sq = [sb(st, f"sq{i}", [128, 1280], F32) for i in range(2)]; usq = [U(), U()]
                ssa = [sb(st, f"ssa{i}", [128, 20], F32) for i in range(2)]; rsa = [sb(st, f"rsa{i}", [128, 20], F32) for i in range(2)]; urs = [U(), U()]
                qn = [sb(st, f"qn{i}", [128, 1280], F32) for i in range(2)]; uqn = [U(), U()]
                rt = [[sb(st, f"rt{p}{i}", [128, 20, 8], F32) for i in range(4)] for p in range(2)]; urt = [[U() for _ in range(4)] for _ in range(2)]
                qb = [sb(st, f"qb{i}", [128, 1024], BF16) for i in range(2)]; uqb = [U(), U()]
                kb = [sb(st, f"kb{i}", [128, 256], BF16) for i in range(2)]; ukb = [U(), U()]
                sq3 = [sq[p][:].rearrange("p (h d) -> p h d", d=64) for p in range(2)]
                qn3 = [qn[p][:].rearrange("p (h d) -> p h d", d=64) for p in range(2)]

                def norm_rot(t, h0, h1, p, banks):
                    nh = h1 - h0
                    k.op(V, lambda e: e.tensor_reduce(out=ssa[p][:, h0:h1], in_=sq3[p][:, h0:h1, :], axis=AX.X, op=ALU.add), r=[usq[p]], w=[urs[p]])
                    yield
                    k.op(V, lambda e: e.tensor_scalar(out=rsa[p][:, h0:h1], in0=ssa[p][:, h0:h1], scalar1=1.0 / 64.0, scalar2=EPS, op0=ALU.mult, op1=ALU.add), r=[urs[p]], w=[urs[p]])
                    yield
                    k.op(A, lambda e: e.activation(out=rsa[p][:, h0:h1], in_=rsa[p][:, h0:h1], func=AF.Sqrt), r=[urs[p]], w=[urs[p]])
                    yield
                    k.op(V, lambda e: e.reciprocal(out=rsa[p][:, h0:h1], in_=rsa[p][:, h0:h1]), r=[urs[p]], w=[urs[p]])
                    yield
                    if h0 == 0:
                        for half in range(2):
                            bk, ubk = banks[half]
                            k.op(V, lambda e, half=half, bk=bk: e.tensor_tensor(out=qn3[p][:, half * 8:(half + 1) * 8, :], in0=bk[:].rearrange("p (h d) -> p h d", d=64), in1=rsa[p][:, half * 8:(half + 1) * 8].unsqueeze(2).broadcast_to([128, 8, 64]), op=ALU.mult), r=[ubk, urs[p]], w=[uqn[p]])
                            yield
                        k.op(G, lambda e: e.tensor_tensor(out=qn3[p][:, 0:16, :], in0=qn3[p][:, 0:16, :], in1=qg[:].unsqueeze(1).broadcast_to([128, 16, 64]), op=ALU.mult), r=[uqn[p], uL], w=[uqn[p]])
                        yield
                    else:
                        bk, ubk = banks[0]
                        k.op(V, lambda e, bk=bk: e.tensor_tensor(out=qn3[p][:, 16:20, :], in0=bk[:, 0:256].rearrange("p (h d) -> p h d", d=64), in1=rsa[p][:, 16:20].unsqueeze(2).broadcast_to([128, 4, 64]), op=ALU.mult), r=[ubk, urs[p]], w=[uqn[p]])
                        yield
                        k.op(G, lambda e: e.tensor_tensor(out=qn3[p][:, 16:20, :], in0=qn3[p][:, 16:20, :], in1=kg[:].unsqueeze(1).broadcast_to([128, 4, 64]), op=ALU.mult), r=[uqn[p], uL], w=[uqn[p]])
                        yield
                    cb = cosa[:, t, :].unsqueeze(1).broadcast_to([128, nh, 8])
                    sbb = sina[:, t, :].unsqueeze(1).broadcast_to([128, nh, 8])
                    x1 = qn3[p][:, h0:h1, 0:8]
                    x2 = qn3[p][:, h0:h1, 8:16]
                    r_, ur_ = rt[p], urt[p]
                    k.op(V, lambda e: e.tensor_tensor(out=r_[0][:, h0:h1, :], in0=x1, in1=cb, op=ALU.mult), r=[uqn[p], uC], w=[ur_[0]])
                    yield
                    k.op(G, lambda e: e.tensor_tensor(out=r_[1][:, h0:h1, :], in0=x2, in1=sbb, op=ALU.mult), r=[uqn[p], uC], w=[ur_[1]])
                    yield
                    k.op(V, lambda e: e.tensor_tensor(out=r_[2][:, h0:h1, :], in0=x2, in1=cb, op=ALU.mult), r=[uqn[p], uC], w=[ur_[2]])
                    yield
                    k.op(G, lambda e: e.tensor_tensor(out=r_[3][:, h0:h1, :], in0=x1, in1=sbb, op=ALU.mult), r=[uqn[p], uC], w=[ur_[3]])
                    yield
                    k.op(V, lambda e: e.tensor_tensor(out=x1, in0=r_[0][:, h0:h1, :], in1=r_[1][:, h0:h1, :], op=ALU.subtract), r=[ur_[0], ur_[1], ur_[2], ur_[3]], w=[uqn[p]])
                    yield
                    k.op(G, lambda e: e.tensor_tensor(out=x2, in0=r_[2][:, h0:h1, :], in1=r_[3][:, h0:h1, :], op=ALU.add), r=[ur_[2], ur_[3]], w=[uqn[p]])
                    yield

                def kv_s1(t):
                    p = t % 2
                    bk, ubk = PF[2 + p], uPF[2 + p]
                    for kk in range(8):
                        k.op(PE, lambda e, kk=kk: e.matmul(bk[:], lhsT=hT[:, kk, t * 128:(t + 1) * 128], rhs=Wt[:, kk, 1024:1536], start=(kk == 0), stop=(kk == 7)), r=[uWt, uhT[t]], w=[ubk])
                    yield
                    k.op(A, lambda e: e.activation(out=sq[p][:, 1024:1280], in_=bk[:, 0:256], func=AF.Square), r=[ubk], w=[usq[p]])
                    yield
                    k.op(A, lambda e: e.activation(out=Vr[:, t, :], in_=bk[:, 256:512], func=AF.Copy), r=[ubk], w=[uVr[t]])
                    yield
                    yield from norm_rot(t, 16, 20, p, [(bk, ubk)])
                    k.op(A, lambda e: e.activation(out=kb[p][:], in_=qn[p][:, 1024:1280], func=AF.Copy), r=[uqn[p]], w=[ukb[p]])
                    yield

                def kv_s2(t):
                    p = t % 2
                    for g in range(4):
                        k.op(PE, lambda e, g=g: e.transpose(out=PT[0:64, g, :], in_=kb[p][:, g * 64:(g + 1) * 64], identity=identb[:]), r=[ukb[p], uC], w=[uPT])
                    k.op(V, lambda e: e.tensor_copy(out=kT[:, :, t * 128:(t + 1) * 128], in_=PT[0:64, 0:4, :]), r=[uPT], w=[ukT[t]])
                def interleave(gens):
                    gens = list(gens)
                    while gens:
                        for g_ in list(gens):
                            try:
                                next(g_)
                            except StopIteration:
                                gens.remove(g_)
                for t0 in range(0, 18, 2):
                    interleave([kv_s1(t0), kv_s1(t0 + 1)])
                    kv_s2(t0)
                    kv_s2(t0 + 1)
                if dbg == 'T1':
                    k.barrier()
                    return
                gT = sb(st, "gT", [64, 16, 512], BF16); ugT = U()
                pt = [sb(st, f"pt{i}", [128, 512], BF16) for i in range(3)]; upt = [U() for _ in range(3)]
                den = sb(st, "den", [64, 512], F32); uden = U()
                rec = sb(st, "rec", [64, 512], F32); urec = U()
                on = sb(st, "on", [64, 512], F32); uon = U()
                on2 = [on, on]; uon2 = [uon, uon]
                pt_b = [sb(st, f"ptb{i}", [128, 512], BF16) for i in range(3)]; upt_b = [U() for _ in range(3)]
                pt2 = [pt, pt_b]; upt2 = [upt, upt_b]
                oaT = sb(st, "oaT", [64, 16, 512], BF16); uoaT = U()
                for tg in range(4):
                    e0 = 128 + tg * 512
                    hr = uhT[e0 // 128:(e0 + 512) // 128]
                    def q_s1(nn):
                        p = nn % 2
                        t = tg * 4 + nn + 1
                        banks = [(PF[2 * p], uPF[2 * p]), (PF[2 * p + 1], uPF[2 * p + 1])]
                        for half in range(2):
                            bk, ubk = banks[half]
                            for kk in range(8):
                                k.op(PE, lambda e, half=half, kk=kk, bk=bk: e.matmul(bk[:], lhsT=hT[:, kk, t * 128:(t + 1) * 128], rhs=Wt[:, kk, half * 512:(half + 1) * 512], start=(kk == 0), stop=(kk == 7)), r=[uWt, uhT[t]], w=[ubk])
                            k.op(A, lambda e, half=half, bk=bk: e.activation(out=sq[p][:, half * 512:(half + 1) * 512], in_=bk[:], func=AF.Square), r=[ubk], w=[usq[p]])
                        yield
                        yield from norm_rot(t, 0, 16, p, banks)
                        k.op(A, lambda e: e.activation(out=qb[p][:], in_=qn[p][:, 0:1024], func=AF.Copy), r=[uqn[p]], w=[uqb[p]])
                        yield

                    def q_s2(nn):
                        p = nn % 2
                        for rr in range(2):
                            for j in range(8):
                                hh = rr * 8 + j
                                k.op(PE, lambda e, j=j, hh=hh: e.transpose(out=PT[0:64, j, :], in_=qb[p][:, hh * 64:(hh + 1) * 64], identity=identb[:]), r=[uqb[p], uC], w=[uPT])
                            k.op(V, lambda e, rr=rr: e.tensor_copy(out=qTg[:, rr * 8:(rr + 1) * 8, nn * 128:(nn + 1) * 128], in_=PT[0:64, :, :]), r=[uPT], w=[uqT[nn]])
                    interleave([q_s1(0), q_s1(1)])
                    q_s2(0)
                    q_s2(1)
                    interleave([q_s1(2), q_s1(3)])
                    q_s2(2)
                    q_s2(3)
                    for hh in range(16):
                        ps, ups = PF[hh % 2], uPF[hh % 2]
                        for kk in range(8):
                            k.op(PE, lambda e, ps=ps, hh=hh, kk=kk, e0=e0: e.matmul(ps[0:64, :], lhsT=Wt[:, kk, 1536 + hh * 64:1536 + (hh + 1) * 64], rhs=hT[:, kk, e0:e0 + 512], start=(kk == 0), stop=(kk == 7)), r=[uWt] + hr, w=[ups])
                        k.op(A, lambda e, ps=ps, hh=hh: e.activation(out=gT[:, hh, :], in_=ps[0:64, :], func=AF.Silu), r=[ups], w=[ugT])
                    def t2_front(idx, nn, g):
                        n = tg * 4 + nn
                        t = n + 1
                        ptc, uptc = pt2[idx % 2], upt2[idx % 2]
                        for mi, m in enumerate((t - 1, t, t + 1)):
                            k.op(PE, lambda e, mi=mi, m=m: e.matmul(PF[2 + mi][:], lhsT=kT[:, g, m * 128:(m + 1) * 128], rhs=qTg[:, 4 * g:4 * g + 4, nn * 128:(nn + 1) * 128], start=True, stop=True), r=[ukT[m], uqT[nn]], w=[uPF[2 + mi]])
                            k.op(A, lambda e, mi=mi: e.activation(out=ptc[mi][:], in_=PF[2 + mi][:], func=AF.Exp, bias=negc[:, 0:1], scale=0.125), r=[uPF[2 + mi], uL], w=[uptc[mi]])
                        mL = 256 if t == 1 else 0
                        mR = 384 if t == 16 else 128
                        k.op(V, lambda e: e.tensor_tensor(out=ptc[0][:].rearrange("p (a i) -> p a i", a=4), in0=ptc[0][:].rearrange("p (a i) -> p a i", a=4), in1=maskb[:, mL:mL + 128].unsqueeze(1).broadcast_to([128, 4, 128]), op=ALU.mult), r=[uptc[0], uC], w=[uptc[0]])
                        k.op(G, lambda e: e.tensor_tensor(out=ptc[2][:].rearrange("p (a i) -> p a i", a=4), in0=ptc[2][:].rearrange("p (a i) -> p a i", a=4), in1=maskb[:, mR:mR + 128].unsqueeze(1).broadcast_to([128, 4, 128]), op=ALU.mult), r=[uptc[2], uC], w=[uptc[2]])

                    def t2_back(idx, nn, g):
                        n = tg * 4 + nn
                        t = n + 1
                        ptc, uptc = pt2[idx % 2], upt2[idx % 2]
                        onc, uonc = on2[idx % 2], uon2[idx % 2]
                        for mi, m in enumerate((t - 1, t, t + 1)):
                            k.op(PE, lambda e, mi=mi, m=m: e.matmul(PF[5][0:64, :], lhsT=Vr[:, m, g * 64:(g + 1) * 64], rhs=ptc[mi][:], start=(mi == 0), stop=(mi == 2)), r=[uVr[m], uptc[mi]], w=[uPF[5]])
                        for mi, m in enumerate((t - 1, t, t + 1)):
                            k.op(PE, lambda e, mi=mi: e.matmul(PF[6][0:64, :], lhsT=onesb[:, 0:64], rhs=ptc[mi][:], start=(mi == 0), stop=(mi == 2)), r=[uC, uptc[mi]], w=[uPF[6]])
                        for ei in range(4):
                            hd = 4 * g + ei
                            k.op(A, lambda e, ei=ei, hd=hd: e.activation(out=den[:, ei * 128:(ei + 1) * 128], in_=PF[6][0:64, ei * 128:(ei + 1) * 128], func=AF.Ln, bias=snke[0:64, hd:hd + 1], scale=1.0), r=[uPF[6], uL], w=[uden])
                        k.op(A, lambda e: e.activation(out=rec[:], in_=den[:], func=AF.Exp, scale=-1.0), r=[uden], w=[urec])
                        k.op(V, lambda e: e.tensor_tensor(out=onc[:], in0=PF[5][0:64, :], in1=rec[:], op=ALU.mult), r=[uPF[5], urec], w=[uonc])
                        k.op(G, lambda e: e.tensor_tensor(out=oaT[:, 4 * g:4 * g + 4, nn * 128:(nn + 1) * 128], in0=onc[:].rearrange("p (a i) -> p a i", a=4), in1=gT[:, 4 * g:4 * g + 4, nn * 128:(nn + 1) * 128], op=ALU.mult), r=[uonc, ugT], w=[uoaT])
                    its = [(nn, g) for nn in range(4) for g in range(4)]
                    for idx, (nn, g) in enumerate(its):
                        t2_front(idx, nn, g)
                        if idx > 0:
                            t2_back(idx - 1, *its[idx - 1])
                    t2_back(len(its) - 1, *its[-1])
                    k.dma(S, OA[:, :, tg * 512:(tg + 1) * 512].rearrange("h d t -> d h t"), oaT[:], r=[uoaT], w=[u_OA])
                k.barrier()

        def phase_R(l):
            with ExitStack() as st:
                Wr = sb(st, "Wr", [128, 8, 2048], BF16); uWr = U()
                qT = sb(st, "rqT", [128, 2, T], BF16); uq = [U() for _ in range(4)]
                kT = sb(st, "rkT", [128, 2, T], BF16); ukk = [U() for _ in range(4)]
                Vv = sb(st, "rV", [128, 16, 512], BF16); uV = [U() for _ in range(16)]
                kt = sb(st, "rkt", [128, 16, 256], BF16); ukt = [U() for _ in range(16)]
                Pp = sb(st, "rP", [128, 16, 512], F32); uP = [U() for _ in range(16)]
                cs = sb(st, "rcs", [128, 2, 512], F32); ucs = U()
                tq = [sb(st, f"rtq{i}", [128, 512], F32) for i in range(4)]; utq = [U() for _ in range(4)]
                Sf = sb(st, "Sf", [128, 2, 512], F32); Sb = sb(st, "Sb", [128, 2, 512], F32); uSf = U(); uSb = U()
                Sfb = sb(st, "Sfb", [128, 2, 512], BF16); Sbb = sb(st, "Sbb", [128, 2, 512], BF16); uSfb = U(); uSbb = U()
                fsum = sb(st, "fsum", [128, 4, 512], F32); ufs = U()
                DTm = sb(st, "DTm", [128, 128], F32); dq = sb(st, "dq", [128, 256], F32); dsg = sb(st, "dsg", [128, 32], F32)
                dk = sb(st, "dk", [128, 2], F32); gC = sb(st, "gC", [128, 2], F32); wr = sb(st, "wr", [128, 8], F32)
                tmpD = sb(st, "tmpD", [128, 256], F32); uH = U()
                kfs = sb(st, "kfs", [128, 256], BF16); kbs = sb(st, "kbs", [128, 256], BF16); ukfs = U(); ukbs = U()
                qd = sb(st, "qd", [128, 2, 128], BF16); uqd = U()
                kdc = sb(st, "kdc", [128, 256], BF16); ukdc = U()
                sd = sb(st, "sd", [128, 128], BF16); usd = U()
                oo = sb(st, "oo", [128, 512], F32); uoo = U()
                onn = sb(st, "onn", [128, 512], F32); uonn = U()
                sgg = sb(st, "sgg", [128, 512], F32); usgg = U()
                og = sb(st, "og", [128, 512], BF16); uog = U()
                og_b = sb(st, "og_b", [128, 512], BF16); uog_b = U()
                qd_b = sb(st, "qd_b", [128, 2, 128], BF16); uqd_b = U()
                qd_c = sb(st, "qd_c", [128, 2, 128], BF16); uqd_c = U()
                kdc_b = sb(st, "kdc_b", [128, 256], BF16); ukdc_b = U()
                Sfb_c = sb(st, "Sfb_c", [128, 2, 512], BF16); uSfb_c = U()
                qd_d = sb(st, "qd_d", [128, 2, 128], BF16); uqd_d = U()
                oo_b, uoo_b = tq[0], utq[0]
                onn_b, uonn_b = tq[1], utq[1]
                sgg_b, usgg_b = tq[2], utq[2]
                bst_b = sb(st, "bst_b", [128, 6], F32); mv_b = sb(st, "mv_b", [128, 2], F32); ubn_b = U()
                gn = sb(st, "gn", [128, 32], F32)
                bst = sb(st, "bst", [128, 6], F32); mv = sb(st, "mv", [128, 2], F32); ubn = U()
                orT = sb(st, "orT", [128, 4, 512], BF16); uorT = U()
                RSTOP = int(os.environ.get("RSTOP", "0"))

                class _Stop(Exception):
                    pass

                def ck(n_):
                    if RSTOP == n_:
                        raise _Stop()
                try:
                  for h in range(4):
                      wl = [(1024, OFF_RV + h * 512), (1536, OFF_RG + h * 512)]
                      if h % 2 == 0:
                          wl = [(0, OFF_RQ + h * 256), (512, OFF_RK + h * 256)] + wl
                      for (dst, off) in wl:
                          k.dma(G, Wr[:, :, dst:dst + 512], w_in[l, :, off:off + 512].rearrange("(k p) c -> p k c", p=128), w=[uWr])
                      qc0 = (h % 2) * 256
                      kc0 = 512 + (h % 2) * 256
                      lgf = lg[:, h:h + 1]
                      lgb = lg[:, 4 + h:5 + h]
                      hc = dict(r=[uL, uC, uH], w=[uH])
                      k.op(A, lambda e: e.activation(out=dsg[:, 0:16], in_=eseg[:, 0:16], func=AF.Exp, scale=lgf), **hc)
                      k.op(A, lambda e: e.activation(out=dsg[:, 16:32], in_=eseg[:, 16:32], func=AF.Exp, scale=lgb), **hc)
                      k.op(A, lambda e: e.activation(out=dk[:, 0:1], in_=ek[:, 0:1], func=AF.Exp, scale=lgf), **hc)
                      k.op(A, lambda e: e.activation(out=dk[:, 1:2], in_=ek[:, 1:2], func=AF.Exp, scale=lgb), **hc)
                      k.op(A, lambda e: e.activation(out=dq[:, 0:128], in_=eq[:, 0:128], func=AF.Exp, scale=lgf), **hc)
                      k.op(A, lambda e: e.activation(out=dq[:, 128:256], in_=eq[:, 128:256], func=AF.Exp, scale=lgb), **hc)
                      k.op(V, lambda e: e.tensor_scalar_mul(out=dq[:], in0=dq[:], scalar1=1.0 / 16.0), **hc)
                      k.op(A, lambda e: e.activation(out=gn[:, 0:16], in_=e128[:], func=AF.Exp, scale=lgf), **hc)
                      k.op(A, lambda e: e.activation(out=gn[:, 16:32], in_=e128[:], func=AF.Exp, scale=lgb), **hc)
                      k.op(A, lambda e: e.activation(out=gC[:, 0:1], in_=lgf, func=AF.Exp, scale=128.0), **hc)
                      k.op(A, lambda e: e.activation(out=gC[:, 1:2], in_=lgb, func=AF.Exp, scale=128.0), **hc)
                      k.op(A, lambda e: e.activation(out=wr[:, 0:4], in_=dist[:, 0:4], func=AF.Exp, scale=lgf), **hc)
                      k.op(A, lambda e: e.activation(out=wr[:, 4:8], in_=dist[:, 4:8], func=AF.Exp, scale=lgb), **hc)
                      k.op(A, lambda e: e.activation(out=tmpD[:, 0:128], in_=em[:, 0:128], func=AF.Exp, scale=lgf), **hc)
                      k.op(A, lambda e: e.activation(out=tmpD[:, 128:256], in_=em[:, 128:256], func=AF.Exp, scale=lgb), **hc)
                      k.op(V, lambda e: e.tensor_tensor(out=tmpD[:], in0=tmpD[:], in1=em[:, 256:512], op=ALU.mult), **hc)
                      k.op(V, lambda e: e.tensor_tensor(out=DTm[:], in0=tmpD[:, 0:128], in1=tmpD[:, 128:256], op=ALU.add), **hc)
                      k.op(V, lambda e: e.tensor_scalar_mul(out=DTm[:], in0=DTm[:], scalar1=1.0 / 16.0), **hc)
                      ck(1)
                      def r0_proj(tg, col0, dstT, ud):
                          e0 = 128 + tg * 512
                          hr = uhT[e0 // 128:(e0 + 512) // 128]
                          for dc in range(2):
                              for kk in range(8):
                                  k.op(PE, lambda e, dc=dc, kk=kk: e.matmul(PF[dc][:], lhsT=Wr[:, kk, col0 + dc * 128:col0 + (dc + 1) * 128], rhs=hT[:, kk, e0:e0 + 512], start=(kk == 0), stop=(kk == 7)), r=[uWr] + hr, w=[uPF[dc]])
                          k.op(V, lambda e: e.tensor_tensor(out=tq[0][:], in0=PF[0][:], in1=cs[:, 0, :], op=ALU.mult), r=[uPF[0], ucs], w=[utq[0]])
                          k.op(V, lambda e: e.tensor_tensor(out=tq[1][:], in0=PF[1][:], in1=cs[:, 1, :], op=ALU.mult), r=[uPF[1], ucs], w=[utq[1]])
                          k.op(V, lambda e: e.tensor_tensor(out=tq[2][:], in0=PF[1][:], in1=cs[:, 0, :], op=ALU.mult), r=[uPF[1], ucs], w=[utq[2]])
                          k.op(V, lambda e: e.tensor_tensor(out=tq[3][:], in0=PF[0][:], in1=cs[:, 1, :], op=ALU.mult), r=[uPF[0], ucs], w=[utq[3]])
                          k.op(V, lambda e: e.tensor_tensor(out=dstT[:, 0, tg * 512:(tg + 1) * 512], in0=tq[0][:], in1=tq[1][:], op=ALU.subtract), r=[utq[0], utq[1]], w=[ud])
                          k.op(G, lambda e: e.tensor_tensor(out=dstT[:, 1, tg * 512:(tg + 1) * 512], in0=tq[2][:], in1=tq[3][:], op=ALU.add), r=[utq[2], utq[3]], w=[ud])

                      def r0_front(tg):
                          k.dma(S, cs[:, 0, :], c_cosr[:, tg * 512:(tg + 1) * 512], w=[ucs])
                          k.dma(S, cs[:, 1, :], c_sinr[:, tg * 512:(tg + 1) * 512], w=[ucs])
                          r0_proj(tg, qc0, qT, uq[tg])
                          for nn in range(4):
                              n = tg * 4 + nn
                              t = n + 1
                              for kk in range(8):
                                  k.op(PE, lambda e, kk=kk, t=t: e.matmul(PF[2][:], lhsT=hT[:, kk, t * 128:(t + 1) * 128], rhs=Wr[:, kk, 1024:1536], start=(kk == 0), stop=(kk == 7)), r=[uWr, uhT[t]], w=[uPF[2]])
                              k.op(A, lambda e, n=n: e.activation(out=Vv[:, n, :], in_=PF[2][:], func=AF.Copy), r=[uPF[2]], w=[uV[n]])
                          r0_proj(tg, kc0, kT, ukk[tg])

                      def r0_back(tg):
                          for nn in range(4):
                              n = tg * 4 + nn
                              for dc in range(2):
                                  k.op(PE, lambda e, dc=dc, n=n: e.transpose(out=PT[:, dc, :], in_=kT[:, dc, n * 128:(n + 1) * 128], identity=identb[:]), r=[ukk[tg], uC], w=[uPT])
                              ptv = PT[:, 0:2, :]
                              k.op(A, lambda e, n=n, ptv=ptv: e.activation(out=kfs[:].rearrange("p (a d) -> p a d", a=2), in_=ptv, func=AF.Copy, scale=dsg[:, n:n + 1]), r=[uPT, uH], w=[ukfs])
                              k.op(A, lambda e, n=n, ptv=ptv: e.activation(out=kbs[:].rearrange("p (a d) -> p a d", a=2), in_=ptv, func=AF.Copy, scale=dsg[:, 16 + n:17 + n]), r=[uPT, uH], w=[ukbs])
                              k.op(A, lambda e, n=n, ptv=ptv: e.activation(out=kt[:, n, :].rearrange("p (a d) -> p a d", a=2), in_=ptv, func=AF.Copy), r=[uPT], w=[ukt[n]])
                              for dc in range(2):
                                  k.op(PE, lambda e, dc=dc, n=n: e.matmul(PF[3 + dc][:], lhsT=kfs[:, dc * 128:(dc + 1) * 128], rhs=Vv[:, n, :], start=(n == 0), stop=(n == 15)), r=[ukfs, uV[n]], w=[uPF[3 + dc]])
                                  k.op(PE, lambda e, dc=dc, n=n: e.matmul(PF[5 + dc][:], lhsT=kbs[:, dc * 128:(dc + 1) * 128], rhs=Vv[:, n, :], start=(n == 0), stop=(n == 15)), r=[ukbs, uV[n]], w=[uPF[5 + dc]])
                      r0_front(0)
                      for tg in range(4):
                          if tg < 3:
                              r0_front(tg + 1)
                          r0_back(tg)
                      if dbg == 'R0':
                          k.barrier()
                          return
                      for j in range(4):
                          k.op(A, lambda e, j=j: e.activation(out=fsum[:, j, :], in_=PF[3 + j][:], func=AF.Copy), r=[uPF[3 + j]], w=[ufs])
                      k.dma(S, fs_bounce.ap().rearrange("(j p) v -> p j v", p=128), fsum[:], r=[ufs], w=[u_fsb])
                      ccs = k.new_sem("cc")
                      k.custom(G, lambda e: e.collective_compute("AllGather", ALU.bypass, replica_groups=RG, ins=[fs_bounce.ap().opt()], outs=[fs_gath.ap().opt()]), ccs, 1, r=[u_fsb], w=[u_fsg])
                      k.op(V, lambda e: e.memset(Sf[:], 0.0), r=[uSf], w=[uSf])
                      k.op(V, lambda e: e.memset(Sb[:], 0.0), r=[uSb], w=[uSb])
                      k.op(V, lambda e: e.memset(Sfb[:], 0.0), r=[uSfb], w=[uSfb])
                      k.op(V, lambda e: e.memset(Sbb[:], 0.0), r=[uSbb], w=[uSbb])
                      SfbX = [Sfb, Sfb_c]
                      uSfbX = [uSfb, uSfb_c]
                      k.op(V, lambda e: e.memset(Sfb_c[:], 0.0), r=[uSfb_c], w=[uSfb_c])
                      for i in range(16):
                          nb, nf = 15 - i, i
                          tgb, tgf = nb // 4, nf // 4
                          cur, nxt = i % 2, (i + 1) % 2
                          k.op(A, lambda e, nb=nb: e.activation(out=kdc[:], in_=kt[:, nb, :], func=AF.Copy, scale=dk[:, 1:2]), r=[ukt[nb], uH], w=[ukdc])
                          k.op(A, lambda e, nf=nf: e.activation(out=kdc_b[:], in_=kt[:, nf, :], func=AF.Copy, scale=dk[:, 0:1]), r=[ukt[nf], uH], w=[ukdc_b])
                          for dc in range(2):
                              k.op(PE, lambda e, dc=dc, nb=nb: e.matmul(PF[1 + dc][:], lhsT=kdc[:, dc * 128:(dc + 1) * 128], rhs=Vv[:, nb, :], start=True, stop=True), r=[ukdc, uV[nb]], w=[uPF[1 + dc]])
                          for dc in range(2):
                              k.op(PE, lambda e, dc=dc, nf=nf: e.matmul(PF[5 + dc][:], lhsT=kdc_b[:, dc * 128:(dc + 1) * 128], rhs=Vv[:, nf, :], start=True, stop=True), r=[ukdc_b, uV[nf]], w=[uPF[5 + dc]])
                          k.op(V, lambda e, nb=nb: e.tensor_tensor(out=qd[:], in0=qT[:, :, nb * 128:(nb + 1) * 128], in1=dq[:, 128:256].unsqueeze(1).broadcast_to([128, 2, 128]), op=ALU.mult), r=[uq[tgb], uH], w=[uqd])
                          k.op(G, lambda e, nf=nf: e.tensor_tensor(out=qd_b[:], in0=qT[:, :, nf * 128:(nf + 1) * 128], in1=dq[:, 0:128].unsqueeze(1).broadcast_to([128, 2, 128]), op=ALU.mult), r=[uq[tgf], uH], w=[uqd_b])
                          for dc in range(2):
                              k.op(PE, lambda e, dc=dc: e.matmul(PF[0][:], lhsT=qd[:, dc, :], rhs=Sbb[:, dc, :], start=(dc == 0), stop=(dc == 1)), r=[uqd, uSbb], w=[uPF[0]])
                          for dc in range(2):
                              k.op(PE, lambda e, dc=dc, nf=nf: e.matmul(PF[3][:, 0:128], lhsT=kT[:, dc, nf * 128:(nf + 1) * 128], rhs=qT[:, dc, nf * 128:(nf + 1) * 128], start=(dc == 0), stop=(dc == 1)), r=[ukk[tgf], uq[tgf]], w=[uPF[3]])
                          k.op(V, lambda e: e.tensor_tensor(out=sd[:], in0=PF[3][:, 0:128], in1=DTm[:], op=ALU.mult), r=[uPF[3], uH], w=[usd])
                          k.op(PE, lambda e, nf=nf: e.matmul(PF[4][:], lhsT=sd[:], rhs=Vv[:, nf, :], start=True, stop=False), r=[usd, uV[nf]], w=[uPF[4]])
                          for dc in range(2):
                              k.op(PE, lambda e, dc=dc, cur=cur: e.matmul(PF[4][:], lhsT=qd_b[:, dc, :], rhs=SfbX[cur][:, dc, :], start=False, stop=(dc == 1)), r=[uqd_b, uSfbX[cur]], w=[uPF[4]])
                          for dc in range(2):
                              k.op(V, lambda e, dc=dc: e.scalar_tensor_tensor(out=Sb[:, dc, :], in0=Sb[:, dc, :], scalar=gC[:, 1:2], in1=PF[1 + dc][:], op0=ALU.mult, op1=ALU.add), r=[uPF[1 + dc], uH, uSb], w=[uSb])
                          for dc in range(2):
                              k.op(V, lambda e, dc=dc: e.scalar_tensor_tensor(out=Sf[:, dc, :], in0=Sf[:, dc, :], scalar=gC[:, 0:1], in1=PF[5 + dc][:], op0=ALU.mult, op1=ALU.add), r=[uPF[5 + dc], uH, uSf], w=[uSf])
                          k.op(A, lambda e: e.activation(out=Sbb[:], in_=Sb[:], func=AF.Copy), r=[uSb], w=[uSbb])
                          k.op(A, lambda e, nxt=nxt: e.activation(out=SfbX[nxt][:], in_=Sf[:], func=AF.Copy), r=[uSf], w=[uSfbX[nxt]])
                          if i <= 7:
                              k.op(A, lambda e, nb=nb: e.activation(out=Pp[:, nb, :], in_=PF[0][:], func=AF.Copy), r=[uPF[0]], w=[uP[nb]])
                              k.op(A, lambda e, nf=nf: e.activation(out=Pp[:, nf, :], in_=PF[4][:], func=AF.Copy), r=[uPF[4]], w=[uP[nf]])
                          else:
                              k.op(V, lambda e, nb=nb: e.tensor_tensor(out=Pp[:, nb, :], in0=PF[0][:], in1=Pp[:, nb, :], op=ALU.add), r=[uPF[0], uP[nb]], w=[uP[nb]])
                              k.op(V, lambda e, nf=nf: e.tensor_tensor(out=Pp[:, nf, :], in0=PF[4][:], in1=Pp[:, nf, :], op=ALU.add), r=[uPF[4], uP[nf]], w=[uP[nf]])
                      for r_ in range(4):
                          k.dma(S, fsum[:], fs_gath.ap()[r_ * 512:(r_ + 1) * 512, :].rearrange("(j p) v -> p j v", p=128), r=[u_fsg], w=[ufs])
                          for dirn, (Sx, uSx) in enumerate(((Sf, uSf), (Sb, uSb))):
                              for dc in range(2):
                                  wcol = wr[:, dirn * 4 + r_:dirn * 4 + r_ + 1]
                                  if r_ == 0:
                                      k.op(V, lambda e, Sx=Sx, dc=dc, dirn=dirn, wcol=wcol: e.tensor_scalar_mul(out=Sx[:, dc, :], in0=fsum[:, dirn * 2 + dc, :], scalar1=wcol), r=[ufs, uH], w=[uSx])
                                  else:
                                      k.op(V, lambda e, Sx=Sx, dc=dc, dirn=dirn, wcol=wcol: e.scalar_tensor_tensor(out=Sx[:, dc, :], in0=fsum[:, dirn * 2 + dc, :], scalar=wcol, in1=Sx[:, dc, :], op0=ALU.mult, op1=ALU.add), r=[ufs, uH, uSx], w=[uSx])
                      k.op(A, lambda e: e.activation(out=Sfb[:], in_=Sf[:], func=AF.Copy), r=[uSf], w=[uSfb])
                      k.op(A, lambda e: e.activation(out=Sbb[:], in_=Sb[:], func=AF.Copy), r=[uSb], w=[uSbb])
                      og2 = [og, og_b]
                      uog2 = [uog, uog_b]
                      qd2 = [qd, qd_b]
                      uqd2 = [uqd, uqd_b]

                      def finish(n):
                          tg = n // 4
                          for vc in range(4):
                              k.op(PE, lambda e, vc=vc, n=n: e.transpose(out=PT[:, vc, :], in_=og2[n % 2][:, vc * 128:(vc + 1) * 128], identity=identb[:]), r=[uog2[n % 2], uC], w=[uPT])
                          k.op(A, lambda e, n=n: e.activation(out=orT[:, :, (n % 4) * 128:(n % 4 + 1) * 128], in_=PT[:, 0:4, :], func=AF.Copy), r=[uPT], w=[uorT])
                          if n % 4 == 3:
                              k.dma(S, OR[h * 4:(h + 1) * 4, :, tg * 512:(tg + 1) * 512].rearrange("c p t -> p c t"), orT[:], r=[uorT], w=[u_OR])
                      oo2 = [oo, oo_b]; uoo2 = [uoo, uoo_b]
                      onn2 = [onn, onn_b]; uonn2 = [uonn, uonn_b]
                      sgg2 = [sgg, sgg_b]; usgg2 = [usgg, usgg_b]
                      bst2 = [bst, bst_b]; mv2 = [mv, mv_b]; ubn2 = [ubn, ubn_b]
                      qdp = [[qd, qd_b], [qd_c, qd_d]]; uqdp = [[uqd, uqd_b], [uqd_c, uqd_d]]
                      pO = [(PF[4], uPF[4]), (PF[0], uPF[0])]
                      pGt = [(PF[5], uPF[5]), (PF[1], uPF[1])]
                      for n0 in range(0, 16, 2):
                          pair = (n0, n0 + 1)
                          for n in pair:
                              p = n % 2
                              t = n + 1
                              bk, ubk = pGt[p]
                              for kk in range(8):
                                  k.op(PE, lambda e, kk=kk, t=t, bk=bk: e.matmul(bk[:], lhsT=hT[:, kk, t * 128:(t + 1) * 128], rhs=Wr[:, kk, 1536:2048], start=(kk == 0), stop=(kk == 7)), r=[uWr, uhT[t]], w=[ubk])
                          for n in pair:
                              p = n % 2
                              bk, ubk = pGt[p]
                              k.op(A, lambda e, p=p, bk=bk: e.activation(out=sgg2[p][:], in_=bk[:], func=AF.Silu), r=[ubk], w=[usgg2[p]])
                          for n in pair:
                              p = n % 2
                              tg = n // 4
                              k.op(V, lambda e, n=n, p=p: e.scalar_tensor_tensor(out=qdp[p][0][:], in0=qT[:, :, n * 128:(n + 1) * 128], scalar=gn[:, n:n + 1], in1=dq[:, 0:128].unsqueeze(1).broadcast_to([128, 2, 128]), op0=ALU.mult, op1=ALU.mult), r=[uq[tg], uH], w=[uqdp[p][0]])
                              k.op(V, lambda e, n=n, p=p: e.scalar_tensor_tensor(out=qdp[p][1][:], in0=qT[:, :, n * 128:(n + 1) * 128], scalar=gn[:, 16 + 15 - n:16 + 16 - n], in1=dq[:, 128:256].unsqueeze(1).broadcast_to([128, 2, 128]), op0=ALU.mult, op1=ALU.mult), r=[uq[tg], uH], w=[uqdp[p][1]])
                          for n in pair:
                              p = n % 2
                              bo, ubo = pO[p]
                              for di, Sxb, uSxb in ((0, Sfb, uSfb), (1, Sbb, uSbb)):
                                  for dc in range(2):
                                      k.op(PE, lambda e, di=di, dc=dc, Sxb=Sxb, p=p, bo=bo: e.matmul(bo[:], lhsT=qdp[p][di][:, dc, :], rhs=Sxb[:, dc, :], start=(di == 0 and dc == 0), stop=(di == 1 and dc == 1)), r=[uqdp[p][di], uSxb], w=[ubo])
                          if n0 > 0:
                              finish(n0 - 2)
                              finish(n0 - 1)
                          for n in pair:
                              p = n % 2
                              bo, ubo = pO[p]
                              k.op(V, lambda e, n=n, p=p, bo=bo: e.tensor_tensor(out=oo2[p][:], in0=bo[:], in1=Pp[:, n, :], op=ALU.add), r=[ubo, uP[n]], w=[uoo2[p]])
                          for n in pair:
                              p = n % 2
                              k.op(V, lambda e, p=p: e.bn_stats(out=bst2[p][:], in_=oo2[p][:]), r=[uoo2[p]], w=[ubn2[p]])
                          for n in pair:
                              p = n % 2
                              k.op(V, lambda e, p=p: e.bn_aggr(out=mv2[p][:], in_=bst2[p][:]), r=[ubn2[p]], w=[ubn2[p]])
                          for n in pair:
                              p = n % 2
                              k.op(V, lambda e, p=p: e.tensor_scalar_add(out=mv2[p][:, 1:2], in0=mv2[p][:, 1:2], scalar1=EPS), r=[ubn2[p]], w=[ubn2[p]])
                          for n in pair:
                              p = n % 2
                              k.op(A, lambda e, p=p: e.activation(out=mv2[p][:, 1:2], in_=mv2[p][:, 1:2], func=AF.Sqrt), r=[ubn2[p]], w=[ubn2[p]])
                          for n in pair:
                              p = n % 2
                              k.op(V, lambda e, p=p: e.reciprocal(out=mv2[p][:, 1:2], in_=mv2[p][:, 1:2]), r=[ubn2[p]], w=[ubn2[p]])
                          for n in pair:
                              p = n % 2
                              k.op(V, lambda e, p=p: e.tensor_scalar(out=onn2[p][:], in0=oo2[p][:], scalar1=mv2[p][:, 0:1], scalar2=mv2[p][:, 1:2], op0=ALU.subtract, op1=ALU.mult), r=[uoo2[p], ubn2[p]], w=[uonn2[p]])
                          for n in pair:
                              p = n % 2
                              k.op(G, lambda e, p=p: e.tensor_tensor(out=og2[p][:], in0=onn2[p][:], in1=sgg2[p][:], op=ALU.mult), r=[uonn2[p], usgg2[p]], w=[uog2[p]])
                      finish(14)
                      finish(15)
                except _Stop:
                    pass
                k.barrier()

        def phase_F(l, xsrc, last):
            with ExitStack() as st:
                mg = sb(st, "mg", [128, 8, T], F32); umg = [U() for _ in range(4)]
                sgm = sb(st, "sgm", [128, 512], F32); usgm = U()
                tmpm = sb(st, "tmpm", [128, 512], F32); utmpm = U()
                for bi, (nk, kp) in enumerate(((8, 128), (16, 128), (16, 64))):
                    with ExitStack() as st2:
                        Wb = sb(st2, f"Wb{bi}", [kp, nk, D], BF16); uWb = U()
                        Wg = sb(st2, f"Wg{bi}", [128, 8, D], BF16); uWg = U()
                        ob = sb(st2, f"ob{bi}", [kp, nk, 512], BF16); uob = U()
                        src_w = (w_conv_out, w_ret_out, w_attn_out)[bi]
                        if bi == 2:
                            view = src_w[l].rearrange("(h d) c -> d h c", d=64)
                        else:
                            view = src_w[l].rearrange("(k p) c -> p k c", p=128)
                        for j in range(0, nk, 4):
                            k.dma(G, Wb[:, j:j + 4, :], view[:, j:j + 4, :], w=[uWb])
                        for j in range(2):
                            c0 = OFF_GL + bi * 1024 + j * 512
                            k.dma(G, Wg[:, :, j * 512:(j + 1) * 512], w_in[l, :, c0:c0 + 512].rearrange("(k p) c -> p k c", p=128), w=[uWg])
                        osrc = (OC, OR, OA)[bi]
                        uos = (u_OC, u_OR, u_OA)[bi]
                        for tg in range(4):
                            e0 = 128 + tg * 512
                            hr = uhT[e0 // 128:(e0 + 512) // 128]
                            k.dma(S, ob[:], osrc[:, :, tg * 512:(tg + 1) * 512].rearrange("c p t -> p c t"), r=[uos], w=[uob])
                            for m in range(8):
                                py, upy = PF[m % 3], uPF[m % 3]
                                pg, upg = PF[3 + m % 3], uPF[3 + m % 3]
                                for j in range(nk):
                                    k.op(PE, lambda e, py=py, j=j, m=m: e.matmul(py[:], lhsT=Wb[:, j, m * 128:(m + 1) * 128], rhs=ob[:, j, :], start=(j == 0), stop=(j == nk - 1)), r=[uWb, uob], w=[upy])
                                for kk in range(8):
                                    k.op(PE, lambda e, pg=pg, kk=kk, m=m, e0=e0: e.matmul(pg[:], lhsT=Wg[:, kk, m * 128:(m + 1) * 128], rhs=hT[:, kk, e0:e0 + 512], start=(kk == 0), stop=(kk == 7)), r=[uWg] + hr, w=[upg])
                                k.op(A, lambda e, pg=pg, m=m, bi=bi: e.activation(out=sgm[:], in_=pg[:], func=AF.Sigmoid, bias=prT[:, m, 34 + bi:35 + bi], scale=1.0), r=[upg, uL], w=[usgm])
                                if bi == 0:
                                    k.op(V, lambda e, py=py, m=m, tg=tg: e.tensor_tensor(out=mg[:, m, tg * 512:(tg + 1) * 512], in0=py[:], in1=sgm[:], op=ALU.mult), r=[upy, usgm], w=[umg[tg]])
                                else:
                                    k.op(V, lambda e, py=py: e.tensor_tensor(out=tmpm[:], in0=py[:], in1=sgm[:], op=ALU.mult), r=[upy, usgm], w=[utmpm])
                                    k.op(G, lambda e, m=m, tg=tg: e.tensor_tensor(out=mg[:, m, tg * 512:(tg + 1) * 512], in0=mg[:, m, tg * 512:(tg + 1) * 512], in1=tmpm[:], op=ALU.add), r=[utmpm, umg[tg]], w=[umg[tg]])
                        k.barrier()
                with ExitStack() as st2:
                    Wo = sb(st2, "Wo", [128, 8, D], BF16); uWo = U()
                    for j in range(2):
                        k.dma(G, Wo[:, j * 4:(j + 1) * 4, :], w_out[l].rearrange("(k p) c -> p k c", p=128)[:, j * 4:(j + 1) * 4, :], w=[uWo])
                    mb = sb(st2, "mb", [128, 8, 128], BF16); umb = U()
                    xt = [sb(st2, f"fxt{i}", [128, D], F32) for i in range(2)]; uxt = [U(), U()]
                    xn = [sb(st2, f"fxn{i}", [128, D], F32) for i in range(2)]; uxn = [U(), U()]
                    for n in range(16):
                        b = n % 2
                        k.op(A, lambda e, n=n: e.activation(out=mb[:], in_=mg[:, :, n * 128:(n + 1) * 128], func=AF.Copy), r=[umg[n // 4]], w=[umb])
                        k.dma(S, xt[b][:], xsrc[128 + n * 128:128 + (n + 1) * 128, :], r=[u_x1e], w=[uxt[b]])
                        for half in range(2):
                            for kk in range(8):
                                k.op(PE, lambda e, half=half, kk=kk: e.matmul(PF[half][:], lhsT=mb[:, kk, :], rhs=Wo[:, kk, half * 512:(half + 1) * 512], start=(kk == 0), stop=(kk == 7)), r=[umb, uWo], w=[uPF[half]])
                            k.op(V, lambda e, half=half, b=b: e.tensor_tensor(out=xn[b][:, half * 512:(half + 1) * 512], in0=PF[half][:], in1=xt[b][:, half * 512:(half + 1) * 512], op=ALU.add), r=[uPF[half], uxt[b]], w=[uxn[b]])
                        if last:
                            k.dma(S, y_out[n * 128:(n + 1) * 128, :], xn[b][:], r=[uxn[b]], w=[u_yout])
                        else:
                            k.dma(S, x1e[128 + n * 128:128 + (n + 1) * 128, :], xn[b][:], r=[uxn[b]], w=[u_x1e])
                            if n == 0:
                                k.dma(S, h_bounce.ap()[0:128, :], xn[b][:], r=[uxn[b]], w=[u_hb])
                            if n == 15:
                                k.dma(S, h_bounce.ap()[128:256, :], xn[b][:], r=[uxn[b]], w=[u_hb])
                    k.barrier()
            if not last:
                ccs = k.new_sem("cch")
                k.custom(G, lambda e: e.collective_compute("AllGather", ALU.bypass, replica_groups=RG, ins=[h_bounce.ap().opt()], outs=[h_gath.ap().opt()]), ccs, 1, r=[u_hb], w=[u_hg])
                with ExitStack() as st2:
                    hb = sb(st2, "hb", [128, 2, D], F32); uhb = U()
                    accL = sb(st2, "accL", [128, D], F32); accR = sb(st2, "accR", [128, D], F32); uaL = U(); uaR = U()
                    for r_ in range(4):
                        k.dma(S, hb[:], h_gath.ap()[r_ * 256:(r_ + 1) * 256, :].rearrange("(a p) f -> p a f", p=128), r=[u_hg], w=[uhb])
                        for (acc, ua, a_, sc) in ((accL, uaL, 1, r_), (accR, uaR, 0, 4 + r_)):
                            if r_ == 0:
                                k.op(V, lambda e, acc=acc, a_=a_, sc=sc: e.tensor_scalar_mul(out=acc[:], in0=hb[:, a_, :], scalar1=sel[:, sc:sc + 1]), r=[uhb, uC], w=[ua])
                            else:
                                k.op(V, lambda e, acc=acc, a_=a_, sc=sc: e.scalar_tensor_tensor(out=acc[:], in0=hb[:, a_, :], scalar=sel[:, sc:sc + 1], in1=acc[:], op0=ALU.mult, op1=ALU.add), r=[uhb, uC, ua], w=[ua])
                    k.dma(S, x1e[0:128, :], accL[:], r=[uaL], w=[u_x1e])
                    k.dma(S, x1e[TE - 128:TE, :], accR[:], r=[uaR], w=[u_x1e])
                    k.barrier()

        for l in range(NL):
            xsrc = x_ext if l == 0 else x1e
            last = (l == NL - 1)
            k.dma(S, lg[:], ret_decay[l:l + 1, :].partition_broadcast(128), w=[uL])
            k.dma(S, qg[:], q_norm_g[l:l + 1, :].partition_broadcast(128), w=[uL])
            k.dma(S, kg[:], k_norm_g[l:l + 1, :].partition_broadcast(128), w=[uL])
            k.dma(S, snk[:], attn_sink[l:l + 1, :].partition_broadcast(128), w=[uL])
            k.dma(S, prm[0:31, :], conv_dw[l], w=[uL])
            k.dma(S, prm[31:32, :], conv_b[l], w=[uL])
            k.dma(S, prm[32:33, :], conv_ln_g[l], w=[uL])
            k.dma(S, prm[33:34, :], conv_ln_b[l], w=[uL])
            k.dma(S, prm[34:37, :], b_gate[l], w=[uL])
            k.op(A, lambda e: e.activation(out=lg[:], in_=lg[:], func=AF.Exp), r=[uL], w=[uL])
            k.op(V, lambda e: e.tensor_scalar_mul(out=lg[:], in0=lg[:], scalar1=-1.0), r=[uL], w=[uL])
            k.op(V, lambda e: e.tensor_tensor(out=g2[:, 0:64], in0=qg[:], in1=qg[:], op=ALU.mult), r=[uL], w=[uL])
            k.op(V, lambda e: e.tensor_tensor(out=g2[:, 64:128], in0=kg[:], in1=kg[:], op=ALU.mult), r=[uL], w=[uL])
            k.op(V, lambda e: e.tensor_reduce(out=tmpc[:, 0:1], in_=g2[:, 0:64], axis=AX.X, op=ALU.max), r=[uL], w=[uL])
            k.op(V, lambda e: e.tensor_reduce(out=tmpc[:, 1:2], in_=g2[:, 64:128], axis=AX.X, op=ALU.max), r=[uL], w=[uL])
            k.op(V, lambda e: e.tensor_tensor(out=tmpc[:, 2:3], in0=tmpc[:, 0:1], in1=tmpc[:, 1:2], op=ALU.mult), r=[uL], w=[uL])
            k.op(A, lambda e: e.activation(out=tmpc[:, 3:4], in_=tmpc[:, 2:3], func=AF.Sqrt), r=[uL], w=[uL])
            k.op(V, lambda e: e.tensor_scalar_mul(out=negc[:], in0=tmpc[:, 3:4], scalar1=-8.0), r=[uL], w=[uL])
            k.op(A, lambda e: e.activation(out=snke[:], in_=snk[:], func=AF.Exp, bias=negc[:, 0:1], scale=1.0), r=[uL], w=[uL])
            for c in range(8):
                k.op(PE, lambda e, c=c: e.transpose(out=PF[0][:, c * 37:(c + 1) * 37], in_=prm[:, c * 128:(c + 1) * 128], identity=ident[0:37, 0:37]), r=[uL, uC], w=[uPF[0]])
            k.op(V, lambda e: e.tensor_copy(out=prT[:].rearrange("p c j -> p (c j)"), in_=PF[0][:, 0:296]), r=[uPF[0]], w=[uL])
            k.barrier()

            stW = ExitStack()
            Wt_pre = sb(stW, "Wt", [128, 8, 2560], BF16); uWt_pre = U()
            stWc = ExitStack()
            Wc = sb(stWc, "Wc", [128, 8, 3072], BF16); uWc = U()
            if 'C' in phases:
                for j in range(6):
                    k.dma(G, Wc[:, :, j * 512:(j + 1) * 512], w_in[l, :, j * 512:(j + 1) * 512].rearrange("(k p) c -> p k c", p=128), w=[uWc])
            with ExitStack() as st:
                g_bc = sb(st, "g_bc", [128, D], F32); ug = U()
                k.dma(S, g_bc[:], norm_g[l:l + 1, :].partition_broadcast(128), w=[ug])
                xt = [sb(st, f"xt{i}", [128, D], F32) for i in range(2)]; uxt = [U(), U()]
                junk = [sb(st, f"junk{i}", [128, D], BF16) for i in range(2)]; ujunk = [U(), U()]
                xs = [sb(st, f"xs{i}", [128, D], BF16) for i in range(2)]; uxs = [U(), U()]
                ssq = [sb(st, f"ssq{i}", [128, 2], F32) for i in range(2)]; ussq = [U(), U()]
                def a_s1(t):
                    b = t % 2
                    k.dma(S, xt[b][:], xsrc[t * 128:(t + 1) * 128, :], r=[u_x1e], w=[uxt[b]])
                    k.op(V, lambda e, b=b: e.memset(ssq[b][:], 0.0), w=[ussq[b]])
                    k.op(A, lambda e, b=b: e.activation(out=junk[b][:], in_=xt[b][:], func=AF.Square, accum_out=ssq[b][:, 0:1]), r=[uxt[b]], w=[ujunk[b], ussq[b]])
                    k.op(V, lambda e, b=b: e.tensor_scalar(out=ssq[b][:, 1:2], in0=ssq[b][:, 0:1], scalar1=1.0 / D, scalar2=EPS, op0=ALU.mult, op1=ALU.add), r=[ussq[b]], w=[ussq[b]])
                    k.op(A, lambda e, b=b: e.activation(out=ssq[b][:, 1:2], in_=ssq[b][:, 1:2], func=AF.Sqrt), r=[ussq[b]], w=[ussq[b]])
                    k.op(V, lambda e, b=b: e.reciprocal(out=ssq[b][:, 1:2], in_=ssq[b][:, 1:2]), r=[ussq[b]], w=[ussq[b]])
                    k.op(V, lambda e, b=b: e.scalar_tensor_tensor(out=xs[b][:], in0=xt[b][:], scalar=ssq[b][:, 1:2], in1=g_bc[:], op0=ALU.mult, op1=ALU.mult), r=[uxt[b], ussq[b], ug], w=[uxs[b]])

                def a_s2(t):
                    b = t % 2
                    for c in range(8):
                        k.op(PE, lambda e, b=b, c=c: e.transpose(out=PT[:, c, :], in_=xs[b][:, c * 128:(c + 1) * 128], identity=identb[:]), r=[uxs[b], uC], w=[uPT])
                    k.op(A, lambda e, t=t: e.activation(out=hT[:, :, t * 128:(t + 1) * 128], in_=PT[:], func=AF.Copy), r=[uPT], w=[uhT[t]])
                a_s1(0)
                for t in range(18):
                    if t < 17:
                        a_s1(t + 1)
                    a_s2(t)
                k.barrier()

            if 'C' in phases:
                with ExitStack() as st:
                    if 'T' in phases:
                        for j in range(5):
                            k.dma(G, Wt_pre[:, :, j * 512:(j + 1) * 512], w_in[l, :, OFF_AQ + j * 512:OFF_AQ + (j + 1) * 512].rearrange("(k p) c -> p k c", p=128), w=[uWt_pre])
                    sig = [sb(st, f"sig{i}", [128, 544], F32) for i in range(2)]; usig = [U(), U()]
                    vv = [sb(st, f"vv{i}", [128, 544], F32) for i in range(2)]; uvv = [U(), U()]
                    acc1 = [sb(st, f"acc1{i}", [128, 512], F32) for i in range(2)]; uacc1 = [U(), U()]
                    acc2 = [sb(st, f"acc2{i}", [128, 512], F32) for i in range(2)]; uacc2 = [U(), U()]
                    tmpk = [sb(st, f"tmpk{i}", [128, 512], F32) for i in range(4)]; utmpk = [U() for _ in range(4)]
                    yy = sb(st, "yy", [128, 8, 512], F32); uyy = [U() for _ in range(8)]
                    ysq = [sb(st, f"ysq{i}", [128, 512], F32) for i in range(2)]; uysq = [U(), U()]
                    mean = sb(st, "mean", [128, 512], F32); msq = sb(st, "msq", [128, 512], F32)
                    rstd = sb(st, "rstd", [128, 512], F32); ustat = U()
                    t1 = [sb(st, f"t1{i}", [128, 512], F32) for i in range(2)]; ut1 = [U(), U()]
                    t2 = t1; ut2 = ut1
                    s1 = t1; us1 = ut1
                    sg = [sb(st, f"sg{i}", [128, 512], F32) for i in range(2)]; usg = [U(), U()]
                    ocT = sb(st, "ocT", [128, 8, 512], BF16); uocT = U()
                    pTls = [PF[4], PF[5]]; uTl = [uPF[4], uPF[5]]
                    pST, pSQ, uST, uSQ = PF[6], PF[4], uPF[6], uPF[4]
                    it = 0
                    tk = 0

                    def proj(tg, c, b):
                        e0 = 128 + tg * 512
                        hr = uhT[(e0 - 16) // 128:(e0 + 528 + 127) // 128]
                        pA, uA, pB, uB = PF[b], uPF[b], PF[2 + b], uPF[2 + b]
                        tb = 0
                        pTl = pTls[b]
                        for (ps, ups, col0, to) in ((pA, uA, c * 128, tb), (pB, uB, 1024 + c * 128, tb + 32)):
                            for kk in range(8):
                                k.op(PE, lambda e, ps=ps, kk=kk, col0=col0: e.matmul(ps[:, 0:512], lhsT=Wc[:, kk, col0:col0 + 128], rhs=hT[:, kk, e0 - 16:e0 + 496], start=(kk == 0), stop=(kk == 7)), r=[uWc] + hr, w=[ups])
                            for kk in range(8):
                                k.op(PE, lambda e, kk=kk, col0=col0, to=to, pTl=pTl: e.matmul(pTl[:, to:to + 32], lhsT=Wc[:, kk, col0:col0 + 128], rhs=hT[:, kk, e0 + 496:e0 + 528], start=(kk == 0), stop=(kk == 7)), r=[uWc] + hr, w=[uTl[b]])
                    for tg in range(4):
                        e0 = 128 + tg * 512
                        hr = uhT[(e0 - 16) // 128:(e0 + 528 + 127) // 128]
                        proj(tg, 0, it % 2)
                        for c in range(8):
                            b = it % 2
                            it += 1
                            pA, uA, pB, uB = PF[b], uPF[b], PF[2 + b], uPF[2 + b]
                            tb = 0
                            pTl = pTls[b]
                            if c < 7:
                                proj(tg, c + 1, it % 2)
                            k.op(A, lambda e, b=b, pB=pB: e.activation(out=sig[b][:, 0:512], in_=pB[:, 0:512], func=AF.Sigmoid), r=[uB], w=[usig[b]])
                            k.op(A, lambda e, b=b, tb=tb, pTl=pTl: e.activation(out=sig[b][:, 512:544], in_=pTl[:, tb + 32:tb + 64], func=AF.Sigmoid), r=[uTl[b]], w=[usig[b]])
                            k.op(V, lambda e, b=b, pA=pA: e.tensor_tensor(out=vv[b][:, 0:512], in0=pA[:, 0:512], in1=sig[b][:, 0:512], op=ALU.mult), r=[uA, usig[b]], w=[uvv[b]])
                            k.op(V, lambda e, b=b, tb=tb, pTl=pTl: e.tensor_tensor(out=vv[b][:, 512:544], in0=pTl[:, tb:tb + 32], in1=sig[b][:, 512:544], op=ALU.mult), r=[uTl[b], usig[b]], w=[uvv[b]])
                            k.op(V, lambda e, c=c, b=b: e.tensor_scalar(out=acc1[b][:], in0=vv[b][:, 1:513], scalar1=prT[:, c, 0:1], scalar2=prT[:, c, 31:32], op0=ALU.mult, op1=ALU.add), r=[uvv[b], uL], w=[uacc1[b]])
                            for kt in range(1, 16):
                                k.op(V, lambda e, c=c, kt=kt, b=b: e.scalar_tensor_tensor(out=acc1[b][:], in0=vv[b][:, kt + 1:kt + 513], scalar=prT[:, c, kt:kt + 1], in1=acc1[b][:], op0=ALU.mult, op1=ALU.add), r=[uvv[b], uL, uacc1[b]], w=[uacc1[b]])
                            k.op(A, lambda e, c=c, b=b: e.activation(out=acc2[b][:], in_=vv[b][:, 17:529], func=AF.Copy, scale=prT[:, c, 16:17]), r=[uvv[b], uL], w=[uacc2[b]])
                            for kt in range(17, 31):
                                j = tk % 4
                                tk += 1
                                k.op(A, lambda e, c=c, kt=kt, j=j, b=b: e.activation(out=tmpk[j][:], in_=vv[b][:, kt + 1:kt + 513], func=AF.Copy, scale=prT[:, c, kt:kt + 1]), r=[uvv[b], uL], w=[utmpk[j]])
                                k.op(G, lambda e, j=j, b=b: e.tensor_tensor(out=acc2[b][:], in0=acc2[b][:], in1=tmpk[j][:], op=ALU.add), r=[utmpk[j], uacc2[b]], w=[uacc2[b]])
                            k.op(G, lambda e, c=c, b=b: e.tensor_tensor(out=yy[:, c, :], in0=acc1[b][:], in1=acc2[b][:], op=ALU.add), r=[uacc1[b], uacc2[b]], w=[uyy[c]])
                        for c in range(8):
                            b = c % 2
                            k.op(A, lambda e, c=c, b=b: e.activation(out=ysq[b][:], in_=yy[:, c, :], func=AF.Square), r=[uyy[c]], w=[uysq[b]])
                            k.op(PE, lambda e, c=c: e.matmul(pST[:], lhsT=onesf[:], rhs=yy[:, c, :], start=(c == 0), stop=(c == 7)), r=[uyy[c], uC], w=[uST])
                            k.op(PE, lambda e, c=c, b=b: e.matmul(pSQ[:], lhsT=onesf[:], rhs=ysq[b][:], start=(c == 0), stop=(c == 7)), r=[uysq[b], uC], w=[uSQ])
                        k.op(V, lambda e: e.tensor_copy(out=mean[:], in_=pST[:]), r=[uST], w=[ustat])
                        k.op(G, lambda e: e.tensor_tensor(out=msq[:], in0=mean[:], in1=mean[:], op=ALU.mult), r=[ustat], w=[ustat])
                        k.op(V, lambda e: e.tensor_tensor(out=rstd[:], in0=pSQ[:], in1=msq[:], op=ALU.subtract), r=[uSQ, ustat], w=[ustat])
                        k.op(V, lambda e: e.tensor_scalar_add(out=rstd[:], in0=rstd[:], scalar1=EPS), r=[ustat], w=[ustat])
                        k.op(A, lambda e: e.activation(out=rstd[:], in_=rstd[:], func=AF.Sqrt), r=[ustat], w=[ustat])
                        k.op(V, lambda e: e.reciprocal(out=rstd[:], in_=rstd[:]), r=[ustat], w=[ustat])
                        for c in range(8):
                            b = c % 2
                            pG, uG = PF[b], uPF[b]
                            for kk in range(8):
                                k.op(PE, lambda e, c=c, kk=kk, pG=pG: e.matmul(pG[:], lhsT=Wc[:, kk, 2048 + c * 128:2048 + (c + 1) * 128], rhs=hT[:, kk, e0:e0 + 512], start=(kk == 0), stop=(kk == 7)), r=[uWc] + hr, w=[uG])
                            k.op(V, lambda e, c=c, b=b: e.tensor_tensor(out=t1[b][:], in0=yy[:, c, :], in1=mean[:], op=ALU.subtract), r=[uyy[c], ustat], w=[ut1[b]])
                            k.op(G, lambda e, b=b: e.tensor_tensor(out=t2[b][:], in0=t1[b][:], in1=rstd[:], op=ALU.mult), r=[ut1[b], ustat], w=[ut2[b]])
                            k.op(A, lambda e, c=c, b=b: e.activation(out=s1[b][:], in_=t2[b][:], func=AF.Silu, bias=prT[:, c, 33:34], scale=prT[:, c, 32:33]), r=[ut2[b], uL], w=[us1[b]])
                            k.op(A, lambda e, b=b, pG=pG: e.activation(out=sg[b][:], in_=pG[:], func=AF.Silu), r=[uG], w=[usg[b]])
                            k.op(V, lambda e, c=c, b=b: e.tensor_tensor(out=ocT[:, c, :], in0=s1[b][:], in1=sg[b][:], op=ALU.mult), r=[us1[b], usg[b]], w=[uocT])
                        k.dma(S, OC[:, :, tg * 512:(tg + 1) * 512].rearrange("c p t -> p c t"), ocT[:], r=[uocT], w=[u_OC])
                    k.barrier()
            stWc.close()
            if 'T' in phases:
                if 'C' not in phases:
                    for j in range(5):
                        k.dma(G, Wt_pre[:, :, j * 512:(j + 1) * 512], w_in[l, :, OFF_AQ + j * 512:OFF_AQ + (j + 1) * 512].rearrange("(k p) c -> p k c", p=128), w=[uWt_pre])
                phase_T(l, Wt_pre, uWt_pre)
            stW.close()
            if 'R' in phases:
                phase_R(l)
            if 'F' in phases:
                phase_F(l, xsrc, last)
        k.barrier()
    return nc


def make_consts(core):
    c = core % 4
    seg = c * T
    p = np.arange(128)
    tt = np.arange(T)
    inv = 10000.0 ** (-np.arange(128, dtype=np.float64) / 128.0)
    ang = inv[:, None] * (seg + tt)[None, :].astype(np.float64)
    cosr = np.cos(ang).astype(np.float32)
    sinr = np.sin(ang).astype(np.float32)
    inva = 500000.0 ** (-np.arange(8, dtype=np.float64) / 8.0)
    pos = (seg - 128 + np.arange(18)[None, :] * 128 + p[:, None]).astype(np.float64)
    anga = pos[:, :, None] * inva[None, None, :]
    cosa = np.cos(anga).astype(np.float32)
    sina = np.sin(anga).astype(np.float32)
    j = p[:, None]
    i = p[None, :]
    maskL = (j >= i).astype(np.float32)
    maskR = (j <= i).astype(np.float32)
    mask = np.concatenate([maskL, maskR, maskL * (1.0 if c > 0 else 0.0), maskR * (1.0 if c < 3 else 0.0)], axis=1)
    eseg = np.zeros((128, 32), np.float32)
    for n in range(16):
        eseg[:, n] = 2047 - (n * 128 + p)
        eseg[:, 16 + n] = n * 128 + p
    ek = np.stack([127 - p, p], axis=1).astype(np.float32)
    eq = np.concatenate([np.tile((np.arange(128) + 1)[None, :], (128, 1)), np.tile((128 - np.arange(128))[None, :], (128, 1))], axis=1).astype(np.float32)
    Ef = np.maximum(i - j, 0); Eb = np.maximum(j - i, 0)
    Mf = (i >= j); Mb = (j > i)
    em = np.concatenate([Ef, Eb, Mf, Mb], axis=1).astype(np.float32)
    BIG = 1.0e6
    dist = np.zeros((128, 8), np.float32)
    sel = np.zeros((128, 8), np.float32)
    for r in range(4):
        dist[:, r] = T * (c - r - 1) if r < c else BIG
        dist[:, 4 + r] = T * (r - c - 1) if r > c else BIG
        sel[:, r] = 1.0 if r == c - 1 else 0.0
        sel[:, 4 + r] = 1.0 if r == c + 1 else 0.0
    return dict(c_cosr=cosr, c_sinr=sinr, c_cosa=cosa, c_sina=sina, c_mask=mask,
                c_ident=np.eye(128, dtype=np.float32), c_eseg=eseg, c_ek=ek, c_eq=eq, c_em=em,
                c_dist=dist, c_sel=sel, c_e128=np.tile((128.0 * np.arange(16, dtype=np.float32))[None, :], (128, 1)))


def make_in_maps(inputs):
    x = np.asarray(inputs['x'], np.float32)
    shared = dict(
        norm_g=np.asarray(inputs['norm_g'], np.float32),
        w_in=np.asarray(inputs['w_in'], np.float32),
        b_gate=np.asarray(inputs['b_gate'], np.float32).reshape(2, 3, D),
        conv_dw=np.asarray(inputs['conv_dw'], np.float32),
        conv_b=np.asarray(inputs['conv_b'], np.float32).reshape(2, 1, D),
        conv_ln_g=np.asarray(inputs['conv_ln_g'], np.float32).reshape(2, 1, D),
        conv_ln_b=np.asarray(inputs['conv_ln_b'], np.float32).reshape(2, 1, D),
        ret_decay=np.asarray(inputs['ret_decay'], np.float32).reshape(2, 8),
        q_norm_g=np.asarray(inputs['q_norm_g'], np.float32),
        k_norm_g=np.asarray(inputs['k_norm_g'], np.float32),
        attn_sink=np.asarray(inputs['attn_sink'], np.float32),
        w_conv_out=np.asarray(inputs['w_conv_out'], np.float32),
        w_ret_out=np.asarray(inputs['w_ret_out'], np.float32),
        w_attn_out=np.asarray(inputs['w_attn_out'], np.float32),
        w_out=np.asarray(inputs['w_out'], np.float32),
    )
    in_maps = []
    for core in range(8):
        b, c = core // 4, core % 4
        xe = np.zeros((TE, D), np.float32)
        lo = c * T - 128
        hi = c * T + T + 128
        slo, shi = max(lo, 0), min(hi, 4 * T)
        xe[slo - lo:shi - lo] = x[b, slo:shi]
        m = dict(shared)
        m['x_ext'] = xe
        m.update(make_consts(core))
        in_maps.append(m)
    return in_maps


_NC = None


def kernel(**inputs):
    global _NC
    if _NC is None:
        _NC = build(2)
    in_maps = make_in_maps(inputs)
    res = run_bass_kernel_spmd(_NC, in_maps, core_ids=list(range(8)))
    out = np.zeros((2, 4 * T, D), np.float32)
    for core in range(8):
        b, c = core // 4, core % 4
        out[b, c * T:(c + 1) * T] = res.results[core]["y_out"]
    return out
```
